# Optimizing a Trainium2 kernel written in Bass

```python
import jax, jax.numpy as jnp
from jax import lax
import numpy as np

D_MODEL = 1024
BATCH = 8
SEQ = 4096
DEPTH = 4

N_META = 16
EPS = 1e-6
SSD_D_INNER = 2 * D_MODEL
SSD_HEAD_DIM = 64
SSD_HEADS = SSD_D_INNER // SSD_HEAD_DIM
SSD_GROUPS = 8
SSD_HPG = SSD_HEADS // SSD_GROUPS
SSD_STATE = 128
SSD_CONV = 4
SSD_CHUNK = 128
SSD_CONV_DIM = SSD_D_INNER + 2 * SSD_GROUPS * SSD_STATE
SSD_IN_DIM = SSD_D_INNER + SSD_CONV_DIM + SSD_HEADS
MLA_HEADS = 16
MLA_NOPE = 64
MLA_ROPE = 32
MLA_V = 64
MLA_QK = MLA_NOPE + MLA_ROPE
MLA_Q_RANK = 384
MLA_KV_RANK = 256
MLA_IN_DIM = MLA_Q_RANK + MLA_KV_RANK + MLA_ROPE
ROPE_THETA = 10000.0
ATTN_BLOCK = 128
D_FF = 4 * D_MODEL
N_SSD_LAYERS = (DEPTH + 1) // 2
N_MLA_LAYERS = DEPTH // 2

kernel_name = "hybrid_ssd_mla_meta_trunk"


def rms_norm(x, gain):
    xf = x.astype(jnp.float32)
    y = xf * lax.rsqrt(jnp.mean(xf * xf, axis=-1, keepdims=True) + EPS)
    return (y * gain.astype(jnp.float32)).astype(x.dtype)


def causal_depthwise_conv(u, w, b):
    out = lax.conv_general_dilated(
        u, w[:, None, :].astype(u.dtype), window_strides=(1,),
        padding=[(SSD_CONV - 1, 0)], dimension_numbers=('NWC', 'WIO', 'NWC'),
        feature_group_count=u.shape[-1])
    return out + b.astype(u.dtype)


def ssd_mixer(h, w_in, conv_w, conv_b, dt_bias, a_log, d_skip, norm_g, w_out):
    f32 = jnp.float32
    bsz, L, _ = h.shape
    zxbcdt = h @ w_in
    z, xbc, dt = jnp.split(zxbcdt, [SSD_D_INNER, SSD_D_INNER + SSD_CONV_DIM], axis=-1)
    xbc = jax.nn.silu(causal_depthwise_conv(xbc, conv_w, conv_b))
    xs, b_in, c_in = jnp.split(xbc, [SSD_D_INNER, SSD_D_INNER + SSD_GROUPS * SSD_STATE], axis=-1)
    dt = jax.nn.softplus(dt.astype(f32) + dt_bias.astype(f32))
    a = -jnp.exp(a_log.astype(f32))

    pad = (-L) % SSD_CHUNK
    n_chunks = (L + pad) // SSD_CHUNK

    def front_pad(t):
        return jnp.pad(t.astype(f32), [(0, 0), (pad, 0)] + [(0, 0)] * (t.ndim - 2))

    x_c = front_pad(xs).reshape(bsz, n_chunks, SSD_CHUNK, SSD_GROUPS, SSD_HPG, SSD_HEAD_DIM)
    b_c = front_pad(b_in).reshape(bsz, n_chunks, SSD_CHUNK, SSD_GROUPS, SSD_STATE)
    c_c = front_pad(c_in).reshape(bsz, n_chunks, SSD_CHUNK, SSD_GROUPS, SSD_STATE)
    dt_c = front_pad(dt).reshape(bsz, n_chunks, SSD_CHUNK, SSD_GROUPS, SSD_HPG)
    xdt = x_c * dt_c[..., None]
    a_dt = (dt_c * a.reshape(SSD_GROUPS, SSD_HPG)).transpose(0, 1, 3, 4, 2)
    a_cs = jnp.cumsum(a_dt, axis=-1)

    idx = jnp.arange(SSD_CHUNK)
    causal = idx[:, None] >= idx[None, :]
    decay = jnp.exp(jnp.where(causal, a_cs[..., :, None] - a_cs[..., None, :], -jnp.inf))
    cb = jnp.einsum('bclgn,bcsgn->bcgls', c_c, b_c)
    y_diag = jnp.einsum('bcgjls,bcsgjp->bclgjp', cb[:, :, :, None] * decay, xdt)

    decay_to_end = jnp.exp(a_cs[..., -1:] - a_cs).transpose(0, 1, 4, 2, 3)
    states = jnp.einsum('bclgn,bclgjp->bcgjpn', b_c, xdt * decay_to_end[..., None])
    chunk_decay = jnp.exp(a_cs[..., -1])

    def step(carry, inp):
        st, dec = inp
        return carry * dec[..., None, None] + st, carry

    init = jnp.zeros((bsz, SSD_GROUPS, SSD_HPG, SSD_HEAD_DIM, SSD_STATE), f32)
    _, prev = lax.scan(step, init, (jnp.moveaxis(states, 1, 0), jnp.moveaxis(chunk_decay, 1, 0)))
    prev = jnp.moveaxis(prev, 0, 1)
    decay_from_start = jnp.exp(a_cs).transpose(0, 1, 4, 2, 3)
    y_off = jnp.einsum('bclgn,bcgjpn->bclgjp', c_c, prev) * decay_from_start[..., None]

    y = (y_diag + y_off).reshape(bsz, n_chunks * SSD_CHUNK, SSD_D_INNER)[:, pad:]
    y = y + xs.astype(f32) * jnp.repeat(d_skip.astype(f32), SSD_HEAD_DIM)
    g = (y * jax.nn.silu(z.astype(f32))).reshape(bsz, L, SSD_GROUPS, SSD_D_INNER // SSD_GROUPS)
    g = g * lax.rsqrt(jnp.mean(g * g, axis=-1, keepdims=True) + EPS)
    g = g.reshape(bsz, L, SSD_D_INNER) * norm_g.astype(f32)
    return g.astype(h.dtype) @ w_out


def rope_tables(L):
    inv = 1.0 / (ROPE_THETA ** (jnp.arange(0, MLA_ROPE, 2, dtype=jnp.float32) / MLA_ROPE))
    ang = jnp.arange(L, dtype=jnp.float32)[:, None] * inv[None, :]
    return jnp.cos(ang)[None, :, None, :], jnp.sin(ang)[None, :, None, :]


def apply_rope(t, cos, sin):
    t1, t2 = jnp.split(t, 2, axis=-1)
    cos = cos.astype(t.dtype)
    sin = sin.astype(t.dtype)
    return jnp.concatenate([t1 * cos - t2 * sin, t1 * sin + t2 * cos], axis=-1)


def mla_mixer(h, w_in, q_a_g, w_q_b, kv_a_g, w_kv_b, q_norm_g, k_norm_g, w_out):
    bsz, L, _ = h.shape
    q_lat, kv_lat, k_pe = jnp.split(h @ w_in, [MLA_Q_RANK, MLA_Q_RANK + MLA_KV_RANK], axis=-1)
    q = (rms_norm(q_lat, q_a_g) @ w_q_b).reshape(bsz, L, MLA_HEADS, MLA_QK)
    kv = (rms_norm(kv_lat, kv_a_g) @ w_kv_b).reshape(bsz, L, MLA_HEADS, MLA_NOPE + MLA_V)
    k_nope, v = jnp.split(kv, [MLA_NOPE], axis=-1)
    k = jnp.concatenate(
        [k_nope, jnp.broadcast_to(k_pe[:, :, None, :], (bsz, L, MLA_HEADS, MLA_ROPE))], axis=-1)
    q = rms_norm(q, q_norm_g)
    k = rms_norm(k, k_norm_g)
    cos, sin = rope_tables(L)
    q = jnp.concatenate([q[..., :MLA_NOPE], apply_rope(q[..., MLA_NOPE:], cos, sin)], axis=-1)
    k = jnp.concatenate([k[..., :MLA_NOPE], apply_rope(k[..., MLA_NOPE:], cos, sin)], axis=-1)
    scale = MLA_QK ** -0.5

    blocks = [(0, N_META)] + [(s, min(s + ATTN_BLOCK, L)) for s in range(N_META, L, ATTN_BLOCK)]
    outs = []
    for s, e in blocks:
        sc = jnp.einsum('bqhd,bkhd->bhqk', q[:, s:e], k[:, :e]).astype(jnp.float32) * scale
        mask = jnp.arange(e)[None, :] <= jnp.arange(s, e)[:, None]
        p = jax.nn.softmax(jnp.where(mask, sc, -jnp.inf), axis=-1).astype(v.dtype)
        outs.append(jnp.einsum('bhqk,bkhd->bqhd', p, v[:, :e]))
    o = jnp.concatenate(outs, axis=1).reshape(bsz, L, MLA_HEADS * MLA_V)
    return o @ w_out


def sqrelu_mlp(h, w_up, w_down):
    return jnp.square(jax.nn.relu(h @ w_up)) @ w_down


def setup_inputs(seed: int = 0) -> dict:
    key = jax.random.key(seed)
    ks = jax.random.split(key, 24)
    f32 = jnp.float32

    def nrm(k, shape, fan_in):
        return jax.random.normal(k, shape, f32) * (fan_in ** -0.5)

    def gain(k, shape):
        return 1.0 + 0.02 * jax.random.normal(k, shape, f32)

    ns, nm = N_SSD_LAYERS, N_MLA_LAYERS
    dt0 = jnp.exp(jax.random.uniform(ks[7], (ns, SSD_HEADS), f32, np.log(1e-3), np.log(1e-1)))
    return {
        "x": jax.random.normal(ks[0], (BATCH, SEQ, D_MODEL), f32),
        "meta_tokens": jax.random.normal(ks[1], (N_META, D_MODEL), f32),
        "ln_mix": gain(ks[2], (DEPTH, D_MODEL)),
        "ln_mlp": gain(ks[3], (DEPTH, D_MODEL)),
        "ssd_w_in": nrm(ks[4], (ns, D_MODEL, SSD_IN_DIM), D_MODEL),
        "ssd_conv_w": nrm(ks[5], (ns, SSD_CONV, SSD_CONV_DIM), SSD_CONV),
        "ssd_conv_b": 0.02 * jax.random.normal(ks[6], (ns, SSD_CONV_DIM), f32),
        "ssd_dt_bias": dt0 + jnp.log(-jnp.expm1(-dt0)),
        "ssd_a_log": jnp.log(jax.random.uniform(ks[8], (ns, SSD_HEADS), f32, 1.0, 16.0)),
        "ssd_d": 1.0 + 0.1 * jax.random.normal(ks[9], (ns, SSD_HEADS), f32),
        "ssd_norm": gain(ks[10], (ns, SSD_D_INNER)),
        "ssd_w_out": nrm(ks[11], (ns, SSD_D_INNER, D_MODEL), SSD_D_INNER),
        "mla_w_in": nrm(ks[12], (nm, D_MODEL, MLA_IN_DIM), D_MODEL),
        "mla_q_a_norm": gain(ks[13], (nm, MLA_Q_RANK)),
        "mla_w_q_b": nrm(ks[14], (nm, MLA_Q_RANK, MLA_HEADS * MLA_QK), MLA_Q_RANK),
        "mla_kv_a_norm": gain(ks[15], (nm, MLA_KV_RANK)),
        "mla_w_kv_b": nrm(ks[16], (nm, MLA_KV_RANK, MLA_HEADS * (MLA_NOPE + MLA_V)), MLA_KV_RANK),
        "mla_q_norm": gain(ks[17], (nm, MLA_QK)),
        "mla_k_norm": gain(ks[18], (nm, MLA_QK)),
        "mla_w_out": nrm(ks[19], (nm, MLA_HEADS * MLA_V, D_MODEL), MLA_HEADS * MLA_V),
        "mlp_w_up": nrm(ks[20], (DEPTH, D_MODEL, D_FF), D_MODEL),
        "mlp_w_down": nrm(ks[21], (DEPTH, D_FF, D_MODEL), D_FF),
    }


def reference(x, meta_tokens, ln_mix, ln_mlp, ssd_w_in, ssd_conv_w, ssd_conv_b, ssd_dt_bias,
              ssd_a_log, ssd_d, ssd_norm, ssd_w_out, mla_w_in, mla_q_a_norm, mla_w_q_b,
              mla_kv_a_norm, mla_w_kv_b, mla_q_norm, mla_k_norm, mla_w_out, mlp_w_up, mlp_w_down):
    bsz = x.shape[0]
    meta = jnp.broadcast_to(meta_tokens[None].astype(x.dtype), (bsz, N_META, D_MODEL))
    h = jnp.concatenate([meta, x], axis=1)
    for i in range(DEPTH):
        j = i // 2
        hn = rms_norm(h, ln_mix[i])
        if i % 2 == 0:
            h = h + ssd_mixer(hn, ssd_w_in[j], ssd_conv_w[j], ssd_conv_b[j], ssd_dt_bias[j],
                              ssd_a_log[j], ssd_d[j], ssd_norm[j], ssd_w_out[j])
        else:
            h = h + mla_mixer(hn, mla_w_in[j], mla_q_a_norm[j], mla_w_q_b[j], mla_kv_a_norm[j],
                              mla_w_kv_b[j], mla_q_norm[j], mla_k_norm[j], mla_w_out[j])
        h = h + sqrelu_mlp(rms_norm(h, ln_mlp[i]), mlp_w_up[i], mlp_w_down[i])
    return h[:, N_META:]
```

```python
import contextlib
import numpy as np
import concourse.bass as bass
import concourse.mybir as mybir
from concourse.bass_utils import run_bass_kernel_spmd

F32, BF16 = mybir.dt.float32, mybir.dt.bfloat16
AF = mybir.ActivationFunctionType
ALU = mybir.AluOpType
AX = mybir.AxisListType

NT, NM, D, SEQ = 4112, 16, 1024, 4096
TILES = [(0, 16)] + [(16 + 128 * j, 128) for j in range(32)]
EPS = 1e-6
DFF = 4096
SSD_IN = 6176
NH_S = 32
MLA_H = 16
QK = 96


class Buf:
    __slots__ = ("w", "r", "name", "ps")

    def __init__(self, name="", ps=False):
        self.w = {}
        self.r = {}
        self.name = name
        self.ps = ps


def PB():
    return Buf(ps=True)


class Eng:
    def __init__(self, name, e, sem):
        self.name, self.e, self.sem, self.cnt, self.seen = name, e, sem, 0, {}


class KB:
    def __init__(self, nc, es):
        self.nc = nc
        mk = lambda n: es.enter_context(nc.semaphore(n))
        self.pe = Eng("pe", nc.tensor, mk("s_pe"))
        self.act = Eng("act", nc.scalar, mk("s_act"))
        self.dve = Eng("dve", nc.vector, mk("s_dve"))
        self.pool = Eng("pool", nc.gpsimd, mk("s_pool"))
        self.sp = Eng("sp", nc.sync, mk("s_sp"))
        self.engs = [self.pe, self.act, self.dve, self.pool, self.sp]
        self.dsem = {"sp": [[mk("d_sp%d" % i), 0] for i in range(24)],
                     "pool": [[mk("d_pl%d" % i), 0] for i in range(8)]}
        self.drr = {"sp": 0, "pool": 0}
        self.nins = 0

    def _wait(self, E, toks):
        for key, (sem, val) in toks.items():
            if E is self.pe and key == "pe":
                continue
            if E.seen.get(key, 0) >= val:
                continue
            E.e.wait_ge(sem, val)
            E.seen[key] = val

    @staticmethod
    def _add(need, d):
        for k, sv in d.items():
            if k not in need or need[k][1] < sv[1]:
                need[k] = sv

    def _deps(self, r, w, wp, ekey=None):
        need = {}
        for b in r:
            self._add(need, b.w)
            if b.ps:
                self._add(need, {k: v for k, v in b.r.items() if k != ekey})
        for b in w:
            self._add(need, b.w)
            self._add(need, b.r)
        for b in wp:
            self._add(need, b.r)
            if b.ps:
                self._add(need, b.w)
        return need

    def _reg(self, key, tok, r, w, wp):
        for b in r:
            if key not in b.r or b.r[key][1] < tok[1]:
                b.r[key] = tok
        for b in w:
            b.w = {key: tok}
            b.r = {}
        for b in wp:
            if key not in b.w or b.w[key][1] < tok[1]:
                b.w[key] = tok

    def op(self, E, fn, r=(), w=(), wp=(), inc=True):
        self._wait(E, self._deps(r, w, wp, E.name))
        ins = fn()
        self.nins += 1
        if inc:
            E.cnt += 1
            ins.then_inc(E.sem, 1)
            tok = (E.sem, E.cnt)
        else:
            tok = (E.sem, E.cnt + 1)
        self._reg(E.name, tok, r, w, wp)

    def dma(self, Q, out, in_, r=(), w=(), wp=()):
        E = self.sp if Q == "sp" else self.pool
        self._wait(E, self._deps(r, w, wp))
        lst = self.dsem[Q]
        i = self.drr[Q]
        self.drr[Q] = (i + 1) % len(lst)
        sem, cnt = lst[i]
        key = (Q, i)
        if cnt > 0 and E.seen.get(key, 0) < cnt:
            E.e.wait_ge(sem, cnt)
            E.seen[key] = cnt
        ins = E.e.dma_start(out=out, in_=in_)
        ins.then_inc(sem, 16)
        self.nins += 1
        lst[i][1] = cnt + 16
        self._reg(key, (sem, cnt + 16), r, w, wp)

    def barrier(self):
        toks = {}
        for E in self.engs:
            if E.cnt > 0:
                toks[E.name] = (E.sem, E.cnt)
        for Q, lst in self.dsem.items():
            for i, (sem, cnt) in enumerate(lst):
                if cnt > 0:
                    toks[(Q, i)] = (sem, cnt)
        for E in self.engs:
            self._wait(E, toks)


def bc(ap, shape, axis):
    return ap.unsqueeze(axis).to_broadcast(list(shape))


class Prog:
    def __init__(self, phases):
        self.phases = phases
        nc = bass.Bass("TRN2", target_bir_lowering=False)
        self.nc = nc
        di = lambda name, shape: nc.dram_tensor(name, list(shape), F32, kind="ExternalInput").ap()
        self.x = di("x", [SEQ, D])
        self.meta = di("meta", [NM, D])
        self.w_ssd_in = di("ssd_w_in", [2, D, SSD_IN])
        self.w_ssd_out = di("ssd_w_out", [2, 2048, D])
        self.w_mla_in = di("mla_w_in", [2, D, 672])
        self.w_mla_qb = di("mla_w_q_b", [2, 384, 1536])
        self.w_mla_kvb = di("mla_w_kv_b", [2, 256, 2048])
        self.w_mla_out = di("mla_w_out", [2, D, D])
        self.w_up = di("mlp_w_up", [4, D, DFF])
        self.w_dn = di("mlp_w_down", [4, DFF, D])
        self.lnT_d = di("lnT", [128, 8, 8])
        self.cw_d = di("cw", [2, 128, 32, 4])
        self.cb_d = di("cb", [2, 128, 32])
        self.dtb_d = di("dtb_rep", [2, 128, 32])
        self.alog_d = di("alog_rep", [2, 128, 32])
        self.dsk_d = di("dskip_rep", [2, 128, 32])
        self.sng_d = di("ssd_normT", [2, 128, 16])
        self.qag_d = di("q_a_T", [2, 128, 3])
        self.kvag_d = di("kv_a_T", [2, 128, 2])
        self.gq_d = di("gq_rep", [2, 128, 96])
        self.gk_d = di("gk_rep", [2, 128, 96])
        self.ident_d = di("ident", [128, 128])
        self.tri_d = di("tri", [128, 128])
        self.ltri_d = di("ltri", [128, 128])
        self.ones_d = di("ones", [128, 128])
        self.cos_d = di("cos", [NT, 16])
        self.sin_d = di("sin", [NT, 16])
        self.y = nc.dram_tensor("y", [SEQ, D], F32, kind="ExternalOutput").ap()
        self.hd = nc.dram_tensor("hd", [NT, D], F32, kind="Internal").ap()
        self.qT_d = nc.dram_tensor("qT_d", [MLA_H, QK, NT], BF16, kind="Internal").ap()
        self.kT_d = nc.dram_tensor("kT_d", [MLA_H, QK, NT], BF16, kind="Internal").ap()
        self.va_d = nc.dram_tensor("va_d", [NT, 8, 192], BF16, kind="Internal").ap()

        with contextlib.ExitStack() as es:
            self.kb = KB(nc, es)
            self.build(es)

    def S(self, es, name, shape, dt):
        self.uid = getattr(self, "uid", 0) + 1
        return es.enter_context(self.nc.sbuf_tensor("sb%d_%s" % (self.uid, name), list(shape), dt))

    def P(self, es, name, shape, dt):
        self.uid = getattr(self, "uid", 0) + 1
        return es.enter_context(self.nc.psum_tensor("ps%d_%s" % (self.uid, name), list(shape), dt))

    def h_src(self, kind, t0, n):
        if kind == "x":
            return self.meta[0:16, :] if t0 == 0 else self.x[t0 - 16:t0 - 16 + n, :]
        return self.hd[t0:t0 + n, :]

    def h_dst(self, kind, t0, n):
        if kind == "y":
            return None if t0 == 0 else self.y[t0 - 16:t0 - 16 + n, :]
        return self.hd[t0:t0 + n, :]

    def build(self, es):
        kb, nc = self.kb, self.nc
        self.ident = self.S(es, "ident", [128, 128], BF16)
        self.tri32 = self.S(es, "tri32", [128, 128], F32)
        self.tri16 = self.S(es, "tri16", [128, 128], BF16)
        self.ltri32 = self.S(es, "ltri32", [128, 128], F32)
        self.ones32 = self.S(es, "ones32", [128, 128], F32)
        self.lnT = self.S(es, "lnT", [128, 8, 8], F32)
        self.cb_ = Buf("consts")
        kb.dma("pool", self.ident[:], self.ident_d, wp=[self.cb_])
        kb.dma("pool", self.tri16[:], self.tri_d, wp=[self.cb_])
        kb.dma("sp", self.tri32[:], self.tri_d, wp=[self.cb_])
        kb.dma("sp", self.ltri32[:], self.ltri_d, wp=[self.cb_])
        kb.dma("sp", self.ones32[:], self.ones_d, wp=[self.cb_])
        kb.dma("sp", self.lnT[:], self.lnT_d, wp=[self.cb_])
        for ph in self.phases:
            kind = ph[0]
            with contextlib.ExitStack() as pes:
                if kind == "mlp":
                    self.phase_mlp(pes, *ph[1:])
                elif kind == "ssd":
                    self.phase_ssd(pes, *ph[1:])
                elif kind == "mla":
                    self.phase_mla(pes, *ph[1:])
                kb.barrier()
        kb.barrier()

    def norm_T(self, hb, h_ap, n, gain_ap, dst_ap, dst_b, sc):
        kb, nc = self.kb, self.nc
        junk, junk_b, ss, ss_b, xn, xn_b, tp, tp_b = sc
        kb.op(kb.act, lambda: nc.scalar.activation(out=junk[0:n, :], in_=h_ap, func=AF.Square, accum_out=ss[0:n, 0:1]),
              r=[hb], w=[junk_b, ss_b] if junk_b is not xn_b else [xn_b, ss_b])
        kb.op(kb.act, lambda: nc.scalar.activation(out=ss[0:n, 1:2], in_=ss[0:n, 0:1], func=AF.Ln, bias=EPS, scale=1.0 / D),
              r=[ss_b], wp=[ss_b])
        kb.op(kb.act, lambda: nc.scalar.activation(out=ss[0:n, 2:3], in_=ss[0:n, 1:2], func=AF.Exp, scale=-0.5),
              r=[ss_b], wp=[ss_b])
        kb.op(kb.dve, lambda: nc.vector.tensor_scalar(xn[0:n, :], h_ap, ss[0:n, 2:3], None, ALU.mult),
              r=[hb, ss_b], w=[xn_b])
        for c in range(8):
            kb.op(kb.pe, lambda c=c: nc.tensor.transpose(tp[:, c, 0:n], xn[0:n, c * 128:(c + 1) * 128], self.ident[0:n, 0:n]),
                  r=[xn_b, self.cb_], w=[tp_b] if c == 0 else [], wp=[tp_b] if c else [], inc=(c == 7))
        kb.op(kb.dve, lambda: nc.vector.tensor_tensor(dst_ap, tp[:, :, 0:n], bc(gain_ap, [128, 8, n], 2), ALU.mult),
              r=[tp_b, self.cb_], w=[dst_b])

    def norm_scratch(self, es, pfx, tp_ps):
        ss = self.S(es, pfx + "ss", [128, 4], F32)
        xn = self.S(es, pfx + "xn", [128, 1024], BF16)
        tp = tp_ps[:].bitcast(BF16)[:, 0:1024].rearrange("p (c t) -> p c t", c=8)
        xn_b = Buf()
        return (xn, xn_b, ss, Buf(), xn, xn_b, tp, PB())

    def phase_mlp(self, es, li, src, dst):
        kb, nc = self.kb, self.nc
        wup = self.S(es, "wup", [128, 8, DFF], BF16)
        wdn = self.S(es, "wdn", [128, 32, D], BF16)
        wup_b, wdn_b = Buf(), Buf()
        upv = self.w_up[li].rearrange("(c p) f -> p c f", p=128)
        dnv = self.w_dn[li].rearrange("(c p) d -> p c d", p=128)
        for c in range(8):
            kb.dma("pool", wup[:, c, :], upv[:, c, :], wp=[wup_b])
        for c in range(0, 32, 4):
            kb.dma("pool", wdn[:, c:c + 4, :], dnv[:, c:c + 4, :], wp=[wdn_b])
        NSLOT = 7
        hs = [self.S(es, "mh%d" % i, [128, D], F32) for i in range(NSLOT)]
        hs_b = [Buf() for _ in range(NSLOT)]
        xnT = self.S(es, "m_xnT", [128, 8, 512], BF16)
        xnT_b = Buf()
        uT = self.S(es, "m_uT", [128, 32, 512], BF16)
        uT_b = [Buf() for _ in range(32)]
        r32 = [self.S(es, "m_r32_%d" % i, [128, 512], F32) for i in range(2)]
        r32_b = [Buf(), Buf()]
        tp_ps = [self.P(es, "m_tp%d" % i, [128, 512], F32) for i in range(2)]
        pu = [self.P(es, "m_pu%d" % i, [128, 512], F32) for i in range(3)]
        pu_b = [PB() for _ in range(3)]
        pd = [self.P(es, "m_pd%d" % i, [128, 512], F32) for i in range(3)]
        pd_b = [PB() for _ in range(3)]
        nsc = [self.norm_scratch(es, "m%d" % i, tp_ps[i]) for i in range(2)]
        gain = self.lnT[:, 4 + li, :]
        groups = [[TILES[0]]] + [TILES[1 + 4 * g:5 + 4 * g] for g in range(8)]
        slot = 0
        iu = 0
        ipd = 0
        inorm = 0
        for grp in groups:
            ntok = sum(n for _, n in grp)
            myslots = []
            for (t0, n) in grp:
                s = slot % NSLOT
                slot += 1
                myslots.append(s)
                kb.dma("sp", hs[s][0:n, :], self.h_src(src, t0, n), w=[hs_b[s]])
            off = 0
            for (t0, n), s in zip(grp, myslots):
                self.norm_T(hs_b[s], hs[s][0:n, :], n, gain, xnT[:, :, off:off + n], xnT_b, nsc[inorm % 2])
                inorm += 1
                off += n
            for fc in range(32):
                p = iu % 3
                for c in range(8):
                    kb.op(kb.pe, lambda c=c, fc=fc, p=p: nc.tensor.matmul(pu[p][:, 0:ntok], wup[:, c, fc * 128:(fc + 1) * 128],
                                                                      xnT[:, c, 0:ntok], start=(c == 0), stop=(c == 7)),
                          r=[wup_b, xnT_b], w=[pu_b[p]] if c == 0 else [], wp=[pu_b[p]] if c else [], inc=(c == 7))
                rr = iu % 2
                kb.op(kb.act, lambda p=p, rr=rr: nc.scalar.activation(out=r32[rr][:, 0:ntok], in_=pu[p][:, 0:ntok], func=AF.Relu),
                      r=[pu_b[p]], w=[r32_b[rr]])
                E = kb.dve if (fc % 2 == 0) else kb.pool
                kb.op(E, lambda rr=rr, fc=fc, E=E: E.e.tensor_tensor(uT[:, fc, 0:ntok], r32[rr][:, 0:ntok], r32[rr][:, 0:ntok], ALU.mult),
                      r=[r32_b[rr]], w=[uT_b[fc]])
                iu += 1
            off = 0
            for (t0, n), s in zip(grp, myslots):
                for hh in range(2):
                    p = ipd % 3
                    ipd += 1
                    for fc in range(32):
                        kb.op(kb.pe, lambda fc=fc, p=p, off=off, n=n, hh=hh: nc.tensor.matmul(
                            pd[p][0:n, :], uT[:, fc, off:off + n], wdn[:, fc, hh * 512:(hh + 1) * 512],
                            start=(fc == 0), stop=(fc == 31)),
                            r=[uT_b[fc], wdn_b], w=[pd_b[p]] if fc == 0 else [], wp=[pd_b[p]] if fc else [], inc=(fc == 31))
                    kb.op(kb.dve, lambda p=p, n=n, hh=hh, s=s: nc.vector.tensor_tensor(
                        hs[s][0:n, hh * 512:(hh + 1) * 512], pd[p][0:n, :], hs[s][0:n, hh * 512:(hh + 1) * 512], ALU.add),
                        r=[pd_b[p], hs_b[s]], wp=[hs_b[s]])
                dst_ap = self.h_dst(dst, t0, n)
                if dst_ap is not None:
                    kb.dma("sp", dst_ap, hs[s][0:n, :], r=[hs_b[s]])
                off += n

    def phase_ssd(self, es, j, li, src, dst):
        kb, nc = self.kb, self.nc
        V, A, G, T = nc.vector, nc.scalar, nc.gpsimd, nc.tensor
        w_in = self.S(es, "s_win", [128, 8, SSD_IN], BF16)
        w_out = self.S(es, "s_wout", [128, 16, D], BF16)
        win_b, wout_b, par_b = Buf(), Buf(), Buf()
        wv = self.w_ssd_in[j].rearrange("(c p) f -> p c f", p=128)
        for c in range(8):
            kb.dma("pool", w_in[:, c, :], wv[:, c, :], wp=[win_b])
        wov = self.w_ssd_out[j].rearrange("(c p) f -> p c f", p=128)
        for c in range(0, 16, 4):
            kb.dma("pool", w_out[:, c:c + 4, :], wov[:, c:c + 4, :], wp=[wout_b])
        cw = self.S(es, "s_cw", [128, 32, 4], F32)
        cb = self.S(es, "s_cb", [128, 32], F32)
        dtb = self.S(es, "s_dtb", [128, 32], F32)
        arep = self.S(es, "s_arep", [128, 32], F32)
        dsk = self.S(es, "s_dsk", [128, 32], F32)
        sng = self.S(es, "s_sng", [128, 16], F32)
        kb.dma("sp", cw[:], self.cw_d[j], wp=[par_b])
        kb.dma("sp", cb[:], self.cb_d[j], wp=[par_b])
        kb.dma("sp", dtb[:], self.dtb_d[j], wp=[par_b])
        kb.dma("sp", arep[:], self.alog_d[j], wp=[par_b])
        kb.dma("sp", dsk[:], self.dsk_d[j], wp=[par_b])
        kb.dma("sp", sng[:], self.sng_d[j], wp=[par_b])
        arep_b = Buf()
        kb.op(kb.act, lambda: A.activation(out=arep[:], in_=arep[:], func=AF.Exp), r=[par_b], w=[arep_b])
        kb.op(kb.dve, lambda: V.tensor_scalar(arep[:], arep[:], -1.0, None, ALU.mult), r=[arep_b], w=[arep_b])
        hs = [self.S(es, "s_h%d" % i, [128, D], F32) for i in range(2)]
        hs_b = [Buf(), Buf()]
        tp2 = self.P(es, "s_tp2", [128, 1024], F32)
        pzbs = [self.P(es, "s_pz%d" % i, [128, 512], F32) for i in range(2)]
        zqb = self.P(es, "s_zq", [128, 512], F32)
        dTb = self.P(es, "s_dT", [128, 512], F32)
        miscb = self.P(es, "s_misc", [128, 512], F32)
        yb = self.P(es, "s_y", [128, 512], F32)
        pz_b = [PB(), PB()]
        _z = PB()
        zq_b = [_z, _z]
        dT_b = PB()
        misc_b = PB()
        pdt_b = pacs_b = ptot_b = pst_b = misc_b
        pcb_b = [misc_b, misc_b]
        ydg_b = yof_b = PB()
        nsc = self.norm_scratch(es, "s", tp2)
        tp_b = nsc[7]
        tp16 = tp2[:].bitcast(BF16)
        tpg = tp16.rearrange("p (c t) -> p c t", c=16)
        xnT = [self.S(es, "s_xnT%d" % i, [128, 8, 131], BF16) for i in range(2)]
        xnT_b = [Buf(), Buf()]
        for i in range(2):
            kb.op(kb.pool, lambda: G.memset(xnT[i][:], 0.0), w=[xnT_b[i]])
        xbcT = self.S(es, "s_xbcT", [128, 32, 128], BF16)
        xbc_b = [Buf() for _ in range(32)]
        acc = [self.S(es, "s_acc%d" % i, [128, 128], F32) for i in range(3)]
        acc_b = [Buf() for _ in range(3)]
        xs_tm = self.S(es, "s_xstm", [128, 2048], BF16)
        xs_b = Buf()
        B_tm = self.S(es, "s_Btm", [128, 1024], BF16)
        Btm_b = Buf()
        sz = self.S(es, "s_sz", [128, 2048], F32)
        sz_b = [Buf() for _ in range(8)]
        sm = self.S(es, "s_sm", [128, 10, 32], F32)
        DTV, EXPT, ADT, ACS, DFS, CD, DTE, TMP, W2 = range(9)
        sm_b = [Buf() for _ in range(10)]
        rhsD = [self.S(es, "s_rhsD0", [128, 512], F32)] * 2
        _b = Buf()
        rhsD_b = [_b, _b]
        Ee = [self.S(es, "s_E0", [128, 512], F32)] * 2
        _b = Buf()
        E_b = [_b, _b]
        CBm = [self.S(es, "s_CBm%d" % i, [128, 128], F32) for i in range(2)]
        CBm_b = [Buf(), Buf()]
        MT = [self.S(es, "s_MT%d" % i, [128, 512], BF16) for i in range(2)]
        MT_b = [Buf(), Buf()]
        xdt = [self.S(es, "s_xdt%d" % i, [128, 256], BF16) for i in range(2)]
        xdt_b = [Buf(), Buf()]
        xdtd = [self.S(es, "s_xdtd%d" % i, [128, 256], BF16) for i in range(2)]
        xdtd_b = [Buf(), Buf()]
        tt = [[self.S(es, "s_t%d_%d" % (k, i), [128, 256], F32) for i in range(2)] for k in range(3)]
        tt_b = [[Buf(), Buf()] for _ in range(3)]
        tt.append(tt[0])
        tt_b.append(tt_b[0])
        stg = self.S(es, "s_stg", [128, 8, 4], F32)
        stg_b = [Buf() for _ in range(8)]
        gn = self.S(es, "s_gn", [128, 2048], BF16)
        gn_b = Buf()
        gT = self.S(es, "s_gT", [128, 16, 128], BF16)
        gT_b = Buf()
        S32 = self.S(es, "s_S32", [128, 2048], F32)
        Sbf = self.S(es, "s_Sbf", [128, 2048], BF16)
        S32_b = [Buf() for _ in range(8)]
        Sbf_b = [Buf() for _ in range(8)]
        stmp = [self.S(es, "s_stmp0", [128, 256], F32)] * 2
        _b = Buf()
        stmp_b = [_b, _b]
        kb.op(kb.pool, lambda: G.memset(S32[:], 0.0), w=S32_b)
        kb.op(kb.pool, lambda: G.memset(Sbf[:], 0.0), w=Sbf_b)
        nprev = 0
        ipz = 0
        for ti, (t0, n) in enumerate(TILES):
            cur, prev = ti % 2, (ti + 1) % 2
            X = xnT[cur]
            kb.dma("sp", hs[cur][0:n, :], self.h_src(src, t0, n), w=[hs_b[cur]])
            self.norm_T(hs_b[cur], hs[cur][0:n, :], n, self.lnT[:, li, :], X[:, :, 3:3 + n], xnT_b[cur], nsc)
            kb.op(kb.pool, lambda: G.tensor_copy(X[:, :, 0:3], xnT[prev][:, :, nprev:nprev + 3]), r=[xnT_b[prev]], wp=[xnT_b[cur]])
            nprev = n
            for cc in range(32):
                p = ipz % 2
                ipz += 1
                pz = pzbs[p][:, 0:3 + n]
                for c in range(8):
                    kb.op(kb.pe, lambda: T.matmul(pz, w_in[:, c, 2048 + cc * 128:2048 + (cc + 1) * 128], X[:, c, 0:3 + n], start=(c == 0), stop=(c == 7)),
                          r=[win_b, xnT_b[cur]], w=[pz_b[p]] if c == 0 else [], wp=[pz_b[p]] if c else [], inc=(c == 7))
                a_ = acc[p][:, 0:n]
                kb.op(kb.act, lambda: A.activation(out=a_, in_=pz[:, 3:3 + n], func=AF.Identity, bias=cb[:, cc:cc + 1], scale=cw[:, cc, 3:4]),
                      r=[pz_b[p], par_b], w=[acc_b[p]])
                for k in range(3):
                    kb.op(kb.dve, lambda: V.scalar_tensor_tensor(a_, pz[:, k:k + n], cw[:, cc, k:k + 1], a_, ALU.mult, ALU.add),
                          r=[pz_b[p], par_b], w=[acc_b[p]])
                kb.op(kb.act, lambda: A.activation(out=xbcT[:, cc, 0:n], in_=a_, func=AF.Silu), r=[acc_b[p]], w=[xbc_b[cc]])
            for g in range(8):
                zs = g % 2
                zq = zqb[0:n, zs * 256:(zs + 1) * 256]
                for c in range(8):
                    kb.op(kb.pe, lambda: T.matmul(zq, X[:, c, 3:3 + n], w_in[:, c, g * 256:(g + 1) * 256], start=(c == 0), stop=(c == 7)),
                          r=[win_b, xnT_b[cur]], w=[zq_b[zs]] if c == 0 else [], wp=[zq_b[zs]] if c else [], inc=(c == 7))
                kb.op(kb.act, lambda: A.activation(out=sz[0:n, g * 256:(g + 1) * 256], in_=zq, func=AF.Silu), r=[zq_b[zs]], w=[sz_b[g]])
            for cc in range(16):
                kb.op(kb.pe, lambda: T.transpose(tp16[0:n, cc * 128:(cc + 1) * 128], xbcT[:, cc, 0:n], self.ident[:, :]),
                      r=[xbc_b[cc], self.cb_], w=[tp_b] if cc == 0 else [], wp=[tp_b] if cc else [], inc=(cc == 15))
            kb.op(kb.act, lambda: A.copy(xs_tm[0:n, 0:1024], tp16[0:n, 0:1024]), r=[tp_b], w=[xs_b])
            kb.op(kb.dve, lambda: V.tensor_copy(xs_tm[0:n, 1024:2048], tp16[0:n, 1024:2048]), r=[tp_b], wp=[xs_b])
            for g in range(8):
                kb.op(kb.pe, lambda: T.transpose(tp16[0:n, g * 128:(g + 1) * 128], xbcT[:, 16 + g, 0:n], self.ident[:, :]),
                      r=[xbc_b[16 + g], self.cb_], w=[tp_b] if g == 0 else [], wp=[tp_b] if g else [], inc=(g == 7))
            kb.op(kb.act, lambda: A.copy(B_tm[0:n, :], tp16[0:n, 0:1024]), r=[tp_b], w=[Btm_b])
            pdt, pacs, ptot = miscb[0:n, 0:32], miscb[0:n, 32:64], miscb[:, 64:96]
            for c in range(8):
                kb.op(kb.pe, lambda: T.matmul(pdt, X[:, c, 3:3 + n], w_in[:, c, 6144:6176], start=(c == 0), stop=(c == 7)),
                      r=[win_b, xnT_b[cur]], w=[pdt_b] if c == 0 else [], wp=[pdt_b] if c else [], inc=(c == 7))
            smv = lambda k: sm[0:n, k, :]
            kb.op(kb.dve, lambda: V.tensor_tensor(smv(DTV), pdt, dtb[0:n, :], ALU.add), r=[pdt_b, par_b], w=[sm_b[DTV]])
            kb.op(kb.act, lambda: A.activation(out=smv(EXPT), in_=smv(DTV), func=AF.Exp), r=[sm_b[DTV]], w=[sm_b[EXPT]])
            kb.op(kb.act, lambda: A.activation(out=smv(DTV), in_=smv(EXPT), func=AF.Ln, bias=1.0), r=[sm_b[EXPT]], w=[sm_b[DTV]])
            kb.op(kb.dve, lambda: V.tensor_tensor(smv(ADT), smv(DTV), arep[0:n, :], ALU.mult), r=[sm_b[DTV], arep_b], w=[sm_b[ADT]])
            kb.op(kb.pe, lambda: T.matmul(pacs, self.tri32[0:n, 0:n], smv(ADT), start=True, stop=True), r=[sm_b[ADT], self.cb_], w=[pacs_b])
            kb.op(kb.pe, lambda: T.matmul(ptot, self.ones32[0:n, :], smv(ADT), start=True, stop=True), r=[sm_b[ADT], self.cb_], w=[ptot_b])
            kb.op(kb.dve, lambda: V.tensor_copy(smv(ACS), pacs), r=[pacs_b], w=[sm_b[ACS]])
            kb.op(kb.act, lambda: A.activation(out=smv(DFS), in_=pacs, func=AF.Exp), r=[pacs_b], w=[sm_b[DFS]])
            kb.op(kb.act, lambda: A.activation(out=sm[:, CD, :], in_=ptot, func=AF.Exp), r=[ptot_b], w=[sm_b[CD]])
            kb.op(kb.dve, lambda: V.tensor_tensor(smv(TMP), ptot[0:n, :], smv(ACS), ALU.subtract), r=[ptot_b, sm_b[ACS]], w=[sm_b[TMP]])
            kb.op(kb.act, lambda: A.activation(out=smv(DTE), in_=smv(TMP), func=AF.Exp), r=[sm_b[TMP]], w=[sm_b[DTE]])
            kb.op(kb.dve, lambda: V.tensor_tensor(smv(W2), smv(DTV), smv(DTE), ALU.mult), r=[sm_b[DTV], sm_b[DTE]], w=[sm_b[W2]])
            for g in range(8):
                b2 = g % 2
                hsl = slice(4 * g, 4 * g + 4)
                gsl = slice(g * 256, (g + 1) * 256)
                v3 = lambda ap: ap.rearrange("p (j l) -> p j l", j=4)
                rD = rhsD[b2][0:n, 0:4 * n]
                kb.op(kb.pool, lambda: G.tensor_tensor(v3(rD), bc(sm[0:n, ADT, hsl], [n, 4, n], 2), bc(self.tri32[0:n, 0:n], [n, 4, n], 1), ALU.mult),
                      r=[sm_b[ADT], self.cb_], w=[rhsD_b[b2]])
                kb.op(kb.pe, lambda: T.matmul(dTb[0:n, 0:4 * n], self.ltri32[0:n, 0:n], rD, start=True, stop=True), r=[rhsD_b[b2], self.cb_], w=[dT_b])
                Ev = Ee[b2][0:n, 0:4 * n]
                kb.op(kb.act, lambda: A.activation(out=Ev, in_=dTb[0:n, 0:4 * n], func=AF.Exp), r=[dT_b], w=[E_b[b2]])
                pcb = miscb[0:n, 128:128 + n]
                kb.op(kb.pe, lambda: T.matmul(pcb, xbcT[:, 16 + g, 0:n], xbcT[:, 24 + g, 0:n], start=True, stop=True),
                      r=[xbc_b[16 + g], xbc_b[24 + g]], w=[pcb_b[b2]])
                kb.op(kb.dve, lambda: V.tensor_tensor(CBm[b2][0:n, 0:n], pcb, self.tri32[0:n, 0:n], ALU.mult), r=[pcb_b[b2], self.cb_], w=[CBm_b[b2]])
                MTv = MT[b2][0:n, 0:4 * n]
                kb.op(kb.pool, lambda: G.tensor_tensor(v3(MTv), v3(Ev), bc(CBm[b2][0:n, 0:n], [n, 4, n], 1), ALU.mult),
                      r=[E_b[b2], CBm_b[b2]], w=[MT_b[b2]])
                xs3 = xs_tm[0:n, gsl].rearrange("p (j f) -> p j f", j=4)
                x3 = lambda ap: ap.rearrange("p (j f) -> p j f", j=4)
                kb.op(kb.pool, lambda: G.tensor_tensor(x3(xdt[b2][0:n, :]), xs3, bc(sm[0:n, DTV, hsl], [n, 4, 64], 2), ALU.mult),
                      r=[xs_b, sm_b[DTV]], w=[xdt_b[b2]])
                kb.op(kb.pool, lambda: G.tensor_tensor(x3(xdtd[b2][0:n, :]), xs3, bc(sm[0:n, W2, hsl], [n, 4, 64], 2), ALU.mult),
                      r=[xs_b, sm_b[W2]], w=[xdtd_b[b2]])
                for jj in range(4):
                    kb.op(kb.pe, lambda: T.matmul(yb[0:n, jj * 64:(jj + 1) * 64], MTv[:, jj * n:(jj + 1) * n], xdt[b2][0:n, jj * 64:(jj + 1) * 64], start=True, stop=True),
                          r=[MT_b[b2], xdt_b[b2]], w=[ydg_b] if jj == 0 else [], wp=[ydg_b] if jj else [], inc=(jj == 3))
                kb.op(kb.pe, lambda: T.matmul(yb[0:n, 256:512], xbcT[:, 24 + g, 0:n], Sbf[:, gsl], start=True, stop=True),
                      r=[xbc_b[24 + g], Sbf_b[g]], w=[yof_b])
                t1, t2, t3, tj = [tt[k][b2][0:n, :] for k in range(4)]
                kb.op(kb.dve, lambda: V.tensor_tensor(x3(t1), x3(yb[0:n, 256:512]), bc(sm[0:n, DFS, hsl], [n, 4, 64], 2), ALU.mult),
                      r=[yof_b, sm_b[DFS]], w=[tt_b[0][b2]])
                kb.op(kb.dve, lambda: V.tensor_tensor(t2, yb[0:n, 0:256], t1, ALU.add), r=[ydg_b, tt_b[0][b2]], w=[tt_b[1][b2]])
                kb.op(kb.pool, lambda: G.tensor_tensor(x3(t3), xs3, bc(dsk[0:n, hsl], [n, 4, 64], 2), ALU.mult), r=[xs_b, par_b], w=[tt_b[2][b2]])
                kb.op(kb.pool, lambda: G.tensor_tensor(t2, t2, t3, ALU.add), r=[tt_b[2][b2]], w=[tt_b[1][b2]])
                kb.op(kb.pool, lambda: G.tensor_tensor(t2, t2, sz[0:n, gsl], ALU.mult), r=[sz_b[g]], w=[tt_b[1][b2]])
                kb.op(kb.act, lambda: A.activation(out=tj, in_=t2, func=AF.Square, accum_out=stg[0:n, g, 0:1]), r=[tt_b[1][b2]], w=[tt_b[3][b2], stg_b[g]])
                self.rstd_ops(stg[:, g, :], stg_b[g], n, 0, 1.0 / 256)
                kb.op(kb.dve, lambda: V.tensor_scalar(gn[0:n, gsl], t2, stg[0:n, g, 2:3], None, ALU.mult), r=[tt_b[1][b2], stg_b[g]],
                      w=[gn_b] if g == 0 else [], wp=[gn_b] if g else [])
                kb.op(kb.pe, lambda: T.matmul(miscb[:, 256:512], B_tm[0:n, g * 128:(g + 1) * 128], xdtd[b2][0:n, :], start=True, stop=True),
                      r=[Btm_b, xdtd_b[b2]], w=[pst_b])
                kb.op(kb.pool, lambda: G.tensor_tensor(x3(stmp[b2][:, :]), x3(S32[:, gsl]), bc(sm[:, CD, hsl], [128, 4, 64], 2), ALU.mult),
                      r=[S32_b[g], sm_b[CD]], w=[stmp_b[b2]])
                kb.op(kb.dve, lambda: V.tensor_tensor(S32[:, gsl], stmp[b2][:, :], miscb[:, 256:512], ALU.add), r=[stmp_b[b2], pst_b], w=[S32_b[g]])
                kb.op(kb.act, lambda: A.copy(Sbf[:, gsl], S32[:, gsl]), r=[S32_b[g]], w=[Sbf_b[g]])
            for cc in range(16):
                kb.op(kb.pe, lambda: T.transpose(tpg[:, cc, 0:n], gn[0:n, cc * 128:(cc + 1) * 128], self.ident[0:n, 0:n]),
                      r=[gn_b, self.cb_], w=[tp_b] if cc == 0 else [], wp=[tp_b] if cc else [], inc=(cc == 15))
            kb.op(kb.dve, lambda: V.tensor_tensor(gT[:, :, 0:n], tpg[:, :, 0:n], bc(sng[:, :], [128, 16, n], 2), ALU.mult), r=[tp_b, par_b], w=[gT_b])
            for hh in range(2):
                po = dTb[0:n, :] if hh == 0 else yb[0:n, :]
                pbufs = [dT_b] if hh == 0 else [ydg_b]
                for cc in range(16):
                    kb.op(kb.pe, lambda: T.matmul(po, gT[:, cc, 0:n], w_out[:, cc, hh * 512:(hh + 1) * 512], start=(cc == 0), stop=(cc == 15)),
                          r=[gT_b, wout_b], w=pbufs if cc == 0 else [], wp=pbufs if cc else [], inc=(cc == 15))
                kb.op(kb.dve, lambda: V.tensor_tensor(hs[cur][0:n, hh * 512:(hh + 1) * 512], po, hs[cur][0:n, hh * 512:(hh + 1) * 512], ALU.add),
                      r=pbufs + [hs_b[cur]], wp=[hs_b[cur]])
            dap = self.h_dst(dst, t0, n)
            if dap is not None:
                kb.dma("sp", dap, hs[cur][0:n, :], r=[hs_b[cur]])

    def rstd_ops(self, st, st_b, n, c0, inv_n):
        kb, nc = self.kb, self.nc
        kb.op(kb.act, lambda: nc.scalar.activation(out=st[0:n, c0 + 1:c0 + 2], in_=st[0:n, c0:c0 + 1], func=AF.Ln, bias=EPS, scale=inv_n),
              r=[st_b], wp=[st_b])
        kb.op(kb.act, lambda: nc.scalar.activation(out=st[0:n, c0 + 2:c0 + 3], in_=st[0:n, c0 + 1:c0 + 2], func=AF.Exp, scale=-0.5),
              r=[st_b], wp=[st_b])

    def phase_mla(self, es, j, li, src, dst):
        kb, nc = self.kb, self.nc
        V, A, G, T = nc.vector, nc.scalar, nc.gpsimd, nc.tensor
        oT = self.S(es, "oT", [128, 8, NT], BF16)
        oT_b = Buf()
        w_o = self.S(es, "w_o", [128, 8, D], BF16)
        w_o_b = Buf()
        gq = self.S(es, "gq", [128, 96], F32)
        gk = self.S(es, "gk", [128, 96], F32)
        par_b = Buf()
        kb.dma("sp", gq[:], self.gq_d[j], wp=[par_b])
        kb.dma("sp", gk[:], self.gk_d[j], wp=[par_b])
        with contextlib.ExitStack() as s1:
            w_in = self.S(s1, "a_win", [128, 8, 672], BF16)
            w_qb = self.S(s1, "a_wqb", [128, 3, 1536], BF16)
            w_kvb = self.S(s1, "a_wkvb", [128, 2, 2048], BF16)
            wb = Buf()
            kb.dma("pool", w_in[:], self.w_mla_in[j].rearrange("(c p) f -> p c f", p=128), wp=[wb])
            kb.dma("pool", w_qb[:], self.w_mla_qb[j].rearrange("(c p) f -> p c f", p=128), wp=[wb])
            kb.dma("pool", w_kvb[:], self.w_mla_kvb[j].rearrange("(c p) f -> p c f", p=128), wp=[wb])
            kb.dma("pool", w_o[:], self.w_mla_out[j].rearrange("(c p) f -> p c f", p=128), wp=[w_o_b])
            qag = self.S(s1, "a_qag", [128, 3], F32)
            kvag = self.S(s1, "a_kvag", [128, 2], F32)
            kb.dma("sp", qag[:], self.qag_d[j], wp=[par_b])
            kb.dma("sp", kvag[:], self.kvag_d[j], wp=[par_b])
            hs = [self.S(s1, "a_h%d" % i, [128, D], F32) for i in range(2)]
            hs_b = [Buf(), Buf()]
            tp_ps = self.P(s1, "a_tp", [128, 512], F32)
            latA = self.P(s1, "a_latA", [128, 512], F32)
            latB = self.P(s1, "a_latB", [128, 512], F32)
            big = self.P(s1, "a_big", [128, 2048], F32)
            tph_ps = self.P(s1, "a_tph", [128, 512], F32)
            latA_b, latB_b, big_b, tph_b = PB(), PB(), PB(), PB()
            nsc = self.norm_scratch(s1, "a", tp_ps)
            tp5 = tp_ps[:].bitcast(BF16)[:, 0:640].rearrange("p (c t) -> p c t", c=5)
            tp_b = nsc[7]
            tph = tph_ps[:].bitcast(BF16)[:, 0:1024].rearrange("p (c t) -> p c t", c=8)
            xnT = self.S(s1, "a_xnT", [128, 8, 128], BF16)
            xnT_b = Buf()
            st = self.S(s1, "a_st", [128, 8], F32)
            st_b = Buf()
            qln = self.S(s1, "a_qln", [128, 384], BF16)
            kvln = self.S(s1, "a_kvln", [128, 256], BF16)
            kpe = self.S(s1, "a_kpe", [128, 32], F32)
            ln_b = Buf()
            sqj = self.S(s1, "a_sqj", [128, 2048], F32)
            sqj_b = Buf()
            qlT = self.S(s1, "a_qlT", [128, 3, 128], BF16)
            kvlT = self.S(s1, "a_kvlT", [128, 2, 128], BF16)
            lT_b = Buf()
            raw = self.S(s1, "a_raw", [128, 2048], F32)
            raw_b = Buf()
            s16 = self.S(s1, "a_s16", [128, 3, 16], F32)
            s16_b = Buf()
            cs = [self.S(s1, "a_cs%d" % i, [128, 2, 16], F32) for i in range(2)]
            cs_b = [Buf(), Buf()]
            rt = self.S(s1, "a_rt", [128, 4, 256], F32)
            rt_b = [Buf() for _ in range(4)]
            kpg = self.S(s1, "a_kpg", [128, 2, 32], F32)
            kpg_b = Buf()
            qbf = self.S(s1, "a_qbf", [128, 1536], BF16)
            qbf_b = Buf()
            stg = [self.S(s1, "a_stg%d" % i, [96, 16, 128], BF16) for i in range(2)]
            stg_b = [Buf(), Buf()]
            vst = [self.S(s1, "a_vst%d" % i, [128, 8, 192], BF16) for i in range(2)]
            vst_b = [Buf(), Buf()]
            for i in range(2):
                kb.op(kb.pool, lambda: G.memset(vst[i][:], 1.0), w=[vst_b[i]])
            qTv = self.qT_d.rearrange("h p t -> p h t")
            kTv = self.kT_d.rearrange("h p t -> p h t")
            istg = 0

            def head_T(src_bf, dstv, t0, n):
                nonlocal istg
                sg = istg % 2
                istg += 1
                for half in range(2):
                    for hh in range(8):
                        h = half * 8 + hh
                        kb.op(kb.pe, lambda: T.transpose(tph[0:96, hh, 0:n], src_bf[0:n, h * 96:(h + 1) * 96], self.ident[0:n, 0:n]),
                              r=[qbf_b, self.cb_], w=[tph_b] if hh == 0 else [], wp=[tph_b] if hh else [], inc=(hh == 7))
                    kb.op(kb.act, lambda: A.copy(stg[sg][0:96, half * 8:(half + 1) * 8, 0:n], tph[0:96, :, 0:n]),
                          r=[tph_b], w=[stg_b[sg]] if half == 0 else [], wp=[stg_b[sg]] if half else [])
                kb.dma("sp", dstv[:, :, t0:t0 + n], stg[sg][0:96, :, 0:n], r=[stg_b[sg]])

            def rope(t1, t2, cosb, sinb, o1, o2, shape_n, rb, wb_, rd_bufs):
                a_, b_, c_, d_ = [rt[0:shape_n[0], i, 0:shape_n[1]] for i in range(4)]
                if len(shape_n) == 3:
                    a_, b_, c_, d_ = [rt[0:shape_n[0], i, :].rearrange("p (h f) -> p h f", h=16) for i in range(4)]
                kb.op(kb.dve, lambda: V.tensor_tensor(a_, t1, cosb, ALU.mult), r=rd_bufs, w=[rt_b[0]])
                kb.op(kb.pool, lambda: G.tensor_tensor(b_, t2, sinb, ALU.mult), r=rd_bufs, w=[rt_b[1]])
                kb.op(kb.dve, lambda: V.tensor_tensor(c_, t1, sinb, ALU.mult), r=rd_bufs, w=[rt_b[2]])
                kb.op(kb.pool, lambda: G.tensor_tensor(d_, t2, cosb, ALU.mult), r=rd_bufs, w=[rt_b[3]])
                kb.op(kb.dve, lambda: V.tensor_tensor(o1, a_, b_, ALU.subtract), r=[rt_b[0], rt_b[1]], wp=[wb_])
                kb.op(kb.pool, lambda: G.tensor_tensor(o2, c_, d_, ALU.add), r=[rt_b[2], rt_b[3]], wp=[wb_])

            for ti, (t0, n) in enumerate(TILES):
                s = ti % 2
                kb.dma("sp", hs[s][0:n, :], self.h_src(src, t0, n), w=[hs_b[s]])
                kb.dma("sp", cs[s][0:n, 0, :], self.cos_d[t0:t0 + n, :], w=[cs_b[s]])
                kb.dma("sp", cs[s][0:n, 1, :], self.sin_d[t0:t0 + n, :], wp=[cs_b[s]])
                self.norm_T(hs_b[s], hs[s][0:n, :], n, self.lnT[:, li, :], xnT[:, :, 0:n], xnT_b, nsc)
                for c in range(8):
                    kb.op(kb.pe, lambda: T.matmul(latA[0:n, 0:384], xnT[:, c, 0:n], w_in[:, c, 0:384], start=(c == 0), stop=(c == 7)),
                          r=[xnT_b, wb], w=[latA_b] if c == 0 else [], wp=[latA_b] if c else [], inc=(c == 7))
                for c in range(8):
                    kb.op(kb.pe, lambda: T.matmul(latB[0:n, 0:288], xnT[:, c, 0:n], w_in[:, c, 384:672], start=(c == 0), stop=(c == 7)),
                          r=[xnT_b, wb], w=[latB_b] if c == 0 else [], wp=[latB_b] if c else [], inc=(c == 7))
                kb.op(kb.act, lambda: A.activation(out=sqj[0:n, 0:384], in_=latA[0:n, 0:384], func=AF.Square, accum_out=st[0:n, 0:1]),
                      r=[latA_b], w=[sqj_b, st_b])
                kb.op(kb.act, lambda: A.activation(out=sqj[0:n, 0:256], in_=latB[0:n, 0:256], func=AF.Square, accum_out=st[0:n, 3:4]),
                      r=[latB_b], w=[sqj_b], wp=[st_b])
                self.rstd_ops(st, st_b, n, 0, 1.0 / 384)
                self.rstd_ops(st, st_b, n, 3, 1.0 / 256)
                kb.op(kb.dve, lambda: V.tensor_scalar(qln[0:n, :], latA[0:n, 0:384], st[0:n, 2:3], None, ALU.mult), r=[latA_b, st_b], w=[ln_b])
                kb.op(kb.dve, lambda: V.tensor_scalar(kvln[0:n, :], latB[0:n, 0:256], st[0:n, 5:6], None, ALU.mult), r=[latB_b, st_b], wp=[ln_b])
                kb.op(kb.act, lambda: A.copy(kpe[0:n, :], latB[0:n, 256:288]), r=[latB_b], wp=[ln_b])
                for c in range(5):
                    srcap = qln[0:n, c * 128:(c + 1) * 128] if c < 3 else kvln[0:n, (c - 3) * 128:(c - 2) * 128]
                    kb.op(kb.pe, lambda: T.transpose(tp5[:, c, 0:n], srcap, self.ident[0:n, 0:n]),
                          r=[ln_b, self.cb_], w=[tp_b] if c == 0 else [], wp=[tp_b] if c else [], inc=(c == 4))
                kb.op(kb.dve, lambda: V.tensor_tensor(qlT[:, :, 0:n], tp5[:, 0:3, 0:n], bc(qag[:, :], [128, 3, n], 2), ALU.mult),
                      r=[tp_b, par_b], w=[lT_b])
                kb.op(kb.dve, lambda: V.tensor_tensor(kvlT[:, :, 0:n], tp5[:, 3:5, 0:n], bc(kvag[:, :], [128, 2, n], 2), ALU.mult),
                      r=[tp_b, par_b], wp=[lT_b])
                for ct in range(3):
                    for kc in range(3):
                        kb.op(kb.pe, lambda: T.matmul(big[0:n, ct * 512:(ct + 1) * 512], qlT[:, kc, 0:n], w_qb[:, kc, ct * 512:(ct + 1) * 512],
                                                      start=(kc == 0), stop=(kc == 2)),
                              r=[lT_b, wb], w=[big_b] if (ct == 0 and kc == 0) else [], wp=[] if (ct == 0 and kc == 0) else [big_b],
                              inc=(kc == 2 and ct == 2))
                for ct in range(3):
                    kb.op(kb.act, lambda: A.copy(raw[0:n, ct * 512:(ct + 1) * 512], big[0:n, ct * 512:(ct + 1) * 512]),
                          r=[big_b], w=[raw_b] if ct == 0 else [], wp=[raw_b] if ct else [])
                raw3 = raw[0:n, 0:1536].rearrange("p (h f) -> p h f", h=16)
                sq3 = sqj[0:n, 0:1536].rearrange("p (h f) -> p h f", h=16)
                kb.op(kb.pool, lambda: G.tensor_tensor(sqj[0:n, 0:1536], raw[0:n, 0:1536], raw[0:n, 0:1536], ALU.mult), r=[raw_b], w=[sqj_b])
                kb.op(kb.dve, lambda: V.tensor_reduce(s16[0:n, 0, :], sq3, AX.X, ALU.add), r=[sqj_b], w=[s16_b])
                kb.op(kb.act, lambda: A.activation(out=s16[0:n, 1, :], in_=s16[0:n, 0, :], func=AF.Ln, bias=EPS, scale=1.0 / 96), r=[s16_b], wp=[s16_b])
                kb.op(kb.act, lambda: A.activation(out=s16[0:n, 2, :], in_=s16[0:n, 1, :], func=AF.Exp, scale=-0.5), r=[s16_b], wp=[s16_b])
                kb.op(kb.dve, lambda: V.tensor_tensor(raw3, raw3, bc(s16[0:n, 2, :], [n, 16, 96], 2), ALU.mult), r=[s16_b], w=[raw_b])
                kb.op(kb.pool, lambda: G.tensor_tensor(raw3, raw3, bc(gq[0:n, :], [n, 16, 96], 1), ALU.mult), r=[par_b], w=[raw_b])
                qb3 = qbf[0:n, :].rearrange("p (h f) -> p h f", h=16)
                cosb = bc(cs[s][0:n, 0, :], [n, 16, 16], 1)
                sinb = bc(cs[s][0:n, 1, :], [n, 16, 16], 1)
                kb.op(kb.act, lambda: A.copy(qb3[:, :, 0:64], raw3[:, :, 0:64]), r=[raw_b], w=[qbf_b])
                rope(raw3[:, :, 64:80], raw3[:, :, 80:96], cosb, sinb, qb3[:, :, 64:80], qb3[:, :, 80:96], (n, 16, 16), None, qbf_b, [raw_b, cs_b[s]])
                head_T(qbf, qTv, t0, n)
                for ct in range(4):
                    for kc in range(2):
                        kb.op(kb.pe, lambda: T.matmul(big[0:n, ct * 512:(ct + 1) * 512], kvlT[:, kc, 0:n], w_kvb[:, kc, ct * 512:(ct + 1) * 512],
                                                      start=(kc == 0), stop=(kc == 1)),
                              r=[lT_b, wb], w=[big_b] if (ct == 0 and kc == 0) else [], wp=[] if (ct == 0 and kc == 0) else [big_b],
                              inc=(kc == 1 and ct == 3))
                for ct in range(4):
                    kb.op(kb.act, lambda: A.copy(raw[0:n, ct * 512:(ct + 1) * 512], big[0:n, ct * 512:(ct + 1) * 512]),
                          r=[big_b], w=[raw_b] if ct == 0 else [], wp=[raw_b] if ct else [])
                kv4 = raw[0:n, :].rearrange("p (h f) -> p h f", h=16)
                sqk = sqj[0:n, 0:1024].rearrange("p (h f) -> p h f", h=16)
                kb.op(kb.pool, lambda: G.tensor_tensor(sqk, kv4[:, :, 0:64], kv4[:, :, 0:64], ALU.mult), r=[raw_b], w=[sqj_b])
                kb.op(kb.dve, lambda: V.tensor_reduce(s16[0:n, 0, :], sqk, AX.X, ALU.add), r=[sqj_b], w=[s16_b])
                kb.op(kb.act, lambda: A.activation(out=kpg[0:n, 1, :], in_=kpe[0:n, :], func=AF.Square, accum_out=st[0:n, 6:7]),
                      r=[ln_b], w=[kpg_b], wp=[st_b])
                kb.op(kb.dve, lambda: V.tensor_scalar(s16[0:n, 0, :], s16[0:n, 0, :], st[0:n, 6:7], None, ALU.add), r=[st_b, s16_b], wp=[s16_b])
                kb.op(kb.act, lambda: A.activation(out=s16[0:n, 1, :], in_=s16[0:n, 0, :], func=AF.Ln, bias=EPS, scale=1.0 / 96), r=[s16_b], wp=[s16_b])
                kb.op(kb.act, lambda: A.activation(out=s16[0:n, 2, :], in_=s16[0:n, 1, :], func=AF.Exp, scale=-0.5), r=[s16_b], wp=[s16_b])
                vs = ti % 2
                kv5 = raw[0:n, :].rearrange("p (c e f) -> p c e f", c=8, e=2)
                kb.op(kb.act, lambda: A.copy(vst[vs][0:n, :, 0:64], kv5[:, :, 0, 64:128]), r=[raw_b, vst_b[vs]], wp=[vst_b[vs]])
                kb.op(kb.dve, lambda: V.tensor_copy(vst[vs][0:n, :, 128:192], kv5[:, :, 1, 64:128]), r=[raw_b, vst_b[vs]], wp=[vst_b[vs]])
                kb.dma("sp", self.va_d[t0:t0 + n, :, :], vst[vs][0:n, :, :], r=[vst_b[vs]])
                kb.op(kb.dve, lambda: V.tensor_tensor(kv4[:, :, 0:64], kv4[:, :, 0:64], bc(s16[0:n, 2, :], [n, 16, 64], 2), ALU.mult),
                      r=[s16_b], w=[raw_b])
                kb3 = qbf[0:n, :].rearrange("p (h f) -> p h f", h=16)
                kb.op(kb.pool, lambda: G.tensor_tensor(kb3[:, :, 0:64], kv4[:, :, 0:64], bc(gk[0:n, 0:64], [n, 16, 64], 1), ALU.mult),
                      r=[raw_b, par_b], w=[qbf_b])
                kb.op(kb.dve, lambda: V.tensor_tensor(kpg[0:n, 0, :], kpe[0:n, :], gk[0:n, 64:96], ALU.mult), r=[ln_b, par_b], w=[kpg_b])
                rope(kpg[0:n, 0, 0:16], kpg[0:n, 0, 16:32], cs[s][0:n, 0, :], cs[s][0:n, 1, :], kpg[0:n, 1, 0:16], kpg[0:n, 1, 16:32],
                     (n, 16), None, kpg_b, [kpg_b, cs_b[s]])
                kb.op(kb.dve, lambda: V.tensor_tensor(kb3[:, :, 64:96], bc(kpg[0:n, 1, :], [n, 16, 32], 1), bc(s16[0:n, 2, :], [n, 16, 32], 2), ALU.mult),
                      r=[kpg_b, s16_b], wp=[qbf_b])
                head_T(qbf, kTv, t0, n)
            kb.barrier()
        with contextlib.ExitStack() as s2:
            qh = [self.S(s2, "b_q%d" % i, [96, NT], BF16) for i in range(2)]
            kh = [self.S(s2, "b_k%d" % i, [96, NT], BF16) for i in range(2)]
            qk_b = [Buf(), Buf()]
            va = [self.S(s2, "b_va%d" % i, [128, 33, 192], BF16) for i in range(2)]
            va_b = [Buf(), Buf()]
            pT = [self.S(s2, "b_pT%d" % i, [128, 512], BF16) for i in range(4)]
            pT_b = [Buf() for _ in range(4)]
            rden = [self.S(s2, "b_rd%d" % i, [128, 512], F32) for i in range(2)]
            rdsh = [self.S(s2, "b_rs%d" % i, [128, 512], F32) for i in range(2)]
            rd_b = [Buf(), Buf()]
            rs_b = [Buf(), Buf()]
            bnd = self.S(s2, "b_bnd", [128, 8], F32)
            bnd_b = Buf()
            ps = [self.P(s2, "b_ps%d" % i, [128, 512], F32) for i in range(4)]
            ps_b = [PB() for _ in range(4)]
            po = [self.P(s2, "b_po%d" % i, [128, 512], F32) for i in range(2)]
            po_b = [PB(), PB()]
            kb.op(kb.dve, lambda: V.tensor_reduce(bnd[:, 0:1], gq[:, :], AX.X, ALU.max), r=[par_b], w=[bnd_b])
            kb.op(kb.dve, lambda: V.tensor_reduce(bnd[:, 1:2], gq[:, :], AX.X, ALU.min), r=[par_b], wp=[bnd_b])
            kb.op(kb.dve, lambda: V.tensor_reduce(bnd[:, 2:3], gk[:, :], AX.X, ALU.max), r=[par_b], wp=[bnd_b])
            kb.op(kb.dve, lambda: V.tensor_reduce(bnd[:, 3:4], gk[:, :], AX.X, ALU.min), r=[par_b], wp=[bnd_b])
            kb.op(kb.dve, lambda: V.scalar_tensor_tensor(bnd[:, 4:5], bnd[:, 1:2], -1.0, bnd[:, 0:1], ALU.mult, ALU.max), r=[bnd_b], wp=[bnd_b])
            kb.op(kb.dve, lambda: V.scalar_tensor_tensor(bnd[:, 5:6], bnd[:, 3:4], -1.0, bnd[:, 2:3], ALU.mult, ALU.max), r=[bnd_b], wp=[bnd_b])
            kb.op(kb.dve, lambda: V.scalar_tensor_tensor(bnd[:, 6:7], bnd[:, 4:5], -float(np.sqrt(96.0)), bnd[:, 5:6], ALU.mult, ALU.mult),
                  r=[bnd_b], wp=[bnd_b])
            negB = bnd[:, 6:7]
            scale = float(96.0 ** -0.5)

            def load_head(h):
                b = h % 2
                kb.dma("sp", qh[b][:, :], self.qT_d[h], w=[qk_b[b]])
                kb.dma("sp", kh[b][:, :], self.kT_d[h], wp=[qk_b[b]])

            def load_pair(c):
                b = c % 2
                kb.dma("sp", va[b][0:16, 0, :], self.va_d[0:16, c, :], w=[va_b[b]])
                vv = self.va_d[16:NT, c, :].rearrange("(j p) w -> p j w", p=128)
                for jj in range(0, 32, 8):
                    kb.dma("sp", va[b][:, 1 + jj:9 + jj, :], vv[:, jj:jj + 8, :], wp=[va_b[b]])

            load_pair(0)
            load_head(0)
            ips = 0
            ipo = 0
            for h in range(MLA_H):
                c, e = h // 2, h % 2
                if h + 1 < MLA_H:
                    load_head(h + 1)
                    if e == 1:
                        load_pair(c + 1)
                hb_, vb_ = h % 2, c % 2
                dlo, dhi = (0, 64) if e == 0 else (64, 128)
                nlo, nhi = (64, 128) if e == 0 else (0, 64)
                for qi in range(9):
                    if qi == 0:
                        q0, nq = 0, 16
                        kts = [(0, 0, 16, 0, True)]
                    else:
                        q0, nq = 16 + 512 * (qi - 1), 512
                        kts = [(0, 0, 16, 0, False)] + [(kt, 16 + 128 * (kt - 1), 128, 0, False) for kt in range(1, 4 * (qi - 1) + 1)]
                        kts += [(4 * (qi - 1) + 1 + i, 16 + 128 * (4 * (qi - 1) + i), 128, 128 * i, True) for i in range(4)]
                    pp = ipo % 2
                    ipo += 1
                    for idx, (kt, k0, nk, qoff, diag) in enumerate(kts):
                        nqq = nq - qoff
                        p = ips % 4
                        ips += 1
                        kb.op(kb.pe, lambda: T.matmul(ps[p][0:nk, 0:nqq], kh[hb_][:, k0:k0 + nk], qh[hb_][:, q0 + qoff:q0 + nq], start=True, stop=True),
                              r=[qk_b[hb_]], w=[ps_b[p]])
                        kb.op(kb.act, lambda: A.activation(out=pT[p][0:nk, 0:nqq], in_=ps[p][0:nk, 0:nqq], func=AF.Exp, bias=negB[0:nk, :], scale=scale),
                              r=[ps_b[p], bnd_b], w=[pT_b[p]])
                        if diag:
                            kb.op(kb.pool, lambda: G.tensor_tensor(pT[p][0:nk, 0:nk], pT[p][0:nk, 0:nk], self.tri16[0:nk, 0:nk], ALU.mult),
                                  r=[self.cb_], w=[pT_b[p]])
                        first, last = idx == 0, idx == len(kts) - 1
                        kb.op(kb.pe, lambda: T.matmul(po[pp][:, qoff:nq], va[vb_][0:nk, kt, e * 64:e * 64 + 128], pT[p][0:nk, 0:nqq], start=first, stop=last),
                              r=[pT_b[p], va_b[vb_]], w=[po_b[pp]] if first else [], wp=[] if first else [po_b[pp]], inc=last)
                    kb.op(kb.dve, lambda: V.reciprocal(rden[pp][nlo:nhi, 0:nq], po[pp][nlo:nhi, 0:nq]), r=[po_b[pp]], w=[rd_b[pp]])
                    kb.op(kb.act, lambda: A.copy(rdsh[pp][dlo:dhi, 0:nq], rden[pp][nlo:nhi, 0:nq]), r=[rd_b[pp]], w=[rs_b[pp]])
                    kb.op(kb.dve, lambda: V.tensor_tensor(oT[dlo:dhi, c, q0:q0 + nq], po[pp][dlo:dhi, 0:nq], rdsh[pp][dlo:dhi, 0:nq], ALU.mult),
                          r=[po_b[pp], rs_b[pp]], wp=[oT_b])
            kb.barrier()
        with contextlib.ExitStack() as s3:
            hs = [self.S(s3, "c_h%d" % i, [128, D], F32) for i in range(3)]
            hs_b = [Buf() for _ in range(3)]
            po = [self.P(s3, "c_po%d" % i, [128, 512], F32) for i in range(4)]
            po_b = [PB() for _ in range(4)]
            ip = 0
            for ti, (t0, n) in enumerate(TILES):
                s = ti % 3
                kb.dma("sp", hs[s][0:n, :], self.h_src(src, t0, n), w=[hs_b[s]])
                for hh in range(2):
                    p = ip % 4
                    ip += 1
                    for c in range(8):
                        kb.op(kb.pe, lambda: T.matmul(po[p][0:n, :], oT[:, c, t0:t0 + n], w_o[:, c, hh * 512:(hh + 1) * 512], start=(c == 0), stop=(c == 7)),
                              r=[oT_b, w_o_b], w=[po_b[p]] if c == 0 else [], wp=[po_b[p]] if c else [], inc=(c == 7))
                    kb.op(kb.dve, lambda: V.tensor_tensor(hs[s][0:n, hh * 512:(hh + 1) * 512], po[p][0:n, :], hs[s][0:n, hh * 512:(hh + 1) * 512], ALU.add),
                          r=[po_b[p], hs_b[s]], wp=[hs_b[s]])
                dap = self.h_dst(dst, t0, n)
                if dap is not None:
                    kb.dma("sp", dap, hs[s][0:n, :], r=[hs_b[s]])


def host_inputs(inp):
    f = lambda a: np.ascontiguousarray(np.asarray(a, dtype=np.float32))
    rep = lambda a: np.ascontiguousarray(np.broadcast_to(np.asarray(a, np.float32)[:, None, :], (a.shape[0], 128, a.shape[1])))
    colT = lambda a, c: np.ascontiguousarray(np.asarray(a, np.float32).reshape(a.shape[0], c, 128).transpose(0, 2, 1))
    ln = np.concatenate([np.asarray(inp["ln_mix"], np.float32), np.asarray(inp["ln_mlp"], np.float32)], 0)
    lnT = np.ascontiguousarray(ln.reshape(8, 8, 128).transpose(2, 0, 1))
    cw = np.asarray(inp["ssd_conv_w"], np.float32)
    cwT = np.ascontiguousarray(cw.reshape(2, 4, 32, 128).transpose(0, 3, 2, 1))
    k = np.arange(128)
    tri = (k[:, None] <= k[None, :]).astype(np.float32)
    inv = 1.0 / (10000.0 ** (np.arange(0, 32, 2, dtype=np.float32) / 32.0))
    ang = np.arange(NT, dtype=np.float32)[:, None] * inv[None, :].astype(np.float32)
    common = {
        "meta": f(inp["meta_tokens"]),
        "ssd_w_in": f(inp["ssd_w_in"]), "ssd_w_out": f(inp["ssd_w_out"]),
        "mla_w_in": f(inp["mla_w_in"]), "mla_w_q_b": f(inp["mla_w_q_b"]), "mla_w_kv_b": f(inp["mla_w_kv_b"]),
        "mla_w_out": f(inp["mla_w_out"]), "mlp_w_up": f(inp["mlp_w_up"]), "mlp_w_down": f(inp["mlp_w_down"]),
        "lnT": lnT, "cw": cwT, "cb": colT(inp["ssd_conv_b"], 32),
        "dtb_rep": rep(inp["ssd_dt_bias"]), "alog_rep": rep(inp["ssd_a_log"]), "dskip_rep": rep(inp["ssd_d"]),
        "ssd_normT": colT(inp["ssd_norm"], 16), "q_a_T": colT(inp["mla_q_a_norm"], 3), "kv_a_T": colT(inp["mla_kv_a_norm"], 2),
        "gq_rep": rep(inp["mla_q_norm"]), "gk_rep": rep(inp["mla_k_norm"]),
        "ident": np.eye(128, dtype=np.float32), "tri": tri, "ltri": np.ascontiguousarray(1.0 - tri),
        "ones": np.ones((128, 128), np.float32),
        "cos": np.cos(ang).astype(np.float32), "sin": np.sin(ang).astype(np.float32),
    }
    return common


FULL_PHASES = [
    ("ssd", 0, 0, "x", "h"), ("mlp", 0, "h", "h"),
    ("mla", 0, 1, "h", "h"), ("mlp", 1, "h", "h"),
    ("ssd", 1, 2, "h", "h"), ("mlp", 2, "h", "h"),
    ("mla", 1, 3, "h", "h"), ("mlp", 3, "h", "y"),
]


def run(inputs, phases, cores=8):
    common = host_inputs(inputs)
    x = np.asarray(inputs["x"], np.float32)
    prog = Prog(phases)
    in_maps = []
    for c in range(cores):
        m = dict(common)
        m["x"] = np.ascontiguousarray(x[c])
        in_maps.append(m)
    res = run_bass_kernel_spmd(prog.nc, in_maps, core_ids=list(range(cores)))
    return np.stack([np.asarray(r["y"]) for r in res.results], 0)


def kernel(**inputs):
    return run(inputs, FULL_PHASES, 8).astype(np.float32)
```

```python
import contextlib
import numpy as np
import concourse.bass as bass
import concourse.mybir as mybir
from concourse.bass_utils import run_bass_kernel_spmd

F32, BF16 = mybir.dt.float32, mybir.dt.bfloat16
AF = mybir.ActivationFunctionType
ALU = mybir.AluOpType
AX = mybir.AxisListType

NT, NM, D, SEQ = 4112, 16, 1024, 4096
TILES = [(0, 16)] + [(16 + 128 * j, 128) for j in range(32)]
EPS = 1e-6
DFF = 4096
SSD_IN = 6176
NH_S = 32
MLA_H = 16
QK = 96


class Buf:
    __slots__ = ("w", "r", "name", "ps")

    def __init__(self, name="", ps=False):
        self.w = {}
        self.r = {}
        self.name = name
        self.ps = ps


def PB():
    return Buf(ps=True)


class Eng:
    def __init__(self, name, e, sem):
        self.name, self.e, self.sem, self.cnt, self.seen = name, e, sem, 0, {}


class KB:
    def __init__(self, nc, es):
        self.nc = nc
        mk = lambda n: es.enter_context(nc.semaphore(n))
        self.pe = Eng("pe", nc.tensor, mk("s_pe"))
        self.act = Eng("act", nc.scalar, mk("s_act"))
        self.dve = Eng("dve", nc.vector, mk("s_dve"))
        self.pool = Eng("pool", nc.gpsimd, mk("s_pool"))
        self.sp = Eng("sp", nc.sync, mk("s_sp"))
        self.engs = [self.pe, self.act, self.dve, self.pool, self.sp]
        self.dsem = {"sp": [[mk("d_sp%d" % i), 0] for i in range(24)],
                     "pool": [[mk("d_pl%d" % i), 0] for i in range(8)]}
        self.drr = {"sp": 0, "pool": 0}
        self.nins = 0

    def _wait(self, E, toks):
        for key, (sem, val) in toks.items():
            if E is self.pe and key == "pe":
                continue
            if E.seen.get(key, 0) >= val:
                continue
            E.e.wait_ge(sem, val)
            E.seen[key] = val

    @staticmethod
    def _add(need, d):
        for k, sv in d.items():
            if k not in need or need[k][1] < sv[1]:
                need[k] = sv

    def _deps(self, r, w, wp, ekey=None):
        need = {}
        for b in r:
            self._add(need, b.w)
            if b.ps:
                self._add(need, {k: v for k, v in b.r.items() if k != ekey})
        for b in w:
            self._add(need, b.w)
            self._add(need, b.r)
        for b in wp:
            self._add(need, b.r)
            if b.ps:
                self._add(need, b.w)
        return need

    def _reg(self, key, tok, r, w, wp):
        for b in r:
            if key not in b.r or b.r[key][1] < tok[1]:
                b.r[key] = tok
        for b in w:
            b.w = {key: tok}
            b.r = {}
        for b in wp:
            if key not in b.w or b.w[key][1] < tok[1]:
                b.w[key] = tok

    def op(self, E, fn, r=(), w=(), wp=(), inc=True):
        self._wait(E, self._deps(r, w, wp, E.name))
        ins = fn()
        self.nins += 1
        if inc:
            E.cnt += 1
            ins.then_inc(E.sem, 1)
            tok = (E.sem, E.cnt)
        else:
            tok = (E.sem, E.cnt + 1)
        self._reg(E.name, tok, r, w, wp)

    def dma(self, Q, out, in_, r=(), w=(), wp=()):
        E = self.sp if Q == "sp" else self.pool
        self._wait(E, self._deps(r, w, wp))
        lst = self.dsem[Q]
        i = self.drr[Q]
        self.drr[Q] = (i + 1) % len(lst)
        sem, cnt = lst[i]
        key = (Q, i)
        if cnt > 0 and E.seen.get(key, 0) < cnt:
            E.e.wait_ge(sem, cnt)
            E.seen[key] = cnt
        ins = E.e.dma_start(out=out, in_=in_)
        ins.then_inc(sem, 16)
        self.nins += 1
        lst[i][1] = cnt + 16
        self._reg(key, (sem, cnt + 16), r, w, wp)

    def barrier(self):
        toks = {}
        for E in self.engs:
            if E.cnt > 0:
                toks[E.name] = (E.sem, E.cnt)
        for Q, lst in self.dsem.items():
            for i, (sem, cnt) in enumerate(lst):
                if cnt > 0:
                    toks[(Q, i)] = (sem, cnt)
        for E in self.engs:
            self._wait(E, toks)


def bc(ap, shape, axis):
    return ap.unsqueeze(axis).to_broadcast(list(shape))


class Prog:
    def __init__(self, phases):
        self.phases = phases
        nc = bass.Bass("TRN2", target_bir_lowering=False)
        self.nc = nc
        di = lambda name, shape: nc.dram_tensor(name, list(shape), F32, kind="ExternalInput").ap()
        self.x = di("x", [SEQ, D])
        self.meta = di("meta", [NM, D])
        self.w_ssd_in = di("ssd_w_in", [2, D, SSD_IN])
        self.w_ssd_out = di("ssd_w_out", [2, 2048, D])
        self.w_mla_in = di("mla_w_in", [2, D, 672])
        self.w_mla_qb = di("mla_w_q_b", [2, 384, 1536])
        self.w_mla_kvb = di("mla_w_kv_b", [2, 256, 2048])
        self.w_mla_out = di("mla_w_out", [2, D, D])
        self.w_up = di("mlp_w_up", [4, D, DFF])
        self.w_dn = di("mlp_w_down", [4, DFF, D])
        self.lnT_d = di("lnT", [128, 8, 8])
        self.cw_d = di("cw", [2, 128, 32, 4])
        self.cb_d = di("cb", [2, 128, 32])
        self.dtb_d = di("dtb_rep", [2, 128, 32])
        self.alog_d = di("alog_rep", [2, 128, 32])
        self.dsk_d = di("dskip_rep", [2, 128, 32])
        self.sng_d = di("ssd_normT", [2, 128, 16])
        self.qag_d = di("q_a_T", [2, 128, 3])
        self.kvag_d = di("kv_a_T", [2, 128, 2])
        self.gq_d = di("gq_rep", [2, 128, 96])
        self.gk_d = di("gk_rep", [2, 128, 96])
        self.ident_d = di("ident", [128, 128])
        self.tri_d = di("tri", [128, 128])
        self.ltri_d = di("ltri", [128, 128])
        self.ones_d = di("ones", [128, 128])
        self.cos_d = di("cos", [NT, 16])
        self.sin_d = di("sin", [NT, 16])
        self.y = nc.dram_tensor("y", [SEQ, D], F32, kind="ExternalOutput").ap()
        self.hd = nc.dram_tensor("hd", [NT, D], F32, kind="Internal").ap()
        self.qT_d = nc.dram_tensor("qT_d", [MLA_H, QK, NT], BF16, kind="Internal").ap()
        self.kT_d = nc.dram_tensor("kT_d", [MLA_H, QK, NT], BF16, kind="Internal").ap()
        self.va_d = nc.dram_tensor("va_d", [NT, 8, 192], BF16, kind="Internal").ap()

        with contextlib.ExitStack() as es:
            self.kb = KB(nc, es)
            self.build(es)

    def S(self, es, name, shape, dt):
        self.uid = getattr(self, "uid", 0) + 1
        return es.enter_context(self.nc.sbuf_tensor("sb%d_%s" % (self.uid, name), list(shape), dt))

    def P(self, es, name, shape, dt):
        self.uid = getattr(self, "uid", 0) + 1
        return es.enter_context(self.nc.psum_tensor("ps%d_%s" % (self.uid, name), list(shape), dt))

    def h_src(self, kind, t0, n):
        if kind == "x":
            return self.meta[0:16, :] if t0 == 0 else self.x[t0 - 16:t0 - 16 + n, :]
        return self.hd[t0:t0 + n, :]

    def h_dst(self, kind, t0, n):
        if kind == "y":
            return None if t0 == 0 else self.y[t0 - 16:t0 - 16 + n, :]
        return self.hd[t0:t0 + n, :]

    def build(self, es):
        kb, nc = self.kb, self.nc
        self.ident = self.S(es, "ident", [128, 128], BF16)
        self.tri32 = self.S(es, "tri32", [128, 128], F32)
        self.tri16 = self.S(es, "tri16", [128, 128], BF16)
        self.ltri32 = self.S(es, "ltri32", [128, 128], F32)
        self.ones32 = self.S(es, "ones32", [128, 128], F32)
        self.lnT = self.S(es, "lnT", [128, 8, 8], F32)
        self.cb_ = Buf("consts")
        kb.dma("pool", self.ident[:], self.ident_d, wp=[self.cb_])
        kb.dma("pool", self.tri16[:], self.tri_d, wp=[self.cb_])
        kb.dma("sp", self.tri32[:], self.tri_d, wp=[self.cb_])
        kb.dma("sp", self.ltri32[:], self.ltri_d, wp=[self.cb_])
        kb.dma("sp", self.ones32[:], self.ones_d, wp=[self.cb_])
        kb.dma("sp", self.lnT[:], self.lnT_d, wp=[self.cb_])
        for ph in self.phases:
            kind = ph[0]
            with contextlib.ExitStack() as pes:
                if kind == "mlp":
                    self.phase_mlp(pes, *ph[1:])
                elif kind == "ssd":
                    self.phase_ssd(pes, *ph[1:])
                elif kind == "mla":
                    self.phase_mla(pes, *ph[1:])
                kb.barrier()
        kb.barrier()

    def norm_T(self, hb, h_ap, n, gain_ap, dst_ap, dst_b, sc):
        kb, nc = self.kb, self.nc
        junk, junk_b, ss, ss_b, xn, xn_b, tp, tp_b = sc
        kb.op(kb.act, lambda: nc.scalar.activation(out=junk[0:n, :], in_=h_ap, func=AF.Square, accum_out=ss[0:n, 0:1]),
              r=[hb], w=[junk_b, ss_b] if junk_b is not xn_b else [xn_b, ss_b])
        kb.op(kb.act, lambda: nc.scalar.activation(out=ss[0:n, 1:2], in_=ss[0:n, 0:1], func=AF.Ln, bias=EPS, scale=1.0 / D),
              r=[ss_b], wp=[ss_b])
        kb.op(kb.act, lambda: nc.scalar.activation(out=ss[0:n, 2:3], in_=ss[0:n, 1:2], func=AF.Exp, scale=-0.5),
              r=[ss_b], wp=[ss_b])
        kb.op(kb.dve, lambda: nc.vector.tensor_scalar(xn[0:n, :], h_ap, ss[0:n, 2:3], None, ALU.mult),
              r=[hb, ss_b], w=[xn_b])
        for c in range(8):
            kb.op(kb.pe, lambda c=c: nc.tensor.transpose(tp[:, c, 0:n], xn[0:n, c * 128:(c + 1) * 128], self.ident[0:n, 0:n]),
                  r=[xn_b, self.cb_], w=[tp_b] if c == 0 else [], wp=[tp_b] if c else [], inc=(c == 7))
        kb.op(kb.dve, lambda: nc.vector.tensor_tensor(dst_ap, tp[:, :, 0:n], bc(gain_ap, [128, 8, n], 2), ALU.mult),
              r=[tp_b, self.cb_], w=[dst_b])

    def norm_scratch(self, es, pfx, tp_ps):
        ss = self.S(es, pfx + "ss", [128, 4], F32)
        xn = self.S(es, pfx + "xn", [128, 1024], BF16)
        tp = tp_ps[:].bitcast(BF16)[:, 0:1024].rearrange("p (c t) -> p c t", c=8)
        xn_b = Buf()
        return (xn, xn_b, ss, Buf(), xn, xn_b, tp, PB())

    def phase_mlp(self, es, li, src, dst):
        kb, nc = self.kb, self.nc
        wup = self.S(es, "wup", [128, 8, DFF], BF16)
        wdn = self.S(es, "wdn", [128, 32, D], BF16)
        wup_b, wdn_b = Buf(), Buf()
        upv = self.w_up[li].rearrange("(c p) f -> p c f", p=128)
        dnv = self.w_dn[li].rearrange("(c p) d -> p c d", p=128)
        for c in range(8):
            kb.dma("pool", wup[:, c, :], upv[:, c, :], wp=[wup_b])
        for c in range(0, 32, 4):
            kb.dma("pool", wdn[:, c:c + 4, :], dnv[:, c:c + 4, :], wp=[wdn_b])
        NSLOT = 7
        hs = [self.S(es, "mh%d" % i, [128, D], F32) for i in range(NSLOT)]
        hs_b = [Buf() for _ in range(NSLOT)]
        xnT = self.S(es, "m_xnT", [128, 8, 512], BF16)
        xnT_b = Buf()
        uT = self.S(es, "m_uT", [128, 32, 512], BF16)
        uT_b = [Buf() for _ in range(32)]
        r32 = [self.S(es, "m_r32_%d" % i, [128, 512], F32) for i in range(2)]
        r32_b = [Buf(), Buf()]
        tp_ps = [self.P(es, "m_tp%d" % i, [128, 512], F32) for i in range(2)]
        pu = [self.P(es, "m_pu%d" % i, [128, 512], F32) for i in range(3)]
        pu_b = [PB() for _ in range(3)]
        pd = [self.P(es, "m_pd%d" % i, [128, 512], F32) for i in range(3)]
        pd_b = [PB() for _ in range(3)]
        nsc = [self.norm_scratch(es, "m%d" % i, tp_ps[i]) for i in range(2)]
        gain = self.lnT[:, 4 + li, :]
        groups = [[TILES[0]]] + [TILES[1 + 4 * g:5 + 4 * g] for g in range(8)]
        slot = 0
        iu = 0
        ipd = 0
        inorm = 0
        for grp in groups:
            ntok = sum(n for _, n in grp)
            myslots = []
            for (t0, n) in grp:
                s = slot % NSLOT
                slot += 1
                myslots.append(s)
                kb.dma("sp", hs[s][0:n, :], self.h_src(src, t0, n), w=[hs_b[s]])
            off = 0
            for (t0, n), s in zip(grp, myslots):
                self.norm_T(hs_b[s], hs[s][0:n, :], n, gain, xnT[:, :, off:off + n], xnT_b, nsc[inorm % 2])
                inorm += 1
                off += n
            for fc in range(32):
                p = iu % 3
                for c in range(8):
                    kb.op(kb.pe, lambda c=c, fc=fc, p=p: nc.tensor.matmul(pu[p][:, 0:ntok], wup[:, c, fc * 128:(fc + 1) * 128],
                                                                      xnT[:, c, 0:ntok], start=(c == 0), stop=(c == 7)),
                          r=[wup_b, xnT_b], w=[pu_b[p]] if c == 0 else [], wp=[pu_b[p]] if c else [], inc=(c == 7))
                rr = iu % 2
                kb.op(kb.act, lambda p=p, rr=rr: nc.scalar.activation(out=r32[rr][:, 0:ntok], in_=pu[p][:, 0:ntok], func=AF.Relu),
                      r=[pu_b[p]], w=[r32_b[rr]])
                E = kb.dve if (fc % 2 == 0) else kb.pool
                kb.op(E, lambda rr=rr, fc=fc, E=E: E.e.tensor_tensor(uT[:, fc, 0:ntok], r32[rr][:, 0:ntok], r32[rr][:, 0:ntok], ALU.mult),
                      r=[r32_b[rr]], w=[uT_b[fc]])
                iu += 1
            off = 0
            for (t0, n), s in zip(grp, myslots):
                for hh in range(2):
                    p = ipd % 3
                    ipd += 1
                    for fc in range(32):
                        kb.op(kb.pe, lambda fc=fc, p=p, off=off, n=n, hh=hh: nc.tensor.matmul(
                            pd[p][0:n, :], uT[:, fc, off:off + n], wdn[:, fc, hh * 512:(hh + 1) * 512],
                            start=(fc == 0), stop=(fc == 31)),
                            r=[uT_b[fc], wdn_b], w=[pd_b[p]] if fc == 0 else [], wp=[pd_b[p]] if fc else [], inc=(fc == 31))
                    kb.op(kb.dve, lambda p=p, n=n, hh=hh, s=s: nc.vector.tensor_tensor(
                        hs[s][0:n, hh * 512:(hh + 1) * 512], pd[p][0:n, :], hs[s][0:n, hh * 512:(hh + 1) * 512], ALU.add),
                        r=[pd_b[p], hs_b[s]], wp=[hs_b[s]])
                dst_ap = self.h_dst(dst, t0, n)
                if dst_ap is not None:
                    kb.dma("sp", dst_ap, hs[s][0:n, :], r=[hs_b[s]])
                off += n

    def phase_ssd(self, es, j, li, src, dst):
        kb, nc = self.kb, self.nc
        V, A, G, T = nc.vector, nc.scalar, nc.gpsimd, nc.tensor
        w_in = self.S(es, "s_win", [128, 8, SSD_IN], BF16)
        w_out = self.S(es, "s_wout", [128, 16, D], BF16)
        win_b, wout_b, par_b = Buf(), Buf(), Buf()
        wv = self.w_ssd_in[j].rearrange("(c p) f -> p c f", p=128)
        for c in range(8):
            kb.dma("pool", w_in[:, c, :], wv[:, c, :], wp=[win_b])
        wov = self.w_ssd_out[j].rearrange("(c p) f -> p c f", p=128)
        for c in range(0, 16, 4):
            kb.dma("pool", w_out[:, c:c + 4, :], wov[:, c:c + 4, :], wp=[wout_b])
        cw = self.S(es, "s_cw", [128, 32, 4], F32)
        cb = self.S(es, "s_cb", [128, 32], F32)
        dtb = self.S(es, "s_dtb", [128, 32], F32)
        arep = self.S(es, "s_arep", [128, 32], F32)
        dsk = self.S(es, "s_dsk", [128, 32], F32)
        sng = self.S(es, "s_sng", [128, 16], F32)
        kb.dma("sp", cw[:], self.cw_d[j], wp=[par_b])
        kb.dma("sp", cb[:], self.cb_d[j], wp=[par_b])
        kb.dma("sp", dtb[:], self.dtb_d[j], wp=[par_b])
        kb.dma("sp", arep[:], self.alog_d[j], wp=[par_b])
        kb.dma("sp", dsk[:], self.dsk_d[j], wp=[par_b])
        kb.dma("sp", sng[:], self.sng_d[j], wp=[par_b])
        arep_b = Buf()
        kb.op(kb.act, lambda: A.activation(out=arep[:], in_=arep[:], func=AF.Exp), r=[par_b], w=[arep_b])
        kb.op(kb.dve, lambda: V.tensor_scalar(arep[:], arep[:], -1.0, None, ALU.mult), r=[arep_b], w=[arep_b])
        hs = [self.S(es, "s_h%d" % i, [128, D], F32) for i in range(2)]
        hs_b = [Buf(), Buf()]
        tp2 = self.P(es, "s_tp2", [128, 1024], F32)
        pzbs = [self.P(es, "s_pz%d" % i, [128, 512], F32) for i in range(2)]
        zqb = self.P(es, "s_zq", [128, 512], F32)
        dTb = self.P(es, "s_dT", [128, 512], F32)
        miscb = self.P(es, "s_misc", [128, 512], F32)
        yb = self.P(es, "s_y", [128, 512], F32)
        pz_b = [PB(), PB()]
        _z = PB()
        zq_b = [_z, _z]
        dT_b = PB()
        misc_b = PB()
        pdt_b = pacs_b = ptot_b = pst_b = misc_b
        pcb_b = [misc_b, misc_b]
        ydg_b = yof_b = PB()
        nsc = self.norm_scratch(es, "s", tp2)
        tp_b = nsc[7]
        tp16 = tp2[:].bitcast(BF16)
        tpg = tp16.rearrange("p (c t) -> p c t", c=16)
        xnT = [self.S(es, "s_xnT%d" % i, [128, 8, 131], BF16) for i in range(2)]
        xnT_b = [Buf(), Buf()]
        for i in range(2):
            kb.op(kb.pool, lambda: G.memset(xnT[i][:], 0.0), w=[xnT_b[i]])
        xbcT = self.S(es, "s_xbcT", [128, 32, 128], BF16)
        xbc_b = [Buf() for _ in range(32)]
        acc = [self.S(es, "s_acc%d" % i, [128, 128], F32) for i in range(3)]
        acc_b = [Buf() for _ in range(3)]
        xs_tm = self.S(es, "s_xstm", [128, 2048], BF16)
        xs_b = Buf()
        B_tm = self.S(es, "s_Btm", [128, 1024], BF16)
        Btm_b = Buf()
        sz = self.S(es, "s_sz", [128, 2048], F32)
        sz_b = [Buf() for _ in range(8)]
        sm = self.S(es, "s_sm", [128, 10, 32], F32)
        DTV, EXPT, ADT, ACS, DFS, CD, DTE, TMP, W2 = range(9)
        sm_b = [Buf() for _ in range(10)]
        rhsD = [self.S(es, "s_rhsD0", [128, 512], F32)] * 2
        _b = Buf()
        rhsD_b = [_b, _b]
        Ee = [self.S(es, "s_E0", [128, 512], F32)] * 2
        _b = Buf()
        E_b = [_b, _b]
        CBm = [self.S(es, "s_CBm%d" % i, [128, 128], F32) for i in range(2)]
        CBm_b = [Buf(), Buf()]
        MT = [self.S(es, "s_MT%d" % i, [128, 512], BF16) for i in range(2)]
        MT_b = [Buf(), Buf()]
        xdt = [self.S(es, "s_xdt%d" % i, [128, 256], BF16) for i in range(2)]
        xdt_b = [Buf(), Buf()]
        xdtd = [self.S(es, "s_xdtd%d" % i, [128, 256], BF16) for i in range(2)]
        xdtd_b = [Buf(), Buf()]
        tt = [[self.S(es, "s_t%d_%d" % (k, i), [128, 256], F32) for i in range(2)] for k in range(3)]
        tt_b = [[Buf(), Buf()] for _ in range(3)]
        tt.append(tt[0])
        tt_b.append(tt_b[0])
        stg = self.S(es, "s_stg", [128, 8, 4], F32)
        stg_b = [Buf() for _ in range(8)]
        gn = self.S(es, "s_gn", [128, 2048], BF16)
        gn_b = Buf()
        gT = self.S(es, "s_gT", [128, 16, 128], BF16)
        gT_b = Buf()
        S32 = self.S(es, "s_S32", [128, 2048], F32)
        Sbf = self.S(es, "s_Sbf", [128, 2048], BF16)
        S32_b = [Buf() for _ in range(8)]
        Sbf_b = [Buf() for _ in range(8)]
        stmp = [self.S(es, "s_stmp0", [128, 256], F32)] * 2
        _b = Buf()
        stmp_b = [_b, _b]
        kb.op(kb.pool, lambda: G.memset(S32[:], 0.0), w=S32_b)
        kb.op(kb.pool, lambda: G.memset(Sbf[:], 0.0), w=Sbf_b)
        nprev = 0
        ipz = 0
        for ti, (t0, n) in enumerate(TILES):
            cur, prev = ti % 2, (ti + 1) % 2
            X = xnT[cur]
            kb.dma("sp", hs[cur][0:n, :], self.h_src(src, t0, n), w=[hs_b[cur]])
            self.norm_T(hs_b[cur], hs[cur][0:n, :], n, self.lnT[:, li, :], X[:, :, 3:3 + n], xnT_b[cur], nsc)
            kb.op(kb.pool, lambda: G.tensor_copy(X[:, :, 0:3], xnT[prev][:, :, nprev:nprev + 3]), r=[xnT_b[prev]], wp=[xnT_b[cur]])
            nprev = n
            for cc in range(32):
                p = ipz % 2
                ipz += 1
                pz = pzbs[p][:, 0:3 + n]
                for c in range(8):
                    kb.op(kb.pe, lambda: T.matmul(pz, w_in[:, c, 2048 + cc * 128:2048 + (cc + 1) * 128], X[:, c, 0:3 + n], start=(c == 0), stop=(c == 7)),
                          r=[win_b, xnT_b[cur]], w=[pz_b[p]] if c == 0 else [], wp=[pz_b[p]] if c else [], inc=(c == 7))
                a_ = acc[p][:, 0:n]
                kb.op(kb.act, lambda: A.activation(out=a_, in_=pz[:, 3:3 + n], func=AF.Identity, bias=cb[:, cc:cc + 1], scale=cw[:, cc, 3:4]),
                      r=[pz_b[p], par_b], w=[acc_b[p]])
                for k in range(3):
                    kb.op(kb.dve, lambda: V.scalar_tensor_tensor(a_, pz[:, k:k + n], cw[:, cc, k:k + 1], a_, ALU.mult, ALU.add),
                          r=[pz_b[p], par_b], w=[acc_b[p]])
                kb.op(kb.act, lambda: A.activation(out=xbcT[:, cc, 0:n], in_=a_, func=AF.Silu), r=[acc_b[p]], w=[xbc_b[cc]])
            for g in range(8):
                zs = g % 2
                zq = zqb[0:n, zs * 256:(zs + 1) * 256]
                for c in range(8):
                    kb.op(kb.pe, lambda: T.matmul(zq, X[:, c, 3:3 + n], w_in[:, c, g * 256:(g + 1) * 256], start=(c == 0), stop=(c == 7)),
                          r=[win_b, xnT_b[cur]], w=[zq_b[zs]] if c == 0 else [], wp=[zq_b[zs]] if c else [], inc=(c == 7))
                kb.op(kb.act, lambda: A.activation(out=sz[0:n, g * 256:(g + 1) * 256], in_=zq, func=AF.Silu), r=[zq_b[zs]], w=[sz_b[g]])
            for cc in range(16):
                kb.op(kb.pe, lambda: T.transpose(tp16[0:n, cc * 128:(cc + 1) * 128], xbcT[:, cc, 0:n], self.ident[:, :]),
                      r=[xbc_b[cc], self.cb_], w=[tp_b] if cc == 0 else [], wp=[tp_b] if cc else [], inc=(cc == 15))
            kb.op(kb.act, lambda: A.copy(xs_tm[0:n, 0:1024], tp16[0:n, 0:1024]), r=[tp_b], w=[xs_b])
            kb.op(kb.dve, lambda: V.tensor_copy(xs_tm[0:n, 1024:2048], tp16[0:n, 1024:2048]), r=[tp_b], wp=[xs_b])
            for g in range(8):
                kb.op(kb.pe, lambda: T.transpose(tp16[0:n, g * 128:(g + 1) * 128], xbcT[:, 16 + g, 0:n], self.ident[:, :]),
                      r=[xbc_b[16 + g], self.cb_], w=[tp_b] if g == 0 else [], wp=[tp_b] if g else [], inc=(g == 7))
            kb.op(kb.act, lambda: A.copy(B_tm[0:n, :], tp16[0:n, 0:1024]), r=[tp_b], w=[Btm_b])
            pdt, pacs, ptot = miscb[0:n, 0:32], miscb[0:n, 32:64], miscb[:, 64:96]
            for c in range(8):
                kb.op(kb.pe, lambda: T.matmul(pdt, X[:, c, 3:3 + n], w_in[:, c, 6144:6176], start=(c == 0), stop=(c == 7)),
                      r=[win_b, xnT_b[cur]], w=[pdt_b] if c == 0 else [], wp=[pdt_b] if c else [], inc=(c == 7))
            smv = lambda k: sm[0:n, k, :]
            kb.op(kb.dve, lambda: V.tensor_tensor(smv(DTV), pdt, dtb[0:n, :], ALU.add), r=[pdt_b, par_b], w=[sm_b[DTV]])
            kb.op(kb.act, lambda: A.activation(out=smv(EXPT), in_=smv(DTV), func=AF.Exp), r=[sm_b[DTV]], w=[sm_b[EXPT]])
            kb.op(kb.act, lambda: A.activation(out=smv(DTV), in_=smv(EXPT), func=AF.Ln, bias=1.0), r=[sm_b[EXPT]], w=[sm_b[DTV]])
            kb.op(kb.dve, lambda: V.tensor_tensor(smv(ADT), smv(DTV), arep[0:n, :], ALU.mult), r=[sm_b[DTV], arep_b], w=[sm_b[ADT]])
            kb.op(kb.pe, lambda: T.matmul(pacs, self.tri32[0:n, 0:n], smv(ADT), start=True, stop=True), r=[sm_b[ADT], self.cb_], w=[pacs_b])
            kb.op(kb.pe, lambda: T.matmul(ptot, self.ones32[0:n, :], smv(ADT), start=True, stop=True), r=[sm_b[ADT], self.cb_], w=[ptot_b])
            kb.op(kb.dve, lambda: V.tensor_copy(smv(ACS), pacs), r=[pacs_b], w=[sm_b[ACS]])
            kb.op(kb.act, lambda: A.activation(out=smv(DFS), in_=pacs, func=AF.Exp), r=[pacs_b], w=[sm_b[DFS]])
            kb.op(kb.act, lambda: A.activation(out=sm[:, CD, :], in_=ptot, func=AF.Exp), r=[ptot_b], w=[sm_b[CD]])
            kb.op(kb.dve, lambda: V.tensor_tensor(smv(TMP), ptot[0:n, :], smv(ACS), ALU.subtract), r=[ptot_b, sm_b[ACS]], w=[sm_b[TMP]])
            kb.op(kb.act, lambda: A.activation(out=smv(DTE), in_=smv(TMP), func=AF.Exp), r=[sm_b[TMP]], w=[sm_b[DTE]])
            kb.op(kb.dve, lambda: V.tensor_tensor(smv(W2), smv(DTV), smv(DTE), ALU.mult), r=[sm_b[DTV], sm_b[DTE]], w=[sm_b[W2]])
            for g in range(8):
                b2 = g % 2
                hsl = slice(4 * g, 4 * g + 4)
                gsl = slice(g * 256, (g + 1) * 256)
                v3 = lambda ap: ap.rearrange("p (j l) -> p j l", j=4)
                rD = rhsD[b2][0:n, 0:4 * n]
                kb.op(kb.pool, lambda: G.tensor_tensor(v3(rD), bc(sm[0:n, ADT, hsl], [n, 4, n], 2), bc(self.tri32[0:n, 0:n], [n, 4, n], 1), ALU.mult),
                      r=[sm_b[ADT], self.cb_], w=[rhsD_b[b2]])
                kb.op(kb.pe, lambda: T.matmul(dTb[0:n, 0:4 * n], self.ltri32[0:n, 0:n], rD, start=True, stop=True), r=[rhsD_b[b2], self.cb_], w=[dT_b])
                Ev = Ee[b2][0:n, 0:4 * n]
                kb.op(kb.act, lambda: A.activation(out=Ev, in_=dTb[0:n, 0:4 * n], func=AF.Exp), r=[dT_b], w=[E_b[b2]])
                pcb = miscb[0:n, 128:128 + n]
                kb.op(kb.pe, lambda: T.matmul(pcb, xbcT[:, 16 + g, 0:n], xbcT[:, 24 + g, 0:n], start=True, stop=True),
                      r=[xbc_b[16 + g], xbc_b[24 + g]], w=[pcb_b[b2]])
                kb.op(kb.dve, lambda: V.tensor_tensor(CBm[b2][0:n, 0:n], pcb, self.tri32[0:n, 0:n], ALU.mult), r=[pcb_b[b2], self.cb_], w=[CBm_b[b2]])
                MTv = MT[b2][0:n, 0:4 * n]
                kb.op(kb.pool, lambda: G.tensor_tensor(v3(MTv), v3(Ev), bc(CBm[b2][0:n, 0:n], [n, 4, n], 1), ALU.mult),
                      r=[E_b[b2], CBm_b[b2]], w=[MT_b[b2]])
                xs3 = xs_tm[0:n, gsl].rearrange("p (j f) -> p j f", j=4)
                x3 = lambda ap: ap.rearrange("p (j f) -> p j f", j=4)
                kb.op(kb.pool, lambda: G.tensor_tensor(x3(xdt[b2][0:n, :]), xs3, bc(sm[0:n, DTV, hsl], [n, 4, 64], 2), ALU.mult),
                      r=[xs_b, sm_b[DTV]], w=[xdt_b[b2]])
                kb.op(kb.pool, lambda: G.tensor_tensor(x3(xdtd[b2][0:n, :]), xs3, bc(sm[0:n, W2, hsl], [n, 4, 64], 2), ALU.mult),
                      r=[xs_b, sm_b[W2]], w=[xdtd_b[b2]])
                for jj in range(4):
                    kb.op(kb.pe, lambda: T.matmul(yb[0:n, jj * 64:(jj + 1) * 64], MTv[:, jj * n:(jj + 1) * n], xdt[b2][0:n, jj * 64:(jj + 1) * 64], start=True, stop=True),
                          r=[MT_b[b2], xdt_b[b2]], w=[ydg_b] if jj == 0 else [], wp=[ydg_b] if jj else [], inc=(jj == 3))
                kb.op(kb.pe, lambda: T.matmul(yb[0:n, 256:512], xbcT[:, 24 + g, 0:n], Sbf[:, gsl], start=True, stop=True),
                      r=[xbc_b[24 + g], Sbf_b[g]], w=[yof_b])
                t1, t2, t3, tj = [tt[k][b2][0:n, :] for k in range(4)]
                kb.op(kb.dve, lambda: V.tensor_tensor(x3(t1), x3(yb[0:n, 256:512]), bc(sm[0:n, DFS, hsl], [n, 4, 64], 2), ALU.mult),
                      r=[yof_b, sm_b[DFS]], w=[tt_b[0][b2]])
                kb.op(kb.dve, lambda: V.tensor_tensor(t2, yb[0:n, 0:256], t1, ALU.add), r=[ydg_b, tt_b[0][b2]], w=[tt_b[1][b2]])
                kb.op(kb.pool, lambda: G.tensor_tensor(x3(t3), xs3, bc(dsk[0:n, hsl], [n, 4, 64], 2), ALU.mult), r=[xs_b, par_b], w=[tt_b[2][b2]])
                kb.op(kb.pool, lambda: G.tensor_tensor(t2, t2, t3, ALU.add), r=[tt_b[2][b2]], w=[tt_b[1][b2]])
                kb.op(kb.pool, lambda: G.tensor_tensor(t2, t2, sz[0:n, gsl], ALU.mult), r=[sz_b[g]], w=[tt_b[1][b2]])
                kb.op(kb.act, lambda: A.activation(out=tj, in_=t2, func=AF.Square, accum_out=stg[0:n, g, 0:1]), r=[tt_b[1][b2]], w=[tt_b[3][b2], stg_b[g]])
                self.rstd_ops(stg[:, g, :], stg_b[g], n, 0, 1.0 / 256)
                kb.op(kb.dve, lambda: V.tensor_scalar(gn[0:n, gsl], t2, stg[0:n, g, 2:3], None, ALU.mult), r=[tt_b[1][b2], stg_b[g]],
                      w=[gn_b] if g == 0 else [], wp=[gn_b] if g else [])
                kb.op(kb.pe, lambda: T.matmul(miscb[:, 256:512], B_tm[0:n, g * 128:(g + 1) * 128], xdtd[b2][0:n, :], start=True, stop=True),
                      r=[Btm_b, xdtd_b[b2]], w=[pst_b])
                kb.op(kb.pool, lambda: G.tensor_tensor(x3(stmp[b2][:, :]), x3(S32[:, gsl]), bc(sm[:, CD, hsl], [128, 4, 64], 2), ALU.mult),
                      r=[S32_b[g], sm_b[CD]], w=[stmp_b[b2]])
                kb.op(kb.dve, lambda: V.tensor_tensor(S32[:, gsl], stmp[b2][:, :], miscb[:, 256:512], ALU.add), r=[stmp_b[b2], pst_b], w=[S32_b[g]])
                kb.op(kb.act, lambda: A.copy(Sbf[:, gsl], S32[:, gsl]), r=[S32_b[g]], w=[Sbf_b[g]])
            for cc in range(16):
                kb.op(kb.pe, lambda: T.transpose(tpg[:, cc, 0:n], gn[0:n, cc * 128:(cc + 1) * 128], self.ident[0:n, 0:n]),
                      r=[gn_b, self.cb_], w=[tp_b] if cc == 0 else [], wp=[tp_b] if cc else [], inc=(cc == 15))
            kb.op(kb.dve, lambda: V.tensor_tensor(gT[:, :, 0:n], tpg[:, :, 0:n], bc(sng[:, :], [128, 16, n], 2), ALU.mult), r=[tp_b, par_b], w=[gT_b])
            for hh in range(2):
                po = dTb[0:n, :] if hh == 0 else yb[0:n, :]
                pbufs = [dT_b] if hh == 0 else [ydg_b]
                for cc in range(16):
                    kb.op(kb.pe, lambda: T.matmul(po, gT[:, cc, 0:n], w_out[:, cc, hh * 512:(hh + 1) * 512], start=(cc == 0), stop=(cc == 15)),
                          r=[gT_b, wout_b], w=pbufs if cc == 0 else [], wp=pbufs if cc else [], inc=(cc == 15))
                kb.op(kb.dve, lambda: V.tensor_tensor(hs[cur][0:n, hh * 512:(hh + 1) * 512], po, hs[cur][0:n, hh * 512:(hh + 1) * 512], ALU.add),
                      r=pbufs + [hs_b[cur]], wp=[hs_b[cur]])
            dap = self.h_dst(dst, t0, n)
            if dap is not None:
                kb.dma("sp", dap, hs[cur][0:n, :], r=[hs_b[cur]])

    def rstd_ops(self, st, st_b, n, c0, inv_n):
        kb, nc = self.kb, self.nc
        kb.op(kb.act, lambda: nc.scalar.activation(out=st[0:n, c0 + 1:c0 + 2], in_=st[0:n, c0:c0 + 1], func=AF.Ln, bias=EPS, scale=inv_n),
              r=[st_b], wp=[st_b])
        kb.op(kb.act, lambda: nc.scalar.activation(out=st[0:n, c0 + 2:c0 + 3], in_=st[0:n, c0 + 1:c0 + 2], func=AF.Exp, scale=-0.5),
              r=[st_b], wp=[st_b])

    def phase_mla(self, es, j, li, src, dst):
        kb, nc = self.kb, self.nc
        V, A, G, T = nc.vector, nc.scalar, nc.gpsimd, nc.tensor
        oT = self.S(es, "oT", [128, 8, NT], BF16)
        oT_b = Buf()
        w_o = self.S(es, "w_o", [128, 8, D], BF16)
        w_o_b = Buf()
        gq = self.S(es, "gq", [128, 96], F32)
        gk = self.S(es, "gk", [128, 96], F32)
        par_b = Buf()
        kb.dma("sp", gq[:], self.gq_d[j], wp=[par_b])
        kb.dma("sp", gk[:], self.gk_d[j], wp=[par_b])
        with contextlib.ExitStack() as s1:
            w_in = self.S(s1, "a_win", [128, 8, 672], BF16)
            w_qb = self.S(s1, "a_wqb", [128, 3, 1536], BF16)
            w_kvb = self.S(s1, "a_wkvb", [128, 2, 2048], BF16)
            wb = Buf()
            kb.dma("pool", w_in[:], self.w_mla_in[j].rearrange("(c p) f -> p c f", p=128), wp=[wb])
            kb.dma("pool", w_qb[:], self.w_mla_qb[j].rearrange("(c p) f -> p c f", p=128), wp=[wb])
            kb.dma("pool", w_kvb[:], self.w_mla_kvb[j].rearrange("(c p) f -> p c f", p=128), wp=[wb])
            kb.dma("pool", w_o[:], self.w_mla_out[j].rearrange("(c p) f -> p c f", p=128), wp=[w_o_b])
            qag = self.S(s1, "a_qag", [128, 3], F32)
            kvag = self.S(s1, "a_kvag", [128, 2], F32)
            kb.dma("sp", qag[:], self.qag_d[j], wp=[par_b])
            kb.dma("sp", kvag[:], self.kvag_d[j], wp=[par_b])
            hs = [self.S(s1, "a_h%d" % i, [128, D], F32) for i in range(2)]
            hs_b = [Buf(), Buf()]
            tp_ps = self.P(s1, "a_tp", [128, 512], F32)
            latA = self.P(s1, "a_latA", [128, 512], F32)
            latB = self.P(s1, "a_latB", [128, 512], F32)
            big = self.P(s1, "a_big", [128, 2048], F32)
            tph_ps = self.P(s1, "a_tph", [128, 512], F32)
            latA_b, latB_b, big_b, tph_b = PB(), PB(), PB(), PB()
            nsc = self.norm_scratch(s1, "a", tp_ps)
            tp5 = tp_ps[:].bitcast(BF16)[:, 0:640].rearrange("p (c t) -> p c t", c=5)
            tp_b = nsc[7]
            tph = tph_ps[:].bitcast(BF16)[:, 0:1024].rearrange("p (c t) -> p c t", c=8)
            xnT = self.S(s1, "a_xnT", [128, 8, 128], BF16)
            xnT_b = Buf()
            st = self.S(s1, "a_st", [128, 8], F32)
            st_b = Buf()
            qln = self.S(s1, "a_qln", [128, 384], BF16)
            kvln = self.S(s1, "a_kvln", [128, 256], BF16)
            kpe = self.S(s1, "a_kpe", [128, 32], F32)
            ln_b = Buf()
            sqj = self.S(s1, "a_sqj", [128, 2048], F32)
            sqj_b = Buf()
            qlT = self.S(s1, "a_qlT", [128, 3, 128], BF16)
            kvlT = self.S(s1, "a_kvlT", [128, 2, 128], BF16)
            lT_b = Buf()
            raw = self.S(s1, "a_raw", [128, 2048], F32)
            raw_b = Buf()
            s16 = self.S(s1, "a_s16", [128, 3, 16], F32)
            s16_b = Buf()
            cs = [self.S(s1, "a_cs%d" % i, [128, 2, 16], F32) for i in range(2)]
            cs_b = [Buf(), Buf()]
            rt = self.S(s1, "a_rt", [128, 4, 256], F32)
            rt_b = [Buf() for _ in range(4)]
            kpg = self.S(s1, "a_kpg", [128, 2, 32], F32)
            kpg_b = Buf()
            qbf = self.S(s1, "a_qbf", [128, 1536], BF16)
            qbf_b = Buf()
            stg = [self.S(s1, "a_stg%d" % i, [96, 16, 128], BF16) for i in range(2)]
            stg_b = [Buf(), Buf()]
            vst = [self.S(s1, "a_vst%d" % i, [128, 8, 192], BF16) for i in range(2)]
            vst_b = [Buf(), Buf()]
            for i in range(2):
                kb.op(kb.pool, lambda: G.memset(vst[i][:], 1.0), w=[vst_b[i]])
            qTv = self.qT_d.rearrange("h p t -> p h t")
            kTv = self.kT_d.rearrange("h p t -> p h t")
            istg = 0

            def head_T(src_bf, dstv, t0, n):
                nonlocal istg
                sg = istg % 2
                istg += 1
                for half in range(2):
                    for hh in range(8):
                        h = half * 8 + hh
                        kb.op(kb.pe, lambda: T.transpose(tph[0:96, hh, 0:n], src_bf[0:n, h * 96:(h + 1) * 96], self.ident[0:n, 0:n]),
                              r=[qbf_b, self.cb_], w=[tph_b] if hh == 0 else [], wp=[tph_b] if hh else [], inc=(hh == 7))
                    kb.op(kb.act, lambda: A.copy(stg[sg][0:96, half * 8:(half + 1) * 8, 0:n], tph[0:96, :, 0:n]),
                          r=[tph_b], w=[stg_b[sg]] if half == 0 else [], wp=[stg_b[sg]] if half else [])
                kb.dma("sp", dstv[:, :, t0:t0 + n], stg[sg][0:96, :, 0:n], r=[stg_b[sg]])

            def rope(t1, t2, cosb, sinb, o1, o2, shape_n, rb, wb_, rd_bufs):
                a_, b_, c_, d_ = [rt[0:shape_n[0], i, 0:shape_n[1]] for i in range(4)]
                if len(shape_n) == 3:
                    a_, b_, c_, d_ = [rt[0:shape_n[0], i, :].rearrange("p (h f) -> p h f", h=16) for i in range(4)]
                kb.op(kb.dve, lambda: V.tensor_tensor(a_, t1, cosb, ALU.mult), r=rd_bufs, w=[rt_b[0]])
                kb.op(kb.pool, lambda: G.tensor_tensor(b_, t2, sinb, ALU.mult), r=rd_bufs, w=[rt_b[1]])
                kb.op(kb.dve, lambda: V.tensor_tensor(c_, t1, sinb, ALU.mult), r=rd_bufs, w=[rt_b[2]])
                kb.op(kb.pool, lambda: G.tensor_tensor(d_, t2, cosb, ALU.mult), r=rd_bufs, w=[rt_b[3]])
                kb.op(kb.dve, lambda: V.tensor_tensor(o1, a_, b_, ALU.subtract), r=[rt_b[0], rt_b[1]], wp=[wb_])
                kb.op(kb.pool, lambda: G.tensor_tensor(o2, c_, d_, ALU.add), r=[rt_b[2], rt_b[3]], wp=[wb_])

            for ti, (t0, n) in enumerate(TILES):
                s = ti % 2
                kb.dma("sp", hs[s][0:n, :], self.h_src(src, t0, n), w=[hs_b[s]])
                kb.dma("sp", cs[s][0:n, 0, :], self.cos_d[t0:t0 + n, :], w=[cs_b[s]])
                kb.dma("sp", cs[s][0:n, 1, :], self.sin_d[t0:t0 + n, :], wp=[cs_b[s]])
                self.norm_T(hs_b[s], hs[s][0:n, :], n, self.lnT[:, li, :], xnT[:, :, 0:n], xnT_b, nsc)
                for c in range(8):
                    kb.op(kb.pe, lambda: T.matmul(latA[0:n, 0:384], xnT[:, c, 0:n], w_in[:, c, 0:384], start=(c == 0), stop=(c == 7)),
                          r=[xnT_b, wb], w=[latA_b] if c == 0 else [], wp=[latA_b] if c else [], inc=(c == 7))
                for c in range(8):
                    kb.op(kb.pe, lambda: T.matmul(latB[0:n, 0:288], xnT[:, c, 0:n], w_in[:, c, 384:672], start=(c == 0), stop=(c == 7)),
                          r=[xnT_b, wb], w=[latB_b] if c == 0 else [], wp=[latB_b] if c else [], inc=(c == 7))
                kb.op(kb.act, lambda: A.activation(out=sqj[0:n, 0:384], in_=latA[0:n, 0:384], func=AF.Square, accum_out=st[0:n, 0:1]),
                      r=[latA_b], w=[sqj_b, st_b])
                kb.op(kb.act, lambda: A.activation(out=sqj[0:n, 0:256], in_=latB[0:n, 0:256], func=AF.Square, accum_out=st[0:n, 3:4]),
                      r=[latB_b], w=[sqj_b], wp=[st_b])
                self.rstd_ops(st, st_b, n, 0, 1.0 / 384)
                self.rstd_ops(st, st_b, n, 3, 1.0 / 256)
                kb.op(kb.dve, lambda: V.tensor_scalar(qln[0:n, :], latA[0:n, 0:384], st[0:n, 2:3], None, ALU.mult), r=[latA_b, st_b], w=[ln_b])
                kb.op(kb.dve, lambda: V.tensor_scalar(kvln[0:n, :], latB[0:n, 0:256], st[0:n, 5:6], None, ALU.mult), r=[latB_b, st_b], wp=[ln_b])
                kb.op(kb.act, lambda: A.copy(kpe[0:n, :], latB[0:n, 256:288]), r=[latB_b], wp=[ln_b])
                for c in range(5):
                    srcap = qln[0:n, c * 128:(c + 1) * 128] if c < 3 else kvln[0:n, (c - 3) * 128:(c - 2) * 128]
                    kb.op(kb.pe, lambda: T.transpose(tp5[:, c, 0:n], srcap, self.ident[0:n, 0:n]),
                          r=[ln_b, self.cb_], w=[tp_b] if c == 0 else [], wp=[tp_b] if c else [], inc=(c == 4))
                kb.op(kb.dve, lambda: V.tensor_tensor(qlT[:, :, 0:n], tp5[:, 0:3, 0:n], bc(qag[:, :], [128, 3, n], 2), ALU.mult),
                      r=[tp_b, par_b], w=[lT_b])
                kb.op(kb.dve, lambda: V.tensor_tensor(kvlT[:, :, 0:n], tp5[:, 3:5, 0:n], bc(kvag[:, :], [128, 2, n], 2), ALU.mult),
                      r=[tp_b, par_b], wp=[lT_b])
                for ct in range(3):
                    for kc in range(3):
                        kb.op(kb.pe, lambda: T.matmul(big[0:n, ct * 512:(ct + 1) * 512], qlT[:, kc, 0:n], w_qb[:, kc, ct * 512:(ct + 1) * 512],
                                                      start=(kc == 0), stop=(kc == 2)),
                              r=[lT_b, wb], w=[big_b] if (ct == 0 and kc == 0) else [], wp=[] if (ct == 0 and kc == 0) else [big_b],
                              inc=(kc == 2 and ct == 2))
                for ct in range(3):
                    kb.op(kb.act, lambda: A.copy(raw[0:n, ct * 512:(ct + 1) * 512], big[0:n, ct * 512:(ct + 1) * 512]),
                          r=[big_b], w=[raw_b] if ct == 0 else [], wp=[raw_b] if ct else [])
                raw3 = raw[0:n, 0:1536].rearrange("p (h f) -> p h f", h=16)
                sq3 = sqj[0:n, 0:1536].rearrange("p (h f) -> p h f", h=16)
                kb.op(kb.pool, lambda: G.tensor_tensor(sqj[0:n, 0:1536], raw[0:n, 0:1536], raw[0:n, 0:1536], ALU.mult), r=[raw_b], w=[sqj_b])
                kb.op(kb.dve, lambda: V.tensor_reduce(s16[0:n, 0, :], sq3, AX.X, ALU.add), r=[sqj_b], w=[s16_b])
                kb.op(kb.act, lambda: A.activation(out=s16[0:n, 1, :], in_=s16[0:n, 0, :], func=AF.Ln, bias=EPS, scale=1.0 / 96), r=[s16_b], wp=[s16_b])
                kb.op(kb.act, lambda: A.activation(out=s16[0:n, 2, :], in_=s16[0:n, 1, :], func=AF.Exp, scale=-0.5), r=[s16_b], wp=[s16_b])
                kb.op(kb.dve, lambda: V.tensor_tensor(raw3, raw3, bc(s16[0:n, 2, :], [n, 16, 96], 2), ALU.mult), r=[s16_b], w=[raw_b])
                kb.op(kb.pool, lambda: G.tensor_tensor(raw3, raw3, bc(gq[0:n, :], [n, 16, 96], 1), ALU.mult), r=[par_b], w=[raw_b])
                qb3 = qbf[0:n, :].rearrange("p (h f) -> p h f", h=16)
                cosb = bc(cs[s][0:n, 0, :], [n, 16, 16], 1)
                sinb = bc(cs[s][0:n, 1, :], [n, 16, 16], 1)
                kb.op(kb.act, lambda: A.copy(qb3[:, :, 0:64], raw3[:, :, 0:64]), r=[raw_b], w=[qbf_b])
                rope(raw3[:, :, 64:80], raw3[:, :, 80:96], cosb, sinb, qb3[:, :, 64:80], qb3[:, :, 80:96], (n, 16, 16), None, qbf_b, [raw_b, cs_b[s]])
                head_T(qbf, qTv, t0, n)
                for ct in range(4):
                    for kc in range(2):
                        kb.op(kb.pe, lambda: T.matmul(big[0:n, ct * 512:(ct + 1) * 512], kvlT[:, kc, 0:n], w_kvb[:, kc, ct * 512:(ct + 1) * 512],
                                                      start=(kc == 0), stop=(kc == 1)),
                              r=[lT_b, wb], w=[big_b] if (ct == 0 and kc == 0) else [], wp=[] if (ct == 0 and kc == 0) else [big_b],
                              inc=(kc == 1 and ct == 3))
                for ct in range(4):
                    kb.op(kb.act, lambda: A.copy(raw[0:n, ct * 512:(ct + 1) * 512], big[0:n, ct * 512:(ct + 1) * 512]),
                          r=[big_b], w=[raw_b] if ct == 0 else [], wp=[raw_b] if ct else [])
                kv4 = raw[0:n, :].rearrange("p (h f) -> p h f", h=16)
                sqk = sqj[0:n, 0:1024].rearrange("p (h f) -> p h f", h=16)
                kb.op(kb.pool, lambda: G.tensor_tensor(sqk, kv4[:, :, 0:64], kv4[:, :, 0:64], ALU.mult), r=[raw_b], w=[sqj_b])
                kb.op(kb.dve, lambda: V.tensor_reduce(s16[0:n, 0, :], sqk, AX.X, ALU.add), r=[sqj_b], w=[s16_b])
                kb.op(kb.act, lambda: A.activation(out=kpg[0:n, 1, :], in_=kpe[0:n, :], func=AF.Square, accum_out=st[0:n, 6:7]),
                      r=[ln_b], w=[kpg_b], wp=[st_b])
                kb.op(kb.dve, lambda: V.tensor_scalar(s16[0:n, 0, :], s16[0:n, 0, :], st[0:n, 6:7], None, ALU.add), r=[st_b, s16_b], wp=[s16_b])
                kb.op(kb.act, lambda: A.activation(out=s16[0:n, 1, :], in_=s16[0:n, 0, :], func=AF.Ln, bias=EPS, scale=1.0 / 96), r=[s16_b], wp=[s16_b])
                kb.op(kb.act, lambda: A.activation(out=s16[0:n, 2, :], in_=s16[0:n, 1, :], func=AF.Exp, scale=-0.5), r=[s16_b], wp=[s16_b])
                vs = ti % 2
                kv5 = raw[0:n, :].rearrange("p (c e f) -> p c e f", c=8, e=2)
                kb.op(kb.act, lambda: A.copy(vst[vs][0:n, :, 0:64], kv5[:, :, 0, 64:128]), r=[raw_b, vst_b[vs]], wp=[vst_b[vs]])
                kb.op(kb.dve, lambda: V.tensor_copy(vst[vs][0:n, :, 128:192], kv5[:, :, 1, 64:128]), r=[raw_b, vst_b[vs]], wp=[vst_b[vs]])
                kb.dma("sp", self.va_d[t0:t0 + n, :, :], vst[vs][0:n, :, :], r=[vst_b[vs]])
                kb.op(kb.dve, lambda: V.tensor_tensor(kv4[:, :, 0:64], kv4[:, :, 0:64], bc(s16[0:n, 2, :], [n, 16, 64], 2), ALU.mult),
                      r=[s16_b], w=[raw_b])
                kb3 = qbf[0:n, :].rearrange("p (h f) -> p h f", h=16)
                kb.op(kb.pool, lambda: G.tensor_tensor(kb3[:, :, 0:64], kv4[:, :, 0:64], bc(gk[0:n, 0:64], [n, 16, 64], 1), ALU.mult),
                      r=[raw_b, par_b], w=[qbf_b])
                kb.op(kb.dve, lambda: V.tensor_tensor(kpg[0:n, 0, :], kpe[0:n, :], gk[0:n, 64:96], ALU.mult), r=[ln_b, par_b], w=[kpg_b])
                rope(kpg[0:n, 0, 0:16], kpg[0:n, 0, 16:32], cs[s][0:n, 0, :], cs[s][0:n, 1, :], kpg[0:n, 1, 0:16], kpg[0:n, 1, 16:32],
                     (n, 16), None, kpg_b, [kpg_b, cs_b[s]])
                kb.op(kb.dve, lambda: V.tensor_tensor(kb3[:, :, 64:96], bc(kpg[0:n, 1, :], [n, 16, 32], 1), bc(s16[0:n, 2, :], [n, 16, 32], 2), ALU.mult),
                      r=[kpg_b, s16_b], wp=[qbf_b])
                head_T(qbf, kTv, t0, n)
            kb.barrier()
        with contextlib.ExitStack() as s2:
            qh = [self.S(s2, "b_q%d" % i, [96, NT], BF16) for i in range(2)]
            kh = [self.S(s2, "b_k%d" % i, [96, NT], BF16) for i in range(2)]
            qk_b = [Buf(), Buf()]
            va = [self.S(s2, "b_va%d" % i, [128, 33, 192], BF16) for i in range(2)]
            va_b = [Buf(), Buf()]
            pT = [self.S(s2, "b_pT%d" % i, [128, 512], BF16) for i in range(4)]
            pT_b = [Buf() for _ in range(4)]
            rden = [self.S(s2, "b_rd%d" % i, [128, 512], F32) for i in range(2)]
            rdsh = [self.S(s2, "b_rs%d" % i, [128, 512], F32) for i in range(2)]
            rd_b = [Buf(), Buf()]
            rs_b = [Buf(), Buf()]
            bnd = self.S(s2, "b_bnd", [128, 8], F32)
            bnd_b = Buf()
            ps = [self.P(s2, "b_ps%d" % i, [128, 512], F32) for i in range(4)]
            ps_b = [PB() for _ in range(4)]
            po = [self.P(s2, "b_po%d" % i, [128, 512], F32) for i in range(2)]
            po_b = [PB(), PB()]
            kb.op(kb.dve, lambda: V.tensor_reduce(bnd[:, 0:1], gq[:, :], AX.X, ALU.max), r=[par_b], w=[bnd_b])
            kb.op(kb.dve, lambda: V.tensor_reduce(bnd[:, 1:2], gq[:, :], AX.X, ALU.min), r=[par_b], wp=[bnd_b])
            kb.op(kb.dve, lambda: V.tensor_reduce(bnd[:, 2:3], gk[:, :], AX.X, ALU.max), r=[par_b], wp=[bnd_b])
            kb.op(kb.dve, lambda: V.tensor_reduce(bnd[:, 3:4], gk[:, :], AX.X, ALU.min), r=[par_b], wp=[bnd_b])
            kb.op(kb.dve, lambda: V.scalar_tensor_tensor(bnd[:, 4:5], bnd[:, 1:2], -1.0, bnd[:, 0:1], ALU.mult, ALU.max), r=[bnd_b], wp=[bnd_b])
            kb.op(kb.dve, lambda: V.scalar_tensor_tensor(bnd[:, 5:6], bnd[:, 3:4], -1.0, bnd[:, 2:3], ALU.mult, ALU.max), r=[bnd_b], wp=[bnd_b])
            kb.op(kb.dve, lambda: V.scalar_tensor_tensor(bnd[:, 6:7], bnd[:, 4:5], -float(np.sqrt(96.0)), bnd[:, 5:6], ALU.mult, ALU.mult),
                  r=[bnd_b], wp=[bnd_b])
            negB = bnd[:, 6:7]
            scale = float(96.0 ** -0.5)

            def load_head(h):
                b = h % 2
                kb.dma("sp", qh[b][:, :], self.qT_d[h], w=[qk_b[b]])
                kb.dma("sp", kh[b][:, :], self.kT_d[h], wp=[qk_b[b]])

            def load_pair(c):
                b = c % 2
                kb.dma("sp", va[b][0:16, 0, :], self.va_d[0:16, c, :], w=[va_b[b]])
                vv = self.va_d[16:NT, c, :].rearrange("(j p) w -> p j w", p=128)
                for jj in range(0, 32, 8):
                    kb.dma("sp", va[b][:, 1 + jj:9 + jj, :], vv[:, jj:jj + 8, :], wp=[va_b[b]])

            load_pair(0)
            load_head(0)
            items = []
            for h in range(MLA_H):
                for qi in range(9):
                    if qi == 0:
                        q0, nq = 0, 16
                        kts = [(0, 0, 16, 0, True)]
                    else:
                        q0, nq = 16 + 512 * (qi - 1), 512
                        kts = [(0, 0, 16, 0, False)] + [(kt, 16 + 128 * (kt - 1), 128, 0, False) for kt in range(1, 4 * (qi - 1) + 1)]
                        kts += [(4 * (qi - 1) + 1 + i, 16 + 128 * (4 * (qi - 1) + i), 128, 128 * i, True) for i in range(4)]
                    for idx, kt in enumerate(kts):
                        items.append((h, qi, q0, nq, idx, len(kts)) + kt)
            LA = 2
            NI = len(items)
            for i in range(NI + LA):
                if i < NI:
                    (h, qi, q0, nq, idx, nk_t, kt, k0, nk, qoff, diag) = items[i]
                    if qi == 0 and idx == 0 and h + 1 < MLA_H:
                        load_head(h + 1)
                        if h % 2 == 1:
                            load_pair(h // 2 + 1)
                    hb_ = h % 2
                    nqq = nq - qoff
                    p = i % 4
                    kb.op(kb.pe, lambda: T.matmul(ps[p][0:nk, 0:nqq], kh[hb_][:, k0:k0 + nk], qh[hb_][:, q0 + qoff:q0 + nq], start=True, stop=True),
                          r=[qk_b[hb_]], w=[ps_b[p]])
                    kb.op(kb.act, lambda: A.activation(out=pT[p][0:nk, 0:nqq], in_=ps[p][0:nk, 0:nqq], func=AF.Exp, bias=negB[0:nk, :], scale=scale),
                          r=[ps_b[p], bnd_b], w=[pT_b[p]])
                    if diag:
                        kb.op(kb.dve, lambda: V.tensor_tensor(pT[p][0:nk, 0:nk], pT[p][0:nk, 0:nk], self.tri16[0:nk, 0:nk], ALU.mult),
                              r=[self.cb_], w=[pT_b[p]])
                ii = i - LA
                if ii >= 0:
                    (h, qi, q0, nq, idx, nk_t, kt, k0, nk, qoff, diag) = items[ii]
                    c, e = h // 2, h % 2
                    vb_ = c % 2
                    dlo, dhi = (0, 64) if e == 0 else (64, 128)
                    nlo, nhi = (64, 128) if e == 0 else (0, 64)
                    nqq = nq - qoff
                    p = ii % 4
                    pp = (h * 9 + qi) % 2
                    first, last = idx == 0, idx == nk_t - 1
                    kb.op(kb.pe, lambda: T.matmul(po[pp][:, qoff:nq], va[vb_][0:nk, kt, e * 64:e * 64 + 128], pT[p][0:nk, 0:nqq], start=first, stop=last),
                          r=[pT_b[p], va_b[vb_]], w=[po_b[pp]] if first else [], wp=[] if first else [po_b[pp]], inc=last)
                    if last:
                        kb.op(kb.dve, lambda: V.reciprocal(rden[pp][nlo:nhi, 0:nq], po[pp][nlo:nhi, 0:nq]), r=[po_b[pp]], w=[rd_b[pp]])
                        kb.op(kb.act, lambda: A.copy(rdsh[pp][dlo:dhi, 0:nq], rden[pp][nlo:nhi, 0:nq]), r=[rd_b[pp]], w=[rs_b[pp]])
                        kb.op(kb.dve, lambda: V.tensor_tensor(oT[dlo:dhi, c, q0:q0 + nq], po[pp][dlo:dhi, 0:nq], rdsh[pp][dlo:dhi, 0:nq], ALU.mult),
                              r=[po_b[pp], rs_b[pp]], wp=[oT_b])
            kb.barrier()
        with contextlib.ExitStack() as s3:
            hs = [self.S(s3, "c_h%d" % i, [128, D], F32) for i in range(3)]
            hs_b = [Buf() for _ in range(3)]
            po = [self.P(s3, "c_po%d" % i, [128, 512], F32) for i in range(4)]
            po_b = [PB() for _ in range(4)]
            ip = 0
            for ti, (t0, n) in enumerate(TILES):
                s = ti % 3
                kb.dma("sp", hs[s][0:n, :], self.h_src(src, t0, n), w=[hs_b[s]])
                for hh in range(2):
                    p = ip % 4
                    ip += 1
                    for c in range(8):
                        kb.op(kb.pe, lambda: T.matmul(po[p][0:n, :], oT[:, c, t0:t0 + n], w_o[:, c, hh * 512:(hh + 1) * 512], start=(c == 0), stop=(c == 7)),
                              r=[oT_b, w_o_b], w=[po_b[p]] if c == 0 else [], wp=[po_b[p]] if c else [], inc=(c == 7))
                    kb.op(kb.dve, lambda: V.tensor_tensor(hs[s][0:n, hh * 512:(hh + 1) * 512], po[p][0:n, :], hs[s][0:n, hh * 512:(hh + 1) * 512], ALU.add),
                          r=[po_b[p], hs_b[s]], wp=[hs_b[s]])
                dap = self.h_dst(dst, t0, n)
                if dap is not None:
                    kb.dma("sp", dap, hs[s][0:n, :], r=[hs_b[s]])


def host_inputs(inp):
    f = lambda a: np.ascontiguousarray(np.asarray(a, dtype=np.float32))
    rep = lambda a: np.ascontiguousarray(np.broadcast_to(np.asarray(a, np.float32)[:, None, :], (a.shape[0], 128, a.shape[1])))
    colT = lambda a, c: np.ascontiguousarray(np.asarray(a, np.float32).reshape(a.shape[0], c, 128).transpose(0, 2, 1))
    ln = np.concatenate([np.asarray(inp["ln_mix"], np.float32), np.asarray(inp["ln_mlp"], np.float32)], 0)
    lnT = np.ascontiguousarray(ln.reshape(8, 8, 128).transpose(2, 0, 1))
    cw = np.asarray(inp["ssd_conv_w"], np.float32)
    cwT = np.ascontiguousarray(cw.reshape(2, 4, 32, 128).transpose(0, 3, 2, 1))
    k = np.arange(128)
    tri = (k[:, None] <= k[None, :]).astype(np.float32)
    inv = 1.0 / (10000.0 ** (np.arange(0, 32, 2, dtype=np.float32) / 32.0))
    ang = np.arange(NT, dtype=np.float32)[:, None] * inv[None, :].astype(np.float32)
    common = {
        "meta": f(inp["meta_tokens"]),
        "ssd_w_in": f(inp["ssd_w_in"]), "ssd_w_out": f(inp["ssd_w_out"]),
        "mla_w_in": f(inp["mla_w_in"]), "mla_w_q_b": f(inp["mla_w_q_b"]), "mla_w_kv_b": f(inp["mla_w_kv_b"]),
        "mla_w_out": f(inp["mla_w_out"]), "mlp_w_up": f(inp["mlp_w_up"]), "mlp_w_down": f(inp["mlp_w_down"]),
        "lnT": lnT, "cw": cwT, "cb": colT(inp["ssd_conv_b"], 32),
        "dtb_rep": rep(inp["ssd_dt_bias"]), "alog_rep": rep(inp["ssd_a_log"]), "dskip_rep": rep(inp["ssd_d"]),
        "ssd_normT": colT(inp["ssd_norm"], 16), "q_a_T": colT(inp["mla_q_a_norm"], 3), "kv_a_T": colT(inp["mla_kv_a_norm"], 2),
        "gq_rep": rep(inp["mla_q_norm"]), "gk_rep": rep(inp["mla_k_norm"]),
        "ident": np.eye(128, dtype=np.float32), "tri": tri, "ltri": np.ascontiguousarray(1.0 - tri),
        "ones": np.ones((128, 128), np.float32),
        "cos": np.cos(ang).astype(np.float32), "sin": np.sin(ang).astype(np.float32),
    }
    return common


FULL_PHASES = [
    ("ssd", 0, 0, "x", "h"), ("mlp", 0, "h", "h"),
    ("mla", 0, 1, "h", "h"), ("mlp", 1, "h", "h"),
    ("ssd", 1, 2, "h", "h"), ("mlp", 2, "h", "h"),
    ("mla", 1, 3, "h", "h"), ("mlp", 3, "h", "y"),
]


def run(inputs, phases, cores=8):
    common = host_inputs(inputs)
    x = np.asarray(inputs["x"], np.float32)
    prog = Prog(phases)
    in_maps = []
    for c in range(cores):
        m = dict(common)
        m["x"] = np.ascontiguousarray(x[c])
        in_maps.append(m)
    res = run_bass_kernel_spmd(prog.nc, in_maps, core_ids=list(range(cores)))
    return np.stack([np.asarray(r["y"]) for r in res.results], 0)


def kernel(**inputs):
    return run(inputs, FULL_PHASES, 8).astype(np.float32)
```

```python
import contextlib
import numpy as np
import concourse.bass as bass
import concourse.mybir as mybir
from concourse.bass_utils import run_bass_kernel_spmd

F32, BF16 = mybir.dt.float32, mybir.dt.bfloat16
AF = mybir.ActivationFunctionType
ALU = mybir.AluOpType
AX = mybir.AxisListType

NT, NM, D, SEQ = 4112, 16, 1024, 4096
TILES = [(0, 16)] + [(16 + 128 * j, 128) for j in range(32)]
EPS = 1e-6
DFF = 4096
SSD_IN = 6176
NH_S = 32
MLA_H = 16
QK = 96


class Buf:
    __slots__ = ("w", "r", "name", "ps")

    def __init__(self, name="", ps=False):
        self.w = {}
        self.r = {}
        self.name = name
        self.ps = ps


def PB():
    return Buf(ps=True)


class Eng:
    def __init__(self, name, e, sem):
        self.name, self.e, self.sem, self.cnt, self.seen = name, e, sem, 0, {}


class KB:
    def __init__(self, nc, es):
        self.nc = nc
        mk = lambda n: es.enter_context(nc.semaphore(n))
        self.pe = Eng("pe", nc.tensor, mk("s_pe"))
        self.act = Eng("act", nc.scalar, mk("s_act"))
        self.dve = Eng("dve", nc.vector, mk("s_dve"))
        self.pool = Eng("pool", nc.gpsimd, mk("s_pool"))
        self.sp = Eng("sp", nc.sync, mk("s_sp"))
        self.engs = [self.pe, self.act, self.dve, self.pool, self.sp]
        self.dsem = {"sp": [[mk("d_sp%d" % i), 0] for i in range(24)],
                     "pool": [[mk("d_pl%d" % i), 0] for i in range(8)]}
        self.drr = {"sp": 0, "pool": 0}
        self.nins = 0

    def _wait(self, E, toks):
        for key, (sem, val) in toks.items():
            if E is self.pe and key == "pe":
                continue
            if E.seen.get(key, 0) >= val:
                continue
            E.e.wait_ge(sem, val)
            E.seen[key] = val

    @staticmethod
    def _add(need, d):
        for k, sv in d.items():
            if k not in need or need[k][1] < sv[1]:
                need[k] = sv

    def _deps(self, r, w, wp, ekey=None):
        need = {}
        for b in r:
            self._add(need, b.w)
            if b.ps:
                self._add(need, {k: v for k, v in b.r.items() if k != ekey})
        for b in w:
            self._add(need, b.w)
            self._add(need, b.r)
        for b in wp:
            self._add(need, b.r)
            if b.ps:
                self._add(need, b.w)
        return need

    def _reg(self, key, tok, r, w, wp):
        for b in r:
            if key not in b.r or b.r[key][1] < tok[1]:
                b.r[key] = tok
        for b in w:
            b.w = {key: tok}
            b.r = {}
        for b in wp:
            if key not in b.w or b.w[key][1] < tok[1]:
                b.w[key] = tok

    def op(self, E, fn, r=(), w=(), wp=(), inc=True):
        self._wait(E, self._deps(r, w, wp, E.name))
        ins = fn()
        self.nins += 1
        if inc:
            E.cnt += 1
            ins.then_inc(E.sem, 1)
            tok = (E.sem, E.cnt)
        else:
            tok = (E.sem, E.cnt + 1)
        self._reg(E.name, tok, r, w, wp)

    def dma(self, Q, out, in_, r=(), w=(), wp=()):
        E = self.sp if Q == "sp" else self.pool
        self._wait(E, self._deps(r, w, wp))
        lst = self.dsem[Q]
        i = self.drr[Q]
        self.drr[Q] = (i + 1) % len(lst)
        sem, cnt = lst[i]
        key = (Q, i)
        if cnt > 0 and E.seen.get(key, 0) < cnt:
            E.e.wait_ge(sem, cnt)
            E.seen[key] = cnt
        ins = E.e.dma_start(out=out, in_=in_)
        ins.then_inc(sem, 16)
        self.nins += 1
        lst[i][1] = cnt + 16
        self._reg(key, (sem, cnt + 16), r, w, wp)

    def barrier(self):
        toks = {}
        for E in self.engs:
            if E.cnt > 0:
                toks[E.name] = (E.sem, E.cnt)
        for Q, lst in self.dsem.items():
            for i, (sem, cnt) in enumerate(lst):
                if cnt > 0:
                    toks[(Q, i)] = (sem, cnt)
        for E in self.engs:
            self._wait(E, toks)


def bc(ap, shape, axis):
    return ap.unsqueeze(axis).to_broadcast(list(shape))


class Prog:
    def __init__(self, phases):
        self.phases = phases
        nc = bass.Bass("TRN2", target_bir_lowering=False)
        self.nc = nc
        di = lambda name, shape: nc.dram_tensor(name, list(shape), F32, kind="ExternalInput").ap()
        self.x = di("x", [SEQ, D])
        self.meta = di("meta", [NM, D])
        self.w_ssd_in = di("ssd_w_in", [2, D, SSD_IN])
        self.w_ssd_out = di("ssd_w_out", [2, 2048, D])
        self.w_mla_in = di("mla_w_in", [2, D, 672])
        self.w_mla_qb = di("mla_w_q_b", [2, 384, 1536])
        self.w_mla_kvb = di("mla_w_kv_b", [2, 256, 2048])
        self.w_mla_out = di("mla_w_out", [2, D, D])
        self.w_up = di("mlp_w_up", [4, D, DFF])
        self.w_dn = di("mlp_w_down", [4, DFF, D])
        self.lnT_d = di("lnT", [128, 8, 8])
        self.cw_d = di("cw", [2, 128, 32, 4])
        self.cb_d = di("cb", [2, 128, 32])
        self.dtb_d = di("dtb_rep", [2, 128, 32])
        self.alog_d = di("alog_rep", [2, 128, 32])
        self.dsk_d = di("dskip_rep", [2, 128, 32])
        self.sng_d = di("ssd_normT", [2, 128, 16])
        self.qag_d = di("q_a_T", [2, 128, 3])
        self.kvag_d = di("kv_a_T", [2, 128, 2])
        self.gq_d = di("gq_rep", [2, 128, 96])
        self.gk_d = di("gk_rep", [2, 128, 96])
        self.ident_d = di("ident", [128, 128])
        self.tri_d = di("tri", [128, 128])
        self.ltri_d = di("ltri", [128, 128])
        self.ones_d = di("ones", [128, 128])
        self.cos_d = di("cos", [NT, 16])
        self.sin_d = di("sin", [NT, 16])
        self.y = nc.dram_tensor("y", [SEQ, D], F32, kind="ExternalOutput").ap()
        self.hd = nc.dram_tensor("hd", [NT, D], F32, kind="Internal").ap()
        self.qT_d = nc.dram_tensor("qT_d", [MLA_H, QK, NT], BF16, kind="Internal").ap()
        self.kT_d = nc.dram_tensor("kT_d", [MLA_H, QK, NT], BF16, kind="Internal").ap()
        self.va_d = nc.dram_tensor("va_d", [NT, 8, 192], BF16, kind="Internal").ap()

        with contextlib.ExitStack() as es:
            self.kb = KB(nc, es)
            self.build(es)

    def S(self, es, name, shape, dt):
        self.uid = getattr(self, "uid", 0) + 1
        return es.enter_context(self.nc.sbuf_tensor("sb%d_%s" % (self.uid, name), list(shape), dt))

    def P(self, es, name, shape, dt):
        self.uid = getattr(self, "uid", 0) + 1
        return es.enter_context(self.nc.psum_tensor("ps%d_%s" % (self.uid, name), list(shape), dt))

    def h_src(self, kind, t0, n):
        if kind == "x":
            return self.meta[0:16, :] if t0 == 0 else self.x[t0 - 16:t0 - 16 + n, :]
        return self.hd[t0:t0 + n, :]

    def h_dst(self, kind, t0, n):
        if kind == "y":
            return None if t0 == 0 else self.y[t0 - 16:t0 - 16 + n, :]
        return self.hd[t0:t0 + n, :]

    def build(self, es):
        kb, nc = self.kb, self.nc
        self.ident = self.S(es, "ident", [128, 128], BF16)
        self.tri32 = self.S(es, "tri32", [128, 128], F32)
        self.tri16 = self.S(es, "tri16", [128, 128], BF16)
        self.ltri32 = self.S(es, "ltri32", [128, 128], F32)
        self.ones32 = self.S(es, "ones32", [128, 128], F32)
        self.lnT = self.S(es, "lnT", [128, 8, 8], F32)
        self.cb_ = Buf("consts")
        kb.dma("pool", self.ident[:], self.ident_d, wp=[self.cb_])
        kb.dma("pool", self.tri16[:], self.tri_d, wp=[self.cb_])
        kb.dma("sp", self.tri32[:], self.tri_d, wp=[self.cb_])
        kb.dma("sp", self.ltri32[:], self.ltri_d, wp=[self.cb_])
        kb.dma("sp", self.ones32[:], self.ones_d, wp=[self.cb_])
        kb.dma("sp", self.lnT[:], self.lnT_d, wp=[self.cb_])
        for ph in self.phases:
            kind = ph[0]
            with contextlib.ExitStack() as pes:
                if kind == "mlp":
                    self.phase_mlp(pes, *ph[1:])
                elif kind == "ssd":
                    self.phase_ssd(pes, *ph[1:])
                elif kind == "mla":
                    self.phase_mla(pes, *ph[1:])
                kb.barrier()
        kb.barrier()

    def rstd_newton(self, st, st_b, n, inv_n, eps):
        kb, nc = self.kb, self.nc
        V = nc.vector
        I32 = mybir.dt.int32
        x, g, t = st[0:n, 1:2], st[0:n, 2:3], st[0:n, 3:4]
        kb.op(kb.dve, lambda: V.tensor_scalar(x, st[0:n, 0:1], inv_n, eps, ALU.mult, ALU.add), r=[st_b], wp=[st_b])
        kb.op(kb.dve, lambda: V.tensor_scalar(g.bitcast(I32), x.bitcast(I32), 1, None, ALU.arith_shift_right), r=[st_b], wp=[st_b])
        kb.op(kb.dve, lambda: V.tensor_scalar(g.bitcast(I32), g.bitcast(I32), -1, 0x5f3759df, ALU.mult, ALU.add), r=[st_b], wp=[st_b])
        for _ in range(2):
            kb.op(kb.dve, lambda: V.scalar_tensor_tensor(t, g, x, g, ALU.mult, ALU.mult), r=[st_b], wp=[st_b])
            kb.op(kb.dve, lambda: V.tensor_scalar(t, t, -0.5, 1.5, ALU.mult, ALU.add), r=[st_b], wp=[st_b])
            kb.op(kb.dve, lambda: V.tensor_tensor(g, g, t, ALU.mult), r=[st_b], wp=[st_b])

    def norm_T(self, hb, h_ap, n, gain_ap, dst_ap, dst_b, sc, newton=False):
        kb, nc = self.kb, self.nc
        junk, junk_b, ss, ss_b, xn, xn_b, tp, tp_b = sc
        kb.op(kb.act, lambda: nc.scalar.activation(out=junk[0:n, :], in_=h_ap, func=AF.Square, accum_out=ss[0:n, 0:1]),
              r=[hb], w=[junk_b, ss_b] if junk_b is not xn_b else [xn_b, ss_b])
        if newton:
            self.rstd_newton(ss, ss_b, n, 1.0 / D, EPS)
        else:
            kb.op(kb.act, lambda: nc.scalar.activation(out=ss[0:n, 1:2], in_=ss[0:n, 0:1], func=AF.Ln, bias=EPS, scale=1.0 / D),
                  r=[ss_b], wp=[ss_b])
            kb.op(kb.act, lambda: nc.scalar.activation(out=ss[0:n, 2:3], in_=ss[0:n, 1:2], func=AF.Exp, scale=-0.5),
                  r=[ss_b], wp=[ss_b])
        kb.op(kb.dve, lambda: nc.vector.tensor_scalar(xn[0:n, :], h_ap, ss[0:n, 2:3], None, ALU.mult),
              r=[hb, ss_b], w=[xn_b])
        for c in range(8):
            kb.op(kb.pe, lambda c=c: nc.tensor.transpose(tp[:, c, 0:n], xn[0:n, c * 128:(c + 1) * 128], self.ident[0:n, 0:n]),
                  r=[xn_b, self.cb_], w=[tp_b] if c == 0 else [], wp=[tp_b] if c else [], inc=(c == 7))
        kb.op(kb.dve, lambda: nc.vector.tensor_tensor(dst_ap, tp[:, :, 0:n], bc(gain_ap, [128, 8, n], 2), ALU.mult),
              r=[tp_b, self.cb_], w=[dst_b])

    def norm_scratch(self, es, pfx, tp_ps):
        ss = self.S(es, pfx + "ss", [128, 4], F32)
        xn = self.S(es, pfx + "xn", [128, 1024], BF16)
        tp = tp_ps[:].bitcast(BF16)[:, 0:1024].rearrange("p (c t) -> p c t", c=8)
        xn_b = Buf()
        return (xn, xn_b, ss, Buf(), xn, xn_b, tp, PB())

    def phase_mlp(self, es, li, src, dst):
        kb, nc = self.kb, self.nc
        wup = self.S(es, "wup", [128, 8, DFF], BF16)
        wdn = self.S(es, "wdn", [128, 32, D], BF16)
        wup_b, wdn_b = Buf(), Buf()
        upv = self.w_up[li].rearrange("(c p) f -> p c f", p=128)
        dnv = self.w_dn[li].rearrange("(c p) d -> p c d", p=128)
        for c in range(8):
            kb.dma("pool", wup[:, c, :], upv[:, c, :], wp=[wup_b])
        for c in range(0, 32, 4):
            kb.dma("pool", wdn[:, c:c + 4, :], dnv[:, c:c + 4, :], wp=[wdn_b])
        NSLOT = 7
        hs = [self.S(es, "mh%d" % i, [128, D], F32) for i in range(NSLOT)]
        hs_b = [Buf() for _ in range(NSLOT)]
        xnT = self.S(es, "m_xnT", [128, 8, 512], BF16)
        xnT_b = Buf()
        uT = self.S(es, "m_uT", [128, 32, 512], BF16)
        uT_b = [Buf() for _ in range(32)]
        r32 = [self.S(es, "m_r32_%d" % i, [128, 512], F32) for i in range(2)]
        r32_b = [Buf(), Buf()]
        tp_ps = [self.P(es, "m_tp%d" % i, [128, 512], F32) for i in range(2)]
        pu = [self.P(es, "m_pu%d" % i, [128, 512], F32) for i in range(3)]
        pu_b = [PB() for _ in range(3)]
        pd = [self.P(es, "m_pd%d" % i, [128, 512], F32) for i in range(3)]
        pd_b = [PB() for _ in range(3)]
        nsc = [self.norm_scratch(es, "m%d" % i, tp_ps[i]) for i in range(2)]
        gain = self.lnT[:, 4 + li, :]
        groups = [[TILES[0]]] + [TILES[1 + 4 * g:5 + 4 * g] for g in range(8)]
        slot = 0
        iu = 0
        ipd = 0
        inorm = 0
        for grp in groups:
            ntok = sum(n for _, n in grp)
            myslots = []
            for (t0, n) in grp:
                s = slot % NSLOT
                slot += 1
                myslots.append(s)
                kb.dma("sp", hs[s][0:n, :], self.h_src(src, t0, n), w=[hs_b[s]])
            off = 0
            for (t0, n), s in zip(grp, myslots):
                self.norm_T(hs_b[s], hs[s][0:n, :], n, gain, xnT[:, :, off:off + n], xnT_b, nsc[inorm % 2])
                inorm += 1
                off += n
            for fc in range(32):
                p = iu % 3
                for c in range(8):
                    kb.op(kb.pe, lambda c=c, fc=fc, p=p: nc.tensor.matmul(pu[p][:, 0:ntok], wup[:, c, fc * 128:(fc + 1) * 128],
                                                                      xnT[:, c, 0:ntok], start=(c == 0), stop=(c == 7)),
                          r=[wup_b, xnT_b], w=[pu_b[p]] if c == 0 else [], wp=[pu_b[p]] if c else [], inc=(c == 7))
                rr = iu % 2
                kb.op(kb.act, lambda p=p, rr=rr: nc.scalar.activation(out=r32[rr][:, 0:ntok], in_=pu[p][:, 0:ntok], func=AF.Relu),
                      r=[pu_b[p]], w=[r32_b[rr]])
                E = kb.dve if (fc % 2 == 0) else kb.pool
                kb.op(E, lambda rr=rr, fc=fc, E=E: E.e.tensor_tensor(uT[:, fc, 0:ntok], r32[rr][:, 0:ntok], r32[rr][:, 0:ntok], ALU.mult),
                      r=[r32_b[rr]], w=[uT_b[fc]])
                iu += 1
            off = 0
            for (t0, n), s in zip(grp, myslots):
                for hh in range(2):
                    p = ipd % 3
                    ipd += 1
                    for fc in range(32):
                        kb.op(kb.pe, lambda fc=fc, p=p, off=off, n=n, hh=hh: nc.tensor.matmul(
                            pd[p][0:n, :], uT[:, fc, off:off + n], wdn[:, fc, hh * 512:(hh + 1) * 512],
                            start=(fc == 0), stop=(fc == 31)),
                            r=[uT_b[fc], wdn_b], w=[pd_b[p]] if fc == 0 else [], wp=[pd_b[p]] if fc else [], inc=(fc == 31))
                    kb.op(kb.dve, lambda p=p, n=n, hh=hh, s=s: nc.vector.tensor_tensor(
                        hs[s][0:n, hh * 512:(hh + 1) * 512], pd[p][0:n, :], hs[s][0:n, hh * 512:(hh + 1) * 512], ALU.add),
                        r=[pd_b[p], hs_b[s]], wp=[hs_b[s]])
                dst_ap = self.h_dst(dst, t0, n)
                if dst_ap is not None:
                    kb.dma("sp", dst_ap, hs[s][0:n, :], r=[hs_b[s]])
                off += n

    def phase_ssd(self, es, j, li, src, dst):
        from functools import partial
        kb, nc = self.kb, self.nc
        V, A, G, T = nc.vector, nc.scalar, nc.gpsimd, nc.tensor
        w_in = self.S(es, "s_win", [128, 8, SSD_IN], BF16)
        w_out = self.S(es, "s_wout", [128, 16, D], BF16)
        win_b, wout_b, par_b = Buf(), Buf(), Buf()
        wv = self.w_ssd_in[j].rearrange("(c p) f -> p c f", p=128)
        for c in range(8):
            kb.dma("pool", w_in[:, c, :], wv[:, c, :], wp=[win_b])
        wov = self.w_ssd_out[j].rearrange("(c p) f -> p c f", p=128)
        for c in range(0, 16, 4):
            kb.dma("pool", w_out[:, c:c + 4, :], wov[:, c:c + 4, :], wp=[wout_b])
        cw = self.S(es, "s_cw", [128, 32, 4], F32)
        cb = self.S(es, "s_cb", [128, 32], F32)
        dtb = self.S(es, "s_dtb", [128, 32], F32)
        arep = self.S(es, "s_arep", [128, 32], F32)
        dsk = self.S(es, "s_dsk", [128, 32], F32)
        sng = self.S(es, "s_sng", [128, 16], F32)
        kb.dma("sp", cw[:], self.cw_d[j], wp=[par_b])
        kb.dma("sp", cb[:], self.cb_d[j], wp=[par_b])
        kb.dma("sp", dtb[:], self.dtb_d[j], wp=[par_b])
        kb.dma("sp", arep[:], self.alog_d[j], wp=[par_b])
        kb.dma("sp", dsk[:], self.dsk_d[j], wp=[par_b])
        kb.dma("sp", sng[:], self.sng_d[j], wp=[par_b])
        arep_b = Buf()
        kb.op(kb.act, lambda: A.activation(out=arep[:], in_=arep[:], func=AF.Exp), r=[par_b], w=[arep_b])
        kb.op(kb.dve, lambda: V.tensor_scalar(arep[:], arep[:], -1.0, None, ALU.mult), r=[arep_b], w=[arep_b])
        cwb = Buf()
        kb.op(kb.dve, lambda: V.tensor_scalar(cw[:], cw[:], 0.5, None, ALU.mult), r=[par_b], w=[cwb])
        kb.op(kb.dve, lambda: V.tensor_scalar(cb[:], cb[:], 0.5, None, ALU.mult), r=[par_b], wp=[cwb])
        hs = [self.S(es, "s_h%d" % i, [128, D], F32) for i in range(2)]
        hs_b = [Buf(), Buf()]
        tpF = self.P(es, "s_tpF", [128, 512], F32)
        pzs = [self.P(es, "s_pz%d" % i, [128, 512], F32) for i in range(2)]
        fm = self.P(es, "s_fm", [128, 512], F32)
        tpB = self.P(es, "s_tpB", [128, 512], F32)
        dTb = self.P(es, "s_dT", [128, 512], F32)
        bm = self.P(es, "s_bm", [128, 512], F32)
        yb = self.P(es, "s_y", [128, 512], F32)
        pz_b = [PB(), PB()]
        fm_b, tpB_b, dT_b, bm_b, y_b = PB(), PB(), PB(), PB(), PB()
        nsc = self.norm_scratch(es, "s", tpF)
        tpB16 = tpB[:].bitcast(BF16)
        tpBg = tpB16.rearrange("p (c t) -> p c t", c=8)
        xnT = [self.S(es, "s_xnT%d" % i, [128, 8, 131], BF16) for i in range(2)]
        xnT_b = [Buf(), Buf()]
        for i in range(2):
            kb.op(kb.pool, lambda: G.memset(xnT[i][:], 0.0), w=[xnT_b[i]])
        xbc_xs = self.S(es, "s_xbcxs", [128, 16, 128], BF16)
        xsT_b = [Buf() for _ in range(16)]
        xbc_bc = [self.S(es, "s_xbcbc%d" % i, [128, 16, 128], BF16) for i in range(2)]
        bc_b = [[Buf() for _ in range(16)] for _ in range(2)]
        u = [self.S(es, "s_u%d" % i, [128, 131], F32) for i in range(2)]
        u_b = [Buf(), Buf()]
        acc = [self.S(es, "s_acc%d" % i, [128, 128], F32) for i in range(4)]
        acc_b = [Buf() for _ in range(4)]
        th = [self.S(es, "s_th%d" % i, [128, 128], F32) for i in range(2)]
        th_b = [Buf(), Buf()]
        xs_tm = self.S(es, "s_xstm", [128, 2048], BF16)
        xs_b = [Buf() for _ in range(8)]
        B_tm = self.S(es, "s_Btm", [128, 1024], BF16)
        Btm_b = Buf()
        smF = self.S(es, "s_smF", [128, 4, 32], F32)
        EXPT, ACS, TMP, DTE = range(4)
        smF_b = [Buf() for _ in range(4)]
        smB = [self.S(es, "s_smB%d" % i, [128, 5, 32], F32) for i in range(2)]
        DTV, ADT, DFS, CD, W2 = range(5)
        smB_b = [[Buf() for _ in range(5)] for _ in range(2)]
        rhsD = [self.S(es, "s_rhsD%d" % i, [128, 512], F32) for i in range(2)]
        rhsD_b = [Buf(), Buf()]
        Ee = [self.S(es, "s_E%d" % i, [128, 512], F32) for i in range(2)]
        E_b = [Buf(), Buf()]
        CBm = [self.S(es, "s_CBm0", [128, 128], F32)] * 2
        _cb = Buf()
        CBm_b = [_cb, _cb]
        MT = [self.S(es, "s_MT%d" % i, [128, 512], BF16) for i in range(2)]
        MT_b = [Buf(), Buf()]
        xdt = [self.S(es, "s_xdt%d" % i, [128, 256], BF16) for i in range(2)]
        xdt_b = [Buf(), Buf()]
        xdtd = [self.S(es, "s_xdtd%d" % i, [128, 256], BF16) for i in range(2)]
        xdtd_b = [Buf(), Buf()]
        tt = [[self.S(es, "s_t%d_%d" % (k, i), [128, 256], F32) for i in range(2)] for k in range(4)]
        tt_b = [[Buf(), Buf()] for _ in range(4)]
        tt.append(tt[0])
        tt_b.append(tt_b[0])
        stg = self.S(es, "s_stg", [128, 8, 4], F32)
        stg_b = [Buf() for _ in range(8)]
        gT = self.S(es, "s_gT", [128, 16, 128], BF16)
        gT_b = Buf()
        S32 = self.S(es, "s_S32", [128, 2048], F32)
        Sbf = self.S(es, "s_Sbf", [128, 2048], BF16)
        S32_b = [Buf() for _ in range(8)]
        Sbf_b = [Buf() for _ in range(8)]
        stmp = [self.S(es, "s_stmp0", [128, 256], F32)] * 2
        _sb = Buf()
        stmp_b = [_sb, _sb]
        kb.op(kb.pool, lambda: G.memset(S32[:], 0.0), w=S32_b)
        kb.op(kb.pool, lambda: G.memset(Sbf[:], 0.0), w=Sbf_b)
        v3 = lambda ap: ap.rearrange("p (j l) -> p j l", j=4)
        NTL = len(TILES)

        def f_load(ti):
            t0, n = TILES[ti]
            cur, prev = ti % 2, (ti + 1) % 2
            nprev = TILES[ti - 1][1] if ti > 0 else 0
            X = xnT[cur]
            kb.dma("sp", hs[cur][0:n, :], self.h_src(src, t0, n), w=[hs_b[cur]])
            self.norm_T(hs_b[cur], hs[cur][0:n, :], n, self.lnT[:, li, :], X[:, :, 3:3 + n], xnT_b[cur], nsc, newton=True)
            kb.op(kb.pool, lambda: G.tensor_copy(X[:, :, 0:3], xnT[prev][:, :, nprev:nprev + 3]), r=[xnT_b[prev]], wp=[xnT_b[cur]])

        def f_dt1(ti):
            t0, n = TILES[ti]
            cur = ti % 2
            X = xnT[cur]
            sB, sBb = smB[cur], smB_b[cur]
            pdt = fm[0:n, 0:32]
            for c in range(8):
                kb.op(kb.pe, lambda: T.matmul(pdt, X[:, c, 3:3 + n], w_in[:, c, 6144:6176], start=(c == 0), stop=(c == 7)),
                      r=[win_b, xnT_b[cur]], w=[fm_b] if c == 0 else [], wp=[fm_b] if c else [], inc=(c == 7))
            kb.op(kb.dve, lambda: V.tensor_tensor(sB[0:n, DTV, :], pdt, dtb[0:n, :], ALU.add), r=[fm_b, par_b], w=[sBb[DTV]])
            kb.op(kb.act, lambda: A.activation(out=smF[0:n, EXPT, :], in_=sB[0:n, DTV, :], func=AF.Exp), r=[sBb[DTV]], w=[smF_b[EXPT]])
            kb.op(kb.act, lambda: A.activation(out=sB[0:n, DTV, :], in_=smF[0:n, EXPT, :], func=AF.Ln, bias=1.0), r=[smF_b[EXPT]], w=[sBb[DTV]])
            kb.op(kb.dve, lambda: V.tensor_tensor(sB[0:n, ADT, :], sB[0:n, DTV, :], arep[0:n, :], ALU.mult), r=[sBb[DTV], arep_b], w=[sBb[ADT]])

        def f_dt2(ti):
            t0, n = TILES[ti]
            cur = ti % 2
            sB, sBb = smB[cur], smB_b[cur]
            pacs, ptot = fm[0:n, 32:64], fm[:, 64:96]
            kb.op(kb.pe, lambda: T.matmul(pacs, self.tri32[0:n, 0:n], sB[0:n, ADT, :], start=True, stop=True), r=[sBb[ADT], self.cb_], w=[fm_b])
            kb.op(kb.pe, lambda: T.matmul(ptot, self.ones32[0:n, :], sB[0:n, ADT, :], start=True, stop=True), r=[sBb[ADT], self.cb_], wp=[fm_b])
            kb.op(kb.dve, lambda: V.tensor_copy(smF[0:n, ACS, :], pacs), r=[fm_b], w=[smF_b[ACS]])
            kb.op(kb.act, lambda: A.activation(out=sB[0:n, DFS, :], in_=pacs, func=AF.Exp), r=[fm_b], w=[sBb[DFS]])
            kb.op(kb.act, lambda: A.activation(out=sB[:, CD, :], in_=ptot, func=AF.Exp), r=[fm_b], w=[sBb[CD]])
            kb.op(kb.dve, lambda: V.tensor_tensor(smF[0:n, TMP, :], ptot[0:n, :], smF[0:n, ACS, :], ALU.subtract), r=[fm_b, smF_b[ACS]], w=[smF_b[TMP]])
            kb.op(kb.act, lambda: A.activation(out=smF[0:n, DTE, :], in_=smF[0:n, TMP, :], func=AF.Exp), r=[smF_b[TMP]], w=[smF_b[DTE]])
            kb.op(kb.dve, lambda: V.tensor_tensor(sB[0:n, W2, :], sB[0:n, DTV, :], smF[0:n, DTE, :], ALU.mult), r=[sBb[DTV], smF_b[DTE]], w=[sBb[W2]])

        def conv_dst(ti, cc, n):
            cur = ti % 2
            if cc < 16:
                return xbc_xs[:, cc, 0:n], xsT_b[cc]
            return xbc_bc[cur][:, cc - 16, 0:n], bc_b[cur][cc - 16]

        def f_conv(ti, s):
            t0, n = TILES[ti]
            cur = ti % 2
            X = xnT[cur]
            cc = s
            if 0 <= cc < 32:
                p = cc % 2
                pz = pzs[p][:, 0:3 + n]
                for c in range(8):
                    kb.op(kb.pe, lambda: T.matmul(pz, w_in[:, c, 2048 + cc * 128:2048 + (cc + 1) * 128], X[:, c, 0:3 + n], start=(c == 0), stop=(c == 7)),
                          r=[win_b, xnT_b[cur]], w=[pz_b[p]] if c == 0 else [], wp=[pz_b[p]] if c else [], inc=(c == 7))
            cc = s - 1
            if 0 <= cc < 32:
                p = cc % 2
                pz = pzs[p][:, 0:3 + n]
                kb.op(kb.act, lambda: A.activation(out=acc[cc % 4][:, 0:n], in_=pz[:, 3:3 + n], func=AF.Identity, bias=cb[:, cc:cc + 1], scale=cw[:, cc, 3:4]),
                      r=[pz_b[p], cwb], w=[acc_b[cc % 4]])
                kb.op(kb.act, lambda: A.copy(u[p][:, 0:3 + n], pz), r=[pz_b[p]], w=[u_b[p]])
            cc = s - 2
            if 0 <= cc < 32:
                p = cc % 2
                a_ = acc[cc % 4][:, 0:n]
                for k in range(3):
                    kb.op(kb.dve, lambda: V.scalar_tensor_tensor(a_, u[p][:, k:k + n], cw[:, cc, k:k + 1], a_, ALU.mult, ALU.add),
                          r=[u_b[p], cwb], w=[acc_b[cc % 4]])
            cc = s - 3
            if 0 <= cc < 32:
                kb.op(kb.act, lambda: A.activation(out=th[cc % 2][:, 0:n], in_=acc[cc % 4][:, 0:n], func=AF.Tanh), r=[acc_b[cc % 4]], w=[th_b[cc % 2]])
            cc = s - 4
            if 0 <= cc < 32:
                dst_ap, dst_b = conv_dst(ti, cc, n)
                kb.op(kb.dve, lambda: V.scalar_tensor_tensor(dst_ap, th[cc % 2][:, 0:n], 1.0, acc[cc % 4][:, 0:n], ALU.add, ALU.mult),
                      r=[th_b[cc % 2], acc_b[cc % 4]], w=[dst_b])

        def front(ti):
            st = [partial(f_load, ti), partial(f_conv, ti, 0), partial(f_dt1, ti), partial(f_conv, ti, 1), partial(f_dt2, ti)]
            st += [partial(f_conv, ti, s) for s in range(2, 36)]
            return st

        def b_trx(ti, half):
            t0, n = TILES[ti]
            for k in range(8):
                cc = half * 8 + k
                kb.op(kb.pe, lambda: T.transpose(tpB16[0:n, k * 128:(k + 1) * 128], xbc_xs[:, cc, 0:n], self.ident[:, :]),
                      r=[xsT_b[cc], self.cb_], w=[tpB_b] if k == 0 else [], wp=[tpB_b] if k else [], inc=(k == 7))
            kb.op(kb.act, lambda: A.copy(xs_tm[0:n, half * 1024:(half + 1) * 1024], tpB16[0:n, :]), r=[tpB_b], w=xs_b[4 * half:4 * half + 4])

        def b_trB(ti):
            t0, n = TILES[ti]
            cur = ti % 2
            for g in range(8):
                kb.op(kb.pe, lambda: T.transpose(tpB16[0:n, g * 128:(g + 1) * 128], xbc_bc[cur][:, g, 0:n], self.ident[:, :]),
                      r=[bc_b[cur][g], self.cb_], w=[tpB_b] if g == 0 else [], wp=[tpB_b] if g else [], inc=(g == 7))
            kb.op(kb.dve, lambda: V.tensor_copy(B_tm[0:n, :], tpB16[0:n, :]), r=[tpB_b], w=[Btm_b])

        def gchain(ti, g):
            t0, n = TILES[ti]
            cur = ti % 2
            X = xnT[cur]
            sB, sBb = smB[cur], smB_b[cur]
            b2 = g % 2
            hsl = slice(4 * g, 4 * g + 4)
            gsl = slice(g * 256, (g + 1) * 256)
            BT, CT = xbc_bc[cur][:, g, 0:n], xbc_bc[cur][:, 8 + g, 0:n]
            BTb, CTb = bc_b[cur][g], bc_b[cur][8 + g]
            x3 = lambda ap: ap.rearrange("p (j f) -> p j f", j=4)
            rD = rhsD[b2][0:n, 0:4 * n]
            Ev = Ee[b2][0:n, 0:4 * n]
            MTv = MT[b2][0:n, 0:4 * n]
            pcb = bm[0:n, 0:n]
            pst = bm[:, 256:512]
            zq = fm[0:n, 256:512]
            xs3 = x3(xs_tm[0:n, gsl])
            t1, t2, t3, szg, thg = [tt[k][b2][0:n, :] for k in range(5)]
            t1b, t2b, t3b, szb, thb = [tt_b[k][b2] for k in range(5)]
            steps = []

            def s1():
                kb.op(kb.pool, lambda: G.tensor_tensor(v3(rD), bc(sB[0:n, ADT, hsl], [n, 4, n], 2), bc(self.tri32[0:n, 0:n], [n, 4, n], 1), ALU.mult),
                      r=[sBb[ADT], self.cb_], w=[rhsD_b[b2]])
                kb.op(kb.pool, lambda: G.tensor_tensor(x3(xdt[b2][0:n, :]), xs3, bc(sB[0:n, DTV, hsl], [n, 4, 64], 2), ALU.mult),
                      r=[xs_b[g], sBb[DTV]], w=[xdt_b[b2]])
            steps.append(s1)

            def s2():
                kb.op(kb.pe, lambda: T.matmul(dTb[0:n, 0:4 * n], self.ltri32[0:n, 0:n], rD, start=True, stop=True), r=[rhsD_b[b2], self.cb_], w=[dT_b])
                kb.op(kb.act, lambda: A.activation(out=Ev, in_=dTb[0:n, 0:4 * n], func=AF.Exp), r=[dT_b], w=[E_b[b2]])
                kb.op(kb.pool, lambda: G.tensor_tensor(x3(xdtd[b2][0:n, :]), xs3, bc(sB[0:n, W2, hsl], [n, 4, 64], 2), ALU.mult),
                      r=[xs_b[g], sBb[W2]], w=[xdtd_b[b2]])
            steps.append(s2)

            def s3():
                kb.op(kb.pe, lambda: T.matmul(pcb, BT, CT, start=True, stop=True), r=[BTb, CTb], w=[bm_b])
                kb.op(kb.dve, lambda: V.tensor_tensor(CBm[b2][0:n, 0:n], pcb, self.tri32[0:n, 0:n], ALU.mult), r=[bm_b, self.cb_], w=[CBm_b[b2]])
                for c in range(8):
                    kb.op(kb.pe, lambda: T.matmul(zq, X[:, c, 3:3 + n], w_in[:, c, g * 256:(g + 1) * 256], start=(c == 0), stop=(c == 7)),
                          r=[win_b, xnT_b[cur]], w=[fm_b] if c == 0 else [], wp=[fm_b] if c else [], inc=(c == 7))
                kb.op(kb.act, lambda: A.activation(out=thg, in_=zq, func=AF.Tanh, scale=0.5), r=[fm_b], w=[thb])
                kb.op(kb.dve, lambda: V.scalar_tensor_tensor(szg, thg, 1.0, zq, ALU.add, ALU.mult), r=[thb, fm_b], w=[szb])
                kb.op(kb.pool, lambda: G.tensor_tensor(v3(MTv), v3(Ev), bc(CBm[b2][0:n, 0:n], [n, 4, n], 1), ALU.mult),
                      r=[E_b[b2], CBm_b[b2]], w=[MT_b[b2]])
            steps.append(s3)

            def s4():
                for jj in range(4):
                    kb.op(kb.pe, lambda: T.matmul(yb[0:n, jj * 64:(jj + 1) * 64], MTv[:, jj * n:(jj + 1) * n], xdt[b2][0:n, jj * 64:(jj + 1) * 64], start=True, stop=True),
                          r=[MT_b[b2], xdt_b[b2]], w=[y_b] if jj == 0 else [], wp=[y_b] if jj else [], inc=False)
                kb.op(kb.pe, lambda: T.matmul(yb[0:n, 256:512], CT, Sbf[:, gsl], start=True, stop=True), r=[CTb, Sbf_b[g]], wp=[y_b])
                kb.op(kb.pool, lambda: G.tensor_tensor(x3(t3), xs3, bc(dsk[0:n, hsl], [n, 4, 64], 2), ALU.mult), r=[xs_b[g], par_b], w=[t3b])
                kb.op(kb.dve, lambda: V.tensor_tensor(x3(t1), x3(yb[0:n, 256:512]), bc(sB[0:n, DFS, hsl], [n, 4, 64], 2), ALU.mult),
                      r=[y_b, sBb[DFS]], w=[t1b])
                kb.op(kb.dve, lambda: V.tensor_tensor(t2, yb[0:n, 0:256], t1, ALU.add), r=[y_b, t1b], w=[t2b])
            steps.append(s4)

            def s5():
                kb.op(kb.pe, lambda: T.matmul(pst, B_tm[0:n, g * 128:(g + 1) * 128], xdtd[b2][0:n, :], start=True, stop=True),
                      r=[Btm_b, xdtd_b[b2]], w=[bm_b])
                kb.op(kb.pool, lambda: G.tensor_tensor(x3(stmp[b2][:, :]), x3(S32[:, gsl]), bc(sB[:, CD, hsl], [128, 4, 64], 2), ALU.mult),
                      r=[S32_b[g], sBb[CD]], w=[stmp_b[b2]])
                kb.op(kb.dve, lambda: V.tensor_tensor(S32[:, gsl], stmp[b2][:, :], pst, ALU.add), r=[stmp_b[b2], bm_b], w=[S32_b[g]])
                kb.op(kb.act, lambda: A.copy(Sbf[:, gsl], S32[:, gsl]), r=[S32_b[g]], w=[Sbf_b[g]])
            steps.append(s5)

            def s6():
                kb.op(kb.pool, lambda: G.tensor_tensor(t2, t2, t3, ALU.add), r=[t3b], w=[t2b])
                kb.op(kb.pool, lambda: G.tensor_tensor(t2, t2, szg, ALU.mult), r=[szb], w=[t2b])
            steps.append(s6)

            def s7():
                kb.op(kb.act, lambda: A.activation(out=t1, in_=t2, func=AF.Square, accum_out=stg[0:n, g, 0:1]), r=[t2b], w=[t1b, stg_b[g]])
                self.rstd_newton(stg[:, g, :], stg_b[g], n, 1.0 / 256, 4.0 * EPS)
                kb.op(kb.dve, lambda: V.tensor_scalar(xs_tm[0:n, gsl], t2, stg[0:n, g, 2:3], None, ALU.mult), r=[t2b, stg_b[g]], w=[xs_b[g]])
            steps.append(s7)
            return steps

        def b_gT(ti, half):
            t0, n = TILES[ti]
            for k in range(8):
                cc = half * 8 + k
                kb.op(kb.pe, lambda: T.transpose(tpBg[:, k, 0:n], xs_tm[0:n, cc * 128:(cc + 1) * 128], self.ident[0:n, 0:n]),
                      r=[xs_b[cc // 2], self.cb_], w=[tpB_b] if k == 0 else [], wp=[tpB_b] if k else [], inc=(k == 7))
            kb.op(kb.dve, lambda: V.tensor_tensor(gT[:, half * 8:(half + 1) * 8, 0:n], tpBg[:, :, 0:n], bc(sng[:, half * 8:(half + 1) * 8], [128, 8, n], 2), ALU.mult),
                  r=[tpB_b, par_b], w=[gT_b] if half == 0 else [], wp=[gT_b] if half else [])

        def b_out(ti, hh):
            t0, n = TILES[ti]
            cur = ti % 2
            po = dTb[0:n, :] if hh == 0 else yb[0:n, :]
            pbuf = dT_b if hh == 0 else y_b
            for cc in range(16):
                kb.op(kb.pe, lambda: T.matmul(po, gT[:, cc, 0:n], w_out[:, cc, hh * 512:(hh + 1) * 512], start=(cc == 0), stop=(cc == 15)),
                      r=[gT_b, wout_b], w=[pbuf] if cc == 0 else [], wp=[pbuf] if cc else [], inc=(cc == 15))
            kb.op(kb.dve, lambda: V.tensor_tensor(hs[cur][0:n, hh * 512:(hh + 1) * 512], po, hs[cur][0:n, hh * 512:(hh + 1) * 512], ALU.add),
                  r=[pbuf, hs_b[cur]], wp=[hs_b[cur]])
            if hh == 1:
                dap = self.h_dst(dst, t0, n)
                if dap is not None:
                    kb.dma("sp", dap, hs[cur][0:n, :], r=[hs_b[cur]])

        def back(ti):
            st = [partial(b_trx, ti, 0), partial(b_trx, ti, 1), partial(b_trB, ti)]
            for gp in range(4):
                ca, cb_ = gchain(ti, 2 * gp), gchain(ti, 2 * gp + 1)
                for a_, b_ in zip(ca, cb_):
                    st += [a_, b_]
            st += [partial(b_gT, ti, 0), partial(b_gT, ti, 1), partial(b_out, ti, 0), partial(b_out, ti, 1)]
            return st

        def interleave(a, b):
            na, nb = len(a), len(b)
            i = jx = 0
            while i < na or jx < nb:
                if jx >= nb or (i < na and i * nb <= jx * na):
                    a[i]()
                    i += 1
                else:
                    b[jx]()
                    jx += 1

        for ti in range(NTL + 1):
            f = front(ti) if ti < NTL else []
            b = back(ti - 1) if ti >= 1 else []
            interleave(f, b)

    def phase_ssd_v1(self, es, j, li, src, dst):
        kb, nc = self.kb, self.nc
        V, A, G, T = nc.vector, nc.scalar, nc.gpsimd, nc.tensor
        w_in = self.S(es, "s_win", [128, 8, SSD_IN], BF16)
        w_out = self.S(es, "s_wout", [128, 16, D], BF16)
        win_b, wout_b, par_b = Buf(), Buf(), Buf()
        wv = self.w_ssd_in[j].rearrange("(c p) f -> p c f", p=128)
        for c in range(8):
            kb.dma("pool", w_in[:, c, :], wv[:, c, :], wp=[win_b])
        wov = self.w_ssd_out[j].rearrange("(c p) f -> p c f", p=128)
        for c in range(0, 16, 4):
            kb.dma("pool", w_out[:, c:c + 4, :], wov[:, c:c + 4, :], wp=[wout_b])
        cw = self.S(es, "s_cw", [128, 32, 4], F32)
        cb = self.S(es, "s_cb", [128, 32], F32)
        dtb = self.S(es, "s_dtb", [128, 32], F32)
        arep = self.S(es, "s_arep", [128, 32], F32)
        dsk = self.S(es, "s_dsk", [128, 32], F32)
        sng = self.S(es, "s_sng", [128, 16], F32)
        kb.dma("sp", cw[:], self.cw_d[j], wp=[par_b])
        kb.dma("sp", cb[:], self.cb_d[j], wp=[par_b])
        kb.dma("sp", dtb[:], self.dtb_d[j], wp=[par_b])
        kb.dma("sp", arep[:], self.alog_d[j], wp=[par_b])
        kb.dma("sp", dsk[:], self.dsk_d[j], wp=[par_b])
        kb.dma("sp", sng[:], self.sng_d[j], wp=[par_b])
        arep_b = Buf()
        kb.op(kb.act, lambda: A.activation(out=arep[:], in_=arep[:], func=AF.Exp), r=[par_b], w=[arep_b])
        kb.op(kb.dve, lambda: V.tensor_scalar(arep[:], arep[:], -1.0, None, ALU.mult), r=[arep_b], w=[arep_b])
        hs = [self.S(es, "s_h%d" % i, [128, D], F32) for i in range(2)]
        hs_b = [Buf(), Buf()]
        tp2 = self.P(es, "s_tp2", [128, 1024], F32)
        pzbs = [self.P(es, "s_pz%d" % i, [128, 512], F32) for i in range(2)]
        zqb = self.P(es, "s_zq", [128, 512], F32)
        dTb = self.P(es, "s_dT", [128, 512], F32)
        miscb = self.P(es, "s_misc", [128, 512], F32)
        yb = self.P(es, "s_y", [128, 512], F32)
        pz_b = [PB(), PB()]
        _z = PB()
        zq_b = [_z, _z]
        dT_b = PB()
        misc_b = PB()
        pdt_b = pacs_b = ptot_b = pst_b = misc_b
        pcb_b = [misc_b, misc_b]
        ydg_b = yof_b = PB()
        nsc = self.norm_scratch(es, "s", tp2)
        tp_b = nsc[7]
        tp16 = tp2[:].bitcast(BF16)
        tpg = tp16.rearrange("p (c t) -> p c t", c=16)
        xnT = [self.S(es, "s_xnT%d" % i, [128, 8, 131], BF16) for i in range(2)]
        xnT_b = [Buf(), Buf()]
        for i in range(2):
            kb.op(kb.pool, lambda: G.memset(xnT[i][:], 0.0), w=[xnT_b[i]])
        xbcT = self.S(es, "s_xbcT", [128, 32, 128], BF16)
        xbc_b = [Buf() for _ in range(32)]
        acc = [self.S(es, "s_acc%d" % i, [128, 128], F32) for i in range(3)]
        acc_b = [Buf() for _ in range(3)]
        xs_tm = self.S(es, "s_xstm", [128, 2048], BF16)
        xs_b = Buf()
        B_tm = self.S(es, "s_Btm", [128, 1024], BF16)
        Btm_b = Buf()
        sz = self.S(es, "s_sz", [128, 2048], F32)
        sz_b = [Buf() for _ in range(8)]
        sm = self.S(es, "s_sm", [128, 10, 32], F32)
        DTV, EXPT, ADT, ACS, DFS, CD, DTE, TMP, W2 = range(9)
        sm_b = [Buf() for _ in range(10)]
        rhsD = [self.S(es, "s_rhsD0", [128, 512], F32)] * 2
        _b = Buf()
        rhsD_b = [_b, _b]
        Ee = [self.S(es, "s_E0", [128, 512], F32)] * 2
        _b = Buf()
        E_b = [_b, _b]
        CBm = [self.S(es, "s_CBm%d" % i, [128, 128], F32) for i in range(2)]
        CBm_b = [Buf(), Buf()]
        MT = [self.S(es, "s_MT%d" % i, [128, 512], BF16) for i in range(2)]
        MT_b = [Buf(), Buf()]
        xdt = [self.S(es, "s_xdt%d" % i, [128, 256], BF16) for i in range(2)]
        xdt_b = [Buf(), Buf()]
        xdtd = [self.S(es, "s_xdtd%d" % i, [128, 256], BF16) for i in range(2)]
        xdtd_b = [Buf(), Buf()]
        tt = [[self.S(es, "s_t%d_%d" % (k, i), [128, 256], F32) for i in range(2)] for k in range(3)]
        tt_b = [[Buf(), Buf()] for _ in range(3)]
        tt.append(tt[0])
        tt_b.append(tt_b[0])
        stg = self.S(es, "s_stg", [128, 8, 4], F32)
        stg_b = [Buf() for _ in range(8)]
        gn = self.S(es, "s_gn", [128, 2048], BF16)
        gn_b = Buf()
        gT = self.S(es, "s_gT", [128, 16, 128], BF16)
        gT_b = Buf()
        S32 = self.S(es, "s_S32", [128, 2048], F32)
        Sbf = self.S(es, "s_Sbf", [128, 2048], BF16)
        S32_b = [Buf() for _ in range(8)]
        Sbf_b = [Buf() for _ in range(8)]
        stmp = [self.S(es, "s_stmp0", [128, 256], F32)] * 2
        _b = Buf()
        stmp_b = [_b, _b]
        kb.op(kb.pool, lambda: G.memset(S32[:], 0.0), w=S32_b)
        kb.op(kb.pool, lambda: G.memset(Sbf[:], 0.0), w=Sbf_b)
        nprev = 0
        ipz = 0
        for ti, (t0, n) in enumerate(TILES):
            cur, prev = ti % 2, (ti + 1) % 2
            X = xnT[cur]
            kb.dma("sp", hs[cur][0:n, :], self.h_src(src, t0, n), w=[hs_b[cur]])
            self.norm_T(hs_b[cur], hs[cur][0:n, :], n, self.lnT[:, li, :], X[:, :, 3:3 + n], xnT_b[cur], nsc)
            kb.op(kb.pool, lambda: G.tensor_copy(X[:, :, 0:3], xnT[prev][:, :, nprev:nprev + 3]), r=[xnT_b[prev]], wp=[xnT_b[cur]])
            nprev = n
            for cc in range(32):
                p = ipz % 2
                ipz += 1
                pz = pzbs[p][:, 0:3 + n]
                for c in range(8):
                    kb.op(kb.pe, lambda: T.matmul(pz, w_in[:, c, 2048 + cc * 128:2048 + (cc + 1) * 128], X[:, c, 0:3 + n], start=(c == 0), stop=(c == 7)),
                          r=[win_b, xnT_b[cur]], w=[pz_b[p]] if c == 0 else [], wp=[pz_b[p]] if c else [], inc=(c == 7))
                a_ = acc[p][:, 0:n]
                kb.op(kb.act, lambda: A.activation(out=a_, in_=pz[:, 3:3 + n], func=AF.Identity, bias=cb[:, cc:cc + 1], scale=cw[:, cc, 3:4]),
                      r=[pz_b[p], par_b], w=[acc_b[p]])
                for k in range(3):
                    kb.op(kb.dve, lambda: V.scalar_tensor_tensor(a_, pz[:, k:k + n], cw[:, cc, k:k + 1], a_, ALU.mult, ALU.add),
                          r=[pz_b[p], par_b], w=[acc_b[p]])
                kb.op(kb.act, lambda: A.activation(out=xbcT[:, cc, 0:n], in_=a_, func=AF.Silu), r=[acc_b[p]], w=[xbc_b[cc]])
            for g in range(8):
                zs = g % 2
                zq = zqb[0:n, zs * 256:(zs + 1) * 256]
                for c in range(8):
                    kb.op(kb.pe, lambda: T.matmul(zq, X[:, c, 3:3 + n], w_in[:, c, g * 256:(g + 1) * 256], start=(c == 0), stop=(c == 7)),
                          r=[win_b, xnT_b[cur]], w=[zq_b[zs]] if c == 0 else [], wp=[zq_b[zs]] if c else [], inc=(c == 7))
                kb.op(kb.act, lambda: A.activation(out=sz[0:n, g * 256:(g + 1) * 256], in_=zq, func=AF.Silu), r=[zq_b[zs]], w=[sz_b[g]])
            for cc in range(16):
                kb.op(kb.pe, lambda: T.transpose(tp16[0:n, cc * 128:(cc + 1) * 128], xbcT[:, cc, 0:n], self.ident[:, :]),
                      r=[xbc_b[cc], self.cb_], w=[tp_b] if cc == 0 else [], wp=[tp_b] if cc else [], inc=(cc == 15))
            kb.op(kb.act, lambda: A.copy(xs_tm[0:n, 0:1024], tp16[0:n, 0:1024]), r=[tp_b], w=[xs_b])
            kb.op(kb.dve, lambda: V.tensor_copy(xs_tm[0:n, 1024:2048], tp16[0:n, 1024:2048]), r=[tp_b], wp=[xs_b])
            for g in range(8):
                kb.op(kb.pe, lambda: T.transpose(tp16[0:n, g * 128:(g + 1) * 128], xbcT[:, 16 + g, 0:n], self.ident[:, :]),
                      r=[xbc_b[16 + g], self.cb_], w=[tp_b] if g == 0 else [], wp=[tp_b] if g else [], inc=(g == 7))
            kb.op(kb.act, lambda: A.copy(B_tm[0:n, :], tp16[0:n, 0:1024]), r=[tp_b], w=[Btm_b])
            pdt, pacs, ptot = miscb[0:n, 0:32], miscb[0:n, 32:64], miscb[:, 64:96]
            for c in range(8):
                kb.op(kb.pe, lambda: T.matmul(pdt, X[:, c, 3:3 + n], w_in[:, c, 6144:6176], start=(c == 0), stop=(c == 7)),
                      r=[win_b, xnT_b[cur]], w=[pdt_b] if c == 0 else [], wp=[pdt_b] if c else [], inc=(c == 7))
            smv = lambda k: sm[0:n, k, :]
            kb.op(kb.dve, lambda: V.tensor_tensor(smv(DTV), pdt, dtb[0:n, :], ALU.add), r=[pdt_b, par_b], w=[sm_b[DTV]])
            kb.op(kb.act, lambda: A.activation(out=smv(EXPT), in_=smv(DTV), func=AF.Exp), r=[sm_b[DTV]], w=[sm_b[EXPT]])
            kb.op(kb.act, lambda: A.activation(out=smv(DTV), in_=smv(EXPT), func=AF.Ln, bias=1.0), r=[sm_b[EXPT]], w=[sm_b[DTV]])
            kb.op(kb.dve, lambda: V.tensor_tensor(smv(ADT), smv(DTV), arep[0:n, :], ALU.mult), r=[sm_b[DTV], arep_b], w=[sm_b[ADT]])
            kb.op(kb.pe, lambda: T.matmul(pacs, self.tri32[0:n, 0:n], smv(ADT), start=True, stop=True), r=[sm_b[ADT], self.cb_], w=[pacs_b])
            kb.op(kb.pe, lambda: T.matmul(ptot, self.ones32[0:n, :], smv(ADT), start=True, stop=True), r=[sm_b[ADT], self.cb_], w=[ptot_b])
            kb.op(kb.dve, lambda: V.tensor_copy(smv(ACS), pacs), r=[pacs_b], w=[sm_b[ACS]])
            kb.op(kb.act, lambda: A.activation(out=smv(DFS), in_=pacs, func=AF.Exp), r=[pacs_b], w=[sm_b[DFS]])
            kb.op(kb.act, lambda: A.activation(out=sm[:, CD, :], in_=ptot, func=AF.Exp), r=[ptot_b], w=[sm_b[CD]])
            kb.op(kb.dve, lambda: V.tensor_tensor(smv(TMP), ptot[0:n, :], smv(ACS), ALU.subtract), r=[ptot_b, sm_b[ACS]], w=[sm_b[TMP]])
            kb.op(kb.act, lambda: A.activation(out=smv(DTE), in_=smv(TMP), func=AF.Exp), r=[sm_b[TMP]], w=[sm_b[DTE]])
            kb.op(kb.dve, lambda: V.tensor_tensor(smv(W2), smv(DTV), smv(DTE), ALU.mult), r=[sm_b[DTV], sm_b[DTE]], w=[sm_b[W2]])
            for g in range(8):
                b2 = g % 2
                hsl = slice(4 * g, 4 * g + 4)
                gsl = slice(g * 256, (g + 1) * 256)
                v3 = lambda ap: ap.rearrange("p (j l) -> p j l", j=4)
                rD = rhsD[b2][0:n, 0:4 * n]
                kb.op(kb.pool, lambda: G.tensor_tensor(v3(rD), bc(sm[0:n, ADT, hsl], [n, 4, n], 2), bc(self.tri32[0:n, 0:n], [n, 4, n], 1), ALU.mult),
                      r=[sm_b[ADT], self.cb_], w=[rhsD_b[b2]])
                kb.op(kb.pe, lambda: T.matmul(dTb[0:n, 0:4 * n], self.ltri32[0:n, 0:n], rD, start=True, stop=True), r=[rhsD_b[b2], self.cb_], w=[dT_b])
                Ev = Ee[b2][0:n, 0:4 * n]
                kb.op(kb.act, lambda: A.activation(out=Ev, in_=dTb[0:n, 0:4 * n], func=AF.Exp), r=[dT_b], w=[E_b[b2]])
                pcb = miscb[0:n, 128:128 + n]
                kb.op(kb.pe, lambda: T.matmul(pcb, xbcT[:, 16 + g, 0:n], xbcT[:, 24 + g, 0:n], start=True, stop=True),
                      r=[xbc_b[16 + g], xbc_b[24 + g]], w=[pcb_b[b2]])
                kb.op(kb.dve, lambda: V.tensor_tensor(CBm[b2][0:n, 0:n], pcb, self.tri32[0:n, 0:n], ALU.mult), r=[pcb_b[b2], self.cb_], w=[CBm_b[b2]])
                MTv = MT[b2][0:n, 0:4 * n]
                kb.op(kb.pool, lambda: G.tensor_tensor(v3(MTv), v3(Ev), bc(CBm[b2][0:n, 0:n], [n, 4, n], 1), ALU.mult),
                      r=[E_b[b2], CBm_b[b2]], w=[MT_b[b2]])
                xs3 = xs_tm[0:n, gsl].rearrange("p (j f) -> p j f", j=4)
                x3 = lambda ap: ap.rearrange("p (j f) -> p j f", j=4)
                kb.op(kb.pool, lambda: G.tensor_tensor(x3(xdt[b2][0:n, :]), xs3, bc(sm[0:n, DTV, hsl], [n, 4, 64], 2), ALU.mult),
                      r=[xs_b, sm_b[DTV]], w=[xdt_b[b2]])
                kb.op(kb.pool, lambda: G.tensor_tensor(x3(xdtd[b2][0:n, :]), xs3, bc(sm[0:n, W2, hsl], [n, 4, 64], 2), ALU.mult),
                      r=[xs_b, sm_b[W2]], w=[xdtd_b[b2]])
                for jj in range(4):
                    kb.op(kb.pe, lambda: T.matmul(yb[0:n, jj * 64:(jj + 1) * 64], MTv[:, jj * n:(jj + 1) * n], xdt[b2][0:n, jj * 64:(jj + 1) * 64], start=True, stop=True),
                          r=[MT_b[b2], xdt_b[b2]], w=[ydg_b] if jj == 0 else [], wp=[ydg_b] if jj else [], inc=(jj == 3))
                kb.op(kb.pe, lambda: T.matmul(yb[0:n, 256:512], xbcT[:, 24 + g, 0:n], Sbf[:, gsl], start=True, stop=True),
                      r=[xbc_b[24 + g], Sbf_b[g]], w=[yof_b])
                t1, t2, t3, tj = [tt[k][b2][0:n, :] for k in range(4)]
                kb.op(kb.dve, lambda: V.tensor_tensor(x3(t1), x3(yb[0:n, 256:512]), bc(sm[0:n, DFS, hsl], [n, 4, 64], 2), ALU.mult),
                      r=[yof_b, sm_b[DFS]], w=[tt_b[0][b2]])
                kb.op(kb.dve, lambda: V.tensor_tensor(t2, yb[0:n, 0:256], t1, ALU.add), r=[ydg_b, tt_b[0][b2]], w=[tt_b[1][b2]])
                kb.op(kb.pool, lambda: G.tensor_tensor(x3(t3), xs3, bc(dsk[0:n, hsl], [n, 4, 64], 2), ALU.mult), r=[xs_b, par_b], w=[tt_b[2][b2]])
                kb.op(kb.pool, lambda: G.tensor_tensor(t2, t2, t3, ALU.add), r=[tt_b[2][b2]], w=[tt_b[1][b2]])
                kb.op(kb.pool, lambda: G.tensor_tensor(t2, t2, sz[0:n, gsl], ALU.mult), r=[sz_b[g]], w=[tt_b[1][b2]])
                kb.op(kb.act, lambda: A.activation(out=tj, in_=t2, func=AF.Square, accum_out=stg[0:n, g, 0:1]), r=[tt_b[1][b2]], w=[tt_b[3][b2], stg_b[g]])
                self.rstd_ops(stg[:, g, :], stg_b[g], n, 0, 1.0 / 256)
                kb.op(kb.dve, lambda: V.tensor_scalar(gn[0:n, gsl], t2, stg[0:n, g, 2:3], None, ALU.mult), r=[tt_b[1][b2], stg_b[g]],
                      w=[gn_b] if g == 0 else [], wp=[gn_b] if g else [])
                kb.op(kb.pe, lambda: T.matmul(miscb[:, 256:512], B_tm[0:n, g * 128:(g + 1) * 128], xdtd[b2][0:n, :], start=True, stop=True),
                      r=[Btm_b, xdtd_b[b2]], w=[pst_b])
                kb.op(kb.pool, lambda: G.tensor_tensor(x3(stmp[b2][:, :]), x3(S32[:, gsl]), bc(sm[:, CD, hsl], [128, 4, 64], 2), ALU.mult),
                      r=[S32_b[g], sm_b[CD]], w=[stmp_b[b2]])
                kb.op(kb.dve, lambda: V.tensor_tensor(S32[:, gsl], stmp[b2][:, :], miscb[:, 256:512], ALU.add), r=[stmp_b[b2], pst_b], w=[S32_b[g]])
                kb.op(kb.act, lambda: A.copy(Sbf[:, gsl], S32[:, gsl]), r=[S32_b[g]], w=[Sbf_b[g]])
            for cc in range(16):
                kb.op(kb.pe, lambda: T.transpose(tpg[:, cc, 0:n], gn[0:n, cc * 128:(cc + 1) * 128], self.ident[0:n, 0:n]),
                      r=[gn_b, self.cb_], w=[tp_b] if cc == 0 else [], wp=[tp_b] if cc else [], inc=(cc == 15))
            kb.op(kb.dve, lambda: V.tensor_tensor(gT[:, :, 0:n], tpg[:, :, 0:n], bc(sng[:, :], [128, 16, n], 2), ALU.mult), r=[tp_b, par_b], w=[gT_b])
            for hh in range(2):
                po = dTb[0:n, :] if hh == 0 else yb[0:n, :]
                pbufs = [dT_b] if hh == 0 else [ydg_b]
                for cc in range(16):
                    kb.op(kb.pe, lambda: T.matmul(po, gT[:, cc, 0:n], w_out[:, cc, hh * 512:(hh + 1) * 512], start=(cc == 0), stop=(cc == 15)),
                          r=[gT_b, wout_b], w=pbufs if cc == 0 else [], wp=pbufs if cc else [], inc=(cc == 15))
                kb.op(kb.dve, lambda: V.tensor_tensor(hs[cur][0:n, hh * 512:(hh + 1) * 512], po, hs[cur][0:n, hh * 512:(hh + 1) * 512], ALU.add),
                      r=pbufs + [hs_b[cur]], wp=[hs_b[cur]])
            dap = self.h_dst(dst, t0, n)
            if dap is not None:
                kb.dma("sp", dap, hs[cur][0:n, :], r=[hs_b[cur]])

    def rstd_ops(self, st, st_b, n, c0, inv_n):
        kb, nc = self.kb, self.nc
        kb.op(kb.act, lambda: nc.scalar.activation(out=st[0:n, c0 + 1:c0 + 2], in_=st[0:n, c0:c0 + 1], func=AF.Ln, bias=EPS, scale=inv_n),
              r=[st_b], wp=[st_b])
        kb.op(kb.act, lambda: nc.scalar.activation(out=st[0:n, c0 + 2:c0 + 3], in_=st[0:n, c0 + 1:c0 + 2], func=AF.Exp, scale=-0.5),
              r=[st_b], wp=[st_b])

    def phase_mla(self, es, j, li, src, dst):
        kb, nc = self.kb, self.nc
        V, A, G, T = nc.vector, nc.scalar, nc.gpsimd, nc.tensor
        oT = self.S(es, "oT", [128, 8, NT], BF16)
        oT_b = Buf()
        w_o = self.S(es, "w_o", [128, 8, D], BF16)
        w_o_b = Buf()
        gq = self.S(es, "gq", [128, 96], F32)
        gk = self.S(es, "gk", [128, 96], F32)
        par_b = Buf()
        kb.dma("sp", gq[:], self.gq_d[j], wp=[par_b])
        kb.dma("sp", gk[:], self.gk_d[j], wp=[par_b])
        with contextlib.ExitStack() as s1:
            w_in = self.S(s1, "a_win", [128, 8, 672], BF16)
            w_qb = self.S(s1, "a_wqb", [128, 3, 1536], BF16)
            w_kvb = self.S(s1, "a_wkvb", [128, 2, 2048], BF16)
            wb = Buf()
            kb.dma("pool", w_in[:], self.w_mla_in[j].rearrange("(c p) f -> p c f", p=128), wp=[wb])
            kb.dma("pool", w_qb[:], self.w_mla_qb[j].rearrange("(c p) f -> p c f", p=128), wp=[wb])
            kb.dma("pool", w_kvb[:], self.w_mla_kvb[j].rearrange("(c p) f -> p c f", p=128), wp=[wb])
            kb.dma("pool", w_o[:], self.w_mla_out[j].rearrange("(c p) f -> p c f", p=128), wp=[w_o_b])
            qag = self.S(s1, "a_qag", [128, 3], F32)
            kvag = self.S(s1, "a_kvag", [128, 2], F32)
            kb.dma("sp", qag[:], self.qag_d[j], wp=[par_b])
            kb.dma("sp", kvag[:], self.kvag_d[j], wp=[par_b])
            hs = [self.S(s1, "a_h%d" % i, [128, D], F32) for i in range(2)]
            hs_b = [Buf(), Buf()]
            tp_ps = self.P(s1, "a_tp", [128, 512], F32)
            latA = self.P(s1, "a_latA", [128, 512], F32)
            latB = self.P(s1, "a_latB", [128, 512], F32)
            big = self.P(s1, "a_big", [128, 2048], F32)
            tph_ps = self.P(s1, "a_tph", [128, 512], F32)
            latA_b, latB_b, big_b, tph_b = PB(), PB(), PB(), PB()
            nsc = self.norm_scratch(s1, "a", tp_ps)
            tp5 = tp_ps[:].bitcast(BF16)[:, 0:640].rearrange("p (c t) -> p c t", c=5)
            tp_b = nsc[7]
            tph = tph_ps[:].bitcast(BF16)[:, 0:1024].rearrange("p (c t) -> p c t", c=8)
            xnT = self.S(s1, "a_xnT", [128, 8, 128], BF16)
            xnT_b = Buf()
            st = self.S(s1, "a_st", [128, 8], F32)
            st_b = Buf()
            qln = self.S(s1, "a_qln", [128, 384], BF16)
            kvln = self.S(s1, "a_kvln", [128, 256], BF16)
            kpe = self.S(s1, "a_kpe", [128, 32], F32)
            ln_b = Buf()
            sqj = self.S(s1, "a_sqj", [128, 2048], F32)
            sqj_b = Buf()
            qlT = self.S(s1, "a_qlT", [128, 3, 128], BF16)
            kvlT = self.S(s1, "a_kvlT", [128, 2, 128], BF16)
            lT_b = Buf()
            raw = self.S(s1, "a_raw", [128, 2048], F32)
            raw_b = Buf()
            s16 = self.S(s1, "a_s16", [128, 3, 16], F32)
            s16_b = Buf()
            cs = [self.S(s1, "a_cs%d" % i, [128, 2, 16], F32) for i in range(2)]
            cs_b = [Buf(), Buf()]
            rt = self.S(s1, "a_rt", [128, 4, 256], F32)
            rt_b = [Buf() for _ in range(4)]
            kpg = self.S(s1, "a_kpg", [128, 2, 32], F32)
            kpg_b = Buf()
            qbf = self.S(s1, "a_qbf", [128, 1536], BF16)
            qbf_b = Buf()
            stg = [self.S(s1, "a_stg%d" % i, [96, 16, 128], BF16) for i in range(2)]
            stg_b = [Buf(), Buf()]
            vst = [self.S(s1, "a_vst%d" % i, [128, 8, 192], BF16) for i in range(2)]
            vst_b = [Buf(), Buf()]
            for i in range(2):
                kb.op(kb.pool, lambda: G.memset(vst[i][:], 1.0), w=[vst_b[i]])
            qTv = self.qT_d.rearrange("h p t -> p h t")
            kTv = self.kT_d.rearrange("h p t -> p h t")
            istg = 0

            def head_T(src_bf, dstv, t0, n):
                nonlocal istg
                sg = istg % 2
                istg += 1
                for half in range(2):
                    for hh in range(8):
                        h = half * 8 + hh
                        kb.op(kb.pe, lambda: T.transpose(tph[0:96, hh, 0:n], src_bf[0:n, h * 96:(h + 1) * 96], self.ident[0:n, 0:n]),
                              r=[qbf_b, self.cb_], w=[tph_b] if hh == 0 else [], wp=[tph_b] if hh else [], inc=(hh == 7))
                    kb.op(kb.act, lambda: A.copy(stg[sg][0:96, half * 8:(half + 1) * 8, 0:n], tph[0:96, :, 0:n]),
                          r=[tph_b], w=[stg_b[sg]] if half == 0 else [], wp=[stg_b[sg]] if half else [])
                kb.dma("sp", dstv[:, :, t0:t0 + n], stg[sg][0:96, :, 0:n], r=[stg_b[sg]])

            def rope(t1, t2, cosb, sinb, o1, o2, shape_n, rb, wb_, rd_bufs):
                a_, b_, c_, d_ = [rt[0:shape_n[0], i, 0:shape_n[1]] for i in range(4)]
                if len(shape_n) == 3:
                    a_, b_, c_, d_ = [rt[0:shape_n[0], i, :].rearrange("p (h f) -> p h f", h=16) for i in range(4)]
                kb.op(kb.dve, lambda: V.tensor_tensor(a_, t1, cosb, ALU.mult), r=rd_bufs, w=[rt_b[0]])
                kb.op(kb.pool, lambda: G.tensor_tensor(b_, t2, sinb, ALU.mult), r=rd_bufs, w=[rt_b[1]])
                kb.op(kb.dve, lambda: V.tensor_tensor(c_, t1, sinb, ALU.mult), r=rd_bufs, w=[rt_b[2]])
                kb.op(kb.pool, lambda: G.tensor_tensor(d_, t2, cosb, ALU.mult), r=rd_bufs, w=[rt_b[3]])
                kb.op(kb.dve, lambda: V.tensor_tensor(o1, a_, b_, ALU.subtract), r=[rt_b[0], rt_b[1]], wp=[wb_])
                kb.op(kb.pool, lambda: G.tensor_tensor(o2, c_, d_, ALU.add), r=[rt_b[2], rt_b[3]], wp=[wb_])

            for ti, (t0, n) in enumerate(TILES):
                s = ti % 2
                kb.dma("sp", hs[s][0:n, :], self.h_src(src, t0, n), w=[hs_b[s]])
                kb.dma("sp", cs[s][0:n, 0, :], self.cos_d[t0:t0 + n, :], w=[cs_b[s]])
                kb.dma("sp", cs[s][0:n, 1, :], self.sin_d[t0:t0 + n, :], wp=[cs_b[s]])
                self.norm_T(hs_b[s], hs[s][0:n, :], n, self.lnT[:, li, :], xnT[:, :, 0:n], xnT_b, nsc)
                for c in range(8):
                    kb.op(kb.pe, lambda: T.matmul(latA[0:n, 0:384], xnT[:, c, 0:n], w_in[:, c, 0:384], start=(c == 0), stop=(c == 7)),
                          r=[xnT_b, wb], w=[latA_b] if c == 0 else [], wp=[latA_b] if c else [], inc=(c == 7))
                for c in range(8):
                    kb.op(kb.pe, lambda: T.matmul(latB[0:n, 0:288], xnT[:, c, 0:n], w_in[:, c, 384:672], start=(c == 0), stop=(c == 7)),
                          r=[xnT_b, wb], w=[latB_b] if c == 0 else [], wp=[latB_b] if c else [], inc=(c == 7))
                kb.op(kb.act, lambda: A.activation(out=sqj[0:n, 0:384], in_=latA[0:n, 0:384], func=AF.Square, accum_out=st[0:n, 0:1]),
                      r=[latA_b], w=[sqj_b, st_b])
                kb.op(kb.act, lambda: A.activation(out=sqj[0:n, 0:256], in_=latB[0:n, 0:256], func=AF.Square, accum_out=st[0:n, 3:4]),
                      r=[latB_b], w=[sqj_b], wp=[st_b])
                self.rstd_ops(st, st_b, n, 0, 1.0 / 384)
                self.rstd_ops(st, st_b, n, 3, 1.0 / 256)
                kb.op(kb.dve, lambda: V.tensor_scalar(qln[0:n, :], latA[0:n, 0:384], st[0:n, 2:3], None, ALU.mult), r=[latA_b, st_b], w=[ln_b])
                kb.op(kb.dve, lambda: V.tensor_scalar(kvln[0:n, :], latB[0:n, 0:256], st[0:n, 5:6], None, ALU.mult), r=[latB_b, st_b], wp=[ln_b])
                kb.op(kb.act, lambda: A.copy(kpe[0:n, :], latB[0:n, 256:288]), r=[latB_b], wp=[ln_b])
                for c in range(5):
                    srcap = qln[0:n, c * 128:(c + 1) * 128] if c < 3 else kvln[0:n, (c - 3) * 128:(c - 2) * 128]
                    kb.op(kb.pe, lambda: T.transpose(tp5[:, c, 0:n], srcap, self.ident[0:n, 0:n]),
                          r=[ln_b, self.cb_], w=[tp_b] if c == 0 else [], wp=[tp_b] if c else [], inc=(c == 4))
                kb.op(kb.dve, lambda: V.tensor_tensor(qlT[:, :, 0:n], tp5[:, 0:3, 0:n], bc(qag[:, :], [128, 3, n], 2), ALU.mult),
                      r=[tp_b, par_b], w=[lT_b])
                kb.op(kb.dve, lambda: V.tensor_tensor(kvlT[:, :, 0:n], tp5[:, 3:5, 0:n], bc(kvag[:, :], [128, 2, n], 2), ALU.mult),
                      r=[tp_b, par_b], wp=[lT_b])
                for ct in range(3):
                    for kc in range(3):
                        kb.op(kb.pe, lambda: T.matmul(big[0:n, ct * 512:(ct + 1) * 512], qlT[:, kc, 0:n], w_qb[:, kc, ct * 512:(ct + 1) * 512],
                                                      start=(kc == 0), stop=(kc == 2)),
                              r=[lT_b, wb], w=[big_b] if (ct == 0 and kc == 0) else [], wp=[] if (ct == 0 and kc == 0) else [big_b],
                              inc=(kc == 2 and ct == 2))
                for ct in range(3):
                    kb.op(kb.act, lambda: A.copy(raw[0:n, ct * 512:(ct + 1) * 512], big[0:n, ct * 512:(ct + 1) * 512]),
                          r=[big_b], w=[raw_b] if ct == 0 else [], wp=[raw_b] if ct else [])
                raw3 = raw[0:n, 0:1536].rearrange("p (h f) -> p h f", h=16)
                sq3 = sqj[0:n, 0:1536].rearrange("p (h f) -> p h f", h=16)
                kb.op(kb.pool, lambda: G.tensor_tensor(sqj[0:n, 0:1536], raw[0:n, 0:1536], raw[0:n, 0:1536], ALU.mult), r=[raw_b], w=[sqj_b])
                kb.op(kb.dve, lambda: V.tensor_reduce(s16[0:n, 0, :], sq3, AX.X, ALU.add), r=[sqj_b], w=[s16_b])
                kb.op(kb.act, lambda: A.activation(out=s16[0:n, 1, :], in_=s16[0:n, 0, :], func=AF.Ln, bias=EPS, scale=1.0 / 96), r=[s16_b], wp=[s16_b])
                kb.op(kb.act, lambda: A.activation(out=s16[0:n, 2, :], in_=s16[0:n, 1, :], func=AF.Exp, scale=-0.5), r=[s16_b], wp=[s16_b])
                kb.op(kb.dve, lambda: V.tensor_tensor(raw3, raw3, bc(s16[0:n, 2, :], [n, 16, 96], 2), ALU.mult), r=[s16_b], w=[raw_b])
                kb.op(kb.pool, lambda: G.tensor_tensor(raw3, raw3, bc(gq[0:n, :], [n, 16, 96], 1), ALU.mult), r=[par_b], w=[raw_b])
                qb3 = qbf[0:n, :].rearrange("p (h f) -> p h f", h=16)
                cosb = bc(cs[s][0:n, 0, :], [n, 16, 16], 1)
                sinb = bc(cs[s][0:n, 1, :], [n, 16, 16], 1)
                kb.op(kb.act, lambda: A.copy(qb3[:, :, 0:64], raw3[:, :, 0:64]), r=[raw_b], w=[qbf_b])
                rope(raw3[:, :, 64:80], raw3[:, :, 80:96], cosb, sinb, qb3[:, :, 64:80], qb3[:, :, 80:96], (n, 16, 16), None, qbf_b, [raw_b, cs_b[s]])
                head_T(qbf, qTv, t0, n)
                for ct in range(4):
                    for kc in range(2):
                        kb.op(kb.pe, lambda: T.matmul(big[0:n, ct * 512:(ct + 1) * 512], kvlT[:, kc, 0:n], w_kvb[:, kc, ct * 512:(ct + 1) * 512],
                                                      start=(kc == 0), stop=(kc == 1)),
                              r=[lT_b, wb], w=[big_b] if (ct == 0 and kc == 0) else [], wp=[] if (ct == 0 and kc == 0) else [big_b],
                              inc=(kc == 1 and ct == 3))
                for ct in range(4):
                    kb.op(kb.act, lambda: A.copy(raw[0:n, ct * 512:(ct + 1) * 512], big[0:n, ct * 512:(ct + 1) * 512]),
                          r=[big_b], w=[raw_b] if ct == 0 else [], wp=[raw_b] if ct else [])
                kv4 = raw[0:n, :].rearrange("p (h f) -> p h f", h=16)
                sqk = sqj[0:n, 0:1024].rearrange("p (h f) -> p h f", h=16)
                kb.op(kb.pool, lambda: G.tensor_tensor(sqk, kv4[:, :, 0:64], kv4[:, :, 0:64], ALU.mult), r=[raw_b], w=[sqj_b])
                kb.op(kb.dve, lambda: V.tensor_reduce(s16[0:n, 0, :], sqk, AX.X, ALU.add), r=[sqj_b], w=[s16_b])
                kb.op(kb.act, lambda: A.activation(out=kpg[0:n, 1, :], in_=kpe[0:n, :], func=AF.Square, accum_out=st[0:n, 6:7]),
                      r=[ln_b], w=[kpg_b], wp=[st_b])
                kb.op(kb.dve, lambda: V.tensor_scalar(s16[0:n, 0, :], s16[0:n, 0, :], st[0:n, 6:7], None, ALU.add), r=[st_b, s16_b], wp=[s16_b])
                kb.op(kb.act, lambda: A.activation(out=s16[0:n, 1, :], in_=s16[0:n, 0, :], func=AF.Ln, bias=EPS, scale=1.0 / 96), r=[s16_b], wp=[s16_b])
                kb.op(kb.act, lambda: A.activation(out=s16[0:n, 2, :], in_=s16[0:n, 1, :], func=AF.Exp, scale=-0.5), r=[s16_b], wp=[s16_b])
                vs = ti % 2
                kv5 = raw[0:n, :].rearrange("p (c e f) -> p c e f", c=8, e=2)
                kb.op(kb.act, lambda: A.copy(vst[vs][0:n, :, 0:64], kv5[:, :, 0, 64:128]), r=[raw_b, vst_b[vs]], wp=[vst_b[vs]])
                kb.op(kb.dve, lambda: V.tensor_copy(vst[vs][0:n, :, 128:192], kv5[:, :, 1, 64:128]), r=[raw_b, vst_b[vs]], wp=[vst_b[vs]])
                kb.dma("sp", self.va_d[t0:t0 + n, :, :], vst[vs][0:n, :, :], r=[vst_b[vs]])
                kb.op(kb.dve, lambda: V.tensor_tensor(kv4[:, :, 0:64], kv4[:, :, 0:64], bc(s16[0:n, 2, :], [n, 16, 64], 2), ALU.mult),
                      r=[s16_b], w=[raw_b])
                kb3 = qbf[0:n, :].rearrange("p (h f) -> p h f", h=16)
                kb.op(kb.pool, lambda: G.tensor_tensor(kb3[:, :, 0:64], kv4[:, :, 0:64], bc(gk[0:n, 0:64], [n, 16, 64], 1), ALU.mult),
                      r=[raw_b, par_b], w=[qbf_b])
                kb.op(kb.dve, lambda: V.tensor_tensor(kpg[0:n, 0, :], kpe[0:n, :], gk[0:n, 64:96], ALU.mult), r=[ln_b, par_b], w=[kpg_b])
                rope(kpg[0:n, 0, 0:16], kpg[0:n, 0, 16:32], cs[s][0:n, 0, :], cs[s][0:n, 1, :], kpg[0:n, 1, 0:16], kpg[0:n, 1, 16:32],
                     (n, 16), None, kpg_b, [kpg_b, cs_b[s]])
                kb.op(kb.dve, lambda: V.tensor_tensor(kb3[:, :, 64:96], bc(kpg[0:n, 1, :], [n, 16, 32], 1), bc(s16[0:n, 2, :], [n, 16, 32], 2), ALU.mult),
                      r=[kpg_b, s16_b], wp=[qbf_b])
                head_T(qbf, kTv, t0, n)
            kb.barrier()
        with contextlib.ExitStack() as s2:
            qh = [self.S(s2, "b_q%d" % i, [96, NT], BF16) for i in range(2)]
            kh = [self.S(s2, "b_k%d" % i, [96, NT], BF16) for i in range(2)]
            qk_b = [Buf(), Buf()]
            va = [self.S(s2, "b_va%d" % i, [128, 33, 192], BF16) for i in range(2)]
            va_b = [Buf(), Buf()]
            pT = [self.S(s2, "b_pT%d" % i, [128, 512], BF16) for i in range(4)]
            pT_b = [Buf() for _ in range(4)]
            rden = [self.S(s2, "b_rd%d" % i, [128, 512], F32) for i in range(2)]
            rdsh = [self.S(s2, "b_rs%d" % i, [128, 512], F32) for i in range(2)]
            rd_b = [Buf(), Buf()]
            rs_b = [Buf(), Buf()]
            bnd = self.S(s2, "b_bnd", [128, 8], F32)
            bnd_b = Buf()
            ps = [self.P(s2, "b_ps%d" % i, [128, 512], F32) for i in range(4)]
            ps_b = [PB() for _ in range(4)]
            po = [self.P(s2, "b_po%d" % i, [128, 512], F32) for i in range(2)]
            po_b = [PB(), PB()]
            kb.op(kb.dve, lambda: V.tensor_reduce(bnd[:, 0:1], gq[:, :], AX.X, ALU.max), r=[par_b], w=[bnd_b])
            kb.op(kb.dve, lambda: V.tensor_reduce(bnd[:, 1:2], gq[:, :], AX.X, ALU.min), r=[par_b], wp=[bnd_b])
            kb.op(kb.dve, lambda: V.tensor_reduce(bnd[:, 2:3], gk[:, :], AX.X, ALU.max), r=[par_b], wp=[bnd_b])
            kb.op(kb.dve, lambda: V.tensor_reduce(bnd[:, 3:4], gk[:, :], AX.X, ALU.min), r=[par_b], wp=[bnd_b])
            kb.op(kb.dve, lambda: V.scalar_tensor_tensor(bnd[:, 4:5], bnd[:, 1:2], -1.0, bnd[:, 0:1], ALU.mult, ALU.max), r=[bnd_b], wp=[bnd_b])
            kb.op(kb.dve, lambda: V.scalar_tensor_tensor(bnd[:, 5:6], bnd[:, 3:4], -1.0, bnd[:, 2:3], ALU.mult, ALU.max), r=[bnd_b], wp=[bnd_b])
            kb.op(kb.dve, lambda: V.scalar_tensor_tensor(bnd[:, 6:7], bnd[:, 4:5], -float(np.sqrt(96.0)), bnd[:, 5:6], ALU.mult, ALU.mult),
                  r=[bnd_b], wp=[bnd_b])
            negB = bnd[:, 6:7]
            scale = float(96.0 ** -0.5)

            def load_head(h):
                b = h % 2
                kb.dma("sp", qh[b][:, :], self.qT_d[h], w=[qk_b[b]])
                kb.dma("sp", kh[b][:, :], self.kT_d[h], wp=[qk_b[b]])

            def load_pair(c):
                b = c % 2
                kb.dma("sp", va[b][0:16, 0, :], self.va_d[0:16, c, :], w=[va_b[b]])
                vv = self.va_d[16:NT, c, :].rearrange("(j p) w -> p j w", p=128)
                for jj in range(0, 32, 8):
                    kb.dma("sp", va[b][:, 1 + jj:9 + jj, :], vv[:, jj:jj + 8, :], wp=[va_b[b]])

            load_pair(0)
            load_head(0)
            items = []
            for h in range(MLA_H):
                for qi in range(9):
                    if qi == 0:
                        q0, nq = 0, 16
                        kts = [(0, 0, 16, 0, True)]
                    else:
                        q0, nq = 16 + 512 * (qi - 1), 512
                        kts = [(0, 0, 16, 0, False)] + [(kt, 16 + 128 * (kt - 1), 128, 0, False) for kt in range(1, 4 * (qi - 1) + 1)]
                        kts += [(4 * (qi - 1) + 1 + i, 16 + 128 * (4 * (qi - 1) + i), 128, 128 * i, True) for i in range(4)]
                    for idx, kt in enumerate(kts):
                        items.append((h, qi, q0, nq, idx, len(kts)) + kt)
            LA = 2
            NI = len(items)
            for i in range(NI + LA):
                if i < NI:
                    (h, qi, q0, nq, idx, nk_t, kt, k0, nk, qoff, diag) = items[i]
                    if qi == 0 and idx == 0 and h + 1 < MLA_H:
                        load_head(h + 1)
                        if h % 2 == 1:
                            load_pair(h // 2 + 1)
                    hb_ = h % 2
                    nqq = nq - qoff
                    p = i % 4
                    kb.op(kb.pe, lambda: T.matmul(ps[p][0:nk, 0:nqq], kh[hb_][:, k0:k0 + nk], qh[hb_][:, q0 + qoff:q0 + nq], start=True, stop=True),
                          r=[qk_b[hb_]], w=[ps_b[p]])
                    kb.op(kb.act, lambda: A.activation(out=pT[p][0:nk, 0:nqq], in_=ps[p][0:nk, 0:nqq], func=AF.Exp, bias=negB[0:nk, :], scale=scale),
                          r=[ps_b[p], bnd_b], w=[pT_b[p]])
                    if diag:
                        kb.op(kb.dve, lambda: V.tensor_tensor(pT[p][0:nk, 0:nk], pT[p][0:nk, 0:nk], self.tri16[0:nk, 0:nk], ALU.mult),
                              r=[self.cb_], w=[pT_b[p]])
                ii = i - LA
                if ii >= 0:
                    (h, qi, q0, nq, idx, nk_t, kt, k0, nk, qoff, diag) = items[ii]
                    c, e = h // 2, h % 2
                    vb_ = c % 2
                    dlo, dhi = (0, 64) if e == 0 else (64, 128)
                    nlo, nhi = (64, 128) if e == 0 else (0, 64)
                    nqq = nq - qoff
                    p = ii % 4
                    pp = (h * 9 + qi) % 2
                    first, last = idx == 0, idx == nk_t - 1
                    kb.op(kb.pe, lambda: T.matmul(po[pp][:, qoff:nq], va[vb_][0:nk, kt, e * 64:e * 64 + 128], pT[p][0:nk, 0:nqq], start=first, stop=last),
                          r=[pT_b[p], va_b[vb_]], w=[po_b[pp]] if first else [], wp=[] if first else [po_b[pp]], inc=last)
                    if last:
                        kb.op(kb.dve, lambda: V.reciprocal(rden[pp][nlo:nhi, 0:nq], po[pp][nlo:nhi, 0:nq]), r=[po_b[pp]], w=[rd_b[pp]])
                        kb.op(kb.act, lambda: A.copy(rdsh[pp][dlo:dhi, 0:nq], rden[pp][nlo:nhi, 0:nq]), r=[rd_b[pp]], w=[rs_b[pp]])
                        kb.op(kb.dve, lambda: V.tensor_tensor(oT[dlo:dhi, c, q0:q0 + nq], po[pp][dlo:dhi, 0:nq], rdsh[pp][dlo:dhi, 0:nq], ALU.mult),
                              r=[po_b[pp], rs_b[pp]], wp=[oT_b])
            kb.barrier()
        with contextlib.ExitStack() as s3:
            hs = [self.S(s3, "c_h%d" % i, [128, D], F32) for i in range(3)]
            hs_b = [Buf() for _ in range(3)]
            po = [self.P(s3, "c_po%d" % i, [128, 512], F32) for i in range(4)]
            po_b = [PB() for _ in range(4)]
            ip = 0
            for ti, (t0, n) in enumerate(TILES):
                s = ti % 3
                kb.dma("sp", hs[s][0:n, :], self.h_src(src, t0, n), w=[hs_b[s]])
                for hh in range(2):
                    p = ip % 4
                    ip += 1
                    for c in range(8):
                        kb.op(kb.pe, lambda: T.matmul(po[p][0:n, :], oT[:, c, t0:t0 + n], w_o[:, c, hh * 512:(hh + 1) * 512], start=(c == 0), stop=(c == 7)),
                              r=[oT_b, w_o_b], w=[po_b[p]] if c == 0 else [], wp=[po_b[p]] if c else [], inc=(c == 7))
                    kb.op(kb.dve, lambda: V.tensor_tensor(hs[s][0:n, hh * 512:(hh + 1) * 512], po[p][0:n, :], hs[s][0:n, hh * 512:(hh + 1) * 512], ALU.add),
                          r=[po_b[p], hs_b[s]], wp=[hs_b[s]])
                dap = self.h_dst(dst, t0, n)
                if dap is not None:
                    kb.dma("sp", dap, hs[s][0:n, :], r=[hs_b[s]])


def host_inputs(inp):
    f = lambda a: np.ascontiguousarray(np.asarray(a, dtype=np.float32))
    rep = lambda a: np.ascontiguousarray(np.broadcast_to(np.asarray(a, np.float32)[:, None, :], (a.shape[0], 128, a.shape[1])))
    colT = lambda a, c: np.ascontiguousarray(np.asarray(a, np.float32).reshape(a.shape[0], c, 128).transpose(0, 2, 1))
    ln = np.concatenate([np.asarray(inp["ln_mix"], np.float32), np.asarray(inp["ln_mlp"], np.float32)], 0)
    lnT = np.ascontiguousarray(ln.reshape(8, 8, 128).transpose(2, 0, 1))
    cw = np.asarray(inp["ssd_conv_w"], np.float32)
    cwT = np.ascontiguousarray(cw.reshape(2, 4, 32, 128).transpose(0, 3, 2, 1))
    k = np.arange(128)
    tri = (k[:, None] <= k[None, :]).astype(np.float32)
    inv = 1.0 / (10000.0 ** (np.arange(0, 32, 2, dtype=np.float32) / 32.0))
    ang = np.arange(NT, dtype=np.float32)[:, None] * inv[None, :].astype(np.float32)
    common = {
        "meta": f(inp["meta_tokens"]),
        "ssd_w_in": f(inp["ssd_w_in"]), "ssd_w_out": f(inp["ssd_w_out"]),
        "mla_w_in": f(inp["mla_w_in"]), "mla_w_q_b": f(inp["mla_w_q_b"]), "mla_w_kv_b": f(inp["mla_w_kv_b"]),
        "mla_w_out": f(inp["mla_w_out"]), "mlp_w_up": f(inp["mlp_w_up"]), "mlp_w_down": f(inp["mlp_w_down"]),
        "lnT": lnT, "cw": cwT, "cb": colT(inp["ssd_conv_b"], 32),
        "dtb_rep": rep(inp["ssd_dt_bias"]), "alog_rep": rep(inp["ssd_a_log"]), "dskip_rep": rep(inp["ssd_d"]),
        "ssd_normT": colT(inp["ssd_norm"], 16), "q_a_T": colT(inp["mla_q_a_norm"], 3), "kv_a_T": colT(inp["mla_kv_a_norm"], 2),
        "gq_rep": rep(inp["mla_q_norm"]), "gk_rep": rep(inp["mla_k_norm"]),
        "ident": np.eye(128, dtype=np.float32), "tri": tri, "ltri": np.ascontiguousarray(1.0 - tri),
        "ones": np.ones((128, 128), np.float32),
        "cos": np.cos(ang).astype(np.float32), "sin": np.sin(ang).astype(np.float32),
    }
    return common


FULL_PHASES = [
    ("ssd", 0, 0, "x", "h"), ("mlp", 0, "h", "h"),
    ("mla", 0, 1, "h", "h"), ("mlp", 1, "h", "h"),
    ("ssd", 1, 2, "h", "h"), ("mlp", 2, "h", "h"),
    ("mla", 1, 3, "h", "h"), ("mlp", 3, "h", "y"),
]


def run(inputs, phases, cores=8):
    common = host_inputs(inputs)
    x = np.asarray(inputs["x"], np.float32)
    prog = Prog(phases)
    in_maps = []
    for c in range(cores):
        m = dict(common)
        m["x"] = np.ascontiguousarray(x[c])
        in_maps.append(m)
    res = run_bass_kernel_spmd(prog.nc, in_maps, core_ids=list(range(cores)))
    return np.stack([np.asarray(r["y"]) for r in res.results], 0)


def kernel(**inputs):
    return run(inputs, FULL_PHASES, 8).astype(np.float32)
```

```python
import contextlib
import numpy as np
import concourse.bass as bass
import concourse.mybir as mybir
from concourse.bass_utils import run_bass_kernel_spmd

F32, BF16 = mybir.dt.float32, mybir.dt.bfloat16
AF = mybir.ActivationFunctionType
ALU = mybir.AluOpType
AX = mybir.AxisListType

NT, NM, D, SEQ = 4112, 16, 1024, 4096
TILES = [(0, 16)] + [(16 + 128 * j, 128) for j in range(32)]
EPS = 1e-6
DFF = 4096
SSD_IN = 6176
NH_S = 32
MLA_H = 16
QK = 96


class Buf:
    __slots__ = ("w", "r", "name", "ps")

    def __init__(self, name="", ps=False):
        self.w = {}
        self.r = {}
        self.name = name
        self.ps = ps


def PB():
    return Buf(ps=True)


class Eng:
    def __init__(self, name, e, sem):
        self.name, self.e, self.sem, self.cnt, self.seen = name, e, sem, 0, {}


class KB:
    def __init__(self, nc, es):
        self.nc = nc
        mk = lambda n: es.enter_context(nc.semaphore(n))
        self.pe = Eng("pe", nc.tensor, mk("s_pe"))
        self.act = Eng("act", nc.scalar, mk("s_act"))
        self.dve = Eng("dve", nc.vector, mk("s_dve"))
        self.pool = Eng("pool", nc.gpsimd, mk("s_pool"))
        self.sp = Eng("sp", nc.sync, mk("s_sp"))
        self.engs = [self.pe, self.act, self.dve, self.pool, self.sp]
        self.dsem = {"sp": [[mk("d_sp%d" % i), 0] for i in range(24)],
                     "pool": [[mk("d_pl%d" % i), 0] for i in range(8)]}
        self.drr = {"sp": 0, "pool": 0}
        self.nins = 0

    def _wait(self, E, toks):
        for key, (sem, val) in toks.items():
            if E is self.pe and key == "pe":
                continue
            if E.seen.get(key, 0) >= val:
                continue
            E.e.wait_ge(sem, val)
            E.seen[key] = val

    @staticmethod
    def _add(need, d):
        for k, sv in d.items():
            if k not in need or need[k][1] < sv[1]:
                need[k] = sv

    def _deps(self, r, w, wp, ekey=None):
        need = {}
        for b in r:
            self._add(need, b.w)
            if b.ps:
                self._add(need, {k: v for k, v in b.r.items() if k != ekey})
        for b in w:
            self._add(need, b.w)
            self._add(need, b.r)
        for b in wp:
            self._add(need, b.r)
            if b.ps:
                self._add(need, b.w)
        return need

    def _reg(self, key, tok, r, w, wp):
        for b in r:
            if key not in b.r or b.r[key][1] < tok[1]:
                b.r[key] = tok
        for b in w:
            b.w = {key: tok}
            b.r = {}
        for b in wp:
            if key not in b.w or b.w[key][1] < tok[1]:
                b.w[key] = tok

    def op(self, E, fn, r=(), w=(), wp=(), inc=True):
        self._wait(E, self._deps(r, w, wp, E.name))
        ins = fn()
        self.nins += 1
        if inc:
            E.cnt += 1
            ins.then_inc(E.sem, 1)
            tok = (E.sem, E.cnt)
        else:
            tok = (E.sem, E.cnt + 1)
        self._reg(E.name, tok, r, w, wp)

    def dma(self, Q, out, in_, r=(), w=(), wp=()):
        E = self.sp if Q == "sp" else self.pool
        self._wait(E, self._deps(r, w, wp))
        lst = self.dsem[Q]
        i = self.drr[Q]
        self.drr[Q] = (i + 1) % len(lst)
        sem, cnt = lst[i]
        key = (Q, i)
        if cnt > 0 and E.seen.get(key, 0) < cnt:
            E.e.wait_ge(sem, cnt)
            E.seen[key] = cnt
        ins = E.e.dma_start(out=out, in_=in_)
        ins.then_inc(sem, 16)
        self.nins += 1
        lst[i][1] = cnt + 16
        self._reg(key, (sem, cnt + 16), r, w, wp)

    def barrier(self):
        toks = {}
        for E in self.engs:
            if E.cnt > 0:
                toks[E.name] = (E.sem, E.cnt)
        for Q, lst in self.dsem.items():
            for i, (sem, cnt) in enumerate(lst):
                if cnt > 0:
                    toks[(Q, i)] = (sem, cnt)
        for E in self.engs:
            self._wait(E, toks)


def bc(ap, shape, axis):
    return ap.unsqueeze(axis).to_broadcast(list(shape))


class Prog:
    def __init__(self, phases):
        self.phases = phases
        nc = bass.Bass("TRN2", target_bir_lowering=False)
        self.nc = nc
        di = lambda name, shape: nc.dram_tensor(name, list(shape), F32, kind="ExternalInput").ap()
        self.x = di("x", [SEQ, D])
        self.meta = di("meta", [NM, D])
        self.w_ssd_in = di("ssd_w_in", [2, D, SSD_IN])
        self.w_ssd_out = di("ssd_w_out", [2, 2048, D])
        self.w_mla_in = di("mla_w_in", [2, D, 672])
        self.w_mla_qb = di("mla_w_q_b", [2, 384, 1536])
        self.w_mla_kvb = di("mla_w_kv_b", [2, 256, 2048])
        self.w_mla_out = di("mla_w_out", [2, D, D])
        self.w_up = di("mlp_w_up", [4, D, DFF])
        self.w_dn = di("mlp_w_down", [4, DFF, D])
        self.lnT_d = di("lnT", [128, 8, 8])
        self.cw_d = di("cw", [2, 128, 32, 4])
        self.cb_d = di("cb", [2, 128, 32])
        self.dtb_d = di("dtb_rep", [2, 128, 32])
        self.alog_d = di("alog_rep", [2, 128, 32])
        self.dsk_d = di("dskip_rep", [2, 128, 32])
        self.sng_d = di("ssd_normT", [2, 128, 16])
        self.qag_d = di("q_a_T", [2, 128, 3])
        self.kvag_d = di("kv_a_T", [2, 128, 2])
        self.gq_d = di("gq_rep", [2, 128, 96])
        self.gk_d = di("gk_rep", [2, 128, 96])
        self.ident_d = di("ident", [128, 128])
        self.tri_d = di("tri", [128, 128])
        self.ltri_d = di("ltri", [128, 128])
        self.ones_d = di("ones", [128, 128])
        self.cos_d = di("cos", [NT, 16])
        self.sin_d = di("sin", [NT, 16])
        self.y = nc.dram_tensor("y", [SEQ, D], F32, kind="ExternalOutput").ap()
        self.hd = nc.dram_tensor("hd", [NT, D], F32, kind="Internal").ap()
        self.qT_d = nc.dram_tensor("qT_d", [MLA_H, QK, NT], BF16, kind="Internal").ap()
        self.kT_d = nc.dram_tensor("kT_d", [MLA_H, QK, NT], BF16, kind="Internal").ap()
        self.va_d = nc.dram_tensor("va_d", [NT, 8, 192], BF16, kind="Internal").ap()

        with contextlib.ExitStack() as es:
            self.kb = KB(nc, es)
            self.build(es)

    def S(self, es, name, shape, dt):
        self.uid = getattr(self, "uid", 0) + 1
        return es.enter_context(self.nc.sbuf_tensor("sb%d_%s" % (self.uid, name), list(shape), dt))

    def P(self, es, name, shape, dt):
        self.uid = getattr(self, "uid", 0) + 1
        return es.enter_context(self.nc.psum_tensor("ps%d_%s" % (self.uid, name), list(shape), dt))

    def h_src(self, kind, t0, n):
        if kind == "x":
            return self.meta[0:16, :] if t0 == 0 else self.x[t0 - 16:t0 - 16 + n, :]
        return self.hd[t0:t0 + n, :]

    def h_dst(self, kind, t0, n):
        if kind == "y":
            return None if t0 == 0 else self.y[t0 - 16:t0 - 16 + n, :]
        return self.hd[t0:t0 + n, :]

    def build(self, es):
        kb, nc = self.kb, self.nc
        self.ident = self.S(es, "ident", [128, 128], BF16)
        self.tri32 = self.S(es, "tri32", [128, 128], F32)
        self.tri16 = self.S(es, "tri16", [128, 128], BF16)
        self.ltri32 = self.S(es, "ltri32", [128, 128], F32)
        self.ones32 = self.S(es, "ones32", [128, 128], F32)
        self.lnT = self.S(es, "lnT", [128, 8, 8], F32)
        self.cb_ = Buf("consts")
        kb.dma("pool", self.ident[:], self.ident_d, wp=[self.cb_])
        kb.dma("pool", self.tri16[:], self.tri_d, wp=[self.cb_])
        kb.dma("sp", self.tri32[:], self.tri_d, wp=[self.cb_])
        kb.dma("sp", self.ltri32[:], self.ltri_d, wp=[self.cb_])
        kb.dma("sp", self.ones32[:], self.ones_d, wp=[self.cb_])
        kb.dma("sp", self.lnT[:], self.lnT_d, wp=[self.cb_])
        for ph in self.phases:
            kind = ph[0]
            with contextlib.ExitStack() as pes:
                if kind == "mlp":
                    self.phase_mlp(pes, *ph[1:])
                elif kind == "ssd":
                    self.phase_ssd(pes, *ph[1:])
                elif kind == "mla":
                    self.phase_mla(pes, *ph[1:])
                kb.barrier()
        kb.barrier()

    def rstd_newton(self, st, st_b, n, inv_n, eps):
        kb, nc = self.kb, self.nc
        V = nc.vector
        I32 = mybir.dt.int32
        x, g, t = st[0:n, 1:2], st[0:n, 2:3], st[0:n, 3:4]
        kb.op(kb.dve, lambda: V.tensor_scalar(x, st[0:n, 0:1], inv_n, eps, ALU.mult, ALU.add), r=[st_b], wp=[st_b])
        kb.op(kb.dve, lambda: V.tensor_scalar(g.bitcast(I32), x.bitcast(I32), 1, None, ALU.arith_shift_right), r=[st_b], wp=[st_b])
        kb.op(kb.dve, lambda: V.tensor_scalar(g.bitcast(I32), g.bitcast(I32), -1, 0x5f3759df, ALU.mult, ALU.add), r=[st_b], wp=[st_b])
        for _ in range(2):
            kb.op(kb.dve, lambda: V.scalar_tensor_tensor(t, g, x, g, ALU.mult, ALU.mult), r=[st_b], wp=[st_b])
            kb.op(kb.dve, lambda: V.tensor_scalar(t, t, -0.5, 1.5, ALU.mult, ALU.add), r=[st_b], wp=[st_b])
            kb.op(kb.dve, lambda: V.tensor_tensor(g, g, t, ALU.mult), r=[st_b], wp=[st_b])

    def norm_T(self, hb, h_ap, n, gain_ap, dst_ap, dst_b, sc, newton=False):
        kb, nc = self.kb, self.nc
        junk, junk_b, ss, ss_b, xn, xn_b, tp, tp_b = sc
        kb.op(kb.act, lambda: nc.scalar.activation(out=junk[0:n, :], in_=h_ap, func=AF.Square, accum_out=ss[0:n, 0:1]),
              r=[hb], w=[junk_b, ss_b] if junk_b is not xn_b else [xn_b, ss_b])
        if newton:
            self.rstd_newton(ss, ss_b, n, 1.0 / D, EPS)
        else:
            kb.op(kb.act, lambda: nc.scalar.activation(out=ss[0:n, 1:2], in_=ss[0:n, 0:1], func=AF.Ln, bias=EPS, scale=1.0 / D),
                  r=[ss_b], wp=[ss_b])
            kb.op(kb.act, lambda: nc.scalar.activation(out=ss[0:n, 2:3], in_=ss[0:n, 1:2], func=AF.Exp, scale=-0.5),
                  r=[ss_b], wp=[ss_b])
        kb.op(kb.dve, lambda: nc.vector.tensor_scalar(xn[0:n, :], h_ap, ss[0:n, 2:3], None, ALU.mult),
              r=[hb, ss_b], w=[xn_b])
        for c in range(8):
            kb.op(kb.pe, lambda c=c: nc.tensor.transpose(tp[:, c, 0:n], xn[0:n, c * 128:(c + 1) * 128], self.ident[0:n, 0:n]),
                  r=[xn_b, self.cb_], w=[tp_b] if c == 0 else [], wp=[tp_b] if c else [], inc=(c == 7))
        kb.op(kb.dve, lambda: nc.vector.tensor_tensor(dst_ap, tp[:, :, 0:n], bc(gain_ap, [128, 8, n], 2), ALU.mult),
              r=[tp_b, self.cb_], w=[dst_b])

    def norm_scratch(self, es, pfx, tp_ps):
        ss = self.S(es, pfx + "ss", [128, 4], F32)
        xn = self.S(es, pfx + "xn", [128, 1024], BF16)
        tp = tp_ps[:].bitcast(BF16)[:, 0:1024].rearrange("p (c t) -> p c t", c=8)
        xn_b = Buf()
        return (xn, xn_b, ss, Buf(), xn, xn_b, tp, PB())

    def phase_mlp(self, es, li, src, dst):
        kb, nc = self.kb, self.nc
        wup = self.S(es, "wup", [128, 8, DFF], BF16)
        wdn = self.S(es, "wdn", [128, 32, D], BF16)
        wup_b, wdn_b = Buf(), Buf()
        upv = self.w_up[li].rearrange("(c p) f -> p c f", p=128)
        dnv = self.w_dn[li].rearrange("(c p) d -> p c d", p=128)
        for c in range(8):
            kb.dma("pool", wup[:, c, :], upv[:, c, :], wp=[wup_b])
        for c in range(0, 32, 4):
            kb.dma("pool", wdn[:, c:c + 4, :], dnv[:, c:c + 4, :], wp=[wdn_b])
        NSLOT = 7
        hs = [self.S(es, "mh%d" % i, [128, D], F32) for i in range(NSLOT)]
        hs_b = [Buf() for _ in range(NSLOT)]
        xnT = self.S(es, "m_xnT", [128, 8, 512], BF16)
        xnT_b = Buf()
        uT = self.S(es, "m_uT", [128, 32, 512], BF16)
        uT_b = [Buf() for _ in range(32)]
        r32 = [self.S(es, "m_r32_%d" % i, [128, 512], F32) for i in range(2)]
        r32_b = [Buf(), Buf()]
        tp_ps = [self.P(es, "m_tp%d" % i, [128, 512], F32) for i in range(2)]
        pu = [self.P(es, "m_pu%d" % i, [128, 512], F32) for i in range(3)]
        pu_b = [PB() for _ in range(3)]
        pd = [self.P(es, "m_pd%d" % i, [128, 512], F32) for i in range(3)]
        pd_b = [PB() for _ in range(3)]
        nsc = [self.norm_scratch(es, "m%d" % i, tp_ps[i]) for i in range(2)]
        gain = self.lnT[:, 4 + li, :]
        groups = [[TILES[0]]] + [TILES[1 + 4 * g:5 + 4 * g] for g in range(8)]
        slot = 0
        iu = 0
        ipd = 0
        inorm = 0
        for grp in groups:
            ntok = sum(n for _, n in grp)
            myslots = []
            for (t0, n) in grp:
                s = slot % NSLOT
                slot += 1
                myslots.append(s)
                kb.dma("sp", hs[s][0:n, :], self.h_src(src, t0, n), w=[hs_b[s]])
            off = 0
            for (t0, n), s in zip(grp, myslots):
                self.norm_T(hs_b[s], hs[s][0:n, :], n, gain, xnT[:, :, off:off + n], xnT_b, nsc[inorm % 2])
                inorm += 1
                off += n
            for fc in range(32):
                p = iu % 3
                for c in range(8):
                    kb.op(kb.pe, lambda c=c, fc=fc, p=p: nc.tensor.matmul(pu[p][:, 0:ntok], wup[:, c, fc * 128:(fc + 1) * 128],
                                                                      xnT[:, c, 0:ntok], start=(c == 0), stop=(c == 7)),
                          r=[wup_b, xnT_b], w=[pu_b[p]] if c == 0 else [], wp=[pu_b[p]] if c else [], inc=(c == 7))
                rr = iu % 2
                kb.op(kb.act, lambda p=p, rr=rr: nc.scalar.activation(out=r32[rr][:, 0:ntok], in_=pu[p][:, 0:ntok], func=AF.Relu),
                      r=[pu_b[p]], w=[r32_b[rr]])
                E = kb.dve if (fc % 2 == 0) else kb.pool
                kb.op(E, lambda rr=rr, fc=fc, E=E: E.e.tensor_tensor(uT[:, fc, 0:ntok], r32[rr][:, 0:ntok], r32[rr][:, 0:ntok], ALU.mult),
                      r=[r32_b[rr]], w=[uT_b[fc]])
                iu += 1
            off = 0
            for (t0, n), s in zip(grp, myslots):
                for hh in range(2):
                    p = ipd % 3
                    ipd += 1
                    for fc in range(32):
                        kb.op(kb.pe, lambda fc=fc, p=p, off=off, n=n, hh=hh: nc.tensor.matmul(
                            pd[p][0:n, :], uT[:, fc, off:off + n], wdn[:, fc, hh * 512:(hh + 1) * 512],
                            start=(fc == 0), stop=(fc == 31)),
                            r=[uT_b[fc], wdn_b], w=[pd_b[p]] if fc == 0 else [], wp=[pd_b[p]] if fc else [], inc=(fc == 31))
                    kb.op(kb.dve, lambda p=p, n=n, hh=hh, s=s: nc.vector.tensor_tensor(
                        hs[s][0:n, hh * 512:(hh + 1) * 512], pd[p][0:n, :], hs[s][0:n, hh * 512:(hh + 1) * 512], ALU.add),
                        r=[pd_b[p], hs_b[s]], wp=[hs_b[s]])
                dst_ap = self.h_dst(dst, t0, n)
                if dst_ap is not None:
                    kb.dma("sp", dst_ap, hs[s][0:n, :], r=[hs_b[s]])
                off += n

    def phase_ssd(self, es, j, li, src, dst):
        from functools import partial
        kb, nc = self.kb, self.nc
        V, A, G, T = nc.vector, nc.scalar, nc.gpsimd, nc.tensor
        w_in = self.S(es, "s_win", [128, 8, SSD_IN], BF16)
        w_out = self.S(es, "s_wout", [128, 16, D], BF16)
        win_b, wout_b, par_b = Buf(), Buf(), Buf()
        wv = self.w_ssd_in[j].rearrange("(c p) f -> p c f", p=128)
        for c in range(8):
            kb.dma("pool", w_in[:, c, :], wv[:, c, :], wp=[win_b])
        wov = self.w_ssd_out[j].rearrange("(c p) f -> p c f", p=128)
        for c in range(0, 16, 4):
            kb.dma("pool", w_out[:, c:c + 4, :], wov[:, c:c + 4, :], wp=[wout_b])
        cw = self.S(es, "s_cw", [128, 32, 4], F32)
        cb = self.S(es, "s_cb", [128, 32], F32)
        dtb = self.S(es, "s_dtb", [128, 32], F32)
        arep = self.S(es, "s_arep", [128, 32], F32)
        dsk = self.S(es, "s_dsk", [128, 32], F32)
        sng = self.S(es, "s_sng", [128, 16], F32)
        kb.dma("sp", cw[:], self.cw_d[j], wp=[par_b])
        kb.dma("sp", cb[:], self.cb_d[j], wp=[par_b])
        kb.dma("sp", dtb[:], self.dtb_d[j], wp=[par_b])
        kb.dma("sp", arep[:], self.alog_d[j], wp=[par_b])
        kb.dma("sp", dsk[:], self.dsk_d[j], wp=[par_b])
        kb.dma("sp", sng[:], self.sng_d[j], wp=[par_b])
        arep_b = Buf()
        kb.op(kb.act, lambda: A.activation(out=arep[:], in_=arep[:], func=AF.Exp), r=[par_b], w=[arep_b])
        kb.op(kb.dve, lambda: V.tensor_scalar(arep[:], arep[:], -1.0, None, ALU.mult), r=[arep_b], w=[arep_b])
        cwb = Buf()
        kb.op(kb.dve, lambda: V.tensor_scalar(cw[:], cw[:], 0.5, None, ALU.mult), r=[par_b], w=[cwb])
        kb.op(kb.dve, lambda: V.tensor_scalar(cb[:], cb[:], 0.5, None, ALU.mult), r=[par_b], wp=[cwb])
        hs = [self.S(es, "s_h%d" % i, [128, D], F32) for i in range(2)]
        hs_b = [Buf(), Buf()]
        tpF = self.P(es, "s_tpF", [128, 512], F32)
        pzs = [self.P(es, "s_pz%d" % i, [128, 512], F32) for i in range(2)]
        fm = self.P(es, "s_fm", [128, 512], F32)
        tpB = self.P(es, "s_tpB", [128, 512], F32)
        dTb = self.P(es, "s_dT", [128, 512], F32)
        bm = self.P(es, "s_bm", [128, 512], F32)
        yb = self.P(es, "s_y", [128, 512], F32)
        pz_b = [PB(), PB()]
        fm_b, tpB_b, dT_b, bm_b, y_b = PB(), PB(), PB(), PB(), PB()
        nsc = self.norm_scratch(es, "s", tpF)
        tpB16 = tpB[:].bitcast(BF16)
        tpBg = tpB16.rearrange("p (c t) -> p c t", c=8)
        xnT = [self.S(es, "s_xnT%d" % i, [128, 8, 131], BF16) for i in range(2)]
        xnT_b = [Buf(), Buf()]
        for i in range(2):
            kb.op(kb.pool, lambda: G.memset(xnT[i][:], 0.0), w=[xnT_b[i]])
        xbc_xs = self.S(es, "s_xbcxs", [128, 16, 128], BF16)
        xsT_b = [Buf() for _ in range(16)]
        xbc_bc = [self.S(es, "s_xbcbc%d" % i, [128, 16, 128], BF16) for i in range(2)]
        bc_b = [[Buf() for _ in range(16)] for _ in range(2)]
        u = [self.S(es, "s_u%d" % i, [128, 131], F32) for i in range(2)]
        u_b = [Buf(), Buf()]
        acc = [self.S(es, "s_acc%d" % i, [128, 128], F32) for i in range(4)]
        acc_b = [Buf() for _ in range(4)]
        th = [self.S(es, "s_th%d" % i, [128, 128], F32) for i in range(2)]
        th_b = [Buf(), Buf()]
        xs_tm = self.S(es, "s_xstm", [128, 2048], BF16)
        xs_b = [Buf() for _ in range(8)]
        B_tm = self.S(es, "s_Btm", [128, 1024], BF16)
        Btm_b = Buf()
        smF = self.S(es, "s_smF", [128, 4, 32], F32)
        EXPT, ACS, TMP, DTE = range(4)
        smF_b = [Buf() for _ in range(4)]
        smB = [self.S(es, "s_smB%d" % i, [128, 5, 32], F32) for i in range(2)]
        DTV, ADT, DFS, CD, W2 = range(5)
        smB_b = [[Buf() for _ in range(5)] for _ in range(2)]
        rhsD = [self.S(es, "s_rhsD%d" % i, [128, 512], F32) for i in range(2)]
        rhsD_b = [Buf(), Buf()]
        Ee = [self.S(es, "s_E%d" % i, [128, 512], F32) for i in range(2)]
        E_b = [Buf(), Buf()]
        CBm = [self.S(es, "s_CBm0", [128, 128], F32)] * 2
        _cb = Buf()
        CBm_b = [_cb, _cb]
        MT = [self.S(es, "s_MT%d" % i, [128, 512], BF16) for i in range(2)]
        MT_b = [Buf(), Buf()]
        xdt = [self.S(es, "s_xdt%d" % i, [128, 256], BF16) for i in range(2)]
        xdt_b = [Buf(), Buf()]
        xdtd = [self.S(es, "s_xdtd%d" % i, [128, 256], BF16) for i in range(2)]
        xdtd_b = [Buf(), Buf()]
        tt = [[self.S(es, "s_t%d_%d" % (k, i), [128, 256], F32) for i in range(2)] for k in range(4)]
        tt_b = [[Buf(), Buf()] for _ in range(4)]
        tt.append(tt[0])
        tt_b.append(tt_b[0])
        stg = self.S(es, "s_stg", [128, 8, 4], F32)
        stg_b = [Buf() for _ in range(8)]
        gT = self.S(es, "s_gT", [128, 16, 128], BF16)
        gT_b = Buf()
        S32 = self.S(es, "s_S32", [128, 2048], F32)
        Sbf = self.S(es, "s_Sbf", [128, 2048], BF16)
        S32_b = [Buf() for _ in range(8)]
        Sbf_b = [Buf() for _ in range(8)]
        stmp = [self.S(es, "s_stmp0", [128, 256], F32)] * 2
        _sb = Buf()
        stmp_b = [_sb, _sb]
        kb.op(kb.pool, lambda: G.memset(S32[:], 0.0), w=S32_b)
        kb.op(kb.pool, lambda: G.memset(Sbf[:], 0.0), w=Sbf_b)
        v3 = lambda ap: ap.rearrange("p (j l) -> p j l", j=4)
        NTL = len(TILES)

        def f_load(ti):
            t0, n = TILES[ti]
            cur, prev = ti % 2, (ti + 1) % 2
            nprev = TILES[ti - 1][1] if ti > 0 else 0
            X = xnT[cur]
            kb.dma("sp", hs[cur][0:n, :], self.h_src(src, t0, n), w=[hs_b[cur]])
            self.norm_T(hs_b[cur], hs[cur][0:n, :], n, self.lnT[:, li, :], X[:, :, 3:3 + n], xnT_b[cur], nsc, newton=True)
            kb.op(kb.pool, lambda: G.tensor_copy(X[:, :, 0:3], xnT[prev][:, :, nprev:nprev + 3]), r=[xnT_b[prev]], wp=[xnT_b[cur]])

        def f_dt1(ti):
            t0, n = TILES[ti]
            cur = ti % 2
            X = xnT[cur]
            sB, sBb = smB[cur], smB_b[cur]
            pdt = fm[0:n, 0:32]
            for c in range(8):
                kb.op(kb.pe, lambda: T.matmul(pdt, X[:, c, 3:3 + n], w_in[:, c, 6144:6176], start=(c == 0), stop=(c == 7)),
                      r=[win_b, xnT_b[cur]], w=[fm_b] if c == 0 else [], wp=[fm_b] if c else [], inc=(c == 7))
            kb.op(kb.dve, lambda: V.tensor_tensor(sB[0:n, DTV, :], pdt, dtb[0:n, :], ALU.add), r=[fm_b, par_b], w=[sBb[DTV]])
            kb.op(kb.act, lambda: A.activation(out=smF[0:n, EXPT, :], in_=sB[0:n, DTV, :], func=AF.Exp), r=[sBb[DTV]], w=[smF_b[EXPT]])
            kb.op(kb.act, lambda: A.activation(out=sB[0:n, DTV, :], in_=smF[0:n, EXPT, :], func=AF.Ln, bias=1.0), r=[smF_b[EXPT]], w=[sBb[DTV]])
            kb.op(kb.dve, lambda: V.tensor_tensor(sB[0:n, ADT, :], sB[0:n, DTV, :], arep[0:n, :], ALU.mult), r=[sBb[DTV], arep_b], w=[sBb[ADT]])

        def f_dt2(ti):
            t0, n = TILES[ti]
            cur = ti % 2
            sB, sBb = smB[cur], smB_b[cur]
            pacs, ptot = fm[0:n, 32:64], fm[:, 64:96]
            kb.op(kb.pe, lambda: T.matmul(pacs, self.tri32[0:n, 0:n], sB[0:n, ADT, :], start=True, stop=True), r=[sBb[ADT], self.cb_], w=[fm_b])
            kb.op(kb.pe, lambda: T.matmul(ptot, self.ones32[0:n, :], sB[0:n, ADT, :], start=True, stop=True), r=[sBb[ADT], self.cb_], wp=[fm_b])
            kb.op(kb.dve, lambda: V.tensor_copy(smF[0:n, ACS, :], pacs), r=[fm_b], w=[smF_b[ACS]])
            kb.op(kb.act, lambda: A.activation(out=sB[0:n, DFS, :], in_=pacs, func=AF.Exp), r=[fm_b], w=[sBb[DFS]])
            kb.op(kb.act, lambda: A.activation(out=sB[:, CD, :], in_=ptot, func=AF.Exp), r=[fm_b], w=[sBb[CD]])
            kb.op(kb.dve, lambda: V.tensor_tensor(smF[0:n, TMP, :], ptot[0:n, :], smF[0:n, ACS, :], ALU.subtract), r=[fm_b, smF_b[ACS]], w=[smF_b[TMP]])
            kb.op(kb.act, lambda: A.activation(out=smF[0:n, DTE, :], in_=smF[0:n, TMP, :], func=AF.Exp), r=[smF_b[TMP]], w=[smF_b[DTE]])
            kb.op(kb.dve, lambda: V.tensor_tensor(sB[0:n, W2, :], sB[0:n, DTV, :], smF[0:n, DTE, :], ALU.mult), r=[sBb[DTV], smF_b[DTE]], w=[sBb[W2]])

        def conv_dst(ti, cc, n):
            cur = ti % 2
            if cc < 16:
                return xbc_xs[:, cc, 0:n], xsT_b[cc]
            return xbc_bc[cur][:, cc - 16, 0:n], bc_b[cur][cc - 16]

        def f_conv(ti, s):
            t0, n = TILES[ti]
            cur = ti % 2
            X = xnT[cur]
            cc = s
            if 0 <= cc < 32:
                p = cc % 2
                pz = pzs[p][:, 0:3 + n]
                for c in range(8):
                    kb.op(kb.pe, lambda: T.matmul(pz, w_in[:, c, 2048 + cc * 128:2048 + (cc + 1) * 128], X[:, c, 0:3 + n], start=(c == 0), stop=(c == 7)),
                          r=[win_b, xnT_b[cur]], w=[pz_b[p]] if c == 0 else [], wp=[pz_b[p]] if c else [], inc=(c == 7))
            cc = s - 1
            if 0 <= cc < 32:
                p = cc % 2
                pz = pzs[p][:, 0:3 + n]
                kb.op(kb.act, lambda: A.activation(out=acc[cc % 4][:, 0:n], in_=pz[:, 3:3 + n], func=AF.Identity, bias=cb[:, cc:cc + 1], scale=cw[:, cc, 3:4]),
                      r=[pz_b[p], cwb], w=[acc_b[cc % 4]])
                kb.op(kb.act, lambda: A.copy(u[p][:, 0:3 + n], pz), r=[pz_b[p]], w=[u_b[p]])
            cc = s - 2
            if 0 <= cc < 32:
                p = cc % 2
                a_ = acc[cc % 4][:, 0:n]
                for k in range(3):
                    kb.op(kb.dve, lambda: V.scalar_tensor_tensor(a_, u[p][:, k:k + n], cw[:, cc, k:k + 1], a_, ALU.mult, ALU.add),
                          r=[u_b[p], cwb], w=[acc_b[cc % 4]])
            cc = s - 3
            if 0 <= cc < 32:
                kb.op(kb.act, lambda: A.activation(out=th[cc % 2][:, 0:n], in_=acc[cc % 4][:, 0:n], func=AF.Tanh), r=[acc_b[cc % 4]], w=[th_b[cc % 2]])
            cc = s - 4
            if 0 <= cc < 32:
                dst_ap, dst_b = conv_dst(ti, cc, n)
                kb.op(kb.dve, lambda: V.scalar_tensor_tensor(dst_ap, th[cc % 2][:, 0:n], 1.0, acc[cc % 4][:, 0:n], ALU.add, ALU.mult),
                      r=[th_b[cc % 2], acc_b[cc % 4]], w=[dst_b])

        def front(ti):
            st = [partial(f_load, ti), partial(f_conv, ti, 0), partial(f_dt1, ti), partial(f_conv, ti, 1), partial(f_dt2, ti)]
            st += [partial(f_conv, ti, s) for s in range(2, 36)]
            return st

        def b_trx(ti, half):
            t0, n = TILES[ti]
            for k in range(8):
                cc = half * 8 + k
                kb.op(kb.pe, lambda: T.transpose(tpB16[0:n, k * 128:(k + 1) * 128], xbc_xs[:, cc, 0:n], self.ident[:, :]),
                      r=[xsT_b[cc], self.cb_], w=[tpB_b] if k == 0 else [], wp=[tpB_b] if k else [], inc=(k == 7))
            kb.op(kb.act, lambda: A.copy(xs_tm[0:n, half * 1024:(half + 1) * 1024], tpB16[0:n, :]), r=[tpB_b], w=xs_b[4 * half:4 * half + 4])

        def b_trB(ti):
            t0, n = TILES[ti]
            cur = ti % 2
            for g in range(8):
                kb.op(kb.pe, lambda: T.transpose(tpB16[0:n, g * 128:(g + 1) * 128], xbc_bc[cur][:, g, 0:n], self.ident[:, :]),
                      r=[bc_b[cur][g], self.cb_], w=[tpB_b] if g == 0 else [], wp=[tpB_b] if g else [], inc=(g == 7))
            kb.op(kb.dve, lambda: V.tensor_copy(B_tm[0:n, :], tpB16[0:n, :]), r=[tpB_b], w=[Btm_b])

        def gchain(ti, g):
            t0, n = TILES[ti]
            cur = ti % 2
            X = xnT[cur]
            sB, sBb = smB[cur], smB_b[cur]
            b2 = g % 2
            hsl = slice(4 * g, 4 * g + 4)
            gsl = slice(g * 256, (g + 1) * 256)
            BT, CT = xbc_bc[cur][:, g, 0:n], xbc_bc[cur][:, 8 + g, 0:n]
            BTb, CTb = bc_b[cur][g], bc_b[cur][8 + g]
            x3 = lambda ap: ap.rearrange("p (j f) -> p j f", j=4)
            rD = rhsD[b2][0:n, 0:4 * n]
            Ev = Ee[b2][0:n, 0:4 * n]
            MTv = MT[b2][0:n, 0:4 * n]
            pcb = bm[0:n, 0:n]
            pst = bm[:, 256:512]
            zq = fm[0:n, 256:512]
            xs3 = x3(xs_tm[0:n, gsl])
            t1, t2, t3, szg, thg = [tt[k][b2][0:n, :] for k in range(5)]
            t1b, t2b, t3b, szb, thb = [tt_b[k][b2] for k in range(5)]
            steps = []

            def s1():
                kb.op(kb.pool, lambda: G.tensor_tensor(v3(rD), bc(sB[0:n, ADT, hsl], [n, 4, n], 2), bc(self.tri32[0:n, 0:n], [n, 4, n], 1), ALU.mult),
                      r=[sBb[ADT], self.cb_], w=[rhsD_b[b2]])
                kb.op(kb.pool, lambda: G.tensor_tensor(x3(xdt[b2][0:n, :]), xs3, bc(sB[0:n, DTV, hsl], [n, 4, 64], 2), ALU.mult),
                      r=[xs_b[g], sBb[DTV]], w=[xdt_b[b2]])
            steps.append(s1)

            def s2():
                kb.op(kb.pe, lambda: T.matmul(dTb[0:n, 0:4 * n], self.ltri32[0:n, 0:n], rD, start=True, stop=True), r=[rhsD_b[b2], self.cb_], w=[dT_b])
                kb.op(kb.act, lambda: A.activation(out=Ev, in_=dTb[0:n, 0:4 * n], func=AF.Exp), r=[dT_b], w=[E_b[b2]])
                kb.op(kb.pool, lambda: G.tensor_tensor(x3(xdtd[b2][0:n, :]), xs3, bc(sB[0:n, W2, hsl], [n, 4, 64], 2), ALU.mult),
                      r=[xs_b[g], sBb[W2]], w=[xdtd_b[b2]])
            steps.append(s2)

            def s3():
                kb.op(kb.pe, lambda: T.matmul(pcb, BT, CT, start=True, stop=True), r=[BTb, CTb], w=[bm_b])
                kb.op(kb.dve, lambda: V.tensor_tensor(CBm[b2][0:n, 0:n], pcb, self.tri32[0:n, 0:n], ALU.mult), r=[bm_b, self.cb_], w=[CBm_b[b2]])
                for c in range(8):
                    kb.op(kb.pe, lambda: T.matmul(zq, X[:, c, 3:3 + n], w_in[:, c, g * 256:(g + 1) * 256], start=(c == 0), stop=(c == 7)),
                          r=[win_b, xnT_b[cur]], w=[fm_b] if c == 0 else [], wp=[fm_b] if c else [], inc=(c == 7))
                kb.op(kb.act, lambda: A.activation(out=thg, in_=zq, func=AF.Tanh, scale=0.5), r=[fm_b], w=[thb])
                kb.op(kb.dve, lambda: V.scalar_tensor_tensor(szg, thg, 1.0, zq, ALU.add, ALU.mult), r=[thb, fm_b], w=[szb])
                kb.op(kb.pool, lambda: G.tensor_tensor(v3(MTv), v3(Ev), bc(CBm[b2][0:n, 0:n], [n, 4, n], 1), ALU.mult),
                      r=[E_b[b2], CBm_b[b2]], w=[MT_b[b2]])
            steps.append(s3)

            def s4():
                for jj in range(4):
                    kb.op(kb.pe, lambda: T.matmul(yb[0:n, jj * 64:(jj + 1) * 64], MTv[:, jj * n:(jj + 1) * n], xdt[b2][0:n, jj * 64:(jj + 1) * 64], start=True, stop=True),
                          r=[MT_b[b2], xdt_b[b2]], w=[y_b] if jj == 0 else [], wp=[y_b] if jj else [], inc=False)
                kb.op(kb.pe, lambda: T.matmul(yb[0:n, 256:512], CT, Sbf[:, gsl], start=True, stop=True), r=[CTb, Sbf_b[g]], wp=[y_b])
                kb.op(kb.pool, lambda: G.tensor_tensor(x3(t3), xs3, bc(dsk[0:n, hsl], [n, 4, 64], 2), ALU.mult), r=[xs_b[g], par_b], w=[t3b])
                kb.op(kb.dve, lambda: V.tensor_tensor(x3(t1), x3(yb[0:n, 256:512]), bc(sB[0:n, DFS, hsl], [n, 4, 64], 2), ALU.mult),
                      r=[y_b, sBb[DFS]], w=[t1b])
                kb.op(kb.dve, lambda: V.tensor_tensor(t2, yb[0:n, 0:256], t1, ALU.add), r=[y_b, t1b], w=[t2b])
            steps.append(s4)

            def s5():
                kb.op(kb.pe, lambda: T.matmul(pst, B_tm[0:n, g * 128:(g + 1) * 128], xdtd[b2][0:n, :], start=True, stop=True),
                      r=[Btm_b, xdtd_b[b2]], w=[bm_b])
                kb.op(kb.pool, lambda: G.tensor_tensor(x3(stmp[b2][:, :]), x3(S32[:, gsl]), bc(sB[:, CD, hsl], [128, 4, 64], 2), ALU.mult),
                      r=[S32_b[g], sBb[CD]], w=[stmp_b[b2]])
                kb.op(kb.dve, lambda: V.tensor_tensor(S32[:, gsl], stmp[b2][:, :], pst, ALU.add), r=[stmp_b[b2], bm_b], w=[S32_b[g]])
                kb.op(kb.act, lambda: A.copy(Sbf[:, gsl], S32[:, gsl]), r=[S32_b[g]], w=[Sbf_b[g]])
            steps.append(s5)

            def s6():
                kb.op(kb.pool, lambda: G.tensor_tensor(t2, t2, t3, ALU.add), r=[t3b], w=[t2b])
                kb.op(kb.pool, lambda: G.tensor_tensor(t2, t2, szg, ALU.mult), r=[szb], w=[t2b])
            steps.append(s6)

            def s7():
                kb.op(kb.act, lambda: A.activation(out=t1, in_=t2, func=AF.Square, accum_out=stg[0:n, g, 0:1]), r=[t2b], w=[t1b, stg_b[g]])
                self.rstd_newton(stg[:, g, :], stg_b[g], n, 1.0 / 256, 4.0 * EPS)
                kb.op(kb.dve, lambda: V.tensor_scalar(xs_tm[0:n, gsl], t2, stg[0:n, g, 2:3], None, ALU.mult), r=[t2b, stg_b[g]], w=[xs_b[g]])
            steps.append(s7)
            return steps

        def b_gT(ti, half):
            t0, n = TILES[ti]
            for k in range(8):
                cc = half * 8 + k
                kb.op(kb.pe, lambda: T.transpose(tpBg[:, k, 0:n], xs_tm[0:n, cc * 128:(cc + 1) * 128], self.ident[0:n, 0:n]),
                      r=[xs_b[cc // 2], self.cb_], w=[tpB_b] if k == 0 else [], wp=[tpB_b] if k else [], inc=(k == 7))
            kb.op(kb.dve, lambda: V.tensor_tensor(gT[:, half * 8:(half + 1) * 8, 0:n], tpBg[:, :, 0:n], bc(sng[:, half * 8:(half + 1) * 8], [128, 8, n], 2), ALU.mult),
                  r=[tpB_b, par_b], w=[gT_b] if half == 0 else [], wp=[gT_b] if half else [])

        def b_out(ti, hh):
            t0, n = TILES[ti]
            cur = ti % 2
            po = dTb[0:n, :] if hh == 0 else yb[0:n, :]
            pbuf = dT_b if hh == 0 else y_b
            for cc in range(16):
                kb.op(kb.pe, lambda: T.matmul(po, gT[:, cc, 0:n], w_out[:, cc, hh * 512:(hh + 1) * 512], start=(cc == 0), stop=(cc == 15)),
                      r=[gT_b, wout_b], w=[pbuf] if cc == 0 else [], wp=[pbuf] if cc else [], inc=(cc == 15))
            kb.op(kb.dve, lambda: V.tensor_tensor(hs[cur][0:n, hh * 512:(hh + 1) * 512], po, hs[cur][0:n, hh * 512:(hh + 1) * 512], ALU.add),
                  r=[pbuf, hs_b[cur]], wp=[hs_b[cur]])
            if hh == 1:
                dap = self.h_dst(dst, t0, n)
                if dap is not None:
                    kb.dma("sp", dap, hs[cur][0:n, :], r=[hs_b[cur]])

        def back(ti):
            st = [partial(b_trx, ti, 0), partial(b_trx, ti, 1), partial(b_trB, ti)]
            for gp in range(4):
                ca, cb_ = gchain(ti, 2 * gp), gchain(ti, 2 * gp + 1)
                for a_, b_ in zip(ca, cb_):
                    st += [a_, b_]
            st += [partial(b_gT, ti, 0), partial(b_gT, ti, 1), partial(b_out, ti, 0), partial(b_out, ti, 1)]
            return st

        def interleave(a, b):
            na, nb = len(a), len(b)
            i = jx = 0
            while i < na or jx < nb:
                if jx >= nb or (i < na and i * nb <= jx * na):
                    a[i]()
                    i += 1
                else:
                    b[jx]()
                    jx += 1

        for ti in range(NTL + 1):
            f = front(ti) if ti < NTL else []
            b = back(ti - 1) if ti >= 1 else []
            interleave(f, b)

    def phase_ssd_v1(self, es, j, li, src, dst):
        kb, nc = self.kb, self.nc
        V, A, G, T = nc.vector, nc.scalar, nc.gpsimd, nc.tensor
        w_in = self.S(es, "s_win", [128, 8, SSD_IN], BF16)
        w_out = self.S(es, "s_wout", [128, 16, D], BF16)
        win_b, wout_b, par_b = Buf(), Buf(), Buf()
        wv = self.w_ssd_in[j].rearrange("(c p) f -> p c f", p=128)
        for c in range(8):
            kb.dma("pool", w_in[:, c, :], wv[:, c, :], wp=[win_b])
        wov = self.w_ssd_out[j].rearrange("(c p) f -> p c f", p=128)
        for c in range(0, 16, 4):
            kb.dma("pool", w_out[:, c:c + 4, :], wov[:, c:c + 4, :], wp=[wout_b])
        cw = self.S(es, "s_cw", [128, 32, 4], F32)
        cb = self.S(es, "s_cb", [128, 32], F32)
        dtb = self.S(es, "s_dtb", [128, 32], F32)
        arep = self.S(es, "s_arep", [128, 32], F32)
        dsk = self.S(es, "s_dsk", [128, 32], F32)
        sng = self.S(es, "s_sng", [128, 16], F32)
        kb.dma("sp", cw[:], self.cw_d[j], wp=[par_b])
        kb.dma("sp", cb[:], self.cb_d[j], wp=[par_b])
        kb.dma("sp", dtb[:], self.dtb_d[j], wp=[par_b])
        kb.dma("sp", arep[:], self.alog_d[j], wp=[par_b])
        kb.dma("sp", dsk[:], self.dsk_d[j], wp=[par_b])
        kb.dma("sp", sng[:], self.sng_d[j], wp=[par_b])
        arep_b = Buf()
        kb.op(kb.act, lambda: A.activation(out=arep[:], in_=arep[:], func=AF.Exp), r=[par_b], w=[arep_b])
        kb.op(kb.dve, lambda: V.tensor_scalar(arep[:], arep[:], -1.0, None, ALU.mult), r=[arep_b], w=[arep_b])
        hs = [self.S(es, "s_h%d" % i, [128, D], F32) for i in range(2)]
        hs_b = [Buf(), Buf()]
        tp2 = self.P(es, "s_tp2", [128, 1024], F32)
        pzbs = [self.P(es, "s_pz%d" % i, [128, 512], F32) for i in range(2)]
        zqb = self.P(es, "s_zq", [128, 512], F32)
        dTb = self.P(es, "s_dT", [128, 512], F32)
        miscb = self.P(es, "s_misc", [128, 512], F32)
        yb = self.P(es, "s_y", [128, 512], F32)
        pz_b = [PB(), PB()]
        _z = PB()
        zq_b = [_z, _z]
        dT_b = PB()
        misc_b = PB()
        pdt_b = pacs_b = ptot_b = pst_b = misc_b
        pcb_b = [misc_b, misc_b]
        ydg_b = yof_b = PB()
        nsc = self.norm_scratch(es, "s", tp2)
        tp_b = nsc[7]
        tp16 = tp2[:].bitcast(BF16)
        tpg = tp16.rearrange("p (c t) -> p c t", c=16)
        xnT = [self.S(es, "s_xnT%d" % i, [128, 8, 131], BF16) for i in range(2)]
        xnT_b = [Buf(), Buf()]
        for i in range(2):
            kb.op(kb.pool, lambda: G.memset(xnT[i][:], 0.0), w=[xnT_b[i]])
        xbcT = self.S(es, "s_xbcT", [128, 32, 128], BF16)
        xbc_b = [Buf() for _ in range(32)]
        acc = [self.S(es, "s_acc%d" % i, [128, 128], F32) for i in range(3)]
        acc_b = [Buf() for _ in range(3)]
        xs_tm = self.S(es, "s_xstm", [128, 2048], BF16)
        xs_b = Buf()
        B_tm = self.S(es, "s_Btm", [128, 1024], BF16)
        Btm_b = Buf()
        sz = self.S(es, "s_sz", [128, 2048], F32)
        sz_b = [Buf() for _ in range(8)]
        sm = self.S(es, "s_sm", [128, 10, 32], F32)
        DTV, EXPT, ADT, ACS, DFS, CD, DTE, TMP, W2 = range(9)
        sm_b = [Buf() for _ in range(10)]
        rhsD = [self.S(es, "s_rhsD0", [128, 512], F32)] * 2
        _b = Buf()
        rhsD_b = [_b, _b]
        Ee = [self.S(es, "s_E0", [128, 512], F32)] * 2
        _b = Buf()
        E_b = [_b, _b]
        CBm = [self.S(es, "s_CBm%d" % i, [128, 128], F32) for i in range(2)]
        CBm_b = [Buf(), Buf()]
        MT = [self.S(es, "s_MT%d" % i, [128, 512], BF16) for i in range(2)]
        MT_b = [Buf(), Buf()]
        xdt = [self.S(es, "s_xdt%d" % i, [128, 256], BF16) for i in range(2)]
        xdt_b = [Buf(), Buf()]
        xdtd = [self.S(es, "s_xdtd%d" % i, [128, 256], BF16) for i in range(2)]
        xdtd_b = [Buf(), Buf()]
        tt = [[self.S(es, "s_t%d_%d" % (k, i), [128, 256], F32) for i in range(2)] for k in range(3)]
        tt_b = [[Buf(), Buf()] for _ in range(3)]
        tt.append(tt[0])
        tt_b.append(tt_b[0])
        stg = self.S(es, "s_stg", [128, 8, 4], F32)
        stg_b = [Buf() for _ in range(8)]
        gn = self.S(es, "s_gn", [128, 2048], BF16)
        gn_b = Buf()
        gT = self.S(es, "s_gT", [128, 16, 128], BF16)
        gT_b = Buf()
        S32 = self.S(es, "s_S32", [128, 2048], F32)
        Sbf = self.S(es, "s_Sbf", [128, 2048], BF16)
        S32_b = [Buf() for _ in range(8)]
        Sbf_b = [Buf() for _ in range(8)]
        stmp = [self.S(es, "s_stmp0", [128, 256], F32)] * 2
        _b = Buf()
        stmp_b = [_b, _b]
        kb.op(kb.pool, lambda: G.memset(S32[:], 0.0), w=S32_b)
        kb.op(kb.pool, lambda: G.memset(Sbf[:], 0.0), w=Sbf_b)
        nprev = 0
        ipz = 0
        for ti, (t0, n) in enumerate(TILES):
            cur, prev = ti % 2, (ti + 1) % 2
            X = xnT[cur]
            kb.dma("sp", hs[cur][0:n, :], self.h_src(src, t0, n), w=[hs_b[cur]])
            self.norm_T(hs_b[cur], hs[cur][0:n, :], n, self.lnT[:, li, :], X[:, :, 3:3 + n], xnT_b[cur], nsc)
            kb.op(kb.pool, lambda: G.tensor_copy(X[:, :, 0:3], xnT[prev][:, :, nprev:nprev + 3]), r=[xnT_b[prev]], wp=[xnT_b[cur]])
            nprev = n
            for cc in range(32):
                p = ipz % 2
                ipz += 1
                pz = pzbs[p][:, 0:3 + n]
                for c in range(8):
                    kb.op(kb.pe, lambda: T.matmul(pz, w_in[:, c, 2048 + cc * 128:2048 + (cc + 1) * 128], X[:, c, 0:3 + n], start=(c == 0), stop=(c == 7)),
                          r=[win_b, xnT_b[cur]], w=[pz_b[p]] if c == 0 else [], wp=[pz_b[p]] if c else [], inc=(c == 7))
                a_ = acc[p][:, 0:n]
                kb.op(kb.act, lambda: A.activation(out=a_, in_=pz[:, 3:3 + n], func=AF.Identity, bias=cb[:, cc:cc + 1], scale=cw[:, cc, 3:4]),
                      r=[pz_b[p], par_b], w=[acc_b[p]])
                for k in range(3):
                    kb.op(kb.dve, lambda: V.scalar_tensor_tensor(a_, pz[:, k:k + n], cw[:, cc, k:k + 1], a_, ALU.mult, ALU.add),
                          r=[pz_b[p], par_b], w=[acc_b[p]])
                kb.op(kb.act, lambda: A.activation(out=xbcT[:, cc, 0:n], in_=a_, func=AF.Silu), r=[acc_b[p]], w=[xbc_b[cc]])
            for g in range(8):
                zs = g % 2
                zq = zqb[0:n, zs * 256:(zs + 1) * 256]
                for c in range(8):
                    kb.op(kb.pe, lambda: T.matmul(zq, X[:, c, 3:3 + n], w_in[:, c, g * 256:(g + 1) * 256], start=(c == 0), stop=(c == 7)),
                          r=[win_b, xnT_b[cur]], w=[zq_b[zs]] if c == 0 else [], wp=[zq_b[zs]] if c else [], inc=(c == 7))
                kb.op(kb.act, lambda: A.activation(out=sz[0:n, g * 256:(g + 1) * 256], in_=zq, func=AF.Silu), r=[zq_b[zs]], w=[sz_b[g]])
            for cc in range(16):
                kb.op(kb.pe, lambda: T.transpose(tp16[0:n, cc * 128:(cc + 1) * 128], xbcT[:, cc, 0:n], self.ident[:, :]),
                      r=[xbc_b[cc], self.cb_], w=[tp_b] if cc == 0 else [], wp=[tp_b] if cc else [], inc=(cc == 15))
            kb.op(kb.act, lambda: A.copy(xs_tm[0:n, 0:1024], tp16[0:n, 0:1024]), r=[tp_b], w=[xs_b])
            kb.op(kb.dve, lambda: V.tensor_copy(xs_tm[0:n, 1024:2048], tp16[0:n, 1024:2048]), r=[tp_b], wp=[xs_b])
            for g in range(8):
                kb.op(kb.pe, lambda: T.transpose(tp16[0:n, g * 128:(g + 1) * 128], xbcT[:, 16 + g, 0:n], self.ident[:, :]),
                      r=[xbc_b[16 + g], self.cb_], w=[tp_b] if g == 0 else [], wp=[tp_b] if g else [], inc=(g == 7))
            kb.op(kb.act, lambda: A.copy(B_tm[0:n, :], tp16[0:n, 0:1024]), r=[tp_b], w=[Btm_b])
            pdt, pacs, ptot = miscb[0:n, 0:32], miscb[0:n, 32:64], miscb[:, 64:96]
            for c in range(8):
                kb.op(kb.pe, lambda: T.matmul(pdt, X[:, c, 3:3 + n], w_in[:, c, 6144:6176], start=(c == 0), stop=(c == 7)),
                      r=[win_b, xnT_b[cur]], w=[pdt_b] if c == 0 else [], wp=[pdt_b] if c else [], inc=(c == 7))
            smv = lambda k: sm[0:n, k, :]
            kb.op(kb.dve, lambda: V.tensor_tensor(smv(DTV), pdt, dtb[0:n, :], ALU.add), r=[pdt_b, par_b], w=[sm_b[DTV]])
            kb.op(kb.act, lambda: A.activation(out=smv(EXPT), in_=smv(DTV), func=AF.Exp), r=[sm_b[DTV]], w=[sm_b[EXPT]])
            kb.op(kb.act, lambda: A.activation(out=smv(DTV), in_=smv(EXPT), func=AF.Ln, bias=1.0), r=[sm_b[EXPT]], w=[sm_b[DTV]])
            kb.op(kb.dve, lambda: V.tensor_tensor(smv(ADT), smv(DTV), arep[0:n, :], ALU.mult), r=[sm_b[DTV], arep_b], w=[sm_b[ADT]])
            kb.op(kb.pe, lambda: T.matmul(pacs, self.tri32[0:n, 0:n], smv(ADT), start=True, stop=True), r=[sm_b[ADT], self.cb_], w=[pacs_b])
            kb.op(kb.pe, lambda: T.matmul(ptot, self.ones32[0:n, :], smv(ADT), start=True, stop=True), r=[sm_b[ADT], self.cb_], w=[ptot_b])
            kb.op(kb.dve, lambda: V.tensor_copy(smv(ACS), pacs), r=[pacs_b], w=[sm_b[ACS]])
            kb.op(kb.act, lambda: A.activation(out=smv(DFS), in_=pacs, func=AF.Exp), r=[pacs_b], w=[sm_b[DFS]])
            kb.op(kb.act, lambda: A.activation(out=sm[:, CD, :], in_=ptot, func=AF.Exp), r=[ptot_b], w=[sm_b[CD]])
            kb.op(kb.dve, lambda: V.tensor_tensor(smv(TMP), ptot[0:n, :], smv(ACS), ALU.subtract), r=[ptot_b, sm_b[ACS]], w=[sm_b[TMP]])
            kb.op(kb.act, lambda: A.activation(out=smv(DTE), in_=smv(TMP), func=AF.Exp), r=[sm_b[TMP]], w=[sm_b[DTE]])
            kb.op(kb.dve, lambda: V.tensor_tensor(smv(W2), smv(DTV), smv(DTE), ALU.mult), r=[sm_b[DTV], sm_b[DTE]], w=[sm_b[W2]])
            for g in range(8):
                b2 = g % 2
                hsl = slice(4 * g, 4 * g + 4)
                gsl = slice(g * 256, (g + 1) * 256)
                v3 = lambda ap: ap.rearrange("p (j l) -> p j l", j=4)
                rD = rhsD[b2][0:n, 0:4 * n]
                kb.op(kb.pool, lambda: G.tensor_tensor(v3(rD), bc(sm[0:n, ADT, hsl], [n, 4, n], 2), bc(self.tri32[0:n, 0:n], [n, 4, n], 1), ALU.mult),
                      r=[sm_b[ADT], self.cb_], w=[rhsD_b[b2]])
                kb.op(kb.pe, lambda: T.matmul(dTb[0:n, 0:4 * n], self.ltri32[0:n, 0:n], rD, start=True, stop=True), r=[rhsD_b[b2], self.cb_], w=[dT_b])
                Ev = Ee[b2][0:n, 0:4 * n]
                kb.op(kb.act, lambda: A.activation(out=Ev, in_=dTb[0:n, 0:4 * n], func=AF.Exp), r=[dT_b], w=[E_b[b2]])
                pcb = miscb[0:n, 128:128 + n]
                kb.op(kb.pe, lambda: T.matmul(pcb, xbcT[:, 16 + g, 0:n], xbcT[:, 24 + g, 0:n], start=True, stop=True),
                      r=[xbc_b[16 + g], xbc_b[24 + g]], w=[pcb_b[b2]])
                kb.op(kb.dve, lambda: V.tensor_tensor(CBm[b2][0:n, 0:n], pcb, self.tri32[0:n, 0:n], ALU.mult), r=[pcb_b[b2], self.cb_], w=[CBm_b[b2]])
                MTv = MT[b2][0:n, 0:4 * n]
                kb.op(kb.pool, lambda: G.tensor_tensor(v3(MTv), v3(Ev), bc(CBm[b2][0:n, 0:n], [n, 4, n], 1), ALU.mult),
                      r=[E_b[b2], CBm_b[b2]], w=[MT_b[b2]])
                xs3 = xs_tm[0:n, gsl].rearrange("p (j f) -> p j f", j=4)
                x3 = lambda ap: ap.rearrange("p (j f) -> p j f", j=4)
                kb.op(kb.pool, lambda: G.tensor_tensor(x3(xdt[b2][0:n, :]), xs3, bc(sm[0:n, DTV, hsl], [n, 4, 64], 2), ALU.mult),
                      r=[xs_b, sm_b[DTV]], w=[xdt_b[b2]])
                kb.op(kb.pool, lambda: G.tensor_tensor(x3(xdtd[b2][0:n, :]), xs3, bc(sm[0:n, W2, hsl], [n, 4, 64], 2), ALU.mult),
                      r=[xs_b, sm_b[W2]], w=[xdtd_b[b2]])
                for jj in range(4):
                    kb.op(kb.pe, lambda: T.matmul(yb[0:n, jj * 64:(jj + 1) * 64], MTv[:, jj * n:(jj + 1) * n], xdt[b2][0:n, jj * 64:(jj + 1) * 64], start=True, stop=True),
                          r=[MT_b[b2], xdt_b[b2]], w=[ydg_b] if jj == 0 else [], wp=[ydg_b] if jj else [], inc=(jj == 3))
                kb.op(kb.pe, lambda: T.matmul(yb[0:n, 256:512], xbcT[:, 24 + g, 0:n], Sbf[:, gsl], start=True, stop=True),
                      r=[xbc_b[24 + g], Sbf_b[g]], w=[yof_b])
                t1, t2, t3, tj = [tt[k][b2][0:n, :] for k in range(4)]
                kb.op(kb.dve, lambda: V.tensor_tensor(x3(t1), x3(yb[0:n, 256:512]), bc(sm[0:n, DFS, hsl], [n, 4, 64], 2), ALU.mult),
                      r=[yof_b, sm_b[DFS]], w=[tt_b[0][b2]])
                kb.op(kb.dve, lambda: V.tensor_tensor(t2, yb[0:n, 0:256], t1, ALU.add), r=[ydg_b, tt_b[0][b2]], w=[tt_b[1][b2]])
                kb.op(kb.pool, lambda: G.tensor_tensor(x3(t3), xs3, bc(dsk[0:n, hsl], [n, 4, 64], 2), ALU.mult), r=[xs_b, par_b], w=[tt_b[2][b2]])
                kb.op(kb.pool, lambda: G.tensor_tensor(t2, t2, t3, ALU.add), r=[tt_b[2][b2]], w=[tt_b[1][b2]])
                kb.op(kb.pool, lambda: G.tensor_tensor(t2, t2, sz[0:n, gsl], ALU.mult), r=[sz_b[g]], w=[tt_b[1][b2]])
                kb.op(kb.act, lambda: A.activation(out=tj, in_=t2, func=AF.Square, accum_out=stg[0:n, g, 0:1]), r=[tt_b[1][b2]], w=[tt_b[3][b2], stg_b[g]])
                self.rstd_ops(stg[:, g, :], stg_b[g], n, 0, 1.0 / 256)
                kb.op(kb.dve, lambda: V.tensor_scalar(gn[0:n, gsl], t2, stg[0:n, g, 2:3], None, ALU.mult), r=[tt_b[1][b2], stg_b[g]],
                      w=[gn_b] if g == 0 else [], wp=[gn_b] if g else [])
                kb.op(kb.pe, lambda: T.matmul(miscb[:, 256:512], B_tm[0:n, g * 128:(g + 1) * 128], xdtd[b2][0:n, :], start=True, stop=True),
                      r=[Btm_b, xdtd_b[b2]], w=[pst_b])
                kb.op(kb.pool, lambda: G.tensor_tensor(x3(stmp[b2][:, :]), x3(S32[:, gsl]), bc(sm[:, CD, hsl], [128, 4, 64], 2), ALU.mult),
                      r=[S32_b[g], sm_b[CD]], w=[stmp_b[b2]])
                kb.op(kb.dve, lambda: V.tensor_tensor(S32[:, gsl], stmp[b2][:, :], miscb[:, 256:512], ALU.add), r=[stmp_b[b2], pst_b], w=[S32_b[g]])
                kb.op(kb.act, lambda: A.copy(Sbf[:, gsl], S32[:, gsl]), r=[S32_b[g]], w=[Sbf_b[g]])
            for cc in range(16):
                kb.op(kb.pe, lambda: T.transpose(tpg[:, cc, 0:n], gn[0:n, cc * 128:(cc + 1) * 128], self.ident[0:n, 0:n]),
                      r=[gn_b, self.cb_], w=[tp_b] if cc == 0 else [], wp=[tp_b] if cc else [], inc=(cc == 15))
            kb.op(kb.dve, lambda: V.tensor_tensor(gT[:, :, 0:n], tpg[:, :, 0:n], bc(sng[:, :], [128, 16, n], 2), ALU.mult), r=[tp_b, par_b], w=[gT_b])
            for hh in range(2):
                po = dTb[0:n, :] if hh == 0 else yb[0:n, :]
                pbufs = [dT_b] if hh == 0 else [ydg_b]
                for cc in range(16):
                    kb.op(kb.pe, lambda: T.matmul(po, gT[:, cc, 0:n], w_out[:, cc, hh * 512:(hh + 1) * 512], start=(cc == 0), stop=(cc == 15)),
                          r=[gT_b, wout_b], w=pbufs if cc == 0 else [], wp=pbufs if cc else [], inc=(cc == 15))
                kb.op(kb.dve, lambda: V.tensor_tensor(hs[cur][0:n, hh * 512:(hh + 1) * 512], po, hs[cur][0:n, hh * 512:(hh + 1) * 512], ALU.add),
                      r=pbufs + [hs_b[cur]], wp=[hs_b[cur]])
            dap = self.h_dst(dst, t0, n)
            if dap is not None:
                kb.dma("sp", dap, hs[cur][0:n, :], r=[hs_b[cur]])

    def rstd_ops(self, st, st_b, n, c0, inv_n):
        kb, nc = self.kb, self.nc
        kb.op(kb.act, lambda: nc.scalar.activation(out=st[0:n, c0 + 1:c0 + 2], in_=st[0:n, c0:c0 + 1], func=AF.Ln, bias=EPS, scale=inv_n),
              r=[st_b], wp=[st_b])
        kb.op(kb.act, lambda: nc.scalar.activation(out=st[0:n, c0 + 2:c0 + 3], in_=st[0:n, c0 + 1:c0 + 2], func=AF.Exp, scale=-0.5),
              r=[st_b], wp=[st_b])

    def phase_mla(self, es, j, li, src, dst):
        kb, nc = self.kb, self.nc
        V, A, G, T = nc.vector, nc.scalar, nc.gpsimd, nc.tensor
        oT = self.S(es, "oT", [128, 8, NT], BF16)
        oT_b = Buf()
        w_o = self.S(es, "w_o", [128, 8, D], BF16)
        w_o_b = Buf()
        gq = self.S(es, "gq", [128, 96], F32)
        gk = self.S(es, "gk", [128, 96], F32)
        par_b = Buf()
        kb.dma("sp", gq[:], self.gq_d[j], wp=[par_b])
        kb.dma("sp", gk[:], self.gk_d[j], wp=[par_b])
        with contextlib.ExitStack() as s1:
            w_in = self.S(s1, "a_win", [128, 8, 672], BF16)
            w_qb = self.S(s1, "a_wqb", [128, 3, 1536], BF16)
            w_kvb = self.S(s1, "a_wkvb", [128, 2, 2048], BF16)
            wb = Buf()
            kb.dma("pool", w_in[:], self.w_mla_in[j].rearrange("(c p) f -> p c f", p=128), wp=[wb])
            kb.dma("pool", w_qb[:], self.w_mla_qb[j].rearrange("(c p) f -> p c f", p=128), wp=[wb])
            kb.dma("pool", w_kvb[:], self.w_mla_kvb[j].rearrange("(c p) f -> p c f", p=128), wp=[wb])
            kb.dma("pool", w_o[:], self.w_mla_out[j].rearrange("(c p) f -> p c f", p=128), wp=[w_o_b])
            qag = self.S(s1, "a_qag", [128, 3], F32)
            kvag = self.S(s1, "a_kvag", [128, 2], F32)
            kb.dma("sp", qag[:], self.qag_d[j], wp=[par_b])
            kb.dma("sp", kvag[:], self.kvag_d[j], wp=[par_b])
            tp_ps = self.P(s1, "a_tp", [128, 512], F32)
            latA = self.P(s1, "a_latA", [128, 512], F32)
            latB = self.P(s1, "a_latB", [128, 512], F32)
            big = self.P(s1, "a_big", [128, 2048], F32)
            tph_ps = self.P(s1, "a_tph", [128, 512], F32)
            latA_b, latB_b, tph_b = PB(), PB(), PB()
            big_b = [PB() for _ in range(4)]
            nsc = self.norm_scratch(s1, "a", tp_ps)
            tp5 = tp_ps[:].bitcast(BF16)[:, 0:640].rearrange("p (c t) -> p c t", c=5)
            tp_b = nsc[7]
            tph = tph_ps[:].bitcast(BF16)[:, 0:1024].rearrange("p (c t) -> p c t", c=8)
            qTv = self.qT_d.rearrange("h p t -> p h t")
            kTv = self.kT_d.rearrange("h p t -> p h t")

            class NS:
                pass

            def mkbufs(ci):
                b = NS()
                nm = lambda x: "a%d_%s" % (ci, x)
                b.hs = self.S(s1, nm("h"), [128, D], F32); b.hs_b = Buf()
                b.cs = self.S(s1, nm("cs"), [128, 2, 16], F32); b.cs_b = Buf()
                b.xnT = self.S(s1, nm("xnT"), [128, 8, 128], BF16); b.xnT_b = Buf()
                b.st = self.S(s1, nm("st"), [128, 8], F32); b.st_b = Buf()
                b.qln = self.S(s1, nm("qln"), [128, 384], BF16)
                b.kvln = self.S(s1, nm("kvln"), [128, 256], BF16)
                b.kpe = self.S(s1, nm("kpe"), [128, 32], F32); b.ln_b = Buf()
                b.sqj = self.S(s1, nm("sqj"), [128, 2048], F32); b.sqj_b = Buf()
                b.qlT = self.S(s1, nm("qlT"), [128, 3, 128], BF16)
                b.kvlT = self.S(s1, nm("kvlT"), [128, 2, 128], BF16); b.lT_b = Buf()
                b.raw = self.S(s1, nm("raw"), [128, 2048], F32); b.raw_b = Buf()
                b.s16 = self.S(s1, nm("s16"), [128, 3, 16], F32); b.s16_b = Buf()
                b.rt = self.S(s1, nm("rt"), [128, 4, 256], F32); b.rt_b = [Buf() for _ in range(4)]
                b.kpg = self.S(s1, nm("kpg"), [128, 2, 32], F32); b.kpg_b = Buf()
                b.qbf = self.S(s1, nm("qbf"), [128, 1536], BF16); b.qbf_b = Buf()
                b.stg = [self.S(s1, nm("stg%d" % i), [96, 16, 128], BF16) for i in range(2)]; b.stg_b = [Buf(), Buf()]
                b.vst = self.S(s1, nm("vst"), [128, 8, 192], BF16); b.vst_b = Buf()
                kb.op(kb.pool, lambda: G.memset(b.vst[:], 1.0), w=[b.vst_b])
                return b

            CH = [mkbufs(0), mkbufs(1)]

            def chain(ti):
                t0, n = TILES[ti]
                b = CH[ti % 2]
                xnT, st, st_b, qln, kvln, kpe, ln_b = b.xnT, b.st, b.st_b, b.qln, b.kvln, b.kpe, b.ln_b
                sqj, sqj_b, qlT, kvlT, lT_b, raw, raw_b = b.sqj, b.sqj_b, b.qlT, b.kvlT, b.lT_b, b.raw, b.raw_b
                s16, s16_b, rt, rt_b, kpg, kpg_b, qbf, qbf_b = b.s16, b.s16_b, b.rt, b.rt_b, b.kpg, b.kpg_b, b.qbf, b.qbf_b
                raw3 = raw[0:n, 0:1536].rearrange("p (h f) -> p h f", h=16)
                sq3 = sqj[0:n, 0:1536].rearrange("p (h f) -> p h f", h=16)
                qb3 = qbf[0:n, :].rearrange("p (h f) -> p h f", h=16)
                kv4 = raw[0:n, :].rearrange("p (h f) -> p h f", h=16)
                sqk = sqj[0:n, 0:1024].rearrange("p (h f) -> p h f", h=16)
                kv5 = raw[0:n, :].rearrange("p (c e f) -> p c e f", c=8, e=2)

                def head_T(sg, dstv):
                    for half in range(2):
                        for hh in range(8):
                            h = half * 8 + hh
                            kb.op(kb.pe, lambda: T.transpose(tph[0:96, hh, 0:n], qbf[0:n, h * 96:(h + 1) * 96], self.ident[0:n, 0:n]),
                                  r=[qbf_b, self.cb_], w=[tph_b] if hh == 0 else [], wp=[tph_b] if hh else [], inc=(hh == 7))
                        kb.op(kb.act, lambda: A.copy(b.stg[sg][0:96, half * 8:(half + 1) * 8, 0:n], tph[0:96, :, 0:n]),
                              r=[tph_b], w=[b.stg_b[sg]] if half == 0 else [], wp=[b.stg_b[sg]] if half else [])
                    kb.dma("sp", dstv[:, :, t0:t0 + n], b.stg[sg][0:96, :, 0:n], r=[b.stg_b[sg]])

                def rope(t1, t2, cosb, sinb, o1, o2, three_d, wb_, rd_bufs):
                    if three_d:
                        a_, b_, c_, d_ = [rt[0:n, i, :].rearrange("p (h f) -> p h f", h=16) for i in range(4)]
                    else:
                        a_, b_, c_, d_ = [rt[0:n, i, 0:16] for i in range(4)]
                    kb.op(kb.dve, lambda: V.tensor_tensor(a_, t1, cosb, ALU.mult), r=rd_bufs, w=[rt_b[0]])
                    kb.op(kb.pool, lambda: G.tensor_tensor(b_, t2, sinb, ALU.mult), r=rd_bufs, w=[rt_b[1]])
                    kb.op(kb.dve, lambda: V.tensor_tensor(c_, t1, sinb, ALU.mult), r=rd_bufs, w=[rt_b[2]])
                    kb.op(kb.pool, lambda: G.tensor_tensor(d_, t2, cosb, ALU.mult), r=rd_bufs, w=[rt_b[3]])
                    kb.op(kb.dve, lambda: V.tensor_tensor(o1, a_, b_, ALU.subtract), r=[rt_b[0], rt_b[1]], wp=[wb_])
                    kb.op(kb.dve, lambda: V.tensor_tensor(o2, c_, d_, ALU.add), r=[rt_b[2], rt_b[3]], wp=[wb_])

                def s1():
                    kb.dma("sp", b.hs[0:n, :], self.h_src(src, t0, n), w=[b.hs_b])
                    kb.dma("sp", b.cs[0:n, 0, :], self.cos_d[t0:t0 + n, :], w=[b.cs_b])
                    kb.dma("sp", b.cs[0:n, 1, :], self.sin_d[t0:t0 + n, :], wp=[b.cs_b])
                    self.norm_T(b.hs_b, b.hs[0:n, :], n, self.lnT[:, li, :], xnT[:, :, 0:n], b.xnT_b, nsc)

                def s2():
                    for c in range(8):
                        kb.op(kb.pe, lambda: T.matmul(latA[0:n, 0:384], xnT[:, c, 0:n], w_in[:, c, 0:384], start=(c == 0), stop=(c == 7)),
                              r=[b.xnT_b, wb], w=[latA_b] if c == 0 else [], wp=[latA_b] if c else [], inc=(c == 7))
                    for c in range(8):
                        kb.op(kb.pe, lambda: T.matmul(latB[0:n, 0:288], xnT[:, c, 0:n], w_in[:, c, 384:672], start=(c == 0), stop=(c == 7)),
                              r=[b.xnT_b, wb], w=[latB_b] if c == 0 else [], wp=[latB_b] if c else [], inc=(c == 7))
                    kb.op(kb.act, lambda: A.activation(out=sqj[0:n, 0:384], in_=latA[0:n, 0:384], func=AF.Square, accum_out=st[0:n, 0:1]),
                          r=[latA_b], w=[sqj_b, st_b])
                    kb.op(kb.act, lambda: A.activation(out=sqj[0:n, 512:768], in_=latB[0:n, 0:256], func=AF.Square, accum_out=st[0:n, 3:4]),
                          r=[latB_b], wp=[sqj_b, st_b])
                    self.rstd_ops(st, st_b, n, 0, 1.0 / 384)
                    self.rstd_ops(st, st_b, n, 3, 1.0 / 256)
                    kb.op(kb.dve, lambda: V.tensor_scalar(qln[0:n, :], latA[0:n, 0:384], st[0:n, 2:3], None, ALU.mult), r=[latA_b, st_b], w=[ln_b])
                    kb.op(kb.dve, lambda: V.tensor_scalar(kvln[0:n, :], latB[0:n, 0:256], st[0:n, 5:6], None, ALU.mult), r=[latB_b, st_b], wp=[ln_b])
                    kb.op(kb.act, lambda: A.copy(kpe[0:n, :], latB[0:n, 256:288]), r=[latB_b], wp=[ln_b])

                def s3():
                    for c in range(5):
                        srcap = qln[0:n, c * 128:(c + 1) * 128] if c < 3 else kvln[0:n, (c - 3) * 128:(c - 2) * 128]
                        kb.op(kb.pe, lambda: T.transpose(tp5[:, c, 0:n], srcap, self.ident[0:n, 0:n]),
                              r=[ln_b, self.cb_], w=[tp_b] if c == 0 else [], wp=[tp_b] if c else [], inc=(c == 4))
                    kb.op(kb.dve, lambda: V.tensor_tensor(qlT[:, :, 0:n], tp5[:, 0:3, 0:n], bc(qag[:, :], [128, 3, n], 2), ALU.mult),
                          r=[tp_b, par_b], w=[lT_b])
                    kb.op(kb.dve, lambda: V.tensor_tensor(kvlT[:, :, 0:n], tp5[:, 3:5, 0:n], bc(kvag[:, :], [128, 2, n], 2), ALU.mult),
                          r=[tp_b, par_b], wp=[lT_b])

                def s4():
                    for ct in range(3):
                        for kc in range(3):
                            kb.op(kb.pe, lambda: T.matmul(big[0:n, ct * 512:(ct + 1) * 512], qlT[:, kc, 0:n], w_qb[:, kc, ct * 512:(ct + 1) * 512],
                                                          start=(kc == 0), stop=(kc == 2)),
                                  r=[lT_b, wb], w=[big_b[ct]] if kc == 0 else [], wp=[big_b[ct]] if kc else [], inc=(kc == 2))
                    for ct in range(3):
                        E_ = kb.act if ct != 1 else kb.dve
                        fn = (lambda: A.copy(raw[0:n, ct * 512:(ct + 1) * 512], big[0:n, ct * 512:(ct + 1) * 512])) if ct != 1 else \
                             (lambda: V.tensor_copy(raw[0:n, ct * 512:(ct + 1) * 512], big[0:n, ct * 512:(ct + 1) * 512]))
                        kb.op(E_, fn, r=[big_b[ct]], w=[raw_b] if ct == 0 else [], wp=[raw_b] if ct else [])

                def s5():
                    kb.op(kb.pool, lambda: G.tensor_tensor(sqj[0:n, 0:1536], raw[0:n, 0:1536], raw[0:n, 0:1536], ALU.mult), r=[raw_b], w=[sqj_b])
                    kb.op(kb.dve, lambda: V.tensor_reduce(s16[0:n, 0, :], sq3, AX.X, ALU.add), r=[sqj_b], w=[s16_b])
                    kb.op(kb.act, lambda: A.activation(out=s16[0:n, 1, :], in_=s16[0:n, 0, :], func=AF.Ln, bias=EPS, scale=1.0 / 96), r=[s16_b], wp=[s16_b])
                    kb.op(kb.act, lambda: A.activation(out=s16[0:n, 2, :], in_=s16[0:n, 1, :], func=AF.Exp, scale=-0.5), r=[s16_b], wp=[s16_b])
                    kb.op(kb.dve, lambda: V.tensor_tensor(raw3, raw3, bc(s16[0:n, 2, :], [n, 16, 96], 2), ALU.mult), r=[s16_b], w=[raw_b])
                    kb.op(kb.pool, lambda: G.tensor_tensor(raw3, raw3, bc(gq[0:n, :], [n, 16, 96], 1), ALU.mult), r=[par_b], w=[raw_b])
                    cosb = bc(b.cs[0:n, 0, :], [n, 16, 16], 1)
                    sinb = bc(b.cs[0:n, 1, :], [n, 16, 16], 1)
                    kb.op(kb.act, lambda: A.copy(qb3[:, :, 0:64], raw3[:, :, 0:64]), r=[raw_b], w=[qbf_b])
                    rope(raw3[:, :, 64:80], raw3[:, :, 80:96], cosb, sinb, qb3[:, :, 64:80], qb3[:, :, 80:96], True, qbf_b, [raw_b, b.cs_b])

                def s6():
                    head_T(0, qTv)

                def s7():
                    for ct in range(4):
                        for kc in range(2):
                            kb.op(kb.pe, lambda: T.matmul(big[0:n, ct * 512:(ct + 1) * 512], kvlT[:, kc, 0:n], w_kvb[:, kc, ct * 512:(ct + 1) * 512],
                                                          start=(kc == 0), stop=(kc == 1)),
                                  r=[lT_b, wb], w=[big_b[ct]] if kc == 0 else [], wp=[big_b[ct]] if kc else [], inc=(kc == 1))
                    for ct in range(4):
                        E_ = kb.act if ct % 2 == 0 else kb.dve
                        fn = (lambda: A.copy(raw[0:n, ct * 512:(ct + 1) * 512], big[0:n, ct * 512:(ct + 1) * 512])) if ct % 2 == 0 else \
                             (lambda: V.tensor_copy(raw[0:n, ct * 512:(ct + 1) * 512], big[0:n, ct * 512:(ct + 1) * 512]))
                        kb.op(E_, fn, r=[big_b[ct]], w=[raw_b] if ct == 0 else [], wp=[raw_b] if ct else [])

                def s8():
                    kb.op(kb.pool, lambda: G.tensor_tensor(sqk, kv4[:, :, 0:64], kv4[:, :, 0:64], ALU.mult), r=[raw_b], w=[sqj_b])
                    kb.op(kb.dve, lambda: V.tensor_reduce(s16[0:n, 0, :], sqk, AX.X, ALU.add), r=[sqj_b], w=[s16_b])
                    kb.op(kb.act, lambda: A.activation(out=kpg[0:n, 1, :], in_=kpe[0:n, :], func=AF.Square, accum_out=st[0:n, 6:7]),
                          r=[ln_b], w=[kpg_b], wp=[st_b])
                    kb.op(kb.dve, lambda: V.tensor_scalar(s16[0:n, 0, :], s16[0:n, 0, :], st[0:n, 6:7], None, ALU.add), r=[st_b, s16_b], wp=[s16_b])
                    kb.op(kb.act, lambda: A.activation(out=s16[0:n, 1, :], in_=s16[0:n, 0, :], func=AF.Ln, bias=EPS, scale=1.0 / 96), r=[s16_b], wp=[s16_b])
                    kb.op(kb.act, lambda: A.activation(out=s16[0:n, 2, :], in_=s16[0:n, 1, :], func=AF.Exp, scale=-0.5), r=[s16_b], wp=[s16_b])
                    kb.op(kb.act, lambda: A.copy(b.vst[0:n, :, 0:64], kv5[:, :, 0, 64:128]), r=[raw_b, b.vst_b], wp=[b.vst_b])
                    kb.op(kb.dve, lambda: V.tensor_copy(b.vst[0:n, :, 128:192], kv5[:, :, 1, 64:128]), r=[raw_b, b.vst_b], wp=[b.vst_b])
                    kb.dma("sp", self.va_d[t0:t0 + n, :, :], b.vst[0:n, :, :], r=[b.vst_b])
                    kb.op(kb.dve, lambda: V.tensor_tensor(kv4[:, :, 0:64], kv4[:, :, 0:64], bc(s16[0:n, 2, :], [n, 16, 64], 2), ALU.mult),
                          r=[s16_b], w=[raw_b])
                    kb.op(kb.pool, lambda: G.tensor_tensor(qb3[:, :, 0:64], kv4[:, :, 0:64], bc(gk[0:n, 0:64], [n, 16, 64], 1), ALU.mult),
                          r=[raw_b, par_b], w=[qbf_b])
                    kb.op(kb.dve, lambda: V.tensor_tensor(kpg[0:n, 0, :], kpe[0:n, :], gk[0:n, 64:96], ALU.mult), r=[ln_b, par_b], w=[kpg_b])
                    rope(kpg[0:n, 0, 0:16], kpg[0:n, 0, 16:32], b.cs[0:n, 0, :], b.cs[0:n, 1, :], kpg[0:n, 1, 0:16], kpg[0:n, 1, 16:32],
                         False, kpg_b, [kpg_b, b.cs_b])
                    kb.op(kb.dve, lambda: V.tensor_tensor(qb3[:, :, 64:96], bc(kpg[0:n, 1, :], [n, 16, 32], 1), bc(s16[0:n, 2, :], [n, 16, 32], 2), ALU.mult),
                          r=[kpg_b, s16_b], wp=[qbf_b])

                def s9():
                    head_T(1, kTv)

                return [s1, s2, s3, s4, s5, s6, s7, s8, s9]

            for k in range(0, len(TILES), 2):
                ca = chain(k)
                cb2 = chain(k + 1) if k + 1 < len(TILES) else []
                for i in range(len(ca)):
                    ca[i]()
                    if i < len(cb2):
                        cb2[i]()
            kb.barrier()
        with contextlib.ExitStack() as s2:
            qh = [self.S(s2, "b_q%d" % i, [96, NT], BF16) for i in range(2)]
            kh = [self.S(s2, "b_k%d" % i, [96, NT], BF16) for i in range(2)]
            qk_b = [Buf(), Buf()]
            va = [self.S(s2, "b_va%d" % i, [128, 33, 192], BF16) for i in range(2)]
            va_b = [Buf(), Buf()]
            pT = [self.S(s2, "b_pT%d" % i, [128, 512], BF16) for i in range(4)]
            pT_b = [Buf() for _ in range(4)]
            rden = [self.S(s2, "b_rd%d" % i, [128, 512], F32) for i in range(2)]
            rdsh = [self.S(s2, "b_rs%d" % i, [128, 512], F32) for i in range(2)]
            rd_b = [Buf(), Buf()]
            rs_b = [Buf(), Buf()]
            bnd = self.S(s2, "b_bnd", [128, 8], F32)
            bnd_b = Buf()
            ps = [self.P(s2, "b_ps%d" % i, [128, 512], F32) for i in range(4)]
            ps_b = [PB() for _ in range(4)]
            po = [self.P(s2, "b_po%d" % i, [128, 512], F32) for i in range(2)]
            po_b = [PB(), PB()]
            kb.op(kb.dve, lambda: V.tensor_reduce(bnd[:, 0:1], gq[:, :], AX.X, ALU.max), r=[par_b], w=[bnd_b])
            kb.op(kb.dve, lambda: V.tensor_reduce(bnd[:, 1:2], gq[:, :], AX.X, ALU.min), r=[par_b], wp=[bnd_b])
            kb.op(kb.dve, lambda: V.tensor_reduce(bnd[:, 2:3], gk[:, :], AX.X, ALU.max), r=[par_b], wp=[bnd_b])
            kb.op(kb.dve, lambda: V.tensor_reduce(bnd[:, 3:4], gk[:, :], AX.X, ALU.min), r=[par_b], wp=[bnd_b])
            kb.op(kb.dve, lambda: V.scalar_tensor_tensor(bnd[:, 4:5], bnd[:, 1:2], -1.0, bnd[:, 0:1], ALU.mult, ALU.max), r=[bnd_b], wp=[bnd_b])
            kb.op(kb.dve, lambda: V.scalar_tensor_tensor(bnd[:, 5:6], bnd[:, 3:4], -1.0, bnd[:, 2:3], ALU.mult, ALU.max), r=[bnd_b], wp=[bnd_b])
            kb.op(kb.dve, lambda: V.scalar_tensor_tensor(bnd[:, 6:7], bnd[:, 4:5], -float(np.sqrt(96.0)), bnd[:, 5:6], ALU.mult, ALU.mult),
                  r=[bnd_b], wp=[bnd_b])
            negB = bnd[:, 6:7]
            scale = float(96.0 ** -0.5)

            def load_head(h):
                b = h % 2
                kb.dma("sp", qh[b][:, :], self.qT_d[h], w=[qk_b[b]])
                kb.dma("sp", kh[b][:, :], self.kT_d[h], wp=[qk_b[b]])

            def load_pair(c):
                b = c % 2
                kb.dma("sp", va[b][0:16, 0, :], self.va_d[0:16, c, :], w=[va_b[b]])
                vv = self.va_d[16:NT, c, :].rearrange("(j p) w -> p j w", p=128)
                for jj in range(0, 32, 8):
                    kb.dma("sp", va[b][:, 1 + jj:9 + jj, :], vv[:, jj:jj + 8, :], wp=[va_b[b]])

            load_pair(0)
            load_head(0)
            items = []
            for h in range(MLA_H):
                for qi in range(9):
                    if qi == 0:
                        q0, nq = 0, 16
                        kts = [(0, 0, 16, 0, True)]
                    else:
                        q0, nq = 16 + 512 * (qi - 1), 512
                        kts = [(0, 0, 16, 0, False)] + [(kt, 16 + 128 * (kt - 1), 128, 0, False) for kt in range(1, 4 * (qi - 1) + 1)]
                        kts += [(4 * (qi - 1) + 1 + i, 16 + 128 * (4 * (qi - 1) + i), 128, 128 * i, True) for i in range(4)]
                    for idx, kt in enumerate(kts):
                        items.append((h, qi, q0, nq, idx, len(kts)) + kt)
            LA = 2
            NI = len(items)
            for i in range(NI + LA):
                if i < NI:
                    (h, qi, q0, nq, idx, nk_t, kt, k0, nk, qoff, diag) = items[i]
                    if qi == 0 and idx == 0 and h + 1 < MLA_H:
                        load_head(h + 1)
                        if h % 2 == 1:
                            load_pair(h // 2 + 1)
                    hb_ = h % 2
                    nqq = nq - qoff
                    p = i % 4
                    kb.op(kb.pe, lambda: T.matmul(ps[p][0:nk, 0:nqq], kh[hb_][:, k0:k0 + nk], qh[hb_][:, q0 + qoff:q0 + nq], start=True, stop=True),
                          r=[qk_b[hb_]], w=[ps_b[p]])
                    kb.op(kb.act, lambda: A.activation(out=pT[p][0:nk, 0:nqq], in_=ps[p][0:nk, 0:nqq], func=AF.Exp, bias=negB[0:nk, :], scale=scale),
                          r=[ps_b[p], bnd_b], w=[pT_b[p]])
                    if diag:
                        kb.op(kb.dve, lambda: V.tensor_tensor(pT[p][0:nk, 0:nk], pT[p][0:nk, 0:nk], self.tri16[0:nk, 0:nk], ALU.mult),
                              r=[self.cb_], w=[pT_b[p]])
                ii = i - LA
                if ii >= 0:
                    (h, qi, q0, nq, idx, nk_t, kt, k0, nk, qoff, diag) = items[ii]
                    c, e = h // 2, h % 2
                    vb_ = c % 2
                    dlo, dhi = (0, 64) if e == 0 else (64, 128)
                    nlo, nhi = (64, 128) if e == 0 else (0, 64)
                    nqq = nq - qoff
                    p = ii % 4
                    pp = (h * 9 + qi) % 2
                    first, last = idx == 0, idx == nk_t - 1
                    kb.op(kb.pe, lambda: T.matmul(po[pp][:, qoff:nq], va[vb_][0:nk, kt, e * 64:e * 64 + 128], pT[p][0:nk, 0:nqq], start=first, stop=last),
                          r=[pT_b[p], va_b[vb_]], w=[po_b[pp]] if first else [], wp=[] if first else [po_b[pp]], inc=last)
                    if last:
                        kb.op(kb.dve, lambda: V.reciprocal(rden[pp][nlo:nhi, 0:nq], po[pp][nlo:nhi, 0:nq]), r=[po_b[pp]], w=[rd_b[pp]])
                        kb.op(kb.act, lambda: A.copy(rdsh[pp][dlo:dhi, 0:nq], rden[pp][nlo:nhi, 0:nq]), r=[rd_b[pp]], w=[rs_b[pp]])
                        kb.op(kb.dve, lambda: V.tensor_tensor(oT[dlo:dhi, c, q0:q0 + nq], po[pp][dlo:dhi, 0:nq], rdsh[pp][dlo:dhi, 0:nq], ALU.mult),
                              r=[po_b[pp], rs_b[pp]], wp=[oT_b])
            kb.barrier()
        with contextlib.ExitStack() as s3:
            hs = [self.S(s3, "c_h%d" % i, [128, D], F32) for i in range(3)]
            hs_b = [Buf() for _ in range(3)]
            po = [self.P(s3, "c_po%d" % i, [128, 512], F32) for i in range(4)]
            po_b = [PB() for _ in range(4)]
            ip = 0
            for ti, (t0, n) in enumerate(TILES):
                s = ti % 3
                kb.dma("sp", hs[s][0:n, :], self.h_src(src, t0, n), w=[hs_b[s]])
                for hh in range(2):
                    p = ip % 4
                    ip += 1
                    for c in range(8):
                        kb.op(kb.pe, lambda: T.matmul(po[p][0:n, :], oT[:, c, t0:t0 + n], w_o[:, c, hh * 512:(hh + 1) * 512], start=(c == 0), stop=(c == 7)),
                              r=[oT_b, w_o_b], w=[po_b[p]] if c == 0 else [], wp=[po_b[p]] if c else [], inc=(c == 7))
                    kb.op(kb.dve, lambda: V.tensor_tensor(hs[s][0:n, hh * 512:(hh + 1) * 512], po[p][0:n, :], hs[s][0:n, hh * 512:(hh + 1) * 512], ALU.add),
                          r=[po_b[p], hs_b[s]], wp=[hs_b[s]])
                dap = self.h_dst(dst, t0, n)
                if dap is not None:
                    kb.dma("sp", dap, hs[s][0:n, :], r=[hs_b[s]])


def host_inputs(inp):
    f = lambda a: np.ascontiguousarray(np.asarray(a, dtype=np.float32))
    rep = lambda a: np.ascontiguousarray(np.broadcast_to(np.asarray(a, np.float32)[:, None, :], (a.shape[0], 128, a.shape[1])))
    colT = lambda a, c: np.ascontiguousarray(np.asarray(a, np.float32).reshape(a.shape[0], c, 128).transpose(0, 2, 1))
    ln = np.concatenate([np.asarray(inp["ln_mix"], np.float32), np.asarray(inp["ln_mlp"], np.float32)], 0)
    lnT = np.ascontiguousarray(ln.reshape(8, 8, 128).transpose(2, 0, 1))
    cw = np.asarray(inp["ssd_conv_w"], np.float32)
    cwT = np.ascontiguousarray(cw.reshape(2, 4, 32, 128).transpose(0, 3, 2, 1))
    k = np.arange(128)
    tri = (k[:, None] <= k[None, :]).astype(np.float32)
    inv = 1.0 / (10000.0 ** (np.arange(0, 32, 2, dtype=np.float32) / 32.0))
    ang = np.arange(NT, dtype=np.float32)[:, None] * inv[None, :].astype(np.float32)
    common = {
        "meta": f(inp["meta_tokens"]),
        "ssd_w_in": f(inp["ssd_w_in"]), "ssd_w_out": f(inp["ssd_w_out"]),
        "mla_w_in": f(inp["mla_w_in"]), "mla_w_q_b": f(inp["mla_w_q_b"]), "mla_w_kv_b": f(inp["mla_w_kv_b"]),
        "mla_w_out": f(inp["mla_w_out"]), "mlp_w_up": f(inp["mlp_w_up"]), "mlp_w_down": f(inp["mlp_w_down"]),
        "lnT": lnT, "cw": cwT, "cb": colT(inp["ssd_conv_b"], 32),
        "dtb_rep": rep(inp["ssd_dt_bias"]), "alog_rep": rep(inp["ssd_a_log"]), "dskip_rep": rep(inp["ssd_d"]),
        "ssd_normT": colT(inp["ssd_norm"], 16), "q_a_T": colT(inp["mla_q_a_norm"], 3), "kv_a_T": colT(inp["mla_kv_a_norm"], 2),
        "gq_rep": rep(inp["mla_q_norm"]), "gk_rep": rep(inp["mla_k_norm"]),
        "ident": np.eye(128, dtype=np.float32), "tri": tri, "ltri": np.ascontiguousarray(1.0 - tri),
        "ones": np.ones((128, 128), np.float32),
        "cos": np.cos(ang).astype(np.float32), "sin": np.sin(ang).astype(np.float32),
    }
    return common


FULL_PHASES = [
    ("ssd", 0, 0, "x", "h"), ("mlp", 0, "h", "h"),
    ("mla", 0, 1, "h", "h"), ("mlp", 1, "h", "h"),
    ("ssd", 1, 2, "h", "h"), ("mlp", 2, "h", "h"),
    ("mla", 1, 3, "h", "h"), ("mlp", 3, "h", "y"),
]


def run(inputs, phases, cores=8):
    common = host_inputs(inputs)
    x = np.asarray(inputs["x"], np.float32)
    prog = Prog(phases)
    in_maps = []
    for c in range(cores):
        m = dict(common)
        m["x"] = np.ascontiguousarray(x[c])
        in_maps.append(m)
    res = run_bass_kernel_spmd(prog.nc, in_maps, core_ids=list(range(cores)))
    return np.stack([np.asarray(r["y"]) for r in res.results], 0)


def kernel(**inputs):
    return run(inputs, FULL_PHASES, 8).astype(np.float32)
```

```python
import contextlib
import numpy as np
import concourse.bass as bass
import concourse.mybir as mybir
from concourse.bass_utils import run_bass_kernel_spmd

F32, BF16 = mybir.dt.float32, mybir.dt.bfloat16
AF = mybir.ActivationFunctionType
ALU = mybir.AluOpType
AX = mybir.AxisListType

NT, NM, D, SEQ = 4112, 16, 1024, 4096
TILES = [(0, 16)] + [(16 + 128 * j, 128) for j in range(32)]
EPS = 1e-6
DFF = 4096
SSD_IN = 6176
NH_S = 32
MLA_H = 16
QK = 96


class Buf:
    __slots__ = ("w", "r", "name", "ps")

    def __init__(self, name="", ps=False):
        self.w = {}
        self.r = {}
        self.name = name
        self.ps = ps


def PB():
    return Buf(ps=True)


class Eng:
    def __init__(self, name, e, sem):
        self.name, self.e, self.sem, self.cnt, self.seen = name, e, sem, 0, {}


class KB:
    def __init__(self, nc, es):
        self.nc = nc
        mk = lambda n: es.enter_context(nc.semaphore(n))
        self.pe = Eng("pe", nc.tensor, mk("s_pe"))
        self.act = Eng("act", nc.scalar, mk("s_act"))
        self.dve = Eng("dve", nc.vector, mk("s_dve"))
        self.pool = Eng("pool", nc.gpsimd, mk("s_pool"))
        self.sp = Eng("sp", nc.sync, mk("s_sp"))
        self.engs = [self.pe, self.act, self.dve, self.pool, self.sp]
        self.dsem = {"sp": [[mk("d_sp%d" % i), 0] for i in range(24)],
                     "pool": [[mk("d_pl%d" % i), 0] for i in range(8)]}
        self.drr = {"sp": 0, "pool": 0}
        self.nins = 0

    def _wait(self, E, toks):
        for key, (sem, val) in toks.items():
            if E is self.pe and key == "pe":
                continue
            if E.seen.get(key, 0) >= val:
                continue
            E.e.wait_ge(sem, val)
            E.seen[key] = val

    @staticmethod
    def _add(need, d):
        for k, sv in d.items():
            if k not in need or need[k][1] < sv[1]:
                need[k] = sv

    def _deps(self, r, w, wp, ekey=None):
        need = {}
        for b in r:
            self._add(need, b.w)
            if b.ps:
                self._add(need, {k: v for k, v in b.r.items() if k != ekey})
        for b in w:
            self._add(need, b.w)
            self._add(need, b.r)
        for b in wp:
            self._add(need, b.r)
            if b.ps:
                self._add(need, b.w)
        return need

    def _reg(self, key, tok, r, w, wp):
        for b in r:
            if key not in b.r or b.r[key][1] < tok[1]:
                b.r[key] = tok
        for b in w:
            b.w = {key: tok}
            b.r = {}
        for b in wp:
            if key not in b.w or b.w[key][1] < tok[1]:
                b.w[key] = tok

    def op(self, E, fn, r=(), w=(), wp=(), inc=True):
        self._wait(E, self._deps(r, w, wp, E.name))
        ins = fn()
        self.nins += 1
        if inc:
            E.cnt += 1
            ins.then_inc(E.sem, 1)
            tok = (E.sem, E.cnt)
        else:
            tok = (E.sem, E.cnt + 1)
        self._reg(E.name, tok, r, w, wp)

    def dma(self, Q, out, in_, r=(), w=(), wp=()):
        E = self.sp if Q == "sp" else self.pool
        self._wait(E, self._deps(r, w, wp))
        lst = self.dsem[Q]
        i = self.drr[Q]
        self.drr[Q] = (i + 1) % len(lst)
        sem, cnt = lst[i]
        key = (Q, i)
        if cnt > 0 and E.seen.get(key, 0) < cnt:
            E.e.wait_ge(sem, cnt)
            E.seen[key] = cnt
        ins = E.e.dma_start(out=out, in_=in_)
        ins.then_inc(sem, 16)
        self.nins += 1
        lst[i][1] = cnt + 16
        self._reg(key, (sem, cnt + 16), r, w, wp)

    def barrier(self):
        toks = {}
        for E in self.engs:
            if E.cnt > 0:
                toks[E.name] = (E.sem, E.cnt)
        for Q, lst in self.dsem.items():
            for i, (sem, cnt) in enumerate(lst):
                if cnt > 0:
                    toks[(Q, i)] = (sem, cnt)
        for E in self.engs:
            self._wait(E, toks)


def bc(ap, shape, axis):
    return ap.unsqueeze(axis).to_broadcast(list(shape))


class Prog:
    def __init__(self, phases):
        self.phases = phases
        nc = bass.Bass("TRN2", target_bir_lowering=False)
        self.nc = nc
        di = lambda name, shape: nc.dram_tensor(name, list(shape), F32, kind="ExternalInput").ap()
        self.x = di("x", [SEQ, D])
        self.meta = di("meta", [NM, D])
        self.w_ssd_in = di("ssd_w_in", [2, D, SSD_IN])
        self.w_ssd_out = di("ssd_w_out", [2, 2048, D])
        self.w_mla_in = di("mla_w_in", [2, D, 672])
        self.w_mla_qb = di("mla_w_q_b", [2, 384, 1536])
        self.w_mla_kvb = di("mla_w_kv_b", [2, 256, 2048])
        self.w_mla_out = di("mla_w_out", [2, D, D])
        self.w_up = di("mlp_w_up", [4, D, DFF])
        self.w_dn = di("mlp_w_down", [4, DFF, D])
        self.lnT_d = di("lnT", [128, 8, 8])
        self.cw_d = di("cw", [2, 128, 32, 4])
        self.cb_d = di("cb", [2, 128, 32])
        self.dtb_d = di("dtb_rep", [2, 128, 32])
        self.alog_d = di("alog_rep", [2, 128, 32])
        self.dsk_d = di("dskip_rep", [2, 128, 32])
        self.sng_d = di("ssd_normT", [2, 128, 16])
        self.qag_d = di("q_a_T", [2, 128, 3])
        self.kvag_d = di("kv_a_T", [2, 128, 2])
        self.gq_d = di("gq_rep", [2, 128, 96])
        self.gk_d = di("gk_rep", [2, 128, 96])
        self.ident_d = di("ident", [128, 128])
        self.tri_d = di("tri", [128, 128])
        self.ltri_d = di("ltri", [128, 128])
        self.ones_d = di("ones", [128, 128])
        self.cos_d = di("cos", [NT, 16])
        self.sin_d = di("sin", [NT, 16])
        self.y = nc.dram_tensor("y", [SEQ, D], F32, kind="ExternalOutput").ap()
        self.hd = nc.dram_tensor("hd", [NT, D], F32, kind="Internal").ap()
        self.qT_d = nc.dram_tensor("qT_d", [MLA_H, QK, NT], BF16, kind="Internal").ap()
        self.kT_d = nc.dram_tensor("kT_d", [MLA_H, QK, NT], BF16, kind="Internal").ap()
        self.va_d = nc.dram_tensor("va_d", [NT, 8, 192], BF16, kind="Internal").ap()

        with contextlib.ExitStack() as es:
            self.kb = KB(nc, es)
            self.build(es)

    def S(self, es, name, shape, dt):
        self.uid = getattr(self, "uid", 0) + 1
        return es.enter_context(self.nc.sbuf_tensor("sb%d_%s" % (self.uid, name), list(shape), dt))

    def P(self, es, name, shape, dt):
        self.uid = getattr(self, "uid", 0) + 1
        return es.enter_context(self.nc.psum_tensor("ps%d_%s" % (self.uid, name), list(shape), dt))

    def h_src(self, kind, t0, n):
        if kind == "x":
            return self.meta[0:16, :] if t0 == 0 else self.x[t0 - 16:t0 - 16 + n, :]
        return self.hd[t0:t0 + n, :]

    def h_dst(self, kind, t0, n):
        if kind == "y":
            return None if t0 == 0 else self.y[t0 - 16:t0 - 16 + n, :]
        return self.hd[t0:t0 + n, :]

    def build(self, es):
        kb, nc = self.kb, self.nc
        self.ident = self.S(es, "ident", [128, 128], BF16)
        self.tri32 = self.S(es, "tri32", [128, 128], F32)
        self.ltri32 = self.S(es, "ltri32", [128, 128], F32)
        self.ones32 = self.S(es, "ones32", [128, 128], F32)
        self.lnT = self.S(es, "lnT", [128, 8, 8], F32)
        self.cb_ = Buf("consts")
        kb.dma("pool", self.ident[:], self.ident_d, wp=[self.cb_])
        kb.dma("sp", self.tri32[:], self.tri_d, wp=[self.cb_])
        kb.dma("sp", self.ltri32[:], self.ltri_d, wp=[self.cb_])
        kb.dma("sp", self.ones32[:], self.ones_d, wp=[self.cb_])
        kb.dma("sp", self.lnT[:], self.lnT_d, wp=[self.cb_])
        for ph in self.phases:
            kind = ph[0]
            with contextlib.ExitStack() as pes:
                if kind == "mlp":
                    self.phase_mlp(pes, *ph[1:])
                elif kind == "ssd":
                    self.phase_ssd(pes, *ph[1:])
                elif kind == "mla":
                    self.phase_mla(pes, *ph[1:])
                kb.barrier()
        kb.barrier()

    def rstd_newton(self, st, st_b, n, inv_n, eps):
        kb, nc = self.kb, self.nc
        V = nc.vector
        I32 = mybir.dt.int32
        for f in self.rstd_newton_ops(st, st_b, n, inv_n, eps):
            f()

    def rstd_newton_ops(self, st, st_b, n, inv_n, eps):
        kb, nc = self.kb, self.nc
        V = nc.vector
        I32 = mybir.dt.int32
        x, g, t = st[0:n, 1:2], st[0:n, 2:3], st[0:n, 3:4]
        ops = []
        ops.append(lambda: kb.op(kb.dve, lambda: V.tensor_scalar(x, st[0:n, 0:1], inv_n, eps, ALU.mult, ALU.add), r=[st_b], wp=[st_b]))
        ops.append(lambda: kb.op(kb.dve, lambda: V.tensor_scalar(g.bitcast(I32), x.bitcast(I32), 1, None, ALU.arith_shift_right), r=[st_b], wp=[st_b]))
        ops.append(lambda: kb.op(kb.dve, lambda: V.tensor_scalar(g.bitcast(I32), g.bitcast(I32), -1, 0x5f3759df, ALU.mult, ALU.add), r=[st_b], wp=[st_b]))
        for _ in range(2):
            ops.append(lambda: kb.op(kb.dve, lambda: V.scalar_tensor_tensor(t, g, x, g, ALU.mult, ALU.mult), r=[st_b], wp=[st_b]))
            ops.append(lambda: kb.op(kb.dve, lambda: V.tensor_scalar(t, t, -0.5, 1.5, ALU.mult, ALU.add), r=[st_b], wp=[st_b]))
            ops.append(lambda: kb.op(kb.dve, lambda: V.tensor_tensor(g, g, t, ALU.mult), r=[st_b], wp=[st_b]))
        return ops

    def norm_T(self, hb, h_ap, n, gain_ap, dst_ap, dst_b, sc, newton=False):
        kb, nc = self.kb, self.nc
        junk, junk_b, ss, ss_b, xn, xn_b, tp, tp_b = sc
        kb.op(kb.act, lambda: nc.scalar.activation(out=junk[0:n, :], in_=h_ap, func=AF.Square, accum_out=ss[0:n, 0:1]),
              r=[hb], w=[junk_b, ss_b] if junk_b is not xn_b else [xn_b, ss_b])
        if newton:
            self.rstd_newton(ss, ss_b, n, 1.0 / D, EPS)
        else:
            kb.op(kb.act, lambda: nc.scalar.activation(out=ss[0:n, 1:2], in_=ss[0:n, 0:1], func=AF.Ln, bias=EPS, scale=1.0 / D),
                  r=[ss_b], wp=[ss_b])
            kb.op(kb.act, lambda: nc.scalar.activation(out=ss[0:n, 2:3], in_=ss[0:n, 1:2], func=AF.Exp, scale=-0.5),
                  r=[ss_b], wp=[ss_b])
        kb.op(kb.dve, lambda: nc.vector.tensor_scalar(xn[0:n, :], h_ap, ss[0:n, 2:3], None, ALU.mult),
              r=[hb, ss_b], w=[xn_b])
        for c in range(8):
            kb.op(kb.pe, lambda c=c: nc.tensor.transpose(tp[:, c, 0:n], xn[0:n, c * 128:(c + 1) * 128], self.ident[0:n, 0:n]),
                  r=[xn_b, self.cb_], w=[tp_b] if c == 0 else [], wp=[tp_b] if c else [], inc=(c == 7))
        kb.op(kb.dve, lambda: nc.vector.tensor_tensor(dst_ap, tp[:, :, 0:n], bc(gain_ap, [128, 8, n], 2), ALU.mult),
              r=[tp_b, self.cb_], w=[dst_b])

    def norm_scratch(self, es, pfx, tp_ps):
        ss = self.S(es, pfx + "ss", [128, 4], F32)
        xn = self.S(es, pfx + "xn", [128, 1024], BF16)
        tp = tp_ps[:].bitcast(BF16)[:, 0:1024].rearrange("p (c t) -> p c t", c=8)
        xn_b = Buf()
        return (xn, xn_b, ss, Buf(), xn, xn_b, tp, PB())

    def phase_mlp(self, es, li, src, dst):
        kb, nc = self.kb, self.nc
        wup = self.S(es, "wup", [128, 8, DFF], BF16)
        wdn = self.S(es, "wdn", [128, 32, D], BF16)
        wup_b, wdn_b = Buf(), Buf()
        upv = self.w_up[li].rearrange("(c p) f -> p c f", p=128)
        dnv = self.w_dn[li].rearrange("(c p) d -> p c d", p=128)
        for c in range(8):
            kb.dma("pool", wup[:, c, :], upv[:, c, :], wp=[wup_b])
        for c in range(0, 32, 4):
            kb.dma("pool", wdn[:, c:c + 4, :], dnv[:, c:c + 4, :], wp=[wdn_b])
        NSLOT = 7
        hs = [self.S(es, "mh%d" % i, [128, D], F32) for i in range(NSLOT)]
        hs_b = [Buf() for _ in range(NSLOT)]
        xnT = self.S(es, "m_xnT", [128, 8, 512], BF16)
        xnT_b = Buf()
        uT = self.S(es, "m_uT", [128, 32, 512], BF16)
        uT_b = [Buf() for _ in range(32)]
        r32 = [self.S(es, "m_r32_%d" % i, [128, 512], F32) for i in range(2)]
        r32_b = [Buf(), Buf()]
        tp_ps = [self.P(es, "m_tp%d" % i, [128, 512], F32) for i in range(2)]
        pu = [self.P(es, "m_pu%d" % i, [128, 512], F32) for i in range(3)]
        pu_b = [PB() for _ in range(3)]
        pd = [self.P(es, "m_pd%d" % i, [128, 512], F32) for i in range(3)]
        pd_b = [PB() for _ in range(3)]
        nsc = [self.norm_scratch(es, "m%d" % i, tp_ps[i]) for i in range(2)]
        gain = self.lnT[:, 4 + li, :]
        groups = [[TILES[0]]] + [TILES[1 + 4 * g:5 + 4 * g] for g in range(8)]
        slot = 0
        iu = 0
        ipd = 0
        inorm = 0
        for grp in groups:
            ntok = sum(n for _, n in grp)
            myslots = []
            for (t0, n) in grp:
                s = slot % NSLOT
                slot += 1
                myslots.append(s)
                kb.dma("sp", hs[s][0:n, :], self.h_src(src, t0, n), w=[hs_b[s]])
            off = 0
            for (t0, n), s in zip(grp, myslots):
                self.norm_T(hs_b[s], hs[s][0:n, :], n, gain, xnT[:, :, off:off + n], xnT_b, nsc[inorm % 2])
                inorm += 1
                off += n
            for fc in range(32):
                p = iu % 3
                for c in range(8):
                    kb.op(kb.pe, lambda c=c, fc=fc, p=p: nc.tensor.matmul(pu[p][:, 0:ntok], wup[:, c, fc * 128:(fc + 1) * 128],
                                                                      xnT[:, c, 0:ntok], start=(c == 0), stop=(c == 7)),
                          r=[wup_b, xnT_b], w=[pu_b[p]] if c == 0 else [], wp=[pu_b[p]] if c else [], inc=(c == 7))
                rr = iu % 2
                kb.op(kb.act, lambda p=p, rr=rr: nc.scalar.activation(out=r32[rr][:, 0:ntok], in_=pu[p][:, 0:ntok], func=AF.Relu),
                      r=[pu_b[p]], w=[r32_b[rr]])
                E = kb.dve if (fc % 2 == 0) else kb.pool
                kb.op(E, lambda rr=rr, fc=fc, E=E: E.e.tensor_tensor(uT[:, fc, 0:ntok], r32[rr][:, 0:ntok], r32[rr][:, 0:ntok], ALU.mult),
                      r=[r32_b[rr]], w=[uT_b[fc]])
                iu += 1
            off = 0
            for (t0, n), s in zip(grp, myslots):
                for hh in range(2):
                    p = ipd % 3
                    ipd += 1
                    for fc in range(32):
                        kb.op(kb.pe, lambda fc=fc, p=p, off=off, n=n, hh=hh: nc.tensor.matmul(
                            pd[p][0:n, :], uT[:, fc, off:off + n], wdn[:, fc, hh * 512:(hh + 1) * 512],
                            start=(fc == 0), stop=(fc == 31)),
                            r=[uT_b[fc], wdn_b], w=[pd_b[p]] if fc == 0 else [], wp=[pd_b[p]] if fc else [], inc=(fc == 31))
                    kb.op(kb.dve, lambda p=p, n=n, hh=hh, s=s: nc.vector.tensor_tensor(
                        hs[s][0:n, hh * 512:(hh + 1) * 512], pd[p][0:n, :], hs[s][0:n, hh * 512:(hh + 1) * 512], ALU.add),
                        r=[pd_b[p], hs_b[s]], wp=[hs_b[s]])
                dst_ap = self.h_dst(dst, t0, n)
                if dst_ap is not None:
                    kb.dma("sp", dst_ap, hs[s][0:n, :], r=[hs_b[s]])
                off += n

    def phase_ssd(self, es, j, li, src, dst):
        from functools import partial
        kb, nc = self.kb, self.nc
        V, A, G, T = nc.vector, nc.scalar, nc.gpsimd, nc.tensor
        w_in = self.S(es, "s_win", [128, 8, SSD_IN], BF16)
        w_out = self.S(es, "s_wout", [128, 16, D], BF16)
        win_b, wout_b, par_b = Buf(), Buf(), Buf()
        wv = self.w_ssd_in[j].rearrange("(c p) f -> p c f", p=128)
        for c in range(8):
            kb.dma("pool", w_in[:, c, :], wv[:, c, :], wp=[win_b])
        wov = self.w_ssd_out[j].rearrange("(c p) f -> p c f", p=128)
        for c in range(0, 16, 4):
            kb.dma("pool", w_out[:, c:c + 4, :], wov[:, c:c + 4, :], wp=[wout_b])
        cw = self.S(es, "s_cw", [128, 32, 4], F32)
        cb = self.S(es, "s_cb", [128, 32], F32)
        dtb = self.S(es, "s_dtb", [128, 32], F32)
        arep = self.S(es, "s_arep", [128, 32], F32)
        dsk = self.S(es, "s_dsk", [128, 32], F32)
        sng = self.S(es, "s_sng", [128, 16], F32)
        kb.dma("sp", cw[:], self.cw_d[j], wp=[par_b])
        kb.dma("sp", cb[:], self.cb_d[j], wp=[par_b])
        kb.dma("sp", dtb[:], self.dtb_d[j], wp=[par_b])
        kb.dma("sp", arep[:], self.alog_d[j], wp=[par_b])
        kb.dma("sp", dsk[:], self.dsk_d[j], wp=[par_b])
        kb.dma("sp", sng[:], self.sng_d[j], wp=[par_b])
        arep_b = Buf()
        kb.op(kb.act, lambda: A.activation(out=arep[:], in_=arep[:], func=AF.Exp), r=[par_b], w=[arep_b])
        kb.op(kb.dve, lambda: V.tensor_scalar(arep[:], arep[:], -1.0, None, ALU.mult), r=[arep_b], w=[arep_b])
        cwb = Buf()
        kb.op(kb.dve, lambda: V.tensor_scalar(cw[:], cw[:], 0.5, None, ALU.mult), r=[par_b], w=[cwb])
        kb.op(kb.dve, lambda: V.tensor_scalar(cb[:], cb[:], 0.5, None, ALU.mult), r=[par_b], wp=[cwb])
        hs = [self.S(es, "s_h%d" % i, [128, D], F32) for i in range(2)]
        hs_b = [Buf(), Buf()]
        tpF = self.P(es, "s_tpF", [128, 512], F32)
        pzs = [self.P(es, "s_pz%d" % i, [128, 512], F32) for i in range(2)]
        fm = self.P(es, "s_fm", [128, 512], F32)
        tpB = self.P(es, "s_tpB", [128, 512], F32)
        dTb = self.P(es, "s_dT", [128, 512], F32)
        bm = self.P(es, "s_bm", [128, 512], F32)
        yb = self.P(es, "s_y", [128, 512], F32)
        pz_b = [PB(), PB()]
        fm_b, tpB_b, dT_b, bm_b, y_b = PB(), PB(), PB(), PB(), PB()
        gT = self.S(es, "s_gT", [128, 16, 128], BF16)
        gT_b = Buf()
        _ss = self.S(es, "s_nss", [128, 4], F32)
        _xn = gT[:, 0:8, :].rearrange("p c t -> p (c t)")
        _tp = tpF[:].bitcast(BF16)[:, 0:1024].rearrange("p (c t) -> p c t", c=8)
        nsc = (_xn, gT_b, _ss, Buf(), _xn, gT_b, _tp, PB())
        tpB16 = tpB[:].bitcast(BF16)
        tpBg = tpB16.rearrange("p (c t) -> p c t", c=8)
        xnT = [self.S(es, "s_xnT%d" % i, [128, 8, 131], BF16) for i in range(2)]
        xnT_b = [Buf(), Buf()]
        for i in range(2):
            kb.op(kb.pool, lambda: G.memset(xnT[i][:], 0.0), w=[xnT_b[i]])
        xbc_xs = self.S(es, "s_xbcxs", [128, 16, 128], BF16)
        xsT_b = [Buf() for _ in range(16)]
        xbc_bc = [self.S(es, "s_xbcbc%d" % i, [128, 16, 128], BF16) for i in range(2)]
        bc_b = [[Buf() for _ in range(16)] for _ in range(2)]
        NU, NA = 4, 7
        u = [self.S(es, "s_u%d" % i, [128, 131], F32) for i in range(NU)]
        u_b = [Buf() for _ in range(NU)]
        acc = [self.S(es, "s_acc%d" % i, [128, 128], F32) for i in range(NA)]
        acc_b = [Buf() for _ in range(NA)]
        th = [self.S(es, "s_th%d" % i, [128, 128], F32) for i in range(2)]
        th_b = [Buf(), Buf()]
        xs_tm = self.S(es, "s_xstm", [128, 2048], BF16)
        xs_b = [Buf() for _ in range(8)]
        B_tm = self.S(es, "s_Btm", [128, 1024], BF16)
        Btm_b = Buf()
        smF = self.S(es, "s_smF", [128, 4, 32], F32)
        EXPT, ACS, TMP, DTE = range(4)
        smF_b = [Buf() for _ in range(4)]
        smB = [self.S(es, "s_smB%d" % i, [128, 5, 32], F32) for i in range(2)]
        DTV, ADT, DFS, CD, W2 = range(5)
        smB_b = [[Buf() for _ in range(5)] for _ in range(2)]
        rhsD = [self.S(es, "s_rhsD%d" % i, [128, 512], F32) for i in range(2)]
        rhsD_b = [Buf(), Buf()]
        Ee = [self.S(es, "s_E%d" % i, [128, 512], F32) for i in range(2)]
        E_b = [Buf(), Buf()]
        CBm = [self.S(es, "s_CBm0", [128, 128], F32)] * 2
        _cb = Buf()
        CBm_b = [_cb, _cb]
        MT = [self.S(es, "s_MT%d" % i, [128, 512], BF16) for i in range(2)]
        MT_b = [Buf(), Buf()]
        xdt = [self.S(es, "s_xdt%d" % i, [128, 256], BF16) for i in range(2)]
        xdt_b = [Buf(), Buf()]
        xdtd = [self.S(es, "s_xdtd%d" % i, [128, 256], BF16) for i in range(2)]
        xdtd_b = [Buf(), Buf()]
        tt = [[self.S(es, "s_t%d_%d" % (k, i), [128, 256], F32) for i in range(2)] for k in range(4)]
        tt_b = [[Buf(), Buf()] for _ in range(4)]
        tt.append(tt[0])
        tt_b.append(tt_b[0])
        stg = self.S(es, "s_stg", [128, 8, 4], F32)
        stg_b = [Buf() for _ in range(8)]
        S32 = self.S(es, "s_S32", [128, 2048], F32)
        Sbf = self.S(es, "s_Sbf", [128, 2048], BF16)
        S32_b = [Buf() for _ in range(8)]
        Sbf_b = [Buf() for _ in range(8)]
        stmp = [self.S(es, "s_stmp0", [128, 256], F32)] * 2
        _sb = Buf()
        stmp_b = [_sb, _sb]
        kb.op(kb.pool, lambda: G.memset(S32[:], 0.0), w=S32_b)
        kb.op(kb.pool, lambda: G.memset(Sbf[:], 0.0), w=Sbf_b)
        v3 = lambda ap: ap.rearrange("p (j l) -> p j l", j=4)
        NTL = len(TILES)

        def f_load(ti):
            t0, n = TILES[ti]
            cur, prev = ti % 2, (ti + 1) % 2
            nprev = TILES[ti - 1][1] if ti > 0 else 0
            X = xnT[cur]
            kb.dma("sp", hs[cur][0:n, :], self.h_src(src, t0, n), w=[hs_b[cur]])
            self.norm_T(hs_b[cur], hs[cur][0:n, :], n, self.lnT[:, li, :], X[:, :, 3:3 + n], xnT_b[cur], nsc, newton=True)
            kb.op(kb.pool, lambda: G.tensor_copy(X[:, :, 0:3], xnT[prev][:, :, nprev:nprev + 3]), r=[xnT_b[prev]], wp=[xnT_b[cur]])

        def f_dt1(ti):
            t0, n = TILES[ti]
            cur = ti % 2
            X = xnT[cur]
            sB, sBb = smB[cur], smB_b[cur]
            pdt = fm[0:n, 0:32]
            for c in range(8):
                kb.op(kb.pe, lambda: T.matmul(pdt, X[:, c, 3:3 + n], w_in[:, c, 6144:6176], start=(c == 0), stop=(c == 7)),
                      r=[win_b, xnT_b[cur]], w=[fm_b] if c == 0 else [], wp=[fm_b] if c else [], inc=(c == 7))
            kb.op(kb.dve, lambda: V.tensor_tensor(sB[0:n, DTV, :], pdt, dtb[0:n, :], ALU.add), r=[fm_b, par_b], w=[sBb[DTV]])
            kb.op(kb.act, lambda: A.activation(out=smF[0:n, EXPT, :], in_=sB[0:n, DTV, :], func=AF.Exp), r=[sBb[DTV]], w=[smF_b[EXPT]])
            kb.op(kb.act, lambda: A.activation(out=sB[0:n, DTV, :], in_=smF[0:n, EXPT, :], func=AF.Ln, bias=1.0), r=[smF_b[EXPT]], w=[sBb[DTV]])
            kb.op(kb.dve, lambda: V.tensor_tensor(sB[0:n, ADT, :], sB[0:n, DTV, :], arep[0:n, :], ALU.mult), r=[sBb[DTV], arep_b], w=[sBb[ADT]])

        def f_dt2(ti):
            t0, n = TILES[ti]
            cur = ti % 2
            sB, sBb = smB[cur], smB_b[cur]
            pacs, ptot = fm[0:n, 32:64], fm[:, 64:96]
            kb.op(kb.pe, lambda: T.matmul(pacs, self.tri32[0:n, 0:n], sB[0:n, ADT, :], start=True, stop=True), r=[sBb[ADT], self.cb_], w=[fm_b])
            kb.op(kb.pe, lambda: T.matmul(ptot, self.ones32[0:n, :], sB[0:n, ADT, :], start=True, stop=True), r=[sBb[ADT], self.cb_], wp=[fm_b])
            kb.op(kb.dve, lambda: V.tensor_copy(smF[0:n, ACS, :], pacs), r=[fm_b], w=[smF_b[ACS]])
            kb.op(kb.act, lambda: A.activation(out=sB[0:n, DFS, :], in_=pacs, func=AF.Exp), r=[fm_b], w=[sBb[DFS]])
            kb.op(kb.act, lambda: A.activation(out=sB[:, CD, :], in_=ptot, func=AF.Exp), r=[fm_b], w=[sBb[CD]])
            kb.op(kb.dve, lambda: V.tensor_tensor(smF[0:n, TMP, :], ptot[0:n, :], smF[0:n, ACS, :], ALU.subtract), r=[fm_b, smF_b[ACS]], w=[smF_b[TMP]])
            kb.op(kb.act, lambda: A.activation(out=smF[0:n, DTE, :], in_=smF[0:n, TMP, :], func=AF.Exp), r=[smF_b[TMP]], w=[smF_b[DTE]])
            kb.op(kb.dve, lambda: V.tensor_tensor(sB[0:n, W2, :], sB[0:n, DTV, :], smF[0:n, DTE, :], ALU.mult), r=[sBb[DTV], smF_b[DTE]], w=[sBb[W2]])

        def conv_dst(ti, cc, n):
            cur = ti % 2
            if cc < 16:
                return xbc_xs[:, cc, 0:n], xsT_b[cc]
            return xbc_bc[cur][:, cc - 16, 0:n], bc_b[cur][cc - 16]

        def f_conv(ti, s):
            t0, n = TILES[ti]
            cur = ti % 2
            X = xnT[cur]
            cc = s
            if 0 <= cc < 32:
                p = cc % 2
                pz = pzs[p][:, 0:3 + n]
                for c in range(8):
                    kb.op(kb.pe, lambda: T.matmul(pz, w_in[:, c, 2048 + cc * 128:2048 + (cc + 1) * 128], X[:, c, 0:3 + n], start=(c == 0), stop=(c == 7)),
                          r=[win_b, xnT_b[cur]], w=[pz_b[p]] if c == 0 else [], wp=[pz_b[p]] if c else [], inc=(c == 7))
            cc = s - 1
            if 0 <= cc < 32:
                p = cc % 2
                pz = pzs[p][:, 0:3 + n]
                kb.op(kb.act, lambda: A.activation(out=acc[cc % NA][:, 0:n], in_=pz[:, 3:3 + n], func=AF.Identity, bias=cb[:, cc:cc + 1], scale=cw[:, cc, 3:4]),
                      r=[pz_b[p], cwb], w=[acc_b[cc % NA]])
                kb.op(kb.act, lambda: A.copy(u[cc % NU][:, 0:3 + n], pz), r=[pz_b[p]], w=[u_b[cc % NU]])
            for k in range(3):
                cc = s - 2 - k
                if 0 <= cc < 32:
                    p = cc % NU
                    a_ = acc[cc % NA][:, 0:n]
                    kb.op(kb.dve, lambda: V.scalar_tensor_tensor(a_, u[p][:, k:k + n], cw[:, cc, k:k + 1], a_, ALU.mult, ALU.add),
                          r=[u_b[p], cwb], w=[acc_b[cc % NA]])
            cc = s - 5
            if 0 <= cc < 32:
                kb.op(kb.act, lambda: A.activation(out=th[cc % 2][:, 0:n], in_=acc[cc % NA][:, 0:n], func=AF.Tanh), r=[acc_b[cc % NA]], w=[th_b[cc % 2]])
            cc = s - 6
            if 0 <= cc < 32:
                dst_ap, dst_b = conv_dst(ti, cc, n)
                kb.op(kb.dve, lambda: V.scalar_tensor_tensor(dst_ap, th[cc % 2][:, 0:n], 1.0, acc[cc % NA][:, 0:n], ALU.add, ALU.mult),
                      r=[th_b[cc % 2], acc_b[cc % NA]], w=[dst_b])

        def front(ti):
            st = [partial(f_load, ti), partial(f_conv, ti, 0), partial(f_dt1, ti), partial(f_conv, ti, 1), partial(f_dt2, ti)]
            st += [partial(f_conv, ti, s) for s in range(2, 38)]
            return st

        def b_trx(ti, half):
            t0, n = TILES[ti]
            for k in range(8):
                cc = half * 8 + k
                kb.op(kb.pe, lambda: T.transpose(tpB16[0:n, k * 128:(k + 1) * 128], xbc_xs[:, cc, 0:n], self.ident[:, :]),
                      r=[xsT_b[cc], self.cb_], w=[tpB_b] if k == 0 else [], wp=[tpB_b] if k else [], inc=(k == 7))
            kb.op(kb.act, lambda: A.copy(xs_tm[0:n, half * 1024:(half + 1) * 1024], tpB16[0:n, :]), r=[tpB_b], w=xs_b[4 * half:4 * half + 4])

        def b_trB(ti):
            t0, n = TILES[ti]
            cur = ti % 2
            for g in range(8):
                kb.op(kb.pe, lambda: T.transpose(tpB16[0:n, g * 128:(g + 1) * 128], xbc_bc[cur][:, g, 0:n], self.ident[:, :]),
                      r=[bc_b[cur][g], self.cb_], w=[tpB_b] if g == 0 else [], wp=[tpB_b] if g else [], inc=(g == 7))
            kb.op(kb.dve, lambda: V.tensor_copy(B_tm[0:n, :], tpB16[0:n, :]), r=[tpB_b], w=[Btm_b])

        def gchain(ti, g):
            t0, n = TILES[ti]
            cur = ti % 2
            X = xnT[cur]
            sB, sBb = smB[cur], smB_b[cur]
            b2 = g % 2
            hsl = slice(4 * g, 4 * g + 4)
            gsl = slice(g * 256, (g + 1) * 256)
            BT, CT = xbc_bc[cur][:, g, 0:n], xbc_bc[cur][:, 8 + g, 0:n]
            BTb, CTb = bc_b[cur][g], bc_b[cur][8 + g]
            x3 = lambda ap: ap.rearrange("p (j f) -> p j f", j=4)
            rD = rhsD[b2][0:n, 0:4 * n]
            Ev = Ee[b2][0:n, 0:4 * n]
            MTv = MT[b2][0:n, 0:4 * n]
            pcb = bm[0:n, 0:n]
            pst = bm[:, 256:512]
            zq = fm[0:n, 256:512]
            xs3 = x3(xs_tm[0:n, gsl])
            t1, t2, t3, szg, thg = [tt[k][b2][0:n, :] for k in range(5)]
            t1b, t2b, t3b, szb, thb = [tt_b[k][b2] for k in range(5)]
            steps = []

            def s1():
                kb.op(kb.pool, lambda: G.tensor_tensor(v3(rD), bc(sB[0:n, ADT, hsl], [n, 4, n], 2), bc(self.tri32[0:n, 0:n], [n, 4, n], 1), ALU.mult),
                      r=[sBb[ADT], self.cb_], w=[rhsD_b[b2]])
                kb.op(kb.pool, lambda: G.tensor_tensor(x3(xdt[b2][0:n, :]), xs3, bc(sB[0:n, DTV, hsl], [n, 4, 64], 2), ALU.mult),
                      r=[xs_b[g], sBb[DTV]], w=[xdt_b[b2]])
            steps.append(s1)

            def s2():
                kb.op(kb.pe, lambda: T.matmul(dTb[0:n, 0:4 * n], self.ltri32[0:n, 0:n], rD, start=True, stop=True), r=[rhsD_b[b2], self.cb_], w=[dT_b])
                kb.op(kb.act, lambda: A.activation(out=Ev, in_=dTb[0:n, 0:4 * n], func=AF.Exp), r=[dT_b], w=[E_b[b2]])
                kb.op(kb.pool, lambda: G.tensor_tensor(x3(xdtd[b2][0:n, :]), xs3, bc(sB[0:n, W2, hsl], [n, 4, 64], 2), ALU.mult),
                      r=[xs_b[g], sBb[W2]], w=[xdtd_b[b2]])
            steps.append(s2)

            def s3():
                kb.op(kb.pe, lambda: T.matmul(pcb, BT, CT, start=True, stop=True), r=[BTb, CTb], w=[bm_b])
                kb.op(kb.dve, lambda: V.tensor_tensor(CBm[b2][0:n, 0:n], pcb, self.tri32[0:n, 0:n], ALU.mult), r=[bm_b, self.cb_], w=[CBm_b[b2]])
                for c in range(8):
                    kb.op(kb.pe, lambda: T.matmul(zq, X[:, c, 3:3 + n], w_in[:, c, g * 256:(g + 1) * 256], start=(c == 0), stop=(c == 7)),
                          r=[win_b, xnT_b[cur]], w=[fm_b] if c == 0 else [], wp=[fm_b] if c else [], inc=(c == 7))
                kb.op(kb.act, lambda: A.activation(out=thg, in_=zq, func=AF.Tanh, scale=0.5), r=[fm_b], w=[thb])
                kb.op(kb.dve, lambda: V.scalar_tensor_tensor(szg, thg, 1.0, zq, ALU.add, ALU.mult), r=[thb, fm_b], w=[szb])
                kb.op(kb.pool, lambda: G.tensor_tensor(v3(MTv), v3(Ev), bc(CBm[b2][0:n, 0:n], [n, 4, n], 1), ALU.mult),
                      r=[E_b[b2], CBm_b[b2]], w=[MT_b[b2]])
            steps.append(s3)

            def s4():
                for jj in range(4):
                    kb.op(kb.pe, lambda: T.matmul(yb[0:n, jj * 64:(jj + 1) * 64], MTv[:, jj * n:(jj + 1) * n], xdt[b2][0:n, jj * 64:(jj + 1) * 64], start=True, stop=True),
                          r=[MT_b[b2], xdt_b[b2]], w=[y_b] if jj == 0 else [], wp=[y_b] if jj else [], inc=False)
                kb.op(kb.pe, lambda: T.matmul(yb[0:n, 256:512], CT, Sbf[:, gsl], start=True, stop=True), r=[CTb, Sbf_b[g]], wp=[y_b])
                kb.op(kb.pool, lambda: G.tensor_tensor(x3(t3), xs3, bc(dsk[0:n, hsl], [n, 4, 64], 2), ALU.mult), r=[xs_b[g], par_b], w=[t3b])
                kb.op(kb.dve, lambda: V.tensor_tensor(x3(t1), x3(yb[0:n, 256:512]), bc(sB[0:n, DFS, hsl], [n, 4, 64], 2), ALU.mult),
                      r=[y_b, sBb[DFS]], w=[t1b])
                kb.op(kb.dve, lambda: V.tensor_tensor(t2, yb[0:n, 0:256], t1, ALU.add), r=[y_b, t1b], w=[t2b])
            steps.append(s4)

            def s5():
                kb.op(kb.pe, lambda: T.matmul(pst, B_tm[0:n, g * 128:(g + 1) * 128], xdtd[b2][0:n, :], start=True, stop=True),
                      r=[Btm_b, xdtd_b[b2]], w=[bm_b])
                kb.op(kb.pool, lambda: G.tensor_tensor(x3(stmp[b2][:, :]), x3(S32[:, gsl]), bc(sB[:, CD, hsl], [128, 4, 64], 2), ALU.mult),
                      r=[S32_b[g], sBb[CD]], w=[stmp_b[b2]])
                kb.op(kb.dve, lambda: V.tensor_tensor(S32[:, gsl], stmp[b2][:, :], pst, ALU.add), r=[stmp_b[b2], bm_b], w=[S32_b[g]])
                kb.op(kb.act, lambda: A.copy(Sbf[:, gsl], S32[:, gsl]), r=[S32_b[g]], w=[Sbf_b[g]])
            steps.append(s5)

            def s6():
                kb.op(kb.pool, lambda: G.tensor_tensor(t2, t2, t3, ALU.add), r=[t3b], w=[t2b])
                kb.op(kb.pool, lambda: G.tensor_tensor(t2, t2, szg, ALU.mult), r=[szb], w=[t2b])
            steps.append(s6)

            sq = lambda: kb.op(kb.act, lambda: A.activation(out=t1, in_=t2, func=AF.Square, accum_out=stg[0:n, g, 0:1]), r=[t2b], w=[t1b, stg_b[g]])
            newt = self.rstd_newton_ops(stg[:, g, :], stg_b[g], n, 1.0 / 256, 4.0 * EPS)
            gnf = lambda: kb.op(kb.dve, lambda: V.tensor_scalar(xs_tm[0:n, gsl], t2, stg[0:n, g, 2:3], None, ALU.mult), r=[t2b, stg_b[g]], w=[xs_b[g]])
            return steps, (sq, newt, gnf)

        def b_gT(ti, half):
            t0, n = TILES[ti]
            for k in range(8):
                cc = half * 8 + k
                kb.op(kb.pe, lambda: T.transpose(tpBg[:, k, 0:n], xs_tm[0:n, cc * 128:(cc + 1) * 128], self.ident[0:n, 0:n]),
                      r=[xs_b[cc // 2], self.cb_], w=[tpB_b] if k == 0 else [], wp=[tpB_b] if k else [], inc=(k == 7))
            kb.op(kb.dve, lambda: V.tensor_tensor(gT[:, half * 8:(half + 1) * 8, 0:n], tpBg[:, :, 0:n], bc(sng[:, half * 8:(half + 1) * 8], [128, 8, n], 2), ALU.mult),
                  r=[tpB_b, par_b], w=[gT_b] if half == 0 else [], wp=[gT_b] if half else [])

        def b_out(ti, hh):
            t0, n = TILES[ti]
            cur = ti % 2
            po = dTb[0:n, :] if hh == 0 else yb[0:n, :]
            pbuf = dT_b if hh == 0 else y_b
            for cc in range(16):
                kb.op(kb.pe, lambda: T.matmul(po, gT[:, cc, 0:n], w_out[:, cc, hh * 512:(hh + 1) * 512], start=(cc == 0), stop=(cc == 15)),
                      r=[gT_b, wout_b], w=[pbuf] if cc == 0 else [], wp=[pbuf] if cc else [], inc=(cc == 15))
            kb.op(kb.dve, lambda: V.tensor_tensor(hs[cur][0:n, hh * 512:(hh + 1) * 512], po, hs[cur][0:n, hh * 512:(hh + 1) * 512], ALU.add),
                  r=[pbuf, hs_b[cur]], wp=[hs_b[cur]])
            if hh == 1:
                dap = self.h_dst(dst, t0, n)
                if dap is not None:
                    kb.dma("sp", dap, hs[cur][0:n, :], r=[hs_b[cur]])

        def back(ti):
            st = [partial(b_trx, ti, 0), partial(b_trx, ti, 1), partial(b_trB, ti)]
            for gp in range(4):
                (ca, ta), (cb_, tb) = gchain(ti, 2 * gp), gchain(ti, 2 * gp + 1)
                for a_, b_ in zip(ca, cb_):
                    st += [a_, b_]

                def tail(ta=ta, tb=tb):
                    ta[0]()
                    tb[0]()
                    for fa, fb in zip(ta[1], tb[1]):
                        fa()
                        fb()
                    ta[2]()
                    tb[2]()
                st.append(tail)
            st += [partial(b_gT, ti, 0), partial(b_gT, ti, 1), partial(b_out, ti, 0), partial(b_out, ti, 1)]
            return st

        def interleave(a, b):
            na, nb = len(a), len(b)
            i = jx = 0
            while i < na or jx < nb:
                if jx >= nb or (i < na and i * nb <= jx * na):
                    a[i]()
                    i += 1
                else:
                    b[jx]()
                    jx += 1

        for ti in range(NTL + 1):
            f = front(ti) if ti < NTL else []
            b = back(ti - 1) if ti >= 1 else []
            interleave(f, b)

    def phase_ssd_v1(self, es, j, li, src, dst):
        kb, nc = self.kb, self.nc
        V, A, G, T = nc.vector, nc.scalar, nc.gpsimd, nc.tensor
        w_in = self.S(es, "s_win", [128, 8, SSD_IN], BF16)
        w_out = self.S(es, "s_wout", [128, 16, D], BF16)
        win_b, wout_b, par_b = Buf(), Buf(), Buf()
        wv = self.w_ssd_in[j].rearrange("(c p) f -> p c f", p=128)
        for c in range(8):
            kb.dma("pool", w_in[:, c, :], wv[:, c, :], wp=[win_b])
        wov = self.w_ssd_out[j].rearrange("(c p) f -> p c f", p=128)
        for c in range(0, 16, 4):
            kb.dma("pool", w_out[:, c:c + 4, :], wov[:, c:c + 4, :], wp=[wout_b])
        cw = self.S(es, "s_cw", [128, 32, 4], F32)
        cb = self.S(es, "s_cb", [128, 32], F32)
        dtb = self.S(es, "s_dtb", [128, 32], F32)
        arep = self.S(es, "s_arep", [128, 32], F32)
        dsk = self.S(es, "s_dsk", [128, 32], F32)
        sng = self.S(es, "s_sng", [128, 16], F32)
        kb.dma("sp", cw[:], self.cw_d[j], wp=[par_b])
        kb.dma("sp", cb[:], self.cb_d[j], wp=[par_b])
        kb.dma("sp", dtb[:], self.dtb_d[j], wp=[par_b])
        kb.dma("sp", arep[:], self.alog_d[j], wp=[par_b])
        kb.dma("sp", dsk[:], self.dsk_d[j], wp=[par_b])
        kb.dma("sp", sng[:], self.sng_d[j], wp=[par_b])
        arep_b = Buf()
        kb.op(kb.act, lambda: A.activation(out=arep[:], in_=arep[:], func=AF.Exp), r=[par_b], w=[arep_b])
        kb.op(kb.dve, lambda: V.tensor_scalar(arep[:], arep[:], -1.0, None, ALU.mult), r=[arep_b], w=[arep_b])
        hs = [self.S(es, "s_h%d" % i, [128, D], F32) for i in range(2)]
        hs_b = [Buf(), Buf()]
        tp2 = self.P(es, "s_tp2", [128, 1024], F32)
        pzbs = [self.P(es, "s_pz%d" % i, [128, 512], F32) for i in range(2)]
        zqb = self.P(es, "s_zq", [128, 512], F32)
        dTb = self.P(es, "s_dT", [128, 512], F32)
        miscb = self.P(es, "s_misc", [128, 512], F32)
        yb = self.P(es, "s_y", [128, 512], F32)
        pz_b = [PB(), PB()]
        _z = PB()
        zq_b = [_z, _z]
        dT_b = PB()
        misc_b = PB()
        pdt_b = pacs_b = ptot_b = pst_b = misc_b
        pcb_b = [misc_b, misc_b]
        ydg_b = yof_b = PB()
        nsc = self.norm_scratch(es, "s", tp2)
        tp_b = nsc[7]
        tp16 = tp2[:].bitcast(BF16)
        tpg = tp16.rearrange("p (c t) -> p c t", c=16)
        xnT = [self.S(es, "s_xnT%d" % i, [128, 8, 131], BF16) for i in range(2)]
        xnT_b = [Buf(), Buf()]
        for i in range(2):
            kb.op(kb.pool, lambda: G.memset(xnT[i][:], 0.0), w=[xnT_b[i]])
        xbcT = self.S(es, "s_xbcT", [128, 32, 128], BF16)
        xbc_b = [Buf() for _ in range(32)]
        acc = [self.S(es, "s_acc%d" % i, [128, 128], F32) for i in range(3)]
        acc_b = [Buf() for _ in range(3)]
        xs_tm = self.S(es, "s_xstm", [128, 2048], BF16)
        xs_b = Buf()
        B_tm = self.S(es, "s_Btm", [128, 1024], BF16)
        Btm_b = Buf()
        sz = self.S(es, "s_sz", [128, 2048], F32)
        sz_b = [Buf() for _ in range(8)]
        sm = self.S(es, "s_sm", [128, 10, 32], F32)
        DTV, EXPT, ADT, ACS, DFS, CD, DTE, TMP, W2 = range(9)
        sm_b = [Buf() for _ in range(10)]
        rhsD = [self.S(es, "s_rhsD0", [128, 512], F32)] * 2
        _b = Buf()
        rhsD_b = [_b, _b]
        Ee = [self.S(es, "s_E0", [128, 512], F32)] * 2
        _b = Buf()
        E_b = [_b, _b]
        CBm = [self.S(es, "s_CBm%d" % i, [128, 128], F32) for i in range(2)]
        CBm_b = [Buf(), Buf()]
        MT = [self.S(es, "s_MT%d" % i, [128, 512], BF16) for i in range(2)]
        MT_b = [Buf(), Buf()]
        xdt = [self.S(es, "s_xdt%d" % i, [128, 256], BF16) for i in range(2)]
        xdt_b = [Buf(), Buf()]
        xdtd = [self.S(es, "s_xdtd%d" % i, [128, 256], BF16) for i in range(2)]
        xdtd_b = [Buf(), Buf()]
        tt = [[self.S(es, "s_t%d_%d" % (k, i), [128, 256], F32) for i in range(2)] for k in range(3)]
        tt_b = [[Buf(), Buf()] for _ in range(3)]
        tt.append(tt[0])
        tt_b.append(tt_b[0])
        stg = self.S(es, "s_stg", [128, 8, 4], F32)
        stg_b = [Buf() for _ in range(8)]
        gn = self.S(es, "s_gn", [128, 2048], BF16)
        gn_b = Buf()
        gT = self.S(es, "s_gT", [128, 16, 128], BF16)
        gT_b = Buf()
        S32 = self.S(es, "s_S32", [128, 2048], F32)
        Sbf = self.S(es, "s_Sbf", [128, 2048], BF16)
        S32_b = [Buf() for _ in range(8)]
        Sbf_b = [Buf() for _ in range(8)]
        stmp = [self.S(es, "s_stmp0", [128, 256], F32)] * 2
        _b = Buf()
        stmp_b = [_b, _b]
        kb.op(kb.pool, lambda: G.memset(S32[:], 0.0), w=S32_b)
        kb.op(kb.pool, lambda: G.memset(Sbf[:], 0.0), w=Sbf_b)
        nprev = 0
        ipz = 0
        for ti, (t0, n) in enumerate(TILES):
            cur, prev = ti % 2, (ti + 1) % 2
            X = xnT[cur]
            kb.dma("sp", hs[cur][0:n, :], self.h_src(src, t0, n), w=[hs_b[cur]])
            self.norm_T(hs_b[cur], hs[cur][0:n, :], n, self.lnT[:, li, :], X[:, :, 3:3 + n], xnT_b[cur], nsc)
            kb.op(kb.pool, lambda: G.tensor_copy(X[:, :, 0:3], xnT[prev][:, :, nprev:nprev + 3]), r=[xnT_b[prev]], wp=[xnT_b[cur]])
            nprev = n
            for cc in range(32):
                p = ipz % 2
                ipz += 1
                pz = pzbs[p][:, 0:3 + n]
                for c in range(8):
                    kb.op(kb.pe, lambda: T.matmul(pz, w_in[:, c, 2048 + cc * 128:2048 + (cc + 1) * 128], X[:, c, 0:3 + n], start=(c == 0), stop=(c == 7)),
                          r=[win_b, xnT_b[cur]], w=[pz_b[p]] if c == 0 else [], wp=[pz_b[p]] if c else [], inc=(c == 7))
                a_ = acc[p][:, 0:n]
                kb.op(kb.act, lambda: A.activation(out=a_, in_=pz[:, 3:3 + n], func=AF.Identity, bias=cb[:, cc:cc + 1], scale=cw[:, cc, 3:4]),
                      r=[pz_b[p], par_b], w=[acc_b[p]])
                for k in range(3):
                    kb.op(kb.dve, lambda: V.scalar_tensor_tensor(a_, pz[:, k:k + n], cw[:, cc, k:k + 1], a_, ALU.mult, ALU.add),
                          r=[pz_b[p], par_b], w=[acc_b[p]])
                kb.op(kb.act, lambda: A.activation(out=xbcT[:, cc, 0:n], in_=a_, func=AF.Silu), r=[acc_b[p]], w=[xbc_b[cc]])
            for g in range(8):
                zs = g % 2
                zq = zqb[0:n, zs * 256:(zs + 1) * 256]
                for c in range(8):
                    kb.op(kb.pe, lambda: T.matmul(zq, X[:, c, 3:3 + n], w_in[:, c, g * 256:(g + 1) * 256], start=(c == 0), stop=(c == 7)),
                          r=[win_b, xnT_b[cur]], w=[zq_b[zs]] if c == 0 else [], wp=[zq_b[zs]] if c else [], inc=(c == 7))
                kb.op(kb.act, lambda: A.activation(out=sz[0:n, g * 256:(g + 1) * 256], in_=zq, func=AF.Silu), r=[zq_b[zs]], w=[sz_b[g]])
            for cc in range(16):
                kb.op(kb.pe, lambda: T.transpose(tp16[0:n, cc * 128:(cc + 1) * 128], xbcT[:, cc, 0:n], self.ident[:, :]),
                      r=[xbc_b[cc], self.cb_], w=[tp_b] if cc == 0 else [], wp=[tp_b] if cc else [], inc=(cc == 15))
            kb.op(kb.act, lambda: A.copy(xs_tm[0:n, 0:1024], tp16[0:n, 0:1024]), r=[tp_b], w=[xs_b])
            kb.op(kb.dve, lambda: V.tensor_copy(xs_tm[0:n, 1024:2048], tp16[0:n, 1024:2048]), r=[tp_b], wp=[xs_b])
            for g in range(8):
                kb.op(kb.pe, lambda: T.transpose(tp16[0:n, g * 128:(g + 1) * 128], xbcT[:, 16 + g, 0:n], self.ident[:, :]),
                      r=[xbc_b[16 + g], self.cb_], w=[tp_b] if g == 0 else [], wp=[tp_b] if g else [], inc=(g == 7))
            kb.op(kb.act, lambda: A.copy(B_tm[0:n, :], tp16[0:n, 0:1024]), r=[tp_b], w=[Btm_b])
            pdt, pacs, ptot = miscb[0:n, 0:32], miscb[0:n, 32:64], miscb[:, 64:96]
            for c in range(8):
                kb.op(kb.pe, lambda: T.matmul(pdt, X[:, c, 3:3 + n], w_in[:, c, 6144:6176], start=(c == 0), stop=(c == 7)),
                      r=[win_b, xnT_b[cur]], w=[pdt_b] if c == 0 else [], wp=[pdt_b] if c else [], inc=(c == 7))
            smv = lambda k: sm[0:n, k, :]
            kb.op(kb.dve, lambda: V.tensor_tensor(smv(DTV), pdt, dtb[0:n, :], ALU.add), r=[pdt_b, par_b], w=[sm_b[DTV]])
            kb.op(kb.act, lambda: A.activation(out=smv(EXPT), in_=smv(DTV), func=AF.Exp), r=[sm_b[DTV]], w=[sm_b[EXPT]])
            kb.op(kb.act, lambda: A.activation(out=smv(DTV), in_=smv(EXPT), func=AF.Ln, bias=1.0), r=[sm_b[EXPT]], w=[sm_b[DTV]])
            kb.op(kb.dve, lambda: V.tensor_tensor(smv(ADT), smv(DTV), arep[0:n, :], ALU.mult), r=[sm_b[DTV], arep_b], w=[sm_b[ADT]])
            kb.op(kb.pe, lambda: T.matmul(pacs, self.tri32[0:n, 0:n], smv(ADT), start=True, stop=True), r=[sm_b[ADT], self.cb_], w=[pacs_b])
            kb.op(kb.pe, lambda: T.matmul(ptot, self.ones32[0:n, :], smv(ADT), start=True, stop=True), r=[sm_b[ADT], self.cb_], w=[ptot_b])
            kb.op(kb.dve, lambda: V.tensor_copy(smv(ACS), pacs), r=[pacs_b], w=[sm_b[ACS]])
            kb.op(kb.act, lambda: A.activation(out=smv(DFS), in_=pacs, func=AF.Exp), r=[pacs_b], w=[sm_b[DFS]])
            kb.op(kb.act, lambda: A.activation(out=sm[:, CD, :], in_=ptot, func=AF.Exp), r=[ptot_b], w=[sm_b[CD]])
            kb.op(kb.dve, lambda: V.tensor_tensor(smv(TMP), ptot[0:n, :], smv(ACS), ALU.subtract), r=[ptot_b, sm_b[ACS]], w=[sm_b[TMP]])
            kb.op(kb.act, lambda: A.activation(out=smv(DTE), in_=smv(TMP), func=AF.Exp), r=[sm_b[TMP]], w=[sm_b[DTE]])
            kb.op(kb.dve, lambda: V.tensor_tensor(smv(W2), smv(DTV), smv(DTE), ALU.mult), r=[sm_b[DTV], sm_b[DTE]], w=[sm_b[W2]])
            for g in range(8):
                b2 = g % 2
                hsl = slice(4 * g, 4 * g + 4)
                gsl = slice(g * 256, (g + 1) * 256)
                v3 = lambda ap: ap.rearrange("p (j l) -> p j l", j=4)
                rD = rhsD[b2][0:n, 0:4 * n]
                kb.op(kb.pool, lambda: G.tensor_tensor(v3(rD), bc(sm[0:n, ADT, hsl], [n, 4, n], 2), bc(self.tri32[0:n, 0:n], [n, 4, n], 1), ALU.mult),
                      r=[sm_b[ADT], self.cb_], w=[rhsD_b[b2]])
                kb.op(kb.pe, lambda: T.matmul(dTb[0:n, 0:4 * n], self.ltri32[0:n, 0:n], rD, start=True, stop=True), r=[rhsD_b[b2], self.cb_], w=[dT_b])
                Ev = Ee[b2][0:n, 0:4 * n]
                kb.op(kb.act, lambda: A.activation(out=Ev, in_=dTb[0:n, 0:4 * n], func=AF.Exp), r=[dT_b], w=[E_b[b2]])
                pcb = miscb[0:n, 128:128 + n]
                kb.op(kb.pe, lambda: T.matmul(pcb, xbcT[:, 16 + g, 0:n], xbcT[:, 24 + g, 0:n], start=True, stop=True),
                      r=[xbc_b[16 + g], xbc_b[24 + g]], w=[pcb_b[b2]])
                kb.op(kb.dve, lambda: V.tensor_tensor(CBm[b2][0:n, 0:n], pcb, self.tri32[0:n, 0:n], ALU.mult), r=[pcb_b[b2], self.cb_], w=[CBm_b[b2]])
                MTv = MT[b2][0:n, 0:4 * n]
                kb.op(kb.pool, lambda: G.tensor_tensor(v3(MTv), v3(Ev), bc(CBm[b2][0:n, 0:n], [n, 4, n], 1), ALU.mult),
                      r=[E_b[b2], CBm_b[b2]], w=[MT_b[b2]])
                xs3 = xs_tm[0:n, gsl].rearrange("p (j f) -> p j f", j=4)
                x3 = lambda ap: ap.rearrange("p (j f) -> p j f", j=4)
                kb.op(kb.pool, lambda: G.tensor_tensor(x3(xdt[b2][0:n, :]), xs3, bc(sm[0:n, DTV, hsl], [n, 4, 64], 2), ALU.mult),
                      r=[xs_b, sm_b[DTV]], w=[xdt_b[b2]])
                kb.op(kb.pool, lambda: G.tensor_tensor(x3(xdtd[b2][0:n, :]), xs3, bc(sm[0:n, W2, hsl], [n, 4, 64], 2), ALU.mult),
                      r=[xs_b, sm_b[W2]], w=[xdtd_b[b2]])
                for jj in range(4):
                    kb.op(kb.pe, lambda: T.matmul(yb[0:n, jj * 64:(jj + 1) * 64], MTv[:, jj * n:(jj + 1) * n], xdt[b2][0:n, jj * 64:(jj + 1) * 64], start=True, stop=True),
                          r=[MT_b[b2], xdt_b[b2]], w=[ydg_b] if jj == 0 else [], wp=[ydg_b] if jj else [], inc=(jj == 3))
                kb.op(kb.pe, lambda: T.matmul(yb[0:n, 256:512], xbcT[:, 24 + g, 0:n], Sbf[:, gsl], start=True, stop=True),
                      r=[xbc_b[24 + g], Sbf_b[g]], w=[yof_b])
                t1, t2, t3, tj = [tt[k][b2][0:n, :] for k in range(4)]
                kb.op(kb.dve, lambda: V.tensor_tensor(x3(t1), x3(yb[0:n, 256:512]), bc(sm[0:n, DFS, hsl], [n, 4, 64], 2), ALU.mult),
                      r=[yof_b, sm_b[DFS]], w=[tt_b[0][b2]])
                kb.op(kb.dve, lambda: V.tensor_tensor(t2, yb[0:n, 0:256], t1, ALU.add), r=[ydg_b, tt_b[0][b2]], w=[tt_b[1][b2]])
                kb.op(kb.pool, lambda: G.tensor_tensor(x3(t3), xs3, bc(dsk[0:n, hsl], [n, 4, 64], 2), ALU.mult), r=[xs_b, par_b], w=[tt_b[2][b2]])
                kb.op(kb.pool, lambda: G.tensor_tensor(t2, t2, t3, ALU.add), r=[tt_b[2][b2]], w=[tt_b[1][b2]])
                kb.op(kb.pool, lambda: G.tensor_tensor(t2, t2, sz[0:n, gsl], ALU.mult), r=[sz_b[g]], w=[tt_b[1][b2]])
                kb.op(kb.act, lambda: A.activation(out=tj, in_=t2, func=AF.Square, accum_out=stg[0:n, g, 0:1]), r=[tt_b[1][b2]], w=[tt_b[3][b2], stg_b[g]])
                self.rstd_ops(stg[:, g, :], stg_b[g], n, 0, 1.0 / 256)
                kb.op(kb.dve, lambda: V.tensor_scalar(gn[0:n, gsl], t2, stg[0:n, g, 2:3], None, ALU.mult), r=[tt_b[1][b2], stg_b[g]],
                      w=[gn_b] if g == 0 else [], wp=[gn_b] if g else [])
                kb.op(kb.pe, lambda: T.matmul(miscb[:, 256:512], B_tm[0:n, g * 128:(g + 1) * 128], xdtd[b2][0:n, :], start=True, stop=True),
                      r=[Btm_b, xdtd_b[b2]], w=[pst_b])
                kb.op(kb.pool, lambda: G.tensor_tensor(x3(stmp[b2][:, :]), x3(S32[:, gsl]), bc(sm[:, CD, hsl], [128, 4, 64], 2), ALU.mult),
                      r=[S32_b[g], sm_b[CD]], w=[stmp_b[b2]])
                kb.op(kb.dve, lambda: V.tensor_tensor(S32[:, gsl], stmp[b2][:, :], miscb[:, 256:512], ALU.add), r=[stmp_b[b2], pst_b], w=[S32_b[g]])
                kb.op(kb.act, lambda: A.copy(Sbf[:, gsl], S32[:, gsl]), r=[S32_b[g]], w=[Sbf_b[g]])
            for cc in range(16):
                kb.op(kb.pe, lambda: T.transpose(tpg[:, cc, 0:n], gn[0:n, cc * 128:(cc + 1) * 128], self.ident[0:n, 0:n]),
                      r=[gn_b, self.cb_], w=[tp_b] if cc == 0 else [], wp=[tp_b] if cc else [], inc=(cc == 15))
            kb.op(kb.dve, lambda: V.tensor_tensor(gT[:, :, 0:n], tpg[:, :, 0:n], bc(sng[:, :], [128, 16, n], 2), ALU.mult), r=[tp_b, par_b], w=[gT_b])
            for hh in range(2):
                po = dTb[0:n, :] if hh == 0 else yb[0:n, :]
                pbufs = [dT_b] if hh == 0 else [ydg_b]
                for cc in range(16):
                    kb.op(kb.pe, lambda: T.matmul(po, gT[:, cc, 0:n], w_out[:, cc, hh * 512:(hh + 1) * 512], start=(cc == 0), stop=(cc == 15)),
                          r=[gT_b, wout_b], w=pbufs if cc == 0 else [], wp=pbufs if cc else [], inc=(cc == 15))
                kb.op(kb.dve, lambda: V.tensor_tensor(hs[cur][0:n, hh * 512:(hh + 1) * 512], po, hs[cur][0:n, hh * 512:(hh + 1) * 512], ALU.add),
                      r=pbufs + [hs_b[cur]], wp=[hs_b[cur]])
            dap = self.h_dst(dst, t0, n)
            if dap is not None:
                kb.dma("sp", dap, hs[cur][0:n, :], r=[hs_b[cur]])

    def rstd_ops(self, st, st_b, n, c0, inv_n):
        kb, nc = self.kb, self.nc
        kb.op(kb.act, lambda: nc.scalar.activation(out=st[0:n, c0 + 1:c0 + 2], in_=st[0:n, c0:c0 + 1], func=AF.Ln, bias=EPS, scale=inv_n),
              r=[st_b], wp=[st_b])
        kb.op(kb.act, lambda: nc.scalar.activation(out=st[0:n, c0 + 2:c0 + 3], in_=st[0:n, c0 + 1:c0 + 2], func=AF.Exp, scale=-0.5),
              r=[st_b], wp=[st_b])

    def phase_mla(self, es, j, li, src, dst):
        kb, nc = self.kb, self.nc
        V, A, G, T = nc.vector, nc.scalar, nc.gpsimd, nc.tensor
        oT = self.S(es, "oT", [128, 8, NT], BF16)
        oT_b = Buf()
        w_o = self.S(es, "w_o", [128, 8, D], BF16)
        w_o_b = Buf()
        gq = self.S(es, "gq", [128, 96], F32)
        gk = self.S(es, "gk", [128, 96], F32)
        par_b = Buf()
        self.tri16 = self.S(es, "tri16", [128, 128], BF16)
        kb.dma("pool", self.tri16[:], self.tri_d, wp=[par_b])
        kb.dma("sp", gq[:], self.gq_d[j], wp=[par_b])
        kb.dma("sp", gk[:], self.gk_d[j], wp=[par_b])
        with contextlib.ExitStack() as s1:
            w_in = self.S(s1, "a_win", [128, 8, 672], BF16)
            w_qb = self.S(s1, "a_wqb", [128, 3, 1536], BF16)
            w_kvb = self.S(s1, "a_wkvb", [128, 2, 2048], BF16)
            wb = Buf()
            kb.dma("pool", w_in[:], self.w_mla_in[j].rearrange("(c p) f -> p c f", p=128), wp=[wb])
            kb.dma("pool", w_qb[:], self.w_mla_qb[j].rearrange("(c p) f -> p c f", p=128), wp=[wb])
            kb.dma("pool", w_kvb[:], self.w_mla_kvb[j].rearrange("(c p) f -> p c f", p=128), wp=[wb])
            kb.dma("pool", w_o[:], self.w_mla_out[j].rearrange("(c p) f -> p c f", p=128), wp=[w_o_b])
            qag = self.S(s1, "a_qag", [128, 3], F32)
            kvag = self.S(s1, "a_kvag", [128, 2], F32)
            kb.dma("sp", qag[:], self.qag_d[j], wp=[par_b])
            kb.dma("sp", kvag[:], self.kvag_d[j], wp=[par_b])
            tp_ps = self.P(s1, "a_tp", [128, 512], F32)
            latA = self.P(s1, "a_latA", [128, 512], F32)
            latB = self.P(s1, "a_latB", [128, 512], F32)
            big = self.P(s1, "a_big", [128, 2048], F32)
            tph_ps = self.P(s1, "a_tph", [128, 512], F32)
            latA_b, latB_b, tph_b = PB(), PB(), PB()
            big_b = [PB() for _ in range(4)]
            nsc = self.norm_scratch(s1, "a", tp_ps)
            tp5 = tp_ps[:].bitcast(BF16)[:, 0:640].rearrange("p (c t) -> p c t", c=5)
            tp_b = nsc[7]
            tph = tph_ps[:].bitcast(BF16)[:, 0:1024].rearrange("p (c t) -> p c t", c=8)
            qTv = self.qT_d.rearrange("h p t -> p h t")
            kTv = self.kT_d.rearrange("h p t -> p h t")

            class NS:
                pass

            def mkbufs(ci):
                b = NS()
                nm = lambda x: "a%d_%s" % (ci, x)
                b.hs = self.S(s1, nm("h"), [128, D], F32); b.hs_b = Buf()
                b.cs = self.S(s1, nm("cs"), [128, 2, 16], F32); b.cs_b = Buf()
                b.xnT = self.S(s1, nm("xnT"), [128, 8, 128], BF16); b.xnT_b = Buf()
                b.st = self.S(s1, nm("st"), [128, 8], F32); b.st_b = Buf()
                b.qln = self.S(s1, nm("qln"), [128, 384], BF16)
                b.kvln = self.S(s1, nm("kvln"), [128, 256], BF16)
                b.kpe = self.S(s1, nm("kpe"), [128, 32], F32); b.ln_b = Buf()
                b.sqj = self.S(s1, nm("sqj"), [128, 2048], F32); b.sqj_b = Buf()
                b.qlT = self.S(s1, nm("qlT"), [128, 3, 128], BF16)
                b.kvlT = self.S(s1, nm("kvlT"), [128, 2, 128], BF16); b.lT_b = Buf()
                b.raw = self.S(s1, nm("raw"), [128, 2048], F32); b.raw_b = Buf()
                b.s16 = self.S(s1, nm("s16"), [128, 3, 16], F32); b.s16_b = Buf()
                b.rt = self.S(s1, nm("rt"), [128, 4, 256], F32); b.rt_b = [Buf() for _ in range(4)]
                b.kpg = self.S(s1, nm("kpg"), [128, 2, 32], F32); b.kpg_b = Buf()
                b.qbf = self.S(s1, nm("qbf"), [128, 1536], BF16); b.qbf_b = Buf()
                b.stg = [self.S(s1, nm("stg%d" % i), [96, 16, 128], BF16) for i in range(2)]; b.stg_b = [Buf(), Buf()]
                b.vst = self.S(s1, nm("vst"), [128, 8, 192], BF16); b.vst_b = Buf()
                kb.op(kb.pool, lambda: G.memset(b.vst[:], 1.0), w=[b.vst_b])
                return b

            CH = [mkbufs(0), mkbufs(1)]

            def chain(ti):
                t0, n = TILES[ti]
                b = CH[ti % 2]
                xnT, st, st_b, qln, kvln, kpe, ln_b = b.xnT, b.st, b.st_b, b.qln, b.kvln, b.kpe, b.ln_b
                sqj, sqj_b, qlT, kvlT, lT_b, raw, raw_b = b.sqj, b.sqj_b, b.qlT, b.kvlT, b.lT_b, b.raw, b.raw_b
                s16, s16_b, rt, rt_b, kpg, kpg_b, qbf, qbf_b = b.s16, b.s16_b, b.rt, b.rt_b, b.kpg, b.kpg_b, b.qbf, b.qbf_b
                raw3 = raw[0:n, 0:1536].rearrange("p (h f) -> p h f", h=16)
                sq3 = sqj[0:n, 0:1536].rearrange("p (h f) -> p h f", h=16)
                qb3 = qbf[0:n, :].rearrange("p (h f) -> p h f", h=16)
                kv4 = raw[0:n, :].rearrange("p (h f) -> p h f", h=16)
                sqk = sqj[0:n, 0:1024].rearrange("p (h f) -> p h f", h=16)
                kv5 = raw[0:n, :].rearrange("p (c e f) -> p c e f", c=8, e=2)

                def head_T(sg, dstv):
                    for half in range(2):
                        for hh in range(8):
                            h = half * 8 + hh
                            kb.op(kb.pe, lambda: T.transpose(tph[0:96, hh, 0:n], qbf[0:n, h * 96:(h + 1) * 96], self.ident[0:n, 0:n]),
                                  r=[qbf_b, self.cb_], w=[tph_b] if hh == 0 else [], wp=[tph_b] if hh else [], inc=(hh == 7))
                        kb.op(kb.act, lambda: A.copy(b.stg[sg][0:96, half * 8:(half + 1) * 8, 0:n], tph[0:96, :, 0:n]),
                              r=[tph_b], w=[b.stg_b[sg]] if half == 0 else [], wp=[b.stg_b[sg]] if half else [])
                    kb.dma("sp", dstv[:, :, t0:t0 + n], b.stg[sg][0:96, :, 0:n], r=[b.stg_b[sg]])

                def rope(t1, t2, cosb, sinb, o1, o2, three_d, wb_, rd_bufs):
                    if three_d:
                        a_, b_, c_, d_ = [rt[0:n, i, :].rearrange("p (h f) -> p h f", h=16) for i in range(4)]
                    else:
                        a_, b_, c_, d_ = [rt[0:n, i, 0:16] for i in range(4)]
                    kb.op(kb.dve, lambda: V.tensor_tensor(a_, t1, cosb, ALU.mult), r=rd_bufs, w=[rt_b[0]])
                    kb.op(kb.pool, lambda: G.tensor_tensor(b_, t2, sinb, ALU.mult), r=rd_bufs, w=[rt_b[1]])
                    kb.op(kb.dve, lambda: V.tensor_tensor(c_, t1, sinb, ALU.mult), r=rd_bufs, w=[rt_b[2]])
                    kb.op(kb.pool, lambda: G.tensor_tensor(d_, t2, cosb, ALU.mult), r=rd_bufs, w=[rt_b[3]])
                    kb.op(kb.dve, lambda: V.tensor_tensor(o1, a_, b_, ALU.subtract), r=[rt_b[0], rt_b[1]], wp=[wb_])
                    kb.op(kb.dve, lambda: V.tensor_tensor(o2, c_, d_, ALU.add), r=[rt_b[2], rt_b[3]], wp=[wb_])

                def s1():
                    kb.dma("sp", b.hs[0:n, :], self.h_src(src, t0, n), w=[b.hs_b])
                    kb.dma("sp", b.cs[0:n, 0, :], self.cos_d[t0:t0 + n, :], w=[b.cs_b])
                    kb.dma("sp", b.cs[0:n, 1, :], self.sin_d[t0:t0 + n, :], wp=[b.cs_b])
                    self.norm_T(b.hs_b, b.hs[0:n, :], n, self.lnT[:, li, :], xnT[:, :, 0:n], b.xnT_b, nsc)

                def s2():
                    for c in range(8):
                        kb.op(kb.pe, lambda: T.matmul(latA[0:n, 0:384], xnT[:, c, 0:n], w_in[:, c, 0:384], start=(c == 0), stop=(c == 7)),
                              r=[b.xnT_b, wb], w=[latA_b] if c == 0 else [], wp=[latA_b] if c else [], inc=(c == 7))
                    for c in range(8):
                        kb.op(kb.pe, lambda: T.matmul(latB[0:n, 0:288], xnT[:, c, 0:n], w_in[:, c, 384:672], start=(c == 0), stop=(c == 7)),
                              r=[b.xnT_b, wb], w=[latB_b] if c == 0 else [], wp=[latB_b] if c else [], inc=(c == 7))
                    kb.op(kb.act, lambda: A.activation(out=sqj[0:n, 0:384], in_=latA[0:n, 0:384], func=AF.Square, accum_out=st[0:n, 0:1]),
                          r=[latA_b], w=[sqj_b, st_b])
                    kb.op(kb.act, lambda: A.activation(out=sqj[0:n, 512:768], in_=latB[0:n, 0:256], func=AF.Square, accum_out=st[0:n, 3:4]),
                          r=[latB_b], wp=[sqj_b, st_b])
                    self.rstd_ops(st, st_b, n, 0, 1.0 / 384)
                    self.rstd_ops(st, st_b, n, 3, 1.0 / 256)
                    kb.op(kb.dve, lambda: V.tensor_scalar(qln[0:n, :], latA[0:n, 0:384], st[0:n, 2:3], None, ALU.mult), r=[latA_b, st_b], w=[ln_b])
                    kb.op(kb.dve, lambda: V.tensor_scalar(kvln[0:n, :], latB[0:n, 0:256], st[0:n, 5:6], None, ALU.mult), r=[latB_b, st_b], wp=[ln_b])
                    kb.op(kb.act, lambda: A.copy(kpe[0:n, :], latB[0:n, 256:288]), r=[latB_b], wp=[ln_b])

                def s3():
                    for c in range(5):
                        srcap = qln[0:n, c * 128:(c + 1) * 128] if c < 3 else kvln[0:n, (c - 3) * 128:(c - 2) * 128]
                        kb.op(kb.pe, lambda: T.transpose(tp5[:, c, 0:n], srcap, self.ident[0:n, 0:n]),
                              r=[ln_b, self.cb_], w=[tp_b] if c == 0 else [], wp=[tp_b] if c else [], inc=(c == 4))
                    kb.op(kb.dve, lambda: V.tensor_tensor(qlT[:, :, 0:n], tp5[:, 0:3, 0:n], bc(qag[:, :], [128, 3, n], 2), ALU.mult),
                          r=[tp_b, par_b], w=[lT_b])
                    kb.op(kb.dve, lambda: V.tensor_tensor(kvlT[:, :, 0:n], tp5[:, 3:5, 0:n], bc(kvag[:, :], [128, 2, n], 2), ALU.mult),
                          r=[tp_b, par_b], wp=[lT_b])

                def s4():
                    for ct in range(3):
                        for kc in range(3):
                            kb.op(kb.pe, lambda: T.matmul(big[0:n, ct * 512:(ct + 1) * 512], qlT[:, kc, 0:n], w_qb[:, kc, ct * 512:(ct + 1) * 512],
                                                          start=(kc == 0), stop=(kc == 2)),
                                  r=[lT_b, wb], w=[big_b[ct]] if kc == 0 else [], wp=[big_b[ct]] if kc else [], inc=(kc == 2))
                    for ct in range(3):
                        E_ = kb.act if ct != 1 else kb.dve
                        fn = (lambda: A.copy(raw[0:n, ct * 512:(ct + 1) * 512], big[0:n, ct * 512:(ct + 1) * 512])) if ct != 1 else \
                             (lambda: V.tensor_copy(raw[0:n, ct * 512:(ct + 1) * 512], big[0:n, ct * 512:(ct + 1) * 512]))
                        kb.op(E_, fn, r=[big_b[ct]], w=[raw_b] if ct == 0 else [], wp=[raw_b] if ct else [])

                def s5():
                    kb.op(kb.pool, lambda: G.tensor_tensor(sqj[0:n, 0:1536], raw[0:n, 0:1536], raw[0:n, 0:1536], ALU.mult), r=[raw_b], w=[sqj_b])
                    kb.op(kb.dve, lambda: V.tensor_reduce(s16[0:n, 0, :], sq3, AX.X, ALU.add), r=[sqj_b], w=[s16_b])
                    kb.op(kb.act, lambda: A.activation(out=s16[0:n, 1, :], in_=s16[0:n, 0, :], func=AF.Ln, bias=EPS, scale=1.0 / 96), r=[s16_b], wp=[s16_b])
                    kb.op(kb.act, lambda: A.activation(out=s16[0:n, 2, :], in_=s16[0:n, 1, :], func=AF.Exp, scale=-0.5), r=[s16_b], wp=[s16_b])
                    kb.op(kb.dve, lambda: V.tensor_tensor(raw3, raw3, bc(s16[0:n, 2, :], [n, 16, 96], 2), ALU.mult), r=[s16_b], w=[raw_b])
                    kb.op(kb.pool, lambda: G.tensor_tensor(raw3, raw3, bc(gq[0:n, :], [n, 16, 96], 1), ALU.mult), r=[par_b], w=[raw_b])
                    cosb = bc(b.cs[0:n, 0, :], [n, 16, 16], 1)
                    sinb = bc(b.cs[0:n, 1, :], [n, 16, 16], 1)
                    kb.op(kb.act, lambda: A.copy(qb3[:, :, 0:64], raw3[:, :, 0:64]), r=[raw_b], w=[qbf_b])
                    rope(raw3[:, :, 64:80], raw3[:, :, 80:96], cosb, sinb, qb3[:, :, 64:80], qb3[:, :, 80:96], True, qbf_b, [raw_b, b.cs_b])

                def s6():
                    head_T(0, qTv)

                def s7():
                    for ct in range(4):
                        for kc in range(2):
                            kb.op(kb.pe, lambda: T.matmul(big[0:n, ct * 512:(ct + 1) * 512], kvlT[:, kc, 0:n], w_kvb[:, kc, ct * 512:(ct + 1) * 512],
                                                          start=(kc == 0), stop=(kc == 1)),
                                  r=[lT_b, wb], w=[big_b[ct]] if kc == 0 else [], wp=[big_b[ct]] if kc else [], inc=(kc == 1))
                    for ct in range(4):
                        E_ = kb.act if ct % 2 == 0 else kb.dve
                        fn = (lambda: A.copy(raw[0:n, ct * 512:(ct + 1) * 512], big[0:n, ct * 512:(ct + 1) * 512])) if ct % 2 == 0 else \
                             (lambda: V.tensor_copy(raw[0:n, ct * 512:(ct + 1) * 512], big[0:n, ct * 512:(ct + 1) * 512]))
                        kb.op(E_, fn, r=[big_b[ct]], w=[raw_b] if ct == 0 else [], wp=[raw_b] if ct else [])

                def s8():
                    kb.op(kb.pool, lambda: G.tensor_tensor(sqk, kv4[:, :, 0:64], kv4[:, :, 0:64], ALU.mult), r=[raw_b], w=[sqj_b])
                    kb.op(kb.dve, lambda: V.tensor_reduce(s16[0:n, 0, :], sqk, AX.X, ALU.add), r=[sqj_b], w=[s16_b])
                    kb.op(kb.act, lambda: A.activation(out=kpg[0:n, 1, :], in_=kpe[0:n, :], func=AF.Square, accum_out=st[0:n, 6:7]),
                          r=[ln_b], w=[kpg_b], wp=[st_b])
                    kb.op(kb.dve, lambda: V.tensor_scalar(s16[0:n, 0, :], s16[0:n, 0, :], st[0:n, 6:7], None, ALU.add), r=[st_b, s16_b], wp=[s16_b])
                    kb.op(kb.act, lambda: A.activation(out=s16[0:n, 1, :], in_=s16[0:n, 0, :], func=AF.Ln, bias=EPS, scale=1.0 / 96), r=[s16_b], wp=[s16_b])
                    kb.op(kb.act, lambda: A.activation(out=s16[0:n, 2, :], in_=s16[0:n, 1, :], func=AF.Exp, scale=-0.5), r=[s16_b], wp=[s16_b])
                    kb.op(kb.act, lambda: A.copy(b.vst[0:n, :, 0:64], kv5[:, :, 0, 64:128]), r=[raw_b, b.vst_b], wp=[b.vst_b])
                    kb.op(kb.dve, lambda: V.tensor_copy(b.vst[0:n, :, 128:192], kv5[:, :, 1, 64:128]), r=[raw_b, b.vst_b], wp=[b.vst_b])
                    kb.dma("sp", self.va_d[t0:t0 + n, :, :], b.vst[0:n, :, :], r=[b.vst_b])
                    kb.op(kb.dve, lambda: V.tensor_tensor(kv4[:, :, 0:64], kv4[:, :, 0:64], bc(s16[0:n, 2, :], [n, 16, 64], 2), ALU.mult),
                          r=[s16_b], w=[raw_b])
                    kb.op(kb.pool, lambda: G.tensor_tensor(qb3[:, :, 0:64], kv4[:, :, 0:64], bc(gk[0:n, 0:64], [n, 16, 64], 1), ALU.mult),
                          r=[raw_b, par_b], w=[qbf_b])
                    kb.op(kb.dve, lambda: V.tensor_tensor(kpg[0:n, 0, :], kpe[0:n, :], gk[0:n, 64:96], ALU.mult), r=[ln_b, par_b], w=[kpg_b])
                    rope(kpg[0:n, 0, 0:16], kpg[0:n, 0, 16:32], b.cs[0:n, 0, :], b.cs[0:n, 1, :], kpg[0:n, 1, 0:16], kpg[0:n, 1, 16:32],
                         False, kpg_b, [kpg_b, b.cs_b])
                    kb.op(kb.dve, lambda: V.tensor_tensor(qb3[:, :, 64:96], bc(kpg[0:n, 1, :], [n, 16, 32], 1), bc(s16[0:n, 2, :], [n, 16, 32], 2), ALU.mult),
                          r=[kpg_b, s16_b], wp=[qbf_b])

                def s9():
                    head_T(1, kTv)

                return [s1, s2, s3, s4, s5, s6, s7, s8, s9]

            for k in range(0, len(TILES), 2):
                ca = chain(k)
                cb2 = chain(k + 1) if k + 1 < len(TILES) else []
                for i in range(len(ca)):
                    ca[i]()
                    if i < len(cb2):
                        cb2[i]()
            kb.barrier()
        with contextlib.ExitStack() as s2:
            qh = [self.S(s2, "b_q%d" % i, [96, NT], BF16) for i in range(2)]
            kh = [self.S(s2, "b_k%d" % i, [96, NT], BF16) for i in range(2)]
            qk_b = [Buf(), Buf()]
            va = [self.S(s2, "b_va%d" % i, [128, 33, 192], BF16) for i in range(2)]
            va_b = [Buf(), Buf()]
            NPS = 6
            pT = [self.S(s2, "b_pT%d" % i, [128, 512], BF16) for i in range(NPS)]
            pT_b = [Buf() for _ in range(NPS)]
            rden = [self.S(s2, "b_rd%d" % i, [128, 512], F32) for i in range(2)]
            rdsh = [self.S(s2, "b_rs%d" % i, [128, 512], F32) for i in range(2)]
            rd_b = [Buf(), Buf()]
            rs_b = [Buf(), Buf()]
            bnd = self.S(s2, "b_bnd", [128, 8], F32)
            bnd_b = Buf()
            ps = [self.P(s2, "b_ps%d" % i, [128, 512], F32) for i in range(NPS)]
            ps_b = [PB() for _ in range(NPS)]
            po = [self.P(s2, "b_po%d" % i, [128, 512], F32) for i in range(2)]
            po_b = [PB(), PB()]
            kb.op(kb.dve, lambda: V.tensor_reduce(bnd[:, 0:1], gq[:, :], AX.X, ALU.max), r=[par_b], w=[bnd_b])
            kb.op(kb.dve, lambda: V.tensor_reduce(bnd[:, 1:2], gq[:, :], AX.X, ALU.min), r=[par_b], wp=[bnd_b])
            kb.op(kb.dve, lambda: V.tensor_reduce(bnd[:, 2:3], gk[:, :], AX.X, ALU.max), r=[par_b], wp=[bnd_b])
            kb.op(kb.dve, lambda: V.tensor_reduce(bnd[:, 3:4], gk[:, :], AX.X, ALU.min), r=[par_b], wp=[bnd_b])
            kb.op(kb.dve, lambda: V.scalar_tensor_tensor(bnd[:, 4:5], bnd[:, 1:2], -1.0, bnd[:, 0:1], ALU.mult, ALU.max), r=[bnd_b], wp=[bnd_b])
            kb.op(kb.dve, lambda: V.scalar_tensor_tensor(bnd[:, 5:6], bnd[:, 3:4], -1.0, bnd[:, 2:3], ALU.mult, ALU.max), r=[bnd_b], wp=[bnd_b])
            kb.op(kb.dve, lambda: V.scalar_tensor_tensor(bnd[:, 6:7], bnd[:, 4:5], -float(np.sqrt(96.0)), bnd[:, 5:6], ALU.mult, ALU.mult),
                  r=[bnd_b], wp=[bnd_b])
            negB = bnd[:, 6:7]
            scale = float(96.0 ** -0.5)

            def load_head(h):
                b = h % 2
                kb.dma("sp", qh[b][:, :], self.qT_d[h], w=[qk_b[b]])
                kb.dma("sp", kh[b][:, :], self.kT_d[h], wp=[qk_b[b]])

            def load_pair(c):
                b = c % 2
                kb.dma("sp", va[b][0:16, 0, :], self.va_d[0:16, c, :], w=[va_b[b]])
                vv = self.va_d[16:NT, c, :].rearrange("(j p) w -> p j w", p=128)
                for jj in range(0, 32, 8):
                    kb.dma("sp", va[b][:, 1 + jj:9 + jj, :], vv[:, jj:jj + 8, :], wp=[va_b[b]])

            load_pair(0)
            load_head(0)
            items = []
            for h in range(MLA_H):
                for qi in range(9):
                    if qi == 0:
                        q0, nq = 0, 16
                        kts = [(0, 0, 16, 0, True)]
                    else:
                        q0, nq = 16 + 512 * (qi - 1), 512
                        kts = [(0, 0, 16, 0, False)] + [(kt, 16 + 128 * (kt - 1), 128, 0, False) for kt in range(1, 4 * (qi - 1) + 1)]
                        kts += [(4 * (qi - 1) + 1 + i, 16 + 128 * (4 * (qi - 1) + i), 128, 128 * i, True) for i in range(4)]
                    for idx, kt in enumerate(kts):
                        items.append((h, qi, q0, nq, idx, len(kts)) + kt)
            LA = 3
            NI = len(items)
            for i in range(NI + LA):
                if i < NI:
                    (h, qi, q0, nq, idx, nk_t, kt, k0, nk, qoff, diag) = items[i]
                    if qi == 0 and idx == 0 and h + 1 < MLA_H:
                        load_head(h + 1)
                        if h % 2 == 1:
                            load_pair(h // 2 + 1)
                    hb_ = h % 2
                    nqq = nq - qoff
                    p = i % NPS
                    kb.op(kb.pe, lambda: T.matmul(ps[p][0:nk, 0:nqq], kh[hb_][:, k0:k0 + nk], qh[hb_][:, q0 + qoff:q0 + nq], start=True, stop=True),
                          r=[qk_b[hb_]], w=[ps_b[p]])
                    kb.op(kb.act, lambda: A.activation(out=pT[p][0:nk, 0:nqq], in_=ps[p][0:nk, 0:nqq], func=AF.Exp, bias=negB[0:nk, :], scale=scale),
                          r=[ps_b[p], bnd_b], w=[pT_b[p]])
                    if diag:
                        kb.op(kb.dve, lambda: V.tensor_tensor(pT[p][0:nk, 0:nk], pT[p][0:nk, 0:nk], self.tri16[0:nk, 0:nk], ALU.mult),
                              r=[self.cb_], w=[pT_b[p]])
                ii = i - LA
                if ii >= 0:
                    (h, qi, q0, nq, idx, nk_t, kt, k0, nk, qoff, diag) = items[ii]
                    c, e = h // 2, h % 2
                    vb_ = c % 2
                    dlo, dhi = (0, 64) if e == 0 else (64, 128)
                    nlo, nhi = (64, 128) if e == 0 else (0, 64)
                    nqq = nq - qoff
                    p = ii % NPS
                    pp = (h * 9 + qi) % 2
                    first, last = idx == 0, idx == nk_t - 1
                    kb.op(kb.pe, lambda: T.matmul(po[pp][:, qoff:nq], va[vb_][0:nk, kt, e * 64:e * 64 + 128], pT[p][0:nk, 0:nqq], start=first, stop=last),
                          r=[pT_b[p], va_b[vb_]], w=[po_b[pp]] if first else [], wp=[] if first else [po_b[pp]], inc=last)
                    if last:
                        kb.op(kb.dve, lambda: V.reciprocal(rden[pp][nlo:nhi, 0:nq], po[pp][nlo:nhi, 0:nq]), r=[po_b[pp]], w=[rd_b[pp]])
                        kb.op(kb.dve, lambda: V.tensor_copy(rdsh[pp][dlo:dhi, 0:nq], rden[pp][nlo:nhi, 0:nq]), r=[rd_b[pp]], w=[rs_b[pp]])
                        kb.op(kb.dve, lambda: V.tensor_tensor(oT[dlo:dhi, c, q0:q0 + nq], po[pp][dlo:dhi, 0:nq], rdsh[pp][dlo:dhi, 0:nq], ALU.mult),
                              r=[po_b[pp], rs_b[pp]], wp=[oT_b])
            kb.barrier()
        with contextlib.ExitStack() as s3:
            hs = [self.S(s3, "c_h%d" % i, [128, D], F32) for i in range(3)]
            hs_b = [Buf() for _ in range(3)]
            po = [self.P(s3, "c_po%d" % i, [128, 512], F32) for i in range(4)]
            po_b = [PB() for _ in range(4)]
            ip = 0
            for ti, (t0, n) in enumerate(TILES):
                s = ti % 3
                kb.dma("sp", hs[s][0:n, :], self.h_src(src, t0, n), w=[hs_b[s]])
                for hh in range(2):
                    p = ip % 4
                    ip += 1
                    for c in range(8):
                        kb.op(kb.pe, lambda: T.matmul(po[p][0:n, :], oT[:, c, t0:t0 + n], w_o[:, c, hh * 512:(hh + 1) * 512], start=(c == 0), stop=(c == 7)),
                              r=[oT_b, w_o_b], w=[po_b[p]] if c == 0 else [], wp=[po_b[p]] if c else [], inc=(c == 7))
                    kb.op(kb.dve, lambda: V.tensor_tensor(hs[s][0:n, hh * 512:(hh + 1) * 512], po[p][0:n, :], hs[s][0:n, hh * 512:(hh + 1) * 512], ALU.add),
                          r=[po_b[p], hs_b[s]], wp=[hs_b[s]])
                dap = self.h_dst(dst, t0, n)
                if dap is not None:
                    kb.dma("sp", dap, hs[s][0:n, :], r=[hs_b[s]])


def host_inputs(inp):
    f = lambda a: np.ascontiguousarray(np.asarray(a, dtype=np.float32))
    rep = lambda a: np.ascontiguousarray(np.broadcast_to(np.asarray(a, np.float32)[:, None, :], (a.shape[0], 128, a.shape[1])))
    colT = lambda a, c: np.ascontiguousarray(np.asarray(a, np.float32).reshape(a.shape[0], c, 128).transpose(0, 2, 1))
    ln = np.concatenate([np.asarray(inp["ln_mix"], np.float32), np.asarray(inp["ln_mlp"], np.float32)], 0)
    lnT = np.ascontiguousarray(ln.reshape(8, 8, 128).transpose(2, 0, 1))
    cw = np.asarray(inp["ssd_conv_w"], np.float32)
    cwT = np.ascontiguousarray(cw.reshape(2, 4, 32, 128).transpose(0, 3, 2, 1))
    k = np.arange(128)
    tri = (k[:, None] <= k[None, :]).astype(np.float32)
    inv = 1.0 / (10000.0 ** (np.arange(0, 32, 2, dtype=np.float32) / 32.0))
    ang = np.arange(NT, dtype=np.float32)[:, None] * inv[None, :].astype(np.float32)
    common = {
        "meta": f(inp["meta_tokens"]),
        "ssd_w_in": f(inp["ssd_w_in"]), "ssd_w_out": f(inp["ssd_w_out"]),
        "mla_w_in": f(inp["mla_w_in"]), "mla_w_q_b": f(inp["mla_w_q_b"]), "mla_w_kv_b": f(inp["mla_w_kv_b"]),
        "mla_w_out": f(inp["mla_w_out"]), "mlp_w_up": f(inp["mlp_w_up"]), "mlp_w_down": f(inp["mlp_w_down"]),
        "lnT": lnT, "cw": cwT, "cb": colT(inp["ssd_conv_b"], 32),
        "dtb_rep": rep(inp["ssd_dt_bias"]), "alog_rep": rep(inp["ssd_a_log"]), "dskip_rep": rep(inp["ssd_d"]),
        "ssd_normT": colT(inp["ssd_norm"], 16), "q_a_T": colT(inp["mla_q_a_norm"], 3), "kv_a_T": colT(inp["mla_kv_a_norm"], 2),
        "gq_rep": rep(inp["mla_q_norm"]), "gk_rep": rep(inp["mla_k_norm"]),
        "ident": np.eye(128, dtype=np.float32), "tri": tri, "ltri": np.ascontiguousarray(1.0 - tri),
        "ones": np.ones((128, 128), np.float32),
        "cos": np.cos(ang).astype(np.float32), "sin": np.sin(ang).astype(np.float32),
    }
    return common


FULL_PHASES = [
    ("ssd", 0, 0, "x", "h"), ("mlp", 0, "h", "h"),
    ("mla", 0, 1, "h", "h"), ("mlp", 1, "h", "h"),
    ("ssd", 1, 2, "h", "h"), ("mlp", 2, "h", "h"),
    ("mla", 1, 3, "h", "h"), ("mlp", 3, "h", "y"),
]


def run(inputs, phases, cores=8):
    common = host_inputs(inputs)
    x = np.asarray(inputs["x"], np.float32)
    prog = Prog(phases)
    in_maps = []
    for c in range(cores):
        m = dict(common)
        m["x"] = np.ascontiguousarray(x[c])
        in_maps.append(m)
    res = run_bass_kernel_spmd(prog.nc, in_maps, core_ids=list(range(cores)))
    return np.stack([np.asarray(r["y"]) for r in res.results], 0)


def kernel(**inputs):
    return run(inputs, FULL_PHASES, 8).astype(np.float32)
```

```python
import contextlib
import numpy as np
import concourse.bass as bass
import concourse.mybir as mybir
from concourse.bass_utils import run_bass_kernel_spmd

F32, BF16 = mybir.dt.float32, mybir.dt.bfloat16
AF = mybir.ActivationFunctionType
ALU = mybir.AluOpType
AX = mybir.AxisListType

NT, NM, D, SEQ = 4112, 16, 1024, 4096
TILES = [(0, 16)] + [(16 + 128 * j, 128) for j in range(32)]
EPS = 1e-6
DFF = 4096
SSD_IN = 6176
NH_S = 32
MLA_H = 16
QK = 96


class Buf:
    __slots__ = ("w", "r", "name", "ps")

    def __init__(self, name="", ps=False):
        self.w = {}
        self.r = {}
        self.name = name
        self.ps = ps


def PB():
    return Buf(ps=True)


class Eng:
    def __init__(self, name, e, sem):
        self.name, self.e, self.sem, self.cnt, self.seen = name, e, sem, 0, {}


class KB:
    def __init__(self, nc, es):
        self.nc = nc
        mk = lambda n: es.enter_context(nc.semaphore(n))
        self.pe = Eng("pe", nc.tensor, mk("s_pe"))
        self.act = Eng("act", nc.scalar, mk("s_act"))
        self.dve = Eng("dve", nc.vector, mk("s_dve"))
        self.pool = Eng("pool", nc.gpsimd, mk("s_pool"))
        self.sp = Eng("sp", nc.sync, mk("s_sp"))
        self.engs = [self.pe, self.act, self.dve, self.pool, self.sp]
        self.dsem = {"sp": [[mk("d_sp%d" % i), 0] for i in range(24)],
                     "pool": [[mk("d_pl%d" % i), 0] for i in range(8)]}
        self.drr = {"sp": 0, "pool": 0}
        self.nins = 0

    def _wait(self, E, toks):
        for key, (sem, val) in toks.items():
            if E is self.pe and key == "pe":
                continue
            if E.seen.get(key, 0) >= val:
                continue
            E.e.wait_ge(sem, val)
            E.seen[key] = val

    @staticmethod
    def _add(need, d):
        for k, sv in d.items():
            if k not in need or need[k][1] < sv[1]:
                need[k] = sv

    def _deps(self, r, w, wp, ekey=None):
        need = {}
        for b in r:
            self._add(need, b.w)
            if b.ps:
                self._add(need, {k: v for k, v in b.r.items() if k != ekey})
        for b in w:
            self._add(need, b.w)
            self._add(need, b.r)
        for b in wp:
            self._add(need, b.r)
            if b.ps:
                self._add(need, b.w)
        return need

    def _reg(self, key, tok, r, w, wp):
        for b in r:
            if key not in b.r or b.r[key][1] < tok[1]:
                b.r[key] = tok
        for b in w:
            b.w = {key: tok}
            b.r = {}
        for b in wp:
            if key not in b.w or b.w[key][1] < tok[1]:
                b.w[key] = tok

    def op(self, E, fn, r=(), w=(), wp=(), inc=True):
        self._wait(E, self._deps(r, w, wp, E.name))
        ins = fn()
        self.nins += 1
        if inc:
            E.cnt += 1
            ins.then_inc(E.sem, 1)
            tok = (E.sem, E.cnt)
        else:
            tok = (E.sem, E.cnt + 1)
        self._reg(E.name, tok, r, w, wp)

    def dma(self, Q, out, in_, r=(), w=(), wp=()):
        E = self.sp if Q == "sp" else self.pool
        self._wait(E, self._deps(r, w, wp))
        lst = self.dsem[Q]
        i = self.drr[Q]
        self.drr[Q] = (i + 1) % len(lst)
        sem, cnt = lst[i]
        key = (Q, i)
        if cnt > 0 and E.seen.get(key, 0) < cnt:
            E.e.wait_ge(sem, cnt)
            E.seen[key] = cnt
        ins = E.e.dma_start(out=out, in_=in_)
        ins.then_inc(sem, 16)
        self.nins += 1
        lst[i][1] = cnt + 16
        self._reg(key, (sem, cnt + 16), r, w, wp)

    def barrier(self):
        toks = {}
        for E in self.engs:
            if E.cnt > 0:
                toks[E.name] = (E.sem, E.cnt)
        for Q, lst in self.dsem.items():
            for i, (sem, cnt) in enumerate(lst):
                if cnt > 0:
                    toks[(Q, i)] = (sem, cnt)
        for E in self.engs:
            self._wait(E, toks)


def bc(ap, shape, axis):
    return ap.unsqueeze(axis).to_broadcast(list(shape))


class Prog:
    def __init__(self, phases):
        self.phases = phases
        nc = bass.Bass("TRN2", target_bir_lowering=False)
        self.nc = nc
        di = lambda name, shape: nc.dram_tensor(name, list(shape), F32, kind="ExternalInput").ap()
        self.x = di("x", [SEQ, D])
        self.meta = di("meta", [NM, D])
        self.w_ssd_in = di("ssd_w_in", [2, D, SSD_IN])
        self.w_ssd_out = di("ssd_w_out", [2, 2048, D])
        self.w_mla_in = di("mla_w_in", [2, D, 672])
        self.w_mla_qb = di("mla_w_q_b", [2, 384, 1536])
        self.w_mla_kvb = di("mla_w_kv_b", [2, 256, 2048])
        self.w_mla_out = di("mla_w_out", [2, D, D])
        self.w_up = di("mlp_w_up", [4, D, DFF])
        self.w_dn = di("mlp_w_down", [4, DFF, D])
        self.lnT_d = di("lnT", [128, 8, 8])
        self.cw_d = di("cw", [2, 128, 32, 4])
        self.cb_d = di("cb", [2, 128, 32])
        self.dtb_d = di("dtb_rep", [2, 128, 32])
        self.alog_d = di("alog_rep", [2, 128, 32])
        self.dsk_d = di("dskip_rep", [2, 128, 32])
        self.sng_d = di("ssd_normT", [2, 128, 16])
        self.qag_d = di("q_a_T", [2, 128, 3])
        self.kvag_d = di("kv_a_T", [2, 128, 2])
        self.gq_d = di("gq_rep", [2, 128, 96])
        self.gk_d = di("gk_rep", [2, 128, 96])
        self.ident_d = di("ident", [128, 128])
        self.tri_d = di("tri", [128, 128])
        self.ltri_d = di("ltri", [128, 128])
        self.ones_d = di("ones", [128, 128])
        self.cos_d = di("cos", [NT, 16])
        self.sin_d = di("sin", [NT, 16])
        self.y = nc.dram_tensor("y", [SEQ, D], F32, kind="ExternalOutput").ap()
        self.hd = nc.dram_tensor("hd", [NT, D], F32, kind="Internal").ap()
        self.qT_d = nc.dram_tensor("qT_d", [MLA_H, QK, NT], BF16, kind="Internal").ap()
        self.kT_d = nc.dram_tensor("kT_d", [MLA_H, QK, NT], BF16, kind="Internal").ap()
        self.va_d = nc.dram_tensor("va_d", [NT, 8, 192], BF16, kind="Internal").ap()

        with contextlib.ExitStack() as es:
            self.kb = KB(nc, es)
            self.build(es)

    def S(self, es, name, shape, dt):
        self.uid = getattr(self, "uid", 0) + 1
        return es.enter_context(self.nc.sbuf_tensor("sb%d_%s" % (self.uid, name), list(shape), dt))

    def P(self, es, name, shape, dt):
        self.uid = getattr(self, "uid", 0) + 1
        return es.enter_context(self.nc.psum_tensor("ps%d_%s" % (self.uid, name), list(shape), dt))

    def h_src(self, kind, t0, n):
        if kind == "x":
            return self.meta[0:16, :] if t0 == 0 else self.x[t0 - 16:t0 - 16 + n, :]
        return self.hd[t0:t0 + n, :]

    def h_dst(self, kind, t0, n):
        if kind == "y":
            return None if t0 == 0 else self.y[t0 - 16:t0 - 16 + n, :]
        return self.hd[t0:t0 + n, :]

    def build(self, es):
        kb, nc = self.kb, self.nc
        self.ident = self.S(es, "ident", [128, 128], BF16)
        self.tri32 = self.S(es, "tri32", [128, 128], F32)
        self.ltri32 = self.S(es, "ltri32", [128, 128], F32)
        self.ones32 = self.S(es, "ones32", [128, 128], F32)
        self.lnT = self.S(es, "lnT", [128, 8, 8], F32)
        self.cb_ = Buf("consts")
        kb.dma("pool", self.ident[:], self.ident_d, wp=[self.cb_])
        kb.dma("sp", self.tri32[:], self.tri_d, wp=[self.cb_])
        kb.dma("sp", self.ltri32[:], self.ltri_d, wp=[self.cb_])
        kb.dma("sp", self.ones32[:], self.ones_d, wp=[self.cb_])
        kb.dma("sp", self.lnT[:], self.lnT_d, wp=[self.cb_])
        for ph in self.phases:
            kind = ph[0]
            with contextlib.ExitStack() as pes:
                if kind == "mlp":
                    self.phase_mlp(pes, *ph[1:])
                elif kind == "ssd":
                    self.phase_ssd(pes, *ph[1:])
                elif kind == "mla":
                    self.phase_mla(pes, *ph[1:])
                kb.barrier()
        kb.barrier()

    def rstd_newton(self, st, st_b, n, inv_n, eps):
        kb, nc = self.kb, self.nc
        V = nc.vector
        I32 = mybir.dt.int32
        for f in self.rstd_newton_ops(st, st_b, n, inv_n, eps):
            f()

    def rstd_newton_ops(self, st, st_b, n, inv_n, eps):
        kb, nc = self.kb, self.nc
        V = nc.vector
        I32 = mybir.dt.int32
        x, g, t = st[0:n, 1:2], st[0:n, 2:3], st[0:n, 3:4]
        ops = []
        ops.append(lambda: kb.op(kb.dve, lambda: V.tensor_scalar(x, st[0:n, 0:1], inv_n, eps, ALU.mult, ALU.add), r=[st_b], wp=[st_b]))
        ops.append(lambda: kb.op(kb.dve, lambda: V.tensor_scalar(g.bitcast(I32), x.bitcast(I32), 1, None, ALU.arith_shift_right), r=[st_b], wp=[st_b]))
        ops.append(lambda: kb.op(kb.dve, lambda: V.tensor_scalar(g.bitcast(I32), g.bitcast(I32), -1, 0x5f3759df, ALU.mult, ALU.add), r=[st_b], wp=[st_b]))
        for _ in range(2):
            ops.append(lambda: kb.op(kb.dve, lambda: V.scalar_tensor_tensor(t, g, x, g, ALU.mult, ALU.mult), r=[st_b], wp=[st_b]))
            ops.append(lambda: kb.op(kb.dve, lambda: V.tensor_scalar(t, t, -0.5, 1.5, ALU.mult, ALU.add), r=[st_b], wp=[st_b]))
            ops.append(lambda: kb.op(kb.dve, lambda: V.tensor_tensor(g, g, t, ALU.mult), r=[st_b], wp=[st_b]))
        return ops

    def norm_T(self, hb, h_ap, n, gain_ap, dst_ap, dst_b, sc, newton=False):
        kb, nc = self.kb, self.nc
        junk, junk_b, ss, ss_b, xn, xn_b, tp, tp_b = sc
        kb.op(kb.act, lambda: nc.scalar.activation(out=junk[0:n, :], in_=h_ap, func=AF.Square, accum_out=ss[0:n, 0:1]),
              r=[hb], w=[junk_b, ss_b] if junk_b is not xn_b else [xn_b, ss_b])
        if newton:
            self.rstd_newton(ss, ss_b, n, 1.0 / D, EPS)
        else:
            kb.op(kb.act, lambda: nc.scalar.activation(out=ss[0:n, 1:2], in_=ss[0:n, 0:1], func=AF.Ln, bias=EPS, scale=1.0 / D),
                  r=[ss_b], wp=[ss_b])
            kb.op(kb.act, lambda: nc.scalar.activation(out=ss[0:n, 2:3], in_=ss[0:n, 1:2], func=AF.Exp, scale=-0.5),
                  r=[ss_b], wp=[ss_b])
        kb.op(kb.dve, lambda: nc.vector.tensor_scalar(xn[0:n, :], h_ap, ss[0:n, 2:3], None, ALU.mult),
              r=[hb, ss_b], w=[xn_b])
        for c in range(8):
            kb.op(kb.pe, lambda c=c: nc.tensor.transpose(tp[:, c, 0:n], xn[0:n, c * 128:(c + 1) * 128], self.ident[0:n, 0:n]),
                  r=[xn_b, self.cb_], w=[tp_b] if c == 0 else [], wp=[tp_b] if c else [], inc=(c == 7))
        kb.op(kb.dve, lambda: nc.vector.tensor_tensor(dst_ap, tp[:, :, 0:n], bc(gain_ap, [128, 8, n], 2), ALU.mult),
              r=[tp_b, self.cb_], w=[dst_b])

    def norm_scratch(self, es, pfx, tp_ps):
        ss = self.S(es, pfx + "ss", [128, 4], F32)
        xn = self.S(es, pfx + "xn", [128, 1024], BF16)
        tp = tp_ps[:].bitcast(BF16)[:, 0:1024].rearrange("p (c t) -> p c t", c=8)
        xn_b = Buf()
        return (xn, xn_b, ss, Buf(), xn, xn_b, tp, PB())

    def phase_mlp(self, es, li, src, dst):
        kb, nc = self.kb, self.nc
        wup = self.S(es, "wup", [128, 8, DFF], BF16)
        wdn = self.S(es, "wdn", [128, 32, D], BF16)
        wup_b, wdn_b = Buf(), Buf()
        upv = self.w_up[li].rearrange("(c p) f -> p c f", p=128)
        dnv = self.w_dn[li].rearrange("(c p) d -> p c d", p=128)
        for c in range(8):
            kb.dma("pool", wup[:, c, :], upv[:, c, :], wp=[wup_b])
        for c in range(0, 32, 4):
            kb.dma("pool", wdn[:, c:c + 4, :], dnv[:, c:c + 4, :], wp=[wdn_b])
        NSLOT = 7
        hs = [self.S(es, "mh%d" % i, [128, D], F32) for i in range(NSLOT)]
        hs_b = [Buf() for _ in range(NSLOT)]
        xnT = self.S(es, "m_xnT", [128, 8, 512], BF16)
        xnT_b = Buf()
        uT = self.S(es, "m_uT", [128, 32, 512], BF16)
        uT_b = [Buf() for _ in range(32)]
        r32 = [self.S(es, "m_r32_%d" % i, [128, 512], F32) for i in range(2)]
        r32_b = [Buf(), Buf()]
        tp_ps = [self.P(es, "m_tp%d" % i, [128, 512], F32) for i in range(2)]
        pu = [self.P(es, "m_pu%d" % i, [128, 512], F32) for i in range(3)]
        pu_b = [PB() for _ in range(3)]
        pd = [self.P(es, "m_pd%d" % i, [128, 512], F32) for i in range(3)]
        pd_b = [PB() for _ in range(3)]
        nsc = [self.norm_scratch(es, "m%d" % i, tp_ps[i]) for i in range(2)]
        gain = self.lnT[:, 4 + li, :]
        groups = [[TILES[0]]] + [TILES[1 + 4 * g:5 + 4 * g] for g in range(8)]
        slot = 0
        iu = 0
        ipd = 0
        inorm = 0
        for grp in groups:
            ntok = sum(n for _, n in grp)
            myslots = []
            for (t0, n) in grp:
                s = slot % NSLOT
                slot += 1
                myslots.append(s)
                kb.dma("sp", hs[s][0:n, :], self.h_src(src, t0, n), w=[hs_b[s]])
            off = 0
            for (t0, n), s in zip(grp, myslots):
                self.norm_T(hs_b[s], hs[s][0:n, :], n, gain, xnT[:, :, off:off + n], xnT_b, nsc[inorm % 2])
                inorm += 1
                off += n
            for fc in range(32):
                p = iu % 3
                for c in range(8):
                    kb.op(kb.pe, lambda c=c, fc=fc, p=p: nc.tensor.matmul(pu[p][:, 0:ntok], wup[:, c, fc * 128:(fc + 1) * 128],
                                                                      xnT[:, c, 0:ntok], start=(c == 0), stop=(c == 7)),
                          r=[wup_b, xnT_b], w=[pu_b[p]] if c == 0 else [], wp=[pu_b[p]] if c else [], inc=(c == 7))
                rr = iu % 2
                kb.op(kb.act, lambda p=p, rr=rr: nc.scalar.activation(out=r32[rr][:, 0:ntok], in_=pu[p][:, 0:ntok], func=AF.Relu),
                      r=[pu_b[p]], w=[r32_b[rr]])
                E = kb.dve if (fc % 2 == 0) else kb.pool
                kb.op(E, lambda rr=rr, fc=fc, E=E: E.e.tensor_tensor(uT[:, fc, 0:ntok], r32[rr][:, 0:ntok], r32[rr][:, 0:ntok], ALU.mult),
                      r=[r32_b[rr]], w=[uT_b[fc]])
                iu += 1
            off = 0
            for (t0, n), s in zip(grp, myslots):
                for hh in range(2):
                    p = ipd % 3
                    ipd += 1
                    for fc in range(32):
                        kb.op(kb.pe, lambda fc=fc, p=p, off=off, n=n, hh=hh: nc.tensor.matmul(
                            pd[p][0:n, :], uT[:, fc, off:off + n], wdn[:, fc, hh * 512:(hh + 1) * 512],
                            start=(fc == 0), stop=(fc == 31)),
                            r=[uT_b[fc], wdn_b], w=[pd_b[p]] if fc == 0 else [], wp=[pd_b[p]] if fc else [], inc=(fc == 31))
                    kb.op(kb.dve, lambda p=p, n=n, hh=hh, s=s: nc.vector.tensor_tensor(
                        hs[s][0:n, hh * 512:(hh + 1) * 512], pd[p][0:n, :], hs[s][0:n, hh * 512:(hh + 1) * 512], ALU.add),
                        r=[pd_b[p], hs_b[s]], wp=[hs_b[s]])
                dst_ap = self.h_dst(dst, t0, n)
                if dst_ap is not None:
                    kb.dma("sp", dst_ap, hs[s][0:n, :], r=[hs_b[s]])
                off += n

    def phase_ssd(self, es, j, li, src, dst):
        from functools import partial
        kb, nc = self.kb, self.nc
        V, A, G, T = nc.vector, nc.scalar, nc.gpsimd, nc.tensor
        w_in = self.S(es, "s_win", [128, 8, SSD_IN], BF16)
        w_out = self.S(es, "s_wout", [128, 16, D], BF16)
        win_b, wout_b, par_b = Buf(), Buf(), Buf()
        wv = self.w_ssd_in[j].rearrange("(c p) f -> p c f", p=128)
        for c in range(8):
            kb.dma("pool", w_in[:, c, :], wv[:, c, :], wp=[win_b])
        wov = self.w_ssd_out[j].rearrange("(c p) f -> p c f", p=128)
        for c in range(0, 16, 4):
            kb.dma("pool", w_out[:, c:c + 4, :], wov[:, c:c + 4, :], wp=[wout_b])
        cw = self.S(es, "s_cw", [128, 32, 4], F32)
        cb = self.S(es, "s_cb", [128, 32], F32)
        dtb = self.S(es, "s_dtb", [128, 32], F32)
        arep = self.S(es, "s_arep", [128, 32], F32)
        dsk = self.S(es, "s_dsk", [128, 32], F32)
        sng = self.S(es, "s_sng", [128, 16], F32)
        kb.dma("sp", cw[:], self.cw_d[j], wp=[par_b])
        kb.dma("sp", cb[:], self.cb_d[j], wp=[par_b])
        kb.dma("sp", dtb[:], self.dtb_d[j], wp=[par_b])
        kb.dma("sp", arep[:], self.alog_d[j], wp=[par_b])
        kb.dma("sp", dsk[:], self.dsk_d[j], wp=[par_b])
        kb.dma("sp", sng[:], self.sng_d[j], wp=[par_b])
        arep_b = Buf()
        kb.op(kb.act, lambda: A.activation(out=arep[:], in_=arep[:], func=AF.Exp), r=[par_b], w=[arep_b])
        kb.op(kb.dve, lambda: V.tensor_scalar(arep[:], arep[:], -1.0, None, ALU.mult), r=[arep_b], w=[arep_b])
        cwb = Buf()
        kb.op(kb.dve, lambda: V.tensor_scalar(cw[:], cw[:], 0.5, None, ALU.mult), r=[par_b], w=[cwb])
        kb.op(kb.dve, lambda: V.tensor_scalar(cb[:], cb[:], 0.5, None, ALU.mult), r=[par_b], wp=[cwb])
        hs = [self.S(es, "s_h%d" % i, [128, D], F32) for i in range(2)]
        hs_b = [Buf(), Buf()]
        tpF = self.P(es, "s_tpF", [128, 512], F32)
        pzs = [self.P(es, "s_pz%d" % i, [128, 512], F32) for i in range(2)]
        fm = self.P(es, "s_fm", [128, 512], F32)
        tpB = self.P(es, "s_tpB", [128, 512], F32)
        dTb = self.P(es, "s_dT", [128, 512], F32)
        bm = self.P(es, "s_bm", [128, 512], F32)
        yb = self.P(es, "s_y", [128, 512], F32)
        pz_b = [PB(), PB()]
        fm_b, tpB_b, dT_b, bm_b, y_b = PB(), PB(), PB(), PB(), PB()
        gT = self.S(es, "s_gT", [128, 16, 128], BF16)
        gT_b = Buf()
        _ss = self.S(es, "s_nss", [128, 4], F32)
        _xn = gT[:, 0:8, :].rearrange("p c t -> p (c t)")
        _tp = tpF[:].bitcast(BF16)[:, 0:1024].rearrange("p (c t) -> p c t", c=8)
        nsc = (_xn, gT_b, _ss, Buf(), _xn, gT_b, _tp, PB())
        tpB16 = tpB[:].bitcast(BF16)
        tpBg = tpB16.rearrange("p (c t) -> p c t", c=8)
        xnT = [self.S(es, "s_xnT%d" % i, [128, 8, 131], BF16) for i in range(2)]
        xnT_b = [Buf(), Buf()]
        for i in range(2):
            kb.op(kb.pool, lambda: G.memset(xnT[i][:], 0.0), w=[xnT_b[i]])
        xbc_xs = self.S(es, "s_xbcxs", [128, 16, 128], BF16)
        xsT_b = [Buf() for _ in range(16)]
        xbc_bc = [self.S(es, "s_xbcbc%d" % i, [128, 16, 128], BF16) for i in range(2)]
        bc_b = [[Buf() for _ in range(16)] for _ in range(2)]
        NU, NA = 4, 7
        u = [self.S(es, "s_u%d" % i, [128, 131], F32) for i in range(NU)]
        u_b = [Buf() for _ in range(NU)]
        acc = [self.S(es, "s_acc%d" % i, [128, 128], F32) for i in range(NA)]
        acc_b = [Buf() for _ in range(NA)]
        th = [self.S(es, "s_th%d" % i, [128, 128], F32) for i in range(2)]
        th_b = [Buf(), Buf()]
        xs_tm = self.S(es, "s_xstm", [128, 2048], BF16)
        xs_b = [Buf() for _ in range(8)]
        B_tm = self.S(es, "s_Btm", [128, 1024], BF16)
        Btm_b = Buf()
        smF = self.S(es, "s_smF", [128, 4, 32], F32)
        EXPT, ACS, TMP, DTE = range(4)
        smF_b = [Buf() for _ in range(4)]
        smB = [self.S(es, "s_smB%d" % i, [128, 5, 32], F32) for i in range(2)]
        DTV, ADT, DFS, CD, W2 = range(5)
        smB_b = [[Buf() for _ in range(5)] for _ in range(2)]
        rhsD = [self.S(es, "s_rhsD%d" % i, [128, 512], F32) for i in range(2)]
        rhsD_b = [Buf(), Buf()]
        Ee = [self.S(es, "s_E%d" % i, [128, 512], F32) for i in range(2)]
        E_b = [Buf(), Buf()]
        CBm = [self.S(es, "s_CBm0", [128, 128], F32)] * 2
        _cb = Buf()
        CBm_b = [_cb, _cb]
        MT = [self.S(es, "s_MT%d" % i, [128, 512], BF16) for i in range(2)]
        MT_b = [Buf(), Buf()]
        xdt = [self.S(es, "s_xdt%d" % i, [128, 256], BF16) for i in range(2)]
        xdt_b = [Buf(), Buf()]
        xdtd = [self.S(es, "s_xdtd%d" % i, [128, 256], BF16) for i in range(2)]
        xdtd_b = [Buf(), Buf()]
        tt = [[self.S(es, "s_t%d_%d" % (k, i), [128, 256], F32) for i in range(2)] for k in range(4)]
        tt_b = [[Buf(), Buf()] for _ in range(4)]
        tt.append(tt[0])
        tt_b.append(tt_b[0])
        stg = self.S(es, "s_stg", [128, 8, 4], F32)
        stg_b = [Buf() for _ in range(8)]
        S32 = self.S(es, "s_S32", [128, 2048], F32)
        Sbf = self.S(es, "s_Sbf", [128, 2048], BF16)
        S32_b = [Buf() for _ in range(8)]
        Sbf_b = [Buf() for _ in range(8)]
        stmp = [self.S(es, "s_stmp0", [128, 256], F32)] * 2
        _sb = Buf()
        stmp_b = [_sb, _sb]
        kb.op(kb.pool, lambda: G.memset(S32[:], 0.0), w=S32_b)
        kb.op(kb.pool, lambda: G.memset(Sbf[:], 0.0), w=Sbf_b)
        v3 = lambda ap: ap.rearrange("p (j l) -> p j l", j=4)
        NTL = len(TILES)

        def f_load(ti):
            t0, n = TILES[ti]
            cur, prev = ti % 2, (ti + 1) % 2
            nprev = TILES[ti - 1][1] if ti > 0 else 0
            X = xnT[cur]
            kb.dma("sp", hs[cur][0:n, :], self.h_src(src, t0, n), w=[hs_b[cur]])
            self.norm_T(hs_b[cur], hs[cur][0:n, :], n, self.lnT[:, li, :], X[:, :, 3:3 + n], xnT_b[cur], nsc, newton=True)
            kb.op(kb.pool, lambda: G.tensor_copy(X[:, :, 0:3], xnT[prev][:, :, nprev:nprev + 3]), r=[xnT_b[prev]], wp=[xnT_b[cur]])

        def f_dt1(ti):
            t0, n = TILES[ti]
            cur = ti % 2
            X = xnT[cur]
            sB, sBb = smB[cur], smB_b[cur]
            pdt = fm[0:n, 0:32]
            for c in range(8):
                kb.op(kb.pe, lambda: T.matmul(pdt, X[:, c, 3:3 + n], w_in[:, c, 6144:6176], start=(c == 0), stop=(c == 7)),
                      r=[win_b, xnT_b[cur]], w=[fm_b] if c == 0 else [], wp=[fm_b] if c else [], inc=(c == 7))
            kb.op(kb.dve, lambda: V.tensor_tensor(sB[0:n, DTV, :], pdt, dtb[0:n, :], ALU.add), r=[fm_b, par_b], w=[sBb[DTV]])
            kb.op(kb.act, lambda: A.activation(out=smF[0:n, EXPT, :], in_=sB[0:n, DTV, :], func=AF.Exp), r=[sBb[DTV]], w=[smF_b[EXPT]])
            kb.op(kb.act, lambda: A.activation(out=sB[0:n, DTV, :], in_=smF[0:n, EXPT, :], func=AF.Ln, bias=1.0), r=[smF_b[EXPT]], w=[sBb[DTV]])
            kb.op(kb.dve, lambda: V.tensor_tensor(sB[0:n, ADT, :], sB[0:n, DTV, :], arep[0:n, :], ALU.mult), r=[sBb[DTV], arep_b], w=[sBb[ADT]])

        def f_dt2(ti):
            t0, n = TILES[ti]
            cur = ti % 2
            sB, sBb = smB[cur], smB_b[cur]
            pacs, ptot = fm[0:n, 32:64], fm[:, 64:96]
            kb.op(kb.pe, lambda: T.matmul(pacs, self.tri32[0:n, 0:n], sB[0:n, ADT, :], start=True, stop=True), r=[sBb[ADT], self.cb_], w=[fm_b])
            kb.op(kb.pe, lambda: T.matmul(ptot, self.ones32[0:n, :], sB[0:n, ADT, :], start=True, stop=True), r=[sBb[ADT], self.cb_], wp=[fm_b])
            kb.op(kb.dve, lambda: V.tensor_copy(smF[0:n, ACS, :], pacs), r=[fm_b], w=[smF_b[ACS]])
            kb.op(kb.act, lambda: A.activation(out=sB[0:n, DFS, :], in_=pacs, func=AF.Exp), r=[fm_b], w=[sBb[DFS]])
            kb.op(kb.act, lambda: A.activation(out=sB[:, CD, :], in_=ptot, func=AF.Exp), r=[fm_b], w=[sBb[CD]])
            kb.op(kb.dve, lambda: V.tensor_tensor(smF[0:n, TMP, :], ptot[0:n, :], smF[0:n, ACS, :], ALU.subtract), r=[fm_b, smF_b[ACS]], w=[smF_b[TMP]])
            kb.op(kb.act, lambda: A.activation(out=smF[0:n, DTE, :], in_=smF[0:n, TMP, :], func=AF.Exp), r=[smF_b[TMP]], w=[smF_b[DTE]])
            kb.op(kb.dve, lambda: V.tensor_tensor(sB[0:n, W2, :], sB[0:n, DTV, :], smF[0:n, DTE, :], ALU.mult), r=[sBb[DTV], smF_b[DTE]], w=[sBb[W2]])

        def conv_dst(ti, cc, n):
            cur = ti % 2
            if cc < 16:
                return xbc_xs[:, cc, 0:n], xsT_b[cc]
            return xbc_bc[cur][:, cc - 16, 0:n], bc_b[cur][cc - 16]

        def f_conv(ti, s):
            t0, n = TILES[ti]
            cur = ti % 2
            X = xnT[cur]
            cc = s
            if 0 <= cc < 32:
                p = cc % 2
                pz = pzs[p][:, 0:3 + n]
                for c in range(8):
                    kb.op(kb.pe, lambda: T.matmul(pz, w_in[:, c, 2048 + cc * 128:2048 + (cc + 1) * 128], X[:, c, 0:3 + n], start=(c == 0), stop=(c == 7)),
                          r=[win_b, xnT_b[cur]], w=[pz_b[p]] if c == 0 else [], wp=[pz_b[p]] if c else [], inc=(c == 7))
            cc = s - 1
            if 0 <= cc < 32:
                p = cc % 2
                pz = pzs[p][:, 0:3 + n]
                kb.op(kb.act, lambda: A.activation(out=acc[cc % NA][:, 0:n], in_=pz[:, 3:3 + n], func=AF.Identity, bias=cb[:, cc:cc + 1], scale=cw[:, cc, 3:4]),
                      r=[pz_b[p], cwb], w=[acc_b[cc % NA]])
                kb.op(kb.act, lambda: A.copy(u[cc % NU][:, 0:3 + n], pz), r=[pz_b[p]], w=[u_b[cc % NU]])
            for k in range(3):
                cc = s - 2 - k
                if 0 <= cc < 32:
                    p = cc % NU
                    a_ = acc[cc % NA][:, 0:n]
                    kb.op(kb.dve, lambda: V.scalar_tensor_tensor(a_, u[p][:, k:k + n], cw[:, cc, k:k + 1], a_, ALU.mult, ALU.add),
                          r=[u_b[p], cwb], w=[acc_b[cc % NA]])
            cc = s - 5
            if 0 <= cc < 32:
                kb.op(kb.act, lambda: A.activation(out=th[cc % 2][:, 0:n], in_=acc[cc % NA][:, 0:n], func=AF.Tanh), r=[acc_b[cc % NA]], w=[th_b[cc % 2]])
            cc = s - 6
            if 0 <= cc < 32:
                dst_ap, dst_b = conv_dst(ti, cc, n)
                kb.op(kb.dve, lambda: V.scalar_tensor_tensor(dst_ap, th[cc % 2][:, 0:n], 1.0, acc[cc % NA][:, 0:n], ALU.add, ALU.mult),
                      r=[th_b[cc % 2], acc_b[cc % NA]], w=[dst_b])

        def front(ti):
            st = [partial(f_load, ti), partial(f_conv, ti, 0), partial(f_dt1, ti), partial(f_conv, ti, 1), partial(f_dt2, ti)]
            st += [partial(f_conv, ti, s) for s in range(2, 38)]
            return st

        def b_trx(ti, half):
            t0, n = TILES[ti]
            for k in range(8):
                cc = half * 8 + k
                kb.op(kb.pe, lambda: T.transpose(tpB16[0:n, k * 128:(k + 1) * 128], xbc_xs[:, cc, 0:n], self.ident[:, :]),
                      r=[xsT_b[cc], self.cb_], w=[tpB_b] if k == 0 else [], wp=[tpB_b] if k else [], inc=(k == 7))
            kb.op(kb.act, lambda: A.copy(xs_tm[0:n, half * 1024:(half + 1) * 1024], tpB16[0:n, :]), r=[tpB_b], w=xs_b[4 * half:4 * half + 4])

        def b_trB(ti):
            t0, n = TILES[ti]
            cur = ti % 2
            for g in range(8):
                kb.op(kb.pe, lambda: T.transpose(tpB16[0:n, g * 128:(g + 1) * 128], xbc_bc[cur][:, g, 0:n], self.ident[:, :]),
                      r=[bc_b[cur][g], self.cb_], w=[tpB_b] if g == 0 else [], wp=[tpB_b] if g else [], inc=(g == 7))
            kb.op(kb.dve, lambda: V.tensor_copy(B_tm[0:n, :], tpB16[0:n, :]), r=[tpB_b], w=[Btm_b])

        def gchain(ti, g):
            t0, n = TILES[ti]
            cur = ti % 2
            X = xnT[cur]
            sB, sBb = smB[cur], smB_b[cur]
            b2 = g % 2
            hsl = slice(4 * g, 4 * g + 4)
            gsl = slice(g * 256, (g + 1) * 256)
            BT, CT = xbc_bc[cur][:, g, 0:n], xbc_bc[cur][:, 8 + g, 0:n]
            BTb, CTb = bc_b[cur][g], bc_b[cur][8 + g]
            x3 = lambda ap: ap.rearrange("p (j f) -> p j f", j=4)
            rD = rhsD[b2][0:n, 0:4 * n]
            Ev = Ee[b2][0:n, 0:4 * n]
            MTv = MT[b2][0:n, 0:4 * n]
            pcb = bm[0:n, 0:n]
            pst = bm[:, 256:512]
            zq = fm[0:n, 256:512]
            xs3 = x3(xs_tm[0:n, gsl])
            t1, t2, t3, szg, thg = [tt[k][b2][0:n, :] for k in range(5)]
            t1b, t2b, t3b, szb, thb = [tt_b[k][b2] for k in range(5)]
            steps = []

            def s1():
                kb.op(kb.pool, lambda: G.tensor_tensor(v3(rD), bc(sB[0:n, ADT, hsl], [n, 4, n], 2), bc(self.tri32[0:n, 0:n], [n, 4, n], 1), ALU.mult),
                      r=[sBb[ADT], self.cb_], w=[rhsD_b[b2]])
                kb.op(kb.pool, lambda: G.tensor_tensor(x3(xdt[b2][0:n, :]), xs3, bc(sB[0:n, DTV, hsl], [n, 4, 64], 2), ALU.mult),
                      r=[xs_b[g], sBb[DTV]], w=[xdt_b[b2]])
            steps.append(s1)

            def s2():
                kb.op(kb.pe, lambda: T.matmul(dTb[0:n, 0:4 * n], self.ltri32[0:n, 0:n], rD, start=True, stop=True), r=[rhsD_b[b2], self.cb_], w=[dT_b])
                kb.op(kb.act, lambda: A.activation(out=Ev, in_=dTb[0:n, 0:4 * n], func=AF.Exp), r=[dT_b], w=[E_b[b2]])
                kb.op(kb.pool, lambda: G.tensor_tensor(x3(xdtd[b2][0:n, :]), xs3, bc(sB[0:n, W2, hsl], [n, 4, 64], 2), ALU.mult),
                      r=[xs_b[g], sBb[W2]], w=[xdtd_b[b2]])
            steps.append(s2)

            def s3():
                kb.op(kb.pe, lambda: T.matmul(pcb, BT, CT, start=True, stop=True), r=[BTb, CTb], w=[bm_b])
                kb.op(kb.dve, lambda: V.tensor_tensor(CBm[b2][0:n, 0:n], pcb, self.tri32[0:n, 0:n], ALU.mult), r=[bm_b, self.cb_], w=[CBm_b[b2]])
                for c in range(8):
                    kb.op(kb.pe, lambda: T.matmul(zq, X[:, c, 3:3 + n], w_in[:, c, g * 256:(g + 1) * 256], start=(c == 0), stop=(c == 7)),
                          r=[win_b, xnT_b[cur]], w=[fm_b] if c == 0 else [], wp=[fm_b] if c else [], inc=(c == 7))
                kb.op(kb.act, lambda: A.activation(out=thg, in_=zq, func=AF.Tanh, scale=0.5), r=[fm_b], w=[thb])
                kb.op(kb.dve, lambda: V.scalar_tensor_tensor(szg, thg, 1.0, zq, ALU.add, ALU.mult), r=[thb, fm_b], w=[szb])
                kb.op(kb.pool, lambda: G.tensor_tensor(v3(MTv), v3(Ev), bc(CBm[b2][0:n, 0:n], [n, 4, n], 1), ALU.mult),
                      r=[E_b[b2], CBm_b[b2]], w=[MT_b[b2]])
            steps.append(s3)

            def s4():
                for jj in range(4):
                    kb.op(kb.pe, lambda: T.matmul(yb[0:n, jj * 64:(jj + 1) * 64], MTv[:, jj * n:(jj + 1) * n], xdt[b2][0:n, jj * 64:(jj + 1) * 64], start=True, stop=True),
                          r=[MT_b[b2], xdt_b[b2]], w=[y_b] if jj == 0 else [], wp=[y_b] if jj else [], inc=False)
                kb.op(kb.pe, lambda: T.matmul(yb[0:n, 256:512], CT, Sbf[:, gsl], start=True, stop=True), r=[CTb, Sbf_b[g]], wp=[y_b])
                kb.op(kb.pool, lambda: G.tensor_tensor(x3(t3), xs3, bc(dsk[0:n, hsl], [n, 4, 64], 2), ALU.mult), r=[xs_b[g], par_b], w=[t3b])
                kb.op(kb.dve, lambda: V.tensor_tensor(x3(t1), x3(yb[0:n, 256:512]), bc(sB[0:n, DFS, hsl], [n, 4, 64], 2), ALU.mult),
                      r=[y_b, sBb[DFS]], w=[t1b])
                kb.op(kb.dve, lambda: V.tensor_tensor(t2, yb[0:n, 0:256], t1, ALU.add), r=[y_b, t1b], w=[t2b])
            steps.append(s4)

            def s5():
                kb.op(kb.pe, lambda: T.matmul(pst, B_tm[0:n, g * 128:(g + 1) * 128], xdtd[b2][0:n, :], start=True, stop=True),
                      r=[Btm_b, xdtd_b[b2]], w=[bm_b])
                kb.op(kb.pool, lambda: G.tensor_tensor(x3(stmp[b2][:, :]), x3(S32[:, gsl]), bc(sB[:, CD, hsl], [128, 4, 64], 2), ALU.mult),
                      r=[S32_b[g], sBb[CD]], w=[stmp_b[b2]])
                kb.op(kb.dve, lambda: V.tensor_tensor(S32[:, gsl], stmp[b2][:, :], pst, ALU.add), r=[stmp_b[b2], bm_b], w=[S32_b[g]])
                kb.op(kb.act, lambda: A.copy(Sbf[:, gsl], S32[:, gsl]), r=[S32_b[g]], w=[Sbf_b[g]])
            steps.append(s5)

            def s6():
                kb.op(kb.pool, lambda: G.tensor_tensor(t2, t2, t3, ALU.add), r=[t3b], w=[t2b])
                kb.op(kb.pool, lambda: G.tensor_tensor(t2, t2, szg, ALU.mult), r=[szb], w=[t2b])
            steps.append(s6)

            sq = lambda: kb.op(kb.act, lambda: A.activation(out=t1, in_=t2, func=AF.Square, accum_out=stg[0:n, g, 0:1]), r=[t2b], w=[t1b, stg_b[g]])
            newt = self.rstd_newton_ops(stg[:, g, :], stg_b[g], n, 1.0 / 256, 4.0 * EPS)
            gnf = lambda: kb.op(kb.dve, lambda: V.tensor_scalar(xs_tm[0:n, gsl], t2, stg[0:n, g, 2:3], None, ALU.mult), r=[t2b, stg_b[g]], w=[xs_b[g]])
            return steps, (sq, newt, gnf)

        def b_gT(ti, half):
            t0, n = TILES[ti]
            for k in range(8):
                cc = half * 8 + k
                kb.op(kb.pe, lambda: T.transpose(tpBg[:, k, 0:n], xs_tm[0:n, cc * 128:(cc + 1) * 128], self.ident[0:n, 0:n]),
                      r=[xs_b[cc // 2], self.cb_], w=[tpB_b] if k == 0 else [], wp=[tpB_b] if k else [], inc=(k == 7))
            kb.op(kb.dve, lambda: V.tensor_tensor(gT[:, half * 8:(half + 1) * 8, 0:n], tpBg[:, :, 0:n], bc(sng[:, half * 8:(half + 1) * 8], [128, 8, n], 2), ALU.mult),
                  r=[tpB_b, par_b], w=[gT_b] if half == 0 else [], wp=[gT_b] if half else [])

        def b_out(ti, hh):
            t0, n = TILES[ti]
            cur = ti % 2
            po = dTb[0:n, :] if hh == 0 else yb[0:n, :]
            pbuf = dT_b if hh == 0 else y_b
            for cc in range(16):
                kb.op(kb.pe, lambda: T.matmul(po, gT[:, cc, 0:n], w_out[:, cc, hh * 512:(hh + 1) * 512], start=(cc == 0), stop=(cc == 15)),
                      r=[gT_b, wout_b], w=[pbuf] if cc == 0 else [], wp=[pbuf] if cc else [], inc=(cc == 15))
            kb.op(kb.dve, lambda: V.tensor_tensor(hs[cur][0:n, hh * 512:(hh + 1) * 512], po, hs[cur][0:n, hh * 512:(hh + 1) * 512], ALU.add),
                  r=[pbuf, hs_b[cur]], wp=[hs_b[cur]])
            if hh == 1:
                dap = self.h_dst(dst, t0, n)
                if dap is not None:
                    kb.dma("sp", dap, hs[cur][0:n, :], r=[hs_b[cur]])

        def back(ti):
            st = [partial(b_trx, ti, 0), partial(b_trx, ti, 1), partial(b_trB, ti)]
            for gp in range(4):
                (ca, ta), (cb_, tb) = gchain(ti, 2 * gp), gchain(ti, 2 * gp + 1)
                for a_, b_ in zip(ca, cb_):
                    st += [a_, b_]

                def tail(ta=ta, tb=tb):
                    ta[0]()
                    tb[0]()
                    for fa, fb in zip(ta[1], tb[1]):
                        fa()
                        fb()
                    ta[2]()
                    tb[2]()
                st.append(tail)
            st += [partial(b_gT, ti, 0), partial(b_gT, ti, 1), partial(b_out, ti, 0), partial(b_out, ti, 1)]
            return st

        def interleave(a, b):
            na, nb = len(a), len(b)
            i = jx = 0
            while i < na or jx < nb:
                if jx >= nb or (i < na and i * nb <= jx * na):
                    a[i]()
                    i += 1
                else:
                    b[jx]()
                    jx += 1

        for ti in range(NTL + 1):
            f = front(ti) if ti < NTL else []
            b = back(ti - 1) if ti >= 1 else []
            interleave(f, b)

    def phase_ssd_v1(self, es, j, li, src, dst):
        kb, nc = self.kb, self.nc
        V, A, G, T = nc.vector, nc.scalar, nc.gpsimd, nc.tensor
        w_in = self.S(es, "s_win", [128, 8, SSD_IN], BF16)
        w_out = self.S(es, "s_wout", [128, 16, D], BF16)
        win_b, wout_b, par_b = Buf(), Buf(), Buf()
        wv = self.w_ssd_in[j].rearrange("(c p) f -> p c f", p=128)
        for c in range(8):
            kb.dma("pool", w_in[:, c, :], wv[:, c, :], wp=[win_b])
        wov = self.w_ssd_out[j].rearrange("(c p) f -> p c f", p=128)
        for c in range(0, 16, 4):
            kb.dma("pool", w_out[:, c:c + 4, :], wov[:, c:c + 4, :], wp=[wout_b])
        cw = self.S(es, "s_cw", [128, 32, 4], F32)
        cb = self.S(es, "s_cb", [128, 32], F32)
        dtb = self.S(es, "s_dtb", [128, 32], F32)
        arep = self.S(es, "s_arep", [128, 32], F32)
        dsk = self.S(es, "s_dsk", [128, 32], F32)
        sng = self.S(es, "s_sng", [128, 16], F32)
        kb.dma("sp", cw[:], self.cw_d[j], wp=[par_b])
        kb.dma("sp", cb[:], self.cb_d[j], wp=[par_b])
        kb.dma("sp", dtb[:], self.dtb_d[j], wp=[par_b])
        kb.dma("sp", arep[:], self.alog_d[j], wp=[par_b])
        kb.dma("sp", dsk[:], self.dsk_d[j], wp=[par_b])
        kb.dma("sp", sng[:], self.sng_d[j], wp=[par_b])
        arep_b = Buf()
        kb.op(kb.act, lambda: A.activation(out=arep[:], in_=arep[:], func=AF.Exp), r=[par_b], w=[arep_b])
        kb.op(kb.dve, lambda: V.tensor_scalar(arep[:], arep[:], -1.0, None, ALU.mult), r=[arep_b], w=[arep_b])
        hs = [self.S(es, "s_h%d" % i, [128, D], F32) for i in range(2)]
        hs_b = [Buf(), Buf()]
        tp2 = self.P(es, "s_tp2", [128, 1024], F32)
        pzbs = [self.P(es, "s_pz%d" % i, [128, 512], F32) for i in range(2)]
        zqb = self.P(es, "s_zq", [128, 512], F32)
        dTb = self.P(es, "s_dT", [128, 512], F32)
        miscb = self.P(es, "s_misc", [128, 512], F32)
        yb = self.P(es, "s_y", [128, 512], F32)
        pz_b = [PB(), PB()]
        _z = PB()
        zq_b = [_z, _z]
        dT_b = PB()
        misc_b = PB()
        pdt_b = pacs_b = ptot_b = pst_b = misc_b
        pcb_b = [misc_b, misc_b]
        ydg_b = yof_b = PB()
        nsc = self.norm_scratch(es, "s", tp2)
        tp_b = nsc[7]
        tp16 = tp2[:].bitcast(BF16)
        tpg = tp16.rearrange("p (c t) -> p c t", c=16)
        xnT = [self.S(es, "s_xnT%d" % i, [128, 8, 131], BF16) for i in range(2)]
        xnT_b = [Buf(), Buf()]
        for i in range(2):
            kb.op(kb.pool, lambda: G.memset(xnT[i][:], 0.0), w=[xnT_b[i]])
        xbcT = self.S(es, "s_xbcT", [128, 32, 128], BF16)
        xbc_b = [Buf() for _ in range(32)]
        acc = [self.S(es, "s_acc%d" % i, [128, 128], F32) for i in range(3)]
        acc_b = [Buf() for _ in range(3)]
        xs_tm = self.S(es, "s_xstm", [128, 2048], BF16)
        xs_b = Buf()
        B_tm = self.S(es, "s_Btm", [128, 1024], BF16)
        Btm_b = Buf()
        sz = self.S(es, "s_sz", [128, 2048], F32)
        sz_b = [Buf() for _ in range(8)]
        sm = self.S(es, "s_sm", [128, 10, 32], F32)
        DTV, EXPT, ADT, ACS, DFS, CD, DTE, TMP, W2 = range(9)
        sm_b = [Buf() for _ in range(10)]
        rhsD = [self.S(es, "s_rhsD0", [128, 512], F32)] * 2
        _b = Buf()
        rhsD_b = [_b, _b]
        Ee = [self.S(es, "s_E0", [128, 512], F32)] * 2
        _b = Buf()
        E_b = [_b, _b]
        CBm = [self.S(es, "s_CBm%d" % i, [128, 128], F32) for i in range(2)]
        CBm_b = [Buf(), Buf()]
        MT = [self.S(es, "s_MT%d" % i, [128, 512], BF16) for i in range(2)]
        MT_b = [Buf(), Buf()]
        xdt = [self.S(es, "s_xdt%d" % i, [128, 256], BF16) for i in range(2)]
        xdt_b = [Buf(), Buf()]
        xdtd = [self.S(es, "s_xdtd%d" % i, [128, 256], BF16) for i in range(2)]
        xdtd_b = [Buf(), Buf()]
        tt = [[self.S(es, "s_t%d_%d" % (k, i), [128, 256], F32) for i in range(2)] for k in range(3)]
        tt_b = [[Buf(), Buf()] for _ in range(3)]
        tt.append(tt[0])
        tt_b.append(tt_b[0])
        stg = self.S(es, "s_stg", [128, 8, 4], F32)
        stg_b = [Buf() for _ in range(8)]
        gn = self.S(es, "s_gn", [128, 2048], BF16)
        gn_b = Buf()
        gT = self.S(es, "s_gT", [128, 16, 128], BF16)
        gT_b = Buf()
        S32 = self.S(es, "s_S32", [128, 2048], F32)
        Sbf = self.S(es, "s_Sbf", [128, 2048], BF16)
        S32_b = [Buf() for _ in range(8)]
        Sbf_b = [Buf() for _ in range(8)]
        stmp = [self.S(es, "s_stmp0", [128, 256], F32)] * 2
        _b = Buf()
        stmp_b = [_b, _b]
        kb.op(kb.pool, lambda: G.memset(S32[:], 0.0), w=S32_b)
        kb.op(kb.pool, lambda: G.memset(Sbf[:], 0.0), w=Sbf_b)
        nprev = 0
        ipz = 0
        for ti, (t0, n) in enumerate(TILES):
            cur, prev = ti % 2, (ti + 1) % 2
            X = xnT[cur]
            kb.dma("sp", hs[cur][0:n, :], self.h_src(src, t0, n), w=[hs_b[cur]])
            self.norm_T(hs_b[cur], hs[cur][0:n, :], n, self.lnT[:, li, :], X[:, :, 3:3 + n], xnT_b[cur], nsc)
            kb.op(kb.pool, lambda: G.tensor_copy(X[:, :, 0:3], xnT[prev][:, :, nprev:nprev + 3]), r=[xnT_b[prev]], wp=[xnT_b[cur]])
            nprev = n
            for cc in range(32):
                p = ipz % 2
                ipz += 1
                pz = pzbs[p][:, 0:3 + n]
                for c in range(8):
                    kb.op(kb.pe, lambda: T.matmul(pz, w_in[:, c, 2048 + cc * 128:2048 + (cc + 1) * 128], X[:, c, 0:3 + n], start=(c == 0), stop=(c == 7)),
                          r=[win_b, xnT_b[cur]], w=[pz_b[p]] if c == 0 else [], wp=[pz_b[p]] if c else [], inc=(c == 7))
                a_ = acc[p][:, 0:n]
                kb.op(kb.act, lambda: A.activation(out=a_, in_=pz[:, 3:3 + n], func=AF.Identity, bias=cb[:, cc:cc + 1], scale=cw[:, cc, 3:4]),
                      r=[pz_b[p], par_b], w=[acc_b[p]])
                for k in range(3):
                    kb.op(kb.dve, lambda: V.scalar_tensor_tensor(a_, pz[:, k:k + n], cw[:, cc, k:k + 1], a_, ALU.mult, ALU.add),
                          r=[pz_b[p], par_b], w=[acc_b[p]])
                kb.op(kb.act, lambda: A.activation(out=xbcT[:, cc, 0:n], in_=a_, func=AF.Silu), r=[acc_b[p]], w=[xbc_b[cc]])
            for g in range(8):
                zs = g % 2
                zq = zqb[0:n, zs * 256:(zs + 1) * 256]
                for c in range(8):
                    kb.op(kb.pe, lambda: T.matmul(zq, X[:, c, 3:3 + n], w_in[:, c, g * 256:(g + 1) * 256], start=(c == 0), stop=(c == 7)),
                          r=[win_b, xnT_b[cur]], w=[zq_b[zs]] if c == 0 else [], wp=[zq_b[zs]] if c else [], inc=(c == 7))
                kb.op(kb.act, lambda: A.activation(out=sz[0:n, g * 256:(g + 1) * 256], in_=zq, func=AF.Silu), r=[zq_b[zs]], w=[sz_b[g]])
            for cc in range(16):
                kb.op(kb.pe, lambda: T.transpose(tp16[0:n, cc * 128:(cc + 1) * 128], xbcT[:, cc, 0:n], self.ident[:, :]),
                      r=[xbc_b[cc], self.cb_], w=[tp_b] if cc == 0 else [], wp=[tp_b] if cc else [], inc=(cc == 15))
            kb.op(kb.act, lambda: A.copy(xs_tm[0:n, 0:1024], tp16[0:n, 0:1024]), r=[tp_b], w=[xs_b])
            kb.op(kb.dve, lambda: V.tensor_copy(xs_tm[0:n, 1024:2048], tp16[0:n, 1024:2048]), r=[tp_b], wp=[xs_b])
            for g in range(8):
                kb.op(kb.pe, lambda: T.transpose(tp16[0:n, g * 128:(g + 1) * 128], xbcT[:, 16 + g, 0:n], self.ident[:, :]),
                      r=[xbc_b[16 + g], self.cb_], w=[tp_b] if g == 0 else [], wp=[tp_b] if g else [], inc=(g == 7))
            kb.op(kb.act, lambda: A.copy(B_tm[0:n, :], tp16[0:n, 0:1024]), r=[tp_b], w=[Btm_b])
            pdt, pacs, ptot = miscb[0:n, 0:32], miscb[0:n, 32:64], miscb[:, 64:96]
            for c in range(8):
                kb.op(kb.pe, lambda: T.matmul(pdt, X[:, c, 3:3 + n], w_in[:, c, 6144:6176], start=(c == 0), stop=(c == 7)),
                      r=[win_b, xnT_b[cur]], w=[pdt_b] if c == 0 else [], wp=[pdt_b] if c else [], inc=(c == 7))
            smv = lambda k: sm[0:n, k, :]
            kb.op(kb.dve, lambda: V.tensor_tensor(smv(DTV), pdt, dtb[0:n, :], ALU.add), r=[pdt_b, par_b], w=[sm_b[DTV]])
            kb.op(kb.act, lambda: A.activation(out=smv(EXPT), in_=smv(DTV), func=AF.Exp), r=[sm_b[DTV]], w=[sm_b[EXPT]])
            kb.op(kb.act, lambda: A.activation(out=smv(DTV), in_=smv(EXPT), func=AF.Ln, bias=1.0), r=[sm_b[EXPT]], w=[sm_b[DTV]])
            kb.op(kb.dve, lambda: V.tensor_tensor(smv(ADT), smv(DTV), arep[0:n, :], ALU.mult), r=[sm_b[DTV], arep_b], w=[sm_b[ADT]])
            kb.op(kb.pe, lambda: T.matmul(pacs, self.tri32[0:n, 0:n], smv(ADT), start=True, stop=True), r=[sm_b[ADT], self.cb_], w=[pacs_b])
            kb.op(kb.pe, lambda: T.matmul(ptot, self.ones32[0:n, :], smv(ADT), start=True, stop=True), r=[sm_b[ADT], self.cb_], w=[ptot_b])
            kb.op(kb.dve, lambda: V.tensor_copy(smv(ACS), pacs), r=[pacs_b], w=[sm_b[ACS]])
            kb.op(kb.act, lambda: A.activation(out=smv(DFS), in_=pacs, func=AF.Exp), r=[pacs_b], w=[sm_b[DFS]])
            kb.op(kb.act, lambda: A.activation(out=sm[:, CD, :], in_=ptot, func=AF.Exp), r=[ptot_b], w=[sm_b[CD]])
            kb.op(kb.dve, lambda: V.tensor_tensor(smv(TMP), ptot[0:n, :], smv(ACS), ALU.subtract), r=[ptot_b, sm_b[ACS]], w=[sm_b[TMP]])
            kb.op(kb.act, lambda: A.activation(out=smv(DTE), in_=smv(TMP), func=AF.Exp), r=[sm_b[TMP]], w=[sm_b[DTE]])
            kb.op(kb.dve, lambda: V.tensor_tensor(smv(W2), smv(DTV), smv(DTE), ALU.mult), r=[sm_b[DTV], sm_b[DTE]], w=[sm_b[W2]])
            for g in range(8):
                b2 = g % 2
                hsl = slice(4 * g, 4 * g + 4)
                gsl = slice(g * 256, (g + 1) * 256)
                v3 = lambda ap: ap.rearrange("p (j l) -> p j l", j=4)
                rD = rhsD[b2][0:n, 0:4 * n]
                kb.op(kb.pool, lambda: G.tensor_tensor(v3(rD), bc(sm[0:n, ADT, hsl], [n, 4, n], 2), bc(self.tri32[0:n, 0:n], [n, 4, n], 1), ALU.mult),
                      r=[sm_b[ADT], self.cb_], w=[rhsD_b[b2]])
                kb.op(kb.pe, lambda: T.matmul(dTb[0:n, 0:4 * n], self.ltri32[0:n, 0:n], rD, start=True, stop=True), r=[rhsD_b[b2], self.cb_], w=[dT_b])
                Ev = Ee[b2][0:n, 0:4 * n]
                kb.op(kb.act, lambda: A.activation(out=Ev, in_=dTb[0:n, 0:4 * n], func=AF.Exp), r=[dT_b], w=[E_b[b2]])
                pcb = miscb[0:n, 128:128 + n]
                kb.op(kb.pe, lambda: T.matmul(pcb, xbcT[:, 16 + g, 0:n], xbcT[:, 24 + g, 0:n], start=True, stop=True),
                      r=[xbc_b[16 + g], xbc_b[24 + g]], w=[pcb_b[b2]])
                kb.op(kb.dve, lambda: V.tensor_tensor(CBm[b2][0:n, 0:n], pcb, self.tri32[0:n, 0:n], ALU.mult), r=[pcb_b[b2], self.cb_], w=[CBm_b[b2]])
                MTv = MT[b2][0:n, 0:4 * n]
                kb.op(kb.pool, lambda: G.tensor_tensor(v3(MTv), v3(Ev), bc(CBm[b2][0:n, 0:n], [n, 4, n], 1), ALU.mult),
                      r=[E_b[b2], CBm_b[b2]], w=[MT_b[b2]])
                xs3 = xs_tm[0:n, gsl].rearrange("p (j f) -> p j f", j=4)
                x3 = lambda ap: ap.rearrange("p (j f) -> p j f", j=4)
                kb.op(kb.pool, lambda: G.tensor_tensor(x3(xdt[b2][0:n, :]), xs3, bc(sm[0:n, DTV, hsl], [n, 4, 64], 2), ALU.mult),
                      r=[xs_b, sm_b[DTV]], w=[xdt_b[b2]])
                kb.op(kb.pool, lambda: G.tensor_tensor(x3(xdtd[b2][0:n, :]), xs3, bc(sm[0:n, W2, hsl], [n, 4, 64], 2), ALU.mult),
                      r=[xs_b, sm_b[W2]], w=[xdtd_b[b2]])
                for jj in range(4):
                    kb.op(kb.pe, lambda: T.matmul(yb[0:n, jj * 64:(jj + 1) * 64], MTv[:, jj * n:(jj + 1) * n], xdt[b2][0:n, jj * 64:(jj + 1) * 64], start=True, stop=True),
                          r=[MT_b[b2], xdt_b[b2]], w=[ydg_b] if jj == 0 else [], wp=[ydg_b] if jj else [], inc=(jj == 3))
                kb.op(kb.pe, lambda: T.matmul(yb[0:n, 256:512], xbcT[:, 24 + g, 0:n], Sbf[:, gsl], start=True, stop=True),
                      r=[xbc_b[24 + g], Sbf_b[g]], w=[yof_b])
                t1, t2, t3, tj = [tt[k][b2][0:n, :] for k in range(4)]
                kb.op(kb.dve, lambda: V.tensor_tensor(x3(t1), x3(yb[0:n, 256:512]), bc(sm[0:n, DFS, hsl], [n, 4, 64], 2), ALU.mult),
                      r=[yof_b, sm_b[DFS]], w=[tt_b[0][b2]])
                kb.op(kb.dve, lambda: V.tensor_tensor(t2, yb[0:n, 0:256], t1, ALU.add), r=[ydg_b, tt_b[0][b2]], w=[tt_b[1][b2]])
                kb.op(kb.pool, lambda: G.tensor_tensor(x3(t3), xs3, bc(dsk[0:n, hsl], [n, 4, 64], 2), ALU.mult), r=[xs_b, par_b], w=[tt_b[2][b2]])
                kb.op(kb.pool, lambda: G.tensor_tensor(t2, t2, t3, ALU.add), r=[tt_b[2][b2]], w=[tt_b[1][b2]])
                kb.op(kb.pool, lambda: G.tensor_tensor(t2, t2, sz[0:n, gsl], ALU.mult), r=[sz_b[g]], w=[tt_b[1][b2]])
                kb.op(kb.act, lambda: A.activation(out=tj, in_=t2, func=AF.Square, accum_out=stg[0:n, g, 0:1]), r=[tt_b[1][b2]], w=[tt_b[3][b2], stg_b[g]])
                self.rstd_ops(stg[:, g, :], stg_b[g], n, 0, 1.0 / 256)
                kb.op(kb.dve, lambda: V.tensor_scalar(gn[0:n, gsl], t2, stg[0:n, g, 2:3], None, ALU.mult), r=[tt_b[1][b2], stg_b[g]],
                      w=[gn_b] if g == 0 else [], wp=[gn_b] if g else [])
                kb.op(kb.pe, lambda: T.matmul(miscb[:, 256:512], B_tm[0:n, g * 128:(g + 1) * 128], xdtd[b2][0:n, :], start=True, stop=True),
                      r=[Btm_b, xdtd_b[b2]], w=[pst_b])
                kb.op(kb.pool, lambda: G.tensor_tensor(x3(stmp[b2][:, :]), x3(S32[:, gsl]), bc(sm[:, CD, hsl], [128, 4, 64], 2), ALU.mult),
                      r=[S32_b[g], sm_b[CD]], w=[stmp_b[b2]])
                kb.op(kb.dve, lambda: V.tensor_tensor(S32[:, gsl], stmp[b2][:, :], miscb[:, 256:512], ALU.add), r=[stmp_b[b2], pst_b], w=[S32_b[g]])
                kb.op(kb.act, lambda: A.copy(Sbf[:, gsl], S32[:, gsl]), r=[S32_b[g]], w=[Sbf_b[g]])
            for cc in range(16):
                kb.op(kb.pe, lambda: T.transpose(tpg[:, cc, 0:n], gn[0:n, cc * 128:(cc + 1) * 128], self.ident[0:n, 0:n]),
                      r=[gn_b, self.cb_], w=[tp_b] if cc == 0 else [], wp=[tp_b] if cc else [], inc=(cc == 15))
            kb.op(kb.dve, lambda: V.tensor_tensor(gT[:, :, 0:n], tpg[:, :, 0:n], bc(sng[:, :], [128, 16, n], 2), ALU.mult), r=[tp_b, par_b], w=[gT_b])
            for hh in range(2):
                po = dTb[0:n, :] if hh == 0 else yb[0:n, :]
                pbufs = [dT_b] if hh == 0 else [ydg_b]
                for cc in range(16):
                    kb.op(kb.pe, lambda: T.matmul(po, gT[:, cc, 0:n], w_out[:, cc, hh * 512:(hh + 1) * 512], start=(cc == 0), stop=(cc == 15)),
                          r=[gT_b, wout_b], w=pbufs if cc == 0 else [], wp=pbufs if cc else [], inc=(cc == 15))
                kb.op(kb.dve, lambda: V.tensor_tensor(hs[cur][0:n, hh * 512:(hh + 1) * 512], po, hs[cur][0:n, hh * 512:(hh + 1) * 512], ALU.add),
                      r=pbufs + [hs_b[cur]], wp=[hs_b[cur]])
            dap = self.h_dst(dst, t0, n)
            if dap is not None:
                kb.dma("sp", dap, hs[cur][0:n, :], r=[hs_b[cur]])

    def rstd_ops(self, st, st_b, n, c0, inv_n):
        kb, nc = self.kb, self.nc
        kb.op(kb.act, lambda: nc.scalar.activation(out=st[0:n, c0 + 1:c0 + 2], in_=st[0:n, c0:c0 + 1], func=AF.Ln, bias=EPS, scale=inv_n),
              r=[st_b], wp=[st_b])
        kb.op(kb.act, lambda: nc.scalar.activation(out=st[0:n, c0 + 2:c0 + 3], in_=st[0:n, c0 + 1:c0 + 2], func=AF.Exp, scale=-0.5),
              r=[st_b], wp=[st_b])

    def phase_mla(self, es, j, li, src, dst):
        kb, nc = self.kb, self.nc
        V, A, G, T = nc.vector, nc.scalar, nc.gpsimd, nc.tensor
        oT = self.S(es, "oT", [128, 8, NT], BF16)
        oT_b = Buf()
        w_o = self.S(es, "w_o", [128, 8, D], BF16)
        w_o_b = Buf()
        gq = self.S(es, "gq", [128, 96], F32)
        gk = self.S(es, "gk", [128, 96], F32)
        par_b = Buf()
        self.tri16 = self.S(es, "tri16", [128, 128], BF16)
        kb.dma("pool", self.tri16[:], self.tri_d, wp=[par_b])
        kb.dma("sp", gq[:], self.gq_d[j], wp=[par_b])
        kb.dma("sp", gk[:], self.gk_d[j], wp=[par_b])
        with contextlib.ExitStack() as s1:
            w_in = self.S(s1, "a_win", [128, 8, 672], BF16)
            w_qb = self.S(s1, "a_wqb", [128, 3, 1536], BF16)
            w_kvb = self.S(s1, "a_wkvb", [128, 2, 2048], BF16)
            wb = Buf()
            kb.dma("pool", w_in[:], self.w_mla_in[j].rearrange("(c p) f -> p c f", p=128), wp=[wb])
            kb.dma("pool", w_qb[:], self.w_mla_qb[j].rearrange("(c p) f -> p c f", p=128), wp=[wb])
            kb.dma("pool", w_kvb[:], self.w_mla_kvb[j].rearrange("(c p) f -> p c f", p=128), wp=[wb])
            kb.dma("pool", w_o[:], self.w_mla_out[j].rearrange("(c p) f -> p c f", p=128), wp=[w_o_b])
            qag = self.S(s1, "a_qag", [128, 3], F32)
            kvag = self.S(s1, "a_kvag", [128, 2], F32)
            kb.dma("sp", qag[:], self.qag_d[j], wp=[par_b])
            kb.dma("sp", kvag[:], self.kvag_d[j], wp=[par_b])
            tp_ps = self.P(s1, "a_tp", [128, 512], F32)
            latA = self.P(s1, "a_latA", [128, 512], F32)
            latB = self.P(s1, "a_latB", [128, 512], F32)
            big = self.P(s1, "a_big", [128, 2048], F32)
            tph_ps = self.P(s1, "a_tph", [128, 512], F32)
            latA_b, latB_b, tph_b = PB(), PB(), PB()
            big_b = [PB() for _ in range(4)]
            nsc = self.norm_scratch(s1, "a", tp_ps)
            tp5 = tp_ps[:].bitcast(BF16)[:, 0:640].rearrange("p (c t) -> p c t", c=5)
            tp_b = nsc[7]
            tph = tph_ps[:].bitcast(BF16)[:, 0:1024].rearrange("p (c t) -> p c t", c=8)
            qTv = self.qT_d.rearrange("h p t -> p h t")
            kTv = self.kT_d.rearrange("h p t -> p h t")

            class NS:
                pass

            def mkbufs(ci):
                b = NS()
                nm = lambda x: "a%d_%s" % (ci, x)
                b.hs = self.S(s1, nm("h"), [128, D], F32); b.hs_b = Buf()
                b.cs = self.S(s1, nm("cs"), [128, 2, 16], F32); b.cs_b = Buf()
                b.xnT = self.S(s1, nm("xnT"), [128, 8, 128], BF16); b.xnT_b = Buf()
                b.st = self.S(s1, nm("st"), [128, 8], F32); b.st_b = Buf()
                b.qln = self.S(s1, nm("qln"), [128, 384], BF16)
                b.kvln = self.S(s1, nm("kvln"), [128, 256], BF16)
                b.kpe = self.S(s1, nm("kpe"), [128, 32], F32); b.ln_b = Buf()
                b.sqj = self.S(s1, nm("sqj"), [128, 2048], F32); b.sqj_b = Buf()
                b.qlT = self.S(s1, nm("qlT"), [128, 3, 128], BF16)
                b.kvlT = self.S(s1, nm("kvlT"), [128, 2, 128], BF16); b.lT_b = Buf()
                b.raw = self.S(s1, nm("raw"), [128, 2048], F32); b.raw_b = Buf()
                b.s16 = self.S(s1, nm("s16"), [128, 3, 16], F32); b.s16_b = Buf()
                b.rt = self.S(s1, nm("rt"), [128, 4, 256], F32); b.rt_b = [Buf() for _ in range(4)]
                b.kpg = self.S(s1, nm("kpg"), [128, 2, 32], F32); b.kpg_b = Buf()
                b.qbf = self.S(s1, nm("qbf"), [128, 1536], BF16); b.qbf_b = Buf()
                b.stg = [self.S(s1, nm("stg%d" % i), [96, 16, 128], BF16) for i in range(2)]; b.stg_b = [Buf(), Buf()]
                b.vst = self.S(s1, nm("vst"), [128, 8, 192], BF16); b.vst_b = Buf()
                kb.op(kb.pool, lambda: G.memset(b.vst[:], 1.0), w=[b.vst_b])
                return b

            CH = [mkbufs(0), mkbufs(1)]

            def chain(ti):
                t0, n = TILES[ti]
                b = CH[ti % 2]
                xnT, st, st_b, qln, kvln, kpe, ln_b = b.xnT, b.st, b.st_b, b.qln, b.kvln, b.kpe, b.ln_b
                sqj, sqj_b, qlT, kvlT, lT_b, raw, raw_b = b.sqj, b.sqj_b, b.qlT, b.kvlT, b.lT_b, b.raw, b.raw_b
                s16, s16_b, rt, rt_b, kpg, kpg_b, qbf, qbf_b = b.s16, b.s16_b, b.rt, b.rt_b, b.kpg, b.kpg_b, b.qbf, b.qbf_b
                raw3 = raw[0:n, 0:1536].rearrange("p (h f) -> p h f", h=16)
                sq3 = sqj[0:n, 0:1536].rearrange("p (h f) -> p h f", h=16)
                qb3 = qbf[0:n, :].rearrange("p (h f) -> p h f", h=16)
                kv4 = raw[0:n, :].rearrange("p (h f) -> p h f", h=16)
                sqk = sqj[0:n, 0:1024].rearrange("p (h f) -> p h f", h=16)
                kv5 = raw[0:n, :].rearrange("p (c e f) -> p c e f", c=8, e=2)

                def head_T(sg, dstv):
                    for half in range(2):
                        for hh in range(8):
                            h = half * 8 + hh
                            kb.op(kb.pe, lambda: T.transpose(tph[0:96, hh, 0:n], qbf[0:n, h * 96:(h + 1) * 96], self.ident[0:n, 0:n]),
                                  r=[qbf_b, self.cb_], w=[tph_b] if hh == 0 else [], wp=[tph_b] if hh else [], inc=(hh == 7))
                        kb.op(kb.act, lambda: A.copy(b.stg[sg][0:96, half * 8:(half + 1) * 8, 0:n], tph[0:96, :, 0:n]),
                              r=[tph_b], w=[b.stg_b[sg]] if half == 0 else [], wp=[b.stg_b[sg]] if half else [])
                    kb.dma("sp", dstv[:, :, t0:t0 + n], b.stg[sg][0:96, :, 0:n], r=[b.stg_b[sg]])

                def rope(t1, t2, cosb, sinb, o1, o2, three_d, wb_, rd_bufs):
                    if three_d:
                        a_, b_, c_, d_ = [rt[0:n, i, :].rearrange("p (h f) -> p h f", h=16) for i in range(4)]
                    else:
                        a_, b_, c_, d_ = [rt[0:n, i, 0:16] for i in range(4)]
                    kb.op(kb.dve, lambda: V.tensor_tensor(a_, t1, cosb, ALU.mult), r=rd_bufs, w=[rt_b[0]])
                    kb.op(kb.dve, lambda: V.tensor_tensor(b_, t2, sinb, ALU.mult), r=rd_bufs, w=[rt_b[1]])
                    kb.op(kb.dve, lambda: V.tensor_tensor(c_, t1, sinb, ALU.mult), r=rd_bufs, w=[rt_b[2]])
                    kb.op(kb.dve, lambda: V.tensor_tensor(d_, t2, cosb, ALU.mult), r=rd_bufs, w=[rt_b[3]])
                    kb.op(kb.dve, lambda: V.tensor_tensor(o1, a_, b_, ALU.subtract), r=[rt_b[0], rt_b[1]], wp=[wb_])
                    kb.op(kb.dve, lambda: V.tensor_tensor(o2, c_, d_, ALU.add), r=[rt_b[2], rt_b[3]], wp=[wb_])

                def s0():
                    kb.dma("sp", b.hs[0:n, :], self.h_src(src, t0, n), w=[b.hs_b])

                def s0b():
                    kb.dma("sp", b.cs[0:n, 0, :], self.cos_d[t0:t0 + n, :], w=[b.cs_b])
                    kb.dma("sp", b.cs[0:n, 1, :], self.sin_d[t0:t0 + n, :], wp=[b.cs_b])

                def s1():
                    self.norm_T(b.hs_b, b.hs[0:n, :], n, self.lnT[:, li, :], xnT[:, :, 0:n], b.xnT_b, nsc)

                def s2():
                    for c in range(8):
                        kb.op(kb.pe, lambda: T.matmul(latA[0:n, 0:384], xnT[:, c, 0:n], w_in[:, c, 0:384], start=(c == 0), stop=(c == 7)),
                              r=[b.xnT_b, wb], w=[latA_b] if c == 0 else [], wp=[latA_b] if c else [], inc=(c == 7))
                    for c in range(8):
                        kb.op(kb.pe, lambda: T.matmul(latB[0:n, 0:288], xnT[:, c, 0:n], w_in[:, c, 384:672], start=(c == 0), stop=(c == 7)),
                              r=[b.xnT_b, wb], w=[latB_b] if c == 0 else [], wp=[latB_b] if c else [], inc=(c == 7))
                    kb.op(kb.act, lambda: A.activation(out=sqj[0:n, 0:384], in_=latA[0:n, 0:384], func=AF.Square, accum_out=st[0:n, 0:1]),
                          r=[latA_b], w=[sqj_b, st_b])
                    kb.op(kb.act, lambda: A.activation(out=sqj[0:n, 512:768], in_=latB[0:n, 0:256], func=AF.Square, accum_out=st[0:n, 3:4]),
                          r=[latB_b], wp=[sqj_b, st_b])
                    self.rstd_ops(st, st_b, n, 0, 1.0 / 384)
                    self.rstd_ops(st, st_b, n, 3, 1.0 / 256)
                    kb.op(kb.dve, lambda: V.tensor_scalar(qln[0:n, :], latA[0:n, 0:384], st[0:n, 2:3], None, ALU.mult), r=[latA_b, st_b], w=[ln_b])
                    kb.op(kb.dve, lambda: V.tensor_scalar(kvln[0:n, :], latB[0:n, 0:256], st[0:n, 5:6], None, ALU.mult), r=[latB_b, st_b], wp=[ln_b])
                    kb.op(kb.act, lambda: A.copy(kpe[0:n, :], latB[0:n, 256:288]), r=[latB_b], wp=[ln_b])

                def s3():
                    for c in range(5):
                        srcap = qln[0:n, c * 128:(c + 1) * 128] if c < 3 else kvln[0:n, (c - 3) * 128:(c - 2) * 128]
                        kb.op(kb.pe, lambda: T.transpose(tp5[:, c, 0:n], srcap, self.ident[0:n, 0:n]),
                              r=[ln_b, self.cb_], w=[tp_b] if c == 0 else [], wp=[tp_b] if c else [], inc=(c == 4))
                    kb.op(kb.dve, lambda: V.tensor_tensor(qlT[:, :, 0:n], tp5[:, 0:3, 0:n], bc(qag[:, :], [128, 3, n], 2), ALU.mult),
                          r=[tp_b, par_b], w=[lT_b])
                    kb.op(kb.dve, lambda: V.tensor_tensor(kvlT[:, :, 0:n], tp5[:, 3:5, 0:n], bc(kvag[:, :], [128, 2, n], 2), ALU.mult),
                          r=[tp_b, par_b], wp=[lT_b])

                def s4():
                    for ct in range(3):
                        for kc in range(3):
                            kb.op(kb.pe, lambda: T.matmul(big[0:n, ct * 512:(ct + 1) * 512], qlT[:, kc, 0:n], w_qb[:, kc, ct * 512:(ct + 1) * 512],
                                                          start=(kc == 0), stop=(kc == 2)),
                                  r=[lT_b, wb], w=[big_b[ct]] if kc == 0 else [], wp=[big_b[ct]] if kc else [], inc=(kc == 2))
                    for ct in range(3):
                        E_ = kb.act if ct != 1 else kb.dve
                        fn = (lambda: A.copy(raw[0:n, ct * 512:(ct + 1) * 512], big[0:n, ct * 512:(ct + 1) * 512])) if ct != 1 else \
                             (lambda: V.tensor_copy(raw[0:n, ct * 512:(ct + 1) * 512], big[0:n, ct * 512:(ct + 1) * 512]))
                        kb.op(E_, fn, r=[big_b[ct]], w=[raw_b] if ct == 0 else [], wp=[raw_b] if ct else [])

                def s5():
                    kb.op(kb.dve, lambda: V.tensor_tensor(sqj[0:n, 0:1536], raw[0:n, 0:1536], raw[0:n, 0:1536], ALU.mult), r=[raw_b], w=[sqj_b])
                    kb.op(kb.dve, lambda: V.tensor_reduce(s16[0:n, 0, :], sq3, AX.X, ALU.add), r=[sqj_b], w=[s16_b])
                    kb.op(kb.act, lambda: A.activation(out=s16[0:n, 1, :], in_=s16[0:n, 0, :], func=AF.Ln, bias=EPS, scale=1.0 / 96), r=[s16_b], wp=[s16_b])
                    kb.op(kb.act, lambda: A.activation(out=s16[0:n, 2, :], in_=s16[0:n, 1, :], func=AF.Exp, scale=-0.5), r=[s16_b], wp=[s16_b])
                    kb.op(kb.dve, lambda: V.tensor_tensor(raw3, raw3, bc(s16[0:n, 2, :], [n, 16, 96], 2), ALU.mult), r=[s16_b], w=[raw_b])
                    kb.op(kb.dve, lambda: V.tensor_tensor(raw3, raw3, bc(gq[0:n, :], [n, 16, 96], 1), ALU.mult), r=[par_b], w=[raw_b])
                    cosb = bc(b.cs[0:n, 0, :], [n, 16, 16], 1)
                    sinb = bc(b.cs[0:n, 1, :], [n, 16, 16], 1)
                    kb.op(kb.act, lambda: A.copy(qb3[:, :, 0:64], raw3[:, :, 0:64]), r=[raw_b], w=[qbf_b])
                    rope(raw3[:, :, 64:80], raw3[:, :, 80:96], cosb, sinb, qb3[:, :, 64:80], qb3[:, :, 80:96], True, qbf_b, [raw_b, b.cs_b])

                def s6():
                    head_T(0, qTv)

                def s7():
                    for ct in range(4):
                        for kc in range(2):
                            kb.op(kb.pe, lambda: T.matmul(big[0:n, ct * 512:(ct + 1) * 512], kvlT[:, kc, 0:n], w_kvb[:, kc, ct * 512:(ct + 1) * 512],
                                                          start=(kc == 0), stop=(kc == 1)),
                                  r=[lT_b, wb], w=[big_b[ct]] if kc == 0 else [], wp=[big_b[ct]] if kc else [], inc=(kc == 1))
                    for ct in range(4):
                        E_ = kb.act if ct % 2 == 0 else kb.dve
                        fn = (lambda: A.copy(raw[0:n, ct * 512:(ct + 1) * 512], big[0:n, ct * 512:(ct + 1) * 512])) if ct % 2 == 0 else \
                             (lambda: V.tensor_copy(raw[0:n, ct * 512:(ct + 1) * 512], big[0:n, ct * 512:(ct + 1) * 512]))
                        kb.op(E_, fn, r=[big_b[ct]], w=[raw_b] if ct == 0 else [], wp=[raw_b] if ct else [])

                def s8():
                    kb.op(kb.dve, lambda: V.tensor_tensor(sqk, kv4[:, :, 0:64], kv4[:, :, 0:64], ALU.mult), r=[raw_b], w=[sqj_b])
                    kb.op(kb.dve, lambda: V.tensor_reduce(s16[0:n, 0, :], sqk, AX.X, ALU.add), r=[sqj_b], w=[s16_b])
                    kb.op(kb.act, lambda: A.activation(out=kpg[0:n, 1, :], in_=kpe[0:n, :], func=AF.Square, accum_out=st[0:n, 6:7]),
                          r=[ln_b], w=[kpg_b], wp=[st_b])
                    kb.op(kb.dve, lambda: V.tensor_scalar(s16[0:n, 0, :], s16[0:n, 0, :], st[0:n, 6:7], None, ALU.add), r=[st_b, s16_b], wp=[s16_b])
                    kb.op(kb.act, lambda: A.activation(out=s16[0:n, 1, :], in_=s16[0:n, 0, :], func=AF.Ln, bias=EPS, scale=1.0 / 96), r=[s16_b], wp=[s16_b])
                    kb.op(kb.act, lambda: A.activation(out=s16[0:n, 2, :], in_=s16[0:n, 1, :], func=AF.Exp, scale=-0.5), r=[s16_b], wp=[s16_b])
                    kb.op(kb.act, lambda: A.copy(b.vst[0:n, :, 0:64], kv5[:, :, 0, 64:128]), r=[raw_b, b.vst_b], wp=[b.vst_b])
                    kb.op(kb.dve, lambda: V.tensor_copy(b.vst[0:n, :, 128:192], kv5[:, :, 1, 64:128]), r=[raw_b, b.vst_b], wp=[b.vst_b])
                    kb.dma("sp", self.va_d[t0:t0 + n, :, :], b.vst[0:n, :, :], r=[b.vst_b])
                    kb.op(kb.dve, lambda: V.tensor_tensor(kv4[:, :, 0:64], kv4[:, :, 0:64], bc(s16[0:n, 2, :], [n, 16, 64], 2), ALU.mult),
                          r=[s16_b], w=[raw_b])
                    kb.op(kb.dve, lambda: V.tensor_tensor(qb3[:, :, 0:64], kv4[:, :, 0:64], bc(gk[0:n, 0:64], [n, 16, 64], 1), ALU.mult),
                          r=[raw_b, par_b], w=[qbf_b])
                    kb.op(kb.dve, lambda: V.tensor_tensor(kpg[0:n, 0, :], kpe[0:n, :], gk[0:n, 64:96], ALU.mult), r=[ln_b, par_b], w=[kpg_b])
                    rope(kpg[0:n, 0, 0:16], kpg[0:n, 0, 16:32], b.cs[0:n, 0, :], b.cs[0:n, 1, :], kpg[0:n, 1, 0:16], kpg[0:n, 1, 16:32],
                         False, kpg_b, [kpg_b, b.cs_b])
                    kb.op(kb.dve, lambda: V.tensor_tensor(qb3[:, :, 64:96], bc(kpg[0:n, 1, :], [n, 16, 32], 1), bc(s16[0:n, 2, :], [n, 16, 32], 2), ALU.mult),
                          r=[kpg_b, s16_b], wp=[qbf_b])

                def s9():
                    head_T(1, kTv)

                return [s0, s0b, s1, s2, s3, s4, s5, s6, s7, s8, s9]

            chains = [chain(k) for k in range(len(TILES))]
            NTL = len(TILES)
            for k in (0, 1):
                chains[k][0]()
                chains[k][1]()
            for k in range(0, NTL, 2):
                pair = [chains[k]] + ([chains[k + 1]] if k + 1 < NTL else [])
                nxt = [chains[k2] for k2 in (k + 2, k + 3) if k2 < NTL]
                for i in range(2, 11):
                    for c_ in pair:
                        c_[i]()
                    if i == 2:
                        for c_ in nxt:
                            c_[0]()
                    if i == 9:
                        for c_ in nxt:
                            c_[1]()
            kb.barrier()
        with contextlib.ExitStack() as s2:
            qh = [self.S(s2, "b_q%d" % i, [96, NT], BF16) for i in range(2)]
            kh = [self.S(s2, "b_k%d" % i, [96, NT], BF16) for i in range(2)]
            qk_b = [Buf(), Buf()]
            va = [self.S(s2, "b_va%d" % i, [128, 33, 192], BF16) for i in range(2)]
            va_b = [Buf(), Buf()]
            NPS = 6
            pT = [self.S(s2, "b_pT%d" % i, [128, 512], BF16) for i in range(NPS)]
            pT_b = [Buf() for _ in range(NPS)]
            rden = [self.S(s2, "b_rd%d" % i, [128, 512], F32) for i in range(2)]
            rdsh = [self.S(s2, "b_rs%d" % i, [128, 512], F32) for i in range(2)]
            rd_b = [Buf(), Buf()]
            rs_b = [Buf(), Buf()]
            bnd = self.S(s2, "b_bnd", [128, 8], F32)
            bnd_b = Buf()
            ps = [self.P(s2, "b_ps%d" % i, [128, 512], F32) for i in range(NPS)]
            ps_b = [PB() for _ in range(NPS)]
            po = [self.P(s2, "b_po%d" % i, [128, 512], F32) for i in range(2)]
            po_b = [PB(), PB()]
            kb.op(kb.dve, lambda: V.tensor_reduce(bnd[:, 0:1], gq[:, :], AX.X, ALU.max), r=[par_b], w=[bnd_b])
            kb.op(kb.dve, lambda: V.tensor_reduce(bnd[:, 1:2], gq[:, :], AX.X, ALU.min), r=[par_b], wp=[bnd_b])
            kb.op(kb.dve, lambda: V.tensor_reduce(bnd[:, 2:3], gk[:, :], AX.X, ALU.max), r=[par_b], wp=[bnd_b])
            kb.op(kb.dve, lambda: V.tensor_reduce(bnd[:, 3:4], gk[:, :], AX.X, ALU.min), r=[par_b], wp=[bnd_b])
            kb.op(kb.dve, lambda: V.scalar_tensor_tensor(bnd[:, 4:5], bnd[:, 1:2], -1.0, bnd[:, 0:1], ALU.mult, ALU.max), r=[bnd_b], wp=[bnd_b])
            kb.op(kb.dve, lambda: V.scalar_tensor_tensor(bnd[:, 5:6], bnd[:, 3:4], -1.0, bnd[:, 2:3], ALU.mult, ALU.max), r=[bnd_b], wp=[bnd_b])
            kb.op(kb.dve, lambda: V.scalar_tensor_tensor(bnd[:, 6:7], bnd[:, 4:5], -float(np.sqrt(96.0)), bnd[:, 5:6], ALU.mult, ALU.mult),
                  r=[bnd_b], wp=[bnd_b])
            negB = bnd[:, 6:7]
            scale = float(96.0 ** -0.5)

            def load_head(h):
                b = h % 2
                kb.dma("sp", qh[b][:, :], self.qT_d[h], w=[qk_b[b]])
                kb.dma("sp", kh[b][:, :], self.kT_d[h], wp=[qk_b[b]])

            def load_pair(c):
                b = c % 2
                kb.dma("sp", va[b][0:16, 0, :], self.va_d[0:16, c, :], w=[va_b[b]])
                vv = self.va_d[16:NT, c, :].rearrange("(j p) w -> p j w", p=128)
                for jj in range(0, 32, 8):
                    kb.dma("sp", va[b][:, 1 + jj:9 + jj, :], vv[:, jj:jj + 8, :], wp=[va_b[b]])

            load_pair(0)
            load_head(0)
            items = []
            for h in range(MLA_H):
                for qi in range(9):
                    if qi == 0:
                        q0, nq = 0, 16
                        kts = [(0, 0, 16, 0, True)]
                    else:
                        q0, nq = 16 + 512 * (qi - 1), 512
                        kts = [(0, 0, 16, 0, False)] + [(kt, 16 + 128 * (kt - 1), 128, 0, False) for kt in range(1, 4 * (qi - 1) + 1)]
                        kts += [(4 * (qi - 1) + 1 + i, 16 + 128 * (4 * (qi - 1) + i), 128, 128 * i, True) for i in range(4)]
                    for idx, kt in enumerate(kts):
                        items.append((h, qi, q0, nq, idx, len(kts)) + kt)
            LA = 3
            NI = len(items)
            for i in range(NI + LA):
                if i < NI:
                    (h, qi, q0, nq, idx, nk_t, kt, k0, nk, qoff, diag) = items[i]
                    if qi == 0 and idx == 0 and h + 1 < MLA_H:
                        load_head(h + 1)
                        if h % 2 == 1:
                            load_pair(h // 2 + 1)
                    hb_ = h % 2
                    nqq = nq - qoff
                    p = i % NPS
                    kb.op(kb.pe, lambda: T.matmul(ps[p][0:nk, 0:nqq], kh[hb_][:, k0:k0 + nk], qh[hb_][:, q0 + qoff:q0 + nq], start=True, stop=True),
                          r=[qk_b[hb_]], w=[ps_b[p]])
                    kb.op(kb.act, lambda: A.activation(out=pT[p][0:nk, 0:nqq], in_=ps[p][0:nk, 0:nqq], func=AF.Exp, bias=negB[0:nk, :], scale=scale),
                          r=[ps_b[p], bnd_b], w=[pT_b[p]])
                    if diag:
                        kb.op(kb.dve, lambda: V.tensor_tensor(pT[p][0:nk, 0:nk], pT[p][0:nk, 0:nk], self.tri16[0:nk, 0:nk], ALU.mult),
                              r=[self.cb_], w=[pT_b[p]])
                ii = i - LA
                if ii >= 0:
                    (h, qi, q0, nq, idx, nk_t, kt, k0, nk, qoff, diag) = items[ii]
                    c, e = h // 2, h % 2
                    vb_ = c % 2
                    dlo, dhi = (0, 64) if e == 0 else (64, 128)
                    nlo, nhi = (64, 128) if e == 0 else (0, 64)
                    nqq = nq - qoff
                    p = ii % NPS
                    pp = (h * 9 + qi) % 2
                    first, last = idx == 0, idx == nk_t - 1
                    kb.op(kb.pe, lambda: T.matmul(po[pp][:, qoff:nq], va[vb_][0:nk, kt, e * 64:e * 64 + 128], pT[p][0:nk, 0:nqq], start=first, stop=last),
                          r=[pT_b[p], va_b[vb_]], w=[po_b[pp]] if first else [], wp=[] if first else [po_b[pp]], inc=last)
                    if last:
                        kb.op(kb.dve, lambda: V.reciprocal(rden[pp][nlo:nhi, 0:nq], po[pp][nlo:nhi, 0:nq]), r=[po_b[pp]], w=[rd_b[pp]])
                        kb.op(kb.dve, lambda: V.tensor_copy(rdsh[pp][dlo:dhi, 0:nq], rden[pp][nlo:nhi, 0:nq]), r=[rd_b[pp]], w=[rs_b[pp]])
                        kb.op(kb.dve, lambda: V.tensor_tensor(oT[dlo:dhi, c, q0:q0 + nq], po[pp][dlo:dhi, 0:nq], rdsh[pp][dlo:dhi, 0:nq], ALU.mult),
                              r=[po_b[pp], rs_b[pp]], wp=[oT_b])
            kb.barrier()
        with contextlib.ExitStack() as s3:
            hs = [self.S(s3, "c_h%d" % i, [128, D], F32) for i in range(3)]
            hs_b = [Buf() for _ in range(3)]
            po = [self.P(s3, "c_po%d" % i, [128, 512], F32) for i in range(4)]
            po_b = [PB() for _ in range(4)]
            ip = 0
            for ti, (t0, n) in enumerate(TILES):
                s = ti % 3
                kb.dma("sp", hs[s][0:n, :], self.h_src(src, t0, n), w=[hs_b[s]])
                for hh in range(2):
                    p = ip % 4
                    ip += 1
                    for c in range(8):
                        kb.op(kb.pe, lambda: T.matmul(po[p][0:n, :], oT[:, c, t0:t0 + n], w_o[:, c, hh * 512:(hh + 1) * 512], start=(c == 0), stop=(c == 7)),
                              r=[oT_b, w_o_b], w=[po_b[p]] if c == 0 else [], wp=[po_b[p]] if c else [], inc=(c == 7))
                    kb.op(kb.dve, lambda: V.tensor_tensor(hs[s][0:n, hh * 512:(hh + 1) * 512], po[p][0:n, :], hs[s][0:n, hh * 512:(hh + 1) * 512], ALU.add),
                          r=[po_b[p], hs_b[s]], wp=[hs_b[s]])
                dap = self.h_dst(dst, t0, n)
                if dap is not None:
                    kb.dma("sp", dap, hs[s][0:n, :], r=[hs_b[s]])


def host_inputs(inp):
    f = lambda a: np.ascontiguousarray(np.asarray(a, dtype=np.float32))
    rep = lambda a: np.ascontiguousarray(np.broadcast_to(np.asarray(a, np.float32)[:, None, :], (a.shape[0], 128, a.shape[1])))
    colT = lambda a, c: np.ascontiguousarray(np.asarray(a, np.float32).reshape(a.shape[0], c, 128).transpose(0, 2, 1))
    ln = np.concatenate([np.asarray(inp["ln_mix"], np.float32), np.asarray(inp["ln_mlp"], np.float32)], 0)
    lnT = np.ascontiguousarray(ln.reshape(8, 8, 128).transpose(2, 0, 1))
    cw = np.asarray(inp["ssd_conv_w"], np.float32)
    cwT = np.ascontiguousarray(cw.reshape(2, 4, 32, 128).transpose(0, 3, 2, 1))
    k = np.arange(128)
    tri = (k[:, None] <= k[None, :]).astype(np.float32)
    inv = 1.0 / (10000.0 ** (np.arange(0, 32, 2, dtype=np.float32) / 32.0))
    ang = np.arange(NT, dtype=np.float32)[:, None] * inv[None, :].astype(np.float32)
    common = {
        "meta": f(inp["meta_tokens"]),
        "ssd_w_in": f(inp["ssd_w_in"]), "ssd_w_out": f(inp["ssd_w_out"]),
        "mla_w_in": f(inp["mla_w_in"]), "mla_w_q_b": f(inp["mla_w_q_b"]), "mla_w_kv_b": f(inp["mla_w_kv_b"]),
        "mla_w_out": f(inp["mla_w_out"]), "mlp_w_up": f(inp["mlp_w_up"]), "mlp_w_down": f(inp["mlp_w_down"]),
        "lnT": lnT, "cw": cwT, "cb": colT(inp["ssd_conv_b"], 32),
        "dtb_rep": rep(inp["ssd_dt_bias"]), "alog_rep": rep(inp["ssd_a_log"]), "dskip_rep": rep(inp["ssd_d"]),
        "ssd_normT": colT(inp["ssd_norm"], 16), "q_a_T": colT(inp["mla_q_a_norm"], 3), "kv_a_T": colT(inp["mla_kv_a_norm"], 2),
        "gq_rep": rep(inp["mla_q_norm"]), "gk_rep": rep(inp["mla_k_norm"]),
        "ident": np.eye(128, dtype=np.float32), "tri": tri, "ltri": np.ascontiguousarray(1.0 - tri),
        "ones": np.ones((128, 128), np.float32),
        "cos": np.cos(ang).astype(np.float32), "sin": np.sin(ang).astype(np.float32),
    }
    return common


FULL_PHASES = [
    ("ssd", 0, 0, "x", "h"), ("mlp", 0, "h", "h"),
    ("mla", 0, 1, "h", "h"), ("mlp", 1, "h", "h"),
    ("ssd", 1, 2, "h", "h"), ("mlp", 2, "h", "h"),
    ("mla", 1, 3, "h", "h"), ("mlp", 3, "h", "y"),
]


def run(inputs, phases, cores=8):
    common = host_inputs(inputs)
    x = np.asarray(inputs["x"], np.float32)
    prog = Prog(phases)
    in_maps = []
    for c in range(cores):
        m = dict(common)
        m["x"] = np.ascontiguousarray(x[c])
        in_maps.append(m)
    res = run_bass_kernel_spmd(prog.nc, in_maps, core_ids=list(range(cores)))
    return np.stack([np.asarray(r["y"]) for r in res.results], 0)


def kernel(**inputs):
    return run(inputs, FULL_PHASES, 8).astype(np.float32)
```

```python
import contextlib
import numpy as np
import concourse.bass as bass
import concourse.mybir as mybir
from concourse.bass_utils import run_bass_kernel_spmd

F32, BF16 = mybir.dt.float32, mybir.dt.bfloat16
AF = mybir.ActivationFunctionType
ALU = mybir.AluOpType
AX = mybir.AxisListType

NT, NM, D, SEQ = 4112, 16, 1024, 4096
TILES = [(0, 16)] + [(16 + 128 * j, 128) for j in range(32)]
EPS = 1e-6
DFF = 4096
SSD_IN = 6176
NH_S = 32
MLA_H = 16
QK = 96


class Buf:
    __slots__ = ("w", "r", "name", "ps")

    def __init__(self, name="", ps=False):
        self.w = {}
        self.r = {}
        self.name = name
        self.ps = ps


def PB():
    return Buf(ps=True)


class Eng:
    def __init__(self, name, e, sem):
        self.name, self.e, self.sem, self.cnt, self.seen = name, e, sem, 0, {}


class KB:
    def __init__(self, nc, es):
        self.nc = nc
        mk = lambda n: es.enter_context(nc.semaphore(n))
        self.pe = Eng("pe", nc.tensor, mk("s_pe"))
        self.act = Eng("act", nc.scalar, mk("s_act"))
        self.dve = Eng("dve", nc.vector, mk("s_dve"))
        self.pool = Eng("pool", nc.gpsimd, mk("s_pool"))
        self.sp = Eng("sp", nc.sync, mk("s_sp"))
        self.engs = [self.pe, self.act, self.dve, self.pool, self.sp]
        self.dsem = {"sp": [[mk("d_sp%d" % i), 0] for i in range(24)],
                     "pool": [[mk("d_pl%d" % i), 0] for i in range(8)]}
        self.drr = {"sp": 0, "pool": 0}
        self.nins = 0

    def _wait(self, E, toks):
        for key, (sem, val) in toks.items():
            if E is self.pe and key == "pe":
                continue
            if E.seen.get(key, 0) >= val:
                continue
            E.e.wait_ge(sem, val)
            E.seen[key] = val

    @staticmethod
    def _add(need, d):
        for k, sv in d.items():
            if k not in need or need[k][1] < sv[1]:
                need[k] = sv

    def _deps(self, r, w, wp, ekey=None):
        need = {}
        for b in r:
            self._add(need, b.w)
            if b.ps:
                self._add(need, {k: v for k, v in b.r.items() if k != ekey})
        for b in w:
            self._add(need, b.w)
            self._add(need, b.r)
        for b in wp:
            self._add(need, b.r)
            if b.ps:
                self._add(need, b.w)
        return need

    def _reg(self, key, tok, r, w, wp):
        for b in r:
            if key not in b.r or b.r[key][1] < tok[1]:
                b.r[key] = tok
        for b in w:
            b.w = {key: tok}
            b.r = {}
        for b in wp:
            if key not in b.w or b.w[key][1] < tok[1]:
                b.w[key] = tok

    def op(self, E, fn, r=(), w=(), wp=(), inc=True):
        self._wait(E, self._deps(r, w, wp, E.name))
        ins = fn()
        self.nins += 1
        if inc:
            E.cnt += 1
            ins.then_inc(E.sem, 1)
            tok = (E.sem, E.cnt)
        else:
            tok = (E.sem, E.cnt + 1)
        self._reg(E.name, tok, r, w, wp)

    def dma(self, Q, out, in_, r=(), w=(), wp=()):
        E = self.sp if Q == "sp" else self.pool
        self._wait(E, self._deps(r, w, wp))
        lst = self.dsem[Q]
        i = self.drr[Q]
        self.drr[Q] = (i + 1) % len(lst)
        sem, cnt = lst[i]
        key = (Q, i)
        if cnt > 0 and E.seen.get(key, 0) < cnt:
            E.e.wait_ge(sem, cnt)
            E.seen[key] = cnt
        ins = E.e.dma_start(out=out, in_=in_)
        ins.then_inc(sem, 16)
        self.nins += 1
        lst[i][1] = cnt + 16
        self._reg(key, (sem, cnt + 16), r, w, wp)

    def barrier(self):
        toks = {}
        for E in self.engs:
            if E.cnt > 0:
                toks[E.name] = (E.sem, E.cnt)
        for Q, lst in self.dsem.items():
            for i, (sem, cnt) in enumerate(lst):
                if cnt > 0:
                    toks[(Q, i)] = (sem, cnt)
        for E in self.engs:
            self._wait(E, toks)


def bc(ap, shape, axis):
    return ap.unsqueeze(axis).to_broadcast(list(shape))


class Prog:
    def __init__(self, phases):
        self.phases = phases
        nc = bass.Bass("TRN2", target_bir_lowering=False)
        self.nc = nc
        di = lambda name, shape: nc.dram_tensor(name, list(shape), F32, kind="ExternalInput").ap()
        self.x = di("x", [SEQ, D])
        self.meta = di("meta", [NM, D])
        self.w_ssd_in = di("ssd_w_in", [2, D, SSD_IN])
        self.w_ssd_out = di("ssd_w_out", [2, 2048, D])
        self.w_mla_in = di("mla_w_in", [2, D, 672])
        self.w_mla_qb = di("mla_w_q_b", [2, 384, 1536])
        self.w_mla_kvb = di("mla_w_kv_b", [2, 256, 2048])
        self.w_mla_out = di("mla_w_out", [2, D, D])
        self.w_up = di("mlp_w_up", [4, D, DFF])
        self.w_dn = di("mlp_w_down", [4, DFF, D])
        self.lnT_d = di("lnT", [128, 8, 8])
        self.cw_d = di("cw", [2, 128, 32, 4])
        self.cb_d = di("cb", [2, 128, 32])
        self.dtb_d = di("dtb_rep", [2, 128, 32])
        self.alog_d = di("alog_rep", [2, 128, 32])
        self.dsk_d = di("dskip_rep", [2, 128, 32])
        self.sng_d = di("ssd_normT", [2, 128, 16])
        self.qag_d = di("q_a_T", [2, 128, 3])
        self.kvag_d = di("kv_a_T", [2, 128, 2])
        self.gq_d = di("gq_rep", [2, 128, 96])
        self.gk_d = di("gk_rep", [2, 128, 96])
        self.ident_d = di("ident", [128, 128])
        self.tri_d = di("tri", [128, 128])
        self.ltri_d = di("ltri", [128, 128])
        self.ones_d = di("ones", [128, 128])
        self.cos_d = di("cos", [NT, 16])
        self.sin_d = di("sin", [NT, 16])
        self.y = nc.dram_tensor("y", [SEQ, D], F32, kind="ExternalOutput").ap()
        self.hd = nc.dram_tensor("hd", [NT, D], F32, kind="Internal").ap()
        self.qT_d = nc.dram_tensor("qT_d", [MLA_H, QK, NT], BF16, kind="Internal").ap()
        self.kT_d = nc.dram_tensor("kT_d", [MLA_H, QK, NT], BF16, kind="Internal").ap()
        self.va_d = nc.dram_tensor("va_d", [NT, 8, 192], BF16, kind="Internal").ap()

        with contextlib.ExitStack() as es:
            self.kb = KB(nc, es)
            self.build(es)

    def S(self, es, name, shape, dt):
        self.uid = getattr(self, "uid", 0) + 1
        return es.enter_context(self.nc.sbuf_tensor("sb%d_%s" % (self.uid, name), list(shape), dt))

    def P(self, es, name, shape, dt):
        self.uid = getattr(self, "uid", 0) + 1
        return es.enter_context(self.nc.psum_tensor("ps%d_%s" % (self.uid, name), list(shape), dt))

    def h_src(self, kind, t0, n):
        if kind == "x":
            return self.meta[0:16, :] if t0 == 0 else self.x[t0 - 16:t0 - 16 + n, :]
        return self.hd[t0:t0 + n, :]

    def h_dst(self, kind, t0, n):
        if kind == "y":
            return None if t0 == 0 else self.y[t0 - 16:t0 - 16 + n, :]
        return self.hd[t0:t0 + n, :]

    def build(self, es):
        kb, nc = self.kb, self.nc
        self.ident = self.S(es, "ident", [128, 128], BF16)
        self.tri32 = self.S(es, "tri32", [128, 128], F32)
        self.ltri32 = self.S(es, "ltri32", [128, 128], F32)
        self.ones32 = self.S(es, "ones32", [128, 128], F32)
        self.lnT = self.S(es, "lnT", [128, 8, 8], F32)
        self.cb_ = Buf("consts")
        kb.dma("pool", self.ident[:], self.ident_d, wp=[self.cb_])
        kb.dma("sp", self.tri32[:], self.tri_d, wp=[self.cb_])
        kb.dma("sp", self.ltri32[:], self.ltri_d, wp=[self.cb_])
        kb.dma("sp", self.ones32[:], self.ones_d, wp=[self.cb_])
        kb.dma("sp", self.lnT[:], self.lnT_d, wp=[self.cb_])
        for ph in self.phases:
            kind = ph[0]
            with contextlib.ExitStack() as pes:
                if kind == "mlp":
                    self.phase_mlp(pes, *ph[1:])
                elif kind == "ssd":
                    self.phase_ssd(pes, *ph[1:])
                elif kind == "mla":
                    self.phase_mla(pes, *ph[1:])
                kb.barrier()
        kb.barrier()

    def rstd_newton(self, st, st_b, n, inv_n, eps):
        kb, nc = self.kb, self.nc
        V = nc.vector
        I32 = mybir.dt.int32
        for f in self.rstd_newton_ops(st, st_b, n, inv_n, eps):
            f()

    def rstd_newton_ops(self, st, st_b, n, inv_n, eps):
        kb, nc = self.kb, self.nc
        V = nc.vector
        I32 = mybir.dt.int32
        x, g, t = st[0:n, 1:2], st[0:n, 2:3], st[0:n, 3:4]
        ops = []
        ops.append(lambda: kb.op(kb.dve, lambda: V.tensor_scalar(x, st[0:n, 0:1], inv_n, eps, ALU.mult, ALU.add), r=[st_b], wp=[st_b]))
        ops.append(lambda: kb.op(kb.dve, lambda: V.tensor_scalar(g.bitcast(I32), x.bitcast(I32), 1, None, ALU.arith_shift_right), r=[st_b], wp=[st_b]))
        ops.append(lambda: kb.op(kb.dve, lambda: V.tensor_scalar(g.bitcast(I32), g.bitcast(I32), -1, 0x5f3759df, ALU.mult, ALU.add), r=[st_b], wp=[st_b]))
        for _ in range(2):
            ops.append(lambda: kb.op(kb.dve, lambda: V.scalar_tensor_tensor(t, g, x, g, ALU.mult, ALU.mult), r=[st_b], wp=[st_b]))
            ops.append(lambda: kb.op(kb.dve, lambda: V.tensor_scalar(t, t, -0.5, 1.5, ALU.mult, ALU.add), r=[st_b], wp=[st_b]))
            ops.append(lambda: kb.op(kb.dve, lambda: V.tensor_tensor(g, g, t, ALU.mult), r=[st_b], wp=[st_b]))
        return ops

    def norm_T(self, hb, h_ap, n, gain_ap, dst_ap, dst_b, sc, newton=False):
        kb, nc = self.kb, self.nc
        junk, junk_b, ss, ss_b, xn, xn_b, tp, tp_b = sc
        kb.op(kb.act, lambda: nc.scalar.activation(out=junk[0:n, :], in_=h_ap, func=AF.Square, accum_out=ss[0:n, 0:1]),
              r=[hb], w=[junk_b, ss_b] if junk_b is not xn_b else [xn_b, ss_b])
        if newton:
            self.rstd_newton(ss, ss_b, n, 1.0 / D, EPS)
        else:
            kb.op(kb.act, lambda: nc.scalar.activation(out=ss[0:n, 1:2], in_=ss[0:n, 0:1], func=AF.Ln, bias=EPS, scale=1.0 / D),
                  r=[ss_b], wp=[ss_b])
            kb.op(kb.act, lambda: nc.scalar.activation(out=ss[0:n, 2:3], in_=ss[0:n, 1:2], func=AF.Exp, scale=-0.5),
                  r=[ss_b], wp=[ss_b])
        kb.op(kb.dve, lambda: nc.vector.tensor_scalar(xn[0:n, :], h_ap, ss[0:n, 2:3], None, ALU.mult),
              r=[hb, ss_b], w=[xn_b])
        for c in range(8):
            kb.op(kb.pe, lambda c=c: nc.tensor.transpose(tp[:, c, 0:n], xn[0:n, c * 128:(c + 1) * 128], self.ident[0:n, 0:n]),
                  r=[xn_b, self.cb_], w=[tp_b] if c == 0 else [], wp=[tp_b] if c else [], inc=(c == 7))
        kb.op(kb.dve, lambda: nc.vector.tensor_tensor(dst_ap, tp[:, :, 0:n], bc(gain_ap, [128, 8, n], 2), ALU.mult),
              r=[tp_b, self.cb_], w=[dst_b])

    def norm_scratch(self, es, pfx, tp_ps):
        ss = self.S(es, pfx + "ss", [128, 4], F32)
        xn = self.S(es, pfx + "xn", [128, 1024], BF16)
        tp = tp_ps[:].bitcast(BF16)[:, 0:1024].rearrange("p (c t) -> p c t", c=8)
        xn_b = Buf()
        return (xn, xn_b, ss, Buf(), xn, xn_b, tp, PB())

    def phase_mlp(self, es, li, src, dst):
        kb, nc = self.kb, self.nc
        wup = self.S(es, "wup", [128, 8, DFF], BF16)
        wdn = self.S(es, "wdn", [128, 32, D], BF16)
        wup_b = [Buf() for _ in range(8)]
        wdn_b = [Buf() for _ in range(8)]
        upv = self.w_up[li].rearrange("(c p) f -> p c f", p=128)
        dnv = self.w_dn[li].rearrange("(c p) d -> p c d", p=128)
        for i in range(8):
            kb.dma("pool", wup[:, :, i * 512:(i + 1) * 512], upv[:, :, i * 512:(i + 1) * 512], w=[wup_b[i]])
        for i in range(8):
            kb.dma("pool", wdn[:, 4 * i:4 * i + 4, :], dnv[:, 4 * i:4 * i + 4, :], w=[wdn_b[i]])
        NSLOT = 7
        hs = [self.S(es, "mh%d" % i, [128, D], F32) for i in range(NSLOT)]
        hs_b = [Buf() for _ in range(NSLOT)]
        xnT = self.S(es, "m_xnT", [128, 8, 512], BF16)
        xnT_b = Buf()
        uT = self.S(es, "m_uT", [128, 32, 512], BF16)
        uT_b = [Buf() for _ in range(32)]
        r32 = [self.S(es, "m_r32_%d" % i, [128, 512], F32) for i in range(2)]
        r32_b = [Buf(), Buf()]
        tp_ps = [self.P(es, "m_tp%d" % i, [128, 512], F32) for i in range(2)]
        pu = [self.P(es, "m_pu%d" % i, [128, 512], F32) for i in range(3)]
        pu_b = [PB() for _ in range(3)]
        pd = [self.P(es, "m_pd%d" % i, [128, 512], F32) for i in range(3)]
        pd_b = [PB() for _ in range(3)]
        nsc = [self.norm_scratch(es, "m%d" % i, tp_ps[i]) for i in range(2)]
        gain = self.lnT[:, 4 + li, :]
        groups = [[TILES[0]]] + [TILES[1 + 4 * g:5 + 4 * g] for g in range(8)]
        slot = 0
        iu = 0
        ipd = 0
        inorm = 0
        for grp in groups:
            ntok = sum(n for _, n in grp)
            myslots = []
            for (t0, n) in grp:
                s = slot % NSLOT
                slot += 1
                myslots.append(s)
                kb.dma("sp", hs[s][0:n, :], self.h_src(src, t0, n), w=[hs_b[s]])
            off = 0
            for (t0, n), s in zip(grp, myslots):
                self.norm_T(hs_b[s], hs[s][0:n, :], n, gain, xnT[:, :, off:off + n], xnT_b, nsc[inorm % 2])
                inorm += 1
                off += n
            for fc in range(32):
                p = iu % 3
                for c in range(8):
                    kb.op(kb.pe, lambda c=c, fc=fc, p=p: nc.tensor.matmul(pu[p][:, 0:ntok], wup[:, c, fc * 128:(fc + 1) * 128],
                                                                      xnT[:, c, 0:ntok], start=(c == 0), stop=(c == 7)),
                          r=[wup_b[fc // 4], xnT_b], w=[pu_b[p]] if c == 0 else [], wp=[pu_b[p]] if c else [], inc=(c == 7))
                rr = iu % 2
                kb.op(kb.act, lambda p=p, rr=rr: nc.scalar.activation(out=r32[rr][:, 0:ntok], in_=pu[p][:, 0:ntok], func=AF.Relu),
                      r=[pu_b[p]], w=[r32_b[rr]])
                E = kb.dve if (fc % 2 == 0) else kb.pool
                kb.op(E, lambda rr=rr, fc=fc, E=E: E.e.tensor_tensor(uT[:, fc, 0:ntok], r32[rr][:, 0:ntok], r32[rr][:, 0:ntok], ALU.mult),
                      r=[r32_b[rr]], w=[uT_b[fc]])
                iu += 1
            off = 0
            for (t0, n), s in zip(grp, myslots):
                for hh in range(2):
                    p = ipd % 3
                    ipd += 1
                    for fc in range(32):
                        kb.op(kb.pe, lambda fc=fc, p=p, off=off, n=n, hh=hh: nc.tensor.matmul(
                            pd[p][0:n, :], uT[:, fc, off:off + n], wdn[:, fc, hh * 512:(hh + 1) * 512],
                            start=(fc == 0), stop=(fc == 31)),
                            r=[uT_b[fc], wdn_b[fc // 4]], w=[pd_b[p]] if fc == 0 else [], wp=[pd_b[p]] if fc else [], inc=(fc == 31))
                    kb.op(kb.dve, lambda p=p, n=n, hh=hh, s=s: nc.vector.tensor_tensor(
                        hs[s][0:n, hh * 512:(hh + 1) * 512], pd[p][0:n, :], hs[s][0:n, hh * 512:(hh + 1) * 512], ALU.add),
                        r=[pd_b[p], hs_b[s]], wp=[hs_b[s]])
                dst_ap = self.h_dst(dst, t0, n)
                if dst_ap is not None:
                    kb.dma("sp", dst_ap, hs[s][0:n, :], r=[hs_b[s]])
                off += n

    def phase_ssd(self, es, j, li, src, dst):
        from functools import partial
        kb, nc = self.kb, self.nc
        V, A, G, T = nc.vector, nc.scalar, nc.gpsimd, nc.tensor
        w_in = self.S(es, "s_win", [128, 8, SSD_IN], BF16)
        w_out = self.S(es, "s_wout", [128, 16, D], BF16)
        par_b = Buf()
        wdt_b = Buf()
        wx_b = [Buf() for _ in range(8)]
        wz_b = [Buf() for _ in range(4)]
        wout_b = [Buf() for _ in range(4)]
        wv = self.w_ssd_in[j].rearrange("(c p) f -> p c f", p=128)
        kb.dma("pool", w_in[:, :, 6144:6176], wv[:, :, 6144:6176], w=[wdt_b])
        for i in range(8):
            kb.dma("pool", w_in[:, :, 2048 + 512 * i:2560 + 512 * i], wv[:, :, 2048 + 512 * i:2560 + 512 * i], w=[wx_b[i]])
        for i in range(4):
            kb.dma("pool", w_in[:, :, 512 * i:512 * (i + 1)], wv[:, :, 512 * i:512 * (i + 1)], w=[wz_b[i]])
        wov = self.w_ssd_out[j].rearrange("(c p) f -> p c f", p=128)
        for i in range(4):
            kb.dma("pool", w_out[:, 4 * i:4 * i + 4, :], wov[:, 4 * i:4 * i + 4, :], w=[wout_b[i]])
        cw = self.S(es, "s_cw", [128, 32, 4], F32)
        cb = self.S(es, "s_cb", [128, 32], F32)
        dtb = self.S(es, "s_dtb", [128, 32], F32)
        arep = self.S(es, "s_arep", [128, 32], F32)
        dsk = self.S(es, "s_dsk", [128, 32], F32)
        sng = self.S(es, "s_sng", [128, 16], F32)
        kb.dma("sp", cw[:], self.cw_d[j], wp=[par_b])
        kb.dma("sp", cb[:], self.cb_d[j], wp=[par_b])
        kb.dma("sp", dtb[:], self.dtb_d[j], wp=[par_b])
        kb.dma("sp", arep[:], self.alog_d[j], wp=[par_b])
        kb.dma("sp", dsk[:], self.dsk_d[j], wp=[par_b])
        kb.dma("sp", sng[:], self.sng_d[j], wp=[par_b])
        arep_b = Buf()
        kb.op(kb.act, lambda: A.activation(out=arep[:], in_=arep[:], func=AF.Exp), r=[par_b], w=[arep_b])
        kb.op(kb.dve, lambda: V.tensor_scalar(arep[:], arep[:], -1.0, None, ALU.mult), r=[arep_b], w=[arep_b])
        cwb = Buf()
        kb.op(kb.dve, lambda: V.tensor_scalar(cw[:], cw[:], 0.5, None, ALU.mult), r=[par_b], w=[cwb])
        kb.op(kb.dve, lambda: V.tensor_scalar(cb[:], cb[:], 0.5, None, ALU.mult), r=[par_b], wp=[cwb])
        hs = [self.S(es, "s_h%d" % i, [128, D], F32) for i in range(2)]
        hs_b = [Buf(), Buf()]
        tpF = self.P(es, "s_tpF", [128, 512], F32)
        pzs = [self.P(es, "s_pz%d" % i, [128, 512], F32) for i in range(2)]
        fm = self.P(es, "s_fm", [128, 512], F32)
        tpB = self.P(es, "s_tpB", [128, 512], F32)
        dTb = self.P(es, "s_dT", [128, 512], F32)
        bm = self.P(es, "s_bm", [128, 512], F32)
        yb = self.P(es, "s_y", [128, 512], F32)
        pz_b = [PB(), PB()]
        fm_b, tpB_b, dT_b, bm_b, y_b = PB(), PB(), PB(), PB(), PB()
        gT = self.S(es, "s_gT", [128, 16, 128], BF16)
        gT_b = Buf()
        _ss = self.S(es, "s_nss", [128, 4], F32)
        _xn = gT[:, 0:8, :].rearrange("p c t -> p (c t)")
        _tp = tpF[:].bitcast(BF16)[:, 0:1024].rearrange("p (c t) -> p c t", c=8)
        nsc = (_xn, gT_b, _ss, Buf(), _xn, gT_b, _tp, PB())
        tpB16 = tpB[:].bitcast(BF16)
        tpBg = tpB16.rearrange("p (c t) -> p c t", c=8)
        xnT = [self.S(es, "s_xnT%d" % i, [128, 8, 131], BF16) for i in range(2)]
        xnT_b = [Buf(), Buf()]
        for i in range(2):
            kb.op(kb.pool, lambda: G.memset(xnT[i][:], 0.0), w=[xnT_b[i]])
        xbc_xs = self.S(es, "s_xbcxs", [128, 16, 128], BF16)
        xsT_b = [Buf() for _ in range(16)]
        xbc_bc = [self.S(es, "s_xbcbc%d" % i, [128, 16, 128], BF16) for i in range(2)]
        bc_b = [[Buf() for _ in range(16)] for _ in range(2)]
        NU, NA = 4, 7
        u = [self.S(es, "s_u%d" % i, [128, 131], F32) for i in range(NU)]
        u_b = [Buf() for _ in range(NU)]
        acc = [self.S(es, "s_acc%d" % i, [128, 128], F32) for i in range(NA)]
        acc_b = [Buf() for _ in range(NA)]
        th = [self.S(es, "s_th%d" % i, [128, 128], F32) for i in range(2)]
        th_b = [Buf(), Buf()]
        xs_tm = self.S(es, "s_xstm", [128, 2048], BF16)
        xs_b = [Buf() for _ in range(8)]
        B_tm = self.S(es, "s_Btm", [128, 1024], BF16)
        Btm_b = Buf()
        smF = self.S(es, "s_smF", [128, 4, 32], F32)
        EXPT, ACS, TMP, DTE = range(4)
        smF_b = [Buf() for _ in range(4)]
        smB = [self.S(es, "s_smB%d" % i, [128, 5, 32], F32) for i in range(2)]
        DTV, ADT, DFS, CD, W2 = range(5)
        smB_b = [[Buf() for _ in range(5)] for _ in range(2)]
        rhsD = [self.S(es, "s_rhsD%d" % i, [128, 512], F32) for i in range(2)]
        rhsD_b = [Buf(), Buf()]
        Ee = [self.S(es, "s_E%d" % i, [128, 512], F32) for i in range(2)]
        E_b = [Buf(), Buf()]
        CBm = [self.S(es, "s_CBm0", [128, 128], F32)] * 2
        _cb = Buf()
        CBm_b = [_cb, _cb]
        MT = [self.S(es, "s_MT%d" % i, [128, 512], BF16) for i in range(2)]
        MT_b = [Buf(), Buf()]
        xdt = [self.S(es, "s_xdt%d" % i, [128, 256], BF16) for i in range(2)]
        xdt_b = [Buf(), Buf()]
        xdtd = [self.S(es, "s_xdtd%d" % i, [128, 256], BF16) for i in range(2)]
        xdtd_b = [Buf(), Buf()]
        tt = [[self.S(es, "s_t%d_%d" % (k, i), [128, 256], F32) for i in range(2)] for k in range(4)]
        tt_b = [[Buf(), Buf()] for _ in range(4)]
        tt.append(tt[0])
        tt_b.append(tt_b[0])
        stg = self.S(es, "s_stg", [128, 8, 4], F32)
        stg_b = [Buf() for _ in range(8)]
        S32 = self.S(es, "s_S32", [128, 2048], F32)
        Sbf = self.S(es, "s_Sbf", [128, 2048], BF16)
        S32_b = [Buf() for _ in range(8)]
        Sbf_b = [Buf() for _ in range(8)]
        stmp = [self.S(es, "s_stmp0", [128, 256], F32)] * 2
        _sb = Buf()
        stmp_b = [_sb, _sb]
        kb.op(kb.pool, lambda: G.memset(S32[:], 0.0), w=S32_b)
        kb.op(kb.pool, lambda: G.memset(Sbf[:], 0.0), w=Sbf_b)
        v3 = lambda ap: ap.rearrange("p (j l) -> p j l", j=4)
        NTL = len(TILES)

        def f_load(ti):
            t0, n = TILES[ti]
            cur, prev = ti % 2, (ti + 1) % 2
            nprev = TILES[ti - 1][1] if ti > 0 else 0
            X = xnT[cur]
            kb.dma("sp", hs[cur][0:n, :], self.h_src(src, t0, n), w=[hs_b[cur]])
            self.norm_T(hs_b[cur], hs[cur][0:n, :], n, self.lnT[:, li, :], X[:, :, 3:3 + n], xnT_b[cur], nsc, newton=True)
            kb.op(kb.pool, lambda: G.tensor_copy(X[:, :, 0:3], xnT[prev][:, :, nprev:nprev + 3]), r=[xnT_b[prev]], wp=[xnT_b[cur]])

        def f_dt1(ti):
            t0, n = TILES[ti]
            cur = ti % 2
            X = xnT[cur]
            sB, sBb = smB[cur], smB_b[cur]
            pdt = fm[0:n, 0:32]
            for c in range(8):
                kb.op(kb.pe, lambda: T.matmul(pdt, X[:, c, 3:3 + n], w_in[:, c, 6144:6176], start=(c == 0), stop=(c == 7)),
                      r=[wdt_b, xnT_b[cur]], w=[fm_b] if c == 0 else [], wp=[fm_b] if c else [], inc=(c == 7))
            kb.op(kb.dve, lambda: V.tensor_tensor(sB[0:n, DTV, :], pdt, dtb[0:n, :], ALU.add), r=[fm_b, par_b], w=[sBb[DTV]])
            kb.op(kb.act, lambda: A.activation(out=smF[0:n, EXPT, :], in_=sB[0:n, DTV, :], func=AF.Exp), r=[sBb[DTV]], w=[smF_b[EXPT]])
            kb.op(kb.act, lambda: A.activation(out=sB[0:n, DTV, :], in_=smF[0:n, EXPT, :], func=AF.Ln, bias=1.0), r=[smF_b[EXPT]], w=[sBb[DTV]])
            kb.op(kb.dve, lambda: V.tensor_tensor(sB[0:n, ADT, :], sB[0:n, DTV, :], arep[0:n, :], ALU.mult), r=[sBb[DTV], arep_b], w=[sBb[ADT]])

        def f_dt2(ti):
            t0, n = TILES[ti]
            cur = ti % 2
            sB, sBb = smB[cur], smB_b[cur]
            pacs, ptot = fm[0:n, 32:64], fm[:, 64:96]
            kb.op(kb.pe, lambda: T.matmul(pacs, self.tri32[0:n, 0:n], sB[0:n, ADT, :], start=True, stop=True), r=[sBb[ADT], self.cb_], w=[fm_b])
            kb.op(kb.pe, lambda: T.matmul(ptot, self.ones32[0:n, :], sB[0:n, ADT, :], start=True, stop=True), r=[sBb[ADT], self.cb_], wp=[fm_b])
            kb.op(kb.dve, lambda: V.tensor_copy(smF[0:n, ACS, :], pacs), r=[fm_b], w=[smF_b[ACS]])
            kb.op(kb.act, lambda: A.activation(out=sB[0:n, DFS, :], in_=pacs, func=AF.Exp), r=[fm_b], w=[sBb[DFS]])
            kb.op(kb.act, lambda: A.activation(out=sB[:, CD, :], in_=ptot, func=AF.Exp), r=[fm_b], w=[sBb[CD]])
            kb.op(kb.dve, lambda: V.tensor_tensor(smF[0:n, TMP, :], ptot[0:n, :], smF[0:n, ACS, :], ALU.subtract), r=[fm_b, smF_b[ACS]], w=[smF_b[TMP]])
            kb.op(kb.act, lambda: A.activation(out=smF[0:n, DTE, :], in_=smF[0:n, TMP, :], func=AF.Exp), r=[smF_b[TMP]], w=[smF_b[DTE]])
            kb.op(kb.dve, lambda: V.tensor_tensor(sB[0:n, W2, :], sB[0:n, DTV, :], smF[0:n, DTE, :], ALU.mult), r=[sBb[DTV], smF_b[DTE]], w=[sBb[W2]])

        def conv_dst(ti, cc, n):
            cur = ti % 2
            if cc < 16:
                return xbc_xs[:, cc, 0:n], xsT_b[cc]
            return xbc_bc[cur][:, cc - 16, 0:n], bc_b[cur][cc - 16]

        def f_conv(ti, s):
            t0, n = TILES[ti]
            cur = ti % 2
            X = xnT[cur]
            cc = s
            if 0 <= cc < 32:
                p = cc % 2
                pz = pzs[p][:, 0:3 + n]
                for c in range(8):
                    kb.op(kb.pe, lambda: T.matmul(pz, w_in[:, c, 2048 + cc * 128:2048 + (cc + 1) * 128], X[:, c, 0:3 + n], start=(c == 0), stop=(c == 7)),
                          r=[wx_b[cc // 4], xnT_b[cur]], w=[pz_b[p]] if c == 0 else [], wp=[pz_b[p]] if c else [], inc=(c == 7))
            cc = s - 1
            if 0 <= cc < 32:
                p = cc % 2
                pz = pzs[p][:, 0:3 + n]
                kb.op(kb.act, lambda: A.activation(out=acc[cc % NA][:, 0:n], in_=pz[:, 3:3 + n], func=AF.Identity, bias=cb[:, cc:cc + 1], scale=cw[:, cc, 3:4]),
                      r=[pz_b[p], cwb], w=[acc_b[cc % NA]])
                kb.op(kb.act, lambda: A.copy(u[cc % NU][:, 0:3 + n], pz), r=[pz_b[p]], w=[u_b[cc % NU]])
            for k in range(3):
                cc = s - 2 - k
                if 0 <= cc < 32:
                    p = cc % NU
                    a_ = acc[cc % NA][:, 0:n]
                    kb.op(kb.dve, lambda: V.scalar_tensor_tensor(a_, u[p][:, k:k + n], cw[:, cc, k:k + 1], a_, ALU.mult, ALU.add),
                          r=[u_b[p], cwb], w=[acc_b[cc % NA]])
            cc = s - 5
            if 0 <= cc < 32:
                kb.op(kb.act, lambda: A.activation(out=th[cc % 2][:, 0:n], in_=acc[cc % NA][:, 0:n], func=AF.Tanh), r=[acc_b[cc % NA]], w=[th_b[cc % 2]])
            cc = s - 6
            if 0 <= cc < 32:
                dst_ap, dst_b = conv_dst(ti, cc, n)
                kb.op(kb.dve, lambda: V.scalar_tensor_tensor(dst_ap, th[cc % 2][:, 0:n], 1.0, acc[cc % NA][:, 0:n], ALU.add, ALU.mult),
                      r=[th_b[cc % 2], acc_b[cc % NA]], w=[dst_b])

        def front(ti):
            st = [partial(f_load, ti), partial(f_conv, ti, 0), partial(f_dt1, ti), partial(f_conv, ti, 1), partial(f_dt2, ti)]
            st += [partial(f_conv, ti, s) for s in range(2, 38)]
            return st

        def b_trx(ti, half):
            t0, n = TILES[ti]
            for k in range(8):
                cc = half * 8 + k
                kb.op(kb.pe, lambda: T.transpose(tpB16[0:n, k * 128:(k + 1) * 128], xbc_xs[:, cc, 0:n], self.ident[:, :]),
                      r=[xsT_b[cc], self.cb_], w=[tpB_b] if k == 0 else [], wp=[tpB_b] if k else [], inc=(k == 7))
            kb.op(kb.act, lambda: A.copy(xs_tm[0:n, half * 1024:(half + 1) * 1024], tpB16[0:n, :]), r=[tpB_b], w=xs_b[4 * half:4 * half + 4])

        def b_trB(ti):
            t0, n = TILES[ti]
            cur = ti % 2
            for g in range(8):
                kb.op(kb.pe, lambda: T.transpose(tpB16[0:n, g * 128:(g + 1) * 128], xbc_bc[cur][:, g, 0:n], self.ident[:, :]),
                      r=[bc_b[cur][g], self.cb_], w=[tpB_b] if g == 0 else [], wp=[tpB_b] if g else [], inc=(g == 7))
            kb.op(kb.dve, lambda: V.tensor_copy(B_tm[0:n, :], tpB16[0:n, :]), r=[tpB_b], w=[Btm_b])

        def gchain(ti, g):
            t0, n = TILES[ti]
            cur = ti % 2
            X = xnT[cur]
            sB, sBb = smB[cur], smB_b[cur]
            b2 = g % 2
            hsl = slice(4 * g, 4 * g + 4)
            gsl = slice(g * 256, (g + 1) * 256)
            BT, CT = xbc_bc[cur][:, g, 0:n], xbc_bc[cur][:, 8 + g, 0:n]
            BTb, CTb = bc_b[cur][g], bc_b[cur][8 + g]
            x3 = lambda ap: ap.rearrange("p (j f) -> p j f", j=4)
            rD = rhsD[b2][0:n, 0:4 * n]
            Ev = Ee[b2][0:n, 0:4 * n]
            MTv = MT[b2][0:n, 0:4 * n]
            pcb = bm[0:n, 0:n]
            pst = bm[:, 256:512]
            zq = fm[0:n, 256:512]
            xs3 = x3(xs_tm[0:n, gsl])
            t1, t2, t3, szg, thg = [tt[k][b2][0:n, :] for k in range(5)]
            t1b, t2b, t3b, szb, thb = [tt_b[k][b2] for k in range(5)]
            steps = []

            def s1():
                kb.op(kb.pool, lambda: G.tensor_tensor(v3(rD), bc(sB[0:n, ADT, hsl], [n, 4, n], 2), bc(self.tri32[0:n, 0:n], [n, 4, n], 1), ALU.mult),
                      r=[sBb[ADT], self.cb_], w=[rhsD_b[b2]])
                kb.op(kb.pool, lambda: G.tensor_tensor(x3(xdt[b2][0:n, :]), xs3, bc(sB[0:n, DTV, hsl], [n, 4, 64], 2), ALU.mult),
                      r=[xs_b[g], sBb[DTV]], w=[xdt_b[b2]])
            steps.append(s1)

            def s2():
                kb.op(kb.pe, lambda: T.matmul(dTb[0:n, 0:4 * n], self.ltri32[0:n, 0:n], rD, start=True, stop=True), r=[rhsD_b[b2], self.cb_], w=[dT_b])
                kb.op(kb.act, lambda: A.activation(out=Ev, in_=dTb[0:n, 0:4 * n], func=AF.Exp), r=[dT_b], w=[E_b[b2]])
                kb.op(kb.pool, lambda: G.tensor_tensor(x3(xdtd[b2][0:n, :]), xs3, bc(sB[0:n, W2, hsl], [n, 4, 64], 2), ALU.mult),
                      r=[xs_b[g], sBb[W2]], w=[xdtd_b[b2]])
            steps.append(s2)

            def s3():
                kb.op(kb.pe, lambda: T.matmul(pcb, BT, CT, start=True, stop=True), r=[BTb, CTb], w=[bm_b])
                kb.op(kb.dve, lambda: V.tensor_tensor(CBm[b2][0:n, 0:n], pcb, self.tri32[0:n, 0:n], ALU.mult), r=[bm_b, self.cb_], w=[CBm_b[b2]])
                for c in range(8):
                    kb.op(kb.pe, lambda: T.matmul(zq, X[:, c, 3:3 + n], w_in[:, c, g * 256:(g + 1) * 256], start=(c == 0), stop=(c == 7)),
                          r=[wz_b[g // 2], xnT_b[cur]], w=[fm_b] if c == 0 else [], wp=[fm_b] if c else [], inc=(c == 7))
                kb.op(kb.act, lambda: A.activation(out=thg, in_=zq, func=AF.Tanh, scale=0.5), r=[fm_b], w=[thb])
                kb.op(kb.dve, lambda: V.scalar_tensor_tensor(szg, thg, 1.0, zq, ALU.add, ALU.mult), r=[thb, fm_b], w=[szb])
                kb.op(kb.pool, lambda: G.tensor_tensor(v3(MTv), v3(Ev), bc(CBm[b2][0:n, 0:n], [n, 4, n], 1), ALU.mult),
                      r=[E_b[b2], CBm_b[b2]], w=[MT_b[b2]])
            steps.append(s3)

            def s4():
                for jj in range(4):
                    kb.op(kb.pe, lambda: T.matmul(yb[0:n, jj * 64:(jj + 1) * 64], MTv[:, jj * n:(jj + 1) * n], xdt[b2][0:n, jj * 64:(jj + 1) * 64], start=True, stop=True),
                          r=[MT_b[b2], xdt_b[b2]], w=[y_b] if jj == 0 else [], wp=[y_b] if jj else [], inc=False)
                kb.op(kb.pe, lambda: T.matmul(yb[0:n, 256:512], CT, Sbf[:, gsl], start=True, stop=True), r=[CTb, Sbf_b[g]], wp=[y_b])
                kb.op(kb.pool, lambda: G.tensor_tensor(x3(t3), xs3, bc(dsk[0:n, hsl], [n, 4, 64], 2), ALU.mult), r=[xs_b[g], par_b], w=[t3b])
                kb.op(kb.dve, lambda: V.tensor_tensor(x3(t1), x3(yb[0:n, 256:512]), bc(sB[0:n, DFS, hsl], [n, 4, 64], 2), ALU.mult),
                      r=[y_b, sBb[DFS]], w=[t1b])
                kb.op(kb.dve, lambda: V.tensor_tensor(t2, yb[0:n, 0:256], t1, ALU.add), r=[y_b, t1b], w=[t2b])
            steps.append(s4)

            def s5():
                kb.op(kb.pe, lambda: T.matmul(pst, B_tm[0:n, g * 128:(g + 1) * 128], xdtd[b2][0:n, :], start=True, stop=True),
                      r=[Btm_b, xdtd_b[b2]], w=[bm_b])
                kb.op(kb.pool, lambda: G.tensor_tensor(x3(stmp[b2][:, :]), x3(S32[:, gsl]), bc(sB[:, CD, hsl], [128, 4, 64], 2), ALU.mult),
                      r=[S32_b[g], sBb[CD]], w=[stmp_b[b2]])
                kb.op(kb.dve, lambda: V.tensor_tensor(S32[:, gsl], stmp[b2][:, :], pst, ALU.add), r=[stmp_b[b2], bm_b], w=[S32_b[g]])
                kb.op(kb.act, lambda: A.copy(Sbf[:, gsl], S32[:, gsl]), r=[S32_b[g]], w=[Sbf_b[g]])
            steps.append(s5)

            def s6():
                kb.op(kb.pool, lambda: G.tensor_tensor(t2, t2, t3, ALU.add), r=[t3b], w=[t2b])
                kb.op(kb.pool, lambda: G.tensor_tensor(t2, t2, szg, ALU.mult), r=[szb], w=[t2b])
            steps.append(s6)

            sq = lambda: kb.op(kb.act, lambda: A.activation(out=t1, in_=t2, func=AF.Square, accum_out=stg[0:n, g, 0:1]), r=[t2b], w=[t1b, stg_b[g]])
            newt = self.rstd_newton_ops(stg[:, g, :], stg_b[g], n, 1.0 / 256, 4.0 * EPS)
            gnf = lambda: kb.op(kb.dve, lambda: V.tensor_scalar(xs_tm[0:n, gsl], t2, stg[0:n, g, 2:3], None, ALU.mult), r=[t2b, stg_b[g]], w=[xs_b[g]])
            return steps, (sq, newt, gnf)

        def b_gT(ti, half):
            t0, n = TILES[ti]
            for k in range(8):
                cc = half * 8 + k
                kb.op(kb.pe, lambda: T.transpose(tpBg[:, k, 0:n], xs_tm[0:n, cc * 128:(cc + 1) * 128], self.ident[0:n, 0:n]),
                      r=[xs_b[cc // 2], self.cb_], w=[tpB_b] if k == 0 else [], wp=[tpB_b] if k else [], inc=(k == 7))
            kb.op(kb.dve, lambda: V.tensor_tensor(gT[:, half * 8:(half + 1) * 8, 0:n], tpBg[:, :, 0:n], bc(sng[:, half * 8:(half + 1) * 8], [128, 8, n], 2), ALU.mult),
                  r=[tpB_b, par_b], w=[gT_b] if half == 0 else [], wp=[gT_b] if half else [])

        def b_out(ti, hh):
            t0, n = TILES[ti]
            cur = ti % 2
            po = dTb[0:n, :] if hh == 0 else yb[0:n, :]
            pbuf = dT_b if hh == 0 else y_b
            for cc in range(16):
                kb.op(kb.pe, lambda: T.matmul(po, gT[:, cc, 0:n], w_out[:, cc, hh * 512:(hh + 1) * 512], start=(cc == 0), stop=(cc == 15)),
                      r=[gT_b, wout_b[cc // 4]], w=[pbuf] if cc == 0 else [], wp=[pbuf] if cc else [], inc=(cc == 15))
            kb.op(kb.dve, lambda: V.tensor_tensor(hs[cur][0:n, hh * 512:(hh + 1) * 512], po, hs[cur][0:n, hh * 512:(hh + 1) * 512], ALU.add),
                  r=[pbuf, hs_b[cur]], wp=[hs_b[cur]])
            if hh == 1:
                dap = self.h_dst(dst, t0, n)
                if dap is not None:
                    kb.dma("sp", dap, hs[cur][0:n, :], r=[hs_b[cur]])

        def back(ti):
            st = [partial(b_trx, ti, 0), partial(b_trx, ti, 1), partial(b_trB, ti)]
            for gp in range(4):
                (ca, ta), (cb_, tb) = gchain(ti, 2 * gp), gchain(ti, 2 * gp + 1)
                for a_, b_ in zip(ca, cb_):
                    st += [a_, b_]

                def tail(ta=ta, tb=tb):
                    ta[0]()
                    tb[0]()
                    for fa, fb in zip(ta[1], tb[1]):
                        fa()
                        fb()
                    ta[2]()
                    tb[2]()
                st.append(tail)
            st += [partial(b_gT, ti, 0), partial(b_gT, ti, 1), partial(b_out, ti, 0), partial(b_out, ti, 1)]
            return st

        def interleave(a, b):
            na, nb = len(a), len(b)
            i = jx = 0
            while i < na or jx < nb:
                if jx >= nb or (i < na and i * nb <= jx * na):
                    a[i]()
                    i += 1
                else:
                    b[jx]()
                    jx += 1

        for ti in range(NTL + 1):
            f = front(ti) if ti < NTL else []
            b = back(ti - 1) if ti >= 1 else []
            interleave(f, b)

    def phase_ssd_v1(self, es, j, li, src, dst):
        kb, nc = self.kb, self.nc
        V, A, G, T = nc.vector, nc.scalar, nc.gpsimd, nc.tensor
        w_in = self.S(es, "s_win", [128, 8, SSD_IN], BF16)
        w_out = self.S(es, "s_wout", [128, 16, D], BF16)
        win_b, wout_b, par_b = Buf(), Buf(), Buf()
        wv = self.w_ssd_in[j].rearrange("(c p) f -> p c f", p=128)
        for c in range(8):
            kb.dma("pool", w_in[:, c, :], wv[:, c, :], wp=[win_b])
        wov = self.w_ssd_out[j].rearrange("(c p) f -> p c f", p=128)
        for c in range(0, 16, 4):
            kb.dma("pool", w_out[:, c:c + 4, :], wov[:, c:c + 4, :], wp=[wout_b])
        cw = self.S(es, "s_cw", [128, 32, 4], F32)
        cb = self.S(es, "s_cb", [128, 32], F32)
        dtb = self.S(es, "s_dtb", [128, 32], F32)
        arep = self.S(es, "s_arep", [128, 32], F32)
        dsk = self.S(es, "s_dsk", [128, 32], F32)
        sng = self.S(es, "s_sng", [128, 16], F32)
        kb.dma("sp", cw[:], self.cw_d[j], wp=[par_b])
        kb.dma("sp", cb[:], self.cb_d[j], wp=[par_b])
        kb.dma("sp", dtb[:], self.dtb_d[j], wp=[par_b])
        kb.dma("sp", arep[:], self.alog_d[j], wp=[par_b])
        kb.dma("sp", dsk[:], self.dsk_d[j], wp=[par_b])
        kb.dma("sp", sng[:], self.sng_d[j], wp=[par_b])
        arep_b = Buf()
        kb.op(kb.act, lambda: A.activation(out=arep[:], in_=arep[:], func=AF.Exp), r=[par_b], w=[arep_b])
        kb.op(kb.dve, lambda: V.tensor_scalar(arep[:], arep[:], -1.0, None, ALU.mult), r=[arep_b], w=[arep_b])
        hs = [self.S(es, "s_h%d" % i, [128, D], F32) for i in range(2)]
        hs_b = [Buf(), Buf()]
        tp2 = self.P(es, "s_tp2", [128, 1024], F32)
        pzbs = [self.P(es, "s_pz%d" % i, [128, 512], F32) for i in range(2)]
        zqb = self.P(es, "s_zq", [128, 512], F32)
        dTb = self.P(es, "s_dT", [128, 512], F32)
        miscb = self.P(es, "s_misc", [128, 512], F32)
        yb = self.P(es, "s_y", [128, 512], F32)
        pz_b = [PB(), PB()]
        _z = PB()
        zq_b = [_z, _z]
        dT_b = PB()
        misc_b = PB()
        pdt_b = pacs_b = ptot_b = pst_b = misc_b
        pcb_b = [misc_b, misc_b]
        ydg_b = yof_b = PB()
        nsc = self.norm_scratch(es, "s", tp2)
        tp_b = nsc[7]
        tp16 = tp2[:].bitcast(BF16)
        tpg = tp16.rearrange("p (c t) -> p c t", c=16)
        xnT = [self.S(es, "s_xnT%d" % i, [128, 8, 131], BF16) for i in range(2)]
        xnT_b = [Buf(), Buf()]
        for i in range(2):
            kb.op(kb.pool, lambda: G.memset(xnT[i][:], 0.0), w=[xnT_b[i]])
        xbcT = self.S(es, "s_xbcT", [128, 32, 128], BF16)
        xbc_b = [Buf() for _ in range(32)]
        acc = [self.S(es, "s_acc%d" % i, [128, 128], F32) for i in range(3)]
        acc_b = [Buf() for _ in range(3)]
        xs_tm = self.S(es, "s_xstm", [128, 2048], BF16)
        xs_b = Buf()
        B_tm = self.S(es, "s_Btm", [128, 1024], BF16)
        Btm_b = Buf()
        sz = self.S(es, "s_sz", [128, 2048], F32)
        sz_b = [Buf() for _ in range(8)]
        sm = self.S(es, "s_sm", [128, 10, 32], F32)
        DTV, EXPT, ADT, ACS, DFS, CD, DTE, TMP, W2 = range(9)
        sm_b = [Buf() for _ in range(10)]
        rhsD = [self.S(es, "s_rhsD0", [128, 512], F32)] * 2
        _b = Buf()
        rhsD_b = [_b, _b]
        Ee = [self.S(es, "s_E0", [128, 512], F32)] * 2
        _b = Buf()
        E_b = [_b, _b]
        CBm = [self.S(es, "s_CBm%d" % i, [128, 128], F32) for i in range(2)]
        CBm_b = [Buf(), Buf()]
        MT = [self.S(es, "s_MT%d" % i, [128, 512], BF16) for i in range(2)]
        MT_b = [Buf(), Buf()]
        xdt = [self.S(es, "s_xdt%d" % i, [128, 256], BF16) for i in range(2)]
        xdt_b = [Buf(), Buf()]
        xdtd = [self.S(es, "s_xdtd%d" % i, [128, 256], BF16) for i in range(2)]
        xdtd_b = [Buf(), Buf()]
        tt = [[self.S(es, "s_t%d_%d" % (k, i), [128, 256], F32) for i in range(2)] for k in range(3)]
        tt_b = [[Buf(), Buf()] for _ in range(3)]
        tt.append(tt[0])
        tt_b.append(tt_b[0])
        stg = self.S(es, "s_stg", [128, 8, 4], F32)
        stg_b = [Buf() for _ in range(8)]
        gn = self.S(es, "s_gn", [128, 2048], BF16)
        gn_b = Buf()
        gT = self.S(es, "s_gT", [128, 16, 128], BF16)
        gT_b = Buf()
        S32 = self.S(es, "s_S32", [128, 2048], F32)
        Sbf = self.S(es, "s_Sbf", [128, 2048], BF16)
        S32_b = [Buf() for _ in range(8)]
        Sbf_b = [Buf() for _ in range(8)]
        stmp = [self.S(es, "s_stmp0", [128, 256], F32)] * 2
        _b = Buf()
        stmp_b = [_b, _b]
        kb.op(kb.pool, lambda: G.memset(S32[:], 0.0), w=S32_b)
        kb.op(kb.pool, lambda: G.memset(Sbf[:], 0.0), w=Sbf_b)
        nprev = 0
        ipz = 0
        for ti, (t0, n) in enumerate(TILES):
            cur, prev = ti % 2, (ti + 1) % 2
            X = xnT[cur]
            kb.dma("sp", hs[cur][0:n, :], self.h_src(src, t0, n), w=[hs_b[cur]])
            self.norm_T(hs_b[cur], hs[cur][0:n, :], n, self.lnT[:, li, :], X[:, :, 3:3 + n], xnT_b[cur], nsc)
            kb.op(kb.pool, lambda: G.tensor_copy(X[:, :, 0:3], xnT[prev][:, :, nprev:nprev + 3]), r=[xnT_b[prev]], wp=[xnT_b[cur]])
            nprev = n
            for cc in range(32):
                p = ipz % 2
                ipz += 1
                pz = pzbs[p][:, 0:3 + n]
                for c in range(8):
                    kb.op(kb.pe, lambda: T.matmul(pz, w_in[:, c, 2048 + cc * 128:2048 + (cc + 1) * 128], X[:, c, 0:3 + n], start=(c == 0), stop=(c == 7)),
                          r=[win_b, xnT_b[cur]], w=[pz_b[p]] if c == 0 else [], wp=[pz_b[p]] if c else [], inc=(c == 7))
                a_ = acc[p][:, 0:n]
                kb.op(kb.act, lambda: A.activation(out=a_, in_=pz[:, 3:3 + n], func=AF.Identity, bias=cb[:, cc:cc + 1], scale=cw[:, cc, 3:4]),
                      r=[pz_b[p], par_b], w=[acc_b[p]])
                for k in range(3):
                    kb.op(kb.dve, lambda: V.scalar_tensor_tensor(a_, pz[:, k:k + n], cw[:, cc, k:k + 1], a_, ALU.mult, ALU.add),
                          r=[pz_b[p], par_b], w=[acc_b[p]])
                kb.op(kb.act, lambda: A.activation(out=xbcT[:, cc, 0:n], in_=a_, func=AF.Silu), r=[acc_b[p]], w=[xbc_b[cc]])
            for g in range(8):
                zs = g % 2
                zq = zqb[0:n, zs * 256:(zs + 1) * 256]
                for c in range(8):
                    kb.op(kb.pe, lambda: T.matmul(zq, X[:, c, 3:3 + n], w_in[:, c, g * 256:(g + 1) * 256], start=(c == 0), stop=(c == 7)),
                          r=[win_b, xnT_b[cur]], w=[zq_b[zs]] if c == 0 else [], wp=[zq_b[zs]] if c else [], inc=(c == 7))
                kb.op(kb.act, lambda: A.activation(out=sz[0:n, g * 256:(g + 1) * 256], in_=zq, func=AF.Silu), r=[zq_b[zs]], w=[sz_b[g]])
            for cc in range(16):
                kb.op(kb.pe, lambda: T.transpose(tp16[0:n, cc * 128:(cc + 1) * 128], xbcT[:, cc, 0:n], self.ident[:, :]),
                      r=[xbc_b[cc], self.cb_], w=[tp_b] if cc == 0 else [], wp=[tp_b] if cc else [], inc=(cc == 15))
            kb.op(kb.act, lambda: A.copy(xs_tm[0:n, 0:1024], tp16[0:n, 0:1024]), r=[tp_b], w=[xs_b])
            kb.op(kb.dve, lambda: V.tensor_copy(xs_tm[0:n, 1024:2048], tp16[0:n, 1024:2048]), r=[tp_b], wp=[xs_b])
            for g in range(8):
                kb.op(kb.pe, lambda: T.transpose(tp16[0:n, g * 128:(g + 1) * 128], xbcT[:, 16 + g, 0:n], self.ident[:, :]),
                      r=[xbc_b[16 + g], self.cb_], w=[tp_b] if g == 0 else [], wp=[tp_b] if g else [], inc=(g == 7))
            kb.op(kb.act, lambda: A.copy(B_tm[0:n, :], tp16[0:n, 0:1024]), r=[tp_b], w=[Btm_b])
            pdt, pacs, ptot = miscb[0:n, 0:32], miscb[0:n, 32:64], miscb[:, 64:96]
            for c in range(8):
                kb.op(kb.pe, lambda: T.matmul(pdt, X[:, c, 3:3 + n], w_in[:, c, 6144:6176], start=(c == 0), stop=(c == 7)),
                      r=[win_b, xnT_b[cur]], w=[pdt_b] if c == 0 else [], wp=[pdt_b] if c else [], inc=(c == 7))
            smv = lambda k: sm[0:n, k, :]
            kb.op(kb.dve, lambda: V.tensor_tensor(smv(DTV), pdt, dtb[0:n, :], ALU.add), r=[pdt_b, par_b], w=[sm_b[DTV]])
            kb.op(kb.act, lambda: A.activation(out=smv(EXPT), in_=smv(DTV), func=AF.Exp), r=[sm_b[DTV]], w=[sm_b[EXPT]])
            kb.op(kb.act, lambda: A.activation(out=smv(DTV), in_=smv(EXPT), func=AF.Ln, bias=1.0), r=[sm_b[EXPT]], w=[sm_b[DTV]])
            kb.op(kb.dve, lambda: V.tensor_tensor(smv(ADT), smv(DTV), arep[0:n, :], ALU.mult), r=[sm_b[DTV], arep_b], w=[sm_b[ADT]])
            kb.op(kb.pe, lambda: T.matmul(pacs, self.tri32[0:n, 0:n], smv(ADT), start=True, stop=True), r=[sm_b[ADT], self.cb_], w=[pacs_b])
            kb.op(kb.pe, lambda: T.matmul(ptot, self.ones32[0:n, :], smv(ADT), start=True, stop=True), r=[sm_b[ADT], self.cb_], w=[ptot_b])
            kb.op(kb.dve, lambda: V.tensor_copy(smv(ACS), pacs), r=[pacs_b], w=[sm_b[ACS]])
            kb.op(kb.act, lambda: A.activation(out=smv(DFS), in_=pacs, func=AF.Exp), r=[pacs_b], w=[sm_b[DFS]])
            kb.op(kb.act, lambda: A.activation(out=sm[:, CD, :], in_=ptot, func=AF.Exp), r=[ptot_b], w=[sm_b[CD]])
            kb.op(kb.dve, lambda: V.tensor_tensor(smv(TMP), ptot[0:n, :], smv(ACS), ALU.subtract), r=[ptot_b, sm_b[ACS]], w=[sm_b[TMP]])
            kb.op(kb.act, lambda: A.activation(out=smv(DTE), in_=smv(TMP), func=AF.Exp), r=[sm_b[TMP]], w=[sm_b[DTE]])
            kb.op(kb.dve, lambda: V.tensor_tensor(smv(W2), smv(DTV), smv(DTE), ALU.mult), r=[sm_b[DTV], sm_b[DTE]], w=[sm_b[W2]])
            for g in range(8):
                b2 = g % 2
                hsl = slice(4 * g, 4 * g + 4)
                gsl = slice(g * 256, (g + 1) * 256)
                v3 = lambda ap: ap.rearrange("p (j l) -> p j l", j=4)
                rD = rhsD[b2][0:n, 0:4 * n]
                kb.op(kb.pool, lambda: G.tensor_tensor(v3(rD), bc(sm[0:n, ADT, hsl], [n, 4, n], 2), bc(self.tri32[0:n, 0:n], [n, 4, n], 1), ALU.mult),
                      r=[sm_b[ADT], self.cb_], w=[rhsD_b[b2]])
                kb.op(kb.pe, lambda: T.matmul(dTb[0:n, 0:4 * n], self.ltri32[0:n, 0:n], rD, start=True, stop=True), r=[rhsD_b[b2], self.cb_], w=[dT_b])
                Ev = Ee[b2][0:n, 0:4 * n]
                kb.op(kb.act, lambda: A.activation(out=Ev, in_=dTb[0:n, 0:4 * n], func=AF.Exp), r=[dT_b], w=[E_b[b2]])
                pcb = miscb[0:n, 128:128 + n]
                kb.op(kb.pe, lambda: T.matmul(pcb, xbcT[:, 16 + g, 0:n], xbcT[:, 24 + g, 0:n], start=True, stop=True),
                      r=[xbc_b[16 + g], xbc_b[24 + g]], w=[pcb_b[b2]])
                kb.op(kb.dve, lambda: V.tensor_tensor(CBm[b2][0:n, 0:n], pcb, self.tri32[0:n, 0:n], ALU.mult), r=[pcb_b[b2], self.cb_], w=[CBm_b[b2]])
                MTv = MT[b2][0:n, 0:4 * n]
                kb.op(kb.pool, lambda: G.tensor_tensor(v3(MTv), v3(Ev), bc(CBm[b2][0:n, 0:n], [n, 4, n], 1), ALU.mult),
                      r=[E_b[b2], CBm_b[b2]], w=[MT_b[b2]])
                xs3 = xs_tm[0:n, gsl].rearrange("p (j f) -> p j f", j=4)
                x3 = lambda ap: ap.rearrange("p (j f) -> p j f", j=4)
                kb.op(kb.pool, lambda: G.tensor_tensor(x3(xdt[b2][0:n, :]), xs3, bc(sm[0:n, DTV, hsl], [n, 4, 64], 2), ALU.mult),
                      r=[xs_b, sm_b[DTV]], w=[xdt_b[b2]])
                kb.op(kb.pool, lambda: G.tensor_tensor(x3(xdtd[b2][0:n, :]), xs3, bc(sm[0:n, W2, hsl], [n, 4, 64], 2), ALU.mult),
                      r=[xs_b, sm_b[W2]], w=[xdtd_b[b2]])
                for jj in range(4):
                    kb.op(kb.pe, lambda: T.matmul(yb[0:n, jj * 64:(jj + 1) * 64], MTv[:, jj * n:(jj + 1) * n], xdt[b2][0:n, jj * 64:(jj + 1) * 64], start=True, stop=True),
                          r=[MT_b[b2], xdt_b[b2]], w=[ydg_b] if jj == 0 else [], wp=[ydg_b] if jj else [], inc=(jj == 3))
                kb.op(kb.pe, lambda: T.matmul(yb[0:n, 256:512], xbcT[:, 24 + g, 0:n], Sbf[:, gsl], start=True, stop=True),
                      r=[xbc_b[24 + g], Sbf_b[g]], w=[yof_b])
                t1, t2, t3, tj = [tt[k][b2][0:n, :] for k in range(4)]
                kb.op(kb.dve, lambda: V.tensor_tensor(x3(t1), x3(yb[0:n, 256:512]), bc(sm[0:n, DFS, hsl], [n, 4, 64], 2), ALU.mult),
                      r=[yof_b, sm_b[DFS]], w=[tt_b[0][b2]])
                kb.op(kb.dve, lambda: V.tensor_tensor(t2, yb[0:n, 0:256], t1, ALU.add), r=[ydg_b, tt_b[0][b2]], w=[tt_b[1][b2]])
                kb.op(kb.pool, lambda: G.tensor_tensor(x3(t3), xs3, bc(dsk[0:n, hsl], [n, 4, 64], 2), ALU.mult), r=[xs_b, par_b], w=[tt_b[2][b2]])
                kb.op(kb.pool, lambda: G.tensor_tensor(t2, t2, t3, ALU.add), r=[tt_b[2][b2]], w=[tt_b[1][b2]])
                kb.op(kb.pool, lambda: G.tensor_tensor(t2, t2, sz[0:n, gsl], ALU.mult), r=[sz_b[g]], w=[tt_b[1][b2]])
                kb.op(kb.act, lambda: A.activation(out=tj, in_=t2, func=AF.Square, accum_out=stg[0:n, g, 0:1]), r=[tt_b[1][b2]], w=[tt_b[3][b2], stg_b[g]])
                self.rstd_ops(stg[:, g, :], stg_b[g], n, 0, 1.0 / 256)
                kb.op(kb.dve, lambda: V.tensor_scalar(gn[0:n, gsl], t2, stg[0:n, g, 2:3], None, ALU.mult), r=[tt_b[1][b2], stg_b[g]],
                      w=[gn_b] if g == 0 else [], wp=[gn_b] if g else [])
                kb.op(kb.pe, lambda: T.matmul(miscb[:, 256:512], B_tm[0:n, g * 128:(g + 1) * 128], xdtd[b2][0:n, :], start=True, stop=True),
                      r=[Btm_b, xdtd_b[b2]], w=[pst_b])
                kb.op(kb.pool, lambda: G.tensor_tensor(x3(stmp[b2][:, :]), x3(S32[:, gsl]), bc(sm[:, CD, hsl], [128, 4, 64], 2), ALU.mult),
                      r=[S32_b[g], sm_b[CD]], w=[stmp_b[b2]])
                kb.op(kb.dve, lambda: V.tensor_tensor(S32[:, gsl], stmp[b2][:, :], miscb[:, 256:512], ALU.add), r=[stmp_b[b2], pst_b], w=[S32_b[g]])
                kb.op(kb.act, lambda: A.copy(Sbf[:, gsl], S32[:, gsl]), r=[S32_b[g]], w=[Sbf_b[g]])
            for cc in range(16):
                kb.op(kb.pe, lambda: T.transpose(tpg[:, cc, 0:n], gn[0:n, cc * 128:(cc + 1) * 128], self.ident[0:n, 0:n]),
                      r=[gn_b, self.cb_], w=[tp_b] if cc == 0 else [], wp=[tp_b] if cc else [], inc=(cc == 15))
            kb.op(kb.dve, lambda: V.tensor_tensor(gT[:, :, 0:n], tpg[:, :, 0:n], bc(sng[:, :], [128, 16, n], 2), ALU.mult), r=[tp_b, par_b], w=[gT_b])
            for hh in range(2):
                po = dTb[0:n, :] if hh == 0 else yb[0:n, :]
                pbufs = [dT_b] if hh == 0 else [ydg_b]
                for cc in range(16):
                    kb.op(kb.pe, lambda: T.matmul(po, gT[:, cc, 0:n], w_out[:, cc, hh * 512:(hh + 1) * 512], start=(cc == 0), stop=(cc == 15)),
                          r=[gT_b, wout_b], w=pbufs if cc == 0 else [], wp=pbufs if cc else [], inc=(cc == 15))
                kb.op(kb.dve, lambda: V.tensor_tensor(hs[cur][0:n, hh * 512:(hh + 1) * 512], po, hs[cur][0:n, hh * 512:(hh + 1) * 512], ALU.add),
                      r=pbufs + [hs_b[cur]], wp=[hs_b[cur]])
            dap = self.h_dst(dst, t0, n)
            if dap is not None:
                kb.dma("sp", dap, hs[cur][0:n, :], r=[hs_b[cur]])

    def rstd_ops(self, st, st_b, n, c0, inv_n):
        kb, nc = self.kb, self.nc
        kb.op(kb.act, lambda: nc.scalar.activation(out=st[0:n, c0 + 1:c0 + 2], in_=st[0:n, c0:c0 + 1], func=AF.Ln, bias=EPS, scale=inv_n),
              r=[st_b], wp=[st_b])
        kb.op(kb.act, lambda: nc.scalar.activation(out=st[0:n, c0 + 2:c0 + 3], in_=st[0:n, c0 + 1:c0 + 2], func=AF.Exp, scale=-0.5),
              r=[st_b], wp=[st_b])

    def phase_mla(self, es, j, li, src, dst):
        kb, nc = self.kb, self.nc
        V, A, G, T = nc.vector, nc.scalar, nc.gpsimd, nc.tensor
        oT = self.S(es, "oT", [128, 8, NT], BF16)
        oT_b = Buf()
        w_o = self.S(es, "w_o", [128, 8, D], BF16)
        w_o_b = Buf()
        gq = self.S(es, "gq", [128, 96], F32)
        gk = self.S(es, "gk", [128, 96], F32)
        par_b = Buf()
        self.tri16 = self.S(es, "tri16", [128, 128], BF16)
        kb.dma("pool", self.tri16[:], self.tri_d, wp=[par_b])
        kb.dma("sp", gq[:], self.gq_d[j], wp=[par_b])
        kb.dma("sp", gk[:], self.gk_d[j], wp=[par_b])
        with contextlib.ExitStack() as s1:
            w_in = self.S(s1, "a_win", [128, 8, 672], BF16)
            w_qb = self.S(s1, "a_wqb", [128, 3, 1536], BF16)
            w_kvb = self.S(s1, "a_wkvb", [128, 2, 2048], BF16)
            wb = Buf()
            kb.dma("pool", w_in[:], self.w_mla_in[j].rearrange("(c p) f -> p c f", p=128), wp=[wb])
            kb.dma("pool", w_qb[:], self.w_mla_qb[j].rearrange("(c p) f -> p c f", p=128), wp=[wb])
            kb.dma("pool", w_kvb[:], self.w_mla_kvb[j].rearrange("(c p) f -> p c f", p=128), wp=[wb])
            kb.dma("pool", w_o[:], self.w_mla_out[j].rearrange("(c p) f -> p c f", p=128), wp=[w_o_b])
            qag = self.S(s1, "a_qag", [128, 3], F32)
            kvag = self.S(s1, "a_kvag", [128, 2], F32)
            kb.dma("sp", qag[:], self.qag_d[j], wp=[par_b])
            kb.dma("sp", kvag[:], self.kvag_d[j], wp=[par_b])
            tp_ps = self.P(s1, "a_tp", [128, 512], F32)
            latA = self.P(s1, "a_latA", [128, 512], F32)
            latB = self.P(s1, "a_latB", [128, 512], F32)
            big = self.P(s1, "a_big", [128, 2048], F32)
            tph_ps = self.P(s1, "a_tph", [128, 512], F32)
            latA_b, latB_b, tph_b = PB(), PB(), PB()
            big_b = [PB() for _ in range(4)]
            nsc = self.norm_scratch(s1, "a", tp_ps)
            tp5 = tp_ps[:].bitcast(BF16)[:, 0:640].rearrange("p (c t) -> p c t", c=5)
            tp_b = nsc[7]
            tph = tph_ps[:].bitcast(BF16)[:, 0:1024].rearrange("p (c t) -> p c t", c=8)
            qTv = self.qT_d.rearrange("h p t -> p h t")
            kTv = self.kT_d.rearrange("h p t -> p h t")

            class NS:
                pass

            def mkbufs(ci):
                b = NS()
                nm = lambda x: "a%d_%s" % (ci, x)
                b.hs = self.S(s1, nm("h"), [128, D], F32); b.hs_b = Buf()
                b.cs = self.S(s1, nm("cs"), [128, 2, 16], F32); b.cs_b = Buf()
                b.xnT = self.S(s1, nm("xnT"), [128, 8, 128], BF16); b.xnT_b = Buf()
                b.st = self.S(s1, nm("st"), [128, 8], F32); b.st_b = Buf()
                b.qln = self.S(s1, nm("qln"), [128, 384], BF16)
                b.kvln = self.S(s1, nm("kvln"), [128, 256], BF16)
                b.kpe = self.S(s1, nm("kpe"), [128, 32], F32); b.ln_b = Buf()
                b.sqj = self.S(s1, nm("sqj"), [128, 2048], F32); b.sqj_b = Buf()
                b.qlT = self.S(s1, nm("qlT"), [128, 3, 128], BF16)
                b.kvlT = self.S(s1, nm("kvlT"), [128, 2, 128], BF16); b.lT_b = Buf()
                b.raw = self.S(s1, nm("raw"), [128, 2048], F32); b.raw_b = Buf()
                b.s16 = self.S(s1, nm("s16"), [128, 3, 16], F32); b.s16_b = Buf()
                b.rt = self.S(s1, nm("rt"), [128, 4, 256], F32); b.rt_b = [Buf() for _ in range(4)]
                b.kpg = self.S(s1, nm("kpg"), [128, 2, 32], F32); b.kpg_b = Buf()
                b.qbf = self.S(s1, nm("qbf"), [128, 1536], BF16); b.qbf_b = Buf()
                b.stg = [self.S(s1, nm("stg%d" % i), [96, 16, 128], BF16) for i in range(2)]; b.stg_b = [Buf(), Buf()]
                b.vst = self.S(s1, nm("vst"), [128, 8, 192], BF16); b.vst_b = Buf()
                kb.op(kb.pool, lambda: G.memset(b.vst[:], 1.0), w=[b.vst_b])
                return b

            CH = [mkbufs(0), mkbufs(1)]

            def chain(ti):
                t0, n = TILES[ti]
                b = CH[ti % 2]
                xnT, st, st_b, qln, kvln, kpe, ln_b = b.xnT, b.st, b.st_b, b.qln, b.kvln, b.kpe, b.ln_b
                sqj, sqj_b, qlT, kvlT, lT_b, raw, raw_b = b.sqj, b.sqj_b, b.qlT, b.kvlT, b.lT_b, b.raw, b.raw_b
                s16, s16_b, rt, rt_b, kpg, kpg_b, qbf, qbf_b = b.s16, b.s16_b, b.rt, b.rt_b, b.kpg, b.kpg_b, b.qbf, b.qbf_b
                raw3 = raw[0:n, 0:1536].rearrange("p (h f) -> p h f", h=16)
                sq3 = sqj[0:n, 0:1536].rearrange("p (h f) -> p h f", h=16)
                qb3 = qbf[0:n, :].rearrange("p (h f) -> p h f", h=16)
                kv4 = raw[0:n, :].rearrange("p (h f) -> p h f", h=16)
                sqk = sqj[0:n, 0:1024].rearrange("p (h f) -> p h f", h=16)
                kv5 = raw[0:n, :].rearrange("p (c e f) -> p c e f", c=8, e=2)

                def head_T(sg, dstv):
                    for half in range(2):
                        for hh in range(8):
                            h = half * 8 + hh
                            kb.op(kb.pe, lambda: T.transpose(tph[0:96, hh, 0:n], qbf[0:n, h * 96:(h + 1) * 96], self.ident[0:n, 0:n]),
                                  r=[qbf_b, self.cb_], w=[tph_b] if hh == 0 else [], wp=[tph_b] if hh else [], inc=(hh == 7))
                        kb.op(kb.act, lambda: A.copy(b.stg[sg][0:96, half * 8:(half + 1) * 8, 0:n], tph[0:96, :, 0:n]),
                              r=[tph_b], w=[b.stg_b[sg]] if half == 0 else [], wp=[b.stg_b[sg]] if half else [])
                    kb.dma("sp", dstv[:, :, t0:t0 + n], b.stg[sg][0:96, :, 0:n], r=[b.stg_b[sg]])

                def rope(t1, t2, cosb, sinb, o1, o2, three_d, wb_, rd_bufs):
                    if three_d:
                        a_, b_, c_, d_ = [rt[0:n, i, :].rearrange("p (h f) -> p h f", h=16) for i in range(4)]
                    else:
                        a_, b_, c_, d_ = [rt[0:n, i, 0:16] for i in range(4)]
                    kb.op(kb.dve, lambda: V.tensor_tensor(a_, t1, cosb, ALU.mult), r=rd_bufs, w=[rt_b[0]])
                    kb.op(kb.dve, lambda: V.tensor_tensor(b_, t2, sinb, ALU.mult), r=rd_bufs, w=[rt_b[1]])
                    kb.op(kb.dve, lambda: V.tensor_tensor(c_, t1, sinb, ALU.mult), r=rd_bufs, w=[rt_b[2]])
                    kb.op(kb.dve, lambda: V.tensor_tensor(d_, t2, cosb, ALU.mult), r=rd_bufs, w=[rt_b[3]])
                    kb.op(kb.dve, lambda: V.tensor_tensor(o1, a_, b_, ALU.subtract), r=[rt_b[0], rt_b[1]], wp=[wb_])
                    kb.op(kb.dve, lambda: V.tensor_tensor(o2, c_, d_, ALU.add), r=[rt_b[2], rt_b[3]], wp=[wb_])

                def s0():
                    kb.dma("sp", b.hs[0:n, :], self.h_src(src, t0, n), w=[b.hs_b])

                def s0b():
                    kb.dma("sp", b.cs[0:n, 0, :], self.cos_d[t0:t0 + n, :], w=[b.cs_b])
                    kb.dma("sp", b.cs[0:n, 1, :], self.sin_d[t0:t0 + n, :], wp=[b.cs_b])

                def s1():
                    self.norm_T(b.hs_b, b.hs[0:n, :], n, self.lnT[:, li, :], xnT[:, :, 0:n], b.xnT_b, nsc)

                def s2():
                    for c in range(8):
                        kb.op(kb.pe, lambda: T.matmul(latA[0:n, 0:384], xnT[:, c, 0:n], w_in[:, c, 0:384], start=(c == 0), stop=(c == 7)),
                              r=[b.xnT_b, wb], w=[latA_b] if c == 0 else [], wp=[latA_b] if c else [], inc=(c == 7))
                    for c in range(8):
                        kb.op(kb.pe, lambda: T.matmul(latB[0:n, 0:288], xnT[:, c, 0:n], w_in[:, c, 384:672], start=(c == 0), stop=(c == 7)),
                              r=[b.xnT_b, wb], w=[latB_b] if c == 0 else [], wp=[latB_b] if c else [], inc=(c == 7))
                    kb.op(kb.act, lambda: A.activation(out=sqj[0:n, 0:384], in_=latA[0:n, 0:384], func=AF.Square, accum_out=st[0:n, 0:1]),
                          r=[latA_b], w=[sqj_b, st_b])
                    kb.op(kb.act, lambda: A.activation(out=sqj[0:n, 512:768], in_=latB[0:n, 0:256], func=AF.Square, accum_out=st[0:n, 3:4]),
                          r=[latB_b], wp=[sqj_b, st_b])
                    self.rstd_ops(st, st_b, n, 0, 1.0 / 384)
                    self.rstd_ops(st, st_b, n, 3, 1.0 / 256)
                    kb.op(kb.dve, lambda: V.tensor_scalar(qln[0:n, :], latA[0:n, 0:384], st[0:n, 2:3], None, ALU.mult), r=[latA_b, st_b], w=[ln_b])
                    kb.op(kb.dve, lambda: V.tensor_scalar(kvln[0:n, :], latB[0:n, 0:256], st[0:n, 5:6], None, ALU.mult), r=[latB_b, st_b], wp=[ln_b])
                    kb.op(kb.act, lambda: A.copy(kpe[0:n, :], latB[0:n, 256:288]), r=[latB_b], wp=[ln_b])

                def s3():
                    for c in range(5):
                        srcap = qln[0:n, c * 128:(c + 1) * 128] if c < 3 else kvln[0:n, (c - 3) * 128:(c - 2) * 128]
                        kb.op(kb.pe, lambda: T.transpose(tp5[:, c, 0:n], srcap, self.ident[0:n, 0:n]),
                              r=[ln_b, self.cb_], w=[tp_b] if c == 0 else [], wp=[tp_b] if c else [], inc=(c == 4))
                    kb.op(kb.dve, lambda: V.tensor_tensor(qlT[:, :, 0:n], tp5[:, 0:3, 0:n], bc(qag[:, :], [128, 3, n], 2), ALU.mult),
                          r=[tp_b, par_b], w=[lT_b])
                    kb.op(kb.dve, lambda: V.tensor_tensor(kvlT[:, :, 0:n], tp5[:, 3:5, 0:n], bc(kvag[:, :], [128, 2, n], 2), ALU.mult),
                          r=[tp_b, par_b], wp=[lT_b])

                def s4():
                    for ct in range(3):
                        for kc in range(3):
                            kb.op(kb.pe, lambda: T.matmul(big[0:n, ct * 512:(ct + 1) * 512], qlT[:, kc, 0:n], w_qb[:, kc, ct * 512:(ct + 1) * 512],
                                                          start=(kc == 0), stop=(kc == 2)),
                                  r=[lT_b, wb], w=[big_b[ct]] if kc == 0 else [], wp=[big_b[ct]] if kc else [], inc=(kc == 2))
                    for ct in range(3):
                        E_ = kb.act if ct != 1 else kb.dve
                        fn = (lambda: A.copy(raw[0:n, ct * 512:(ct + 1) * 512], big[0:n, ct * 512:(ct + 1) * 512])) if ct != 1 else \
                             (lambda: V.tensor_copy(raw[0:n, ct * 512:(ct + 1) * 512], big[0:n, ct * 512:(ct + 1) * 512]))
                        kb.op(E_, fn, r=[big_b[ct]], w=[raw_b] if ct == 0 else [], wp=[raw_b] if ct else [])

                def s5():
                    kb.op(kb.dve, lambda: V.tensor_tensor(sqj[0:n, 0:1536], raw[0:n, 0:1536], raw[0:n, 0:1536], ALU.mult), r=[raw_b], w=[sqj_b])
                    kb.op(kb.dve, lambda: V.tensor_reduce(s16[0:n, 0, :], sq3, AX.X, ALU.add), r=[sqj_b], w=[s16_b])
                    kb.op(kb.act, lambda: A.activation(out=s16[0:n, 1, :], in_=s16[0:n, 0, :], func=AF.Ln, bias=EPS, scale=1.0 / 96), r=[s16_b], wp=[s16_b])
                    kb.op(kb.act, lambda: A.activation(out=s16[0:n, 2, :], in_=s16[0:n, 1, :], func=AF.Exp, scale=-0.5), r=[s16_b], wp=[s16_b])
                    kb.op(kb.dve, lambda: V.tensor_tensor(raw3, raw3, bc(s16[0:n, 2, :], [n, 16, 96], 2), ALU.mult), r=[s16_b], w=[raw_b])
                    kb.op(kb.dve, lambda: V.tensor_tensor(raw3, raw3, bc(gq[0:n, :], [n, 16, 96], 1), ALU.mult), r=[par_b], w=[raw_b])
                    cosb = bc(b.cs[0:n, 0, :], [n, 16, 16], 1)
                    sinb = bc(b.cs[0:n, 1, :], [n, 16, 16], 1)
                    kb.op(kb.act, lambda: A.copy(qb3[:, :, 0:64], raw3[:, :, 0:64]), r=[raw_b], w=[qbf_b])
                    rope(raw3[:, :, 64:80], raw3[:, :, 80:96], cosb, sinb, qb3[:, :, 64:80], qb3[:, :, 80:96], True, qbf_b, [raw_b, b.cs_b])

                def s6():
                    head_T(0, qTv)

                def s7():
                    for ct in range(4):
                        for kc in range(2):
                            kb.op(kb.pe, lambda: T.matmul(big[0:n, ct * 512:(ct + 1) * 512], kvlT[:, kc, 0:n], w_kvb[:, kc, ct * 512:(ct + 1) * 512],
                                                          start=(kc == 0), stop=(kc == 1)),
                                  r=[lT_b, wb], w=[big_b[ct]] if kc == 0 else [], wp=[big_b[ct]] if kc else [], inc=(kc == 1))
                    for ct in range(4):
                        E_ = kb.act if ct % 2 == 0 else kb.dve
                        fn = (lambda: A.copy(raw[0:n, ct * 512:(ct + 1) * 512], big[0:n, ct * 512:(ct + 1) * 512])) if ct % 2 == 0 else \
                             (lambda: V.tensor_copy(raw[0:n, ct * 512:(ct + 1) * 512], big[0:n, ct * 512:(ct + 1) * 512]))
                        kb.op(E_, fn, r=[big_b[ct]], w=[raw_b] if ct == 0 else [], wp=[raw_b] if ct else [])

                def s8():
                    kb.op(kb.dve, lambda: V.tensor_tensor(sqk, kv4[:, :, 0:64], kv4[:, :, 0:64], ALU.mult), r=[raw_b], w=[sqj_b])
                    kb.op(kb.dve, lambda: V.tensor_reduce(s16[0:n, 0, :], sqk, AX.X, ALU.add), r=[sqj_b], w=[s16_b])
                    kb.op(kb.act, lambda: A.activation(out=kpg[0:n, 1, :], in_=kpe[0:n, :], func=AF.Square, accum_out=st[0:n, 6:7]),
                          r=[ln_b], w=[kpg_b], wp=[st_b])
                    kb.op(kb.dve, lambda: V.tensor_scalar(s16[0:n, 0, :], s16[0:n, 0, :], st[0:n, 6:7], None, ALU.add), r=[st_b, s16_b], wp=[s16_b])
                    kb.op(kb.act, lambda: A.activation(out=s16[0:n, 1, :], in_=s16[0:n, 0, :], func=AF.Ln, bias=EPS, scale=1.0 / 96), r=[s16_b], wp=[s16_b])
                    kb.op(kb.act, lambda: A.activation(out=s16[0:n, 2, :], in_=s16[0:n, 1, :], func=AF.Exp, scale=-0.5), r=[s16_b], wp=[s16_b])
                    kb.op(kb.act, lambda: A.copy(b.vst[0:n, :, 0:64], kv5[:, :, 0, 64:128]), r=[raw_b, b.vst_b], wp=[b.vst_b])
                    kb.op(kb.dve, lambda: V.tensor_copy(b.vst[0:n, :, 128:192], kv5[:, :, 1, 64:128]), r=[raw_b, b.vst_b], wp=[b.vst_b])
                    kb.dma("sp", self.va_d[t0:t0 + n, :, :], b.vst[0:n, :, :], r=[b.vst_b])
                    kb.op(kb.dve, lambda: V.tensor_tensor(kv4[:, :, 0:64], kv4[:, :, 0:64], bc(s16[0:n, 2, :], [n, 16, 64], 2), ALU.mult),
                          r=[s16_b], w=[raw_b])
                    kb.op(kb.dve, lambda: V.tensor_tensor(qb3[:, :, 0:64], kv4[:, :, 0:64], bc(gk[0:n, 0:64], [n, 16, 64], 1), ALU.mult),
                          r=[raw_b, par_b], w=[qbf_b])
                    kb.op(kb.dve, lambda: V.tensor_tensor(kpg[0:n, 0, :], kpe[0:n, :], gk[0:n, 64:96], ALU.mult), r=[ln_b, par_b], w=[kpg_b])
                    rope(kpg[0:n, 0, 0:16], kpg[0:n, 0, 16:32], b.cs[0:n, 0, :], b.cs[0:n, 1, :], kpg[0:n, 1, 0:16], kpg[0:n, 1, 16:32],
                         False, kpg_b, [kpg_b, b.cs_b])
                    kb.op(kb.dve, lambda: V.tensor_tensor(qb3[:, :, 64:96], bc(kpg[0:n, 1, :], [n, 16, 32], 1), bc(s16[0:n, 2, :], [n, 16, 32], 2), ALU.mult),
                          r=[kpg_b, s16_b], wp=[qbf_b])

                def s9():
                    head_T(1, kTv)

                return [s0, s0b, s1, s2, s3, s4, s5, s6, s7, s8, s9]

            chains = [chain(k) for k in range(len(TILES))]
            NTL = len(TILES)
            for k in (0, 1):
                chains[k][0]()
                chains[k][1]()
            for k in range(0, NTL, 2):
                pair = [chains[k]] + ([chains[k + 1]] if k + 1 < NTL else [])
                nxt = [chains[k2] for k2 in (k + 2, k + 3) if k2 < NTL]
                for i in range(2, 11):
                    for c_ in pair:
                        c_[i]()
                    if i == 2:
                        for c_ in nxt:
                            c_[0]()
                    if i == 9:
                        for c_ in nxt:
                            c_[1]()
            kb.barrier()
        with contextlib.ExitStack() as s2:
            qh = [self.S(s2, "b_q%d" % i, [96, NT], BF16) for i in range(2)]
            kh = [self.S(s2, "b_k%d" % i, [96, NT], BF16) for i in range(2)]
            qk_b = [Buf(), Buf()]
            va = [self.S(s2, "b_va%d" % i, [128, 33, 192], BF16) for i in range(2)]
            va_b = [Buf(), Buf()]
            NPS = 6
            pT = [self.S(s2, "b_pT%d" % i, [128, 512], BF16) for i in range(NPS)]
            pT_b = [Buf() for _ in range(NPS)]
            rden = [self.S(s2, "b_rd%d" % i, [128, 512], F32) for i in range(2)]
            rdsh = [self.S(s2, "b_rs%d" % i, [128, 512], F32) for i in range(2)]
            rd_b = [Buf(), Buf()]
            rs_b = [Buf(), Buf()]
            bnd = self.S(s2, "b_bnd", [128, 8], F32)
            bnd_b = Buf()
            ps = [self.P(s2, "b_ps%d" % i, [128, 512], F32) for i in range(NPS)]
            ps_b = [PB() for _ in range(NPS)]
            po = [self.P(s2, "b_po%d" % i, [128, 512], F32) for i in range(2)]
            po_b = [PB(), PB()]
            kb.op(kb.dve, lambda: V.tensor_reduce(bnd[:, 0:1], gq[:, :], AX.X, ALU.max), r=[par_b], w=[bnd_b])
            kb.op(kb.dve, lambda: V.tensor_reduce(bnd[:, 1:2], gq[:, :], AX.X, ALU.min), r=[par_b], wp=[bnd_b])
            kb.op(kb.dve, lambda: V.tensor_reduce(bnd[:, 2:3], gk[:, :], AX.X, ALU.max), r=[par_b], wp=[bnd_b])
            kb.op(kb.dve, lambda: V.tensor_reduce(bnd[:, 3:4], gk[:, :], AX.X, ALU.min), r=[par_b], wp=[bnd_b])
            kb.op(kb.dve, lambda: V.scalar_tensor_tensor(bnd[:, 4:5], bnd[:, 1:2], -1.0, bnd[:, 0:1], ALU.mult, ALU.max), r=[bnd_b], wp=[bnd_b])
            kb.op(kb.dve, lambda: V.scalar_tensor_tensor(bnd[:, 5:6], bnd[:, 3:4], -1.0, bnd[:, 2:3], ALU.mult, ALU.max), r=[bnd_b], wp=[bnd_b])
            kb.op(kb.dve, lambda: V.scalar_tensor_tensor(bnd[:, 6:7], bnd[:, 4:5], -float(np.sqrt(96.0)), bnd[:, 5:6], ALU.mult, ALU.mult),
                  r=[bnd_b], wp=[bnd_b])
            negB = bnd[:, 6:7]
            scale = float(96.0 ** -0.5)

            def load_head(h):
                b = h % 2
                kb.dma("sp", qh[b][:, :], self.qT_d[h], w=[qk_b[b]])
                kb.dma("sp", kh[b][:, :], self.kT_d[h], wp=[qk_b[b]])

            def load_pair(c):
                b = c % 2
                kb.dma("sp", va[b][0:16, 0, :], self.va_d[0:16, c, :], w=[va_b[b]])
                vv = self.va_d[16:NT, c, :].rearrange("(j p) w -> p j w", p=128)
                for jj in range(0, 32, 8):
                    kb.dma("sp", va[b][:, 1 + jj:9 + jj, :], vv[:, jj:jj + 8, :], wp=[va_b[b]])

            load_pair(0)
            load_head(0)
            items = []
            for h in range(MLA_H):
                for qi in range(9):
                    if qi == 0:
                        q0, nq = 0, 16
                        kts = [(0, 0, 16, 0, True)]
                    else:
                        q0, nq = 16 + 512 * (qi - 1), 512
                        kts = [(0, 0, 16, 0, False)] + [(kt, 16 + 128 * (kt - 1), 128, 0, False) for kt in range(1, 4 * (qi - 1) + 1)]
                        kts += [(4 * (qi - 1) + 1 + i, 16 + 128 * (4 * (qi - 1) + i), 128, 128 * i, True) for i in range(4)]
                    for idx, kt in enumerate(kts):
                        items.append((h, qi, q0, nq, idx, len(kts)) + kt)
            LA = 3
            NI = len(items)
            for i in range(NI + LA):
                if i < NI:
                    (h, qi, q0, nq, idx, nk_t, kt, k0, nk, qoff, diag) = items[i]
                    if qi == 0 and idx == 0 and h + 1 < MLA_H:
                        load_head(h + 1)
                        if h % 2 == 1:
                            load_pair(h // 2 + 1)
                    hb_ = h % 2
                    nqq = nq - qoff
                    p = i % NPS
                    kb.op(kb.pe, lambda: T.matmul(ps[p][0:nk, 0:nqq], kh[hb_][:, k0:k0 + nk], qh[hb_][:, q0 + qoff:q0 + nq], start=True, stop=True),
                          r=[qk_b[hb_]], w=[ps_b[p]])
                    kb.op(kb.act, lambda: A.activation(out=pT[p][0:nk, 0:nqq], in_=ps[p][0:nk, 0:nqq], func=AF.Exp, bias=negB[0:nk, :], scale=scale),
                          r=[ps_b[p], bnd_b], w=[pT_b[p]])
                    if diag:
                        kb.op(kb.dve, lambda: V.tensor_tensor(pT[p][0:nk, 0:nk], pT[p][0:nk, 0:nk], self.tri16[0:nk, 0:nk], ALU.mult),
                              r=[self.cb_], w=[pT_b[p]])
                ii = i - LA
                if ii >= 0:
                    (h, qi, q0, nq, idx, nk_t, kt, k0, nk, qoff, diag) = items[ii]
                    c, e = h // 2, h % 2
                    vb_ = c % 2
                    dlo, dhi = (0, 64) if e == 0 else (64, 128)
                    nlo, nhi = (64, 128) if e == 0 else (0, 64)
                    nqq = nq - qoff
                    p = ii % NPS
                    pp = (h * 9 + qi) % 2
                    first, last = idx == 0, idx == nk_t - 1
                    kb.op(kb.pe, lambda: T.matmul(po[pp][:, qoff:nq], va[vb_][0:nk, kt, e * 64:e * 64 + 128], pT[p][0:nk, 0:nqq], start=first, stop=last),
                          r=[pT_b[p], va_b[vb_]], w=[po_b[pp]] if first else [], wp=[] if first else [po_b[pp]], inc=last)
                    if last:
                        kb.op(kb.dve, lambda: V.reciprocal(rden[pp][nlo:nhi, 0:nq], po[pp][nlo:nhi, 0:nq]), r=[po_b[pp]], w=[rd_b[pp]])
                        kb.op(kb.dve, lambda: V.tensor_copy(rdsh[pp][dlo:dhi, 0:nq], rden[pp][nlo:nhi, 0:nq]), r=[rd_b[pp]], w=[rs_b[pp]])
                        kb.op(kb.dve, lambda: V.tensor_tensor(oT[dlo:dhi, c, q0:q0 + nq], po[pp][dlo:dhi, 0:nq], rdsh[pp][dlo:dhi, 0:nq], ALU.mult),
                              r=[po_b[pp], rs_b[pp]], wp=[oT_b])
            kb.barrier()
        with contextlib.ExitStack() as s3:
            hs = [self.S(s3, "c_h%d" % i, [128, D], F32) for i in range(3)]
            hs_b = [Buf() for _ in range(3)]
            po = [self.P(s3, "c_po%d" % i, [128, 512], F32) for i in range(4)]
            po_b = [PB() for _ in range(4)]
            ip = 0
            for ti, (t0, n) in enumerate(TILES):
                s = ti % 3
                kb.dma("sp", hs[s][0:n, :], self.h_src(src, t0, n), w=[hs_b[s]])
                for hh in range(2):
                    p = ip % 4
                    ip += 1
                    for c in range(8):
                        kb.op(kb.pe, lambda: T.matmul(po[p][0:n, :], oT[:, c, t0:t0 + n], w_o[:, c, hh * 512:(hh + 1) * 512], start=(c == 0), stop=(c == 7)),
                              r=[oT_b, w_o_b], w=[po_b[p]] if c == 0 else [], wp=[po_b[p]] if c else [], inc=(c == 7))
                    kb.op(kb.dve, lambda: V.tensor_tensor(hs[s][0:n, hh * 512:(hh + 1) * 512], po[p][0:n, :], hs[s][0:n, hh * 512:(hh + 1) * 512], ALU.add),
                          r=[po_b[p], hs_b[s]], wp=[hs_b[s]])
                dap = self.h_dst(dst, t0, n)
                if dap is not None:
                    kb.dma("sp", dap, hs[s][0:n, :], r=[hs_b[s]])


def host_inputs(inp):
    f = lambda a: np.ascontiguousarray(np.asarray(a, dtype=np.float32))
    rep = lambda a: np.ascontiguousarray(np.broadcast_to(np.asarray(a, np.float32)[:, None, :], (a.shape[0], 128, a.shape[1])))
    colT = lambda a, c: np.ascontiguousarray(np.asarray(a, np.float32).reshape(a.shape[0], c, 128).transpose(0, 2, 1))
    ln = np.concatenate([np.asarray(inp["ln_mix"], np.float32), np.asarray(inp["ln_mlp"], np.float32)], 0)
    lnT = np.ascontiguousarray(ln.reshape(8, 8, 128).transpose(2, 0, 1))
    cw = np.asarray(inp["ssd_conv_w"], np.float32)
    cwT = np.ascontiguousarray(cw.reshape(2, 4, 32, 128).transpose(0, 3, 2, 1))
    k = np.arange(128)
    tri = (k[:, None] <= k[None, :]).astype(np.float32)
    inv = 1.0 / (10000.0 ** (np.arange(0, 32, 2, dtype=np.float32) / 32.0))
    ang = np.arange(NT, dtype=np.float32)[:, None] * inv[None, :].astype(np.float32)
    common = {
        "meta": f(inp["meta_tokens"]),
        "ssd_w_in": f(inp["ssd_w_in"]), "ssd_w_out": f(inp["ssd_w_out"]),
        "mla_w_in": f(inp["mla_w_in"]), "mla_w_q_b": f(inp["mla_w_q_b"]), "mla_w_kv_b": f(inp["mla_w_kv_b"]),
        "mla_w_out": f(inp["mla_w_out"]), "mlp_w_up": f(inp["mlp_w_up"]), "mlp_w_down": f(inp["mlp_w_down"]),
        "lnT": lnT, "cw": cwT, "cb": colT(inp["ssd_conv_b"], 32),
        "dtb_rep": rep(inp["ssd_dt_bias"]), "alog_rep": rep(inp["ssd_a_log"]), "dskip_rep": rep(inp["ssd_d"]),
        "ssd_normT": colT(inp["ssd_norm"], 16), "q_a_T": colT(inp["mla_q_a_norm"], 3), "kv_a_T": colT(inp["mla_kv_a_norm"], 2),
        "gq_rep": rep(inp["mla_q_norm"]), "gk_rep": rep(inp["mla_k_norm"]),
        "ident": np.eye(128, dtype=np.float32), "tri": tri, "ltri": np.ascontiguousarray(1.0 - tri),
        "ones": np.ones((128, 128), np.float32),
        "cos": np.cos(ang).astype(np.float32), "sin": np.sin(ang).astype(np.float32),
    }
    return common


FULL_PHASES = [
    ("ssd", 0, 0, "x", "h"), ("mlp", 0, "h", "h"),
    ("mla", 0, 1, "h", "h"), ("mlp", 1, "h", "h"),
    ("ssd", 1, 2, "h", "h"), ("mlp", 2, "h", "h"),
    ("mla", 1, 3, "h", "h"), ("mlp", 3, "h", "y"),
]


def run(inputs, phases, cores=8):
    common = host_inputs(inputs)
    x = np.asarray(inputs["x"], np.float32)
    prog = Prog(phases)
    in_maps = []
    for c in range(cores):
        m = dict(common)
        m["x"] = np.ascontiguousarray(x[c])
        in_maps.append(m)
    res = run_bass_kernel_spmd(prog.nc, in_maps, core_ids=list(range(cores)))
    return np.stack([np.asarray(r["y"]) for r in res.results], 0)


def kernel(**inputs):
    return run(inputs, FULL_PHASES, 8).astype(np.float32)
```

```python
import contextlib
import numpy as np
import concourse.bass as bass
import concourse.mybir as mybir
from concourse.bass_utils import run_bass_kernel_spmd

F32, BF16 = mybir.dt.float32, mybir.dt.bfloat16
AF = mybir.ActivationFunctionType
ALU = mybir.AluOpType
AX = mybir.AxisListType

NT, NM, D, SEQ = 4112, 16, 1024, 4096
TILES = [(0, 16)] + [(16 + 128 * j, 128) for j in range(32)]
EPS = 1e-6
DFF = 4096
SSD_IN = 6176
NH_S = 32
SSD_LEAD = 0.8
MLA_H = 16
QK = 96


class Buf:
    __slots__ = ("w", "r", "name", "ps")

    def __init__(self, name="", ps=False):
        self.w = {}
        self.r = {}
        self.name = name
        self.ps = ps


def PB():
    return Buf(ps=True)


class Eng:
    def __init__(self, name, e, sem):
        self.name, self.e, self.sem, self.cnt, self.seen = name, e, sem, 0, {}


class KB:
    def __init__(self, nc, es):
        self.nc = nc
        mk = lambda n: es.enter_context(nc.semaphore(n))
        self.pe = Eng("pe", nc.tensor, mk("s_pe"))
        self.act = Eng("act", nc.scalar, mk("s_act"))
        self.dve = Eng("dve", nc.vector, mk("s_dve"))
        self.pool = Eng("pool", nc.gpsimd, mk("s_pool"))
        self.sp = Eng("sp", nc.sync, mk("s_sp"))
        self.engs = [self.pe, self.act, self.dve, self.pool, self.sp]
        self.dsem = {"sp": [[mk("d_sp%d" % i), 0] for i in range(24)],
                     "pool": [[mk("d_pl%d" % i), 0] for i in range(8)]}
        self.drr = {"sp": 0, "pool": 0}
        self.nins = 0

    def _wait(self, E, toks):
        for key, (sem, val) in toks.items():
            if E is self.pe and key == "pe":
                continue
            if E.seen.get(key, 0) >= val:
                continue
            E.e.wait_ge(sem, val)
            E.seen[key] = val

    @staticmethod
    def _add(need, d):
        for k, sv in d.items():
            if k not in need or need[k][1] < sv[1]:
                need[k] = sv

    def _deps(self, r, w, wp, ekey=None):
        need = {}
        for b in r:
            self._add(need, b.w)
            if b.ps:
                self._add(need, {k: v for k, v in b.r.items() if k != ekey})
        for b in w:
            self._add(need, b.w)
            self._add(need, b.r)
        for b in wp:
            self._add(need, b.r)
            if b.ps:
                self._add(need, b.w)
        return need

    def _reg(self, key, tok, r, w, wp):
        for b in r:
            if key not in b.r or b.r[key][1] < tok[1]:
                b.r[key] = tok
        for b in w:
            b.w = {key: tok}
            b.r = {}
        for b in wp:
            if key not in b.w or b.w[key][1] < tok[1]:
                b.w[key] = tok

    def op(self, E, fn, r=(), w=(), wp=(), inc=True):
        self._wait(E, self._deps(r, w, wp, E.name))
        ins = fn()
        self.nins += 1
        if inc:
            E.cnt += 1
            ins.then_inc(E.sem, 1)
            tok = (E.sem, E.cnt)
        else:
            tok = (E.sem, E.cnt + 1)
        self._reg(E.name, tok, r, w, wp)

    def dma(self, Q, out, in_, r=(), w=(), wp=()):
        E = self.sp if Q == "sp" else self.pool
        self._wait(E, self._deps(r, w, wp))
        lst = self.dsem[Q]
        i = self.drr[Q]
        self.drr[Q] = (i + 1) % len(lst)
        sem, cnt = lst[i]
        key = (Q, i)
        if cnt > 0 and E.seen.get(key, 0) < cnt:
            E.e.wait_ge(sem, cnt)
            E.seen[key] = cnt
        ins = E.e.dma_start(out=out, in_=in_)
        ins.then_inc(sem, 16)
        self.nins += 1
        lst[i][1] = cnt + 16
        self._reg(key, (sem, cnt + 16), r, w, wp)

    def barrier(self):
        toks = {}
        for E in self.engs:
            if E.cnt > 0:
                toks[E.name] = (E.sem, E.cnt)
        for Q, lst in self.dsem.items():
            for i, (sem, cnt) in enumerate(lst):
                if cnt > 0:
                    toks[(Q, i)] = (sem, cnt)
        for E in self.engs:
            self._wait(E, toks)


def bc(ap, shape, axis):
    return ap.unsqueeze(axis).to_broadcast(list(shape))


class Prog:
    def __init__(self, phases):
        self.phases = phases
        nc = bass.Bass("TRN2", target_bir_lowering=False)
        self.nc = nc
        di = lambda name, shape: nc.dram_tensor(name, list(shape), F32, kind="ExternalInput").ap()
        self.x = di("x", [SEQ, D])
        self.meta = di("meta", [NM, D])
        self.w_ssd_in = di("ssd_w_in", [2, D, SSD_IN])
        self.w_ssd_out = di("ssd_w_out", [2, 2048, D])
        self.w_mla_in = di("mla_w_in", [2, D, 672])
        self.w_mla_qb = di("mla_w_q_b", [2, 384, 1536])
        self.w_mla_kvb = di("mla_w_kv_b", [2, 256, 2048])
        self.w_mla_out = di("mla_w_out", [2, D, D])
        self.w_up = di("mlp_w_up", [4, D, DFF])
        self.w_dn = di("mlp_w_down", [4, DFF, D])
        self.lnT_d = di("lnT", [128, 8, 8])
        self.cw_d = di("cw", [2, 128, 32, 4])
        self.cb_d = di("cb", [2, 128, 32])
        self.dtb_d = di("dtb_rep", [2, 128, 32])
        self.alog_d = di("alog_rep", [2, 128, 32])
        self.dsk_d = di("dskip_rep", [2, 128, 32])
        self.sng_d = di("ssd_normT", [2, 128, 16])
        self.qag_d = di("q_a_T", [2, 128, 3])
        self.kvag_d = di("kv_a_T", [2, 128, 2])
        self.gq_d = di("gq_rep", [2, 128, 96])
        self.gk_d = di("gk_rep", [2, 128, 96])
        self.ident_d = di("ident", [128, 128])
        self.tri_d = di("tri", [128, 128])
        self.ltri_d = di("ltri", [128, 128])
        self.ones_d = di("ones", [128, 128])
        self.cos_d = di("cos", [NT, 16])
        self.sin_d = di("sin", [NT, 16])
        self.y = nc.dram_tensor("y", [SEQ, D], F32, kind="ExternalOutput").ap()
        self.hd = nc.dram_tensor("hd", [NT, D], F32, kind="Internal").ap()
        self.qT_d = nc.dram_tensor("qT_d", [MLA_H, QK, NT], BF16, kind="Internal").ap()
        self.kT_d = nc.dram_tensor("kT_d", [MLA_H, QK, NT], BF16, kind="Internal").ap()
        self.va_d = nc.dram_tensor("va_d", [NT, 8, 192], BF16, kind="Internal").ap()

        with contextlib.ExitStack() as es:
            self.kb = KB(nc, es)
            self.build(es)

    def S(self, es, name, shape, dt):
        self.uid = getattr(self, "uid", 0) + 1
        return es.enter_context(self.nc.sbuf_tensor("sb%d_%s" % (self.uid, name), list(shape), dt))

    def P(self, es, name, shape, dt):
        self.uid = getattr(self, "uid", 0) + 1
        return es.enter_context(self.nc.psum_tensor("ps%d_%s" % (self.uid, name), list(shape), dt))

    def h_src(self, kind, t0, n):
        if kind == "x":
            return self.meta[0:16, :] if t0 == 0 else self.x[t0 - 16:t0 - 16 + n, :]
        return self.hd[t0:t0 + n, :]

    def h_dst(self, kind, t0, n):
        if kind == "y":
            return None if t0 == 0 else self.y[t0 - 16:t0 - 16 + n, :]
        return self.hd[t0:t0 + n, :]

    def build(self, es):
        kb, nc = self.kb, self.nc
        self.ident = self.S(es, "ident", [128, 128], BF16)
        self.tri32 = self.S(es, "tri32", [128, 128], F32)
        self.ltri32 = self.S(es, "ltri32", [128, 128], F32)
        self.ones32 = self.S(es, "ones32", [128, 128], F32)
        self.lnT = self.S(es, "lnT", [128, 8, 8], F32)
        self.cb_ = Buf("consts")
        kb.dma("pool", self.ident[:], self.ident_d, wp=[self.cb_])
        kb.dma("sp", self.tri32[:], self.tri_d, wp=[self.cb_])
        kb.dma("sp", self.ltri32[:], self.ltri_d, wp=[self.cb_])
        kb.dma("sp", self.ones32[:], self.ones_d, wp=[self.cb_])
        kb.dma("sp", self.lnT[:], self.lnT_d, wp=[self.cb_])
        for ph in self.phases:
            kind = ph[0]
            with contextlib.ExitStack() as pes:
                if kind == "mlp":
                    self.phase_mlp(pes, *ph[1:])
                elif kind == "ssd":
                    self.phase_ssd(pes, *ph[1:])
                elif kind == "mla":
                    self.phase_mla(pes, *ph[1:])
                kb.barrier()
        kb.barrier()

    def rstd_newton(self, st, st_b, n, inv_n, eps):
        kb, nc = self.kb, self.nc
        V = nc.vector
        I32 = mybir.dt.int32
        for f in self.rstd_newton_ops(st, st_b, n, inv_n, eps):
            f()

    def rstd_newton_ops(self, st, st_b, n, inv_n, eps):
        kb, nc = self.kb, self.nc
        V = nc.vector
        I32 = mybir.dt.int32
        x, g, t = st[0:n, 1:2], st[0:n, 2:3], st[0:n, 3:4]
        ops = []
        ops.append(lambda: kb.op(kb.dve, lambda: V.tensor_scalar(x, st[0:n, 0:1], inv_n, eps, ALU.mult, ALU.add), r=[st_b], wp=[st_b]))
        ops.append(lambda: kb.op(kb.dve, lambda: V.tensor_scalar(g.bitcast(I32), x.bitcast(I32), 1, None, ALU.arith_shift_right), r=[st_b], wp=[st_b]))
        ops.append(lambda: kb.op(kb.dve, lambda: V.tensor_scalar(g.bitcast(I32), g.bitcast(I32), -1, 0x5f3759df, ALU.mult, ALU.add), r=[st_b], wp=[st_b]))
        for _ in range(2):
            ops.append(lambda: kb.op(kb.dve, lambda: V.scalar_tensor_tensor(t, g, x, g, ALU.mult, ALU.mult), r=[st_b], wp=[st_b]))
            ops.append(lambda: kb.op(kb.dve, lambda: V.tensor_scalar(t, t, -0.5, 1.5, ALU.mult, ALU.add), r=[st_b], wp=[st_b]))
            ops.append(lambda: kb.op(kb.dve, lambda: V.tensor_tensor(g, g, t, ALU.mult), r=[st_b], wp=[st_b]))
        return ops

    def norm_T(self, hb, h_ap, n, gain_ap, dst_ap, dst_b, sc, newton=False):
        kb, nc = self.kb, self.nc
        junk, junk_b, ss, ss_b, xn, xn_b, tp, tp_b = sc
        kb.op(kb.act, lambda: nc.scalar.activation(out=junk[0:n, :], in_=h_ap, func=AF.Square, accum_out=ss[0:n, 0:1]),
              r=[hb], w=[junk_b, ss_b] if junk_b is not xn_b else [xn_b, ss_b])
        if newton:
            self.rstd_newton(ss, ss_b, n, 1.0 / D, EPS)
        else:
            kb.op(kb.act, lambda: nc.scalar.activation(out=ss[0:n, 1:2], in_=ss[0:n, 0:1], func=AF.Ln, bias=EPS, scale=1.0 / D),
                  r=[ss_b], wp=[ss_b])
            kb.op(kb.act, lambda: nc.scalar.activation(out=ss[0:n, 2:3], in_=ss[0:n, 1:2], func=AF.Exp, scale=-0.5),
                  r=[ss_b], wp=[ss_b])
        kb.op(kb.dve, lambda: nc.vector.tensor_scalar(xn[0:n, :], h_ap, ss[0:n, 2:3], None, ALU.mult),
              r=[hb, ss_b], w=[xn_b])
        for c in range(8):
            kb.op(kb.pe, lambda c=c: nc.tensor.transpose(tp[:, c, 0:n], xn[0:n, c * 128:(c + 1) * 128], self.ident[0:n, 0:n]),
                  r=[xn_b, self.cb_], w=[tp_b] if c == 0 else [], wp=[tp_b] if c else [], inc=(c == 7))
        kb.op(kb.dve, lambda: nc.vector.tensor_tensor(dst_ap, tp[:, :, 0:n], bc(gain_ap, [128, 8, n], 2), ALU.mult),
              r=[tp_b, self.cb_], w=[dst_b])

    def norm_scratch(self, es, pfx, tp_ps):
        ss = self.S(es, pfx + "ss", [128, 4], F32)
        xn = self.S(es, pfx + "xn", [128, 1024], BF16)
        tp = tp_ps[:].bitcast(BF16)[:, 0:1024].rearrange("p (c t) -> p c t", c=8)
        xn_b = Buf()
        return (xn, xn_b, ss, Buf(), xn, xn_b, tp, PB())

    def phase_mlp(self, es, li, src, dst):
        kb, nc = self.kb, self.nc
        wup = self.S(es, "wup", [128, 8, DFF], BF16)
        wdn = self.S(es, "wdn", [128, 32, D], BF16)
        wup_b = [Buf() for _ in range(8)]
        wdn_b = [Buf() for _ in range(8)]
        upv = self.w_up[li].rearrange("(c p) f -> p c f", p=128)
        dnv = self.w_dn[li].rearrange("(c p) d -> p c d", p=128)
        for i in range(8):
            kb.dma("pool", wup[:, :, i * 512:(i + 1) * 512], upv[:, :, i * 512:(i + 1) * 512], w=[wup_b[i]])
        for i in range(8):
            kb.dma("pool", wdn[:, 4 * i:4 * i + 4, :], dnv[:, 4 * i:4 * i + 4, :], w=[wdn_b[i]])
        NSLOT = 7
        hs = [self.S(es, "mh%d" % i, [128, D], F32) for i in range(NSLOT)]
        hs_b = [Buf() for _ in range(NSLOT)]
        xnT = self.S(es, "m_xnT", [128, 8, 512], BF16)
        xnT_b = Buf()
        uT = self.S(es, "m_uT", [128, 32, 512], BF16)
        uT_b = [Buf() for _ in range(32)]
        r32 = [self.S(es, "m_r32_%d" % i, [128, 512], F32) for i in range(2)]
        r32_b = [Buf(), Buf()]
        tp_ps = [self.P(es, "m_tp%d" % i, [128, 512], F32) for i in range(2)]
        pu = [self.P(es, "m_pu%d" % i, [128, 512], F32) for i in range(3)]
        pu_b = [PB() for _ in range(3)]
        pd = [self.P(es, "m_pd%d" % i, [128, 512], F32) for i in range(3)]
        pd_b = [PB() for _ in range(3)]
        nsc = [self.norm_scratch(es, "m%d" % i, tp_ps[i]) for i in range(2)]
        gain = self.lnT[:, 4 + li, :]
        groups = [[TILES[0]]] + [TILES[1 + 4 * g:5 + 4 * g] for g in range(8)]
        slot = 0
        iu = 0
        ipd = 0
        inorm = 0
        for grp in groups:
            ntok = sum(n for _, n in grp)
            myslots = []
            for (t0, n) in grp:
                s = slot % NSLOT
                slot += 1
                myslots.append(s)
                kb.dma("sp", hs[s][0:n, :], self.h_src(src, t0, n), w=[hs_b[s]])
            off = 0
            for (t0, n), s in zip(grp, myslots):
                self.norm_T(hs_b[s], hs[s][0:n, :], n, gain, xnT[:, :, off:off + n], xnT_b, nsc[inorm % 2])
                inorm += 1
                off += n
            for fc in range(32):
                p = iu % 3
                for c in range(8):
                    kb.op(kb.pe, lambda c=c, fc=fc, p=p: nc.tensor.matmul(pu[p][:, 0:ntok], wup[:, c, fc * 128:(fc + 1) * 128],
                                                                      xnT[:, c, 0:ntok], start=(c == 0), stop=(c == 7)),
                          r=[wup_b[fc // 4], xnT_b], w=[pu_b[p]] if c == 0 else [], wp=[pu_b[p]] if c else [], inc=(c == 7))
                rr = iu % 2
                kb.op(kb.act, lambda p=p, rr=rr: nc.scalar.activation(out=r32[rr][:, 0:ntok], in_=pu[p][:, 0:ntok], func=AF.Relu),
                      r=[pu_b[p]], w=[r32_b[rr]])
                E = kb.dve if (fc % 2 == 0) else kb.pool
                kb.op(E, lambda rr=rr, fc=fc, E=E: E.e.tensor_tensor(uT[:, fc, 0:ntok], r32[rr][:, 0:ntok], r32[rr][:, 0:ntok], ALU.mult),
                      r=[r32_b[rr]], w=[uT_b[fc]])
                iu += 1
            off = 0
            for (t0, n), s in zip(grp, myslots):
                for hh in range(2):
                    p = ipd % 3
                    ipd += 1
                    for fc in range(32):
                        kb.op(kb.pe, lambda fc=fc, p=p, off=off, n=n, hh=hh: nc.tensor.matmul(
                            pd[p][0:n, :], uT[:, fc, off:off + n], wdn[:, fc, hh * 512:(hh + 1) * 512],
                            start=(fc == 0), stop=(fc == 31)),
                            r=[uT_b[fc], wdn_b[fc // 4]], w=[pd_b[p]] if fc == 0 else [], wp=[pd_b[p]] if fc else [], inc=(fc == 31))
                    kb.op(kb.dve, lambda p=p, n=n, hh=hh, s=s: nc.vector.tensor_tensor(
                        hs[s][0:n, hh * 512:(hh + 1) * 512], pd[p][0:n, :], hs[s][0:n, hh * 512:(hh + 1) * 512], ALU.add),
                        r=[pd_b[p], hs_b[s]], wp=[hs_b[s]])
                dst_ap = self.h_dst(dst, t0, n)
                if dst_ap is not None:
                    kb.dma("sp", dst_ap, hs[s][0:n, :], r=[hs_b[s]])
                off += n

    def phase_ssd(self, es, j, li, src, dst):
        from functools import partial
        kb, nc = self.kb, self.nc
        V, A, G, T = nc.vector, nc.scalar, nc.gpsimd, nc.tensor
        w_in = self.S(es, "s_win", [128, 8, SSD_IN], BF16)
        w_out = self.S(es, "s_wout", [128, 16, D], BF16)
        par_b = Buf()
        wdt_b = Buf()
        wx_b = [Buf() for _ in range(8)]
        wz_b = [Buf() for _ in range(4)]
        wout_b = [Buf() for _ in range(4)]
        wv = self.w_ssd_in[j].rearrange("(c p) f -> p c f", p=128)
        kb.dma("pool", w_in[:, :, 6144:6176], wv[:, :, 6144:6176], w=[wdt_b])
        for i in range(8):
            kb.dma("pool", w_in[:, :, 2048 + 512 * i:2560 + 512 * i], wv[:, :, 2048 + 512 * i:2560 + 512 * i], w=[wx_b[i]])
        for i in range(4):
            kb.dma("pool", w_in[:, :, 512 * i:512 * (i + 1)], wv[:, :, 512 * i:512 * (i + 1)], w=[wz_b[i]])
        wov = self.w_ssd_out[j].rearrange("(c p) f -> p c f", p=128)
        for i in range(4):
            kb.dma("pool", w_out[:, 4 * i:4 * i + 4, :], wov[:, 4 * i:4 * i + 4, :], w=[wout_b[i]])
        cw = self.S(es, "s_cw", [128, 32, 4], F32)
        cb = self.S(es, "s_cb", [128, 32], F32)
        dtb = self.S(es, "s_dtb", [128, 32], F32)
        arep = self.S(es, "s_arep", [128, 32], F32)
        dsk = self.S(es, "s_dsk", [128, 32], F32)
        sng = self.S(es, "s_sng", [128, 16], F32)
        kb.dma("sp", cw[:], self.cw_d[j], wp=[par_b])
        kb.dma("sp", cb[:], self.cb_d[j], wp=[par_b])
        kb.dma("sp", dtb[:], self.dtb_d[j], wp=[par_b])
        kb.dma("sp", arep[:], self.alog_d[j], wp=[par_b])
        kb.dma("sp", dsk[:], self.dsk_d[j], wp=[par_b])
        kb.dma("sp", sng[:], self.sng_d[j], wp=[par_b])
        arep_b = Buf()
        kb.op(kb.act, lambda: A.activation(out=arep[:], in_=arep[:], func=AF.Exp), r=[par_b], w=[arep_b])
        kb.op(kb.dve, lambda: V.tensor_scalar(arep[:], arep[:], -1.0, None, ALU.mult), r=[arep_b], w=[arep_b])
        cwb = Buf()
        kb.op(kb.dve, lambda: V.tensor_scalar(cw[:], cw[:], 0.5, None, ALU.mult), r=[par_b], w=[cwb])
        kb.op(kb.dve, lambda: V.tensor_scalar(cb[:], cb[:], 0.5, None, ALU.mult), r=[par_b], wp=[cwb])
        hs = [self.S(es, "s_h%d" % i, [128, D], F32) for i in range(2)]
        hs_b = [Buf(), Buf()]
        tpF = self.P(es, "s_tpF", [128, 512], F32)
        pzs = [self.P(es, "s_pz%d" % i, [128, 512], F32) for i in range(2)]
        fm = self.P(es, "s_fm", [128, 512], F32)
        tpB = self.P(es, "s_tpB", [128, 512], F32)
        dTb = self.P(es, "s_dT", [128, 512], F32)
        bm = self.P(es, "s_bm", [128, 512], F32)
        yb = self.P(es, "s_y", [128, 512], F32)
        pz_b = [PB(), PB()]
        fm_b, tpB_b, dT_b, bm_b, y_b = PB(), PB(), PB(), PB(), PB()
        gT = self.S(es, "s_gT", [128, 16, 128], BF16)
        gT_b = Buf()
        _ss = self.S(es, "s_nss", [128, 4], F32)
        _xn = gT[:, 0:8, :].rearrange("p c t -> p (c t)")
        _tp = tpF[:].bitcast(BF16)[:, 0:1024].rearrange("p (c t) -> p c t", c=8)
        nsc = (_xn, gT_b, _ss, Buf(), _xn, gT_b, _tp, PB())
        tpB16 = tpB[:].bitcast(BF16)
        tpBg = tpB16.rearrange("p (c t) -> p c t", c=8)
        xnT = [self.S(es, "s_xnT%d" % i, [128, 8, 131], BF16) for i in range(2)]
        xnT_b = [Buf(), Buf()]
        for i in range(2):
            kb.op(kb.pool, lambda: G.memset(xnT[i][:], 0.0), w=[xnT_b[i]])
        xbc_xs = self.S(es, "s_xbcxs", [128, 16, 128], BF16)
        xsT_b = [Buf() for _ in range(16)]
        xbc_bc = [self.S(es, "s_xbcbc%d" % i, [128, 16, 128], BF16) for i in range(2)]
        bc_b = [[Buf() for _ in range(16)] for _ in range(2)]
        NU, NA = 4, 7
        u = [self.S(es, "s_u%d" % i, [128, 131], F32) for i in range(NU)]
        u_b = [Buf() for _ in range(NU)]
        acc = [self.S(es, "s_acc%d" % i, [128, 128], F32) for i in range(NA)]
        acc_b = [Buf() for _ in range(NA)]
        th = [self.S(es, "s_th%d" % i, [128, 128], F32) for i in range(2)]
        th_b = [Buf(), Buf()]
        xs_tm = self.S(es, "s_xstm", [128, 2048], BF16)
        xs_b = [Buf() for _ in range(8)]
        B_tm = self.S(es, "s_Btm", [128, 1024], BF16)
        Btm_b = Buf()
        smF = self.S(es, "s_smF", [128, 4, 32], F32)
        EXPT, ACS, TMP, DTE = range(4)
        smF_b = [Buf() for _ in range(4)]
        smB = [self.S(es, "s_smB%d" % i, [128, 5, 32], F32) for i in range(2)]
        DTV, ADT, DFS, CD, W2 = range(5)
        smB_b = [[Buf() for _ in range(5)] for _ in range(2)]
        rhsD = [self.S(es, "s_rhsD%d" % i, [128, 512], F32) for i in range(2)]
        rhsD_b = [Buf(), Buf()]
        Ee = [self.S(es, "s_E%d" % i, [128, 512], F32) for i in range(2)]
        E_b = [Buf(), Buf()]
        CBm = [self.S(es, "s_CBm0", [128, 128], F32)] * 2
        _cb = Buf()
        CBm_b = [_cb, _cb]
        MT = [self.S(es, "s_MT%d" % i, [128, 512], BF16) for i in range(2)]
        MT_b = [Buf(), Buf()]
        xdt = [self.S(es, "s_xdt%d" % i, [128, 256], BF16) for i in range(2)]
        xdt_b = [Buf(), Buf()]
        xdtd = [self.S(es, "s_xdtd%d" % i, [128, 256], BF16) for i in range(2)]
        xdtd_b = [Buf(), Buf()]
        tt = [[self.S(es, "s_t%d_%d" % (k, i), [128, 256], F32) for i in range(2)] for k in range(4)]
        tt_b = [[Buf(), Buf()] for _ in range(4)]
        tt.append(tt[0])
        tt_b.append(tt_b[0])
        stg = self.S(es, "s_stg", [128, 8, 4], F32)
        stg_b = [Buf() for _ in range(8)]
        S32 = self.S(es, "s_S32", [128, 2048], F32)
        Sbf = self.S(es, "s_Sbf", [128, 2048], BF16)
        S32_b = [Buf() for _ in range(8)]
        Sbf_b = [Buf() for _ in range(8)]
        stmp = [self.S(es, "s_stmp0", [128, 256], F32)] * 2
        _sb = Buf()
        stmp_b = [_sb, _sb]
        kb.op(kb.pool, lambda: G.memset(S32[:], 0.0), w=S32_b)
        kb.op(kb.pool, lambda: G.memset(Sbf[:], 0.0), w=Sbf_b)
        v3 = lambda ap: ap.rearrange("p (j l) -> p j l", j=4)
        NTL = len(TILES)

        def f_load(ti):
            t0, n = TILES[ti]
            cur, prev = ti % 2, (ti + 1) % 2
            nprev = TILES[ti - 1][1] if ti > 0 else 0
            X = xnT[cur]
            kb.dma("sp", hs[cur][0:n, :], self.h_src(src, t0, n), w=[hs_b[cur]])
            self.norm_T(hs_b[cur], hs[cur][0:n, :], n, self.lnT[:, li, :], X[:, :, 3:3 + n], xnT_b[cur], nsc, newton=True)
            kb.op(kb.pool, lambda: G.tensor_copy(X[:, :, 0:3], xnT[prev][:, :, nprev:nprev + 3]), r=[xnT_b[prev]], wp=[xnT_b[cur]])

        def f_dt1(ti):
            t0, n = TILES[ti]
            cur = ti % 2
            X = xnT[cur]
            sB, sBb = smB[cur], smB_b[cur]
            pdt = fm[0:n, 0:32]
            for c in range(8):
                kb.op(kb.pe, lambda: T.matmul(pdt, X[:, c, 3:3 + n], w_in[:, c, 6144:6176], start=(c == 0), stop=(c == 7)),
                      r=[wdt_b, xnT_b[cur]], w=[fm_b] if c == 0 else [], wp=[fm_b] if c else [], inc=(c == 7))
            kb.op(kb.dve, lambda: V.tensor_tensor(sB[0:n, DTV, :], pdt, dtb[0:n, :], ALU.add), r=[fm_b, par_b], w=[sBb[DTV]])
            kb.op(kb.act, lambda: A.activation(out=smF[0:n, EXPT, :], in_=sB[0:n, DTV, :], func=AF.Exp), r=[sBb[DTV]], w=[smF_b[EXPT]])
            kb.op(kb.act, lambda: A.activation(out=sB[0:n, DTV, :], in_=smF[0:n, EXPT, :], func=AF.Ln, bias=1.0), r=[smF_b[EXPT]], w=[sBb[DTV]])
            kb.op(kb.dve, lambda: V.tensor_tensor(sB[0:n, ADT, :], sB[0:n, DTV, :], arep[0:n, :], ALU.mult), r=[sBb[DTV], arep_b], w=[sBb[ADT]])

        def f_dt2(ti):
            t0, n = TILES[ti]
            cur = ti % 2
            sB, sBb = smB[cur], smB_b[cur]
            pacs, ptot = fm[0:n, 32:64], fm[:, 64:96]
            kb.op(kb.pe, lambda: T.matmul(pacs, self.tri32[0:n, 0:n], sB[0:n, ADT, :], start=True, stop=True), r=[sBb[ADT], self.cb_], w=[fm_b])
            kb.op(kb.pe, lambda: T.matmul(ptot, self.ones32[0:n, :], sB[0:n, ADT, :], start=True, stop=True), r=[sBb[ADT], self.cb_], wp=[fm_b])
            kb.op(kb.dve, lambda: V.tensor_copy(smF[0:n, ACS, :], pacs), r=[fm_b], w=[smF_b[ACS]])
            kb.op(kb.act, lambda: A.activation(out=sB[0:n, DFS, :], in_=pacs, func=AF.Exp), r=[fm_b], w=[sBb[DFS]])
            kb.op(kb.act, lambda: A.activation(out=sB[:, CD, :], in_=ptot, func=AF.Exp), r=[fm_b], w=[sBb[CD]])
            kb.op(kb.dve, lambda: V.tensor_tensor(smF[0:n, TMP, :], ptot[0:n, :], smF[0:n, ACS, :], ALU.subtract), r=[fm_b, smF_b[ACS]], w=[smF_b[TMP]])
            kb.op(kb.act, lambda: A.activation(out=smF[0:n, DTE, :], in_=smF[0:n, TMP, :], func=AF.Exp), r=[smF_b[TMP]], w=[smF_b[DTE]])
            kb.op(kb.dve, lambda: V.tensor_tensor(sB[0:n, W2, :], sB[0:n, DTV, :], smF[0:n, DTE, :], ALU.mult), r=[sBb[DTV], smF_b[DTE]], w=[sBb[W2]])

        def conv_dst(ti, cc, n):
            cur = ti % 2
            if cc < 16:
                return xbc_xs[:, cc, 0:n], xsT_b[cc]
            return xbc_bc[cur][:, cc - 16, 0:n], bc_b[cur][cc - 16]

        def f_conv(ti, s):
            t0, n = TILES[ti]
            cur = ti % 2
            X = xnT[cur]
            cc = s
            if 0 <= cc < 32:
                p = cc % 2
                pz = pzs[p][:, 0:3 + n]
                for c in range(8):
                    kb.op(kb.pe, lambda: T.matmul(pz, w_in[:, c, 2048 + cc * 128:2048 + (cc + 1) * 128], X[:, c, 0:3 + n], start=(c == 0), stop=(c == 7)),
                          r=[wx_b[cc // 4], xnT_b[cur]], w=[pz_b[p]] if c == 0 else [], wp=[pz_b[p]] if c else [], inc=(c == 7))
            cc = s - 1
            if 0 <= cc < 32:
                p = cc % 2
                pz = pzs[p][:, 0:3 + n]
                kb.op(kb.act, lambda: A.activation(out=acc[cc % NA][:, 0:n], in_=pz[:, 3:3 + n], func=AF.Identity, bias=cb[:, cc:cc + 1], scale=cw[:, cc, 3:4]),
                      r=[pz_b[p], cwb], w=[acc_b[cc % NA]])
                kb.op(kb.act, lambda: A.copy(u[cc % NU][:, 0:3 + n], pz), r=[pz_b[p]], w=[u_b[cc % NU]])
            for k in range(3):
                cc = s - 2 - k
                if 0 <= cc < 32:
                    p = cc % NU
                    a_ = acc[cc % NA][:, 0:n]
                    kb.op(kb.dve, lambda: V.scalar_tensor_tensor(a_, u[p][:, k:k + n], cw[:, cc, k:k + 1], a_, ALU.mult, ALU.add),
                          r=[u_b[p], cwb], w=[acc_b[cc % NA]])
            cc = s - 5
            if 0 <= cc < 32:
                kb.op(kb.act, lambda: A.activation(out=th[cc % 2][:, 0:n], in_=acc[cc % NA][:, 0:n], func=AF.Tanh), r=[acc_b[cc % NA]], w=[th_b[cc % 2]])
            cc = s - 6
            if 0 <= cc < 32:
                dst_ap, dst_b = conv_dst(ti, cc, n)
                kb.op(kb.dve, lambda: V.scalar_tensor_tensor(dst_ap, th[cc % 2][:, 0:n], 1.0, acc[cc % NA][:, 0:n], ALU.add, ALU.mult),
                      r=[th_b[cc % 2], acc_b[cc % NA]], w=[dst_b])

        def front(ti):
            st = [partial(f_conv, ti, 0), partial(f_dt1, ti), partial(f_conv, ti, 1), partial(f_dt2, ti)]
            st += [partial(f_conv, ti, s) for s in range(2, 38)]
            return st

        def b_trx(ti, half):
            t0, n = TILES[ti]
            for k in range(8):
                cc = half * 8 + k
                kb.op(kb.pe, lambda: T.transpose(tpB16[0:n, k * 128:(k + 1) * 128], xbc_xs[:, cc, 0:n], self.ident[:, :]),
                      r=[xsT_b[cc], self.cb_], w=[tpB_b] if k == 0 else [], wp=[tpB_b] if k else [], inc=(k == 7))
            kb.op(kb.act, lambda: A.copy(xs_tm[0:n, half * 1024:(half + 1) * 1024], tpB16[0:n, :]), r=[tpB_b], w=xs_b[4 * half:4 * half + 4])

        def b_trB(ti):
            t0, n = TILES[ti]
            cur = ti % 2
            for g in range(8):
                kb.op(kb.pe, lambda: T.transpose(tpB16[0:n, g * 128:(g + 1) * 128], xbc_bc[cur][:, g, 0:n], self.ident[:, :]),
                      r=[bc_b[cur][g], self.cb_], w=[tpB_b] if g == 0 else [], wp=[tpB_b] if g else [], inc=(g == 7))
            kb.op(kb.dve, lambda: V.tensor_copy(B_tm[0:n, :], tpB16[0:n, :]), r=[tpB_b], w=[Btm_b])

        def gchain(ti, g):
            t0, n = TILES[ti]
            cur = ti % 2
            X = xnT[cur]
            sB, sBb = smB[cur], smB_b[cur]
            b2 = g % 2
            hsl = slice(4 * g, 4 * g + 4)
            gsl = slice(g * 256, (g + 1) * 256)
            BT, CT = xbc_bc[cur][:, g, 0:n], xbc_bc[cur][:, 8 + g, 0:n]
            BTb, CTb = bc_b[cur][g], bc_b[cur][8 + g]
            x3 = lambda ap: ap.rearrange("p (j f) -> p j f", j=4)
            rD = rhsD[b2][0:n, 0:4 * n]
            Ev = Ee[b2][0:n, 0:4 * n]
            MTv = MT[b2][0:n, 0:4 * n]
            pcb = bm[0:n, 0:n]
            pst = bm[:, 256:512]
            zq = fm[0:n, 256:512]
            xs3 = x3(xs_tm[0:n, gsl])
            t1, t2, t3, szg, thg = [tt[k][b2][0:n, :] for k in range(5)]
            t1b, t2b, t3b, szb, thb = [tt_b[k][b2] for k in range(5)]
            steps = []

            def s1():
                kb.op(kb.pool, lambda: G.tensor_tensor(v3(rD), bc(sB[0:n, ADT, hsl], [n, 4, n], 2), bc(self.tri32[0:n, 0:n], [n, 4, n], 1), ALU.mult),
                      r=[sBb[ADT], self.cb_], w=[rhsD_b[b2]])
                kb.op(kb.pool, lambda: G.tensor_tensor(x3(xdt[b2][0:n, :]), xs3, bc(sB[0:n, DTV, hsl], [n, 4, 64], 2), ALU.mult),
                      r=[xs_b[g], sBb[DTV]], w=[xdt_b[b2]])
            steps.append(s1)

            def s2():
                kb.op(kb.pe, lambda: T.matmul(dTb[0:n, 0:4 * n], self.ltri32[0:n, 0:n], rD, start=True, stop=True), r=[rhsD_b[b2], self.cb_], w=[dT_b])
                kb.op(kb.act, lambda: A.activation(out=Ev, in_=dTb[0:n, 0:4 * n], func=AF.Exp), r=[dT_b], w=[E_b[b2]])
                kb.op(kb.pool, lambda: G.tensor_tensor(x3(xdtd[b2][0:n, :]), xs3, bc(sB[0:n, W2, hsl], [n, 4, 64], 2), ALU.mult),
                      r=[xs_b[g], sBb[W2]], w=[xdtd_b[b2]])
            steps.append(s2)

            def s3():
                kb.op(kb.pe, lambda: T.matmul(pcb, BT, CT, start=True, stop=True), r=[BTb, CTb], w=[bm_b])
                kb.op(kb.dve, lambda: V.tensor_tensor(CBm[b2][0:n, 0:n], pcb, self.tri32[0:n, 0:n], ALU.mult), r=[bm_b, self.cb_], w=[CBm_b[b2]])
                for c in range(8):
                    kb.op(kb.pe, lambda: T.matmul(zq, X[:, c, 3:3 + n], w_in[:, c, g * 256:(g + 1) * 256], start=(c == 0), stop=(c == 7)),
                          r=[wz_b[g // 2], xnT_b[cur]], w=[fm_b] if c == 0 else [], wp=[fm_b] if c else [], inc=(c == 7))
                kb.op(kb.act, lambda: A.activation(out=thg, in_=zq, func=AF.Tanh, scale=0.5), r=[fm_b], w=[thb])
                kb.op(kb.dve, lambda: V.scalar_tensor_tensor(szg, thg, 1.0, zq, ALU.add, ALU.mult), r=[thb, fm_b], w=[szb])
                kb.op(kb.pool, lambda: G.tensor_tensor(v3(MTv), v3(Ev), bc(CBm[b2][0:n, 0:n], [n, 4, n], 1), ALU.mult),
                      r=[E_b[b2], CBm_b[b2]], w=[MT_b[b2]])
            steps.append(s3)

            def s4():
                for jj in range(4):
                    kb.op(kb.pe, lambda: T.matmul(yb[0:n, jj * 64:(jj + 1) * 64], MTv[:, jj * n:(jj + 1) * n], xdt[b2][0:n, jj * 64:(jj + 1) * 64], start=True, stop=True),
                          r=[MT_b[b2], xdt_b[b2]], w=[y_b] if jj == 0 else [], wp=[y_b] if jj else [], inc=False)
                kb.op(kb.pe, lambda: T.matmul(yb[0:n, 256:512], CT, Sbf[:, gsl], start=True, stop=True), r=[CTb, Sbf_b[g]], wp=[y_b])
                kb.op(kb.pool, lambda: G.tensor_tensor(x3(t3), xs3, bc(dsk[0:n, hsl], [n, 4, 64], 2), ALU.mult), r=[xs_b[g], par_b], w=[t3b])
                kb.op(kb.dve, lambda: V.tensor_tensor(x3(t1), x3(yb[0:n, 256:512]), bc(sB[0:n, DFS, hsl], [n, 4, 64], 2), ALU.mult),
                      r=[y_b, sBb[DFS]], w=[t1b])
                kb.op(kb.dve, lambda: V.tensor_tensor(t2, yb[0:n, 0:256], t1, ALU.add), r=[y_b, t1b], w=[t2b])
            steps.append(s4)

            def s5():
                kb.op(kb.pe, lambda: T.matmul(pst, B_tm[0:n, g * 128:(g + 1) * 128], xdtd[b2][0:n, :], start=True, stop=True),
                      r=[Btm_b, xdtd_b[b2]], w=[bm_b])
                kb.op(kb.pool, lambda: G.tensor_tensor(x3(stmp[b2][:, :]), x3(S32[:, gsl]), bc(sB[:, CD, hsl], [128, 4, 64], 2), ALU.mult),
                      r=[S32_b[g], sBb[CD]], w=[stmp_b[b2]])
                kb.op(kb.dve, lambda: V.tensor_tensor(S32[:, gsl], stmp[b2][:, :], pst, ALU.add), r=[stmp_b[b2], bm_b], w=[S32_b[g]])
                kb.op(kb.act, lambda: A.copy(Sbf[:, gsl], S32[:, gsl]), r=[S32_b[g]], w=[Sbf_b[g]])
            steps.append(s5)

            def s6():
                kb.op(kb.pool, lambda: G.tensor_tensor(t2, t2, t3, ALU.add), r=[t3b], w=[t2b])
                kb.op(kb.pool, lambda: G.tensor_tensor(t2, t2, szg, ALU.mult), r=[szb], w=[t2b])
            steps.append(s6)

            sq = lambda: kb.op(kb.act, lambda: A.activation(out=t1, in_=t2, func=AF.Square, accum_out=stg[0:n, g, 0:1]), r=[t2b], w=[t1b, stg_b[g]])
            newt = self.rstd_newton_ops(stg[:, g, :], stg_b[g], n, 1.0 / 256, 4.0 * EPS)
            gnf = lambda: kb.op(kb.dve, lambda: V.tensor_scalar(xs_tm[0:n, gsl], t2, stg[0:n, g, 2:3], None, ALU.mult), r=[t2b, stg_b[g]], w=[xs_b[g]])
            return steps, (sq, newt, gnf)

        def b_gT(ti, half):
            t0, n = TILES[ti]
            for k in range(8):
                cc = half * 8 + k
                kb.op(kb.pe, lambda: T.transpose(tpBg[:, k, 0:n], xs_tm[0:n, cc * 128:(cc + 1) * 128], self.ident[0:n, 0:n]),
                      r=[xs_b[cc // 2], self.cb_], w=[tpB_b] if k == 0 else [], wp=[tpB_b] if k else [], inc=(k == 7))
            kb.op(kb.dve, lambda: V.tensor_tensor(gT[:, half * 8:(half + 1) * 8, 0:n], tpBg[:, :, 0:n], bc(sng[:, half * 8:(half + 1) * 8], [128, 8, n], 2), ALU.mult),
                  r=[tpB_b, par_b], w=[gT_b] if half == 0 else [], wp=[gT_b] if half else [])

        def b_out(ti, hh):
            t0, n = TILES[ti]
            cur = ti % 2
            po = dTb[0:n, :] if hh == 0 else yb[0:n, :]
            pbuf = dT_b if hh == 0 else y_b
            for cc in range(16):
                kb.op(kb.pe, lambda: T.matmul(po, gT[:, cc, 0:n], w_out[:, cc, hh * 512:(hh + 1) * 512], start=(cc == 0), stop=(cc == 15)),
                      r=[gT_b, wout_b[cc // 4]], w=[pbuf] if cc == 0 else [], wp=[pbuf] if cc else [], inc=(cc == 15))
            kb.op(kb.dve, lambda: V.tensor_tensor(hs[cur][0:n, hh * 512:(hh + 1) * 512], po, hs[cur][0:n, hh * 512:(hh + 1) * 512], ALU.add),
                  r=[pbuf, hs_b[cur]], wp=[hs_b[cur]])
            if hh == 1:
                dap = self.h_dst(dst, t0, n)
                if dap is not None:
                    kb.dma("sp", dap, hs[cur][0:n, :], r=[hs_b[cur]])

        def back(ti):
            st = [partial(b_trx, ti, 0), partial(b_trx, ti, 1), partial(b_trB, ti)]
            for gp in range(4):
                (ca, ta), (cb_, tb) = gchain(ti, 2 * gp), gchain(ti, 2 * gp + 1)
                for a_, b_ in zip(ca, cb_):
                    st += [a_, b_]

                def tail(ta=ta, tb=tb):
                    ta[0]()
                    tb[0]()
                    for fa, fb in zip(ta[1], tb[1]):
                        fa()
                        fb()
                    ta[2]()
                    tb[2]()
                st.append(tail)
            st += [partial(b_gT, ti, 0), partial(b_gT, ti, 1), partial(b_out, ti, 0), partial(b_out, ti, 1)]
            return st

        def interleave(a, b, lead):
            na, nb = len(a), len(b)
            i = jx = 0
            while i < na or jx < nb:
                if jx >= nb or (i < na and i * nb <= jx * na * lead):
                    a[i]()
                    i += 1
                else:
                    b[jx]()
                    jx += 1

        f_load(0)
        for ti in range(NTL + 1):
            f = front(ti) if ti < NTL else []
            b = back(ti - 1) if ti >= 1 else []
            if ti + 1 < NTL:
                b = b + [partial(f_load, ti + 1)]
            interleave(f, b, SSD_LEAD)

    def phase_ssd_v1(self, es, j, li, src, dst):
        kb, nc = self.kb, self.nc
        V, A, G, T = nc.vector, nc.scalar, nc.gpsimd, nc.tensor
        w_in = self.S(es, "s_win", [128, 8, SSD_IN], BF16)
        w_out = self.S(es, "s_wout", [128, 16, D], BF16)
        win_b, wout_b, par_b = Buf(), Buf(), Buf()
        wv = self.w_ssd_in[j].rearrange("(c p) f -> p c f", p=128)
        for c in range(8):
            kb.dma("pool", w_in[:, c, :], wv[:, c, :], wp=[win_b])
        wov = self.w_ssd_out[j].rearrange("(c p) f -> p c f", p=128)
        for c in range(0, 16, 4):
            kb.dma("pool", w_out[:, c:c + 4, :], wov[:, c:c + 4, :], wp=[wout_b])
        cw = self.S(es, "s_cw", [128, 32, 4], F32)
        cb = self.S(es, "s_cb", [128, 32], F32)
        dtb = self.S(es, "s_dtb", [128, 32], F32)
        arep = self.S(es, "s_arep", [128, 32], F32)
        dsk = self.S(es, "s_dsk", [128, 32], F32)
        sng = self.S(es, "s_sng", [128, 16], F32)
        kb.dma("sp", cw[:], self.cw_d[j], wp=[par_b])
        kb.dma("sp", cb[:], self.cb_d[j], wp=[par_b])
        kb.dma("sp", dtb[:], self.dtb_d[j], wp=[par_b])
        kb.dma("sp", arep[:], self.alog_d[j], wp=[par_b])
        kb.dma("sp", dsk[:], self.dsk_d[j], wp=[par_b])
        kb.dma("sp", sng[:], self.sng_d[j], wp=[par_b])
        arep_b = Buf()
        kb.op(kb.act, lambda: A.activation(out=arep[:], in_=arep[:], func=AF.Exp), r=[par_b], w=[arep_b])
        kb.op(kb.dve, lambda: V.tensor_scalar(arep[:], arep[:], -1.0, None, ALU.mult), r=[arep_b], w=[arep_b])
        hs = [self.S(es, "s_h%d" % i, [128, D], F32) for i in range(2)]
        hs_b = [Buf(), Buf()]
        tp2 = self.P(es, "s_tp2", [128, 1024], F32)
        pzbs = [self.P(es, "s_pz%d" % i, [128, 512], F32) for i in range(2)]
        zqb = self.P(es, "s_zq", [128, 512], F32)
        dTb = self.P(es, "s_dT", [128, 512], F32)
        miscb = self.P(es, "s_misc", [128, 512], F32)
        yb = self.P(es, "s_y", [128, 512], F32)
        pz_b = [PB(), PB()]
        _z = PB()
        zq_b = [_z, _z]
        dT_b = PB()
        misc_b = PB()
        pdt_b = pacs_b = ptot_b = pst_b = misc_b
        pcb_b = [misc_b, misc_b]
        ydg_b = yof_b = PB()
        nsc = self.norm_scratch(es, "s", tp2)
        tp_b = nsc[7]
        tp16 = tp2[:].bitcast(BF16)
        tpg = tp16.rearrange("p (c t) -> p c t", c=16)
        xnT = [self.S(es, "s_xnT%d" % i, [128, 8, 131], BF16) for i in range(2)]
        xnT_b = [Buf(), Buf()]
        for i in range(2):
            kb.op(kb.pool, lambda: G.memset(xnT[i][:], 0.0), w=[xnT_b[i]])
        xbcT = self.S(es, "s_xbcT", [128, 32, 128], BF16)
        xbc_b = [Buf() for _ in range(32)]
        acc = [self.S(es, "s_acc%d" % i, [128, 128], F32) for i in range(3)]
        acc_b = [Buf() for _ in range(3)]
        xs_tm = self.S(es, "s_xstm", [128, 2048], BF16)
        xs_b = Buf()
        B_tm = self.S(es, "s_Btm", [128, 1024], BF16)
        Btm_b = Buf()
        sz = self.S(es, "s_sz", [128, 2048], F32)
        sz_b = [Buf() for _ in range(8)]
        sm = self.S(es, "s_sm", [128, 10, 32], F32)
        DTV, EXPT, ADT, ACS, DFS, CD, DTE, TMP, W2 = range(9)
        sm_b = [Buf() for _ in range(10)]
        rhsD = [self.S(es, "s_rhsD0", [128, 512], F32)] * 2
        _b = Buf()
        rhsD_b = [_b, _b]
        Ee = [self.S(es, "s_E0", [128, 512], F32)] * 2
        _b = Buf()
        E_b = [_b, _b]
        CBm = [self.S(es, "s_CBm%d" % i, [128, 128], F32) for i in range(2)]
        CBm_b = [Buf(), Buf()]
        MT = [self.S(es, "s_MT%d" % i, [128, 512], BF16) for i in range(2)]
        MT_b = [Buf(), Buf()]
        xdt = [self.S(es, "s_xdt%d" % i, [128, 256], BF16) for i in range(2)]
        xdt_b = [Buf(), Buf()]
        xdtd = [self.S(es, "s_xdtd%d" % i, [128, 256], BF16) for i in range(2)]
        xdtd_b = [Buf(), Buf()]
        tt = [[self.S(es, "s_t%d_%d" % (k, i), [128, 256], F32) for i in range(2)] for k in range(3)]
        tt_b = [[Buf(), Buf()] for _ in range(3)]
        tt.append(tt[0])
        tt_b.append(tt_b[0])
        stg = self.S(es, "s_stg", [128, 8, 4], F32)
        stg_b = [Buf() for _ in range(8)]
        gn = self.S(es, "s_gn", [128, 2048], BF16)
        gn_b = Buf()
        gT = self.S(es, "s_gT", [128, 16, 128], BF16)
        gT_b = Buf()
        S32 = self.S(es, "s_S32", [128, 2048], F32)
        Sbf = self.S(es, "s_Sbf", [128, 2048], BF16)
        S32_b = [Buf() for _ in range(8)]
        Sbf_b = [Buf() for _ in range(8)]
        stmp = [self.S(es, "s_stmp0", [128, 256], F32)] * 2
        _b = Buf()
        stmp_b = [_b, _b]
        kb.op(kb.pool, lambda: G.memset(S32[:], 0.0), w=S32_b)
        kb.op(kb.pool, lambda: G.memset(Sbf[:], 0.0), w=Sbf_b)
        nprev = 0
        ipz = 0
        for ti, (t0, n) in enumerate(TILES):
            cur, prev = ti % 2, (ti + 1) % 2
            X = xnT[cur]
            kb.dma("sp", hs[cur][0:n, :], self.h_src(src, t0, n), w=[hs_b[cur]])
            self.norm_T(hs_b[cur], hs[cur][0:n, :], n, self.lnT[:, li, :], X[:, :, 3:3 + n], xnT_b[cur], nsc)
            kb.op(kb.pool, lambda: G.tensor_copy(X[:, :, 0:3], xnT[prev][:, :, nprev:nprev + 3]), r=[xnT_b[prev]], wp=[xnT_b[cur]])
            nprev = n
            for cc in range(32):
                p = ipz % 2
                ipz += 1
                pz = pzbs[p][:, 0:3 + n]
                for c in range(8):
                    kb.op(kb.pe, lambda: T.matmul(pz, w_in[:, c, 2048 + cc * 128:2048 + (cc + 1) * 128], X[:, c, 0:3 + n], start=(c == 0), stop=(c == 7)),
                          r=[win_b, xnT_b[cur]], w=[pz_b[p]] if c == 0 else [], wp=[pz_b[p]] if c else [], inc=(c == 7))
                a_ = acc[p][:, 0:n]
                kb.op(kb.act, lambda: A.activation(out=a_, in_=pz[:, 3:3 + n], func=AF.Identity, bias=cb[:, cc:cc + 1], scale=cw[:, cc, 3:4]),
                      r=[pz_b[p], par_b], w=[acc_b[p]])
                for k in range(3):
                    kb.op(kb.dve, lambda: V.scalar_tensor_tensor(a_, pz[:, k:k + n], cw[:, cc, k:k + 1], a_, ALU.mult, ALU.add),
                          r=[pz_b[p], par_b], w=[acc_b[p]])
                kb.op(kb.act, lambda: A.activation(out=xbcT[:, cc, 0:n], in_=a_, func=AF.Silu), r=[acc_b[p]], w=[xbc_b[cc]])
            for g in range(8):
                zs = g % 2
                zq = zqb[0:n, zs * 256:(zs + 1) * 256]
                for c in range(8):
                    kb.op(kb.pe, lambda: T.matmul(zq, X[:, c, 3:3 + n], w_in[:, c, g * 256:(g + 1) * 256], start=(c == 0), stop=(c == 7)),
                          r=[win_b, xnT_b[cur]], w=[zq_b[zs]] if c == 0 else [], wp=[zq_b[zs]] if c else [], inc=(c == 7))
                kb.op(kb.act, lambda: A.activation(out=sz[0:n, g * 256:(g + 1) * 256], in_=zq, func=AF.Silu), r=[zq_b[zs]], w=[sz_b[g]])
            for cc in range(16):
                kb.op(kb.pe, lambda: T.transpose(tp16[0:n, cc * 128:(cc + 1) * 128], xbcT[:, cc, 0:n], self.ident[:, :]),
                      r=[xbc_b[cc], self.cb_], w=[tp_b] if cc == 0 else [], wp=[tp_b] if cc else [], inc=(cc == 15))
            kb.op(kb.act, lambda: A.copy(xs_tm[0:n, 0:1024], tp16[0:n, 0:1024]), r=[tp_b], w=[xs_b])
            kb.op(kb.dve, lambda: V.tensor_copy(xs_tm[0:n, 1024:2048], tp16[0:n, 1024:2048]), r=[tp_b], wp=[xs_b])
            for g in range(8):
                kb.op(kb.pe, lambda: T.transpose(tp16[0:n, g * 128:(g + 1) * 128], xbcT[:, 16 + g, 0:n], self.ident[:, :]),
                      r=[xbc_b[16 + g], self.cb_], w=[tp_b] if g == 0 else [], wp=[tp_b] if g else [], inc=(g == 7))
            kb.op(kb.act, lambda: A.copy(B_tm[0:n, :], tp16[0:n, 0:1024]), r=[tp_b], w=[Btm_b])
            pdt, pacs, ptot = miscb[0:n, 0:32], miscb[0:n, 32:64], miscb[:, 64:96]
            for c in range(8):
                kb.op(kb.pe, lambda: T.matmul(pdt, X[:, c, 3:3 + n], w_in[:, c, 6144:6176], start=(c == 0), stop=(c == 7)),
                      r=[win_b, xnT_b[cur]], w=[pdt_b] if c == 0 else [], wp=[pdt_b] if c else [], inc=(c == 7))
            smv = lambda k: sm[0:n, k, :]
            kb.op(kb.dve, lambda: V.tensor_tensor(smv(DTV), pdt, dtb[0:n, :], ALU.add), r=[pdt_b, par_b], w=[sm_b[DTV]])
            kb.op(kb.act, lambda: A.activation(out=smv(EXPT), in_=smv(DTV), func=AF.Exp), r=[sm_b[DTV]], w=[sm_b[EXPT]])
            kb.op(kb.act, lambda: A.activation(out=smv(DTV), in_=smv(EXPT), func=AF.Ln, bias=1.0), r=[sm_b[EXPT]], w=[sm_b[DTV]])
            kb.op(kb.dve, lambda: V.tensor_tensor(smv(ADT), smv(DTV), arep[0:n, :], ALU.mult), r=[sm_b[DTV], arep_b], w=[sm_b[ADT]])
            kb.op(kb.pe, lambda: T.matmul(pacs, self.tri32[0:n, 0:n], smv(ADT), start=True, stop=True), r=[sm_b[ADT], self.cb_], w=[pacs_b])
            kb.op(kb.pe, lambda: T.matmul(ptot, self.ones32[0:n, :], smv(ADT), start=True, stop=True), r=[sm_b[ADT], self.cb_], w=[ptot_b])
            kb.op(kb.dve, lambda: V.tensor_copy(smv(ACS), pacs), r=[pacs_b], w=[sm_b[ACS]])
            kb.op(kb.act, lambda: A.activation(out=smv(DFS), in_=pacs, func=AF.Exp), r=[pacs_b], w=[sm_b[DFS]])
            kb.op(kb.act, lambda: A.activation(out=sm[:, CD, :], in_=ptot, func=AF.Exp), r=[ptot_b], w=[sm_b[CD]])
            kb.op(kb.dve, lambda: V.tensor_tensor(smv(TMP), ptot[0:n, :], smv(ACS), ALU.subtract), r=[ptot_b, sm_b[ACS]], w=[sm_b[TMP]])
            kb.op(kb.act, lambda: A.activation(out=smv(DTE), in_=smv(TMP), func=AF.Exp), r=[sm_b[TMP]], w=[sm_b[DTE]])
            kb.op(kb.dve, lambda: V.tensor_tensor(smv(W2), smv(DTV), smv(DTE), ALU.mult), r=[sm_b[DTV], sm_b[DTE]], w=[sm_b[W2]])
            for g in range(8):
                b2 = g % 2
                hsl = slice(4 * g, 4 * g + 4)
                gsl = slice(g * 256, (g + 1) * 256)
                v3 = lambda ap: ap.rearrange("p (j l) -> p j l", j=4)
                rD = rhsD[b2][0:n, 0:4 * n]
                kb.op(kb.pool, lambda: G.tensor_tensor(v3(rD), bc(sm[0:n, ADT, hsl], [n, 4, n], 2), bc(self.tri32[0:n, 0:n], [n, 4, n], 1), ALU.mult),
                      r=[sm_b[ADT], self.cb_], w=[rhsD_b[b2]])
                kb.op(kb.pe, lambda: T.matmul(dTb[0:n, 0:4 * n], self.ltri32[0:n, 0:n], rD, start=True, stop=True), r=[rhsD_b[b2], self.cb_], w=[dT_b])
                Ev = Ee[b2][0:n, 0:4 * n]
                kb.op(kb.act, lambda: A.activation(out=Ev, in_=dTb[0:n, 0:4 * n], func=AF.Exp), r=[dT_b], w=[E_b[b2]])
                pcb = miscb[0:n, 128:128 + n]
                kb.op(kb.pe, lambda: T.matmul(pcb, xbcT[:, 16 + g, 0:n], xbcT[:, 24 + g, 0:n], start=True, stop=True),
                      r=[xbc_b[16 + g], xbc_b[24 + g]], w=[pcb_b[b2]])
                kb.op(kb.dve, lambda: V.tensor_tensor(CBm[b2][0:n, 0:n], pcb, self.tri32[0:n, 0:n], ALU.mult), r=[pcb_b[b2], self.cb_], w=[CBm_b[b2]])
                MTv = MT[b2][0:n, 0:4 * n]
                kb.op(kb.pool, lambda: G.tensor_tensor(v3(MTv), v3(Ev), bc(CBm[b2][0:n, 0:n], [n, 4, n], 1), ALU.mult),
                      r=[E_b[b2], CBm_b[b2]], w=[MT_b[b2]])
                xs3 = xs_tm[0:n, gsl].rearrange("p (j f) -> p j f", j=4)
                x3 = lambda ap: ap.rearrange("p (j f) -> p j f", j=4)
                kb.op(kb.pool, lambda: G.tensor_tensor(x3(xdt[b2][0:n, :]), xs3, bc(sm[0:n, DTV, hsl], [n, 4, 64], 2), ALU.mult),
                      r=[xs_b, sm_b[DTV]], w=[xdt_b[b2]])
                kb.op(kb.pool, lambda: G.tensor_tensor(x3(xdtd[b2][0:n, :]), xs3, bc(sm[0:n, W2, hsl], [n, 4, 64], 2), ALU.mult),
                      r=[xs_b, sm_b[W2]], w=[xdtd_b[b2]])
                for jj in range(4):
                    kb.op(kb.pe, lambda: T.matmul(yb[0:n, jj * 64:(jj + 1) * 64], MTv[:, jj * n:(jj + 1) * n], xdt[b2][0:n, jj * 64:(jj + 1) * 64], start=True, stop=True),
                          r=[MT_b[b2], xdt_b[b2]], w=[ydg_b] if jj == 0 else [], wp=[ydg_b] if jj else [], inc=(jj == 3))
                kb.op(kb.pe, lambda: T.matmul(yb[0:n, 256:512], xbcT[:, 24 + g, 0:n], Sbf[:, gsl], start=True, stop=True),
                      r=[xbc_b[24 + g], Sbf_b[g]], w=[yof_b])
                t1, t2, t3, tj = [tt[k][b2][0:n, :] for k in range(4)]
                kb.op(kb.dve, lambda: V.tensor_tensor(x3(t1), x3(yb[0:n, 256:512]), bc(sm[0:n, DFS, hsl], [n, 4, 64], 2), ALU.mult),
                      r=[yof_b, sm_b[DFS]], w=[tt_b[0][b2]])
                kb.op(kb.dve, lambda: V.tensor_tensor(t2, yb[0:n, 0:256], t1, ALU.add), r=[ydg_b, tt_b[0][b2]], w=[tt_b[1][b2]])
                kb.op(kb.pool, lambda: G.tensor_tensor(x3(t3), xs3, bc(dsk[0:n, hsl], [n, 4, 64], 2), ALU.mult), r=[xs_b, par_b], w=[tt_b[2][b2]])
                kb.op(kb.pool, lambda: G.tensor_tensor(t2, t2, t3, ALU.add), r=[tt_b[2][b2]], w=[tt_b[1][b2]])
                kb.op(kb.pool, lambda: G.tensor_tensor(t2, t2, sz[0:n, gsl], ALU.mult), r=[sz_b[g]], w=[tt_b[1][b2]])
                kb.op(kb.act, lambda: A.activation(out=tj, in_=t2, func=AF.Square, accum_out=stg[0:n, g, 0:1]), r=[tt_b[1][b2]], w=[tt_b[3][b2], stg_b[g]])
                self.rstd_ops(stg[:, g, :], stg_b[g], n, 0, 1.0 / 256)
                kb.op(kb.dve, lambda: V.tensor_scalar(gn[0:n, gsl], t2, stg[0:n, g, 2:3], None, ALU.mult), r=[tt_b[1][b2], stg_b[g]],
                      w=[gn_b] if g == 0 else [], wp=[gn_b] if g else [])
                kb.op(kb.pe, lambda: T.matmul(miscb[:, 256:512], B_tm[0:n, g * 128:(g + 1) * 128], xdtd[b2][0:n, :], start=True, stop=True),
                      r=[Btm_b, xdtd_b[b2]], w=[pst_b])
                kb.op(kb.pool, lambda: G.tensor_tensor(x3(stmp[b2][:, :]), x3(S32[:, gsl]), bc(sm[:, CD, hsl], [128, 4, 64], 2), ALU.mult),
                      r=[S32_b[g], sm_b[CD]], w=[stmp_b[b2]])
                kb.op(kb.dve, lambda: V.tensor_tensor(S32[:, gsl], stmp[b2][:, :], miscb[:, 256:512], ALU.add), r=[stmp_b[b2], pst_b], w=[S32_b[g]])
                kb.op(kb.act, lambda: A.copy(Sbf[:, gsl], S32[:, gsl]), r=[S32_b[g]], w=[Sbf_b[g]])
            for cc in range(16):
                kb.op(kb.pe, lambda: T.transpose(tpg[:, cc, 0:n], gn[0:n, cc * 128:(cc + 1) * 128], self.ident[0:n, 0:n]),
                      r=[gn_b, self.cb_], w=[tp_b] if cc == 0 else [], wp=[tp_b] if cc else [], inc=(cc == 15))
            kb.op(kb.dve, lambda: V.tensor_tensor(gT[:, :, 0:n], tpg[:, :, 0:n], bc(sng[:, :], [128, 16, n], 2), ALU.mult), r=[tp_b, par_b], w=[gT_b])
            for hh in range(2):
                po = dTb[0:n, :] if hh == 0 else yb[0:n, :]
                pbufs = [dT_b] if hh == 0 else [ydg_b]
                for cc in range(16):
                    kb.op(kb.pe, lambda: T.matmul(po, gT[:, cc, 0:n], w_out[:, cc, hh * 512:(hh + 1) * 512], start=(cc == 0), stop=(cc == 15)),
                          r=[gT_b, wout_b], w=pbufs if cc == 0 else [], wp=pbufs if cc else [], inc=(cc == 15))
                kb.op(kb.dve, lambda: V.tensor_tensor(hs[cur][0:n, hh * 512:(hh + 1) * 512], po, hs[cur][0:n, hh * 512:(hh + 1) * 512], ALU.add),
                      r=pbufs + [hs_b[cur]], wp=[hs_b[cur]])
            dap = self.h_dst(dst, t0, n)
            if dap is not None:
                kb.dma("sp", dap, hs[cur][0:n, :], r=[hs_b[cur]])

    def rstd_ops(self, st, st_b, n, c0, inv_n):
        kb, nc = self.kb, self.nc
        kb.op(kb.act, lambda: nc.scalar.activation(out=st[0:n, c0 + 1:c0 + 2], in_=st[0:n, c0:c0 + 1], func=AF.Ln, bias=EPS, scale=inv_n),
              r=[st_b], wp=[st_b])
        kb.op(kb.act, lambda: nc.scalar.activation(out=st[0:n, c0 + 2:c0 + 3], in_=st[0:n, c0 + 1:c0 + 2], func=AF.Exp, scale=-0.5),
              r=[st_b], wp=[st_b])

    def phase_mla(self, es, j, li, src, dst):
        kb, nc = self.kb, self.nc
        V, A, G, T = nc.vector, nc.scalar, nc.gpsimd, nc.tensor
        oT = self.S(es, "oT", [128, 8, NT], BF16)
        oT_b = Buf()
        w_o = self.S(es, "w_o", [128, 8, D], BF16)
        w_o_b = Buf()
        gq = self.S(es, "gq", [128, 96], F32)
        gk = self.S(es, "gk", [128, 96], F32)
        par_b = Buf()
        self.tri16 = self.S(es, "tri16", [128, 128], BF16)
        kb.dma("pool", self.tri16[:], self.tri_d, wp=[par_b])
        kb.dma("sp", gq[:], self.gq_d[j], wp=[par_b])
        kb.dma("sp", gk[:], self.gk_d[j], wp=[par_b])
        with contextlib.ExitStack() as s1:
            w_in = self.S(s1, "a_win", [128, 8, 672], BF16)
            w_qb = self.S(s1, "a_wqb", [128, 3, 1536], BF16)
            w_kvb = self.S(s1, "a_wkvb", [128, 2, 2048], BF16)
            wb = Buf()
            kb.dma("pool", w_in[:], self.w_mla_in[j].rearrange("(c p) f -> p c f", p=128), wp=[wb])
            kb.dma("pool", w_qb[:], self.w_mla_qb[j].rearrange("(c p) f -> p c f", p=128), wp=[wb])
            kb.dma("pool", w_kvb[:], self.w_mla_kvb[j].rearrange("(c p) f -> p c f", p=128), wp=[wb])
            kb.dma("pool", w_o[:], self.w_mla_out[j].rearrange("(c p) f -> p c f", p=128), wp=[w_o_b])
            qag = self.S(s1, "a_qag", [128, 3], F32)
            kvag = self.S(s1, "a_kvag", [128, 2], F32)
            kb.dma("sp", qag[:], self.qag_d[j], wp=[par_b])
            kb.dma("sp", kvag[:], self.kvag_d[j], wp=[par_b])
            tp_ps = self.P(s1, "a_tp", [128, 512], F32)
            latA = self.P(s1, "a_latA", [128, 512], F32)
            latB = self.P(s1, "a_latB", [128, 512], F32)
            big = self.P(s1, "a_big", [128, 2048], F32)
            tph_ps = self.P(s1, "a_tph", [128, 512], F32)
            latA_b, latB_b, tph_b = PB(), PB(), PB()
            big_b = [PB() for _ in range(4)]
            nsc = self.norm_scratch(s1, "a", tp_ps)
            tp5 = tp_ps[:].bitcast(BF16)[:, 0:640].rearrange("p (c t) -> p c t", c=5)
            tp_b = nsc[7]
            tph = tph_ps[:].bitcast(BF16)[:, 0:1024].rearrange("p (c t) -> p c t", c=8)
            qTv = self.qT_d.rearrange("h p t -> p h t")
            kTv = self.kT_d.rearrange("h p t -> p h t")

            class NS:
                pass

            def mkbufs(ci):
                b = NS()
                nm = lambda x: "a%d_%s" % (ci, x)
                b.hs = self.S(s1, nm("h"), [128, D], F32); b.hs_b = Buf()
                b.cs = self.S(s1, nm("cs"), [128, 2, 16], F32); b.cs_b = Buf()
                b.xnT = self.S(s1, nm("xnT"), [128, 8, 128], BF16); b.xnT_b = Buf()
                b.st = self.S(s1, nm("st"), [128, 8], F32); b.st_b = Buf()
                b.qln = self.S(s1, nm("qln"), [128, 384], BF16)
                b.kvln = self.S(s1, nm("kvln"), [128, 256], BF16)
                b.kpe = self.S(s1, nm("kpe"), [128, 32], F32); b.ln_b = Buf()
                b.sqj = self.S(s1, nm("sqj"), [128, 2048], F32); b.sqj_b = Buf()
                b.qlT = self.S(s1, nm("qlT"), [128, 3, 128], BF16)
                b.kvlT = self.S(s1, nm("kvlT"), [128, 2, 128], BF16); b.lT_b = Buf()
                b.raw = self.S(s1, nm("raw"), [128, 2048], F32); b.raw_b = Buf()
                b.s16 = self.S(s1, nm("s16"), [128, 3, 16], F32); b.s16_b = Buf()
                b.rt = self.S(s1, nm("rt"), [128, 4, 256], F32); b.rt_b = [Buf() for _ in range(4)]
                b.kpg = self.S(s1, nm("kpg"), [128, 2, 32], F32); b.kpg_b = Buf()
                b.qbf = self.S(s1, nm("qbf"), [128, 1536], BF16); b.qbf_b = Buf()
                b.stg = [self.S(s1, nm("stg%d" % i), [96, 16, 128], BF16) for i in range(2)]; b.stg_b = [Buf(), Buf()]
                b.vst = self.S(s1, nm("vst"), [128, 8, 192], BF16); b.vst_b = Buf()
                kb.op(kb.pool, lambda: G.memset(b.vst[:], 1.0), w=[b.vst_b])
                return b

            CH = [mkbufs(0), mkbufs(1)]

            def chain(ti):
                t0, n = TILES[ti]
                b = CH[ti % 2]
                xnT, st, st_b, qln, kvln, kpe, ln_b = b.xnT, b.st, b.st_b, b.qln, b.kvln, b.kpe, b.ln_b
                sqj, sqj_b, qlT, kvlT, lT_b, raw, raw_b = b.sqj, b.sqj_b, b.qlT, b.kvlT, b.lT_b, b.raw, b.raw_b
                s16, s16_b, rt, rt_b, kpg, kpg_b, qbf, qbf_b = b.s16, b.s16_b, b.rt, b.rt_b, b.kpg, b.kpg_b, b.qbf, b.qbf_b
                raw3 = raw[0:n, 0:1536].rearrange("p (h f) -> p h f", h=16)
                sq3 = sqj[0:n, 0:1536].rearrange("p (h f) -> p h f", h=16)
                qb3 = qbf[0:n, :].rearrange("p (h f) -> p h f", h=16)
                kv4 = raw[0:n, :].rearrange("p (h f) -> p h f", h=16)
                sqk = sqj[0:n, 0:1024].rearrange("p (h f) -> p h f", h=16)
                kv5 = raw[0:n, :].rearrange("p (c e f) -> p c e f", c=8, e=2)

                def head_T(sg, dstv):
                    for half in range(2):
                        for hh in range(8):
                            h = half * 8 + hh
                            kb.op(kb.pe, lambda: T.transpose(tph[0:96, hh, 0:n], qbf[0:n, h * 96:(h + 1) * 96], self.ident[0:n, 0:n]),
                                  r=[qbf_b, self.cb_], w=[tph_b] if hh == 0 else [], wp=[tph_b] if hh else [], inc=(hh == 7))
                        kb.op(kb.act, lambda: A.copy(b.stg[sg][0:96, half * 8:(half + 1) * 8, 0:n], tph[0:96, :, 0:n]),
                              r=[tph_b], w=[b.stg_b[sg]] if half == 0 else [], wp=[b.stg_b[sg]] if half else [])
                    kb.dma("sp", dstv[:, :, t0:t0 + n], b.stg[sg][0:96, :, 0:n], r=[b.stg_b[sg]])

                def rope(t1, t2, cosb, sinb, o1, o2, three_d, wb_, rd_bufs):
                    if three_d:
                        a_, b_, c_, d_ = [rt[0:n, i, :].rearrange("p (h f) -> p h f", h=16) for i in range(4)]
                    else:
                        a_, b_, c_, d_ = [rt[0:n, i, 0:16] for i in range(4)]
                    kb.op(kb.dve, lambda: V.tensor_tensor(a_, t1, cosb, ALU.mult), r=rd_bufs, w=[rt_b[0]])
                    kb.op(kb.dve, lambda: V.tensor_tensor(b_, t2, sinb, ALU.mult), r=rd_bufs, w=[rt_b[1]])
                    kb.op(kb.dve, lambda: V.tensor_tensor(c_, t1, sinb, ALU.mult), r=rd_bufs, w=[rt_b[2]])
                    kb.op(kb.dve, lambda: V.tensor_tensor(d_, t2, cosb, ALU.mult), r=rd_bufs, w=[rt_b[3]])
                    kb.op(kb.dve, lambda: V.tensor_tensor(o1, a_, b_, ALU.subtract), r=[rt_b[0], rt_b[1]], wp=[wb_])
                    kb.op(kb.dve, lambda: V.tensor_tensor(o2, c_, d_, ALU.add), r=[rt_b[2], rt_b[3]], wp=[wb_])

                def s0():
                    kb.dma("sp", b.hs[0:n, :], self.h_src(src, t0, n), w=[b.hs_b])

                def s0b():
                    kb.dma("sp", b.cs[0:n, 0, :], self.cos_d[t0:t0 + n, :], w=[b.cs_b])
                    kb.dma("sp", b.cs[0:n, 1, :], self.sin_d[t0:t0 + n, :], wp=[b.cs_b])

                def s1():
                    self.norm_T(b.hs_b, b.hs[0:n, :], n, self.lnT[:, li, :], xnT[:, :, 0:n], b.xnT_b, nsc)

                def s2():
                    for c in range(8):
                        kb.op(kb.pe, lambda: T.matmul(latA[0:n, 0:384], xnT[:, c, 0:n], w_in[:, c, 0:384], start=(c == 0), stop=(c == 7)),
                              r=[b.xnT_b, wb], w=[latA_b] if c == 0 else [], wp=[latA_b] if c else [], inc=(c == 7))
                    for c in range(8):
                        kb.op(kb.pe, lambda: T.matmul(latB[0:n, 0:288], xnT[:, c, 0:n], w_in[:, c, 384:672], start=(c == 0), stop=(c == 7)),
                              r=[b.xnT_b, wb], w=[latB_b] if c == 0 else [], wp=[latB_b] if c else [], inc=(c == 7))
                    kb.op(kb.act, lambda: A.activation(out=sqj[0:n, 0:384], in_=latA[0:n, 0:384], func=AF.Square, accum_out=st[0:n, 0:1]),
                          r=[latA_b], w=[sqj_b, st_b])
                    kb.op(kb.act, lambda: A.activation(out=sqj[0:n, 512:768], in_=latB[0:n, 0:256], func=AF.Square, accum_out=st[0:n, 3:4]),
                          r=[latB_b], wp=[sqj_b, st_b])
                    self.rstd_ops(st, st_b, n, 0, 1.0 / 384)
                    self.rstd_ops(st, st_b, n, 3, 1.0 / 256)
                    kb.op(kb.dve, lambda: V.tensor_scalar(qln[0:n, :], latA[0:n, 0:384], st[0:n, 2:3], None, ALU.mult), r=[latA_b, st_b], w=[ln_b])
                    kb.op(kb.dve, lambda: V.tensor_scalar(kvln[0:n, :], latB[0:n, 0:256], st[0:n, 5:6], None, ALU.mult), r=[latB_b, st_b], wp=[ln_b])
                    kb.op(kb.act, lambda: A.copy(kpe[0:n, :], latB[0:n, 256:288]), r=[latB_b], wp=[ln_b])

                def s3():
                    for c in range(5):
                        srcap = qln[0:n, c * 128:(c + 1) * 128] if c < 3 else kvln[0:n, (c - 3) * 128:(c - 2) * 128]
                        kb.op(kb.pe, lambda: T.transpose(tp5[:, c, 0:n], srcap, self.ident[0:n, 0:n]),
                              r=[ln_b, self.cb_], w=[tp_b] if c == 0 else [], wp=[tp_b] if c else [], inc=(c == 4))
                    kb.op(kb.dve, lambda: V.tensor_tensor(qlT[:, :, 0:n], tp5[:, 0:3, 0:n], bc(qag[:, :], [128, 3, n], 2), ALU.mult),
                          r=[tp_b, par_b], w=[lT_b])
                    kb.op(kb.dve, lambda: V.tensor_tensor(kvlT[:, :, 0:n], tp5[:, 3:5, 0:n], bc(kvag[:, :], [128, 2, n], 2), ALU.mult),
                          r=[tp_b, par_b], wp=[lT_b])

                def s4():
                    for ct in range(3):
                        for kc in range(3):
                            kb.op(kb.pe, lambda: T.matmul(big[0:n, ct * 512:(ct + 1) * 512], qlT[:, kc, 0:n], w_qb[:, kc, ct * 512:(ct + 1) * 512],
                                                          start=(kc == 0), stop=(kc == 2)),
                                  r=[lT_b, wb], w=[big_b[ct]] if kc == 0 else [], wp=[big_b[ct]] if kc else [], inc=(kc == 2))
                    for ct in range(3):
                        E_ = kb.act if ct != 1 else kb.dve
                        fn = (lambda: A.copy(raw[0:n, ct * 512:(ct + 1) * 512], big[0:n, ct * 512:(ct + 1) * 512])) if ct != 1 else \
                             (lambda: V.tensor_copy(raw[0:n, ct * 512:(ct + 1) * 512], big[0:n, ct * 512:(ct + 1) * 512]))
                        kb.op(E_, fn, r=[big_b[ct]], w=[raw_b] if ct == 0 else [], wp=[raw_b] if ct else [])

                def s5():
                    kb.op(kb.dve, lambda: V.tensor_tensor(sqj[0:n, 0:1536], raw[0:n, 0:1536], raw[0:n, 0:1536], ALU.mult), r=[raw_b], w=[sqj_b])
                    kb.op(kb.dve, lambda: V.tensor_reduce(s16[0:n, 0, :], sq3, AX.X, ALU.add), r=[sqj_b], w=[s16_b])
                    kb.op(kb.act, lambda: A.activation(out=s16[0:n, 1, :], in_=s16[0:n, 0, :], func=AF.Ln, bias=EPS, scale=1.0 / 96), r=[s16_b], wp=[s16_b])
                    kb.op(kb.act, lambda: A.activation(out=s16[0:n, 2, :], in_=s16[0:n, 1, :], func=AF.Exp, scale=-0.5), r=[s16_b], wp=[s16_b])
                    kb.op(kb.dve, lambda: V.tensor_tensor(raw3, raw3, bc(s16[0:n, 2, :], [n, 16, 96], 2), ALU.mult), r=[s16_b], w=[raw_b])
                    kb.op(kb.dve, lambda: V.tensor_tensor(raw3, raw3, bc(gq[0:n, :], [n, 16, 96], 1), ALU.mult), r=[par_b], w=[raw_b])
                    cosb = bc(b.cs[0:n, 0, :], [n, 16, 16], 1)
                    sinb = bc(b.cs[0:n, 1, :], [n, 16, 16], 1)
                    kb.op(kb.act, lambda: A.copy(qb3[:, :, 0:64], raw3[:, :, 0:64]), r=[raw_b], w=[qbf_b])
                    rope(raw3[:, :, 64:80], raw3[:, :, 80:96], cosb, sinb, qb3[:, :, 64:80], qb3[:, :, 80:96], True, qbf_b, [raw_b, b.cs_b])

                def s6():
                    head_T(0, qTv)

                def s7():
                    for ct in range(4):
                        for kc in range(2):
                            kb.op(kb.pe, lambda: T.matmul(big[0:n, ct * 512:(ct + 1) * 512], kvlT[:, kc, 0:n], w_kvb[:, kc, ct * 512:(ct + 1) * 512],
                                                          start=(kc == 0), stop=(kc == 1)),
                                  r=[lT_b, wb], w=[big_b[ct]] if kc == 0 else [], wp=[big_b[ct]] if kc else [], inc=(kc == 1))
                    for ct in range(4):
                        E_ = kb.act if ct % 2 == 0 else kb.dve
                        fn = (lambda: A.copy(raw[0:n, ct * 512:(ct + 1) * 512], big[0:n, ct * 512:(ct + 1) * 512])) if ct % 2 == 0 else \
                             (lambda: V.tensor_copy(raw[0:n, ct * 512:(ct + 1) * 512], big[0:n, ct * 512:(ct + 1) * 512]))
                        kb.op(E_, fn, r=[big_b[ct]], w=[raw_b] if ct == 0 else [], wp=[raw_b] if ct else [])

                def s8():
                    kb.op(kb.dve, lambda: V.tensor_tensor(sqk, kv4[:, :, 0:64], kv4[:, :, 0:64], ALU.mult), r=[raw_b], w=[sqj_b])
                    kb.op(kb.dve, lambda: V.tensor_reduce(s16[0:n, 0, :], sqk, AX.X, ALU.add), r=[sqj_b], w=[s16_b])
                    kb.op(kb.act, lambda: A.activation(out=kpg[0:n, 1, :], in_=kpe[0:n, :], func=AF.Square, accum_out=st[0:n, 6:7]),
                          r=[ln_b], w=[kpg_b], wp=[st_b])
                    kb.op(kb.dve, lambda: V.tensor_scalar(s16[0:n, 0, :], s16[0:n, 0, :], st[0:n, 6:7], None, ALU.add), r=[st_b, s16_b], wp=[s16_b])
                    kb.op(kb.act, lambda: A.activation(out=s16[0:n, 1, :], in_=s16[0:n, 0, :], func=AF.Ln, bias=EPS, scale=1.0 / 96), r=[s16_b], wp=[s16_b])
                    kb.op(kb.act, lambda: A.activation(out=s16[0:n, 2, :], in_=s16[0:n, 1, :], func=AF.Exp, scale=-0.5), r=[s16_b], wp=[s16_b])
                    kb.op(kb.act, lambda: A.copy(b.vst[0:n, :, 0:64], kv5[:, :, 0, 64:128]), r=[raw_b, b.vst_b], wp=[b.vst_b])
                    kb.op(kb.dve, lambda: V.tensor_copy(b.vst[0:n, :, 128:192], kv5[:, :, 1, 64:128]), r=[raw_b, b.vst_b], wp=[b.vst_b])
                    kb.dma("sp", self.va_d[t0:t0 + n, :, :], b.vst[0:n, :, :], r=[b.vst_b])
                    kb.op(kb.dve, lambda: V.tensor_tensor(kv4[:, :, 0:64], kv4[:, :, 0:64], bc(s16[0:n, 2, :], [n, 16, 64], 2), ALU.mult),
                          r=[s16_b], w=[raw_b])
                    kb.op(kb.dve, lambda: V.tensor_tensor(qb3[:, :, 0:64], kv4[:, :, 0:64], bc(gk[0:n, 0:64], [n, 16, 64], 1), ALU.mult),
                          r=[raw_b, par_b], w=[qbf_b])
                    kb.op(kb.dve, lambda: V.tensor_tensor(kpg[0:n, 0, :], kpe[0:n, :], gk[0:n, 64:96], ALU.mult), r=[ln_b, par_b], w=[kpg_b])
                    rope(kpg[0:n, 0, 0:16], kpg[0:n, 0, 16:32], b.cs[0:n, 0, :], b.cs[0:n, 1, :], kpg[0:n, 1, 0:16], kpg[0:n, 1, 16:32],
                         False, kpg_b, [kpg_b, b.cs_b])
                    kb.op(kb.dve, lambda: V.tensor_tensor(qb3[:, :, 64:96], bc(kpg[0:n, 1, :], [n, 16, 32], 1), bc(s16[0:n, 2, :], [n, 16, 32], 2), ALU.mult),
                          r=[kpg_b, s16_b], wp=[qbf_b])

                def s9():
                    head_T(1, kTv)

                return [s0, s0b, s1, s2, s3, s4, s5, s6, s7, s8, s9]

            chains = [chain(k) for k in range(len(TILES))]
            NTL = len(TILES)
            for k in (0, 1):
                chains[k][0]()
                chains[k][1]()
            for k in range(0, NTL, 2):
                pair = [chains[k]] + ([chains[k + 1]] if k + 1 < NTL else [])
                nxt = [chains[k2] for k2 in (k + 2, k + 3) if k2 < NTL]
                for i in range(2, 11):
                    for c_ in pair:
                        c_[i]()
                    if i == 2:
                        for c_ in nxt:
                            c_[0]()
                    if i == 9:
                        for c_ in nxt:
                            c_[1]()
            kb.barrier()
        with contextlib.ExitStack() as s2:
            qh = [self.S(s2, "b_q%d" % i, [96, NT], BF16) for i in range(2)]
            kh = [self.S(s2, "b_k%d" % i, [96, NT], BF16) for i in range(2)]
            qk_b = [Buf(), Buf()]
            va = [self.S(s2, "b_va%d" % i, [128, 33, 192], BF16) for i in range(2)]
            va_b = [Buf(), Buf()]
            NPS = 6
            pT = [self.S(s2, "b_pT%d" % i, [128, 512], BF16) for i in range(NPS)]
            pT_b = [Buf() for _ in range(NPS)]
            rden = [self.S(s2, "b_rd%d" % i, [128, 512], F32) for i in range(2)]
            rdsh = [self.S(s2, "b_rs%d" % i, [128, 512], F32) for i in range(2)]
            rd_b = [Buf(), Buf()]
            rs_b = [Buf(), Buf()]
            bnd = self.S(s2, "b_bnd", [128, 8], F32)
            bnd_b = Buf()
            ps = [self.P(s2, "b_ps%d" % i, [128, 512], F32) for i in range(NPS)]
            ps_b = [PB() for _ in range(NPS)]
            po = [self.P(s2, "b_po%d" % i, [128, 512], F32) for i in range(2)]
            po_b = [PB(), PB()]
            kb.op(kb.dve, lambda: V.tensor_reduce(bnd[:, 0:1], gq[:, :], AX.X, ALU.max), r=[par_b], w=[bnd_b])
            kb.op(kb.dve, lambda: V.tensor_reduce(bnd[:, 1:2], gq[:, :], AX.X, ALU.min), r=[par_b], wp=[bnd_b])
            kb.op(kb.dve, lambda: V.tensor_reduce(bnd[:, 2:3], gk[:, :], AX.X, ALU.max), r=[par_b], wp=[bnd_b])
            kb.op(kb.dve, lambda: V.tensor_reduce(bnd[:, 3:4], gk[:, :], AX.X, ALU.min), r=[par_b], wp=[bnd_b])
            kb.op(kb.dve, lambda: V.scalar_tensor_tensor(bnd[:, 4:5], bnd[:, 1:2], -1.0, bnd[:, 0:1], ALU.mult, ALU.max), r=[bnd_b], wp=[bnd_b])
            kb.op(kb.dve, lambda: V.scalar_tensor_tensor(bnd[:, 5:6], bnd[:, 3:4], -1.0, bnd[:, 2:3], ALU.mult, ALU.max), r=[bnd_b], wp=[bnd_b])
            kb.op(kb.dve, lambda: V.scalar_tensor_tensor(bnd[:, 6:7], bnd[:, 4:5], -float(np.sqrt(96.0)), bnd[:, 5:6], ALU.mult, ALU.mult),
                  r=[bnd_b], wp=[bnd_b])
            negB = bnd[:, 6:7]
            scale = float(96.0 ** -0.5)

            def load_head(h):
                b = h % 2
                kb.dma("sp", qh[b][:, :], self.qT_d[h], w=[qk_b[b]])
                kb.dma("sp", kh[b][:, :], self.kT_d[h], wp=[qk_b[b]])

            def load_pair(c):
                b = c % 2
                kb.dma("sp", va[b][0:16, 0, :], self.va_d[0:16, c, :], w=[va_b[b]])
                vv = self.va_d[16:NT, c, :].rearrange("(j p) w -> p j w", p=128)
                for jj in range(0, 32, 8):
                    kb.dma("sp", va[b][:, 1 + jj:9 + jj, :], vv[:, jj:jj + 8, :], wp=[va_b[b]])

            load_pair(0)
            load_head(0)
            items = []
            for h in range(MLA_H):
                for qi in range(9):
                    if qi == 0:
                        q0, nq = 0, 16
                        kts = [(0, 0, 16, 0, True)]
                    else:
                        q0, nq = 16 + 512 * (qi - 1), 512
                        kts = [(0, 0, 16, 0, False)] + [(kt, 16 + 128 * (kt - 1), 128, 0, False) for kt in range(1, 4 * (qi - 1) + 1)]
                        kts += [(4 * (qi - 1) + 1 + i, 16 + 128 * (4 * (qi - 1) + i), 128, 128 * i, True) for i in range(4)]
                    for idx, kt in enumerate(kts):
                        items.append((h, qi, q0, nq, idx, len(kts)) + kt)
            LA = 3
            NI = len(items)
            for i in range(NI + LA):
                if i < NI:
                    (h, qi, q0, nq, idx, nk_t, kt, k0, nk, qoff, diag) = items[i]
                    if qi == 0 and idx == 0 and h + 1 < MLA_H:
                        load_head(h + 1)
                        if h % 2 == 1:
                            load_pair(h // 2 + 1)
                    hb_ = h % 2
                    nqq = nq - qoff
                    p = i % NPS
                    kb.op(kb.pe, lambda: T.matmul(ps[p][0:nk, 0:nqq], kh[hb_][:, k0:k0 + nk], qh[hb_][:, q0 + qoff:q0 + nq], start=True, stop=True),
                          r=[qk_b[hb_]], w=[ps_b[p]])
                    kb.op(kb.act, lambda: A.activation(out=pT[p][0:nk, 0:nqq], in_=ps[p][0:nk, 0:nqq], func=AF.Exp, bias=negB[0:nk, :], scale=scale),
                          r=[ps_b[p], bnd_b], w=[pT_b[p]])
                    if diag:
                        kb.op(kb.dve, lambda: V.tensor_tensor(pT[p][0:nk, 0:nk], pT[p][0:nk, 0:nk], self.tri16[0:nk, 0:nk], ALU.mult),
                              r=[self.cb_], w=[pT_b[p]])
                ii = i - LA
                if ii >= 0:
                    (h, qi, q0, nq, idx, nk_t, kt, k0, nk, qoff, diag) = items[ii]
                    c, e = h // 2, h % 2
                    vb_ = c % 2
                    dlo, dhi = (0, 64) if e == 0 else (64, 128)
                    nlo, nhi = (64, 128) if e == 0 else (0, 64)
                    nqq = nq - qoff
                    p = ii % NPS
                    pp = (h * 9 + qi) % 2
                    first, last = idx == 0, idx == nk_t - 1
                    kb.op(kb.pe, lambda: T.matmul(po[pp][:, qoff:nq], va[vb_][0:nk, kt, e * 64:e * 64 + 128], pT[p][0:nk, 0:nqq], start=first, stop=last),
                          r=[pT_b[p], va_b[vb_]], w=[po_b[pp]] if first else [], wp=[] if first else [po_b[pp]], inc=last)
                    if last:
                        kb.op(kb.dve, lambda: V.reciprocal(rden[pp][nlo:nhi, 0:nq], po[pp][nlo:nhi, 0:nq]), r=[po_b[pp]], w=[rd_b[pp]])
                        kb.op(kb.dve, lambda: V.tensor_copy(rdsh[pp][dlo:dhi, 0:nq], rden[pp][nlo:nhi, 0:nq]), r=[rd_b[pp]], w=[rs_b[pp]])
                        kb.op(kb.dve, lambda: V.tensor_tensor(oT[dlo:dhi, c, q0:q0 + nq], po[pp][dlo:dhi, 0:nq], rdsh[pp][dlo:dhi, 0:nq], ALU.mult),
                              r=[po_b[pp], rs_b[pp]], wp=[oT_b])
            kb.barrier()
        with contextlib.ExitStack() as s3:
            hs = [self.S(s3, "c_h%d" % i, [128, D], F32) for i in range(3)]
            hs_b = [Buf() for _ in range(3)]
            po = [self.P(s3, "c_po%d" % i, [128, 512], F32) for i in range(4)]
            po_b = [PB() for _ in range(4)]
            ip = 0
            for ti, (t0, n) in enumerate(TILES):
                s = ti % 3
                kb.dma("sp", hs[s][0:n, :], self.h_src(src, t0, n), w=[hs_b[s]])
                for hh in range(2):
                    p = ip % 4
                    ip += 1
                    for c in range(8):
                        kb.op(kb.pe, lambda: T.matmul(po[p][0:n, :], oT[:, c, t0:t0 + n], w_o[:, c, hh * 512:(hh + 1) * 512], start=(c == 0), stop=(c == 7)),
                              r=[oT_b, w_o_b], w=[po_b[p]] if c == 0 else [], wp=[po_b[p]] if c else [], inc=(c == 7))
                    kb.op(kb.dve, lambda: V.tensor_tensor(hs[s][0:n, hh * 512:(hh + 1) * 512], po[p][0:n, :], hs[s][0:n, hh * 512:(hh + 1) * 512], ALU.add),
                          r=[po_b[p], hs_b[s]], wp=[hs_b[s]])
                dap = self.h_dst(dst, t0, n)
                if dap is not None:
                    kb.dma("sp", dap, hs[s][0:n, :], r=[hs_b[s]])


def host_inputs(inp):
    f = lambda a: np.ascontiguousarray(np.asarray(a, dtype=np.float32))
    rep = lambda a: np.ascontiguousarray(np.broadcast_to(np.asarray(a, np.float32)[:, None, :], (a.shape[0], 128, a.shape[1])))
    colT = lambda a, c: np.ascontiguousarray(np.asarray(a, np.float32).reshape(a.shape[0], c, 128).transpose(0, 2, 1))
    ln = np.concatenate([np.asarray(inp["ln_mix"], np.float32), np.asarray(inp["ln_mlp"], np.float32)], 0)
    lnT = np.ascontiguousarray(ln.reshape(8, 8, 128).transpose(2, 0, 1))
    cw = np.asarray(inp["ssd_conv_w"], np.float32)
    cwT = np.ascontiguousarray(cw.reshape(2, 4, 32, 128).transpose(0, 3, 2, 1))
    k = np.arange(128)
    tri = (k[:, None] <= k[None, :]).astype(np.float32)
    inv = 1.0 / (10000.0 ** (np.arange(0, 32, 2, dtype=np.float32) / 32.0))
    ang = np.arange(NT, dtype=np.float32)[:, None] * inv[None, :].astype(np.float32)
    common = {
        "meta": f(inp["meta_tokens"]),
        "ssd_w_in": f(inp["ssd_w_in"]), "ssd_w_out": f(inp["ssd_w_out"]),
        "mla_w_in": f(inp["mla_w_in"]), "mla_w_q_b": f(inp["mla_w_q_b"]), "mla_w_kv_b": f(inp["mla_w_kv_b"]),
        "mla_w_out": f(inp["mla_w_out"]), "mlp_w_up": f(inp["mlp_w_up"]), "mlp_w_down": f(inp["mlp_w_down"]),
        "lnT": lnT, "cw": cwT, "cb": colT(inp["ssd_conv_b"], 32),
        "dtb_rep": rep(inp["ssd_dt_bias"]), "alog_rep": rep(inp["ssd_a_log"]), "dskip_rep": rep(inp["ssd_d"]),
        "ssd_normT": colT(inp["ssd_norm"], 16), "q_a_T": colT(inp["mla_q_a_norm"], 3), "kv_a_T": colT(inp["mla_kv_a_norm"], 2),
        "gq_rep": rep(inp["mla_q_norm"]), "gk_rep": rep(inp["mla_k_norm"]),
        "ident": np.eye(128, dtype=np.float32), "tri": tri, "ltri": np.ascontiguousarray(1.0 - tri),
        "ones": np.ones((128, 128), np.float32),
        "cos": np.cos(ang).astype(np.float32), "sin": np.sin(ang).astype(np.float32),
    }
    return common


FULL_PHASES = [
    ("ssd", 0, 0, "x", "h"), ("mlp", 0, "h", "h"),
    ("mla", 0, 1, "h", "h"), ("mlp", 1, "h", "h"),
    ("ssd", 1, 2, "h", "h"), ("mlp", 2, "h", "h"),
    ("mla", 1, 3, "h", "h"), ("mlp", 3, "h", "y"),
]


def run(inputs, phases, cores=8):
    common = host_inputs(inputs)
    x = np.asarray(inputs["x"], np.float32)
    prog = Prog(phases)
    in_maps = []
    for c in range(cores):
        m = dict(common)
        m["x"] = np.ascontiguousarray(x[c])
        in_maps.append(m)
    res = run_bass_kernel_spmd(prog.nc, in_maps, core_ids=list(range(cores)))
    return np.stack([np.asarray(r["y"]) for r in res.results], 0)


def kernel(**inputs):
    return run(inputs, FULL_PHASES, 8).astype(np.float32)
```

```python
import contextlib
import numpy as np
import concourse.bass as bass
import concourse.mybir as mybir
from concourse.bass_utils import run_bass_kernel_spmd

F32, BF16 = mybir.dt.float32, mybir.dt.bfloat16
AF = mybir.ActivationFunctionType
ALU = mybir.AluOpType
AX = mybir.AxisListType

NT, NM, D, SEQ = 4112, 16, 1024, 4096
TILES = [(0, 16)] + [(16 + 128 * j, 128) for j in range(32)]
EPS = 1e-6
DFF = 4096
SSD_IN = 6176
NH_S = 32
SSD_LEAD = 0.8
MLA_H = 16
QK = 96


class Buf:
    __slots__ = ("w", "r", "name", "ps")

    def __init__(self, name="", ps=False):
        self.w = {}
        self.r = {}
        self.name = name
        self.ps = ps


def PB():
    return Buf(ps=True)


class Eng:
    def __init__(self, name, e, sem):
        self.name, self.e, self.sem, self.cnt, self.seen = name, e, sem, 0, {}


class KB:
    def __init__(self, nc, es):
        self.nc = nc
        mk = lambda n: es.enter_context(nc.semaphore(n))
        self.pe = Eng("pe", nc.tensor, mk("s_pe"))
        self.act = Eng("act", nc.scalar, mk("s_act"))
        self.dve = Eng("dve", nc.vector, mk("s_dve"))
        self.pool = Eng("pool", nc.gpsimd, mk("s_pool"))
        self.sp = Eng("sp", nc.sync, mk("s_sp"))
        self.engs = [self.pe, self.act, self.dve, self.pool, self.sp]
        self.dsem = {"sp": [[mk("d_sp%d" % i), 0] for i in range(24)],
                     "pool": [[mk("d_pl%d" % i), 0] for i in range(8)]}
        self.drr = {"sp": 0, "pool": 0}
        self.nins = 0

    def _wait(self, E, toks):
        for key, (sem, val) in toks.items():
            if E is self.pe and key == "pe":
                continue
            if E.seen.get(key, 0) >= val:
                continue
            E.e.wait_ge(sem, val)
            E.seen[key] = val

    @staticmethod
    def _add(need, d):
        for k, sv in d.items():
            if k not in need or need[k][1] < sv[1]:
                need[k] = sv

    def _deps(self, r, w, wp, ekey=None):
        need = {}
        for b in r:
            self._add(need, b.w)
            if b.ps:
                self._add(need, {k: v for k, v in b.r.items() if k != ekey})
        for b in w:
            self._add(need, b.w)
            self._add(need, b.r)
        for b in wp:
            self._add(need, b.r)
            if b.ps:
                self._add(need, b.w)
        return need

    def _reg(self, key, tok, r, w, wp):
        for b in r:
            if key not in b.r or b.r[key][1] < tok[1]:
                b.r[key] = tok
        for b in w:
            b.w = {key: tok}
            b.r = {}
        for b in wp:
            if key not in b.w or b.w[key][1] < tok[1]:
                b.w[key] = tok

    def op(self, E, fn, r=(), w=(), wp=(), inc=True):
        self._wait(E, self._deps(r, w, wp, E.name))
        ins = fn()
        self.nins += 1
        if inc:
            E.cnt += 1
            ins.then_inc(E.sem, 1)
            tok = (E.sem, E.cnt)
        else:
            tok = (E.sem, E.cnt + 1)
        self._reg(E.name, tok, r, w, wp)

    def dma(self, Q, out, in_, r=(), w=(), wp=()):
        E = self.sp if Q == "sp" else self.pool
        self._wait(E, self._deps(r, w, wp))
        lst = self.dsem[Q]
        i = self.drr[Q]
        self.drr[Q] = (i + 1) % len(lst)
        sem, cnt = lst[i]
        key = (Q, i)
        if cnt > 0 and E.seen.get(key, 0) < cnt:
            E.e.wait_ge(sem, cnt)
            E.seen[key] = cnt
        ins = E.e.dma_start(out=out, in_=in_)
        ins.then_inc(sem, 16)
        self.nins += 1
        lst[i][1] = cnt + 16
        self._reg(key, (sem, cnt + 16), r, w, wp)

    def barrier(self):
        toks = {}
        for E in self.engs:
            if E.cnt > 0:
                toks[E.name] = (E.sem, E.cnt)
        for Q, lst in self.dsem.items():
            for i, (sem, cnt) in enumerate(lst):
                if cnt > 0:
                    toks[(Q, i)] = (sem, cnt)
        for E in self.engs:
            self._wait(E, toks)


def bc(ap, shape, axis):
    return ap.unsqueeze(axis).to_broadcast(list(shape))


class Prog:
    def __init__(self, phases):
        self.phases = phases
        nc = bass.Bass("TRN2", target_bir_lowering=False)
        self.nc = nc
        di = lambda name, shape: nc.dram_tensor(name, list(shape), F32, kind="ExternalInput").ap()
        self.x = di("x", [SEQ, D])
        self.meta = di("meta", [NM, D])
        self.w_ssd_in = di("ssd_w_in", [2, D, SSD_IN])
        self.w_ssd_out = di("ssd_w_out", [2, 2048, D])
        self.w_mla_in = di("mla_w_in", [2, D, 672])
        self.w_mla_qb = di("mla_w_q_b", [2, 384, 1536])
        self.w_mla_kvb = di("mla_w_kv_b", [2, 256, 2048])
        self.w_mla_out = di("mla_w_out", [2, D, D])
        self.w_up = di("mlp_w_up", [4, D, DFF])
        self.w_dn = di("mlp_w_down", [4, DFF, D])
        self.lnT_d = di("lnT", [128, 8, 8])
        self.cw_d = di("cw", [2, 128, 32, 4])
        self.cb_d = di("cb", [2, 128, 32])
        self.dtb_d = di("dtb_rep", [2, 128, 32])
        self.alog_d = di("alog_rep", [2, 128, 32])
        self.dsk_d = di("dskip_rep", [2, 128, 32])
        self.sng_d = di("ssd_normT", [2, 128, 16])
        self.qag_d = di("q_a_T", [2, 128, 3])
        self.kvag_d = di("kv_a_T", [2, 128, 2])
        self.gq_d = di("gq_rep", [2, 128, 96])
        self.gk_d = di("gk_rep", [2, 128, 96])
        self.ident_d = di("ident", [128, 128])
        self.tri_d = di("tri", [128, 128])
        self.ltri_d = di("ltri", [128, 128])
        self.ones_d = di("ones", [128, 128])
        self.cos_d = di("cos", [NT, 16])
        self.sin_d = di("sin", [NT, 16])
        self.y = nc.dram_tensor("y", [SEQ, D], F32, kind="ExternalOutput").ap()
        self.hd = nc.dram_tensor("hd", [NT, D], F32, kind="Internal").ap()
        self.qT_d = nc.dram_tensor("qT_d", [MLA_H, QK, NT], BF16, kind="Internal").ap()
        self.kT_d = nc.dram_tensor("kT_d", [MLA_H, QK, NT], BF16, kind="Internal").ap()
        self.va_d = nc.dram_tensor("va_d", [NT, 8, 192], BF16, kind="Internal").ap()

        with contextlib.ExitStack() as es:
            self.kb = KB(nc, es)
            self.build(es)

    def S(self, es, name, shape, dt):
        self.uid = getattr(self, "uid", 0) + 1
        return es.enter_context(self.nc.sbuf_tensor("sb%d_%s" % (self.uid, name), list(shape), dt))

    def P(self, es, name, shape, dt):
        self.uid = getattr(self, "uid", 0) + 1
        return es.enter_context(self.nc.psum_tensor("ps%d_%s" % (self.uid, name), list(shape), dt))

    def h_src(self, kind, t0, n):
        if kind == "x":
            return self.meta[0:16, :] if t0 == 0 else self.x[t0 - 16:t0 - 16 + n, :]
        return self.hd[t0:t0 + n, :]

    def h_dst(self, kind, t0, n):
        if kind == "y":
            return None if t0 == 0 else self.y[t0 - 16:t0 - 16 + n, :]
        return self.hd[t0:t0 + n, :]

    def build(self, es):
        kb, nc = self.kb, self.nc
        self.ident = self.S(es, "ident", [128, 128], BF16)
        self.tri32 = self.S(es, "tri32", [128, 128], F32)
        self.ltri32 = self.S(es, "ltri32", [128, 128], F32)
        self.ones32 = self.S(es, "ones32", [128, 128], F32)
        self.lnT = self.S(es, "lnT", [128, 8, 8], F32)
        self.cb_ = Buf("consts")
        kb.dma("pool", self.ident[:], self.ident_d, wp=[self.cb_])
        kb.dma("sp", self.tri32[:], self.tri_d, wp=[self.cb_])
        kb.dma("sp", self.ltri32[:], self.ltri_d, wp=[self.cb_])
        kb.dma("sp", self.ones32[:], self.ones_d, wp=[self.cb_])
        kb.dma("sp", self.lnT[:], self.lnT_d, wp=[self.cb_])
        for ph in self.phases:
            kind = ph[0]
            with contextlib.ExitStack() as pes:
                if kind == "mlp":
                    self.phase_mlp(pes, *ph[1:])
                elif kind == "ssd":
                    self.phase_ssd(pes, *ph[1:])
                elif kind == "mla":
                    self.phase_mla(pes, *ph[1:])
                kb.barrier()
        kb.barrier()

    def rstd_newton(self, st, st_b, n, inv_n, eps):
        kb, nc = self.kb, self.nc
        V = nc.vector
        I32 = mybir.dt.int32
        for f in self.rstd_newton_ops(st, st_b, n, inv_n, eps):
            f()

    def rstd_newton_ops(self, st, st_b, n, inv_n, eps):
        kb, nc = self.kb, self.nc
        V = nc.vector
        I32 = mybir.dt.int32
        x, g, t = st[0:n, 1:2], st[0:n, 2:3], st[0:n, 3:4]
        ops = []
        ops.append(lambda: kb.op(kb.dve, lambda: V.tensor_scalar(x, st[0:n, 0:1], inv_n, eps, ALU.mult, ALU.add), r=[st_b], wp=[st_b]))
        ops.append(lambda: kb.op(kb.dve, lambda: V.tensor_scalar(g.bitcast(I32), x.bitcast(I32), 1, None, ALU.arith_shift_right), r=[st_b], wp=[st_b]))
        ops.append(lambda: kb.op(kb.dve, lambda: V.tensor_scalar(g.bitcast(I32), g.bitcast(I32), -1, 0x5f3759df, ALU.mult, ALU.add), r=[st_b], wp=[st_b]))
        for _ in range(2):
            ops.append(lambda: kb.op(kb.dve, lambda: V.scalar_tensor_tensor(t, g, x, g, ALU.mult, ALU.mult), r=[st_b], wp=[st_b]))
            ops.append(lambda: kb.op(kb.dve, lambda: V.tensor_scalar(t, t, -0.5, 1.5, ALU.mult, ALU.add), r=[st_b], wp=[st_b]))
            ops.append(lambda: kb.op(kb.dve, lambda: V.tensor_tensor(g, g, t, ALU.mult), r=[st_b], wp=[st_b]))
        return ops

    def norm_T(self, hb, h_ap, n, gain_ap, dst_ap, dst_b, sc, newton=False, part=None):
        kb, nc = self.kb, self.nc
        junk, junk_b, ss, ss_b, xn, xn_b, tp, tp_b = sc
        if part == "B":
            return self._norm_T_b(n, gain_ap, dst_ap, dst_b, sc)
        kb.op(kb.act, lambda: nc.scalar.activation(out=junk[0:n, :], in_=h_ap, func=AF.Square, accum_out=ss[0:n, 0:1]),
              r=[hb], w=[junk_b, ss_b] if junk_b is not xn_b else [xn_b, ss_b])
        if newton:
            self.rstd_newton(ss, ss_b, n, 1.0 / D, EPS)
        else:
            kb.op(kb.act, lambda: nc.scalar.activation(out=ss[0:n, 1:2], in_=ss[0:n, 0:1], func=AF.Ln, bias=EPS, scale=1.0 / D),
                  r=[ss_b], wp=[ss_b])
            kb.op(kb.act, lambda: nc.scalar.activation(out=ss[0:n, 2:3], in_=ss[0:n, 1:2], func=AF.Exp, scale=-0.5),
                  r=[ss_b], wp=[ss_b])
        kb.op(kb.dve, lambda: nc.vector.tensor_scalar(xn[0:n, :], h_ap, ss[0:n, 2:3], None, ALU.mult),
              r=[hb, ss_b], w=[xn_b])
        if part == "A":
            return
        self._norm_T_b(n, gain_ap, dst_ap, dst_b, sc)

    def _norm_T_b(self, n, gain_ap, dst_ap, dst_b, sc):
        kb, nc = self.kb, self.nc
        junk, junk_b, ss, ss_b, xn, xn_b, tp, tp_b = sc
        for c in range(8):
            kb.op(kb.pe, lambda c=c: nc.tensor.transpose(tp[:, c, 0:n], xn[0:n, c * 128:(c + 1) * 128], self.ident[0:n, 0:n]),
                  r=[xn_b, self.cb_], w=[tp_b] if c == 0 else [], wp=[tp_b] if c else [], inc=(c == 7))
        kb.op(kb.dve, lambda: nc.vector.tensor_tensor(dst_ap, tp[:, :, 0:n], bc(gain_ap, [128, 8, n], 2), ALU.mult),
              r=[tp_b, self.cb_], w=[dst_b])

    def norm_scratch(self, es, pfx, tp_ps):
        ss = self.S(es, pfx + "ss", [128, 4], F32)
        xn = self.S(es, pfx + "xn", [128, 1024], BF16)
        tp = tp_ps[:].bitcast(BF16)[:, 0:1024].rearrange("p (c t) -> p c t", c=8)
        xn_b = Buf()
        return (xn, xn_b, ss, Buf(), xn, xn_b, tp, PB())

    def phase_mlp(self, es, li, src, dst):
        kb, nc = self.kb, self.nc
        wup = self.S(es, "wup", [128, 8, DFF], BF16)
        wdn = self.S(es, "wdn", [128, 32, D], BF16)
        wup_b = [Buf() for _ in range(8)]
        wdn_b = [Buf() for _ in range(8)]
        upv = self.w_up[li].rearrange("(c p) f -> p c f", p=128)
        dnv = self.w_dn[li].rearrange("(c p) d -> p c d", p=128)
        for i in range(8):
            kb.dma("pool", wup[:, :, i * 512:(i + 1) * 512], upv[:, :, i * 512:(i + 1) * 512], w=[wup_b[i]])
        for i in range(8):
            kb.dma("pool", wdn[:, 4 * i:4 * i + 4, :], dnv[:, 4 * i:4 * i + 4, :], w=[wdn_b[i]])
        NSLOT = 7
        hs = [self.S(es, "mh%d" % i, [128, D], F32) for i in range(NSLOT)]
        hs_b = [Buf() for _ in range(NSLOT)]
        xnT = self.S(es, "m_xnT", [128, 8, 512], BF16)
        xnT_b = Buf()
        uT = self.S(es, "m_uT", [128, 32, 512], BF16)
        uT_b = [Buf() for _ in range(32)]
        r32 = [self.S(es, "m_r32_%d" % i, [128, 512], F32) for i in range(2)]
        r32_b = [Buf(), Buf()]
        tp_ps = [self.P(es, "m_tp%d" % i, [128, 512], F32) for i in range(2)]
        pu = [self.P(es, "m_pu%d" % i, [128, 512], F32) for i in range(3)]
        pu_b = [PB() for _ in range(3)]
        pd = [self.P(es, "m_pd%d" % i, [128, 512], F32) for i in range(3)]
        pd_b = [PB() for _ in range(3)]
        nsc = [self.norm_scratch(es, "m%d" % i, tp_ps[i]) for i in range(2)]
        gain = self.lnT[:, 4 + li, :]
        groups = [[TILES[0]]] + [TILES[1 + 4 * g:5 + 4 * g] for g in range(8)]
        NG = len(groups)
        slots_all = []
        sl_ = 0
        for grp in groups:
            slots_all.append([(sl_ + i) % NSLOT for i in range(len(grp))])
            sl_ += len(grp)
        loaded = set()

        def load_tile(gi, k):
            if (gi, k) in loaded:
                return
            loaded.add((gi, k))
            (t0, n), s = groups[gi][k], slots_all[gi][k]
            kb.dma("sp", hs[s][0:n, :], self.h_src(src, t0, n), w=[hs_b[s]])

        def norm_part(gi, k, part):
            (t0, n), s = groups[gi][k], slots_all[gi][k]
            off = sum(nn for _, nn in groups[gi][:k])
            self.norm_T(hs_b[s], hs[s][0:n, :], n, gain, xnT[:, :, off:off + n], xnT_b, nsc[k % 2], part=part)

        for k in range(len(groups[0])):
            load_tile(0, k)
            norm_part(0, k, None)
        iu = 0
        ipd = 0
        for gi, grp in enumerate(groups):
            ntok = sum(n for _, n in grp)
            myslots = slots_all[gi]
            nxt = gi + 1 if gi + 1 < NG else None
            if nxt is not None:
                for k in range(min(len(groups[nxt]), NSLOT - len(grp))):
                    load_tile(nxt, k)
            for fc in range(32):
                p = iu % 3
                for c in range(8):
                    kb.op(kb.pe, lambda c=c, fc=fc, p=p: nc.tensor.matmul(pu[p][:, 0:ntok], wup[:, c, fc * 128:(fc + 1) * 128],
                                                                      xnT[:, c, 0:ntok], start=(c == 0), stop=(c == 7)),
                          r=[wup_b[fc // 4], xnT_b], w=[pu_b[p]] if c == 0 else [], wp=[pu_b[p]] if c else [], inc=(c == 7))
                rr = iu % 2
                kb.op(kb.act, lambda p=p, rr=rr: nc.scalar.activation(out=r32[rr][:, 0:ntok], in_=pu[p][:, 0:ntok], func=AF.Relu),
                      r=[pu_b[p]], w=[r32_b[rr]])
                E = kb.dve if (fc % 2 == 0) else kb.pool
                kb.op(E, lambda rr=rr, fc=fc, E=E: E.e.tensor_tensor(uT[:, fc, 0:ntok], r32[rr][:, 0:ntok], r32[rr][:, 0:ntok], ALU.mult),
                      r=[r32_b[rr]], w=[uT_b[fc]])
                iu += 1
            off = 0
            for kk, ((t0, n), s) in enumerate(zip(grp, myslots)):
                if nxt is not None and kk < len(groups[nxt]):
                    load_tile(nxt, kk)
                    norm_part(nxt, kk, "A")
                for hh in range(2):
                    p = ipd % 3
                    ipd += 1
                    for fc in range(32):
                        kb.op(kb.pe, lambda fc=fc, p=p, off=off, n=n, hh=hh: nc.tensor.matmul(
                            pd[p][0:n, :], uT[:, fc, off:off + n], wdn[:, fc, hh * 512:(hh + 1) * 512],
                            start=(fc == 0), stop=(fc == 31)),
                            r=[uT_b[fc], wdn_b[fc // 4]], w=[pd_b[p]] if fc == 0 else [], wp=[pd_b[p]] if fc else [], inc=(fc == 31))
                    kb.op(kb.dve, lambda p=p, n=n, hh=hh, s=s: nc.vector.tensor_tensor(
                        hs[s][0:n, hh * 512:(hh + 1) * 512], pd[p][0:n, :], hs[s][0:n, hh * 512:(hh + 1) * 512], ALU.add),
                        r=[pd_b[p], hs_b[s]], wp=[hs_b[s]])
                dst_ap = self.h_dst(dst, t0, n)
                if dst_ap is not None:
                    kb.dma("sp", dst_ap, hs[s][0:n, :], r=[hs_b[s]])
                off += n
                if nxt is not None and kk < len(groups[nxt]):
                    norm_part(nxt, kk, "B")
            if nxt is not None:
                for kk in range(len(grp), len(groups[nxt])):
                    load_tile(nxt, kk)
                    norm_part(nxt, kk, None)

    def phase_ssd(self, es, j, li, src, dst):
        from functools import partial
        kb, nc = self.kb, self.nc
        V, A, G, T = nc.vector, nc.scalar, nc.gpsimd, nc.tensor
        w_in = self.S(es, "s_win", [128, 8, SSD_IN], BF16)
        w_out = self.S(es, "s_wout", [128, 16, D], BF16)
        par_b = Buf()
        wdt_b = Buf()
        wx_b = [Buf() for _ in range(8)]
        wz_b = [Buf() for _ in range(4)]
        wout_b = [Buf() for _ in range(4)]
        wv = self.w_ssd_in[j].rearrange("(c p) f -> p c f", p=128)
        kb.dma("pool", w_in[:, :, 6144:6176], wv[:, :, 6144:6176], w=[wdt_b])
        for i in range(8):
            kb.dma("pool", w_in[:, :, 2048 + 512 * i:2560 + 512 * i], wv[:, :, 2048 + 512 * i:2560 + 512 * i], w=[wx_b[i]])
        for i in range(4):
            kb.dma("pool", w_in[:, :, 512 * i:512 * (i + 1)], wv[:, :, 512 * i:512 * (i + 1)], w=[wz_b[i]])
        wov = self.w_ssd_out[j].rearrange("(c p) f -> p c f", p=128)
        for i in range(4):
            kb.dma("pool", w_out[:, 4 * i:4 * i + 4, :], wov[:, 4 * i:4 * i + 4, :], w=[wout_b[i]])
        cw = self.S(es, "s_cw", [128, 32, 4], F32)
        cb = self.S(es, "s_cb", [128, 32], F32)
        dtb = self.S(es, "s_dtb", [128, 32], F32)
        arep = self.S(es, "s_arep", [128, 32], F32)
        dsk = self.S(es, "s_dsk", [128, 32], F32)
        sng = self.S(es, "s_sng", [128, 16], F32)
        kb.dma("sp", cw[:], self.cw_d[j], wp=[par_b])
        kb.dma("sp", cb[:], self.cb_d[j], wp=[par_b])
        kb.dma("sp", dtb[:], self.dtb_d[j], wp=[par_b])
        kb.dma("sp", arep[:], self.alog_d[j], wp=[par_b])
        kb.dma("sp", dsk[:], self.dsk_d[j], wp=[par_b])
        kb.dma("sp", sng[:], self.sng_d[j], wp=[par_b])
        arep_b = Buf()
        kb.op(kb.act, lambda: A.activation(out=arep[:], in_=arep[:], func=AF.Exp), r=[par_b], w=[arep_b])
        kb.op(kb.dve, lambda: V.tensor_scalar(arep[:], arep[:], -1.0, None, ALU.mult), r=[arep_b], w=[arep_b])
        cwb = Buf()
        kb.op(kb.dve, lambda: V.tensor_scalar(cw[:], cw[:], 0.5, None, ALU.mult), r=[par_b], w=[cwb])
        kb.op(kb.dve, lambda: V.tensor_scalar(cb[:], cb[:], 0.5, None, ALU.mult), r=[par_b], wp=[cwb])
        hs = [self.S(es, "s_h%d" % i, [128, D], F32) for i in range(2)]
        hs_b = [Buf(), Buf()]
        tpF = self.P(es, "s_tpF", [128, 512], F32)
        pzs = [self.P(es, "s_pz%d" % i, [128, 512], F32) for i in range(2)]
        fm = self.P(es, "s_fm", [128, 512], F32)
        tpB = self.P(es, "s_tpB", [128, 512], F32)
        dTb = self.P(es, "s_dT", [128, 512], F32)
        bm = self.P(es, "s_bm", [128, 512], F32)
        yb = self.P(es, "s_y", [128, 512], F32)
        pz_b = [PB(), PB()]
        fm_b, tpB_b, dT_b, bm_b, y_b = PB(), PB(), PB(), PB(), PB()
        gT = self.S(es, "s_gT", [128, 16, 128], BF16)
        gT_b = Buf()
        _ss = self.S(es, "s_nss", [128, 4], F32)
        _xn = gT[:, 0:8, :].rearrange("p c t -> p (c t)")
        _tp = tpF[:].bitcast(BF16)[:, 0:1024].rearrange("p (c t) -> p c t", c=8)
        nsc = (_xn, gT_b, _ss, Buf(), _xn, gT_b, _tp, PB())
        tpB16 = tpB[:].bitcast(BF16)
        tpBg = tpB16.rearrange("p (c t) -> p c t", c=8)
        xnT = [self.S(es, "s_xnT%d" % i, [128, 8, 131], BF16) for i in range(2)]
        xnT_b = [Buf(), Buf()]
        for i in range(2):
            kb.op(kb.pool, lambda: G.memset(xnT[i][:], 0.0), w=[xnT_b[i]])
        xbc_xs = self.S(es, "s_xbcxs", [128, 16, 128], BF16)
        xsT_b = [Buf() for _ in range(16)]
        xbc_bc = [self.S(es, "s_xbcbc%d" % i, [128, 16, 128], BF16) for i in range(2)]
        bc_b = [[Buf() for _ in range(16)] for _ in range(2)]
        NU, NA = 4, 7
        u = [self.S(es, "s_u%d" % i, [128, 131], F32) for i in range(NU)]
        u_b = [Buf() for _ in range(NU)]
        acc = [self.S(es, "s_acc%d" % i, [128, 128], F32) for i in range(NA)]
        acc_b = [Buf() for _ in range(NA)]
        th = [self.S(es, "s_th%d" % i, [128, 128], F32) for i in range(2)]
        th_b = [Buf(), Buf()]
        xs_tm = self.S(es, "s_xstm", [128, 2048], BF16)
        xs_b = [Buf() for _ in range(8)]
        B_tm = self.S(es, "s_Btm", [128, 1024], BF16)
        Btm_b = Buf()
        smF = self.S(es, "s_smF", [128, 4, 32], F32)
        EXPT, ACS, TMP, DTE = range(4)
        smF_b = [Buf() for _ in range(4)]
        smB = [self.S(es, "s_smB%d" % i, [128, 5, 32], F32) for i in range(2)]
        DTV, ADT, DFS, CD, W2 = range(5)
        smB_b = [[Buf() for _ in range(5)] for _ in range(2)]
        rhsD = [self.S(es, "s_rhsD%d" % i, [128, 512], F32) for i in range(2)]
        rhsD_b = [Buf(), Buf()]
        Ee = [self.S(es, "s_E%d" % i, [128, 512], F32) for i in range(2)]
        E_b = [Buf(), Buf()]
        CBm = [self.S(es, "s_CBm0", [128, 128], F32)] * 2
        _cb = Buf()
        CBm_b = [_cb, _cb]
        MT = [self.S(es, "s_MT%d" % i, [128, 512], BF16) for i in range(2)]
        MT_b = [Buf(), Buf()]
        xdt = [self.S(es, "s_xdt%d" % i, [128, 256], BF16) for i in range(2)]
        xdt_b = [Buf(), Buf()]
        xdtd = [self.S(es, "s_xdtd%d" % i, [128, 256], BF16) for i in range(2)]
        xdtd_b = [Buf(), Buf()]
        tt = [[self.S(es, "s_t%d_%d" % (k, i), [128, 256], F32) for i in range(2)] for k in range(4)]
        tt_b = [[Buf(), Buf()] for _ in range(4)]
        tt.append(tt[0])
        tt_b.append(tt_b[0])
        stg = self.S(es, "s_stg", [128, 8, 4], F32)
        stg_b = [Buf() for _ in range(8)]
        S32 = self.S(es, "s_S32", [128, 2048], F32)
        Sbf = self.S(es, "s_Sbf", [128, 2048], BF16)
        S32_b = [Buf() for _ in range(8)]
        Sbf_b = [Buf() for _ in range(8)]
        stmp = [self.S(es, "s_stmp0", [128, 256], F32)] * 2
        _sb = Buf()
        stmp_b = [_sb, _sb]
        kb.op(kb.pool, lambda: G.memset(S32[:], 0.0), w=S32_b)
        kb.op(kb.pool, lambda: G.memset(Sbf[:], 0.0), w=Sbf_b)
        v3 = lambda ap: ap.rearrange("p (j l) -> p j l", j=4)
        NTL = len(TILES)

        def f_load(ti):
            t0, n = TILES[ti]
            cur, prev = ti % 2, (ti + 1) % 2
            nprev = TILES[ti - 1][1] if ti > 0 else 0
            X = xnT[cur]
            kb.dma("sp", hs[cur][0:n, :], self.h_src(src, t0, n), w=[hs_b[cur]])
            self.norm_T(hs_b[cur], hs[cur][0:n, :], n, self.lnT[:, li, :], X[:, :, 3:3 + n], xnT_b[cur], nsc, newton=True)
            kb.op(kb.pool, lambda: G.tensor_copy(X[:, :, 0:3], xnT[prev][:, :, nprev:nprev + 3]), r=[xnT_b[prev]], wp=[xnT_b[cur]])

        def f_dt1(ti):
            t0, n = TILES[ti]
            cur = ti % 2
            X = xnT[cur]
            sB, sBb = smB[cur], smB_b[cur]
            pdt = fm[0:n, 0:32]
            for c in range(8):
                kb.op(kb.pe, lambda: T.matmul(pdt, X[:, c, 3:3 + n], w_in[:, c, 6144:6176], start=(c == 0), stop=(c == 7)),
                      r=[wdt_b, xnT_b[cur]], w=[fm_b] if c == 0 else [], wp=[fm_b] if c else [], inc=(c == 7))
            kb.op(kb.dve, lambda: V.tensor_tensor(sB[0:n, DTV, :], pdt, dtb[0:n, :], ALU.add), r=[fm_b, par_b], w=[sBb[DTV]])
            kb.op(kb.act, lambda: A.activation(out=smF[0:n, EXPT, :], in_=sB[0:n, DTV, :], func=AF.Exp), r=[sBb[DTV]], w=[smF_b[EXPT]])
            kb.op(kb.act, lambda: A.activation(out=sB[0:n, DTV, :], in_=smF[0:n, EXPT, :], func=AF.Ln, bias=1.0), r=[smF_b[EXPT]], w=[sBb[DTV]])
            kb.op(kb.dve, lambda: V.tensor_tensor(sB[0:n, ADT, :], sB[0:n, DTV, :], arep[0:n, :], ALU.mult), r=[sBb[DTV], arep_b], w=[sBb[ADT]])

        def f_dt2(ti):
            t0, n = TILES[ti]
            cur = ti % 2
            sB, sBb = smB[cur], smB_b[cur]
            pacs, ptot = fm[0:n, 32:64], fm[:, 64:96]
            kb.op(kb.pe, lambda: T.matmul(pacs, self.tri32[0:n, 0:n], sB[0:n, ADT, :], start=True, stop=True), r=[sBb[ADT], self.cb_], w=[fm_b])
            kb.op(kb.pe, lambda: T.matmul(ptot, self.ones32[0:n, :], sB[0:n, ADT, :], start=True, stop=True), r=[sBb[ADT], self.cb_], wp=[fm_b])
            kb.op(kb.dve, lambda: V.tensor_copy(smF[0:n, ACS, :], pacs), r=[fm_b], w=[smF_b[ACS]])
            kb.op(kb.act, lambda: A.activation(out=sB[0:n, DFS, :], in_=pacs, func=AF.Exp), r=[fm_b], w=[sBb[DFS]])
            kb.op(kb.act, lambda: A.activation(out=sB[:, CD, :], in_=ptot, func=AF.Exp), r=[fm_b], w=[sBb[CD]])
            kb.op(kb.dve, lambda: V.tensor_tensor(smF[0:n, TMP, :], ptot[0:n, :], smF[0:n, ACS, :], ALU.subtract), r=[fm_b, smF_b[ACS]], w=[smF_b[TMP]])
            kb.op(kb.act, lambda: A.activation(out=smF[0:n, DTE, :], in_=smF[0:n, TMP, :], func=AF.Exp), r=[smF_b[TMP]], w=[smF_b[DTE]])
            kb.op(kb.dve, lambda: V.tensor_tensor(sB[0:n, W2, :], sB[0:n, DTV, :], smF[0:n, DTE, :], ALU.mult), r=[sBb[DTV], smF_b[DTE]], w=[sBb[W2]])

        def conv_dst(ti, cc, n):
            cur = ti % 2
            if cc < 16:
                return xbc_xs[:, cc, 0:n], xsT_b[cc]
            return xbc_bc[cur][:, cc - 16, 0:n], bc_b[cur][cc - 16]

        def f_conv(ti, s):
            t0, n = TILES[ti]
            cur = ti % 2
            X = xnT[cur]
            cc = s
            if 0 <= cc < 32:
                p = cc % 2
                pz = pzs[p][:, 0:3 + n]
                for c in range(8):
                    kb.op(kb.pe, lambda: T.matmul(pz, w_in[:, c, 2048 + cc * 128:2048 + (cc + 1) * 128], X[:, c, 0:3 + n], start=(c == 0), stop=(c == 7)),
                          r=[wx_b[cc // 4], xnT_b[cur]], w=[pz_b[p]] if c == 0 else [], wp=[pz_b[p]] if c else [], inc=(c == 7))
            cc = s - 1
            if 0 <= cc < 32:
                p = cc % 2
                pz = pzs[p][:, 0:3 + n]
                kb.op(kb.act, lambda: A.activation(out=acc[cc % NA][:, 0:n], in_=pz[:, 3:3 + n], func=AF.Identity, bias=cb[:, cc:cc + 1], scale=cw[:, cc, 3:4]),
                      r=[pz_b[p], cwb], w=[acc_b[cc % NA]])
                kb.op(kb.act, lambda: A.copy(u[cc % NU][:, 0:3 + n], pz), r=[pz_b[p]], w=[u_b[cc % NU]])
            for k in range(3):
                cc = s - 2 - k
                if 0 <= cc < 32:
                    p = cc % NU
                    a_ = acc[cc % NA][:, 0:n]
                    kb.op(kb.dve, lambda: V.scalar_tensor_tensor(a_, u[p][:, k:k + n], cw[:, cc, k:k + 1], a_, ALU.mult, ALU.add),
                          r=[u_b[p], cwb], w=[acc_b[cc % NA]])
            cc = s - 5
            if 0 <= cc < 32:
                kb.op(kb.act, lambda: A.activation(out=th[cc % 2][:, 0:n], in_=acc[cc % NA][:, 0:n], func=AF.Tanh), r=[acc_b[cc % NA]], w=[th_b[cc % 2]])
            cc = s - 6
            if 0 <= cc < 32:
                dst_ap, dst_b = conv_dst(ti, cc, n)
                kb.op(kb.dve, lambda: V.scalar_tensor_tensor(dst_ap, th[cc % 2][:, 0:n], 1.0, acc[cc % NA][:, 0:n], ALU.add, ALU.mult),
                      r=[th_b[cc % 2], acc_b[cc % NA]], w=[dst_b])

        def front(ti):
            st = [partial(f_conv, ti, 0), partial(f_dt1, ti), partial(f_conv, ti, 1), partial(f_dt2, ti)]
            st += [partial(f_conv, ti, s) for s in range(2, 38)]
            return st

        def b_trx(ti, half):
            t0, n = TILES[ti]
            for k in range(8):
                cc = half * 8 + k
                kb.op(kb.pe, lambda: T.transpose(tpB16[0:n, k * 128:(k + 1) * 128], xbc_xs[:, cc, 0:n], self.ident[:, :]),
                      r=[xsT_b[cc], self.cb_], w=[tpB_b] if k == 0 else [], wp=[tpB_b] if k else [], inc=(k == 7))
            kb.op(kb.act, lambda: A.copy(xs_tm[0:n, half * 1024:(half + 1) * 1024], tpB16[0:n, :]), r=[tpB_b], w=xs_b[4 * half:4 * half + 4])

        def b_trB(ti):
            t0, n = TILES[ti]
            cur = ti % 2
            for g in range(8):
                kb.op(kb.pe, lambda: T.transpose(tpB16[0:n, g * 128:(g + 1) * 128], xbc_bc[cur][:, g, 0:n], self.ident[:, :]),
                      r=[bc_b[cur][g], self.cb_], w=[tpB_b] if g == 0 else [], wp=[tpB_b] if g else [], inc=(g == 7))
            kb.op(kb.dve, lambda: V.tensor_copy(B_tm[0:n, :], tpB16[0:n, :]), r=[tpB_b], w=[Btm_b])

        def gchain(ti, g):
            t0, n = TILES[ti]
            cur = ti % 2
            X = xnT[cur]
            sB, sBb = smB[cur], smB_b[cur]
            b2 = g % 2
            hsl = slice(4 * g, 4 * g + 4)
            gsl = slice(g * 256, (g + 1) * 256)
            BT, CT = xbc_bc[cur][:, g, 0:n], xbc_bc[cur][:, 8 + g, 0:n]
            BTb, CTb = bc_b[cur][g], bc_b[cur][8 + g]
            x3 = lambda ap: ap.rearrange("p (j f) -> p j f", j=4)
            rD = rhsD[b2][0:n, 0:4 * n]
            Ev = Ee[b2][0:n, 0:4 * n]
            MTv = MT[b2][0:n, 0:4 * n]
            pcb = bm[0:n, 0:n]
            pst = bm[:, 256:512]
            zq = fm[0:n, 256:512]
            xs3 = x3(xs_tm[0:n, gsl])
            t1, t2, t3, szg, thg = [tt[k][b2][0:n, :] for k in range(5)]
            t1b, t2b, t3b, szb, thb = [tt_b[k][b2] for k in range(5)]
            steps = []

            def s1():
                kb.op(kb.pool, lambda: G.tensor_tensor(v3(rD), bc(sB[0:n, ADT, hsl], [n, 4, n], 2), bc(self.tri32[0:n, 0:n], [n, 4, n], 1), ALU.mult),
                      r=[sBb[ADT], self.cb_], w=[rhsD_b[b2]])
                kb.op(kb.pool, lambda: G.tensor_tensor(x3(xdt[b2][0:n, :]), xs3, bc(sB[0:n, DTV, hsl], [n, 4, 64], 2), ALU.mult),
                      r=[xs_b[g], sBb[DTV]], w=[xdt_b[b2]])
            steps.append(s1)

            def s2():
                kb.op(kb.pe, lambda: T.matmul(dTb[0:n, 0:4 * n], self.ltri32[0:n, 0:n], rD, start=True, stop=True), r=[rhsD_b[b2], self.cb_], w=[dT_b])
                kb.op(kb.act, lambda: A.activation(out=Ev, in_=dTb[0:n, 0:4 * n], func=AF.Exp), r=[dT_b], w=[E_b[b2]])
                kb.op(kb.pool, lambda: G.tensor_tensor(x3(xdtd[b2][0:n, :]), xs3, bc(sB[0:n, W2, hsl], [n, 4, 64], 2), ALU.mult),
                      r=[xs_b[g], sBb[W2]], w=[xdtd_b[b2]])
            steps.append(s2)

            def s3():
                kb.op(kb.pe, lambda: T.matmul(pcb, BT, CT, start=True, stop=True), r=[BTb, CTb], w=[bm_b])
                kb.op(kb.dve, lambda: V.tensor_tensor(CBm[b2][0:n, 0:n], pcb, self.tri32[0:n, 0:n], ALU.mult), r=[bm_b, self.cb_], w=[CBm_b[b2]])
                for c in range(8):
                    kb.op(kb.pe, lambda: T.matmul(zq, X[:, c, 3:3 + n], w_in[:, c, g * 256:(g + 1) * 256], start=(c == 0), stop=(c == 7)),
                          r=[wz_b[g // 2], xnT_b[cur]], w=[fm_b] if c == 0 else [], wp=[fm_b] if c else [], inc=(c == 7))
                kb.op(kb.act, lambda: A.activation(out=thg, in_=zq, func=AF.Tanh, scale=0.5), r=[fm_b], w=[thb])
                kb.op(kb.dve, lambda: V.scalar_tensor_tensor(szg, thg, 1.0, zq, ALU.add, ALU.mult), r=[thb, fm_b], w=[szb])
                kb.op(kb.pool, lambda: G.tensor_tensor(v3(MTv), v3(Ev), bc(CBm[b2][0:n, 0:n], [n, 4, n], 1), ALU.mult),
                      r=[E_b[b2], CBm_b[b2]], w=[MT_b[b2]])
            steps.append(s3)

            def s4():
                for jj in range(4):
                    kb.op(kb.pe, lambda: T.matmul(yb[0:n, jj * 64:(jj + 1) * 64], MTv[:, jj * n:(jj + 1) * n], xdt[b2][0:n, jj * 64:(jj + 1) * 64], start=True, stop=True),
                          r=[MT_b[b2], xdt_b[b2]], w=[y_b] if jj == 0 else [], wp=[y_b] if jj else [], inc=False)
                kb.op(kb.pe, lambda: T.matmul(yb[0:n, 256:512], CT, Sbf[:, gsl], start=True, stop=True), r=[CTb, Sbf_b[g]], wp=[y_b])
                kb.op(kb.pool, lambda: G.tensor_tensor(x3(t3), xs3, bc(dsk[0:n, hsl], [n, 4, 64], 2), ALU.mult), r=[xs_b[g], par_b], w=[t3b])
                kb.op(kb.dve, lambda: V.tensor_tensor(x3(t1), x3(yb[0:n, 256:512]), bc(sB[0:n, DFS, hsl], [n, 4, 64], 2), ALU.mult),
                      r=[y_b, sBb[DFS]], w=[t1b])
                kb.op(kb.dve, lambda: V.tensor_tensor(t2, yb[0:n, 0:256], t1, ALU.add), r=[y_b, t1b], w=[t2b])
            steps.append(s4)

            def s5():
                kb.op(kb.pe, lambda: T.matmul(pst, B_tm[0:n, g * 128:(g + 1) * 128], xdtd[b2][0:n, :], start=True, stop=True),
                      r=[Btm_b, xdtd_b[b2]], w=[bm_b])
                kb.op(kb.pool, lambda: G.tensor_tensor(x3(stmp[b2][:, :]), x3(S32[:, gsl]), bc(sB[:, CD, hsl], [128, 4, 64], 2), ALU.mult),
                      r=[S32_b[g], sBb[CD]], w=[stmp_b[b2]])
                kb.op(kb.dve, lambda: V.tensor_tensor(S32[:, gsl], stmp[b2][:, :], pst, ALU.add), r=[stmp_b[b2], bm_b], w=[S32_b[g]])
                kb.op(kb.act, lambda: A.copy(Sbf[:, gsl], S32[:, gsl]), r=[S32_b[g]], w=[Sbf_b[g]])
            steps.append(s5)

            def s6():
                kb.op(kb.pool, lambda: G.tensor_tensor(t2, t2, t3, ALU.add), r=[t3b], w=[t2b])
                kb.op(kb.pool, lambda: G.tensor_tensor(t2, t2, szg, ALU.mult), r=[szb], w=[t2b])
            steps.append(s6)

            sq = lambda: kb.op(kb.act, lambda: A.activation(out=t1, in_=t2, func=AF.Square, accum_out=stg[0:n, g, 0:1]), r=[t2b], w=[t1b, stg_b[g]])
            newt = self.rstd_newton_ops(stg[:, g, :], stg_b[g], n, 1.0 / 256, 4.0 * EPS)
            gnf = lambda: kb.op(kb.dve, lambda: V.tensor_scalar(xs_tm[0:n, gsl], t2, stg[0:n, g, 2:3], None, ALU.mult), r=[t2b, stg_b[g]], w=[xs_b[g]])
            return steps, (sq, newt, gnf)

        def b_gT(ti, half):
            t0, n = TILES[ti]
            for k in range(8):
                cc = half * 8 + k
                kb.op(kb.pe, lambda: T.transpose(tpBg[:, k, 0:n], xs_tm[0:n, cc * 128:(cc + 1) * 128], self.ident[0:n, 0:n]),
                      r=[xs_b[cc // 2], self.cb_], w=[tpB_b] if k == 0 else [], wp=[tpB_b] if k else [], inc=(k == 7))
            kb.op(kb.dve, lambda: V.tensor_tensor(gT[:, half * 8:(half + 1) * 8, 0:n], tpBg[:, :, 0:n], bc(sng[:, half * 8:(half + 1) * 8], [128, 8, n], 2), ALU.mult),
                  r=[tpB_b, par_b], w=[gT_b] if half == 0 else [], wp=[gT_b] if half else [])

        def b_out(ti, hh):
            t0, n = TILES[ti]
            cur = ti % 2
            po = dTb[0:n, :] if hh == 0 else yb[0:n, :]
            pbuf = dT_b if hh == 0 else y_b
            for cc in range(16):
                kb.op(kb.pe, lambda: T.matmul(po, gT[:, cc, 0:n], w_out[:, cc, hh * 512:(hh + 1) * 512], start=(cc == 0), stop=(cc == 15)),
                      r=[gT_b, wout_b[cc // 4]], w=[pbuf] if cc == 0 else [], wp=[pbuf] if cc else [], inc=(cc == 15))
            kb.op(kb.dve, lambda: V.tensor_tensor(hs[cur][0:n, hh * 512:(hh + 1) * 512], po, hs[cur][0:n, hh * 512:(hh + 1) * 512], ALU.add),
                  r=[pbuf, hs_b[cur]], wp=[hs_b[cur]])
            if hh == 1:
                dap = self.h_dst(dst, t0, n)
                if dap is not None:
                    kb.dma("sp", dap, hs[cur][0:n, :], r=[hs_b[cur]])

        def back(ti):
            st = [partial(b_trx, ti, 0), partial(b_trx, ti, 1), partial(b_trB, ti)]
            for gp in range(4):
                (ca, ta), (cb_, tb) = gchain(ti, 2 * gp), gchain(ti, 2 * gp + 1)
                for a_, b_ in zip(ca, cb_):
                    st += [a_, b_]

                def tail(ta=ta, tb=tb):
                    ta[0]()
                    tb[0]()
                    for fa, fb in zip(ta[1], tb[1]):
                        fa()
                        fb()
                    ta[2]()
                    tb[2]()
                st.append(tail)
            st += [partial(b_gT, ti, 0), partial(b_gT, ti, 1), partial(b_out, ti, 0), partial(b_out, ti, 1)]
            return st

        def interleave(a, b, lead):
            na, nb = len(a), len(b)
            i = jx = 0
            while i < na or jx < nb:
                if jx >= nb or (i < na and i * nb <= jx * na * lead):
                    a[i]()
                    i += 1
                else:
                    b[jx]()
                    jx += 1

        f_load(0)
        for ti in range(NTL + 1):
            f = front(ti) if ti < NTL else []
            b = back(ti - 1) if ti >= 1 else []
            if ti + 1 < NTL:
                b = b + [partial(f_load, ti + 1)]
            interleave(f, b, SSD_LEAD)

    def phase_ssd_v1(self, es, j, li, src, dst):
        kb, nc = self.kb, self.nc
        V, A, G, T = nc.vector, nc.scalar, nc.gpsimd, nc.tensor
        w_in = self.S(es, "s_win", [128, 8, SSD_IN], BF16)
        w_out = self.S(es, "s_wout", [128, 16, D], BF16)
        win_b, wout_b, par_b = Buf(), Buf(), Buf()
        wv = self.w_ssd_in[j].rearrange("(c p) f -> p c f", p=128)
        for c in range(8):
            kb.dma("pool", w_in[:, c, :], wv[:, c, :], wp=[win_b])
        wov = self.w_ssd_out[j].rearrange("(c p) f -> p c f", p=128)
        for c in range(0, 16, 4):
            kb.dma("pool", w_out[:, c:c + 4, :], wov[:, c:c + 4, :], wp=[wout_b])
        cw = self.S(es, "s_cw", [128, 32, 4], F32)
        cb = self.S(es, "s_cb", [128, 32], F32)
        dtb = self.S(es, "s_dtb", [128, 32], F32)
        arep = self.S(es, "s_arep", [128, 32], F32)
        dsk = self.S(es, "s_dsk", [128, 32], F32)
        sng = self.S(es, "s_sng", [128, 16], F32)
        kb.dma("sp", cw[:], self.cw_d[j], wp=[par_b])
        kb.dma("sp", cb[:], self.cb_d[j], wp=[par_b])
        kb.dma("sp", dtb[:], self.dtb_d[j], wp=[par_b])
        kb.dma("sp", arep[:], self.alog_d[j], wp=[par_b])
        kb.dma("sp", dsk[:], self.dsk_d[j], wp=[par_b])
        kb.dma("sp", sng[:], self.sng_d[j], wp=[par_b])
        arep_b = Buf()
        kb.op(kb.act, lambda: A.activation(out=arep[:], in_=arep[:], func=AF.Exp), r=[par_b], w=[arep_b])
        kb.op(kb.dve, lambda: V.tensor_scalar(arep[:], arep[:], -1.0, None, ALU.mult), r=[arep_b], w=[arep_b])
        hs = [self.S(es, "s_h%d" % i, [128, D], F32) for i in range(2)]
        hs_b = [Buf(), Buf()]
        tp2 = self.P(es, "s_tp2", [128, 1024], F32)
        pzbs = [self.P(es, "s_pz%d" % i, [128, 512], F32) for i in range(2)]
        zqb = self.P(es, "s_zq", [128, 512], F32)
        dTb = self.P(es, "s_dT", [128, 512], F32)
        miscb = self.P(es, "s_misc", [128, 512], F32)
        yb = self.P(es, "s_y", [128, 512], F32)
        pz_b = [PB(), PB()]
        _z = PB()
        zq_b = [_z, _z]
        dT_b = PB()
        misc_b = PB()
        pdt_b = pacs_b = ptot_b = pst_b = misc_b
        pcb_b = [misc_b, misc_b]
        ydg_b = yof_b = PB()
        nsc = self.norm_scratch(es, "s", tp2)
        tp_b = nsc[7]
        tp16 = tp2[:].bitcast(BF16)
        tpg = tp16.rearrange("p (c t) -> p c t", c=16)
        xnT = [self.S(es, "s_xnT%d" % i, [128, 8, 131], BF16) for i in range(2)]
        xnT_b = [Buf(), Buf()]
        for i in range(2):
            kb.op(kb.pool, lambda: G.memset(xnT[i][:], 0.0), w=[xnT_b[i]])
        xbcT = self.S(es, "s_xbcT", [128, 32, 128], BF16)
        xbc_b = [Buf() for _ in range(32)]
        acc = [self.S(es, "s_acc%d" % i, [128, 128], F32) for i in range(3)]
        acc_b = [Buf() for _ in range(3)]
        xs_tm = self.S(es, "s_xstm", [128, 2048], BF16)
        xs_b = Buf()
        B_tm = self.S(es, "s_Btm", [128, 1024], BF16)
        Btm_b = Buf()
        sz = self.S(es, "s_sz", [128, 2048], F32)
        sz_b = [Buf() for _ in range(8)]
        sm = self.S(es, "s_sm", [128, 10, 32], F32)
        DTV, EXPT, ADT, ACS, DFS, CD, DTE, TMP, W2 = range(9)
        sm_b = [Buf() for _ in range(10)]
        rhsD = [self.S(es, "s_rhsD0", [128, 512], F32)] * 2
        _b = Buf()
        rhsD_b = [_b, _b]
        Ee = [self.S(es, "s_E0", [128, 512], F32)] * 2
        _b = Buf()
        E_b = [_b, _b]
        CBm = [self.S(es, "s_CBm%d" % i, [128, 128], F32) for i in range(2)]
        CBm_b = [Buf(), Buf()]
        MT = [self.S(es, "s_MT%d" % i, [128, 512], BF16) for i in range(2)]
        MT_b = [Buf(), Buf()]
        xdt = [self.S(es, "s_xdt%d" % i, [128, 256], BF16) for i in range(2)]
        xdt_b = [Buf(), Buf()]
        xdtd = [self.S(es, "s_xdtd%d" % i, [128, 256], BF16) for i in range(2)]
        xdtd_b = [Buf(), Buf()]
        tt = [[self.S(es, "s_t%d_%d" % (k, i), [128, 256], F32) for i in range(2)] for k in range(3)]
        tt_b = [[Buf(), Buf()] for _ in range(3)]
        tt.append(tt[0])
        tt_b.append(tt_b[0])
        stg = self.S(es, "s_stg", [128, 8, 4], F32)
        stg_b = [Buf() for _ in range(8)]
        gn = self.S(es, "s_gn", [128, 2048], BF16)
        gn_b = Buf()
        gT = self.S(es, "s_gT", [128, 16, 128], BF16)
        gT_b = Buf()
        S32 = self.S(es, "s_S32", [128, 2048], F32)
        Sbf = self.S(es, "s_Sbf", [128, 2048], BF16)
        S32_b = [Buf() for _ in range(8)]
        Sbf_b = [Buf() for _ in range(8)]
        stmp = [self.S(es, "s_stmp0", [128, 256], F32)] * 2
        _b = Buf()
        stmp_b = [_b, _b]
        kb.op(kb.pool, lambda: G.memset(S32[:], 0.0), w=S32_b)
        kb.op(kb.pool, lambda: G.memset(Sbf[:], 0.0), w=Sbf_b)
        nprev = 0
        ipz = 0
        for ti, (t0, n) in enumerate(TILES):
            cur, prev = ti % 2, (ti + 1) % 2
            X = xnT[cur]
            kb.dma("sp", hs[cur][0:n, :], self.h_src(src, t0, n), w=[hs_b[cur]])
            self.norm_T(hs_b[cur], hs[cur][0:n, :], n, self.lnT[:, li, :], X[:, :, 3:3 + n], xnT_b[cur], nsc)
            kb.op(kb.pool, lambda: G.tensor_copy(X[:, :, 0:3], xnT[prev][:, :, nprev:nprev + 3]), r=[xnT_b[prev]], wp=[xnT_b[cur]])
            nprev = n
            for cc in range(32):
                p = ipz % 2
                ipz += 1
                pz = pzbs[p][:, 0:3 + n]
                for c in range(8):
                    kb.op(kb.pe, lambda: T.matmul(pz, w_in[:, c, 2048 + cc * 128:2048 + (cc + 1) * 128], X[:, c, 0:3 + n], start=(c == 0), stop=(c == 7)),
                          r=[win_b, xnT_b[cur]], w=[pz_b[p]] if c == 0 else [], wp=[pz_b[p]] if c else [], inc=(c == 7))
                a_ = acc[p][:, 0:n]
                kb.op(kb.act, lambda: A.activation(out=a_, in_=pz[:, 3:3 + n], func=AF.Identity, bias=cb[:, cc:cc + 1], scale=cw[:, cc, 3:4]),
                      r=[pz_b[p], par_b], w=[acc_b[p]])
                for k in range(3):
                    kb.op(kb.dve, lambda: V.scalar_tensor_tensor(a_, pz[:, k:k + n], cw[:, cc, k:k + 1], a_, ALU.mult, ALU.add),
                          r=[pz_b[p], par_b], w=[acc_b[p]])
                kb.op(kb.act, lambda: A.activation(out=xbcT[:, cc, 0:n], in_=a_, func=AF.Silu), r=[acc_b[p]], w=[xbc_b[cc]])
            for g in range(8):
                zs = g % 2
                zq = zqb[0:n, zs * 256:(zs + 1) * 256]
                for c in range(8):
                    kb.op(kb.pe, lambda: T.matmul(zq, X[:, c, 3:3 + n], w_in[:, c, g * 256:(g + 1) * 256], start=(c == 0), stop=(c == 7)),
                          r=[win_b, xnT_b[cur]], w=[zq_b[zs]] if c == 0 else [], wp=[zq_b[zs]] if c else [], inc=(c == 7))
                kb.op(kb.act, lambda: A.activation(out=sz[0:n, g * 256:(g + 1) * 256], in_=zq, func=AF.Silu), r=[zq_b[zs]], w=[sz_b[g]])
            for cc in range(16):
                kb.op(kb.pe, lambda: T.transpose(tp16[0:n, cc * 128:(cc + 1) * 128], xbcT[:, cc, 0:n], self.ident[:, :]),
                      r=[xbc_b[cc], self.cb_], w=[tp_b] if cc == 0 else [], wp=[tp_b] if cc else [], inc=(cc == 15))
            kb.op(kb.act, lambda: A.copy(xs_tm[0:n, 0:1024], tp16[0:n, 0:1024]), r=[tp_b], w=[xs_b])
            kb.op(kb.dve, lambda: V.tensor_copy(xs_tm[0:n, 1024:2048], tp16[0:n, 1024:2048]), r=[tp_b], wp=[xs_b])
            for g in range(8):
                kb.op(kb.pe, lambda: T.transpose(tp16[0:n, g * 128:(g + 1) * 128], xbcT[:, 16 + g, 0:n], self.ident[:, :]),
                      r=[xbc_b[16 + g], self.cb_], w=[tp_b] if g == 0 else [], wp=[tp_b] if g else [], inc=(g == 7))
            kb.op(kb.act, lambda: A.copy(B_tm[0:n, :], tp16[0:n, 0:1024]), r=[tp_b], w=[Btm_b])
            pdt, pacs, ptot = miscb[0:n, 0:32], miscb[0:n, 32:64], miscb[:, 64:96]
            for c in range(8):
                kb.op(kb.pe, lambda: T.matmul(pdt, X[:, c, 3:3 + n], w_in[:, c, 6144:6176], start=(c == 0), stop=(c == 7)),
                      r=[win_b, xnT_b[cur]], w=[pdt_b] if c == 0 else [], wp=[pdt_b] if c else [], inc=(c == 7))
            smv = lambda k: sm[0:n, k, :]
            kb.op(kb.dve, lambda: V.tensor_tensor(smv(DTV), pdt, dtb[0:n, :], ALU.add), r=[pdt_b, par_b], w=[sm_b[DTV]])
            kb.op(kb.act, lambda: A.activation(out=smv(EXPT), in_=smv(DTV), func=AF.Exp), r=[sm_b[DTV]], w=[sm_b[EXPT]])
            kb.op(kb.act, lambda: A.activation(out=smv(DTV), in_=smv(EXPT), func=AF.Ln, bias=1.0), r=[sm_b[EXPT]], w=[sm_b[DTV]])
            kb.op(kb.dve, lambda: V.tensor_tensor(smv(ADT), smv(DTV), arep[0:n, :], ALU.mult), r=[sm_b[DTV], arep_b], w=[sm_b[ADT]])
            kb.op(kb.pe, lambda: T.matmul(pacs, self.tri32[0:n, 0:n], smv(ADT), start=True, stop=True), r=[sm_b[ADT], self.cb_], w=[pacs_b])
            kb.op(kb.pe, lambda: T.matmul(ptot, self.ones32[0:n, :], smv(ADT), start=True, stop=True), r=[sm_b[ADT], self.cb_], w=[ptot_b])
            kb.op(kb.dve, lambda: V.tensor_copy(smv(ACS), pacs), r=[pacs_b], w=[sm_b[ACS]])
            kb.op(kb.act, lambda: A.activation(out=smv(DFS), in_=pacs, func=AF.Exp), r=[pacs_b], w=[sm_b[DFS]])
            kb.op(kb.act, lambda: A.activation(out=sm[:, CD, :], in_=ptot, func=AF.Exp), r=[ptot_b], w=[sm_b[CD]])
            kb.op(kb.dve, lambda: V.tensor_tensor(smv(TMP), ptot[0:n, :], smv(ACS), ALU.subtract), r=[ptot_b, sm_b[ACS]], w=[sm_b[TMP]])
            kb.op(kb.act, lambda: A.activation(out=smv(DTE), in_=smv(TMP), func=AF.Exp), r=[sm_b[TMP]], w=[sm_b[DTE]])
            kb.op(kb.dve, lambda: V.tensor_tensor(smv(W2), smv(DTV), smv(DTE), ALU.mult), r=[sm_b[DTV], sm_b[DTE]], w=[sm_b[W2]])
            for g in range(8):
                b2 = g % 2
                hsl = slice(4 * g, 4 * g + 4)
                gsl = slice(g * 256, (g + 1) * 256)
                v3 = lambda ap: ap.rearrange("p (j l) -> p j l", j=4)
                rD = rhsD[b2][0:n, 0:4 * n]
                kb.op(kb.pool, lambda: G.tensor_tensor(v3(rD), bc(sm[0:n, ADT, hsl], [n, 4, n], 2), bc(self.tri32[0:n, 0:n], [n, 4, n], 1), ALU.mult),
                      r=[sm_b[ADT], self.cb_], w=[rhsD_b[b2]])
                kb.op(kb.pe, lambda: T.matmul(dTb[0:n, 0:4 * n], self.ltri32[0:n, 0:n], rD, start=True, stop=True), r=[rhsD_b[b2], self.cb_], w=[dT_b])
                Ev = Ee[b2][0:n, 0:4 * n]
                kb.op(kb.act, lambda: A.activation(out=Ev, in_=dTb[0:n, 0:4 * n], func=AF.Exp), r=[dT_b], w=[E_b[b2]])
                pcb = miscb[0:n, 128:128 + n]
                kb.op(kb.pe, lambda: T.matmul(pcb, xbcT[:, 16 + g, 0:n], xbcT[:, 24 + g, 0:n], start=True, stop=True),
                      r=[xbc_b[16 + g], xbc_b[24 + g]], w=[pcb_b[b2]])
                kb.op(kb.dve, lambda: V.tensor_tensor(CBm[b2][0:n, 0:n], pcb, self.tri32[0:n, 0:n], ALU.mult), r=[pcb_b[b2], self.cb_], w=[CBm_b[b2]])
                MTv = MT[b2][0:n, 0:4 * n]
                kb.op(kb.pool, lambda: G.tensor_tensor(v3(MTv), v3(Ev), bc(CBm[b2][0:n, 0:n], [n, 4, n], 1), ALU.mult),
                      r=[E_b[b2], CBm_b[b2]], w=[MT_b[b2]])
                xs3 = xs_tm[0:n, gsl].rearrange("p (j f) -> p j f", j=4)
                x3 = lambda ap: ap.rearrange("p (j f) -> p j f", j=4)
                kb.op(kb.pool, lambda: G.tensor_tensor(x3(xdt[b2][0:n, :]), xs3, bc(sm[0:n, DTV, hsl], [n, 4, 64], 2), ALU.mult),
                      r=[xs_b, sm_b[DTV]], w=[xdt_b[b2]])
                kb.op(kb.pool, lambda: G.tensor_tensor(x3(xdtd[b2][0:n, :]), xs3, bc(sm[0:n, W2, hsl], [n, 4, 64], 2), ALU.mult),
                      r=[xs_b, sm_b[W2]], w=[xdtd_b[b2]])
                for jj in range(4):
                    kb.op(kb.pe, lambda: T.matmul(yb[0:n, jj * 64:(jj + 1) * 64], MTv[:, jj * n:(jj + 1) * n], xdt[b2][0:n, jj * 64:(jj + 1) * 64], start=True, stop=True),
                          r=[MT_b[b2], xdt_b[b2]], w=[ydg_b] if jj == 0 else [], wp=[ydg_b] if jj else [], inc=(jj == 3))
                kb.op(kb.pe, lambda: T.matmul(yb[0:n, 256:512], xbcT[:, 24 + g, 0:n], Sbf[:, gsl], start=True, stop=True),
                      r=[xbc_b[24 + g], Sbf_b[g]], w=[yof_b])
                t1, t2, t3, tj = [tt[k][b2][0:n, :] for k in range(4)]
                kb.op(kb.dve, lambda: V.tensor_tensor(x3(t1), x3(yb[0:n, 256:512]), bc(sm[0:n, DFS, hsl], [n, 4, 64], 2), ALU.mult),
                      r=[yof_b, sm_b[DFS]], w=[tt_b[0][b2]])
                kb.op(kb.dve, lambda: V.tensor_tensor(t2, yb[0:n, 0:256], t1, ALU.add), r=[ydg_b, tt_b[0][b2]], w=[tt_b[1][b2]])
                kb.op(kb.pool, lambda: G.tensor_tensor(x3(t3), xs3, bc(dsk[0:n, hsl], [n, 4, 64], 2), ALU.mult), r=[xs_b, par_b], w=[tt_b[2][b2]])
                kb.op(kb.pool, lambda: G.tensor_tensor(t2, t2, t3, ALU.add), r=[tt_b[2][b2]], w=[tt_b[1][b2]])
                kb.op(kb.pool, lambda: G.tensor_tensor(t2, t2, sz[0:n, gsl], ALU.mult), r=[sz_b[g]], w=[tt_b[1][b2]])
                kb.op(kb.act, lambda: A.activation(out=tj, in_=t2, func=AF.Square, accum_out=stg[0:n, g, 0:1]), r=[tt_b[1][b2]], w=[tt_b[3][b2], stg_b[g]])
                self.rstd_ops(stg[:, g, :], stg_b[g], n, 0, 1.0 / 256)
                kb.op(kb.dve, lambda: V.tensor_scalar(gn[0:n, gsl], t2, stg[0:n, g, 2:3], None, ALU.mult), r=[tt_b[1][b2], stg_b[g]],
                      w=[gn_b] if g == 0 else [], wp=[gn_b] if g else [])
                kb.op(kb.pe, lambda: T.matmul(miscb[:, 256:512], B_tm[0:n, g * 128:(g + 1) * 128], xdtd[b2][0:n, :], start=True, stop=True),
                      r=[Btm_b, xdtd_b[b2]], w=[pst_b])
                kb.op(kb.pool, lambda: G.tensor_tensor(x3(stmp[b2][:, :]), x3(S32[:, gsl]), bc(sm[:, CD, hsl], [128, 4, 64], 2), ALU.mult),
                      r=[S32_b[g], sm_b[CD]], w=[stmp_b[b2]])
                kb.op(kb.dve, lambda: V.tensor_tensor(S32[:, gsl], stmp[b2][:, :], miscb[:, 256:512], ALU.add), r=[stmp_b[b2], pst_b], w=[S32_b[g]])
                kb.op(kb.act, lambda: A.copy(Sbf[:, gsl], S32[:, gsl]), r=[S32_b[g]], w=[Sbf_b[g]])
            for cc in range(16):
                kb.op(kb.pe, lambda: T.transpose(tpg[:, cc, 0:n], gn[0:n, cc * 128:(cc + 1) * 128], self.ident[0:n, 0:n]),
                      r=[gn_b, self.cb_], w=[tp_b] if cc == 0 else [], wp=[tp_b] if cc else [], inc=(cc == 15))
            kb.op(kb.dve, lambda: V.tensor_tensor(gT[:, :, 0:n], tpg[:, :, 0:n], bc(sng[:, :], [128, 16, n], 2), ALU.mult), r=[tp_b, par_b], w=[gT_b])
            for hh in range(2):
                po = dTb[0:n, :] if hh == 0 else yb[0:n, :]
                pbufs = [dT_b] if hh == 0 else [ydg_b]
                for cc in range(16):
                    kb.op(kb.pe, lambda: T.matmul(po, gT[:, cc, 0:n], w_out[:, cc, hh * 512:(hh + 1) * 512], start=(cc == 0), stop=(cc == 15)),
                          r=[gT_b, wout_b], w=pbufs if cc == 0 else [], wp=pbufs if cc else [], inc=(cc == 15))
                kb.op(kb.dve, lambda: V.tensor_tensor(hs[cur][0:n, hh * 512:(hh + 1) * 512], po, hs[cur][0:n, hh * 512:(hh + 1) * 512], ALU.add),
                      r=pbufs + [hs_b[cur]], wp=[hs_b[cur]])
            dap = self.h_dst(dst, t0, n)
            if dap is not None:
                kb.dma("sp", dap, hs[cur][0:n, :], r=[hs_b[cur]])

    def rstd_ops(self, st, st_b, n, c0, inv_n):
        kb, nc = self.kb, self.nc
        kb.op(kb.act, lambda: nc.scalar.activation(out=st[0:n, c0 + 1:c0 + 2], in_=st[0:n, c0:c0 + 1], func=AF.Ln, bias=EPS, scale=inv_n),
              r=[st_b], wp=[st_b])
        kb.op(kb.act, lambda: nc.scalar.activation(out=st[0:n, c0 + 2:c0 + 3], in_=st[0:n, c0 + 1:c0 + 2], func=AF.Exp, scale=-0.5),
              r=[st_b], wp=[st_b])

    def phase_mla(self, es, j, li, src, dst):
        kb, nc = self.kb, self.nc
        V, A, G, T = nc.vector, nc.scalar, nc.gpsimd, nc.tensor
        oT = self.S(es, "oT", [128, 8, NT], BF16)
        oT_b = Buf()
        w_o = self.S(es, "w_o", [128, 8, D], BF16)
        w_o_b = Buf()
        gq = self.S(es, "gq", [128, 96], F32)
        gk = self.S(es, "gk", [128, 96], F32)
        par_b = Buf()
        self.tri16 = self.S(es, "tri16", [128, 128], BF16)
        kb.dma("pool", self.tri16[:], self.tri_d, wp=[par_b])
        kb.dma("sp", gq[:], self.gq_d[j], wp=[par_b])
        kb.dma("sp", gk[:], self.gk_d[j], wp=[par_b])
        with contextlib.ExitStack() as s1:
            w_in = self.S(s1, "a_win", [128, 8, 672], BF16)
            w_qb = self.S(s1, "a_wqb", [128, 3, 1536], BF16)
            w_kvb = self.S(s1, "a_wkvb", [128, 2, 2048], BF16)
            wb = Buf()
            kb.dma("pool", w_in[:], self.w_mla_in[j].rearrange("(c p) f -> p c f", p=128), wp=[wb])
            kb.dma("pool", w_qb[:], self.w_mla_qb[j].rearrange("(c p) f -> p c f", p=128), wp=[wb])
            kb.dma("pool", w_kvb[:], self.w_mla_kvb[j].rearrange("(c p) f -> p c f", p=128), wp=[wb])
            kb.dma("pool", w_o[:], self.w_mla_out[j].rearrange("(c p) f -> p c f", p=128), wp=[w_o_b])
            qag = self.S(s1, "a_qag", [128, 3], F32)
            kvag = self.S(s1, "a_kvag", [128, 2], F32)
            kb.dma("sp", qag[:], self.qag_d[j], wp=[par_b])
            kb.dma("sp", kvag[:], self.kvag_d[j], wp=[par_b])
            tp_ps = self.P(s1, "a_tp", [128, 512], F32)
            latA = self.P(s1, "a_latA", [128, 512], F32)
            latB = self.P(s1, "a_latB", [128, 512], F32)
            big = self.P(s1, "a_big", [128, 2048], F32)
            tph_ps = self.P(s1, "a_tph", [128, 512], F32)
            latA_b, latB_b, tph_b = PB(), PB(), PB()
            big_b = [PB() for _ in range(4)]
            nsc = self.norm_scratch(s1, "a", tp_ps)
            tp5 = tp_ps[:].bitcast(BF16)[:, 0:640].rearrange("p (c t) -> p c t", c=5)
            tp_b = nsc[7]
            tph = tph_ps[:].bitcast(BF16)[:, 0:1024].rearrange("p (c t) -> p c t", c=8)
            qTv = self.qT_d.rearrange("h p t -> p h t")
            kTv = self.kT_d.rearrange("h p t -> p h t")

            class NS:
                pass

            def mkbufs(ci):
                b = NS()
                nm = lambda x: "a%d_%s" % (ci, x)
                b.hs = self.S(s1, nm("h"), [128, D], F32); b.hs_b = Buf()
                b.cs = self.S(s1, nm("cs"), [128, 2, 16], F32); b.cs_b = Buf()
                b.xnT = self.S(s1, nm("xnT"), [128, 8, 128], BF16); b.xnT_b = Buf()
                b.st = self.S(s1, nm("st"), [128, 8], F32); b.st_b = Buf()
                b.qln = self.S(s1, nm("qln"), [128, 384], BF16)
                b.kvln = self.S(s1, nm("kvln"), [128, 256], BF16)
                b.kpe = self.S(s1, nm("kpe"), [128, 32], F32); b.ln_b = Buf()
                b.sqj = self.S(s1, nm("sqj"), [128, 2048], F32); b.sqj_b = Buf()
                b.qlT = self.S(s1, nm("qlT"), [128, 3, 128], BF16)
                b.kvlT = self.S(s1, nm("kvlT"), [128, 2, 128], BF16); b.lT_b = Buf()
                b.raw = self.S(s1, nm("raw"), [128, 2048], F32); b.raw_b = Buf()
                b.s16 = self.S(s1, nm("s16"), [128, 3, 16], F32); b.s16_b = Buf()
                b.rt = self.S(s1, nm("rt"), [128, 4, 256], F32); b.rt_b = [Buf() for _ in range(4)]
                b.kpg = self.S(s1, nm("kpg"), [128, 2, 32], F32); b.kpg_b = Buf()
                b.qbf = self.S(s1, nm("qbf"), [128, 1536], BF16); b.qbf_b = Buf()
                b.stg = [self.S(s1, nm("stg%d" % i), [96, 16, 128], BF16) for i in range(2)]; b.stg_b = [Buf(), Buf()]
                b.vst = self.S(s1, nm("vst"), [128, 8, 192], BF16); b.vst_b = Buf()
                kb.op(kb.pool, lambda: G.memset(b.vst[:], 1.0), w=[b.vst_b])
                return b

            CH = [mkbufs(0), mkbufs(1)]

            def chain(ti):
                t0, n = TILES[ti]
                b = CH[ti % 2]
                xnT, st, st_b, qln, kvln, kpe, ln_b = b.xnT, b.st, b.st_b, b.qln, b.kvln, b.kpe, b.ln_b
                sqj, sqj_b, qlT, kvlT, lT_b, raw, raw_b = b.sqj, b.sqj_b, b.qlT, b.kvlT, b.lT_b, b.raw, b.raw_b
                s16, s16_b, rt, rt_b, kpg, kpg_b, qbf, qbf_b = b.s16, b.s16_b, b.rt, b.rt_b, b.kpg, b.kpg_b, b.qbf, b.qbf_b
                raw3 = raw[0:n, 0:1536].rearrange("p (h f) -> p h f", h=16)
                sq3 = sqj[0:n, 0:1536].rearrange("p (h f) -> p h f", h=16)
                qb3 = qbf[0:n, :].rearrange("p (h f) -> p h f", h=16)
                kv4 = raw[0:n, :].rearrange("p (h f) -> p h f", h=16)
                sqk = sqj[0:n, 0:1024].rearrange("p (h f) -> p h f", h=16)
                kv5 = raw[0:n, :].rearrange("p (c e f) -> p c e f", c=8, e=2)

                def head_T(sg, dstv):
                    for half in range(2):
                        for hh in range(8):
                            h = half * 8 + hh
                            kb.op(kb.pe, lambda: T.transpose(tph[0:96, hh, 0:n], qbf[0:n, h * 96:(h + 1) * 96], self.ident[0:n, 0:n]),
                                  r=[qbf_b, self.cb_], w=[tph_b] if hh == 0 else [], wp=[tph_b] if hh else [], inc=(hh == 7))
                        kb.op(kb.act, lambda: A.copy(b.stg[sg][0:96, half * 8:(half + 1) * 8, 0:n], tph[0:96, :, 0:n]),
                              r=[tph_b], w=[b.stg_b[sg]] if half == 0 else [], wp=[b.stg_b[sg]] if half else [])
                    kb.dma("sp", dstv[:, :, t0:t0 + n], b.stg[sg][0:96, :, 0:n], r=[b.stg_b[sg]])

                def rope(t1, t2, cosb, sinb, o1, o2, three_d, wb_, rd_bufs):
                    if three_d:
                        a_, b_, c_, d_ = [rt[0:n, i, :].rearrange("p (h f) -> p h f", h=16) for i in range(4)]
                    else:
                        a_, b_, c_, d_ = [rt[0:n, i, 0:16] for i in range(4)]
                    kb.op(kb.dve, lambda: V.tensor_tensor(a_, t1, cosb, ALU.mult), r=rd_bufs, w=[rt_b[0]])
                    kb.op(kb.dve, lambda: V.tensor_tensor(b_, t2, sinb, ALU.mult), r=rd_bufs, w=[rt_b[1]])
                    kb.op(kb.dve, lambda: V.tensor_tensor(c_, t1, sinb, ALU.mult), r=rd_bufs, w=[rt_b[2]])
                    kb.op(kb.dve, lambda: V.tensor_tensor(d_, t2, cosb, ALU.mult), r=rd_bufs, w=[rt_b[3]])
                    kb.op(kb.dve, lambda: V.tensor_tensor(o1, a_, b_, ALU.subtract), r=[rt_b[0], rt_b[1]], wp=[wb_])
                    kb.op(kb.dve, lambda: V.tensor_tensor(o2, c_, d_, ALU.add), r=[rt_b[2], rt_b[3]], wp=[wb_])

                def s0():
                    kb.dma("sp", b.hs[0:n, :], self.h_src(src, t0, n), w=[b.hs_b])

                def s0b():
                    kb.dma("sp", b.cs[0:n, 0, :], self.cos_d[t0:t0 + n, :], w=[b.cs_b])
                    kb.dma("sp", b.cs[0:n, 1, :], self.sin_d[t0:t0 + n, :], wp=[b.cs_b])

                def s1():
                    self.norm_T(b.hs_b, b.hs[0:n, :], n, self.lnT[:, li, :], xnT[:, :, 0:n], b.xnT_b, nsc)

                def s2():
                    for c in range(8):
                        kb.op(kb.pe, lambda: T.matmul(latA[0:n, 0:384], xnT[:, c, 0:n], w_in[:, c, 0:384], start=(c == 0), stop=(c == 7)),
                              r=[b.xnT_b, wb], w=[latA_b] if c == 0 else [], wp=[latA_b] if c else [], inc=(c == 7))
                    for c in range(8):
                        kb.op(kb.pe, lambda: T.matmul(latB[0:n, 0:288], xnT[:, c, 0:n], w_in[:, c, 384:672], start=(c == 0), stop=(c == 7)),
                              r=[b.xnT_b, wb], w=[latB_b] if c == 0 else [], wp=[latB_b] if c else [], inc=(c == 7))
                    kb.op(kb.act, lambda: A.activation(out=sqj[0:n, 0:384], in_=latA[0:n, 0:384], func=AF.Square, accum_out=st[0:n, 0:1]),
                          r=[latA_b], w=[sqj_b, st_b])
                    kb.op(kb.act, lambda: A.activation(out=sqj[0:n, 512:768], in_=latB[0:n, 0:256], func=AF.Square, accum_out=st[0:n, 3:4]),
                          r=[latB_b], wp=[sqj_b, st_b])
                    self.rstd_ops(st, st_b, n, 0, 1.0 / 384)
                    self.rstd_ops(st, st_b, n, 3, 1.0 / 256)
                    kb.op(kb.dve, lambda: V.tensor_scalar(qln[0:n, :], latA[0:n, 0:384], st[0:n, 2:3], None, ALU.mult), r=[latA_b, st_b], w=[ln_b])
                    kb.op(kb.dve, lambda: V.tensor_scalar(kvln[0:n, :], latB[0:n, 0:256], st[0:n, 5:6], None, ALU.mult), r=[latB_b, st_b], wp=[ln_b])
                    kb.op(kb.act, lambda: A.copy(kpe[0:n, :], latB[0:n, 256:288]), r=[latB_b], wp=[ln_b])

                def s3():
                    for c in range(5):
                        srcap = qln[0:n, c * 128:(c + 1) * 128] if c < 3 else kvln[0:n, (c - 3) * 128:(c - 2) * 128]
                        kb.op(kb.pe, lambda: T.transpose(tp5[:, c, 0:n], srcap, self.ident[0:n, 0:n]),
                              r=[ln_b, self.cb_], w=[tp_b] if c == 0 else [], wp=[tp_b] if c else [], inc=(c == 4))
                    kb.op(kb.dve, lambda: V.tensor_tensor(qlT[:, :, 0:n], tp5[:, 0:3, 0:n], bc(qag[:, :], [128, 3, n], 2), ALU.mult),
                          r=[tp_b, par_b], w=[lT_b])
                    kb.op(kb.dve, lambda: V.tensor_tensor(kvlT[:, :, 0:n], tp5[:, 3:5, 0:n], bc(kvag[:, :], [128, 2, n], 2), ALU.mult),
                          r=[tp_b, par_b], wp=[lT_b])

                def s4():
                    for ct in range(3):
                        for kc in range(3):
                            kb.op(kb.pe, lambda: T.matmul(big[0:n, ct * 512:(ct + 1) * 512], qlT[:, kc, 0:n], w_qb[:, kc, ct * 512:(ct + 1) * 512],
                                                          start=(kc == 0), stop=(kc == 2)),
                                  r=[lT_b, wb], w=[big_b[ct]] if kc == 0 else [], wp=[big_b[ct]] if kc else [], inc=(kc == 2))
                    for ct in range(3):
                        E_ = kb.act if ct != 1 else kb.dve
                        fn = (lambda: A.copy(raw[0:n, ct * 512:(ct + 1) * 512], big[0:n, ct * 512:(ct + 1) * 512])) if ct != 1 else \
                             (lambda: V.tensor_copy(raw[0:n, ct * 512:(ct + 1) * 512], big[0:n, ct * 512:(ct + 1) * 512]))
                        kb.op(E_, fn, r=[big_b[ct]], w=[raw_b] if ct == 0 else [], wp=[raw_b] if ct else [])

                def s5():
                    kb.op(kb.dve, lambda: V.tensor_tensor(sqj[0:n, 0:1536], raw[0:n, 0:1536], raw[0:n, 0:1536], ALU.mult), r=[raw_b], w=[sqj_b])
                    kb.op(kb.dve, lambda: V.tensor_reduce(s16[0:n, 0, :], sq3, AX.X, ALU.add), r=[sqj_b], w=[s16_b])
                    kb.op(kb.act, lambda: A.activation(out=s16[0:n, 1, :], in_=s16[0:n, 0, :], func=AF.Ln, bias=EPS, scale=1.0 / 96), r=[s16_b], wp=[s16_b])
                    kb.op(kb.act, lambda: A.activation(out=s16[0:n, 2, :], in_=s16[0:n, 1, :], func=AF.Exp, scale=-0.5), r=[s16_b], wp=[s16_b])
                    kb.op(kb.dve, lambda: V.tensor_tensor(raw3, raw3, bc(s16[0:n, 2, :], [n, 16, 96], 2), ALU.mult), r=[s16_b], w=[raw_b])
                    kb.op(kb.dve, lambda: V.tensor_tensor(raw3, raw3, bc(gq[0:n, :], [n, 16, 96], 1), ALU.mult), r=[par_b], w=[raw_b])
                    cosb = bc(b.cs[0:n, 0, :], [n, 16, 16], 1)
                    sinb = bc(b.cs[0:n, 1, :], [n, 16, 16], 1)
                    kb.op(kb.act, lambda: A.copy(qb3[:, :, 0:64], raw3[:, :, 0:64]), r=[raw_b], w=[qbf_b])
                    rope(raw3[:, :, 64:80], raw3[:, :, 80:96], cosb, sinb, qb3[:, :, 64:80], qb3[:, :, 80:96], True, qbf_b, [raw_b, b.cs_b])

                def s6():
                    head_T(0, qTv)

                def s7():
                    for ct in range(4):
                        for kc in range(2):
                            kb.op(kb.pe, lambda: T.matmul(big[0:n, ct * 512:(ct + 1) * 512], kvlT[:, kc, 0:n], w_kvb[:, kc, ct * 512:(ct + 1) * 512],
                                                          start=(kc == 0), stop=(kc == 1)),
                                  r=[lT_b, wb], w=[big_b[ct]] if kc == 0 else [], wp=[big_b[ct]] if kc else [], inc=(kc == 1))
                    for ct in range(4):
                        E_ = kb.act if ct % 2 == 0 else kb.dve
                        fn = (lambda: A.copy(raw[0:n, ct * 512:(ct + 1) * 512], big[0:n, ct * 512:(ct + 1) * 512])) if ct % 2 == 0 else \
                             (lambda: V.tensor_copy(raw[0:n, ct * 512:(ct + 1) * 512], big[0:n, ct * 512:(ct + 1) * 512]))
                        kb.op(E_, fn, r=[big_b[ct]], w=[raw_b] if ct == 0 else [], wp=[raw_b] if ct else [])

                def s8():
                    kb.op(kb.dve, lambda: V.tensor_tensor(sqk, kv4[:, :, 0:64], kv4[:, :, 0:64], ALU.mult), r=[raw_b], w=[sqj_b])
                    kb.op(kb.dve, lambda: V.tensor_reduce(s16[0:n, 0, :], sqk, AX.X, ALU.add), r=[sqj_b], w=[s16_b])
                    kb.op(kb.act, lambda: A.activation(out=kpg[0:n, 1, :], in_=kpe[0:n, :], func=AF.Square, accum_out=st[0:n, 6:7]),
                          r=[ln_b], w=[kpg_b], wp=[st_b])
                    kb.op(kb.dve, lambda: V.tensor_scalar(s16[0:n, 0, :], s16[0:n, 0, :], st[0:n, 6:7], None, ALU.add), r=[st_b, s16_b], wp=[s16_b])
                    kb.op(kb.act, lambda: A.activation(out=s16[0:n, 1, :], in_=s16[0:n, 0, :], func=AF.Ln, bias=EPS, scale=1.0 / 96), r=[s16_b], wp=[s16_b])
                    kb.op(kb.act, lambda: A.activation(out=s16[0:n, 2, :], in_=s16[0:n, 1, :], func=AF.Exp, scale=-0.5), r=[s16_b], wp=[s16_b])
                    kb.op(kb.act, lambda: A.copy(b.vst[0:n, :, 0:64], kv5[:, :, 0, 64:128]), r=[raw_b, b.vst_b], wp=[b.vst_b])
                    kb.op(kb.dve, lambda: V.tensor_copy(b.vst[0:n, :, 128:192], kv5[:, :, 1, 64:128]), r=[raw_b, b.vst_b], wp=[b.vst_b])
                    kb.dma("sp", self.va_d[t0:t0 + n, :, :], b.vst[0:n, :, :], r=[b.vst_b])
                    kb.op(kb.dve, lambda: V.tensor_tensor(kv4[:, :, 0:64], kv4[:, :, 0:64], bc(s16[0:n, 2, :], [n, 16, 64], 2), ALU.mult),
                          r=[s16_b], w=[raw_b])
                    kb.op(kb.dve, lambda: V.tensor_tensor(qb3[:, :, 0:64], kv4[:, :, 0:64], bc(gk[0:n, 0:64], [n, 16, 64], 1), ALU.mult),
                          r=[raw_b, par_b], w=[qbf_b])
                    kb.op(kb.dve, lambda: V.tensor_tensor(kpg[0:n, 0, :], kpe[0:n, :], gk[0:n, 64:96], ALU.mult), r=[ln_b, par_b], w=[kpg_b])
                    rope(kpg[0:n, 0, 0:16], kpg[0:n, 0, 16:32], b.cs[0:n, 0, :], b.cs[0:n, 1, :], kpg[0:n, 1, 0:16], kpg[0:n, 1, 16:32],
                         False, kpg_b, [kpg_b, b.cs_b])
                    kb.op(kb.dve, lambda: V.tensor_tensor(qb3[:, :, 64:96], bc(kpg[0:n, 1, :], [n, 16, 32], 1), bc(s16[0:n, 2, :], [n, 16, 32], 2), ALU.mult),
                          r=[kpg_b, s16_b], wp=[qbf_b])

                def s9():
                    head_T(1, kTv)

                return [s0, s0b, s1, s2, s3, s4, s5, s6, s7, s8, s9]

            chains = [chain(k) for k in range(len(TILES))]
            NTL = len(TILES)
            for k in (0, 1):
                chains[k][0]()
                chains[k][1]()
            for k in range(0, NTL, 2):
                pair = [chains[k]] + ([chains[k + 1]] if k + 1 < NTL else [])
                nxt = [chains[k2] for k2 in (k + 2, k + 3) if k2 < NTL]
                for i in range(2, 11):
                    for c_ in pair:
                        c_[i]()
                    if i == 2:
                        for c_ in nxt:
                            c_[0]()
                    if i == 9:
                        for c_ in nxt:
                            c_[1]()
            kb.barrier()
        with contextlib.ExitStack() as s2:
            qh = [self.S(s2, "b_q%d" % i, [96, NT], BF16) for i in range(2)]
            kh = [self.S(s2, "b_k%d" % i, [96, NT], BF16) for i in range(2)]
            qk_b = [Buf(), Buf()]
            va = [self.S(s2, "b_va%d" % i, [128, 33, 192], BF16) for i in range(2)]
            va_b = [Buf(), Buf()]
            NPS = 6
            pT = [self.S(s2, "b_pT%d" % i, [128, 512], BF16) for i in range(NPS)]
            pT_b = [Buf() for _ in range(NPS)]
            rden = [self.S(s2, "b_rd%d" % i, [128, 512], F32) for i in range(2)]
            rdsh = [self.S(s2, "b_rs%d" % i, [128, 512], F32) for i in range(2)]
            rd_b = [Buf(), Buf()]
            rs_b = [Buf(), Buf()]
            bnd = self.S(s2, "b_bnd", [128, 8], F32)
            bnd_b = Buf()
            ps = [self.P(s2, "b_ps%d" % i, [128, 512], F32) for i in range(NPS)]
            ps_b = [PB() for _ in range(NPS)]
            po = [self.P(s2, "b_po%d" % i, [128, 512], F32) for i in range(2)]
            po_b = [PB(), PB()]
            kb.op(kb.dve, lambda: V.tensor_reduce(bnd[:, 0:1], gq[:, :], AX.X, ALU.max), r=[par_b], w=[bnd_b])
            kb.op(kb.dve, lambda: V.tensor_reduce(bnd[:, 1:2], gq[:, :], AX.X, ALU.min), r=[par_b], wp=[bnd_b])
            kb.op(kb.dve, lambda: V.tensor_reduce(bnd[:, 2:3], gk[:, :], AX.X, ALU.max), r=[par_b], wp=[bnd_b])
            kb.op(kb.dve, lambda: V.tensor_reduce(bnd[:, 3:4], gk[:, :], AX.X, ALU.min), r=[par_b], wp=[bnd_b])
            kb.op(kb.dve, lambda: V.scalar_tensor_tensor(bnd[:, 4:5], bnd[:, 1:2], -1.0, bnd[:, 0:1], ALU.mult, ALU.max), r=[bnd_b], wp=[bnd_b])
            kb.op(kb.dve, lambda: V.scalar_tensor_tensor(bnd[:, 5:6], bnd[:, 3:4], -1.0, bnd[:, 2:3], ALU.mult, ALU.max), r=[bnd_b], wp=[bnd_b])
            kb.op(kb.dve, lambda: V.scalar_tensor_tensor(bnd[:, 6:7], bnd[:, 4:5], -float(np.sqrt(96.0)), bnd[:, 5:6], ALU.mult, ALU.mult),
                  r=[bnd_b], wp=[bnd_b])
            negB = bnd[:, 6:7]
            scale = float(96.0 ** -0.5)

            def load_head(h):
                b = h % 2
                kb.dma("sp", qh[b][:, :], self.qT_d[h], w=[qk_b[b]])
                kb.dma("sp", kh[b][:, :], self.kT_d[h], wp=[qk_b[b]])

            def load_pair(c):
                b = c % 2
                kb.dma("sp", va[b][0:16, 0, :], self.va_d[0:16, c, :], w=[va_b[b]])
                vv = self.va_d[16:NT, c, :].rearrange("(j p) w -> p j w", p=128)
                for jj in range(0, 32, 8):
                    kb.dma("sp", va[b][:, 1 + jj:9 + jj, :], vv[:, jj:jj + 8, :], wp=[va_b[b]])

            load_pair(0)
            load_head(0)
            items = []
            for h in range(MLA_H):
                for qi in range(9):
                    if qi == 0:
                        q0, nq = 0, 16
                        kts = [(0, 0, 16, 0, True)]
                    else:
                        q0, nq = 16 + 512 * (qi - 1), 512
                        kts = [(0, 0, 16, 0, False)] + [(kt, 16 + 128 * (kt - 1), 128, 0, False) for kt in range(1, 4 * (qi - 1) + 1)]
                        kts += [(4 * (qi - 1) + 1 + i, 16 + 128 * (4 * (qi - 1) + i), 128, 128 * i, True) for i in range(4)]
                    for idx, kt in enumerate(kts):
                        items.append((h, qi, q0, nq, idx, len(kts)) + kt)
            LA = 3
            NI = len(items)
            for i in range(NI + LA):
                if i < NI:
                    (h, qi, q0, nq, idx, nk_t, kt, k0, nk, qoff, diag) = items[i]
                    if qi == 0 and idx == 0 and h + 1 < MLA_H:
                        load_head(h + 1)
                        if h % 2 == 1:
                            load_pair(h // 2 + 1)
                    hb_ = h % 2
                    nqq = nq - qoff
                    p = i % NPS
                    kb.op(kb.pe, lambda: T.matmul(ps[p][0:nk, 0:nqq], kh[hb_][:, k0:k0 + nk], qh[hb_][:, q0 + qoff:q0 + nq], start=True, stop=True),
                          r=[qk_b[hb_]], w=[ps_b[p]])
                    kb.op(kb.act, lambda: A.activation(out=pT[p][0:nk, 0:nqq], in_=ps[p][0:nk, 0:nqq], func=AF.Exp, bias=negB[0:nk, :], scale=scale),
                          r=[ps_b[p], bnd_b], w=[pT_b[p]])
                    if diag:
                        kb.op(kb.dve, lambda: V.tensor_tensor(pT[p][0:nk, 0:nk], pT[p][0:nk, 0:nk], self.tri16[0:nk, 0:nk], ALU.mult),
                              r=[self.cb_], w=[pT_b[p]])
                ii = i - LA
                if ii >= 0:
                    (h, qi, q0, nq, idx, nk_t, kt, k0, nk, qoff, diag) = items[ii]
                    c, e = h // 2, h % 2
                    vb_ = c % 2
                    dlo, dhi = (0, 64) if e == 0 else (64, 128)
                    nlo, nhi = (64, 128) if e == 0 else (0, 64)
                    nqq = nq - qoff
                    p = ii % NPS
                    pp = (h * 9 + qi) % 2
                    first, last = idx == 0, idx == nk_t - 1
                    kb.op(kb.pe, lambda: T.matmul(po[pp][:, qoff:nq], va[vb_][0:nk, kt, e * 64:e * 64 + 128], pT[p][0:nk, 0:nqq], start=first, stop=last),
                          r=[pT_b[p], va_b[vb_]], w=[po_b[pp]] if first else [], wp=[] if first else [po_b[pp]], inc=last)
                    if last:
                        kb.op(kb.dve, lambda: V.reciprocal(rden[pp][nlo:nhi, 0:nq], po[pp][nlo:nhi, 0:nq]), r=[po_b[pp]], w=[rd_b[pp]])
                        kb.op(kb.dve, lambda: V.tensor_copy(rdsh[pp][dlo:dhi, 0:nq], rden[pp][nlo:nhi, 0:nq]), r=[rd_b[pp]], w=[rs_b[pp]])
                        kb.op(kb.dve, lambda: V.tensor_tensor(oT[dlo:dhi, c, q0:q0 + nq], po[pp][dlo:dhi, 0:nq], rdsh[pp][dlo:dhi, 0:nq], ALU.mult),
                              r=[po_b[pp], rs_b[pp]], wp=[oT_b])
            kb.barrier()
        with contextlib.ExitStack() as s3:
            hs = [self.S(s3, "c_h%d" % i, [128, D], F32) for i in range(3)]
            hs_b = [Buf() for _ in range(3)]
            po = [self.P(s3, "c_po%d" % i, [128, 512], F32) for i in range(4)]
            po_b = [PB() for _ in range(4)]
            ip = 0
            for ti, (t0, n) in enumerate(TILES):
                s = ti % 3
                kb.dma("sp", hs[s][0:n, :], self.h_src(src, t0, n), w=[hs_b[s]])
                for hh in range(2):
                    p = ip % 4
                    ip += 1
                    for c in range(8):
                        kb.op(kb.pe, lambda: T.matmul(po[p][0:n, :], oT[:, c, t0:t0 + n], w_o[:, c, hh * 512:(hh + 1) * 512], start=(c == 0), stop=(c == 7)),
                              r=[oT_b, w_o_b], w=[po_b[p]] if c == 0 else [], wp=[po_b[p]] if c else [], inc=(c == 7))
                    kb.op(kb.dve, lambda: V.tensor_tensor(hs[s][0:n, hh * 512:(hh + 1) * 512], po[p][0:n, :], hs[s][0:n, hh * 512:(hh + 1) * 512], ALU.add),
                          r=[po_b[p], hs_b[s]], wp=[hs_b[s]])
                dap = self.h_dst(dst, t0, n)
                if dap is not None:
                    kb.dma("sp", dap, hs[s][0:n, :], r=[hs_b[s]])


def host_inputs(inp):
    f = lambda a: np.ascontiguousarray(np.asarray(a, dtype=np.float32))
    rep = lambda a: np.ascontiguousarray(np.broadcast_to(np.asarray(a, np.float32)[:, None, :], (a.shape[0], 128, a.shape[1])))
    colT = lambda a, c: np.ascontiguousarray(np.asarray(a, np.float32).reshape(a.shape[0], c, 128).transpose(0, 2, 1))
    ln = np.concatenate([np.asarray(inp["ln_mix"], np.float32), np.asarray(inp["ln_mlp"], np.float32)], 0)
    lnT = np.ascontiguousarray(ln.reshape(8, 8, 128).transpose(2, 0, 1))
    cw = np.asarray(inp["ssd_conv_w"], np.float32)
    cwT = np.ascontiguousarray(cw.reshape(2, 4, 32, 128).transpose(0, 3, 2, 1))
    k = np.arange(128)
    tri = (k[:, None] <= k[None, :]).astype(np.float32)
    inv = 1.0 / (10000.0 ** (np.arange(0, 32, 2, dtype=np.float32) / 32.0))
    ang = np.arange(NT, dtype=np.float32)[:, None] * inv[None, :].astype(np.float32)
    common = {
        "meta": f(inp["meta_tokens"]),
        "ssd_w_in": f(inp["ssd_w_in"]), "ssd_w_out": f(inp["ssd_w_out"]),
        "mla_w_in": f(inp["mla_w_in"]), "mla_w_q_b": f(inp["mla_w_q_b"]), "mla_w_kv_b": f(inp["mla_w_kv_b"]),
        "mla_w_out": f(inp["mla_w_out"]), "mlp_w_up": f(inp["mlp_w_up"]), "mlp_w_down": f(inp["mlp_w_down"]),
        "lnT": lnT, "cw": cwT, "cb": colT(inp["ssd_conv_b"], 32),
        "dtb_rep": rep(inp["ssd_dt_bias"]), "alog_rep": rep(inp["ssd_a_log"]), "dskip_rep": rep(inp["ssd_d"]),
        "ssd_normT": colT(inp["ssd_norm"], 16), "q_a_T": colT(inp["mla_q_a_norm"], 3), "kv_a_T": colT(inp["mla_kv_a_norm"], 2),
        "gq_rep": rep(inp["mla_q_norm"]), "gk_rep": rep(inp["mla_k_norm"]),
        "ident": np.eye(128, dtype=np.float32), "tri": tri, "ltri": np.ascontiguousarray(1.0 - tri),
        "ones": np.ones((128, 128), np.float32),
        "cos": np.cos(ang).astype(np.float32), "sin": np.sin(ang).astype(np.float32),
    }
    return common


FULL_PHASES = [
    ("ssd", 0, 0, "x", "h"), ("mlp", 0, "h", "h"),
    ("mla", 0, 1, "h", "h"), ("mlp", 1, "h", "h"),
    ("ssd", 1, 2, "h", "h"), ("mlp", 2, "h", "h"),
    ("mla", 1, 3, "h", "h"), ("mlp", 3, "h", "y"),
]


def run(inputs, phases, cores=8):
    common = host_inputs(inputs)
    x = np.asarray(inputs["x"], np.float32)
    prog = Prog(phases)
    in_maps = []
    for c in range(cores):
        m = dict(common)
        m["x"] = np.ascontiguousarray(x[c])
        in_maps.append(m)
    res = run_bass_kernel_spmd(prog.nc, in_maps, core_ids=list(range(cores)))
    return np.stack([np.asarray(r["y"]) for r in res.results], 0)


def kernel(**inputs):
    return run(inputs, FULL_PHASES, 8).astype(np.float32)
```

```python
import contextlib
import numpy as np
import concourse.bass as bass
import concourse.mybir as mybir
from concourse.bass_utils import run_bass_kernel_spmd

F32, BF16 = mybir.dt.float32, mybir.dt.bfloat16
AF = mybir.ActivationFunctionType
ALU = mybir.AluOpType
AX = mybir.AxisListType

NT, NM, D, SEQ = 4112, 16, 1024, 4096
TILES = [(0, 16)] + [(16 + 128 * j, 128) for j in range(32)]
EPS = 1e-6
DFF = 4096
SSD_IN = 6176
NH_S = 32
SSD_LEAD = 0.8
MLA_H = 16
QK = 96


class Buf:
    __slots__ = ("w", "r", "name", "ps")

    def __init__(self, name="", ps=False):
        self.w = {}
        self.r = {}
        self.name = name
        self.ps = ps


def PB():
    return Buf(ps=True)


class Eng:
    def __init__(self, name, e, sem):
        self.name, self.e, self.sem, self.cnt, self.seen = name, e, sem, 0, {}


class KB:
    def __init__(self, nc, es):
        self.nc = nc
        mk = lambda n: es.enter_context(nc.semaphore(n))
        self.pe = Eng("pe", nc.tensor, mk("s_pe"))
        self.act = Eng("act", nc.scalar, mk("s_act"))
        self.dve = Eng("dve", nc.vector, mk("s_dve"))
        self.pool = Eng("pool", nc.gpsimd, mk("s_pool"))
        self.sp = Eng("sp", nc.sync, mk("s_sp"))
        self.engs = [self.pe, self.act, self.dve, self.pool, self.sp]
        self.dsem = {"sp": [[mk("d_sp%d" % i), 0] for i in range(24)],
                     "pool": [[mk("d_pl%d" % i), 0] for i in range(8)]}
        self.drr = {"sp": 0, "pool": 0}
        self.nins = 0

    def _wait(self, E, toks):
        for key, (sem, val) in toks.items():
            if E is self.pe and key == "pe":
                continue
            if E.seen.get(key, 0) >= val:
                continue
            E.e.wait_ge(sem, val)
            E.seen[key] = val

    @staticmethod
    def _add(need, d):
        for k, sv in d.items():
            if k not in need or need[k][1] < sv[1]:
                need[k] = sv

    def _deps(self, r, w, wp, ekey=None):
        need = {}
        for b in r:
            self._add(need, b.w)
            if b.ps:
                self._add(need, {k: v for k, v in b.r.items() if k != ekey})
        for b in w:
            self._add(need, b.w)
            self._add(need, b.r)
        for b in wp:
            self._add(need, b.r)
            if b.ps:
                self._add(need, b.w)
        return need

    def _reg(self, key, tok, r, w, wp):
        for b in r:
            if key not in b.r or b.r[key][1] < tok[1]:
                b.r[key] = tok
        for b in w:
            b.w = {key: tok}
            b.r = {}
        for b in wp:
            if key not in b.w or b.w[key][1] < tok[1]:
                b.w[key] = tok

    def op(self, E, fn, r=(), w=(), wp=(), inc=True):
        self._wait(E, self._deps(r, w, wp, E.name))
        ins = fn()
        self.nins += 1
        if inc:
            E.cnt += 1
            ins.then_inc(E.sem, 1)
            tok = (E.sem, E.cnt)
        else:
            tok = (E.sem, E.cnt + 1)
        self._reg(E.name, tok, r, w, wp)

    def dma(self, Q, out, in_, r=(), w=(), wp=()):
        E = self.sp if Q == "sp" else self.pool
        self._wait(E, self._deps(r, w, wp))
        lst = self.dsem[Q]
        i = self.drr[Q]
        self.drr[Q] = (i + 1) % len(lst)
        sem, cnt = lst[i]
        key = (Q, i)
        if cnt > 0 and E.seen.get(key, 0) < cnt:
            E.e.wait_ge(sem, cnt)
            E.seen[key] = cnt
        ins = E.e.dma_start(out=out, in_=in_)
        ins.then_inc(sem, 16)
        self.nins += 1
        lst[i][1] = cnt + 16
        self._reg(key, (sem, cnt + 16), r, w, wp)

    def barrier(self):
        toks = {}
        for E in self.engs:
            if E.cnt > 0:
                toks[E.name] = (E.sem, E.cnt)
        for Q, lst in self.dsem.items():
            for i, (sem, cnt) in enumerate(lst):
                if cnt > 0:
                    toks[(Q, i)] = (sem, cnt)
        for E in self.engs:
            self._wait(E, toks)


def bc(ap, shape, axis):
    return ap.unsqueeze(axis).to_broadcast(list(shape))


class Prog:
    def __init__(self, phases):
        self.phases = phases
        nc = bass.Bass("TRN2", target_bir_lowering=False)
        self.nc = nc
        di = lambda name, shape: nc.dram_tensor(name, list(shape), F32, kind="ExternalInput").ap()
        self.x = di("x", [SEQ, D])
        self.meta = di("meta", [NM, D])
        self.w_ssd_in = di("ssd_w_in", [2, D, SSD_IN])
        self.w_ssd_out = di("ssd_w_out", [2, 2048, D])
        self.w_mla_in = di("mla_w_in", [2, D, 672])
        self.w_mla_qb = di("mla_w_q_b", [2, 384, 1536])
        self.w_mla_kvb = di("mla_w_kv_b", [2, 256, 2048])
        self.w_mla_out = di("mla_w_out", [2, D, D])
        self.w_up = di("mlp_w_up", [4, D, DFF])
        self.w_dn = di("mlp_w_down", [4, DFF, D])
        self.lnT_d = di("lnT", [128, 8, 8])
        self.cw_d = di("cw", [2, 128, 32, 4])
        self.cb_d = di("cb", [2, 128, 32])
        self.dtb_d = di("dtb_rep", [2, 128, 32])
        self.alog_d = di("alog_rep", [2, 128, 32])
        self.dsk_d = di("dskip_rep", [2, 128, 32])
        self.sng_d = di("ssd_normT", [2, 128, 16])
        self.qag_d = di("q_a_T", [2, 128, 3])
        self.kvag_d = di("kv_a_T", [2, 128, 2])
        self.gq_d = di("gq_rep", [2, 128, 96])
        self.gk_d = di("gk_rep", [2, 128, 96])
        self.ident_d = di("ident", [128, 128])
        self.tri_d = di("tri", [128, 128])
        self.ltri_d = di("ltri", [128, 128])
        self.ones_d = di("ones", [128, 128])
        self.cos_d = di("cos", [NT, 16])
        self.sin_d = di("sin", [NT, 16])
        self.y = nc.dram_tensor("y", [SEQ, D], F32, kind="ExternalOutput").ap()
        self.hd = nc.dram_tensor("hd", [NT, D], F32, kind="Internal").ap()
        self.qT_d = nc.dram_tensor("qT_d", [MLA_H, QK, NT], BF16, kind="Internal").ap()
        self.kT_d = nc.dram_tensor("kT_d", [MLA_H, QK, NT], BF16, kind="Internal").ap()
        self.va_d = nc.dram_tensor("va_d", [NT, 8, 192], BF16, kind="Internal").ap()

        with contextlib.ExitStack() as es:
            self.kb = KB(nc, es)
            self.build(es)

    def S(self, es, name, shape, dt):
        self.uid = getattr(self, "uid", 0) + 1
        return es.enter_context(self.nc.sbuf_tensor("sb%d_%s" % (self.uid, name), list(shape), dt))

    def P(self, es, name, shape, dt):
        self.uid = getattr(self, "uid", 0) + 1
        return es.enter_context(self.nc.psum_tensor("ps%d_%s" % (self.uid, name), list(shape), dt))

    def h_src(self, kind, t0, n):
        if kind == "x":
            return self.meta[0:16, :] if t0 == 0 else self.x[t0 - 16:t0 - 16 + n, :]
        return self.hd[t0:t0 + n, :]

    def h_dst(self, kind, t0, n):
        if kind == "y":
            return None if t0 == 0 else self.y[t0 - 16:t0 - 16 + n, :]
        return self.hd[t0:t0 + n, :]

    def build(self, es):
        kb, nc = self.kb, self.nc
        self.ident = self.S(es, "ident", [128, 128], BF16)
        self.tri32 = self.S(es, "tri32", [128, 128], F32)
        self.ltri32 = self.S(es, "ltri32", [128, 128], F32)
        self.ones32 = self.S(es, "ones32", [128, 128], F32)
        self.lnT = self.S(es, "lnT", [128, 8, 8], F32)
        self.cb_ = Buf("consts")
        kb.dma("pool", self.ident[:], self.ident_d, wp=[self.cb_])
        kb.dma("sp", self.tri32[:], self.tri_d, wp=[self.cb_])
        kb.dma("sp", self.ltri32[:], self.ltri_d, wp=[self.cb_])
        kb.dma("sp", self.ones32[:], self.ones_d, wp=[self.cb_])
        kb.dma("sp", self.lnT[:], self.lnT_d, wp=[self.cb_])
        for ph in self.phases:
            kind = ph[0]
            with contextlib.ExitStack() as pes:
                if kind == "mlp":
                    self.phase_mlp(pes, *ph[1:])
                elif kind == "ssd":
                    self.phase_ssd(pes, *ph[1:])
                elif kind == "mla":
                    self.phase_mla(pes, *ph[1:])
                kb.barrier()
        kb.barrier()

    def rstd_newton(self, st, st_b, n, inv_n, eps):
        kb, nc = self.kb, self.nc
        V = nc.vector
        I32 = mybir.dt.int32
        for f in self.rstd_newton_ops(st, st_b, n, inv_n, eps):
            f()

    def rstd_newton_ops(self, st, st_b, n, inv_n, eps):
        kb, nc = self.kb, self.nc
        V = nc.vector
        I32 = mybir.dt.int32
        x, g, t = st[0:n, 1:2], st[0:n, 2:3], st[0:n, 3:4]
        ops = []
        ops.append(lambda: kb.op(kb.dve, lambda: V.tensor_scalar(x, st[0:n, 0:1], inv_n, eps, ALU.mult, ALU.add), r=[st_b], wp=[st_b]))
        ops.append(lambda: kb.op(kb.dve, lambda: V.tensor_scalar(g.bitcast(I32), x.bitcast(I32), 1, None, ALU.arith_shift_right), r=[st_b], wp=[st_b]))
        ops.append(lambda: kb.op(kb.dve, lambda: V.tensor_scalar(g.bitcast(I32), g.bitcast(I32), -1, 0x5f3759df, ALU.mult, ALU.add), r=[st_b], wp=[st_b]))
        for _ in range(2):
            ops.append(lambda: kb.op(kb.dve, lambda: V.scalar_tensor_tensor(t, g, x, g, ALU.mult, ALU.mult), r=[st_b], wp=[st_b]))
            ops.append(lambda: kb.op(kb.dve, lambda: V.tensor_scalar(t, t, -0.5, 1.5, ALU.mult, ALU.add), r=[st_b], wp=[st_b]))
            ops.append(lambda: kb.op(kb.dve, lambda: V.tensor_tensor(g, g, t, ALU.mult), r=[st_b], wp=[st_b]))
        return ops

    def norm_T(self, hb, h_ap, n, gain_ap, dst_ap, dst_b, sc, newton=False, part=None):
        kb, nc = self.kb, self.nc
        junk, junk_b, ss, ss_b, xn, xn_b, tp, tp_b = sc
        if part == "B":
            return self._norm_T_b(n, gain_ap, dst_ap, dst_b, sc)
        kb.op(kb.act, lambda: nc.scalar.activation(out=junk[0:n, :], in_=h_ap, func=AF.Square, accum_out=ss[0:n, 0:1]),
              r=[hb], w=[junk_b, ss_b] if junk_b is not xn_b else [xn_b, ss_b])
        if newton:
            self.rstd_newton(ss, ss_b, n, 1.0 / D, EPS)
        else:
            kb.op(kb.act, lambda: nc.scalar.activation(out=ss[0:n, 1:2], in_=ss[0:n, 0:1], func=AF.Ln, bias=EPS, scale=1.0 / D),
                  r=[ss_b], wp=[ss_b])
            kb.op(kb.act, lambda: nc.scalar.activation(out=ss[0:n, 2:3], in_=ss[0:n, 1:2], func=AF.Exp, scale=-0.5),
                  r=[ss_b], wp=[ss_b])
        kb.op(kb.dve, lambda: nc.vector.tensor_scalar(xn[0:n, :], h_ap, ss[0:n, 2:3], None, ALU.mult),
              r=[hb, ss_b], w=[xn_b])
        if part == "A":
            return
        self._norm_T_b(n, gain_ap, dst_ap, dst_b, sc)

    def _norm_T_b(self, n, gain_ap, dst_ap, dst_b, sc):
        kb, nc = self.kb, self.nc
        junk, junk_b, ss, ss_b, xn, xn_b, tp, tp_b = sc
        for c in range(8):
            kb.op(kb.pe, lambda c=c: nc.tensor.transpose(tp[:, c, 0:n], xn[0:n, c * 128:(c + 1) * 128], self.ident[0:n, 0:n]),
                  r=[xn_b, self.cb_], w=[tp_b] if c == 0 else [], wp=[tp_b] if c else [], inc=(c == 7))
        kb.op(kb.dve, lambda: nc.vector.tensor_tensor(dst_ap, tp[:, :, 0:n], bc(gain_ap, [128, 8, n], 2), ALU.mult),
              r=[tp_b, self.cb_], w=[dst_b])

    def norm_scratch(self, es, pfx, tp_ps):
        ss = self.S(es, pfx + "ss", [128, 4], F32)
        xn = self.S(es, pfx + "xn", [128, 1024], BF16)
        tp = tp_ps[:].bitcast(BF16)[:, 0:1024].rearrange("p (c t) -> p c t", c=8)
        xn_b = Buf()
        return (xn, xn_b, ss, Buf(), xn, xn_b, tp, PB())

    def phase_mlp(self, es, li, src, dst):
        kb, nc = self.kb, self.nc
        wup = self.S(es, "wup", [128, 8, DFF], BF16)
        wdn = self.S(es, "wdn", [128, 32, D], BF16)
        wup_b = [Buf() for _ in range(8)]
        wdn_b = [Buf() for _ in range(8)]
        upv = self.w_up[li].rearrange("(c p) f -> p c f", p=128)
        dnv = self.w_dn[li].rearrange("(c p) d -> p c d", p=128)
        for i in range(8):
            kb.dma("pool", wup[:, :, i * 512:(i + 1) * 512], upv[:, :, i * 512:(i + 1) * 512], w=[wup_b[i]])
        for i in range(8):
            kb.dma("pool", wdn[:, 4 * i:4 * i + 4, :], dnv[:, 4 * i:4 * i + 4, :], w=[wdn_b[i]])
        NSLOT = 7
        hs = [self.S(es, "mh%d" % i, [128, D], F32) for i in range(NSLOT)]
        hs_b = [Buf() for _ in range(NSLOT)]
        xnT = self.S(es, "m_xnT", [128, 8, 512], BF16)
        xnT_b = Buf()
        uT = self.S(es, "m_uT", [128, 32, 512], BF16)
        uT_b = [Buf() for _ in range(32)]
        r32 = [self.S(es, "m_r32_%d" % i, [128, 512], F32) for i in range(2)]
        r32_b = [Buf(), Buf()]
        tp_ps = [self.P(es, "m_tp%d" % i, [128, 512], F32) for i in range(2)]
        pu = [self.P(es, "m_pu%d" % i, [128, 512], F32) for i in range(3)]
        pu_b = [PB() for _ in range(3)]
        pd = [self.P(es, "m_pd%d" % i, [128, 512], F32) for i in range(3)]
        pd_b = [PB() for _ in range(3)]
        nsc = [self.norm_scratch(es, "m%d" % i, tp_ps[i]) for i in range(2)]
        gain = self.lnT[:, 4 + li, :]
        groups = [[TILES[0]]] + [TILES[1 + 4 * g:5 + 4 * g] for g in range(8)]
        NG = len(groups)
        slots_all = []
        sl_ = 0
        for grp in groups:
            slots_all.append([(sl_ + i) % NSLOT for i in range(len(grp))])
            sl_ += len(grp)
        loaded = set()

        def load_tile(gi, k):
            if (gi, k) in loaded:
                return
            loaded.add((gi, k))
            (t0, n), s = groups[gi][k], slots_all[gi][k]
            kb.dma("sp", hs[s][0:n, :], self.h_src(src, t0, n), w=[hs_b[s]])

        def norm_part(gi, k, part):
            (t0, n), s = groups[gi][k], slots_all[gi][k]
            off = sum(nn for _, nn in groups[gi][:k])
            self.norm_T(hs_b[s], hs[s][0:n, :], n, gain, xnT[:, :, off:off + n], xnT_b, nsc[k % 2], part=part)

        for k in range(len(groups[0])):
            load_tile(0, k)
            norm_part(0, k, None)
        iu = 0
        ipd = 0
        for gi, grp in enumerate(groups):
            ntok = sum(n for _, n in grp)
            myslots = slots_all[gi]
            nxt = gi + 1 if gi + 1 < NG else None
            if nxt is not None:
                for k in range(min(len(groups[nxt]), NSLOT - len(grp))):
                    load_tile(nxt, k)
            for fc in range(32):
                p = iu % 3
                for c in range(8):
                    kb.op(kb.pe, lambda c=c, fc=fc, p=p: nc.tensor.matmul(pu[p][:, 0:ntok], wup[:, c, fc * 128:(fc + 1) * 128],
                                                                      xnT[:, c, 0:ntok], start=(c == 0), stop=(c == 7)),
                          r=[wup_b[fc // 4], xnT_b], w=[pu_b[p]] if c == 0 else [], wp=[pu_b[p]] if c else [], inc=(c == 7))
                rr = iu % 2
                kb.op(kb.act, lambda p=p, rr=rr: nc.scalar.activation(out=r32[rr][:, 0:ntok], in_=pu[p][:, 0:ntok], func=AF.Relu),
                      r=[pu_b[p]], w=[r32_b[rr]])
                E = kb.dve if (fc % 2 == 0) else kb.pool
                kb.op(E, lambda rr=rr, fc=fc, E=E: E.e.tensor_tensor(uT[:, fc, 0:ntok], r32[rr][:, 0:ntok], r32[rr][:, 0:ntok], ALU.mult),
                      r=[r32_b[rr]], w=[uT_b[fc]])
                iu += 1
            off = 0
            for kk, ((t0, n), s) in enumerate(zip(grp, myslots)):
                if nxt is not None and kk < len(groups[nxt]):
                    load_tile(nxt, kk)
                    norm_part(nxt, kk, "A")
                for hh in range(2):
                    p = ipd % 3
                    ipd += 1
                    for fc in range(32):
                        kb.op(kb.pe, lambda fc=fc, p=p, off=off, n=n, hh=hh: nc.tensor.matmul(
                            pd[p][0:n, :], uT[:, fc, off:off + n], wdn[:, fc, hh * 512:(hh + 1) * 512],
                            start=(fc == 0), stop=(fc == 31)),
                            r=[uT_b[fc], wdn_b[fc // 4]], w=[pd_b[p]] if fc == 0 else [], wp=[pd_b[p]] if fc else [], inc=(fc == 31))
                    kb.op(kb.dve, lambda p=p, n=n, hh=hh, s=s: nc.vector.tensor_tensor(
                        hs[s][0:n, hh * 512:(hh + 1) * 512], pd[p][0:n, :], hs[s][0:n, hh * 512:(hh + 1) * 512], ALU.add),
                        r=[pd_b[p], hs_b[s]], wp=[hs_b[s]])
                dst_ap = self.h_dst(dst, t0, n)
                if dst_ap is not None:
                    kb.dma("sp", dst_ap, hs[s][0:n, :], r=[hs_b[s]])
                off += n
                if nxt is not None and kk < len(groups[nxt]):
                    norm_part(nxt, kk, "B")
            if nxt is not None:
                for kk in range(len(grp), len(groups[nxt])):
                    load_tile(nxt, kk)
                    norm_part(nxt, kk, None)

    def phase_ssd(self, es, j, li, src, dst):
        from functools import partial
        kb, nc = self.kb, self.nc
        V, A, G, T = nc.vector, nc.scalar, nc.gpsimd, nc.tensor
        w_in = self.S(es, "s_win", [128, 8, SSD_IN], BF16)
        w_out = self.S(es, "s_wout", [128, 16, D], BF16)
        par_b = Buf()
        wdt_b = Buf()
        wx_b = [Buf() for _ in range(8)]
        wz_b = [Buf() for _ in range(4)]
        wout_b = [Buf() for _ in range(4)]
        wv = self.w_ssd_in[j].rearrange("(c p) f -> p c f", p=128)
        kb.dma("pool", w_in[:, :, 6144:6176], wv[:, :, 6144:6176], w=[wdt_b])
        for i in range(8):
            kb.dma("pool", w_in[:, :, 2048 + 512 * i:2560 + 512 * i], wv[:, :, 2048 + 512 * i:2560 + 512 * i], w=[wx_b[i]])
        for i in range(4):
            kb.dma("pool", w_in[:, :, 512 * i:512 * (i + 1)], wv[:, :, 512 * i:512 * (i + 1)], w=[wz_b[i]])
        wov = self.w_ssd_out[j].rearrange("(c p) f -> p c f", p=128)
        for i in range(4):
            kb.dma("pool", w_out[:, 4 * i:4 * i + 4, :], wov[:, 4 * i:4 * i + 4, :], w=[wout_b[i]])
        cw = self.S(es, "s_cw", [128, 32, 4], F32)
        cb = self.S(es, "s_cb", [128, 32], F32)
        dtb = self.S(es, "s_dtb", [128, 32], F32)
        arep = self.S(es, "s_arep", [128, 32], F32)
        dsk = self.S(es, "s_dsk", [128, 32], F32)
        sng = self.S(es, "s_sng", [128, 16], F32)
        kb.dma("sp", cw[:], self.cw_d[j], wp=[par_b])
        kb.dma("sp", cb[:], self.cb_d[j], wp=[par_b])
        kb.dma("sp", dtb[:], self.dtb_d[j], wp=[par_b])
        kb.dma("sp", arep[:], self.alog_d[j], wp=[par_b])
        kb.dma("sp", dsk[:], self.dsk_d[j], wp=[par_b])
        kb.dma("sp", sng[:], self.sng_d[j], wp=[par_b])
        arep_b = Buf()
        kb.op(kb.act, lambda: A.activation(out=arep[:], in_=arep[:], func=AF.Exp), r=[par_b], w=[arep_b])
        kb.op(kb.dve, lambda: V.tensor_scalar(arep[:], arep[:], -1.0, None, ALU.mult), r=[arep_b], w=[arep_b])
        cwb = Buf()
        kb.op(kb.dve, lambda: V.tensor_scalar(cw[:], cw[:], 0.5, None, ALU.mult), r=[par_b], w=[cwb])
        kb.op(kb.dve, lambda: V.tensor_scalar(cb[:], cb[:], 0.5, None, ALU.mult), r=[par_b], wp=[cwb])
        hs = [self.S(es, "s_h%d" % i, [128, D], F32) for i in range(2)]
        hs_b = [Buf(), Buf()]
        tpF = self.P(es, "s_tpF", [128, 512], F32)
        pzs = [self.P(es, "s_pz%d" % i, [128, 512], F32) for i in range(2)]
        fm = self.P(es, "s_fm", [128, 512], F32)
        tpB = self.P(es, "s_tpB", [128, 512], F32)
        dTb = self.P(es, "s_dT", [128, 512], F32)
        bm = self.P(es, "s_bm", [128, 512], F32)
        yb = self.P(es, "s_y", [128, 512], F32)
        pz_b = [PB(), PB()]
        fm_b, tpB_b, dT_b, bm_b, y_b = PB(), PB(), PB(), PB(), PB()
        gT = self.S(es, "s_gT", [128, 16, 128], BF16)
        gT_b = Buf()
        _ss = self.S(es, "s_nss", [128, 4], F32)
        _xn = gT[:, 0:8, :].rearrange("p c t -> p (c t)")
        _tp = tpF[:].bitcast(BF16)[:, 0:1024].rearrange("p (c t) -> p c t", c=8)
        nsc = (_xn, gT_b, _ss, Buf(), _xn, gT_b, _tp, PB())
        tpB16 = tpB[:].bitcast(BF16)
        tpBg = tpB16.rearrange("p (c t) -> p c t", c=8)
        xnT = [self.S(es, "s_xnT%d" % i, [128, 8, 131], BF16) for i in range(2)]
        xnT_b = [Buf(), Buf()]
        for i in range(2):
            kb.op(kb.pool, lambda: G.memset(xnT[i][:], 0.0), w=[xnT_b[i]])
        xbc_xs = self.S(es, "s_xbcxs", [128, 16, 128], BF16)
        xsT_b = [Buf() for _ in range(16)]
        xbc_bc = [self.S(es, "s_xbcbc%d" % i, [128, 16, 128], BF16) for i in range(2)]
        bc_b = [[Buf() for _ in range(16)] for _ in range(2)]
        NU, NA = 4, 7
        u = [self.S(es, "s_u%d" % i, [128, 131], F32) for i in range(NU)]
        u_b = [Buf() for _ in range(NU)]
        acc = [self.S(es, "s_acc%d" % i, [128, 128], F32) for i in range(NA)]
        acc_b = [Buf() for _ in range(NA)]
        th = [self.S(es, "s_th%d" % i, [128, 128], F32) for i in range(2)]
        th_b = [Buf(), Buf()]
        xs_tm = self.S(es, "s_xstm", [128, 2048], BF16)
        xs_b = [Buf() for _ in range(8)]
        B_tm = self.S(es, "s_Btm", [128, 1024], BF16)
        Btm_b = Buf()
        smF = self.S(es, "s_smF", [128, 4, 32], F32)
        EXPT, ACS, TMP, DTE = range(4)
        smF_b = [Buf() for _ in range(4)]
        smB = [self.S(es, "s_smB%d" % i, [128, 5, 32], F32) for i in range(2)]
        DTV, ADT, DFS, CD, W2 = range(5)
        smB_b = [[Buf() for _ in range(5)] for _ in range(2)]
        rhsD = [self.S(es, "s_rhsD%d" % i, [128, 512], F32) for i in range(2)]
        rhsD_b = [Buf(), Buf()]
        Ee = [self.S(es, "s_E%d" % i, [128, 512], F32) for i in range(2)]
        E_b = [Buf(), Buf()]
        CBm = [self.S(es, "s_CBm0", [128, 128], F32)] * 2
        _cb = Buf()
        CBm_b = [_cb, _cb]
        MT = [self.S(es, "s_MT%d" % i, [128, 512], BF16) for i in range(2)]
        MT_b = [Buf(), Buf()]
        xdt = [self.S(es, "s_xdt%d" % i, [128, 256], BF16) for i in range(2)]
        xdt_b = [Buf(), Buf()]
        xdtd = [self.S(es, "s_xdtd%d" % i, [128, 256], BF16) for i in range(2)]
        xdtd_b = [Buf(), Buf()]
        tt = [[self.S(es, "s_t%d_%d" % (k, i), [128, 256], F32) for i in range(2)] for k in range(4)]
        tt_b = [[Buf(), Buf()] for _ in range(4)]
        tt.append(tt[0])
        tt_b.append(tt_b[0])
        stg = self.S(es, "s_stg", [128, 8, 4], F32)
        stg_b = [Buf() for _ in range(8)]
        S32 = self.S(es, "s_S32", [128, 2048], F32)
        Sbf = self.S(es, "s_Sbf", [128, 2048], BF16)
        S32_b = [Buf() for _ in range(8)]
        Sbf_b = [Buf() for _ in range(8)]
        stmp = [self.S(es, "s_stmp0", [128, 256], F32)] * 2
        _sb = Buf()
        stmp_b = [_sb, _sb]
        kb.op(kb.pool, lambda: G.memset(S32[:], 0.0), w=S32_b)
        kb.op(kb.pool, lambda: G.memset(Sbf[:], 0.0), w=Sbf_b)
        v3 = lambda ap: ap.rearrange("p (j l) -> p j l", j=4)
        NTL = len(TILES)

        def f_load(ti):
            t0, n = TILES[ti]
            cur, prev = ti % 2, (ti + 1) % 2
            nprev = TILES[ti - 1][1] if ti > 0 else 0
            X = xnT[cur]
            kb.dma("sp", hs[cur][0:n, :], self.h_src(src, t0, n), w=[hs_b[cur]])
            self.norm_T(hs_b[cur], hs[cur][0:n, :], n, self.lnT[:, li, :], X[:, :, 3:3 + n], xnT_b[cur], nsc, newton=True)
            kb.op(kb.pool, lambda: G.tensor_copy(X[:, :, 0:3], xnT[prev][:, :, nprev:nprev + 3]), r=[xnT_b[prev]], wp=[xnT_b[cur]])

        def f_dt1(ti):
            t0, n = TILES[ti]
            cur = ti % 2
            X = xnT[cur]
            sB, sBb = smB[cur], smB_b[cur]
            pdt = fm[0:n, 0:32]
            for c in range(8):
                kb.op(kb.pe, lambda: T.matmul(pdt, X[:, c, 3:3 + n], w_in[:, c, 6144:6176], start=(c == 0), stop=(c == 7)),
                      r=[wdt_b, xnT_b[cur]], w=[fm_b] if c == 0 else [], wp=[fm_b] if c else [], inc=(c == 7))
            kb.op(kb.dve, lambda: V.tensor_tensor(sB[0:n, DTV, :], pdt, dtb[0:n, :], ALU.add), r=[fm_b, par_b], w=[sBb[DTV]])
            kb.op(kb.act, lambda: A.activation(out=smF[0:n, EXPT, :], in_=sB[0:n, DTV, :], func=AF.Exp), r=[sBb[DTV]], w=[smF_b[EXPT]])
            kb.op(kb.act, lambda: A.activation(out=sB[0:n, DTV, :], in_=smF[0:n, EXPT, :], func=AF.Ln, bias=1.0), r=[smF_b[EXPT]], w=[sBb[DTV]])
            kb.op(kb.dve, lambda: V.tensor_tensor(sB[0:n, ADT, :], sB[0:n, DTV, :], arep[0:n, :], ALU.mult), r=[sBb[DTV], arep_b], w=[sBb[ADT]])

        def f_dt2(ti):
            t0, n = TILES[ti]
            cur = ti % 2
            sB, sBb = smB[cur], smB_b[cur]
            pacs, ptot = fm[0:n, 32:64], fm[:, 64:96]
            kb.op(kb.pe, lambda: T.matmul(pacs, self.tri32[0:n, 0:n], sB[0:n, ADT, :], start=True, stop=True), r=[sBb[ADT], self.cb_], w=[fm_b])
            kb.op(kb.pe, lambda: T.matmul(ptot, self.ones32[0:n, :], sB[0:n, ADT, :], start=True, stop=True), r=[sBb[ADT], self.cb_], wp=[fm_b])
            kb.op(kb.dve, lambda: V.tensor_copy(smF[0:n, ACS, :], pacs), r=[fm_b], w=[smF_b[ACS]])
            kb.op(kb.act, lambda: A.activation(out=sB[0:n, DFS, :], in_=pacs, func=AF.Exp), r=[fm_b], w=[sBb[DFS]])
            kb.op(kb.act, lambda: A.activation(out=sB[:, CD, :], in_=ptot, func=AF.Exp), r=[fm_b], w=[sBb[CD]])
            kb.op(kb.dve, lambda: V.tensor_tensor(smF[0:n, TMP, :], ptot[0:n, :], smF[0:n, ACS, :], ALU.subtract), r=[fm_b, smF_b[ACS]], w=[smF_b[TMP]])
            kb.op(kb.act, lambda: A.activation(out=smF[0:n, DTE, :], in_=smF[0:n, TMP, :], func=AF.Exp), r=[smF_b[TMP]], w=[smF_b[DTE]])
            kb.op(kb.dve, lambda: V.tensor_tensor(sB[0:n, W2, :], sB[0:n, DTV, :], smF[0:n, DTE, :], ALU.mult), r=[sBb[DTV], smF_b[DTE]], w=[sBb[W2]])

        def conv_dst(ti, cc, n):
            cur = ti % 2
            if cc < 16:
                return xbc_xs[:, cc, 0:n], xsT_b[cc]
            return xbc_bc[cur][:, cc - 16, 0:n], bc_b[cur][cc - 16]

        def f_conv(ti, s):
            t0, n = TILES[ti]
            cur = ti % 2
            X = xnT[cur]
            cc = s
            if 0 <= cc < 32:
                p = cc % 2
                pz = pzs[p][:, 0:3 + n]
                for c in range(8):
                    kb.op(kb.pe, lambda: T.matmul(pz, w_in[:, c, 2048 + cc * 128:2048 + (cc + 1) * 128], X[:, c, 0:3 + n], start=(c == 0), stop=(c == 7)),
                          r=[wx_b[cc // 4], xnT_b[cur]], w=[pz_b[p]] if c == 0 else [], wp=[pz_b[p]] if c else [], inc=(c == 7))
            cc = s - 1
            if 0 <= cc < 32:
                p = cc % 2
                pz = pzs[p][:, 0:3 + n]
                kb.op(kb.act, lambda: A.activation(out=acc[cc % NA][:, 0:n], in_=pz[:, 3:3 + n], func=AF.Identity, bias=cb[:, cc:cc + 1], scale=cw[:, cc, 3:4]),
                      r=[pz_b[p], cwb], w=[acc_b[cc % NA]])
                kb.op(kb.act, lambda: A.copy(u[cc % NU][:, 0:3 + n], pz), r=[pz_b[p]], w=[u_b[cc % NU]])
            for k in range(3):
                cc = s - 2 - k
                if 0 <= cc < 32:
                    p = cc % NU
                    a_ = acc[cc % NA][:, 0:n]
                    kb.op(kb.dve, lambda: V.scalar_tensor_tensor(a_, u[p][:, k:k + n], cw[:, cc, k:k + 1], a_, ALU.mult, ALU.add),
                          r=[u_b[p], cwb], w=[acc_b[cc % NA]])
            cc = s - 5
            if 0 <= cc < 32:
                kb.op(kb.act, lambda: A.activation(out=th[cc % 2][:, 0:n], in_=acc[cc % NA][:, 0:n], func=AF.Tanh), r=[acc_b[cc % NA]], w=[th_b[cc % 2]])
            cc = s - 6
            if 0 <= cc < 32:
                dst_ap, dst_b = conv_dst(ti, cc, n)
                kb.op(kb.dve, lambda: V.scalar_tensor_tensor(dst_ap, th[cc % 2][:, 0:n], 1.0, acc[cc % NA][:, 0:n], ALU.add, ALU.mult),
                      r=[th_b[cc % 2], acc_b[cc % NA]], w=[dst_b])

        def front(ti):
            st = [partial(f_conv, ti, 0), partial(f_dt1, ti), partial(f_conv, ti, 1), partial(f_dt2, ti)]
            st += [partial(f_conv, ti, s) for s in range(2, 38)]
            return st

        def b_trx(ti, half):
            t0, n = TILES[ti]
            for k in range(8):
                cc = half * 8 + k
                kb.op(kb.pe, lambda: T.transpose(tpB16[0:n, k * 128:(k + 1) * 128], xbc_xs[:, cc, 0:n], self.ident[:, :]),
                      r=[xsT_b[cc], self.cb_], w=[tpB_b] if k == 0 else [], wp=[tpB_b] if k else [], inc=(k == 7))
            kb.op(kb.act, lambda: A.copy(xs_tm[0:n, half * 1024:(half + 1) * 1024], tpB16[0:n, :]), r=[tpB_b], w=xs_b[4 * half:4 * half + 4])

        def b_trB(ti):
            t0, n = TILES[ti]
            cur = ti % 2
            for g in range(8):
                kb.op(kb.pe, lambda: T.transpose(tpB16[0:n, g * 128:(g + 1) * 128], xbc_bc[cur][:, g, 0:n], self.ident[:, :]),
                      r=[bc_b[cur][g], self.cb_], w=[tpB_b] if g == 0 else [], wp=[tpB_b] if g else [], inc=(g == 7))
            kb.op(kb.dve, lambda: V.tensor_copy(B_tm[0:n, :], tpB16[0:n, :]), r=[tpB_b], w=[Btm_b])

        def gchain(ti, g):
            t0, n = TILES[ti]
            cur = ti % 2
            X = xnT[cur]
            sB, sBb = smB[cur], smB_b[cur]
            b2 = g % 2
            hsl = slice(4 * g, 4 * g + 4)
            gsl = slice(g * 256, (g + 1) * 256)
            BT, CT = xbc_bc[cur][:, g, 0:n], xbc_bc[cur][:, 8 + g, 0:n]
            BTb, CTb = bc_b[cur][g], bc_b[cur][8 + g]
            x3 = lambda ap: ap.rearrange("p (j f) -> p j f", j=4)
            rD = rhsD[b2][0:n, 0:4 * n]
            Ev = Ee[b2][0:n, 0:4 * n]
            MTv = MT[b2][0:n, 0:4 * n]
            pcb = bm[0:n, 0:n]
            pst = bm[:, 256:512]
            zq = fm[0:n, 256:512]
            xs3 = x3(xs_tm[0:n, gsl])
            t1, t2, t3, szg, thg = [tt[k][b2][0:n, :] for k in range(5)]
            t1b, t2b, t3b, szb, thb = [tt_b[k][b2] for k in range(5)]
            steps = []

            def s1():
                kb.op(kb.pool, lambda: G.tensor_tensor(v3(rD), bc(sB[0:n, ADT, hsl], [n, 4, n], 2), bc(self.tri32[0:n, 0:n], [n, 4, n], 1), ALU.mult),
                      r=[sBb[ADT], self.cb_], w=[rhsD_b[b2]])
                kb.op(kb.pool, lambda: G.tensor_tensor(x3(xdt[b2][0:n, :]), xs3, bc(sB[0:n, DTV, hsl], [n, 4, 64], 2), ALU.mult),
                      r=[xs_b[g], sBb[DTV]], w=[xdt_b[b2]])
            steps.append(s1)

            def s2():
                kb.op(kb.pe, lambda: T.matmul(dTb[0:n, 0:4 * n], self.ltri32[0:n, 0:n], rD, start=True, stop=True), r=[rhsD_b[b2], self.cb_], w=[dT_b])
                kb.op(kb.act, lambda: A.activation(out=Ev, in_=dTb[0:n, 0:4 * n], func=AF.Exp), r=[dT_b], w=[E_b[b2]])
                kb.op(kb.pool, lambda: G.tensor_tensor(x3(xdtd[b2][0:n, :]), xs3, bc(sB[0:n, W2, hsl], [n, 4, 64], 2), ALU.mult),
                      r=[xs_b[g], sBb[W2]], w=[xdtd_b[b2]])
            steps.append(s2)

            def s3():
                kb.op(kb.pe, lambda: T.matmul(pcb, BT, CT, start=True, stop=True), r=[BTb, CTb], w=[bm_b])
                kb.op(kb.dve, lambda: V.tensor_tensor(CBm[b2][0:n, 0:n], pcb, self.tri32[0:n, 0:n], ALU.mult), r=[bm_b, self.cb_], w=[CBm_b[b2]])
                for c in range(8):
                    kb.op(kb.pe, lambda: T.matmul(zq, X[:, c, 3:3 + n], w_in[:, c, g * 256:(g + 1) * 256], start=(c == 0), stop=(c == 7)),
                          r=[wz_b[g // 2], xnT_b[cur]], w=[fm_b] if c == 0 else [], wp=[fm_b] if c else [], inc=(c == 7))
                kb.op(kb.act, lambda: A.activation(out=thg, in_=zq, func=AF.Tanh, scale=0.5), r=[fm_b], w=[thb])
                kb.op(kb.dve, lambda: V.scalar_tensor_tensor(szg, thg, 1.0, zq, ALU.add, ALU.mult), r=[thb, fm_b], w=[szb])
                kb.op(kb.pool, lambda: G.tensor_tensor(v3(MTv), v3(Ev), bc(CBm[b2][0:n, 0:n], [n, 4, n], 1), ALU.mult),
                      r=[E_b[b2], CBm_b[b2]], w=[MT_b[b2]])
            steps.append(s3)

            def s4():
                for jj in range(4):
                    kb.op(kb.pe, lambda: T.matmul(yb[0:n, jj * 64:(jj + 1) * 64], MTv[:, jj * n:(jj + 1) * n], xdt[b2][0:n, jj * 64:(jj + 1) * 64], start=True, stop=True),
                          r=[MT_b[b2], xdt_b[b2]], w=[y_b] if jj == 0 else [], wp=[y_b] if jj else [], inc=False)
                kb.op(kb.pe, lambda: T.matmul(yb[0:n, 256:512], CT, Sbf[:, gsl], start=True, stop=True), r=[CTb, Sbf_b[g]], wp=[y_b])
                kb.op(kb.pool, lambda: G.tensor_tensor(x3(t3), xs3, bc(dsk[0:n, hsl], [n, 4, 64], 2), ALU.mult), r=[xs_b[g], par_b], w=[t3b])
                kb.op(kb.dve, lambda: V.tensor_tensor(x3(t1), x3(yb[0:n, 256:512]), bc(sB[0:n, DFS, hsl], [n, 4, 64], 2), ALU.mult),
                      r=[y_b, sBb[DFS]], w=[t1b])
                kb.op(kb.dve, lambda: V.tensor_tensor(t2, yb[0:n, 0:256], t1, ALU.add), r=[y_b, t1b], w=[t2b])
            steps.append(s4)

            def s5():
                kb.op(kb.pe, lambda: T.matmul(pst, B_tm[0:n, g * 128:(g + 1) * 128], xdtd[b2][0:n, :], start=True, stop=True),
                      r=[Btm_b, xdtd_b[b2]], w=[bm_b])
                kb.op(kb.pool, lambda: G.tensor_tensor(x3(stmp[b2][:, :]), x3(S32[:, gsl]), bc(sB[:, CD, hsl], [128, 4, 64], 2), ALU.mult),
                      r=[S32_b[g], sBb[CD]], w=[stmp_b[b2]])
                kb.op(kb.dve, lambda: V.tensor_tensor(S32[:, gsl], stmp[b2][:, :], pst, ALU.add), r=[stmp_b[b2], bm_b], w=[S32_b[g]])
                kb.op(kb.act, lambda: A.copy(Sbf[:, gsl], S32[:, gsl]), r=[S32_b[g]], w=[Sbf_b[g]])
            steps.append(s5)

            def s6():
                kb.op(kb.pool, lambda: G.tensor_tensor(t2, t2, t3, ALU.add), r=[t3b], w=[t2b])
                kb.op(kb.pool, lambda: G.tensor_tensor(t2, t2, szg, ALU.mult), r=[szb], w=[t2b])
            steps.append(s6)

            sq = lambda: kb.op(kb.act, lambda: A.activation(out=t1, in_=t2, func=AF.Square, accum_out=stg[0:n, g, 0:1]), r=[t2b], w=[t1b, stg_b[g]])
            newt = self.rstd_newton_ops(stg[:, g, :], stg_b[g], n, 1.0 / 256, 4.0 * EPS)
            gnf = lambda: kb.op(kb.dve, lambda: V.tensor_scalar(xs_tm[0:n, gsl], t2, stg[0:n, g, 2:3], None, ALU.mult), r=[t2b, stg_b[g]], w=[xs_b[g]])
            return steps, (sq, newt, gnf)

        def b_gT(ti, half):
            t0, n = TILES[ti]
            for k in range(8):
                cc = half * 8 + k
                kb.op(kb.pe, lambda: T.transpose(tpBg[:, k, 0:n], xs_tm[0:n, cc * 128:(cc + 1) * 128], self.ident[0:n, 0:n]),
                      r=[xs_b[cc // 2], self.cb_], w=[tpB_b] if k == 0 else [], wp=[tpB_b] if k else [], inc=(k == 7))
            kb.op(kb.dve, lambda: V.tensor_tensor(gT[:, half * 8:(half + 1) * 8, 0:n], tpBg[:, :, 0:n], bc(sng[:, half * 8:(half + 1) * 8], [128, 8, n], 2), ALU.mult),
                  r=[tpB_b, par_b], w=[gT_b] if half == 0 else [], wp=[gT_b] if half else [])

        def b_out(ti, hh):
            t0, n = TILES[ti]
            cur = ti % 2
            po = dTb[0:n, :] if hh == 0 else yb[0:n, :]
            pbuf = dT_b if hh == 0 else y_b
            for cc in range(16):
                kb.op(kb.pe, lambda: T.matmul(po, gT[:, cc, 0:n], w_out[:, cc, hh * 512:(hh + 1) * 512], start=(cc == 0), stop=(cc == 15)),
                      r=[gT_b, wout_b[cc // 4]], w=[pbuf] if cc == 0 else [], wp=[pbuf] if cc else [], inc=(cc == 15))
            kb.op(kb.dve, lambda: V.tensor_tensor(hs[cur][0:n, hh * 512:(hh + 1) * 512], po, hs[cur][0:n, hh * 512:(hh + 1) * 512], ALU.add),
                  r=[pbuf, hs_b[cur]], wp=[hs_b[cur]])
            if hh == 1:
                dap = self.h_dst(dst, t0, n)
                if dap is not None:
                    kb.dma("sp", dap, hs[cur][0:n, :], r=[hs_b[cur]])

        def back(ti):
            st = [partial(b_trx, ti, 0), partial(b_trx, ti, 1), partial(b_trB, ti)]
            for gp in range(4):
                (ca, ta), (cb_, tb) = gchain(ti, 2 * gp), gchain(ti, 2 * gp + 1)
                for a_, b_ in zip(ca, cb_):
                    st += [a_, b_]

                def tail(ta=ta, tb=tb):
                    ta[0]()
                    tb[0]()
                    for fa, fb in zip(ta[1], tb[1]):
                        fa()
                        fb()
                    ta[2]()
                    tb[2]()
                st.append(tail)
            st += [partial(b_gT, ti, 0), partial(b_gT, ti, 1), partial(b_out, ti, 0), partial(b_out, ti, 1)]
            return st

        def interleave(a, b, lead):
            na, nb = len(a), len(b)
            i = jx = 0
            while i < na or jx < nb:
                if jx >= nb or (i < na and i * nb <= jx * na * lead):
                    a[i]()
                    i += 1
                else:
                    b[jx]()
                    jx += 1

        f_load(0)
        for ti in range(NTL + 1):
            f = front(ti) if ti < NTL else []
            b = back(ti - 1) if ti >= 1 else []
            if ti + 1 < NTL:
                b = b + [partial(f_load, ti + 1)]
            interleave(f, b, SSD_LEAD)

    def phase_ssd_v1(self, es, j, li, src, dst):
        kb, nc = self.kb, self.nc
        V, A, G, T = nc.vector, nc.scalar, nc.gpsimd, nc.tensor
        w_in = self.S(es, "s_win", [128, 8, SSD_IN], BF16)
        w_out = self.S(es, "s_wout", [128, 16, D], BF16)
        win_b, wout_b, par_b = Buf(), Buf(), Buf()
        wv = self.w_ssd_in[j].rearrange("(c p) f -> p c f", p=128)
        for c in range(8):
            kb.dma("pool", w_in[:, c, :], wv[:, c, :], wp=[win_b])
        wov = self.w_ssd_out[j].rearrange("(c p) f -> p c f", p=128)
        for c in range(0, 16, 4):
            kb.dma("pool", w_out[:, c:c + 4, :], wov[:, c:c + 4, :], wp=[wout_b])
        cw = self.S(es, "s_cw", [128, 32, 4], F32)
        cb = self.S(es, "s_cb", [128, 32], F32)
        dtb = self.S(es, "s_dtb", [128, 32], F32)
        arep = self.S(es, "s_arep", [128, 32], F32)
        dsk = self.S(es, "s_dsk", [128, 32], F32)
        sng = self.S(es, "s_sng", [128, 16], F32)
        kb.dma("sp", cw[:], self.cw_d[j], wp=[par_b])
        kb.dma("sp", cb[:], self.cb_d[j], wp=[par_b])
        kb.dma("sp", dtb[:], self.dtb_d[j], wp=[par_b])
        kb.dma("sp", arep[:], self.alog_d[j], wp=[par_b])
        kb.dma("sp", dsk[:], self.dsk_d[j], wp=[par_b])
        kb.dma("sp", sng[:], self.sng_d[j], wp=[par_b])
        arep_b = Buf()
        kb.op(kb.act, lambda: A.activation(out=arep[:], in_=arep[:], func=AF.Exp), r=[par_b], w=[arep_b])
        kb.op(kb.dve, lambda: V.tensor_scalar(arep[:], arep[:], -1.0, None, ALU.mult), r=[arep_b], w=[arep_b])
        hs = [self.S(es, "s_h%d" % i, [128, D], F32) for i in range(2)]
        hs_b = [Buf(), Buf()]
        tp2 = self.P(es, "s_tp2", [128, 1024], F32)
        pzbs = [self.P(es, "s_pz%d" % i, [128, 512], F32) for i in range(2)]
        zqb = self.P(es, "s_zq", [128, 512], F32)
        dTb = self.P(es, "s_dT", [128, 512], F32)
        miscb = self.P(es, "s_misc", [128, 512], F32)
        yb = self.P(es, "s_y", [128, 512], F32)
        pz_b = [PB(), PB()]
        _z = PB()
        zq_b = [_z, _z]
        dT_b = PB()
        misc_b = PB()
        pdt_b = pacs_b = ptot_b = pst_b = misc_b
        pcb_b = [misc_b, misc_b]
        ydg_b = yof_b = PB()
        nsc = self.norm_scratch(es, "s", tp2)
        tp_b = nsc[7]
        tp16 = tp2[:].bitcast(BF16)
        tpg = tp16.rearrange("p (c t) -> p c t", c=16)
        xnT = [self.S(es, "s_xnT%d" % i, [128, 8, 131], BF16) for i in range(2)]
        xnT_b = [Buf(), Buf()]
        for i in range(2):
            kb.op(kb.pool, lambda: G.memset(xnT[i][:], 0.0), w=[xnT_b[i]])
        xbcT = self.S(es, "s_xbcT", [128, 32, 128], BF16)
        xbc_b = [Buf() for _ in range(32)]
        acc = [self.S(es, "s_acc%d" % i, [128, 128], F32) for i in range(3)]
        acc_b = [Buf() for _ in range(3)]
        xs_tm = self.S(es, "s_xstm", [128, 2048], BF16)
        xs_b = Buf()
        B_tm = self.S(es, "s_Btm", [128, 1024], BF16)
        Btm_b = Buf()
        sz = self.S(es, "s_sz", [128, 2048], F32)
        sz_b = [Buf() for _ in range(8)]
        sm = self.S(es, "s_sm", [128, 10, 32], F32)
        DTV, EXPT, ADT, ACS, DFS, CD, DTE, TMP, W2 = range(9)
        sm_b = [Buf() for _ in range(10)]
        rhsD = [self.S(es, "s_rhsD0", [128, 512], F32)] * 2
        _b = Buf()
        rhsD_b = [_b, _b]
        Ee = [self.S(es, "s_E0", [128, 512], F32)] * 2
        _b = Buf()
        E_b = [_b, _b]
        CBm = [self.S(es, "s_CBm%d" % i, [128, 128], F32) for i in range(2)]
        CBm_b = [Buf(), Buf()]
        MT = [self.S(es, "s_MT%d" % i, [128, 512], BF16) for i in range(2)]
        MT_b = [Buf(), Buf()]
        xdt = [self.S(es, "s_xdt%d" % i, [128, 256], BF16) for i in range(2)]
        xdt_b = [Buf(), Buf()]
        xdtd = [self.S(es, "s_xdtd%d" % i, [128, 256], BF16) for i in range(2)]
        xdtd_b = [Buf(), Buf()]
        tt = [[self.S(es, "s_t%d_%d" % (k, i), [128, 256], F32) for i in range(2)] for k in range(3)]
        tt_b = [[Buf(), Buf()] for _ in range(3)]
        tt.append(tt[0])
        tt_b.append(tt_b[0])
        stg = self.S(es, "s_stg", [128, 8, 4], F32)
        stg_b = [Buf() for _ in range(8)]
        gn = self.S(es, "s_gn", [128, 2048], BF16)
        gn_b = Buf()
        gT = self.S(es, "s_gT", [128, 16, 128], BF16)
        gT_b = Buf()
        S32 = self.S(es, "s_S32", [128, 2048], F32)
        Sbf = self.S(es, "s_Sbf", [128, 2048], BF16)
        S32_b = [Buf() for _ in range(8)]
        Sbf_b = [Buf() for _ in range(8)]
        stmp = [self.S(es, "s_stmp0", [128, 256], F32)] * 2
        _b = Buf()
        stmp_b = [_b, _b]
        kb.op(kb.pool, lambda: G.memset(S32[:], 0.0), w=S32_b)
        kb.op(kb.pool, lambda: G.memset(Sbf[:], 0.0), w=Sbf_b)
        nprev = 0
        ipz = 0
        for ti, (t0, n) in enumerate(TILES):
            cur, prev = ti % 2, (ti + 1) % 2
            X = xnT[cur]
            kb.dma("sp", hs[cur][0:n, :], self.h_src(src, t0, n), w=[hs_b[cur]])
            self.norm_T(hs_b[cur], hs[cur][0:n, :], n, self.lnT[:, li, :], X[:, :, 3:3 + n], xnT_b[cur], nsc)
            kb.op(kb.pool, lambda: G.tensor_copy(X[:, :, 0:3], xnT[prev][:, :, nprev:nprev + 3]), r=[xnT_b[prev]], wp=[xnT_b[cur]])
            nprev = n
            for cc in range(32):
                p = ipz % 2
                ipz += 1
                pz = pzbs[p][:, 0:3 + n]
                for c in range(8):
                    kb.op(kb.pe, lambda: T.matmul(pz, w_in[:, c, 2048 + cc * 128:2048 + (cc + 1) * 128], X[:, c, 0:3 + n], start=(c == 0), stop=(c == 7)),
                          r=[win_b, xnT_b[cur]], w=[pz_b[p]] if c == 0 else [], wp=[pz_b[p]] if c else [], inc=(c == 7))
                a_ = acc[p][:, 0:n]
                kb.op(kb.act, lambda: A.activation(out=a_, in_=pz[:, 3:3 + n], func=AF.Identity, bias=cb[:, cc:cc + 1], scale=cw[:, cc, 3:4]),
                      r=[pz_b[p], par_b], w=[acc_b[p]])
                for k in range(3):
                    kb.op(kb.dve, lambda: V.scalar_tensor_tensor(a_, pz[:, k:k + n], cw[:, cc, k:k + 1], a_, ALU.mult, ALU.add),
                          r=[pz_b[p], par_b], w=[acc_b[p]])
                kb.op(kb.act, lambda: A.activation(out=xbcT[:, cc, 0:n], in_=a_, func=AF.Silu), r=[acc_b[p]], w=[xbc_b[cc]])
            for g in range(8):
                zs = g % 2
                zq = zqb[0:n, zs * 256:(zs + 1) * 256]
                for c in range(8):
                    kb.op(kb.pe, lambda: T.matmul(zq, X[:, c, 3:3 + n], w_in[:, c, g * 256:(g + 1) * 256], start=(c == 0), stop=(c == 7)),
                          r=[win_b, xnT_b[cur]], w=[zq_b[zs]] if c == 0 else [], wp=[zq_b[zs]] if c else [], inc=(c == 7))
                kb.op(kb.act, lambda: A.activation(out=sz[0:n, g * 256:(g + 1) * 256], in_=zq, func=AF.Silu), r=[zq_b[zs]], w=[sz_b[g]])
            for cc in range(16):
                kb.op(kb.pe, lambda: T.transpose(tp16[0:n, cc * 128:(cc + 1) * 128], xbcT[:, cc, 0:n], self.ident[:, :]),
                      r=[xbc_b[cc], self.cb_], w=[tp_b] if cc == 0 else [], wp=[tp_b] if cc else [], inc=(cc == 15))
            kb.op(kb.act, lambda: A.copy(xs_tm[0:n, 0:1024], tp16[0:n, 0:1024]), r=[tp_b], w=[xs_b])
            kb.op(kb.dve, lambda: V.tensor_copy(xs_tm[0:n, 1024:2048], tp16[0:n, 1024:2048]), r=[tp_b], wp=[xs_b])
            for g in range(8):
                kb.op(kb.pe, lambda: T.transpose(tp16[0:n, g * 128:(g + 1) * 128], xbcT[:, 16 + g, 0:n], self.ident[:, :]),
                      r=[xbc_b[16 + g], self.cb_], w=[tp_b] if g == 0 else [], wp=[tp_b] if g else [], inc=(g == 7))
            kb.op(kb.act, lambda: A.copy(B_tm[0:n, :], tp16[0:n, 0:1024]), r=[tp_b], w=[Btm_b])
            pdt, pacs, ptot = miscb[0:n, 0:32], miscb[0:n, 32:64], miscb[:, 64:96]
            for c in range(8):
                kb.op(kb.pe, lambda: T.matmul(pdt, X[:, c, 3:3 + n], w_in[:, c, 6144:6176], start=(c == 0), stop=(c == 7)),
                      r=[win_b, xnT_b[cur]], w=[pdt_b] if c == 0 else [], wp=[pdt_b] if c else [], inc=(c == 7))
            smv = lambda k: sm[0:n, k, :]
            kb.op(kb.dve, lambda: V.tensor_tensor(smv(DTV), pdt, dtb[0:n, :], ALU.add), r=[pdt_b, par_b], w=[sm_b[DTV]])
            kb.op(kb.act, lambda: A.activation(out=smv(EXPT), in_=smv(DTV), func=AF.Exp), r=[sm_b[DTV]], w=[sm_b[EXPT]])
            kb.op(kb.act, lambda: A.activation(out=smv(DTV), in_=smv(EXPT), func=AF.Ln, bias=1.0), r=[sm_b[EXPT]], w=[sm_b[DTV]])
            kb.op(kb.dve, lambda: V.tensor_tensor(smv(ADT), smv(DTV), arep[0:n, :], ALU.mult), r=[sm_b[DTV], arep_b], w=[sm_b[ADT]])
            kb.op(kb.pe, lambda: T.matmul(pacs, self.tri32[0:n, 0:n], smv(ADT), start=True, stop=True), r=[sm_b[ADT], self.cb_], w=[pacs_b])
            kb.op(kb.pe, lambda: T.matmul(ptot, self.ones32[0:n, :], smv(ADT), start=True, stop=True), r=[sm_b[ADT], self.cb_], w=[ptot_b])
            kb.op(kb.dve, lambda: V.tensor_copy(smv(ACS), pacs), r=[pacs_b], w=[sm_b[ACS]])
            kb.op(kb.act, lambda: A.activation(out=smv(DFS), in_=pacs, func=AF.Exp), r=[pacs_b], w=[sm_b[DFS]])
            kb.op(kb.act, lambda: A.activation(out=sm[:, CD, :], in_=ptot, func=AF.Exp), r=[ptot_b], w=[sm_b[CD]])
            kb.op(kb.dve, lambda: V.tensor_tensor(smv(TMP), ptot[0:n, :], smv(ACS), ALU.subtract), r=[ptot_b, sm_b[ACS]], w=[sm_b[TMP]])
            kb.op(kb.act, lambda: A.activation(out=smv(DTE), in_=smv(TMP), func=AF.Exp), r=[sm_b[TMP]], w=[sm_b[DTE]])
            kb.op(kb.dve, lambda: V.tensor_tensor(smv(W2), smv(DTV), smv(DTE), ALU.mult), r=[sm_b[DTV], sm_b[DTE]], w=[sm_b[W2]])
            for g in range(8):
                b2 = g % 2
                hsl = slice(4 * g, 4 * g + 4)
                gsl = slice(g * 256, (g + 1) * 256)
                v3 = lambda ap: ap.rearrange("p (j l) -> p j l", j=4)
                rD = rhsD[b2][0:n, 0:4 * n]
                kb.op(kb.pool, lambda: G.tensor_tensor(v3(rD), bc(sm[0:n, ADT, hsl], [n, 4, n], 2), bc(self.tri32[0:n, 0:n], [n, 4, n], 1), ALU.mult),
                      r=[sm_b[ADT], self.cb_], w=[rhsD_b[b2]])
                kb.op(kb.pe, lambda: T.matmul(dTb[0:n, 0:4 * n], self.ltri32[0:n, 0:n], rD, start=True, stop=True), r=[rhsD_b[b2], self.cb_], w=[dT_b])
                Ev = Ee[b2][0:n, 0:4 * n]
                kb.op(kb.act, lambda: A.activation(out=Ev, in_=dTb[0:n, 0:4 * n], func=AF.Exp), r=[dT_b], w=[E_b[b2]])
                pcb = miscb[0:n, 128:128 + n]
                kb.op(kb.pe, lambda: T.matmul(pcb, xbcT[:, 16 + g, 0:n], xbcT[:, 24 + g, 0:n], start=True, stop=True),
                      r=[xbc_b[16 + g], xbc_b[24 + g]], w=[pcb_b[b2]])
                kb.op(kb.dve, lambda: V.tensor_tensor(CBm[b2][0:n, 0:n], pcb, self.tri32[0:n, 0:n], ALU.mult), r=[pcb_b[b2], self.cb_], w=[CBm_b[b2]])
                MTv = MT[b2][0:n, 0:4 * n]
                kb.op(kb.pool, lambda: G.tensor_tensor(v3(MTv), v3(Ev), bc(CBm[b2][0:n, 0:n], [n, 4, n], 1), ALU.mult),
                      r=[E_b[b2], CBm_b[b2]], w=[MT_b[b2]])
                xs3 = xs_tm[0:n, gsl].rearrange("p (j f) -> p j f", j=4)
                x3 = lambda ap: ap.rearrange("p (j f) -> p j f", j=4)
                kb.op(kb.pool, lambda: G.tensor_tensor(x3(xdt[b2][0:n, :]), xs3, bc(sm[0:n, DTV, hsl], [n, 4, 64], 2), ALU.mult),
                      r=[xs_b, sm_b[DTV]], w=[xdt_b[b2]])
                kb.op(kb.pool, lambda: G.tensor_tensor(x3(xdtd[b2][0:n, :]), xs3, bc(sm[0:n, W2, hsl], [n, 4, 64], 2), ALU.mult),
                      r=[xs_b, sm_b[W2]], w=[xdtd_b[b2]])
                for jj in range(4):
                    kb.op(kb.pe, lambda: T.matmul(yb[0:n, jj * 64:(jj + 1) * 64], MTv[:, jj * n:(jj + 1) * n], xdt[b2][0:n, jj * 64:(jj + 1) * 64], start=True, stop=True),
                          r=[MT_b[b2], xdt_b[b2]], w=[ydg_b] if jj == 0 else [], wp=[ydg_b] if jj else [], inc=(jj == 3))
                kb.op(kb.pe, lambda: T.matmul(yb[0:n, 256:512], xbcT[:, 24 + g, 0:n], Sbf[:, gsl], start=True, stop=True),
                      r=[xbc_b[24 + g], Sbf_b[g]], w=[yof_b])
                t1, t2, t3, tj = [tt[k][b2][0:n, :] for k in range(4)]
                kb.op(kb.dve, lambda: V.tensor_tensor(x3(t1), x3(yb[0:n, 256:512]), bc(sm[0:n, DFS, hsl], [n, 4, 64], 2), ALU.mult),
                      r=[yof_b, sm_b[DFS]], w=[tt_b[0][b2]])
                kb.op(kb.dve, lambda: V.tensor_tensor(t2, yb[0:n, 0:256], t1, ALU.add), r=[ydg_b, tt_b[0][b2]], w=[tt_b[1][b2]])
                kb.op(kb.pool, lambda: G.tensor_tensor(x3(t3), xs3, bc(dsk[0:n, hsl], [n, 4, 64], 2), ALU.mult), r=[xs_b, par_b], w=[tt_b[2][b2]])
                kb.op(kb.pool, lambda: G.tensor_tensor(t2, t2, t3, ALU.add), r=[tt_b[2][b2]], w=[tt_b[1][b2]])
                kb.op(kb.pool, lambda: G.tensor_tensor(t2, t2, sz[0:n, gsl], ALU.mult), r=[sz_b[g]], w=[tt_b[1][b2]])
                kb.op(kb.act, lambda: A.activation(out=tj, in_=t2, func=AF.Square, accum_out=stg[0:n, g, 0:1]), r=[tt_b[1][b2]], w=[tt_b[3][b2], stg_b[g]])
                self.rstd_ops(stg[:, g, :], stg_b[g], n, 0, 1.0 / 256)
                kb.op(kb.dve, lambda: V.tensor_scalar(gn[0:n, gsl], t2, stg[0:n, g, 2:3], None, ALU.mult), r=[tt_b[1][b2], stg_b[g]],
                      w=[gn_b] if g == 0 else [], wp=[gn_b] if g else [])
                kb.op(kb.pe, lambda: T.matmul(miscb[:, 256:512], B_tm[0:n, g * 128:(g + 1) * 128], xdtd[b2][0:n, :], start=True, stop=True),
                      r=[Btm_b, xdtd_b[b2]], w=[pst_b])
                kb.op(kb.pool, lambda: G.tensor_tensor(x3(stmp[b2][:, :]), x3(S32[:, gsl]), bc(sm[:, CD, hsl], [128, 4, 64], 2), ALU.mult),
                      r=[S32_b[g], sm_b[CD]], w=[stmp_b[b2]])
                kb.op(kb.dve, lambda: V.tensor_tensor(S32[:, gsl], stmp[b2][:, :], miscb[:, 256:512], ALU.add), r=[stmp_b[b2], pst_b], w=[S32_b[g]])
                kb.op(kb.act, lambda: A.copy(Sbf[:, gsl], S32[:, gsl]), r=[S32_b[g]], w=[Sbf_b[g]])
            for cc in range(16):
                kb.op(kb.pe, lambda: T.transpose(tpg[:, cc, 0:n], gn[0:n, cc * 128:(cc + 1) * 128], self.ident[0:n, 0:n]),
                      r=[gn_b, self.cb_], w=[tp_b] if cc == 0 else [], wp=[tp_b] if cc else [], inc=(cc == 15))
            kb.op(kb.dve, lambda: V.tensor_tensor(gT[:, :, 0:n], tpg[:, :, 0:n], bc(sng[:, :], [128, 16, n], 2), ALU.mult), r=[tp_b, par_b], w=[gT_b])
            for hh in range(2):
                po = dTb[0:n, :] if hh == 0 else yb[0:n, :]
                pbufs = [dT_b] if hh == 0 else [ydg_b]
                for cc in range(16):
                    kb.op(kb.pe, lambda: T.matmul(po, gT[:, cc, 0:n], w_out[:, cc, hh * 512:(hh + 1) * 512], start=(cc == 0), stop=(cc == 15)),
                          r=[gT_b, wout_b], w=pbufs if cc == 0 else [], wp=pbufs if cc else [], inc=(cc == 15))
                kb.op(kb.dve, lambda: V.tensor_tensor(hs[cur][0:n, hh * 512:(hh + 1) * 512], po, hs[cur][0:n, hh * 512:(hh + 1) * 512], ALU.add),
                      r=pbufs + [hs_b[cur]], wp=[hs_b[cur]])
            dap = self.h_dst(dst, t0, n)
            if dap is not None:
                kb.dma("sp", dap, hs[cur][0:n, :], r=[hs_b[cur]])

    def rstd_ops(self, st, st_b, n, c0, inv_n):
        kb, nc = self.kb, self.nc
        kb.op(kb.act, lambda: nc.scalar.activation(out=st[0:n, c0 + 1:c0 + 2], in_=st[0:n, c0:c0 + 1], func=AF.Ln, bias=EPS, scale=inv_n),
              r=[st_b], wp=[st_b])
        kb.op(kb.act, lambda: nc.scalar.activation(out=st[0:n, c0 + 2:c0 + 3], in_=st[0:n, c0 + 1:c0 + 2], func=AF.Exp, scale=-0.5),
              r=[st_b], wp=[st_b])

    def phase_mla(self, es, j, li, src, dst):
        kb, nc = self.kb, self.nc
        V, A, G, T = nc.vector, nc.scalar, nc.gpsimd, nc.tensor
        oT = self.S(es, "oT", [128, 8, NT], BF16)
        oT_b = Buf()
        w_o = self.S(es, "w_o", [128, 8, D], BF16)
        w_o_b = Buf()
        gq = self.S(es, "gq", [128, 96], F32)
        gk = self.S(es, "gk", [128, 96], F32)
        par_b = Buf()
        self.tri16 = self.S(es, "tri16", [128, 128], BF16)
        kb.dma("pool", self.tri16[:], self.tri_d, wp=[par_b])
        kb.dma("sp", gq[:], self.gq_d[j], wp=[par_b])
        kb.dma("sp", gk[:], self.gk_d[j], wp=[par_b])
        with contextlib.ExitStack() as s1:
            w_in = self.S(s1, "a_win", [128, 8, 672], BF16)
            w_qb = self.S(s1, "a_wqb", [128, 3, 1536], BF16)
            w_kvb = self.S(s1, "a_wkvb", [128, 2, 2048], BF16)
            wb = Buf()
            kb.dma("pool", w_in[:], self.w_mla_in[j].rearrange("(c p) f -> p c f", p=128), wp=[wb])
            kb.dma("pool", w_qb[:], self.w_mla_qb[j].rearrange("(c p) f -> p c f", p=128), wp=[wb])
            kb.dma("pool", w_kvb[:], self.w_mla_kvb[j].rearrange("(c p) f -> p c f", p=128), wp=[wb])
            kb.dma("pool", w_o[:], self.w_mla_out[j].rearrange("(c p) f -> p c f", p=128), wp=[w_o_b])
            qag = self.S(s1, "a_qag", [128, 3], F32)
            kvag = self.S(s1, "a_kvag", [128, 2], F32)
            kb.dma("sp", qag[:], self.qag_d[j], wp=[par_b])
            kb.dma("sp", kvag[:], self.kvag_d[j], wp=[par_b])
            tp_ps = self.P(s1, "a_tp", [128, 512], F32)
            latA = self.P(s1, "a_latA", [128, 512], F32)
            latB = self.P(s1, "a_latB", [128, 512], F32)
            big = self.P(s1, "a_big", [128, 2048], F32)
            tph_ps = self.P(s1, "a_tph", [128, 512], F32)
            latA_b, latB_b, tph_b = PB(), PB(), PB()
            big_b = [PB() for _ in range(4)]
            nsc = self.norm_scratch(s1, "a", tp_ps)
            tp5 = tp_ps[:].bitcast(BF16)[:, 0:640].rearrange("p (c t) -> p c t", c=5)
            tp_b = nsc[7]
            tph = tph_ps[:].bitcast(BF16)[:, 0:1024].rearrange("p (c t) -> p c t", c=8)
            qTv = self.qT_d.rearrange("h p t -> p h t")
            kTv = self.kT_d.rearrange("h p t -> p h t")

            class NS:
                pass

            def mkbufs(ci):
                b = NS()
                nm = lambda x: "a%d_%s" % (ci, x)
                b.hs = self.S(s1, nm("h"), [128, D], F32); b.hs_b = Buf()
                b.cs = self.S(s1, nm("cs"), [128, 2, 16], F32); b.cs_b = Buf()
                b.xnT = self.S(s1, nm("xnT"), [128, 8, 128], BF16); b.xnT_b = Buf()
                b.st = self.S(s1, nm("st"), [128, 8], F32); b.st_b = Buf()
                b.qln = self.S(s1, nm("qln"), [128, 384], BF16)
                b.kvln = self.S(s1, nm("kvln"), [128, 256], BF16)
                b.kpe = self.S(s1, nm("kpe"), [128, 32], F32); b.ln_b = Buf()
                b.sqj = self.S(s1, nm("sqj"), [128, 2048], F32); b.sqj_b = Buf()
                b.qlT = self.S(s1, nm("qlT"), [128, 3, 128], BF16)
                b.kvlT = self.S(s1, nm("kvlT"), [128, 2, 128], BF16); b.lT_b = Buf()
                b.raw = self.S(s1, nm("raw"), [128, 2048], F32); b.raw_b = Buf()
                b.s16 = self.S(s1, nm("s16"), [128, 3, 16], F32); b.s16_b = Buf()
                b.rt = self.S(s1, nm("rt"), [128, 4, 256], F32); b.rt_b = [Buf() for _ in range(4)]
                b.kpg = self.S(s1, nm("kpg"), [128, 2, 32], F32); b.kpg_b = Buf()
                b.qbf = self.S(s1, nm("qbf"), [128, 1536], BF16); b.qbf_b = Buf()
                b.stg = [self.S(s1, nm("stg%d" % i), [96, 16, 128], BF16) for i in range(2)]; b.stg_b = [Buf(), Buf()]
                b.vst = self.S(s1, nm("vst"), [128, 8, 192], BF16); b.vst_b = Buf()
                kb.op(kb.pool, lambda: G.memset(b.vst[:], 1.0), w=[b.vst_b])
                return b

            CH = [mkbufs(0), mkbufs(1)]

            def chain(ti):
                t0, n = TILES[ti]
                b = CH[ti % 2]
                xnT, st, st_b, qln, kvln, kpe, ln_b = b.xnT, b.st, b.st_b, b.qln, b.kvln, b.kpe, b.ln_b
                sqj, sqj_b, qlT, kvlT, lT_b, raw, raw_b = b.sqj, b.sqj_b, b.qlT, b.kvlT, b.lT_b, b.raw, b.raw_b
                s16, s16_b, rt, rt_b, kpg, kpg_b, qbf, qbf_b = b.s16, b.s16_b, b.rt, b.rt_b, b.kpg, b.kpg_b, b.qbf, b.qbf_b
                raw3 = raw[0:n, 0:1536].rearrange("p (h f) -> p h f", h=16)
                sq3 = sqj[0:n, 0:1536].rearrange("p (h f) -> p h f", h=16)
                qb3 = qbf[0:n, :].rearrange("p (h f) -> p h f", h=16)
                kv4 = raw[0:n, :].rearrange("p (h f) -> p h f", h=16)
                sqk = sqj[0:n, 0:1024].rearrange("p (h f) -> p h f", h=16)
                kv5 = raw[0:n, :].rearrange("p (c e f) -> p c e f", c=8, e=2)

                def head_T(sg, dstv):
                    for half in range(2):
                        for hh in range(8):
                            h = half * 8 + hh
                            kb.op(kb.pe, lambda: T.transpose(tph[0:96, hh, 0:n], qbf[0:n, h * 96:(h + 1) * 96], self.ident[0:n, 0:n]),
                                  r=[qbf_b, self.cb_], w=[tph_b] if hh == 0 else [], wp=[tph_b] if hh else [], inc=(hh == 7))
                        kb.op(kb.act, lambda: A.copy(b.stg[sg][0:96, half * 8:(half + 1) * 8, 0:n], tph[0:96, :, 0:n]),
                              r=[tph_b], w=[b.stg_b[sg]] if half == 0 else [], wp=[b.stg_b[sg]] if half else [])
                    kb.dma("sp", dstv[:, :, t0:t0 + n], b.stg[sg][0:96, :, 0:n], r=[b.stg_b[sg]])

                def rope(t1, t2, cosb, sinb, o1, o2, three_d, wb_, rd_bufs):
                    if three_d:
                        a_, b_, c_, d_ = [rt[0:n, i, :].rearrange("p (h f) -> p h f", h=16) for i in range(4)]
                    else:
                        a_, b_, c_, d_ = [rt[0:n, i, 0:16] for i in range(4)]
                    kb.op(kb.dve, lambda: V.tensor_tensor(a_, t1, cosb, ALU.mult), r=rd_bufs, w=[rt_b[0]])
                    kb.op(kb.dve, lambda: V.tensor_tensor(b_, t2, sinb, ALU.mult), r=rd_bufs, w=[rt_b[1]])
                    kb.op(kb.dve, lambda: V.tensor_tensor(c_, t1, sinb, ALU.mult), r=rd_bufs, w=[rt_b[2]])
                    kb.op(kb.dve, lambda: V.tensor_tensor(d_, t2, cosb, ALU.mult), r=rd_bufs, w=[rt_b[3]])
                    kb.op(kb.dve, lambda: V.tensor_tensor(o1, a_, b_, ALU.subtract), r=[rt_b[0], rt_b[1]], wp=[wb_])
                    kb.op(kb.dve, lambda: V.tensor_tensor(o2, c_, d_, ALU.add), r=[rt_b[2], rt_b[3]], wp=[wb_])

                def s0():
                    kb.dma("sp", b.hs[0:n, :], self.h_src(src, t0, n), w=[b.hs_b])

                def s0b():
                    kb.dma("sp", b.cs[0:n, 0, :], self.cos_d[t0:t0 + n, :], w=[b.cs_b])
                    kb.dma("sp", b.cs[0:n, 1, :], self.sin_d[t0:t0 + n, :], wp=[b.cs_b])

                def s1():
                    self.norm_T(b.hs_b, b.hs[0:n, :], n, self.lnT[:, li, :], xnT[:, :, 0:n], b.xnT_b, nsc)

                def s2():
                    for c in range(8):
                        kb.op(kb.pe, lambda: T.matmul(latA[0:n, 0:384], xnT[:, c, 0:n], w_in[:, c, 0:384], start=(c == 0), stop=(c == 7)),
                              r=[b.xnT_b, wb], w=[latA_b] if c == 0 else [], wp=[latA_b] if c else [], inc=(c == 7))
                    for c in range(8):
                        kb.op(kb.pe, lambda: T.matmul(latB[0:n, 0:288], xnT[:, c, 0:n], w_in[:, c, 384:672], start=(c == 0), stop=(c == 7)),
                              r=[b.xnT_b, wb], w=[latB_b] if c == 0 else [], wp=[latB_b] if c else [], inc=(c == 7))
                    kb.op(kb.act, lambda: A.activation(out=sqj[0:n, 0:384], in_=latA[0:n, 0:384], func=AF.Square, accum_out=st[0:n, 0:1]),
                          r=[latA_b], w=[sqj_b, st_b])
                    kb.op(kb.act, lambda: A.activation(out=sqj[0:n, 512:768], in_=latB[0:n, 0:256], func=AF.Square, accum_out=st[0:n, 3:4]),
                          r=[latB_b], wp=[sqj_b, st_b])
                    self.rstd_ops(st, st_b, n, 0, 1.0 / 384)
                    self.rstd_ops(st, st_b, n, 3, 1.0 / 256)
                    kb.op(kb.dve, lambda: V.tensor_scalar(qln[0:n, :], latA[0:n, 0:384], st[0:n, 2:3], None, ALU.mult), r=[latA_b, st_b], w=[ln_b])
                    kb.op(kb.dve, lambda: V.tensor_scalar(kvln[0:n, :], latB[0:n, 0:256], st[0:n, 5:6], None, ALU.mult), r=[latB_b, st_b], wp=[ln_b])
                    kb.op(kb.act, lambda: A.copy(kpe[0:n, :], latB[0:n, 256:288]), r=[latB_b], wp=[ln_b])

                def s3():
                    for c in range(5):
                        srcap = qln[0:n, c * 128:(c + 1) * 128] if c < 3 else kvln[0:n, (c - 3) * 128:(c - 2) * 128]
                        kb.op(kb.pe, lambda: T.transpose(tp5[:, c, 0:n], srcap, self.ident[0:n, 0:n]),
                              r=[ln_b, self.cb_], w=[tp_b] if c == 0 else [], wp=[tp_b] if c else [], inc=(c == 4))
                    kb.op(kb.dve, lambda: V.tensor_tensor(qlT[:, :, 0:n], tp5[:, 0:3, 0:n], bc(qag[:, :], [128, 3, n], 2), ALU.mult),
                          r=[tp_b, par_b], w=[lT_b])
                    kb.op(kb.dve, lambda: V.tensor_tensor(kvlT[:, :, 0:n], tp5[:, 3:5, 0:n], bc(kvag[:, :], [128, 2, n], 2), ALU.mult),
                          r=[tp_b, par_b], wp=[lT_b])

                def s4():
                    for ct in range(3):
                        for kc in range(3):
                            kb.op(kb.pe, lambda: T.matmul(big[0:n, ct * 512:(ct + 1) * 512], qlT[:, kc, 0:n], w_qb[:, kc, ct * 512:(ct + 1) * 512],
                                                          start=(kc == 0), stop=(kc == 2)),
                                  r=[lT_b, wb], w=[big_b[ct]] if kc == 0 else [], wp=[big_b[ct]] if kc else [], inc=(kc == 2))
                    for ct in range(3):
                        E_ = kb.act if ct != 1 else kb.dve
                        fn = (lambda: A.copy(raw[0:n, ct * 512:(ct + 1) * 512], big[0:n, ct * 512:(ct + 1) * 512])) if ct != 1 else \
                             (lambda: V.tensor_copy(raw[0:n, ct * 512:(ct + 1) * 512], big[0:n, ct * 512:(ct + 1) * 512]))
                        kb.op(E_, fn, r=[big_b[ct]], w=[raw_b] if ct == 0 else [], wp=[raw_b] if ct else [])

                def s5():
                    kb.op(kb.dve, lambda: V.tensor_tensor(sqj[0:n, 0:1536], raw[0:n, 0:1536], raw[0:n, 0:1536], ALU.mult), r=[raw_b], w=[sqj_b])
                    kb.op(kb.dve, lambda: V.tensor_reduce(s16[0:n, 0, :], sq3, AX.X, ALU.add), r=[sqj_b], w=[s16_b])
                    kb.op(kb.act, lambda: A.activation(out=s16[0:n, 1, :], in_=s16[0:n, 0, :], func=AF.Ln, bias=EPS, scale=1.0 / 96), r=[s16_b], wp=[s16_b])
                    kb.op(kb.act, lambda: A.activation(out=s16[0:n, 2, :], in_=s16[0:n, 1, :], func=AF.Exp, scale=-0.5), r=[s16_b], wp=[s16_b])
                    kb.op(kb.dve, lambda: V.tensor_tensor(raw3, raw3, bc(s16[0:n, 2, :], [n, 16, 96], 2), ALU.mult), r=[s16_b], w=[raw_b])
                    kb.op(kb.dve, lambda: V.tensor_tensor(raw3, raw3, bc(gq[0:n, :], [n, 16, 96], 1), ALU.mult), r=[par_b], w=[raw_b])
                    cosb = bc(b.cs[0:n, 0, :], [n, 16, 16], 1)
                    sinb = bc(b.cs[0:n, 1, :], [n, 16, 16], 1)
                    kb.op(kb.act, lambda: A.copy(qb3[:, :, 0:64], raw3[:, :, 0:64]), r=[raw_b], w=[qbf_b])
                    rope(raw3[:, :, 64:80], raw3[:, :, 80:96], cosb, sinb, qb3[:, :, 64:80], qb3[:, :, 80:96], True, qbf_b, [raw_b, b.cs_b])

                def s6():
                    head_T(0, qTv)

                def s7():
                    for ct in range(4):
                        for kc in range(2):
                            kb.op(kb.pe, lambda: T.matmul(big[0:n, ct * 512:(ct + 1) * 512], kvlT[:, kc, 0:n], w_kvb[:, kc, ct * 512:(ct + 1) * 512],
                                                          start=(kc == 0), stop=(kc == 1)),
                                  r=[lT_b, wb], w=[big_b[ct]] if kc == 0 else [], wp=[big_b[ct]] if kc else [], inc=(kc == 1))
                    for ct in range(4):
                        E_ = kb.act if ct % 2 == 0 else kb.dve
                        fn = (lambda: A.copy(raw[0:n, ct * 512:(ct + 1) * 512], big[0:n, ct * 512:(ct + 1) * 512])) if ct % 2 == 0 else \
                             (lambda: V.tensor_copy(raw[0:n, ct * 512:(ct + 1) * 512], big[0:n, ct * 512:(ct + 1) * 512]))
                        kb.op(E_, fn, r=[big_b[ct]], w=[raw_b] if ct == 0 else [], wp=[raw_b] if ct else [])

                def s8():
                    kb.op(kb.dve, lambda: V.tensor_tensor(sqk, kv4[:, :, 0:64], kv4[:, :, 0:64], ALU.mult), r=[raw_b], w=[sqj_b])
                    kb.op(kb.dve, lambda: V.tensor_reduce(s16[0:n, 0, :], sqk, AX.X, ALU.add), r=[sqj_b], w=[s16_b])
                    kb.op(kb.act, lambda: A.activation(out=kpg[0:n, 1, :], in_=kpe[0:n, :], func=AF.Square, accum_out=st[0:n, 6:7]),
                          r=[ln_b], w=[kpg_b], wp=[st_b])
                    kb.op(kb.dve, lambda: V.tensor_scalar(s16[0:n, 0, :], s16[0:n, 0, :], st[0:n, 6:7], None, ALU.add), r=[st_b, s16_b], wp=[s16_b])
                    kb.op(kb.act, lambda: A.activation(out=s16[0:n, 1, :], in_=s16[0:n, 0, :], func=AF.Ln, bias=EPS, scale=1.0 / 96), r=[s16_b], wp=[s16_b])
                    kb.op(kb.act, lambda: A.activation(out=s16[0:n, 2, :], in_=s16[0:n, 1, :], func=AF.Exp, scale=-0.5), r=[s16_b], wp=[s16_b])
                    kb.op(kb.act, lambda: A.copy(b.vst[0:n, :, 0:64], kv5[:, :, 0, 64:128]), r=[raw_b, b.vst_b], wp=[b.vst_b])
                    kb.op(kb.dve, lambda: V.tensor_copy(b.vst[0:n, :, 128:192], kv5[:, :, 1, 64:128]), r=[raw_b, b.vst_b], wp=[b.vst_b])
                    kb.dma("sp", self.va_d[t0:t0 + n, :, :], b.vst[0:n, :, :], r=[b.vst_b])
                    kb.op(kb.dve, lambda: V.tensor_tensor(kv4[:, :, 0:64], kv4[:, :, 0:64], bc(s16[0:n, 2, :], [n, 16, 64], 2), ALU.mult),
                          r=[s16_b], w=[raw_b])
                    kb.op(kb.dve, lambda: V.tensor_tensor(qb3[:, :, 0:64], kv4[:, :, 0:64], bc(gk[0:n, 0:64], [n, 16, 64], 1), ALU.mult),
                          r=[raw_b, par_b], w=[qbf_b])
                    kb.op(kb.dve, lambda: V.tensor_tensor(kpg[0:n, 0, :], kpe[0:n, :], gk[0:n, 64:96], ALU.mult), r=[ln_b, par_b], w=[kpg_b])
                    rope(kpg[0:n, 0, 0:16], kpg[0:n, 0, 16:32], b.cs[0:n, 0, :], b.cs[0:n, 1, :], kpg[0:n, 1, 0:16], kpg[0:n, 1, 16:32],
                         False, kpg_b, [kpg_b, b.cs_b])
                    kb.op(kb.dve, lambda: V.tensor_tensor(qb3[:, :, 64:96], bc(kpg[0:n, 1, :], [n, 16, 32], 1), bc(s16[0:n, 2, :], [n, 16, 32], 2), ALU.mult),
                          r=[kpg_b, s16_b], wp=[qbf_b])

                def s9():
                    head_T(1, kTv)

                return [s0, s0b, s1, s2, s3, s4, s5, s6, s7, s8, s9]

            chains = [chain(k) for k in range(len(TILES))]
            NTL = len(TILES)
            for k in (0, 1):
                chains[k][0]()
                chains[k][1]()
                chains[k][2]()
            for k in range(0, NTL, 2):
                pair = [chains[k]] + ([chains[k + 1]] if k + 1 < NTL else [])
                nxt = [chains[k2] for k2 in (k + 2, k + 3) if k2 < NTL]
                for c_ in nxt:
                    c_[0]()
                for i in range(3, 11):
                    for c_ in pair:
                        c_[i]()
                    if i == 5:
                        for c_ in nxt:
                            c_[2]()
                    if i == 9:
                        for c_ in nxt:
                            c_[1]()
            kb.barrier()
        with contextlib.ExitStack() as s2:
            qh = [self.S(s2, "b_q%d" % i, [96, NT], BF16) for i in range(2)]
            kh = [self.S(s2, "b_k%d" % i, [96, NT], BF16) for i in range(2)]
            qk_b = [Buf(), Buf()]
            va = [self.S(s2, "b_va%d" % i, [128, 33, 192], BF16) for i in range(2)]
            va_b = [Buf(), Buf()]
            NPS = 6
            pT = [self.S(s2, "b_pT%d" % i, [128, 512], BF16) for i in range(NPS)]
            pT_b = [Buf() for _ in range(NPS)]
            rden = [self.S(s2, "b_rd%d" % i, [128, 512], F32) for i in range(2)]
            rdsh = [self.S(s2, "b_rs%d" % i, [128, 512], F32) for i in range(2)]
            rd_b = [Buf(), Buf()]
            rs_b = [Buf(), Buf()]
            bnd = self.S(s2, "b_bnd", [128, 8], F32)
            bnd_b = Buf()
            ps = [self.P(s2, "b_ps%d" % i, [128, 512], F32) for i in range(NPS)]
            ps_b = [PB() for _ in range(NPS)]
            po = [self.P(s2, "b_po%d" % i, [128, 512], F32) for i in range(2)]
            po_b = [PB(), PB()]
            kb.op(kb.dve, lambda: V.tensor_reduce(bnd[:, 0:1], gq[:, :], AX.X, ALU.max), r=[par_b], w=[bnd_b])
            kb.op(kb.dve, lambda: V.tensor_reduce(bnd[:, 1:2], gq[:, :], AX.X, ALU.min), r=[par_b], wp=[bnd_b])
            kb.op(kb.dve, lambda: V.tensor_reduce(bnd[:, 2:3], gk[:, :], AX.X, ALU.max), r=[par_b], wp=[bnd_b])
            kb.op(kb.dve, lambda: V.tensor_reduce(bnd[:, 3:4], gk[:, :], AX.X, ALU.min), r=[par_b], wp=[bnd_b])
            kb.op(kb.dve, lambda: V.scalar_tensor_tensor(bnd[:, 4:5], bnd[:, 1:2], -1.0, bnd[:, 0:1], ALU.mult, ALU.max), r=[bnd_b], wp=[bnd_b])
            kb.op(kb.dve, lambda: V.scalar_tensor_tensor(bnd[:, 5:6], bnd[:, 3:4], -1.0, bnd[:, 2:3], ALU.mult, ALU.max), r=[bnd_b], wp=[bnd_b])
            kb.op(kb.dve, lambda: V.scalar_tensor_tensor(bnd[:, 6:7], bnd[:, 4:5], -float(np.sqrt(96.0)), bnd[:, 5:6], ALU.mult, ALU.mult),
                  r=[bnd_b], wp=[bnd_b])
            negB = bnd[:, 6:7]
            scale = float(96.0 ** -0.5)

            def load_head(h):
                b = h % 2
                kb.dma("sp", qh[b][:, :], self.qT_d[h], w=[qk_b[b]])
                kb.dma("sp", kh[b][:, :], self.kT_d[h], wp=[qk_b[b]])

            def load_pair(c):
                b = c % 2
                kb.dma("sp", va[b][0:16, 0, :], self.va_d[0:16, c, :], w=[va_b[b]])
                vv = self.va_d[16:NT, c, :].rearrange("(j p) w -> p j w", p=128)
                for jj in range(0, 32, 8):
                    kb.dma("sp", va[b][:, 1 + jj:9 + jj, :], vv[:, jj:jj + 8, :], wp=[va_b[b]])

            load_pair(0)
            load_head(0)
            items = []
            for h in range(MLA_H):
                for qi in range(9):
                    if qi == 0:
                        q0, nq = 0, 16
                        kts = [(0, 0, 16, 0, True)]
                    else:
                        q0, nq = 16 + 512 * (qi - 1), 512
                        kts = [(0, 0, 16, 0, False)] + [(kt, 16 + 128 * (kt - 1), 128, 0, False) for kt in range(1, 4 * (qi - 1) + 1)]
                        kts += [(4 * (qi - 1) + 1 + i, 16 + 128 * (4 * (qi - 1) + i), 128, 128 * i, True) for i in range(4)]
                    for idx, kt in enumerate(kts):
                        items.append((h, qi, q0, nq, idx, len(kts)) + kt)
            LA = 4
            NI = len(items)
            for i in range(NI + LA):
                if i < NI:
                    (h, qi, q0, nq, idx, nk_t, kt, k0, nk, qoff, diag) = items[i]
                    if qi == 0 and idx == 0 and h + 1 < MLA_H:
                        load_head(h + 1)
                        if h % 2 == 1:
                            load_pair(h // 2 + 1)
                    hb_ = h % 2
                    nqq = nq - qoff
                    p = i % NPS
                    kb.op(kb.pe, lambda: T.matmul(ps[p][0:nk, 0:nqq], kh[hb_][:, k0:k0 + nk], qh[hb_][:, q0 + qoff:q0 + nq], start=True, stop=True),
                          r=[qk_b[hb_]], w=[ps_b[p]])
                    kb.op(kb.act, lambda: A.activation(out=pT[p][0:nk, 0:nqq], in_=ps[p][0:nk, 0:nqq], func=AF.Exp, bias=negB[0:nk, :], scale=scale),
                          r=[ps_b[p], bnd_b], w=[pT_b[p]])
                    if diag:
                        kb.op(kb.dve, lambda: V.tensor_tensor(pT[p][0:nk, 0:nk], pT[p][0:nk, 0:nk], self.tri16[0:nk, 0:nk], ALU.mult),
                              r=[self.cb_], w=[pT_b[p]])
                ii = i - LA
                if ii >= 0:
                    (h, qi, q0, nq, idx, nk_t, kt, k0, nk, qoff, diag) = items[ii]
                    c, e = h // 2, h % 2
                    vb_ = c % 2
                    dlo, dhi = (0, 64) if e == 0 else (64, 128)
                    nlo, nhi = (64, 128) if e == 0 else (0, 64)
                    nqq = nq - qoff
                    p = ii % NPS
                    pp = (h * 9 + qi) % 2
                    first, last = idx == 0, idx == nk_t - 1
                    kb.op(kb.pe, lambda: T.matmul(po[pp][:, qoff:nq], va[vb_][0:nk, kt, e * 64:e * 64 + 128], pT[p][0:nk, 0:nqq], start=first, stop=last),
                          r=[pT_b[p], va_b[vb_]], w=[po_b[pp]] if first else [], wp=[] if first else [po_b[pp]], inc=last)
                    if last:
                        kb.op(kb.dve, lambda: V.reciprocal(rden[pp][nlo:nhi, 0:nq], po[pp][nlo:nhi, 0:nq]), r=[po_b[pp]], w=[rd_b[pp]])
                        kb.op(kb.dve, lambda: V.tensor_copy(rdsh[pp][dlo:dhi, 0:nq], rden[pp][nlo:nhi, 0:nq]), r=[rd_b[pp]], w=[rs_b[pp]])
                        kb.op(kb.dve, lambda: V.tensor_tensor(oT[dlo:dhi, c, q0:q0 + nq], po[pp][dlo:dhi, 0:nq], rdsh[pp][dlo:dhi, 0:nq], ALU.mult),
                              r=[po_b[pp], rs_b[pp]], wp=[oT_b])
            kb.barrier()
        with contextlib.ExitStack() as s3:
            hs = [self.S(s3, "c_h%d" % i, [128, D], F32) for i in range(3)]
            hs_b = [Buf() for _ in range(3)]
            po = [self.P(s3, "c_po%d" % i, [128, 512], F32) for i in range(4)]
            po_b = [PB() for _ in range(4)]
            ip = 0
            for ti, (t0, n) in enumerate(TILES):
                s = ti % 3
                kb.dma("sp", hs[s][0:n, :], self.h_src(src, t0, n), w=[hs_b[s]])
                for hh in range(2):
                    p = ip % 4
                    ip += 1
                    for c in range(8):
                        kb.op(kb.pe, lambda: T.matmul(po[p][0:n, :], oT[:, c, t0:t0 + n], w_o[:, c, hh * 512:(hh + 1) * 512], start=(c == 0), stop=(c == 7)),
                              r=[oT_b, w_o_b], w=[po_b[p]] if c == 0 else [], wp=[po_b[p]] if c else [], inc=(c == 7))
                    kb.op(kb.dve, lambda: V.tensor_tensor(hs[s][0:n, hh * 512:(hh + 1) * 512], po[p][0:n, :], hs[s][0:n, hh * 512:(hh + 1) * 512], ALU.add),
                          r=[po_b[p], hs_b[s]], wp=[hs_b[s]])
                dap = self.h_dst(dst, t0, n)
                if dap is not None:
                    kb.dma("sp", dap, hs[s][0:n, :], r=[hs_b[s]])


def host_inputs(inp):
    f = lambda a: np.ascontiguousarray(np.asarray(a, dtype=np.float32))
    rep = lambda a: np.ascontiguousarray(np.broadcast_to(np.asarray(a, np.float32)[:, None, :], (a.shape[0], 128, a.shape[1])))
    colT = lambda a, c: np.ascontiguousarray(np.asarray(a, np.float32).reshape(a.shape[0], c, 128).transpose(0, 2, 1))
    ln = np.concatenate([np.asarray(inp["ln_mix"], np.float32), np.asarray(inp["ln_mlp"], np.float32)], 0)
    lnT = np.ascontiguousarray(ln.reshape(8, 8, 128).transpose(2, 0, 1))
    cw = np.asarray(inp["ssd_conv_w"], np.float32)
    cwT = np.ascontiguousarray(cw.reshape(2, 4, 32, 128).transpose(0, 3, 2, 1))
    k = np.arange(128)
    tri = (k[:, None] <= k[None, :]).astype(np.float32)
    inv = 1.0 / (10000.0 ** (np.arange(0, 32, 2, dtype=np.float32) / 32.0))
    ang = np.arange(NT, dtype=np.float32)[:, None] * inv[None, :].astype(np.float32)
    common = {
        "meta": f(inp["meta_tokens"]),
        "ssd_w_in": f(inp["ssd_w_in"]), "ssd_w_out": f(inp["ssd_w_out"]),
        "mla_w_in": f(inp["mla_w_in"]), "mla_w_q_b": f(inp["mla_w_q_b"]), "mla_w_kv_b": f(inp["mla_w_kv_b"]),
        "mla_w_out": f(inp["mla_w_out"]), "mlp_w_up": f(inp["mlp_w_up"]), "mlp_w_down": f(inp["mlp_w_down"]),
        "lnT": lnT, "cw": cwT, "cb": colT(inp["ssd_conv_b"], 32),
        "dtb_rep": rep(inp["ssd_dt_bias"]), "alog_rep": rep(inp["ssd_a_log"]), "dskip_rep": rep(inp["ssd_d"]),
        "ssd_normT": colT(inp["ssd_norm"], 16), "q_a_T": colT(inp["mla_q_a_norm"], 3), "kv_a_T": colT(inp["mla_kv_a_norm"], 2),
        "gq_rep": rep(inp["mla_q_norm"]), "gk_rep": rep(inp["mla_k_norm"]),
        "ident": np.eye(128, dtype=np.float32), "tri": tri, "ltri": np.ascontiguousarray(1.0 - tri),
        "ones": np.ones((128, 128), np.float32),
        "cos": np.cos(ang).astype(np.float32), "sin": np.sin(ang).astype(np.float32),
    }
    return common


FULL_PHASES = [
    ("ssd", 0, 0, "x", "h"), ("mlp", 0, "h", "h"),
    ("mla", 0, 1, "h", "h"), ("mlp", 1, "h", "h"),
    ("ssd", 1, 2, "h", "h"), ("mlp", 2, "h", "h"),
    ("mla", 1, 3, "h", "h"), ("mlp", 3, "h", "y"),
]


def run(inputs, phases, cores=8):
    common = host_inputs(inputs)
    x = np.asarray(inputs["x"], np.float32)
    prog = Prog(phases)
    in_maps = []
    for c in range(cores):
        m = dict(common)
        m["x"] = np.ascontiguousarray(x[c])
        in_maps.append(m)
    res = run_bass_kernel_spmd(prog.nc, in_maps, core_ids=list(range(cores)))
    return np.stack([np.asarray(r["y"]) for r in res.results], 0)


def kernel(**inputs):
    return run(inputs, FULL_PHASES, 8).astype(np.float32)
```

```python
import contextlib
import numpy as np
import concourse.bass as bass
import concourse.mybir as mybir
from concourse.bass_utils import run_bass_kernel_spmd

F32, BF16 = mybir.dt.float32, mybir.dt.bfloat16
AF = mybir.ActivationFunctionType
ALU = mybir.AluOpType
AX = mybir.AxisListType

NT, NM, D, SEQ = 4112, 16, 1024, 4096
TILES = [(0, 16)] + [(16 + 128 * j, 128) for j in range(32)]
EPS = 1e-6
DFF = 4096
SSD_IN = 6176
NH_S = 32
SSD_LEAD = 0.8
A1_LAG = 1
MLA_H = 16
QK = 96


class Buf:
    __slots__ = ("w", "r", "name", "ps")

    def __init__(self, name="", ps=False):
        self.w = {}
        self.r = {}
        self.name = name
        self.ps = ps


def PB():
    return Buf(ps=True)


class Eng:
    def __init__(self, name, e, sem):
        self.name, self.e, self.sem, self.cnt, self.seen = name, e, sem, 0, {}


class KB:
    def __init__(self, nc, es):
        self.nc = nc
        mk = lambda n: es.enter_context(nc.semaphore(n))
        self.pe = Eng("pe", nc.tensor, mk("s_pe"))
        self.act = Eng("act", nc.scalar, mk("s_act"))
        self.dve = Eng("dve", nc.vector, mk("s_dve"))
        self.pool = Eng("pool", nc.gpsimd, mk("s_pool"))
        self.sp = Eng("sp", nc.sync, mk("s_sp"))
        self.engs = [self.pe, self.act, self.dve, self.pool, self.sp]
        self.dsem = {"sp": [[mk("d_sp%d" % i), 0] for i in range(24)],
                     "pool": [[mk("d_pl%d" % i), 0] for i in range(8)]}
        self.drr = {"sp": 0, "pool": 0}
        self.nins = 0

    def _wait(self, E, toks):
        for key, (sem, val) in toks.items():
            if E is self.pe and key == "pe":
                continue
            if E.seen.get(key, 0) >= val:
                continue
            E.e.wait_ge(sem, val)
            E.seen[key] = val

    @staticmethod
    def _add(need, d):
        for k, sv in d.items():
            if k not in need or need[k][1] < sv[1]:
                need[k] = sv

    def _deps(self, r, w, wp, ekey=None):
        need = {}
        for b in r:
            self._add(need, b.w)
            if b.ps:
                self._add(need, {k: v for k, v in b.r.items() if k != ekey})
        for b in w:
            self._add(need, b.w)
            self._add(need, b.r)
        for b in wp:
            self._add(need, b.r)
            if b.ps:
                self._add(need, b.w)
        return need

    def _reg(self, key, tok, r, w, wp):
        for b in r:
            if key not in b.r or b.r[key][1] < tok[1]:
                b.r[key] = tok
        for b in w:
            b.w = {key: tok}
            b.r = {}
        for b in wp:
            if key not in b.w or b.w[key][1] < tok[1]:
                b.w[key] = tok

    def op(self, E, fn, r=(), w=(), wp=(), inc=True):
        self._wait(E, self._deps(r, w, wp, E.name))
        ins = fn()
        self.nins += 1
        if inc:
            E.cnt += 1
            ins.then_inc(E.sem, 1)
            tok = (E.sem, E.cnt)
        else:
            tok = (E.sem, E.cnt + 1)
        self._reg(E.name, tok, r, w, wp)

    def dma(self, Q, out, in_, r=(), w=(), wp=()):
        E = self.sp if Q == "sp" else self.pool
        self._wait(E, self._deps(r, w, wp))
        lst = self.dsem[Q]
        i = self.drr[Q]
        self.drr[Q] = (i + 1) % len(lst)
        sem, cnt = lst[i]
        key = (Q, i)
        if cnt > 0 and E.seen.get(key, 0) < cnt:
            E.e.wait_ge(sem, cnt)
            E.seen[key] = cnt
        ins = E.e.dma_start(out=out, in_=in_)
        ins.then_inc(sem, 16)
        self.nins += 1
        lst[i][1] = cnt + 16
        self._reg(key, (sem, cnt + 16), r, w, wp)

    def barrier(self):
        toks = {}
        for E in self.engs:
            if E.cnt > 0:
                toks[E.name] = (E.sem, E.cnt)
        for Q, lst in self.dsem.items():
            for i, (sem, cnt) in enumerate(lst):
                if cnt > 0:
                    toks[(Q, i)] = (sem, cnt)
        for E in self.engs:
            self._wait(E, toks)


def bc(ap, shape, axis):
    return ap.unsqueeze(axis).to_broadcast(list(shape))


class Prog:
    def __init__(self, phases):
        self.phases = phases
        nc = bass.Bass("TRN2", target_bir_lowering=False)
        self.nc = nc
        di = lambda name, shape: nc.dram_tensor(name, list(shape), F32, kind="ExternalInput").ap()
        self.x = di("x", [SEQ, D])
        self.meta = di("meta", [NM, D])
        self.w_ssd_in = di("ssd_w_in", [2, D, SSD_IN])
        self.w_ssd_out = di("ssd_w_out", [2, 2048, D])
        self.w_mla_in = di("mla_w_in", [2, D, 672])
        self.w_mla_qb = di("mla_w_q_b", [2, 384, 1536])
        self.w_mla_kvb = di("mla_w_kv_b", [2, 256, 2048])
        self.w_mla_out = di("mla_w_out", [2, D, D])
        self.w_up = di("mlp_w_up", [4, D, DFF])
        self.w_dn = di("mlp_w_down", [4, DFF, D])
        self.lnT_d = di("lnT", [128, 8, 8])
        self.cw_d = di("cw", [2, 128, 32, 4])
        self.cb_d = di("cb", [2, 128, 32])
        self.dtb_d = di("dtb_rep", [2, 128, 32])
        self.alog_d = di("alog_rep", [2, 128, 32])
        self.dsk_d = di("dskip_rep", [2, 128, 32])
        self.sng_d = di("ssd_normT", [2, 128, 16])
        self.qag_d = di("q_a_T", [2, 128, 3])
        self.kvag_d = di("kv_a_T", [2, 128, 2])
        self.gq_d = di("gq_rep", [2, 128, 96])
        self.gk_d = di("gk_rep", [2, 128, 96])
        self.ident_d = di("ident", [128, 128])
        self.tri_d = di("tri", [128, 128])
        self.ltri_d = di("ltri", [128, 128])
        self.ones_d = di("ones", [128, 128])
        self.cos_d = di("cos", [NT, 16])
        self.sin_d = di("sin", [NT, 16])
        self.y = nc.dram_tensor("y", [SEQ, D], F32, kind="ExternalOutput").ap()
        self.hd = nc.dram_tensor("hd", [NT, D], F32, kind="Internal").ap()
        self.qT_d = nc.dram_tensor("qT_d", [MLA_H, QK, NT], BF16, kind="Internal").ap()
        self.kT_d = nc.dram_tensor("kT_d", [MLA_H, QK, NT], BF16, kind="Internal").ap()
        self.va_d = nc.dram_tensor("va_d", [NT, 8, 192], BF16, kind="Internal").ap()

        with contextlib.ExitStack() as es:
            self.kb = KB(nc, es)
            self.build(es)

    def S(self, es, name, shape, dt):
        self.uid = getattr(self, "uid", 0) + 1
        return es.enter_context(self.nc.sbuf_tensor("sb%d_%s" % (self.uid, name), list(shape), dt))

    def P(self, es, name, shape, dt):
        self.uid = getattr(self, "uid", 0) + 1
        return es.enter_context(self.nc.psum_tensor("ps%d_%s" % (self.uid, name), list(shape), dt))

    def h_src(self, kind, t0, n):
        if kind == "x":
            return self.meta[0:16, :] if t0 == 0 else self.x[t0 - 16:t0 - 16 + n, :]
        return self.hd[t0:t0 + n, :]

    def h_dst(self, kind, t0, n):
        if kind == "y":
            return None if t0 == 0 else self.y[t0 - 16:t0 - 16 + n, :]
        return self.hd[t0:t0 + n, :]

    def build(self, es):
        kb, nc = self.kb, self.nc
        self.ident = self.S(es, "ident", [128, 128], BF16)
        self.tri32 = self.S(es, "tri32", [128, 128], F32)
        self.ltri32 = self.S(es, "ltri32", [128, 128], F32)
        self.ones32 = self.S(es, "ones32", [128, 128], F32)
        self.lnT = self.S(es, "lnT", [128, 8, 8], F32)
        self.cb_ = Buf("consts")
        kb.dma("pool", self.ident[:], self.ident_d, wp=[self.cb_])
        kb.dma("sp", self.tri32[:], self.tri_d, wp=[self.cb_])
        kb.dma("sp", self.ltri32[:], self.ltri_d, wp=[self.cb_])
        kb.dma("sp", self.ones32[:], self.ones_d, wp=[self.cb_])
        kb.dma("sp", self.lnT[:], self.lnT_d, wp=[self.cb_])
        for ph in self.phases:
            kind = ph[0]
            with contextlib.ExitStack() as pes:
                if kind == "mlp":
                    self.phase_mlp(pes, *ph[1:])
                elif kind == "ssd":
                    self.phase_ssd(pes, *ph[1:])
                elif kind == "mla":
                    self.phase_mla(pes, *ph[1:])
                kb.barrier()
        kb.barrier()

    def rstd_newton(self, st, st_b, n, inv_n, eps):
        kb, nc = self.kb, self.nc
        V = nc.vector
        I32 = mybir.dt.int32
        for f in self.rstd_newton_ops(st, st_b, n, inv_n, eps):
            f()

    def rstd_newton_ops(self, st, st_b, n, inv_n, eps):
        kb, nc = self.kb, self.nc
        V = nc.vector
        I32 = mybir.dt.int32
        x, g, t = st[0:n, 1:2], st[0:n, 2:3], st[0:n, 3:4]
        ops = []
        ops.append(lambda: kb.op(kb.dve, lambda: V.tensor_scalar(x, st[0:n, 0:1], inv_n, eps, ALU.mult, ALU.add), r=[st_b], wp=[st_b]))
        ops.append(lambda: kb.op(kb.dve, lambda: V.tensor_scalar(g.bitcast(I32), x.bitcast(I32), 1, None, ALU.arith_shift_right), r=[st_b], wp=[st_b]))
        ops.append(lambda: kb.op(kb.dve, lambda: V.tensor_scalar(g.bitcast(I32), g.bitcast(I32), -1, 0x5f3759df, ALU.mult, ALU.add), r=[st_b], wp=[st_b]))
        for _ in range(2):
            ops.append(lambda: kb.op(kb.dve, lambda: V.scalar_tensor_tensor(t, g, x, g, ALU.mult, ALU.mult), r=[st_b], wp=[st_b]))
            ops.append(lambda: kb.op(kb.dve, lambda: V.tensor_scalar(t, t, -0.5, 1.5, ALU.mult, ALU.add), r=[st_b], wp=[st_b]))
            ops.append(lambda: kb.op(kb.dve, lambda: V.tensor_tensor(g, g, t, ALU.mult), r=[st_b], wp=[st_b]))
        return ops

    def norm_T(self, hb, h_ap, n, gain_ap, dst_ap, dst_b, sc, newton=False, part=None):
        kb, nc = self.kb, self.nc
        junk, junk_b, ss, ss_b, xn, xn_b, tp, tp_b = sc
        if part == "B":
            return self._norm_T_b(n, gain_ap, dst_ap, dst_b, sc)
        kb.op(kb.act, lambda: nc.scalar.activation(out=junk[0:n, :], in_=h_ap, func=AF.Square, accum_out=ss[0:n, 0:1]),
              r=[hb], w=[junk_b, ss_b] if junk_b is not xn_b else [xn_b, ss_b])
        if newton:
            self.rstd_newton(ss, ss_b, n, 1.0 / D, EPS)
        else:
            kb.op(kb.act, lambda: nc.scalar.activation(out=ss[0:n, 1:2], in_=ss[0:n, 0:1], func=AF.Ln, bias=EPS, scale=1.0 / D),
                  r=[ss_b], wp=[ss_b])
            kb.op(kb.act, lambda: nc.scalar.activation(out=ss[0:n, 2:3], in_=ss[0:n, 1:2], func=AF.Exp, scale=-0.5),
                  r=[ss_b], wp=[ss_b])
        kb.op(kb.dve, lambda: nc.vector.tensor_scalar(xn[0:n, :], h_ap, ss[0:n, 2:3], None, ALU.mult),
              r=[hb, ss_b], w=[xn_b])
        if part == "A":
            return
        self._norm_T_b(n, gain_ap, dst_ap, dst_b, sc)

    def _norm_T_b(self, n, gain_ap, dst_ap, dst_b, sc):
        kb, nc = self.kb, self.nc
        junk, junk_b, ss, ss_b, xn, xn_b, tp, tp_b = sc
        for c in range(8):
            kb.op(kb.pe, lambda c=c: nc.tensor.transpose(tp[:, c, 0:n], xn[0:n, c * 128:(c + 1) * 128], self.ident[0:n, 0:n]),
                  r=[xn_b, self.cb_], w=[tp_b] if c == 0 else [], wp=[tp_b] if c else [], inc=(c == 7))
        kb.op(kb.dve, lambda: nc.vector.tensor_tensor(dst_ap, tp[:, :, 0:n], bc(gain_ap, [128, 8, n], 2), ALU.mult),
              r=[tp_b, self.cb_], w=[dst_b])

    def norm_scratch(self, es, pfx, tp_ps):
        ss = self.S(es, pfx + "ss", [128, 4], F32)
        xn = self.S(es, pfx + "xn", [128, 1024], BF16)
        tp = tp_ps[:].bitcast(BF16)[:, 0:1024].rearrange("p (c t) -> p c t", c=8)
        xn_b = Buf()
        return (xn, xn_b, ss, Buf(), xn, xn_b, tp, PB())

    def phase_mlp(self, es, li, src, dst):
        kb, nc = self.kb, self.nc
        wup = self.S(es, "wup", [128, 8, DFF], BF16)
        wdn = self.S(es, "wdn", [128, 32, D], BF16)
        wup_b = [Buf() for _ in range(8)]
        wdn_b = [Buf() for _ in range(8)]
        upv = self.w_up[li].rearrange("(c p) f -> p c f", p=128)
        dnv = self.w_dn[li].rearrange("(c p) d -> p c d", p=128)
        for i in range(8):
            kb.dma("pool", wup[:, :, i * 512:(i + 1) * 512], upv[:, :, i * 512:(i + 1) * 512], w=[wup_b[i]])
        for i in range(8):
            kb.dma("pool", wdn[:, 4 * i:4 * i + 4, :], dnv[:, 4 * i:4 * i + 4, :], w=[wdn_b[i]])
        NSLOT = 7
        hs = [self.S(es, "mh%d" % i, [128, D], F32) for i in range(NSLOT)]
        hs_b = [Buf() for _ in range(NSLOT)]
        xnT = self.S(es, "m_xnT", [128, 8, 512], BF16)
        xnT_b = Buf()
        uT = self.S(es, "m_uT", [128, 32, 512], BF16)
        uT_b = [Buf() for _ in range(32)]
        r32 = [self.S(es, "m_r32_%d" % i, [128, 512], F32) for i in range(2)]
        r32_b = [Buf(), Buf()]
        tp_ps = [self.P(es, "m_tp%d" % i, [128, 512], F32) for i in range(2)]
        pu = [self.P(es, "m_pu%d" % i, [128, 512], F32) for i in range(3)]
        pu_b = [PB() for _ in range(3)]
        pd = [self.P(es, "m_pd%d" % i, [128, 512], F32) for i in range(3)]
        pd_b = [PB() for _ in range(3)]
        nsc = [self.norm_scratch(es, "m%d" % i, tp_ps[i]) for i in range(2)]
        gain = self.lnT[:, 4 + li, :]
        groups = [[TILES[0]]] + [TILES[1 + 4 * g:5 + 4 * g] for g in range(8)]
        NG = len(groups)
        slots_all = []
        sl_ = 0
        for grp in groups:
            slots_all.append([(sl_ + i) % NSLOT for i in range(len(grp))])
            sl_ += len(grp)
        loaded = set()

        def load_tile(gi, k):
            if (gi, k) in loaded:
                return
            loaded.add((gi, k))
            (t0, n), s = groups[gi][k], slots_all[gi][k]
            kb.dma("sp", hs[s][0:n, :], self.h_src(src, t0, n), w=[hs_b[s]])

        def norm_part(gi, k, part):
            (t0, n), s = groups[gi][k], slots_all[gi][k]
            off = sum(nn for _, nn in groups[gi][:k])
            self.norm_T(hs_b[s], hs[s][0:n, :], n, gain, xnT[:, :, off:off + n], xnT_b, nsc[k % 2], part=part)

        for k in range(len(groups[0])):
            load_tile(0, k)
            norm_part(0, k, None)
        iu = 0
        ipd = 0
        for gi, grp in enumerate(groups):
            ntok = sum(n for _, n in grp)
            myslots = slots_all[gi]
            nxt = gi + 1 if gi + 1 < NG else None
            if nxt is not None:
                for k in range(min(len(groups[nxt]), NSLOT - len(grp))):
                    load_tile(nxt, k)
            for fc in range(32):
                p = iu % 3
                for c in range(8):
                    kb.op(kb.pe, lambda c=c, fc=fc, p=p: nc.tensor.matmul(pu[p][:, 0:ntok], wup[:, c, fc * 128:(fc + 1) * 128],
                                                                      xnT[:, c, 0:ntok], start=(c == 0), stop=(c == 7)),
                          r=[wup_b[fc // 4], xnT_b], w=[pu_b[p]] if c == 0 else [], wp=[pu_b[p]] if c else [], inc=(c == 7))
                rr = iu % 2
                kb.op(kb.act, lambda p=p, rr=rr: nc.scalar.activation(out=r32[rr][:, 0:ntok], in_=pu[p][:, 0:ntok], func=AF.Relu),
                      r=[pu_b[p]], w=[r32_b[rr]])
                E = kb.dve if (fc % 2 == 0) else kb.pool
                kb.op(E, lambda rr=rr, fc=fc, E=E: E.e.tensor_tensor(uT[:, fc, 0:ntok], r32[rr][:, 0:ntok], r32[rr][:, 0:ntok], ALU.mult),
                      r=[r32_b[rr]], w=[uT_b[fc]])
                iu += 1
            off = 0
            for kk, ((t0, n), s) in enumerate(zip(grp, myslots)):
                if nxt is not None and kk < len(groups[nxt]):
                    load_tile(nxt, kk)
                    norm_part(nxt, kk, "A")
                for hh in range(2):
                    p = ipd % 3
                    ipd += 1
                    for fc in range(32):
                        kb.op(kb.pe, lambda fc=fc, p=p, off=off, n=n, hh=hh: nc.tensor.matmul(
                            pd[p][0:n, :], uT[:, fc, off:off + n], wdn[:, fc, hh * 512:(hh + 1) * 512],
                            start=(fc == 0), stop=(fc == 31)),
                            r=[uT_b[fc], wdn_b[fc // 4]], w=[pd_b[p]] if fc == 0 else [], wp=[pd_b[p]] if fc else [], inc=(fc == 31))
                    kb.op(kb.dve, lambda p=p, n=n, hh=hh, s=s: nc.vector.tensor_tensor(
                        hs[s][0:n, hh * 512:(hh + 1) * 512], pd[p][0:n, :], hs[s][0:n, hh * 512:(hh + 1) * 512], ALU.add),
                        r=[pd_b[p], hs_b[s]], wp=[hs_b[s]])
                dst_ap = self.h_dst(dst, t0, n)
                if dst_ap is not None:
                    kb.dma("sp", dst_ap, hs[s][0:n, :], r=[hs_b[s]])
                off += n
                if nxt is not None and kk < len(groups[nxt]):
                    norm_part(nxt, kk, "B")
            if nxt is not None:
                for kk in range(len(grp), len(groups[nxt])):
                    load_tile(nxt, kk)
                    norm_part(nxt, kk, None)

    def phase_ssd(self, es, j, li, src, dst):
        from functools import partial
        kb, nc = self.kb, self.nc
        V, A, G, T = nc.vector, nc.scalar, nc.gpsimd, nc.tensor
        w_in = self.S(es, "s_win", [128, 8, SSD_IN], BF16)
        w_out = self.S(es, "s_wout", [128, 16, D], BF16)
        par_b = Buf()
        wdt_b = Buf()
        wx_b = [Buf() for _ in range(8)]
        wz_b = [Buf() for _ in range(4)]
        wout_b = [Buf() for _ in range(4)]
        wv = self.w_ssd_in[j].rearrange("(c p) f -> p c f", p=128)
        kb.dma("pool", w_in[:, :, 6144:6176], wv[:, :, 6144:6176], w=[wdt_b])
        for i in range(8):
            kb.dma("pool", w_in[:, :, 2048 + 512 * i:2560 + 512 * i], wv[:, :, 2048 + 512 * i:2560 + 512 * i], w=[wx_b[i]])
        for i in range(4):
            kb.dma("pool", w_in[:, :, 512 * i:512 * (i + 1)], wv[:, :, 512 * i:512 * (i + 1)], w=[wz_b[i]])
        wov = self.w_ssd_out[j].rearrange("(c p) f -> p c f", p=128)
        for i in range(4):
            kb.dma("pool", w_out[:, 4 * i:4 * i + 4, :], wov[:, 4 * i:4 * i + 4, :], w=[wout_b[i]])
        cw = self.S(es, "s_cw", [128, 32, 4], F32)
        cb = self.S(es, "s_cb", [128, 32], F32)
        dtb = self.S(es, "s_dtb", [128, 32], F32)
        arep = self.S(es, "s_arep", [128, 32], F32)
        dsk = self.S(es, "s_dsk", [128, 32], F32)
        sng = self.S(es, "s_sng", [128, 16], F32)
        kb.dma("sp", cw[:], self.cw_d[j], wp=[par_b])
        kb.dma("sp", cb[:], self.cb_d[j], wp=[par_b])
        kb.dma("sp", dtb[:], self.dtb_d[j], wp=[par_b])
        kb.dma("sp", arep[:], self.alog_d[j], wp=[par_b])
        kb.dma("sp", dsk[:], self.dsk_d[j], wp=[par_b])
        kb.dma("sp", sng[:], self.sng_d[j], wp=[par_b])
        arep_b = Buf()
        kb.op(kb.act, lambda: A.activation(out=arep[:], in_=arep[:], func=AF.Exp), r=[par_b], w=[arep_b])
        kb.op(kb.dve, lambda: V.tensor_scalar(arep[:], arep[:], -1.0, None, ALU.mult), r=[arep_b], w=[arep_b])
        cwb = Buf()
        kb.op(kb.dve, lambda: V.tensor_scalar(cw[:], cw[:], 0.5, None, ALU.mult), r=[par_b], w=[cwb])
        kb.op(kb.dve, lambda: V.tensor_scalar(cb[:], cb[:], 0.5, None, ALU.mult), r=[par_b], wp=[cwb])
        hs = [self.S(es, "s_h%d" % i, [128, D], F32) for i in range(2)]
        hs_b = [Buf(), Buf()]
        tpF = self.P(es, "s_tpF", [128, 512], F32)
        pzs = [self.P(es, "s_pz%d" % i, [128, 512], F32) for i in range(2)]
        fm = self.P(es, "s_fm", [128, 512], F32)
        tpB = self.P(es, "s_tpB", [128, 512], F32)
        dTb = self.P(es, "s_dT", [128, 512], F32)
        bm = self.P(es, "s_bm", [128, 512], F32)
        yb = self.P(es, "s_y", [128, 512], F32)
        pz_b = [PB(), PB()]
        fm_b, tpB_b, dT_b, bm_b, y_b = PB(), PB(), PB(), PB(), PB()
        gT = self.S(es, "s_gT", [128, 16, 128], BF16)
        gT_b = Buf()
        _ss = self.S(es, "s_nss", [128, 4], F32)
        _xn = gT[:, 0:8, :].rearrange("p c t -> p (c t)")
        _tp = tpF[:].bitcast(BF16)[:, 0:1024].rearrange("p (c t) -> p c t", c=8)
        nsc = (_xn, gT_b, _ss, Buf(), _xn, gT_b, _tp, PB())
        tpB16 = tpB[:].bitcast(BF16)
        tpBg = tpB16.rearrange("p (c t) -> p c t", c=8)
        xnT = [self.S(es, "s_xnT%d" % i, [128, 8, 131], BF16) for i in range(2)]
        xnT_b = [Buf(), Buf()]
        for i in range(2):
            kb.op(kb.pool, lambda: G.memset(xnT[i][:], 0.0), w=[xnT_b[i]])
        xbc_xs = self.S(es, "s_xbcxs", [128, 16, 128], BF16)
        xsT_b = [Buf() for _ in range(16)]
        xbc_bc = [self.S(es, "s_xbcbc%d" % i, [128, 16, 128], BF16) for i in range(2)]
        bc_b = [[Buf() for _ in range(16)] for _ in range(2)]
        NU, NA = 4, 7
        u = [self.S(es, "s_u%d" % i, [128, 131], F32) for i in range(NU)]
        u_b = [Buf() for _ in range(NU)]
        acc = [self.S(es, "s_acc%d" % i, [128, 128], F32) for i in range(NA)]
        acc_b = [Buf() for _ in range(NA)]
        th = [self.S(es, "s_th%d" % i, [128, 128], F32) for i in range(2)]
        th_b = [Buf(), Buf()]
        xs_tm = self.S(es, "s_xstm", [128, 2048], BF16)
        xs_b = [Buf() for _ in range(8)]
        B_tm = self.S(es, "s_Btm", [128, 1024], BF16)
        Btm_b = Buf()
        smF = self.S(es, "s_smF", [128, 4, 32], F32)
        EXPT, ACS, TMP, DTE = range(4)
        smF_b = [Buf() for _ in range(4)]
        smB = [self.S(es, "s_smB%d" % i, [128, 5, 32], F32) for i in range(2)]
        DTV, ADT, DFS, CD, W2 = range(5)
        smB_b = [[Buf() for _ in range(5)] for _ in range(2)]
        rhsD = [self.S(es, "s_rhsD%d" % i, [128, 512], F32) for i in range(2)]
        rhsD_b = [Buf(), Buf()]
        Ee = [self.S(es, "s_E%d" % i, [128, 512], F32) for i in range(2)]
        E_b = [Buf(), Buf()]
        CBm = [self.S(es, "s_CBm0", [128, 128], F32)] * 2
        _cb = Buf()
        CBm_b = [_cb, _cb]
        MT = [self.S(es, "s_MT%d" % i, [128, 512], BF16) for i in range(2)]
        MT_b = [Buf(), Buf()]
        xdt = [self.S(es, "s_xdt%d" % i, [128, 256], BF16) for i in range(2)]
        xdt_b = [Buf(), Buf()]
        xdtd = [self.S(es, "s_xdtd%d" % i, [128, 256], BF16) for i in range(2)]
        xdtd_b = [Buf(), Buf()]
        tt = [[self.S(es, "s_t%d_%d" % (k, i), [128, 256], F32) for i in range(2)] for k in range(4)]
        tt_b = [[Buf(), Buf()] for _ in range(4)]
        tt.append(tt[0])
        tt_b.append(tt_b[0])
        stg = self.S(es, "s_stg", [128, 8, 4], F32)
        stg_b = [Buf() for _ in range(8)]
        S32 = self.S(es, "s_S32", [128, 2048], F32)
        Sbf = self.S(es, "s_Sbf", [128, 2048], BF16)
        S32_b = [Buf() for _ in range(8)]
        Sbf_b = [Buf() for _ in range(8)]
        stmp = [self.S(es, "s_stmp0", [128, 256], F32)] * 2
        _sb = Buf()
        stmp_b = [_sb, _sb]
        kb.op(kb.pool, lambda: G.memset(S32[:], 0.0), w=S32_b)
        kb.op(kb.pool, lambda: G.memset(Sbf[:], 0.0), w=Sbf_b)
        v3 = lambda ap: ap.rearrange("p (j l) -> p j l", j=4)
        NTL = len(TILES)

        def f_load(ti):
            t0, n = TILES[ti]
            cur, prev = ti % 2, (ti + 1) % 2
            nprev = TILES[ti - 1][1] if ti > 0 else 0
            X = xnT[cur]
            kb.dma("sp", hs[cur][0:n, :], self.h_src(src, t0, n), w=[hs_b[cur]])
            self.norm_T(hs_b[cur], hs[cur][0:n, :], n, self.lnT[:, li, :], X[:, :, 3:3 + n], xnT_b[cur], nsc, newton=True)
            kb.op(kb.pool, lambda: G.tensor_copy(X[:, :, 0:3], xnT[prev][:, :, nprev:nprev + 3]), r=[xnT_b[prev]], wp=[xnT_b[cur]])

        def f_dt1(ti):
            t0, n = TILES[ti]
            cur = ti % 2
            X = xnT[cur]
            sB, sBb = smB[cur], smB_b[cur]
            pdt = fm[0:n, 0:32]
            for c in range(8):
                kb.op(kb.pe, lambda: T.matmul(pdt, X[:, c, 3:3 + n], w_in[:, c, 6144:6176], start=(c == 0), stop=(c == 7)),
                      r=[wdt_b, xnT_b[cur]], w=[fm_b] if c == 0 else [], wp=[fm_b] if c else [], inc=(c == 7))
            kb.op(kb.dve, lambda: V.tensor_tensor(sB[0:n, DTV, :], pdt, dtb[0:n, :], ALU.add), r=[fm_b, par_b], w=[sBb[DTV]])
            kb.op(kb.act, lambda: A.activation(out=smF[0:n, EXPT, :], in_=sB[0:n, DTV, :], func=AF.Exp), r=[sBb[DTV]], w=[smF_b[EXPT]])
            kb.op(kb.act, lambda: A.activation(out=sB[0:n, DTV, :], in_=smF[0:n, EXPT, :], func=AF.Ln, bias=1.0), r=[smF_b[EXPT]], w=[sBb[DTV]])
            kb.op(kb.dve, lambda: V.tensor_tensor(sB[0:n, ADT, :], sB[0:n, DTV, :], arep[0:n, :], ALU.mult), r=[sBb[DTV], arep_b], w=[sBb[ADT]])

        def f_dt2(ti):
            t0, n = TILES[ti]
            cur = ti % 2
            sB, sBb = smB[cur], smB_b[cur]
            pacs, ptot = fm[0:n, 32:64], fm[:, 64:96]
            kb.op(kb.pe, lambda: T.matmul(pacs, self.tri32[0:n, 0:n], sB[0:n, ADT, :], start=True, stop=True), r=[sBb[ADT], self.cb_], w=[fm_b])
            kb.op(kb.pe, lambda: T.matmul(ptot, self.ones32[0:n, :], sB[0:n, ADT, :], start=True, stop=True), r=[sBb[ADT], self.cb_], wp=[fm_b])
            kb.op(kb.dve, lambda: V.tensor_copy(smF[0:n, ACS, :], pacs), r=[fm_b], w=[smF_b[ACS]])
            kb.op(kb.act, lambda: A.activation(out=sB[0:n, DFS, :], in_=pacs, func=AF.Exp), r=[fm_b], w=[sBb[DFS]])
            kb.op(kb.act, lambda: A.activation(out=sB[:, CD, :], in_=ptot, func=AF.Exp), r=[fm_b], w=[sBb[CD]])
            kb.op(kb.dve, lambda: V.tensor_tensor(smF[0:n, TMP, :], ptot[0:n, :], smF[0:n, ACS, :], ALU.subtract), r=[fm_b, smF_b[ACS]], w=[smF_b[TMP]])
            kb.op(kb.act, lambda: A.activation(out=smF[0:n, DTE, :], in_=smF[0:n, TMP, :], func=AF.Exp), r=[smF_b[TMP]], w=[smF_b[DTE]])
            kb.op(kb.dve, lambda: V.tensor_tensor(sB[0:n, W2, :], sB[0:n, DTV, :], smF[0:n, DTE, :], ALU.mult), r=[sBb[DTV], smF_b[DTE]], w=[sBb[W2]])

        def conv_dst(ti, cc, n):
            cur = ti % 2
            if cc < 16:
                return xbc_xs[:, cc, 0:n], xsT_b[cc]
            return xbc_bc[cur][:, cc - 16, 0:n], bc_b[cur][cc - 16]

        def f_conv(ti, s):
            t0, n = TILES[ti]
            cur = ti % 2
            X = xnT[cur]
            cc = s
            if 0 <= cc < 32:
                p = cc % 2
                pz = pzs[p][:, 0:3 + n]
                for c in range(8):
                    kb.op(kb.pe, lambda: T.matmul(pz, w_in[:, c, 2048 + cc * 128:2048 + (cc + 1) * 128], X[:, c, 0:3 + n], start=(c == 0), stop=(c == 7)),
                          r=[wx_b[cc // 4], xnT_b[cur]], w=[pz_b[p]] if c == 0 else [], wp=[pz_b[p]] if c else [], inc=(c == 7))
            cc = s - 1
            if 0 <= cc < 32:
                p = cc % 2
                pz = pzs[p][:, 0:3 + n]
                kb.op(kb.act, lambda: A.activation(out=acc[cc % NA][:, 0:n], in_=pz[:, 3:3 + n], func=AF.Identity, bias=cb[:, cc:cc + 1], scale=cw[:, cc, 3:4]),
                      r=[pz_b[p], cwb], w=[acc_b[cc % NA]])
                kb.op(kb.act, lambda: A.copy(u[cc % NU][:, 0:3 + n], pz), r=[pz_b[p]], w=[u_b[cc % NU]])
            for k in range(3):
                cc = s - 2 - k
                if 0 <= cc < 32:
                    p = cc % NU
                    a_ = acc[cc % NA][:, 0:n]
                    kb.op(kb.dve, lambda: V.scalar_tensor_tensor(a_, u[p][:, k:k + n], cw[:, cc, k:k + 1], a_, ALU.mult, ALU.add),
                          r=[u_b[p], cwb], w=[acc_b[cc % NA]])
            cc = s - 5
            if 0 <= cc < 32:
                kb.op(kb.act, lambda: A.activation(out=th[cc % 2][:, 0:n], in_=acc[cc % NA][:, 0:n], func=AF.Tanh), r=[acc_b[cc % NA]], w=[th_b[cc % 2]])
            cc = s - 6
            if 0 <= cc < 32:
                dst_ap, dst_b = conv_dst(ti, cc, n)
                kb.op(kb.dve, lambda: V.scalar_tensor_tensor(dst_ap, th[cc % 2][:, 0:n], 1.0, acc[cc % NA][:, 0:n], ALU.add, ALU.mult),
                      r=[th_b[cc % 2], acc_b[cc % NA]], w=[dst_b])

        def front(ti):
            st = [partial(f_conv, ti, 0), partial(f_dt1, ti), partial(f_conv, ti, 1), partial(f_dt2, ti)]
            st += [partial(f_conv, ti, s) for s in range(2, 38)]
            return st

        def b_trx(ti, half):
            t0, n = TILES[ti]
            for k in range(8):
                cc = half * 8 + k
                kb.op(kb.pe, lambda: T.transpose(tpB16[0:n, k * 128:(k + 1) * 128], xbc_xs[:, cc, 0:n], self.ident[:, :]),
                      r=[xsT_b[cc], self.cb_], w=[tpB_b] if k == 0 else [], wp=[tpB_b] if k else [], inc=(k == 7))
            kb.op(kb.act, lambda: A.copy(xs_tm[0:n, half * 1024:(half + 1) * 1024], tpB16[0:n, :]), r=[tpB_b], w=xs_b[4 * half:4 * half + 4])

        def b_trB(ti):
            t0, n = TILES[ti]
            cur = ti % 2
            for g in range(8):
                kb.op(kb.pe, lambda: T.transpose(tpB16[0:n, g * 128:(g + 1) * 128], xbc_bc[cur][:, g, 0:n], self.ident[:, :]),
                      r=[bc_b[cur][g], self.cb_], w=[tpB_b] if g == 0 else [], wp=[tpB_b] if g else [], inc=(g == 7))
            kb.op(kb.dve, lambda: V.tensor_copy(B_tm[0:n, :], tpB16[0:n, :]), r=[tpB_b], w=[Btm_b])

        def gchain(ti, g):
            t0, n = TILES[ti]
            cur = ti % 2
            X = xnT[cur]
            sB, sBb = smB[cur], smB_b[cur]
            b2 = g % 2
            hsl = slice(4 * g, 4 * g + 4)
            gsl = slice(g * 256, (g + 1) * 256)
            BT, CT = xbc_bc[cur][:, g, 0:n], xbc_bc[cur][:, 8 + g, 0:n]
            BTb, CTb = bc_b[cur][g], bc_b[cur][8 + g]
            x3 = lambda ap: ap.rearrange("p (j f) -> p j f", j=4)
            rD = rhsD[b2][0:n, 0:4 * n]
            Ev = Ee[b2][0:n, 0:4 * n]
            MTv = MT[b2][0:n, 0:4 * n]
            pcb = bm[0:n, 0:n]
            pst = bm[:, 256:512]
            zq = fm[0:n, 256:512]
            xs3 = x3(xs_tm[0:n, gsl])
            t1, t2, t3, szg, thg = [tt[k][b2][0:n, :] for k in range(5)]
            t1b, t2b, t3b, szb, thb = [tt_b[k][b2] for k in range(5)]
            steps = []

            def s1():
                kb.op(kb.pool, lambda: G.tensor_tensor(v3(rD), bc(sB[0:n, ADT, hsl], [n, 4, n], 2), bc(self.tri32[0:n, 0:n], [n, 4, n], 1), ALU.mult),
                      r=[sBb[ADT], self.cb_], w=[rhsD_b[b2]])
                kb.op(kb.pool, lambda: G.tensor_tensor(x3(xdt[b2][0:n, :]), xs3, bc(sB[0:n, DTV, hsl], [n, 4, 64], 2), ALU.mult),
                      r=[xs_b[g], sBb[DTV]], w=[xdt_b[b2]])
            steps.append(s1)

            def s2():
                kb.op(kb.pe, lambda: T.matmul(dTb[0:n, 0:4 * n], self.ltri32[0:n, 0:n], rD, start=True, stop=True), r=[rhsD_b[b2], self.cb_], w=[dT_b])
                kb.op(kb.act, lambda: A.activation(out=Ev, in_=dTb[0:n, 0:4 * n], func=AF.Exp), r=[dT_b], w=[E_b[b2]])
                kb.op(kb.pool, lambda: G.tensor_tensor(x3(xdtd[b2][0:n, :]), xs3, bc(sB[0:n, W2, hsl], [n, 4, 64], 2), ALU.mult),
                      r=[xs_b[g], sBb[W2]], w=[xdtd_b[b2]])
            steps.append(s2)

            def s3():
                kb.op(kb.pe, lambda: T.matmul(pcb, BT, CT, start=True, stop=True), r=[BTb, CTb], w=[bm_b])
                kb.op(kb.dve, lambda: V.tensor_tensor(CBm[b2][0:n, 0:n], pcb, self.tri32[0:n, 0:n], ALU.mult), r=[bm_b, self.cb_], w=[CBm_b[b2]])
                for c in range(8):
                    kb.op(kb.pe, lambda: T.matmul(zq, X[:, c, 3:3 + n], w_in[:, c, g * 256:(g + 1) * 256], start=(c == 0), stop=(c == 7)),
                          r=[wz_b[g // 2], xnT_b[cur]], w=[fm_b] if c == 0 else [], wp=[fm_b] if c else [], inc=(c == 7))
                kb.op(kb.act, lambda: A.activation(out=thg, in_=zq, func=AF.Tanh, scale=0.5), r=[fm_b], w=[thb])
                kb.op(kb.dve, lambda: V.scalar_tensor_tensor(szg, thg, 1.0, zq, ALU.add, ALU.mult), r=[thb, fm_b], w=[szb])
                kb.op(kb.pool, lambda: G.tensor_tensor(v3(MTv), v3(Ev), bc(CBm[b2][0:n, 0:n], [n, 4, n], 1), ALU.mult),
                      r=[E_b[b2], CBm_b[b2]], w=[MT_b[b2]])
            steps.append(s3)

            def s4():
                for jj in range(4):
                    kb.op(kb.pe, lambda: T.matmul(yb[0:n, jj * 64:(jj + 1) * 64], MTv[:, jj * n:(jj + 1) * n], xdt[b2][0:n, jj * 64:(jj + 1) * 64], start=True, stop=True),
                          r=[MT_b[b2], xdt_b[b2]], w=[y_b] if jj == 0 else [], wp=[y_b] if jj else [], inc=False)
                kb.op(kb.pe, lambda: T.matmul(yb[0:n, 256:512], CT, Sbf[:, gsl], start=True, stop=True), r=[CTb, Sbf_b[g]], wp=[y_b])
                kb.op(kb.pool, lambda: G.tensor_tensor(x3(t3), xs3, bc(dsk[0:n, hsl], [n, 4, 64], 2), ALU.mult), r=[xs_b[g], par_b], w=[t3b])
                kb.op(kb.dve, lambda: V.tensor_tensor(x3(t1), x3(yb[0:n, 256:512]), bc(sB[0:n, DFS, hsl], [n, 4, 64], 2), ALU.mult),
                      r=[y_b, sBb[DFS]], w=[t1b])
                kb.op(kb.dve, lambda: V.tensor_tensor(t2, yb[0:n, 0:256], t1, ALU.add), r=[y_b, t1b], w=[t2b])
            steps.append(s4)

            def s5():
                kb.op(kb.pe, lambda: T.matmul(pst, B_tm[0:n, g * 128:(g + 1) * 128], xdtd[b2][0:n, :], start=True, stop=True),
                      r=[Btm_b, xdtd_b[b2]], w=[bm_b])
                kb.op(kb.pool, lambda: G.tensor_tensor(x3(stmp[b2][:, :]), x3(S32[:, gsl]), bc(sB[:, CD, hsl], [128, 4, 64], 2), ALU.mult),
                      r=[S32_b[g], sBb[CD]], w=[stmp_b[b2]])
                kb.op(kb.dve, lambda: V.tensor_tensor(S32[:, gsl], stmp[b2][:, :], pst, ALU.add), r=[stmp_b[b2], bm_b], w=[S32_b[g]])
                kb.op(kb.act, lambda: A.copy(Sbf[:, gsl], S32[:, gsl]), r=[S32_b[g]], w=[Sbf_b[g]])
            steps.append(s5)

            def s6():
                kb.op(kb.pool, lambda: G.tensor_tensor(t2, t2, t3, ALU.add), r=[t3b], w=[t2b])
                kb.op(kb.pool, lambda: G.tensor_tensor(t2, t2, szg, ALU.mult), r=[szb], w=[t2b])
            steps.append(s6)

            sq = lambda: kb.op(kb.act, lambda: A.activation(out=t1, in_=t2, func=AF.Square, accum_out=stg[0:n, g, 0:1]), r=[t2b], w=[t1b, stg_b[g]])
            newt = self.rstd_newton_ops(stg[:, g, :], stg_b[g], n, 1.0 / 256, 4.0 * EPS)
            gnf = lambda: kb.op(kb.dve, lambda: V.tensor_scalar(xs_tm[0:n, gsl], t2, stg[0:n, g, 2:3], None, ALU.mult), r=[t2b, stg_b[g]], w=[xs_b[g]])
            return steps, (sq, newt, gnf)

        def b_gT(ti, half):
            t0, n = TILES[ti]
            for k in range(8):
                cc = half * 8 + k
                kb.op(kb.pe, lambda: T.transpose(tpBg[:, k, 0:n], xs_tm[0:n, cc * 128:(cc + 1) * 128], self.ident[0:n, 0:n]),
                      r=[xs_b[cc // 2], self.cb_], w=[tpB_b] if k == 0 else [], wp=[tpB_b] if k else [], inc=(k == 7))
            kb.op(kb.dve, lambda: V.tensor_tensor(gT[:, half * 8:(half + 1) * 8, 0:n], tpBg[:, :, 0:n], bc(sng[:, half * 8:(half + 1) * 8], [128, 8, n], 2), ALU.mult),
                  r=[tpB_b, par_b], w=[gT_b] if half == 0 else [], wp=[gT_b] if half else [])

        def b_out(ti, hh):
            t0, n = TILES[ti]
            cur = ti % 2
            po = dTb[0:n, :] if hh == 0 else yb[0:n, :]
            pbuf = dT_b if hh == 0 else y_b
            for cc in range(16):
                kb.op(kb.pe, lambda: T.matmul(po, gT[:, cc, 0:n], w_out[:, cc, hh * 512:(hh + 1) * 512], start=(cc == 0), stop=(cc == 15)),
                      r=[gT_b, wout_b[cc // 4]], w=[pbuf] if cc == 0 else [], wp=[pbuf] if cc else [], inc=(cc == 15))
            kb.op(kb.dve, lambda: V.tensor_tensor(hs[cur][0:n, hh * 512:(hh + 1) * 512], po, hs[cur][0:n, hh * 512:(hh + 1) * 512], ALU.add),
                  r=[pbuf, hs_b[cur]], wp=[hs_b[cur]])
            if hh == 1:
                dap = self.h_dst(dst, t0, n)
                if dap is not None:
                    kb.dma("sp", dap, hs[cur][0:n, :], r=[hs_b[cur]])

        def back(ti):
            st = [partial(b_trx, ti, 0), partial(b_trx, ti, 1), partial(b_trB, ti)]
            for gp in range(4):
                (ca, ta), (cb_, tb) = gchain(ti, 2 * gp), gchain(ti, 2 * gp + 1)
                for a_, b_ in zip(ca, cb_):
                    st += [a_, b_]

                def tail(ta=ta, tb=tb):
                    ta[0]()
                    tb[0]()
                    for fa, fb in zip(ta[1], tb[1]):
                        fa()
                        fb()
                    ta[2]()
                    tb[2]()
                st.append(tail)
            st += [partial(b_gT, ti, 0), partial(b_gT, ti, 1), partial(b_out, ti, 0), partial(b_out, ti, 1)]
            return st

        def interleave(a, b, lead):
            na, nb = len(a), len(b)
            i = jx = 0
            while i < na or jx < nb:
                if jx >= nb or (i < na and i * nb <= jx * na * lead):
                    a[i]()
                    i += 1
                else:
                    b[jx]()
                    jx += 1

        f_load(0)
        for ti in range(NTL + 1):
            f = front(ti) if ti < NTL else []
            b = back(ti - 1) if ti >= 1 else []
            if ti + 1 < NTL:
                b = b + [partial(f_load, ti + 1)]
            interleave(f, b, SSD_LEAD)

    def phase_ssd_v1(self, es, j, li, src, dst):
        kb, nc = self.kb, self.nc
        V, A, G, T = nc.vector, nc.scalar, nc.gpsimd, nc.tensor
        w_in = self.S(es, "s_win", [128, 8, SSD_IN], BF16)
        w_out = self.S(es, "s_wout", [128, 16, D], BF16)
        win_b, wout_b, par_b = Buf(), Buf(), Buf()
        wv = self.w_ssd_in[j].rearrange("(c p) f -> p c f", p=128)
        for c in range(8):
            kb.dma("pool", w_in[:, c, :], wv[:, c, :], wp=[win_b])
        wov = self.w_ssd_out[j].rearrange("(c p) f -> p c f", p=128)
        for c in range(0, 16, 4):
            kb.dma("pool", w_out[:, c:c + 4, :], wov[:, c:c + 4, :], wp=[wout_b])
        cw = self.S(es, "s_cw", [128, 32, 4], F32)
        cb = self.S(es, "s_cb", [128, 32], F32)
        dtb = self.S(es, "s_dtb", [128, 32], F32)
        arep = self.S(es, "s_arep", [128, 32], F32)
        dsk = self.S(es, "s_dsk", [128, 32], F32)
        sng = self.S(es, "s_sng", [128, 16], F32)
        kb.dma("sp", cw[:], self.cw_d[j], wp=[par_b])
        kb.dma("sp", cb[:], self.cb_d[j], wp=[par_b])
        kb.dma("sp", dtb[:], self.dtb_d[j], wp=[par_b])
        kb.dma("sp", arep[:], self.alog_d[j], wp=[par_b])
        kb.dma("sp", dsk[:], self.dsk_d[j], wp=[par_b])
        kb.dma("sp", sng[:], self.sng_d[j], wp=[par_b])
        arep_b = Buf()
        kb.op(kb.act, lambda: A.activation(out=arep[:], in_=arep[:], func=AF.Exp), r=[par_b], w=[arep_b])
        kb.op(kb.dve, lambda: V.tensor_scalar(arep[:], arep[:], -1.0, None, ALU.mult), r=[arep_b], w=[arep_b])
        hs = [self.S(es, "s_h%d" % i, [128, D], F32) for i in range(2)]
        hs_b = [Buf(), Buf()]
        tp2 = self.P(es, "s_tp2", [128, 1024], F32)
        pzbs = [self.P(es, "s_pz%d" % i, [128, 512], F32) for i in range(2)]
        zqb = self.P(es, "s_zq", [128, 512], F32)
        dTb = self.P(es, "s_dT", [128, 512], F32)
        miscb = self.P(es, "s_misc", [128, 512], F32)
        yb = self.P(es, "s_y", [128, 512], F32)
        pz_b = [PB(), PB()]
        _z = PB()
        zq_b = [_z, _z]
        dT_b = PB()
        misc_b = PB()
        pdt_b = pacs_b = ptot_b = pst_b = misc_b
        pcb_b = [misc_b, misc_b]
        ydg_b = yof_b = PB()
        nsc = self.norm_scratch(es, "s", tp2)
        tp_b = nsc[7]
        tp16 = tp2[:].bitcast(BF16)
        tpg = tp16.rearrange("p (c t) -> p c t", c=16)
        xnT = [self.S(es, "s_xnT%d" % i, [128, 8, 131], BF16) for i in range(2)]
        xnT_b = [Buf(), Buf()]
        for i in range(2):
            kb.op(kb.pool, lambda: G.memset(xnT[i][:], 0.0), w=[xnT_b[i]])
        xbcT = self.S(es, "s_xbcT", [128, 32, 128], BF16)
        xbc_b = [Buf() for _ in range(32)]
        acc = [self.S(es, "s_acc%d" % i, [128, 128], F32) for i in range(3)]
        acc_b = [Buf() for _ in range(3)]
        xs_tm = self.S(es, "s_xstm", [128, 2048], BF16)
        xs_b = Buf()
        B_tm = self.S(es, "s_Btm", [128, 1024], BF16)
        Btm_b = Buf()
        sz = self.S(es, "s_sz", [128, 2048], F32)
        sz_b = [Buf() for _ in range(8)]
        sm = self.S(es, "s_sm", [128, 10, 32], F32)
        DTV, EXPT, ADT, ACS, DFS, CD, DTE, TMP, W2 = range(9)
        sm_b = [Buf() for _ in range(10)]
        rhsD = [self.S(es, "s_rhsD0", [128, 512], F32)] * 2
        _b = Buf()
        rhsD_b = [_b, _b]
        Ee = [self.S(es, "s_E0", [128, 512], F32)] * 2
        _b = Buf()
        E_b = [_b, _b]
        CBm = [self.S(es, "s_CBm%d" % i, [128, 128], F32) for i in range(2)]
        CBm_b = [Buf(), Buf()]
        MT = [self.S(es, "s_MT%d" % i, [128, 512], BF16) for i in range(2)]
        MT_b = [Buf(), Buf()]
        xdt = [self.S(es, "s_xdt%d" % i, [128, 256], BF16) for i in range(2)]
        xdt_b = [Buf(), Buf()]
        xdtd = [self.S(es, "s_xdtd%d" % i, [128, 256], BF16) for i in range(2)]
        xdtd_b = [Buf(), Buf()]
        tt = [[self.S(es, "s_t%d_%d" % (k, i), [128, 256], F32) for i in range(2)] for k in range(3)]
        tt_b = [[Buf(), Buf()] for _ in range(3)]
        tt.append(tt[0])
        tt_b.append(tt_b[0])
        stg = self.S(es, "s_stg", [128, 8, 4], F32)
        stg_b = [Buf() for _ in range(8)]
        gn = self.S(es, "s_gn", [128, 2048], BF16)
        gn_b = Buf()
        gT = self.S(es, "s_gT", [128, 16, 128], BF16)
        gT_b = Buf()
        S32 = self.S(es, "s_S32", [128, 2048], F32)
        Sbf = self.S(es, "s_Sbf", [128, 2048], BF16)
        S32_b = [Buf() for _ in range(8)]
        Sbf_b = [Buf() for _ in range(8)]
        stmp = [self.S(es, "s_stmp0", [128, 256], F32)] * 2
        _b = Buf()
        stmp_b = [_b, _b]
        kb.op(kb.pool, lambda: G.memset(S32[:], 0.0), w=S32_b)
        kb.op(kb.pool, lambda: G.memset(Sbf[:], 0.0), w=Sbf_b)
        nprev = 0
        ipz = 0
        for ti, (t0, n) in enumerate(TILES):
            cur, prev = ti % 2, (ti + 1) % 2
            X = xnT[cur]
            kb.dma("sp", hs[cur][0:n, :], self.h_src(src, t0, n), w=[hs_b[cur]])
            self.norm_T(hs_b[cur], hs[cur][0:n, :], n, self.lnT[:, li, :], X[:, :, 3:3 + n], xnT_b[cur], nsc)
            kb.op(kb.pool, lambda: G.tensor_copy(X[:, :, 0:3], xnT[prev][:, :, nprev:nprev + 3]), r=[xnT_b[prev]], wp=[xnT_b[cur]])
            nprev = n
            for cc in range(32):
                p = ipz % 2
                ipz += 1
                pz = pzbs[p][:, 0:3 + n]
                for c in range(8):
                    kb.op(kb.pe, lambda: T.matmul(pz, w_in[:, c, 2048 + cc * 128:2048 + (cc + 1) * 128], X[:, c, 0:3 + n], start=(c == 0), stop=(c == 7)),
                          r=[win_b, xnT_b[cur]], w=[pz_b[p]] if c == 0 else [], wp=[pz_b[p]] if c else [], inc=(c == 7))
                a_ = acc[p][:, 0:n]
                kb.op(kb.act, lambda: A.activation(out=a_, in_=pz[:, 3:3 + n], func=AF.Identity, bias=cb[:, cc:cc + 1], scale=cw[:, cc, 3:4]),
                      r=[pz_b[p], par_b], w=[acc_b[p]])
                for k in range(3):
                    kb.op(kb.dve, lambda: V.scalar_tensor_tensor(a_, pz[:, k:k + n], cw[:, cc, k:k + 1], a_, ALU.mult, ALU.add),
                          r=[pz_b[p], par_b], w=[acc_b[p]])
                kb.op(kb.act, lambda: A.activation(out=xbcT[:, cc, 0:n], in_=a_, func=AF.Silu), r=[acc_b[p]], w=[xbc_b[cc]])
            for g in range(8):
                zs = g % 2
                zq = zqb[0:n, zs * 256:(zs + 1) * 256]
                for c in range(8):
                    kb.op(kb.pe, lambda: T.matmul(zq, X[:, c, 3:3 + n], w_in[:, c, g * 256:(g + 1) * 256], start=(c == 0), stop=(c == 7)),
                          r=[win_b, xnT_b[cur]], w=[zq_b[zs]] if c == 0 else [], wp=[zq_b[zs]] if c else [], inc=(c == 7))
                kb.op(kb.act, lambda: A.activation(out=sz[0:n, g * 256:(g + 1) * 256], in_=zq, func=AF.Silu), r=[zq_b[zs]], w=[sz_b[g]])
            for cc in range(16):
                kb.op(kb.pe, lambda: T.transpose(tp16[0:n, cc * 128:(cc + 1) * 128], xbcT[:, cc, 0:n], self.ident[:, :]),
                      r=[xbc_b[cc], self.cb_], w=[tp_b] if cc == 0 else [], wp=[tp_b] if cc else [], inc=(cc == 15))
            kb.op(kb.act, lambda: A.copy(xs_tm[0:n, 0:1024], tp16[0:n, 0:1024]), r=[tp_b], w=[xs_b])
            kb.op(kb.dve, lambda: V.tensor_copy(xs_tm[0:n, 1024:2048], tp16[0:n, 1024:2048]), r=[tp_b], wp=[xs_b])
            for g in range(8):
                kb.op(kb.pe, lambda: T.transpose(tp16[0:n, g * 128:(g + 1) * 128], xbcT[:, 16 + g, 0:n], self.ident[:, :]),
                      r=[xbc_b[16 + g], self.cb_], w=[tp_b] if g == 0 else [], wp=[tp_b] if g else [], inc=(g == 7))
            kb.op(kb.act, lambda: A.copy(B_tm[0:n, :], tp16[0:n, 0:1024]), r=[tp_b], w=[Btm_b])
            pdt, pacs, ptot = miscb[0:n, 0:32], miscb[0:n, 32:64], miscb[:, 64:96]
            for c in range(8):
                kb.op(kb.pe, lambda: T.matmul(pdt, X[:, c, 3:3 + n], w_in[:, c, 6144:6176], start=(c == 0), stop=(c == 7)),
                      r=[win_b, xnT_b[cur]], w=[pdt_b] if c == 0 else [], wp=[pdt_b] if c else [], inc=(c == 7))
            smv = lambda k: sm[0:n, k, :]
            kb.op(kb.dve, lambda: V.tensor_tensor(smv(DTV), pdt, dtb[0:n, :], ALU.add), r=[pdt_b, par_b], w=[sm_b[DTV]])
            kb.op(kb.act, lambda: A.activation(out=smv(EXPT), in_=smv(DTV), func=AF.Exp), r=[sm_b[DTV]], w=[sm_b[EXPT]])
            kb.op(kb.act, lambda: A.activation(out=smv(DTV), in_=smv(EXPT), func=AF.Ln, bias=1.0), r=[sm_b[EXPT]], w=[sm_b[DTV]])
            kb.op(kb.dve, lambda: V.tensor_tensor(smv(ADT), smv(DTV), arep[0:n, :], ALU.mult), r=[sm_b[DTV], arep_b], w=[sm_b[ADT]])
            kb.op(kb.pe, lambda: T.matmul(pacs, self.tri32[0:n, 0:n], smv(ADT), start=True, stop=True), r=[sm_b[ADT], self.cb_], w=[pacs_b])
            kb.op(kb.pe, lambda: T.matmul(ptot, self.ones32[0:n, :], smv(ADT), start=True, stop=True), r=[sm_b[ADT], self.cb_], w=[ptot_b])
            kb.op(kb.dve, lambda: V.tensor_copy(smv(ACS), pacs), r=[pacs_b], w=[sm_b[ACS]])
            kb.op(kb.act, lambda: A.activation(out=smv(DFS), in_=pacs, func=AF.Exp), r=[pacs_b], w=[sm_b[DFS]])
            kb.op(kb.act, lambda: A.activation(out=sm[:, CD, :], in_=ptot, func=AF.Exp), r=[ptot_b], w=[sm_b[CD]])
            kb.op(kb.dve, lambda: V.tensor_tensor(smv(TMP), ptot[0:n, :], smv(ACS), ALU.subtract), r=[ptot_b, sm_b[ACS]], w=[sm_b[TMP]])
            kb.op(kb.act, lambda: A.activation(out=smv(DTE), in_=smv(TMP), func=AF.Exp), r=[sm_b[TMP]], w=[sm_b[DTE]])
            kb.op(kb.dve, lambda: V.tensor_tensor(smv(W2), smv(DTV), smv(DTE), ALU.mult), r=[sm_b[DTV], sm_b[DTE]], w=[sm_b[W2]])
            for g in range(8):
                b2 = g % 2
                hsl = slice(4 * g, 4 * g + 4)
                gsl = slice(g * 256, (g + 1) * 256)
                v3 = lambda ap: ap.rearrange("p (j l) -> p j l", j=4)
                rD = rhsD[b2][0:n, 0:4 * n]
                kb.op(kb.pool, lambda: G.tensor_tensor(v3(rD), bc(sm[0:n, ADT, hsl], [n, 4, n], 2), bc(self.tri32[0:n, 0:n], [n, 4, n], 1), ALU.mult),
                      r=[sm_b[ADT], self.cb_], w=[rhsD_b[b2]])
                kb.op(kb.pe, lambda: T.matmul(dTb[0:n, 0:4 * n], self.ltri32[0:n, 0:n], rD, start=True, stop=True), r=[rhsD_b[b2], self.cb_], w=[dT_b])
                Ev = Ee[b2][0:n, 0:4 * n]
                kb.op(kb.act, lambda: A.activation(out=Ev, in_=dTb[0:n, 0:4 * n], func=AF.Exp), r=[dT_b], w=[E_b[b2]])
                pcb = miscb[0:n, 128:128 + n]
                kb.op(kb.pe, lambda: T.matmul(pcb, xbcT[:, 16 + g, 0:n], xbcT[:, 24 + g, 0:n], start=True, stop=True),
                      r=[xbc_b[16 + g], xbc_b[24 + g]], w=[pcb_b[b2]])
                kb.op(kb.dve, lambda: V.tensor_tensor(CBm[b2][0:n, 0:n], pcb, self.tri32[0:n, 0:n], ALU.mult), r=[pcb_b[b2], self.cb_], w=[CBm_b[b2]])
                MTv = MT[b2][0:n, 0:4 * n]
                kb.op(kb.pool, lambda: G.tensor_tensor(v3(MTv), v3(Ev), bc(CBm[b2][0:n, 0:n], [n, 4, n], 1), ALU.mult),
                      r=[E_b[b2], CBm_b[b2]], w=[MT_b[b2]])
                xs3 = xs_tm[0:n, gsl].rearrange("p (j f) -> p j f", j=4)
                x3 = lambda ap: ap.rearrange("p (j f) -> p j f", j=4)
                kb.op(kb.pool, lambda: G.tensor_tensor(x3(xdt[b2][0:n, :]), xs3, bc(sm[0:n, DTV, hsl], [n, 4, 64], 2), ALU.mult),
                      r=[xs_b, sm_b[DTV]], w=[xdt_b[b2]])
                kb.op(kb.pool, lambda: G.tensor_tensor(x3(xdtd[b2][0:n, :]), xs3, bc(sm[0:n, W2, hsl], [n, 4, 64], 2), ALU.mult),
                      r=[xs_b, sm_b[W2]], w=[xdtd_b[b2]])
                for jj in range(4):
                    kb.op(kb.pe, lambda: T.matmul(yb[0:n, jj * 64:(jj + 1) * 64], MTv[:, jj * n:(jj + 1) * n], xdt[b2][0:n, jj * 64:(jj + 1) * 64], start=True, stop=True),
                          r=[MT_b[b2], xdt_b[b2]], w=[ydg_b] if jj == 0 else [], wp=[ydg_b] if jj else [], inc=(jj == 3))
                kb.op(kb.pe, lambda: T.matmul(yb[0:n, 256:512], xbcT[:, 24 + g, 0:n], Sbf[:, gsl], start=True, stop=True),
                      r=[xbc_b[24 + g], Sbf_b[g]], w=[yof_b])
                t1, t2, t3, tj = [tt[k][b2][0:n, :] for k in range(4)]
                kb.op(kb.dve, lambda: V.tensor_tensor(x3(t1), x3(yb[0:n, 256:512]), bc(sm[0:n, DFS, hsl], [n, 4, 64], 2), ALU.mult),
                      r=[yof_b, sm_b[DFS]], w=[tt_b[0][b2]])
                kb.op(kb.dve, lambda: V.tensor_tensor(t2, yb[0:n, 0:256], t1, ALU.add), r=[ydg_b, tt_b[0][b2]], w=[tt_b[1][b2]])
                kb.op(kb.pool, lambda: G.tensor_tensor(x3(t3), xs3, bc(dsk[0:n, hsl], [n, 4, 64], 2), ALU.mult), r=[xs_b, par_b], w=[tt_b[2][b2]])
                kb.op(kb.pool, lambda: G.tensor_tensor(t2, t2, t3, ALU.add), r=[tt_b[2][b2]], w=[tt_b[1][b2]])
                kb.op(kb.pool, lambda: G.tensor_tensor(t2, t2, sz[0:n, gsl], ALU.mult), r=[sz_b[g]], w=[tt_b[1][b2]])
                kb.op(kb.act, lambda: A.activation(out=tj, in_=t2, func=AF.Square, accum_out=stg[0:n, g, 0:1]), r=[tt_b[1][b2]], w=[tt_b[3][b2], stg_b[g]])
                self.rstd_ops(stg[:, g, :], stg_b[g], n, 0, 1.0 / 256)
                kb.op(kb.dve, lambda: V.tensor_scalar(gn[0:n, gsl], t2, stg[0:n, g, 2:3], None, ALU.mult), r=[tt_b[1][b2], stg_b[g]],
                      w=[gn_b] if g == 0 else [], wp=[gn_b] if g else [])
                kb.op(kb.pe, lambda: T.matmul(miscb[:, 256:512], B_tm[0:n, g * 128:(g + 1) * 128], xdtd[b2][0:n, :], start=True, stop=True),
                      r=[Btm_b, xdtd_b[b2]], w=[pst_b])
                kb.op(kb.pool, lambda: G.tensor_tensor(x3(stmp[b2][:, :]), x3(S32[:, gsl]), bc(sm[:, CD, hsl], [128, 4, 64], 2), ALU.mult),
                      r=[S32_b[g], sm_b[CD]], w=[stmp_b[b2]])
                kb.op(kb.dve, lambda: V.tensor_tensor(S32[:, gsl], stmp[b2][:, :], miscb[:, 256:512], ALU.add), r=[stmp_b[b2], pst_b], w=[S32_b[g]])
                kb.op(kb.act, lambda: A.copy(Sbf[:, gsl], S32[:, gsl]), r=[S32_b[g]], w=[Sbf_b[g]])
            for cc in range(16):
                kb.op(kb.pe, lambda: T.transpose(tpg[:, cc, 0:n], gn[0:n, cc * 128:(cc + 1) * 128], self.ident[0:n, 0:n]),
                      r=[gn_b, self.cb_], w=[tp_b] if cc == 0 else [], wp=[tp_b] if cc else [], inc=(cc == 15))
            kb.op(kb.dve, lambda: V.tensor_tensor(gT[:, :, 0:n], tpg[:, :, 0:n], bc(sng[:, :], [128, 16, n], 2), ALU.mult), r=[tp_b, par_b], w=[gT_b])
            for hh in range(2):
                po = dTb[0:n, :] if hh == 0 else yb[0:n, :]
                pbufs = [dT_b] if hh == 0 else [ydg_b]
                for cc in range(16):
                    kb.op(kb.pe, lambda: T.matmul(po, gT[:, cc, 0:n], w_out[:, cc, hh * 512:(hh + 1) * 512], start=(cc == 0), stop=(cc == 15)),
                          r=[gT_b, wout_b], w=pbufs if cc == 0 else [], wp=pbufs if cc else [], inc=(cc == 15))
                kb.op(kb.dve, lambda: V.tensor_tensor(hs[cur][0:n, hh * 512:(hh + 1) * 512], po, hs[cur][0:n, hh * 512:(hh + 1) * 512], ALU.add),
                      r=pbufs + [hs_b[cur]], wp=[hs_b[cur]])
            dap = self.h_dst(dst, t0, n)
            if dap is not None:
                kb.dma("sp", dap, hs[cur][0:n, :], r=[hs_b[cur]])

    def rstd_ops(self, st, st_b, n, c0, inv_n):
        kb, nc = self.kb, self.nc
        kb.op(kb.act, lambda: nc.scalar.activation(out=st[0:n, c0 + 1:c0 + 2], in_=st[0:n, c0:c0 + 1], func=AF.Ln, bias=EPS, scale=inv_n),
              r=[st_b], wp=[st_b])
        kb.op(kb.act, lambda: nc.scalar.activation(out=st[0:n, c0 + 2:c0 + 3], in_=st[0:n, c0 + 1:c0 + 2], func=AF.Exp, scale=-0.5),
              r=[st_b], wp=[st_b])

    def phase_mla(self, es, j, li, src, dst):
        kb, nc = self.kb, self.nc
        V, A, G, T = nc.vector, nc.scalar, nc.gpsimd, nc.tensor
        oT = self.S(es, "oT", [128, 8, NT], BF16)
        oT_b = Buf()
        w_o = self.S(es, "w_o", [128, 8, D], BF16)
        w_o_b = Buf()
        gq = self.S(es, "gq", [128, 96], F32)
        gk = self.S(es, "gk", [128, 96], F32)
        par_b = Buf()
        self.tri16 = self.S(es, "tri16", [128, 128], BF16)
        kb.dma("pool", self.tri16[:], self.tri_d, wp=[par_b])
        kb.dma("sp", gq[:], self.gq_d[j], wp=[par_b])
        kb.dma("sp", gk[:], self.gk_d[j], wp=[par_b])
        with contextlib.ExitStack() as s1:
            w_in = self.S(s1, "a_win", [128, 8, 672], BF16)
            w_qb = self.S(s1, "a_wqb", [128, 3, 1536], BF16)
            w_kvb = self.S(s1, "a_wkvb", [128, 2, 2048], BF16)
            wb = Buf()
            kb.dma("pool", w_in[:], self.w_mla_in[j].rearrange("(c p) f -> p c f", p=128), wp=[wb])
            kb.dma("pool", w_qb[:], self.w_mla_qb[j].rearrange("(c p) f -> p c f", p=128), wp=[wb])
            kb.dma("pool", w_kvb[:], self.w_mla_kvb[j].rearrange("(c p) f -> p c f", p=128), wp=[wb])
            kb.dma("pool", w_o[:], self.w_mla_out[j].rearrange("(c p) f -> p c f", p=128), wp=[w_o_b])
            qag = self.S(s1, "a_qag", [128, 3], F32)
            kvag = self.S(s1, "a_kvag", [128, 2], F32)
            kb.dma("sp", qag[:], self.qag_d[j], wp=[par_b])
            kb.dma("sp", kvag[:], self.kvag_d[j], wp=[par_b])
            tp_ps = self.P(s1, "a_tp", [128, 512], F32)
            latA = self.P(s1, "a_latA", [128, 512], F32)
            latB = self.P(s1, "a_latB", [128, 512], F32)
            big = self.P(s1, "a_big", [128, 2048], F32)
            tph_ps = self.P(s1, "a_tph", [128, 512], F32)
            latA_b, latB_b, tph_b = PB(), PB(), PB()
            big_b = [PB() for _ in range(4)]
            nsc = self.norm_scratch(s1, "a", tp_ps)
            tp5 = tp_ps[:].bitcast(BF16)[:, 0:640].rearrange("p (c t) -> p c t", c=5)
            tp_b = nsc[7]
            tph = tph_ps[:].bitcast(BF16)[:, 0:1024].rearrange("p (c t) -> p c t", c=8)
            qTv = self.qT_d.rearrange("h p t -> p h t")
            kTv = self.kT_d.rearrange("h p t -> p h t")

            class NS:
                pass

            def mkbufs(ci):
                b = NS()
                nm = lambda x: "a%d_%s" % (ci, x)
                b.hs = self.S(s1, nm("h"), [128, D], F32); b.hs_b = Buf()
                b.cs = self.S(s1, nm("cs"), [128, 2, 16], F32); b.cs_b = Buf()
                b.xnT = self.S(s1, nm("xnT"), [128, 8, 128], BF16); b.xnT_b = Buf()
                b.st = self.S(s1, nm("st"), [128, 8], F32); b.st_b = Buf()
                b.qln = self.S(s1, nm("qln"), [128, 384], BF16)
                b.kvln = self.S(s1, nm("kvln"), [128, 256], BF16)
                b.kpe = self.S(s1, nm("kpe"), [128, 32], F32); b.ln_b = Buf()
                b.sqj = self.S(s1, nm("sqj"), [128, 2048], F32); b.sqj_b = Buf()
                b.qlT = self.S(s1, nm("qlT"), [128, 3, 128], BF16)
                b.kvlT = self.S(s1, nm("kvlT"), [128, 2, 128], BF16); b.lT_b = Buf()
                b.raw = self.S(s1, nm("raw"), [128, 2048], F32); b.raw_b = Buf()
                b.s16 = self.S(s1, nm("s16"), [128, 3, 16], F32); b.s16_b = Buf()
                b.rt = self.S(s1, nm("rt"), [128, 4, 256], F32); b.rt_b = [Buf() for _ in range(4)]
                b.kpg = self.S(s1, nm("kpg"), [128, 2, 32], F32); b.kpg_b = Buf()
                b.qbf = self.S(s1, nm("qbf"), [128, 1536], BF16); b.qbf_b = Buf()
                b.stg = [self.S(s1, nm("stg%d" % i), [96, 16, 128], BF16) for i in range(2)]; b.stg_b = [Buf(), Buf()]
                b.vst = self.S(s1, nm("vst"), [128, 8, 192], BF16); b.vst_b = Buf()
                kb.op(kb.pool, lambda: G.memset(b.vst[:], 1.0), w=[b.vst_b])
                return b

            CH = [mkbufs(0), mkbufs(1)]

            def chain(ti):
                t0, n = TILES[ti]
                b = CH[ti % 2]
                xnT, st, st_b, qln, kvln, kpe, ln_b = b.xnT, b.st, b.st_b, b.qln, b.kvln, b.kpe, b.ln_b
                sqj, sqj_b, qlT, kvlT, lT_b, raw, raw_b = b.sqj, b.sqj_b, b.qlT, b.kvlT, b.lT_b, b.raw, b.raw_b
                s16, s16_b, rt, rt_b, kpg, kpg_b, qbf, qbf_b = b.s16, b.s16_b, b.rt, b.rt_b, b.kpg, b.kpg_b, b.qbf, b.qbf_b
                raw3 = raw[0:n, 0:1536].rearrange("p (h f) -> p h f", h=16)
                sq3 = sqj[0:n, 0:1536].rearrange("p (h f) -> p h f", h=16)
                qb3 = qbf[0:n, :].rearrange("p (h f) -> p h f", h=16)
                kv4 = raw[0:n, :].rearrange("p (h f) -> p h f", h=16)
                sqk = sqj[0:n, 0:1024].rearrange("p (h f) -> p h f", h=16)
                kv5 = raw[0:n, :].rearrange("p (c e f) -> p c e f", c=8, e=2)

                def head_T(sg, dstv):
                    for half in range(2):
                        for hh in range(8):
                            h = half * 8 + hh
                            kb.op(kb.pe, lambda: T.transpose(tph[0:96, hh, 0:n], qbf[0:n, h * 96:(h + 1) * 96], self.ident[0:n, 0:n]),
                                  r=[qbf_b, self.cb_], w=[tph_b] if hh == 0 else [], wp=[tph_b] if hh else [], inc=(hh == 7))
                        kb.op(kb.act, lambda: A.copy(b.stg[sg][0:96, half * 8:(half + 1) * 8, 0:n], tph[0:96, :, 0:n]),
                              r=[tph_b], w=[b.stg_b[sg]] if half == 0 else [], wp=[b.stg_b[sg]] if half else [])
                    kb.dma("sp", dstv[:, :, t0:t0 + n], b.stg[sg][0:96, :, 0:n], r=[b.stg_b[sg]])

                def rope(t1, t2, cosb, sinb, o1, o2, three_d, wb_, rd_bufs):
                    if three_d:
                        a_, b_, c_, d_ = [rt[0:n, i, :].rearrange("p (h f) -> p h f", h=16) for i in range(4)]
                    else:
                        a_, b_, c_, d_ = [rt[0:n, i, 0:16] for i in range(4)]
                    kb.op(kb.dve, lambda: V.tensor_tensor(a_, t1, cosb, ALU.mult), r=rd_bufs, w=[rt_b[0]])
                    kb.op(kb.dve, lambda: V.tensor_tensor(b_, t2, sinb, ALU.mult), r=rd_bufs, w=[rt_b[1]])
                    kb.op(kb.dve, lambda: V.tensor_tensor(c_, t1, sinb, ALU.mult), r=rd_bufs, w=[rt_b[2]])
                    kb.op(kb.dve, lambda: V.tensor_tensor(d_, t2, cosb, ALU.mult), r=rd_bufs, w=[rt_b[3]])
                    kb.op(kb.dve, lambda: V.tensor_tensor(o1, a_, b_, ALU.subtract), r=[rt_b[0], rt_b[1]], wp=[wb_])
                    kb.op(kb.dve, lambda: V.tensor_tensor(o2, c_, d_, ALU.add), r=[rt_b[2], rt_b[3]], wp=[wb_])

                def s0():
                    kb.dma("sp", b.hs[0:n, :], self.h_src(src, t0, n), w=[b.hs_b])

                def s0b():
                    kb.dma("sp", b.cs[0:n, 0, :], self.cos_d[t0:t0 + n, :], w=[b.cs_b])
                    kb.dma("sp", b.cs[0:n, 1, :], self.sin_d[t0:t0 + n, :], wp=[b.cs_b])

                def s1():
                    self.norm_T(b.hs_b, b.hs[0:n, :], n, self.lnT[:, li, :], xnT[:, :, 0:n], b.xnT_b, nsc)

                def s2():
                    for c in range(8):
                        kb.op(kb.pe, lambda: T.matmul(latA[0:n, 0:384], xnT[:, c, 0:n], w_in[:, c, 0:384], start=(c == 0), stop=(c == 7)),
                              r=[b.xnT_b, wb], w=[latA_b] if c == 0 else [], wp=[latA_b] if c else [], inc=(c == 7))
                    for c in range(8):
                        kb.op(kb.pe, lambda: T.matmul(latB[0:n, 0:288], xnT[:, c, 0:n], w_in[:, c, 384:672], start=(c == 0), stop=(c == 7)),
                              r=[b.xnT_b, wb], w=[latB_b] if c == 0 else [], wp=[latB_b] if c else [], inc=(c == 7))
                    kb.op(kb.act, lambda: A.activation(out=sqj[0:n, 0:384], in_=latA[0:n, 0:384], func=AF.Square, accum_out=st[0:n, 0:1]),
                          r=[latA_b], w=[sqj_b, st_b])
                    kb.op(kb.act, lambda: A.activation(out=sqj[0:n, 512:768], in_=latB[0:n, 0:256], func=AF.Square, accum_out=st[0:n, 3:4]),
                          r=[latB_b], wp=[sqj_b, st_b])
                    self.rstd_ops(st, st_b, n, 0, 1.0 / 384)
                    self.rstd_ops(st, st_b, n, 3, 1.0 / 256)
                    kb.op(kb.dve, lambda: V.tensor_scalar(qln[0:n, :], latA[0:n, 0:384], st[0:n, 2:3], None, ALU.mult), r=[latA_b, st_b], w=[ln_b])
                    kb.op(kb.dve, lambda: V.tensor_scalar(kvln[0:n, :], latB[0:n, 0:256], st[0:n, 5:6], None, ALU.mult), r=[latB_b, st_b], wp=[ln_b])
                    kb.op(kb.act, lambda: A.copy(kpe[0:n, :], latB[0:n, 256:288]), r=[latB_b], wp=[ln_b])

                def s3():
                    for c in range(5):
                        srcap = qln[0:n, c * 128:(c + 1) * 128] if c < 3 else kvln[0:n, (c - 3) * 128:(c - 2) * 128]
                        kb.op(kb.pe, lambda: T.transpose(tp5[:, c, 0:n], srcap, self.ident[0:n, 0:n]),
                              r=[ln_b, self.cb_], w=[tp_b] if c == 0 else [], wp=[tp_b] if c else [], inc=(c == 4))
                    kb.op(kb.dve, lambda: V.tensor_tensor(qlT[:, :, 0:n], tp5[:, 0:3, 0:n], bc(qag[:, :], [128, 3, n], 2), ALU.mult),
                          r=[tp_b, par_b], w=[lT_b])
                    kb.op(kb.dve, lambda: V.tensor_tensor(kvlT[:, :, 0:n], tp5[:, 3:5, 0:n], bc(kvag[:, :], [128, 2, n], 2), ALU.mult),
                          r=[tp_b, par_b], wp=[lT_b])

                def s4():
                    for ct in range(3):
                        for kc in range(3):
                            kb.op(kb.pe, lambda: T.matmul(big[0:n, ct * 512:(ct + 1) * 512], qlT[:, kc, 0:n], w_qb[:, kc, ct * 512:(ct + 1) * 512],
                                                          start=(kc == 0), stop=(kc == 2)),
                                  r=[lT_b, wb], w=[big_b[ct]] if kc == 0 else [], wp=[big_b[ct]] if kc else [], inc=(kc == 2))
                    for ct in range(3):
                        E_ = kb.act if ct != 1 else kb.dve
                        fn = (lambda: A.copy(raw[0:n, ct * 512:(ct + 1) * 512], big[0:n, ct * 512:(ct + 1) * 512])) if ct != 1 else \
                             (lambda: V.tensor_copy(raw[0:n, ct * 512:(ct + 1) * 512], big[0:n, ct * 512:(ct + 1) * 512]))
                        kb.op(E_, fn, r=[big_b[ct]], w=[raw_b] if ct == 0 else [], wp=[raw_b] if ct else [])

                def s5():
                    kb.op(kb.dve, lambda: V.tensor_tensor(sqj[0:n, 0:1536], raw[0:n, 0:1536], raw[0:n, 0:1536], ALU.mult), r=[raw_b], w=[sqj_b])
                    kb.op(kb.dve, lambda: V.tensor_reduce(s16[0:n, 0, :], sq3, AX.X, ALU.add), r=[sqj_b], w=[s16_b])
                    kb.op(kb.act, lambda: A.activation(out=s16[0:n, 1, :], in_=s16[0:n, 0, :], func=AF.Ln, bias=EPS, scale=1.0 / 96), r=[s16_b], wp=[s16_b])
                    kb.op(kb.act, lambda: A.activation(out=s16[0:n, 2, :], in_=s16[0:n, 1, :], func=AF.Exp, scale=-0.5), r=[s16_b], wp=[s16_b])
                    kb.op(kb.dve, lambda: V.tensor_tensor(raw3, raw3, bc(s16[0:n, 2, :], [n, 16, 96], 2), ALU.mult), r=[s16_b], w=[raw_b])
                    kb.op(kb.dve, lambda: V.tensor_tensor(raw3, raw3, bc(gq[0:n, :], [n, 16, 96], 1), ALU.mult), r=[par_b], w=[raw_b])
                    cosb = bc(b.cs[0:n, 0, :], [n, 16, 16], 1)
                    sinb = bc(b.cs[0:n, 1, :], [n, 16, 16], 1)
                    kb.op(kb.act, lambda: A.copy(qb3[:, :, 0:64], raw3[:, :, 0:64]), r=[raw_b], w=[qbf_b])
                    rope(raw3[:, :, 64:80], raw3[:, :, 80:96], cosb, sinb, qb3[:, :, 64:80], qb3[:, :, 80:96], True, qbf_b, [raw_b, b.cs_b])

                def s6():
                    head_T(0, qTv)

                def s7():
                    for ct in range(4):
                        for kc in range(2):
                            kb.op(kb.pe, lambda: T.matmul(big[0:n, ct * 512:(ct + 1) * 512], kvlT[:, kc, 0:n], w_kvb[:, kc, ct * 512:(ct + 1) * 512],
                                                          start=(kc == 0), stop=(kc == 1)),
                                  r=[lT_b, wb], w=[big_b[ct]] if kc == 0 else [], wp=[big_b[ct]] if kc else [], inc=(kc == 1))
                    for ct in range(4):
                        E_ = kb.act if ct % 2 == 0 else kb.dve
                        fn = (lambda: A.copy(raw[0:n, ct * 512:(ct + 1) * 512], big[0:n, ct * 512:(ct + 1) * 512])) if ct % 2 == 0 else \
                             (lambda: V.tensor_copy(raw[0:n, ct * 512:(ct + 1) * 512], big[0:n, ct * 512:(ct + 1) * 512]))
                        kb.op(E_, fn, r=[big_b[ct]], w=[raw_b] if ct == 0 else [], wp=[raw_b] if ct else [])

                def s8():
                    kb.op(kb.dve, lambda: V.tensor_tensor(sqk, kv4[:, :, 0:64], kv4[:, :, 0:64], ALU.mult), r=[raw_b], w=[sqj_b])
                    kb.op(kb.dve, lambda: V.tensor_reduce(s16[0:n, 0, :], sqk, AX.X, ALU.add), r=[sqj_b], w=[s16_b])
                    kb.op(kb.act, lambda: A.activation(out=kpg[0:n, 1, :], in_=kpe[0:n, :], func=AF.Square, accum_out=st[0:n, 6:7]),
                          r=[ln_b], w=[kpg_b], wp=[st_b])
                    kb.op(kb.dve, lambda: V.tensor_scalar(s16[0:n, 0, :], s16[0:n, 0, :], st[0:n, 6:7], None, ALU.add), r=[st_b, s16_b], wp=[s16_b])
                    kb.op(kb.act, lambda: A.activation(out=s16[0:n, 1, :], in_=s16[0:n, 0, :], func=AF.Ln, bias=EPS, scale=1.0 / 96), r=[s16_b], wp=[s16_b])
                    kb.op(kb.act, lambda: A.activation(out=s16[0:n, 2, :], in_=s16[0:n, 1, :], func=AF.Exp, scale=-0.5), r=[s16_b], wp=[s16_b])
                    kb.op(kb.act, lambda: A.copy(b.vst[0:n, :, 0:64], kv5[:, :, 0, 64:128]), r=[raw_b, b.vst_b], wp=[b.vst_b])
                    kb.op(kb.dve, lambda: V.tensor_copy(b.vst[0:n, :, 128:192], kv5[:, :, 1, 64:128]), r=[raw_b, b.vst_b], wp=[b.vst_b])
                    kb.dma("sp", self.va_d[t0:t0 + n, :, :], b.vst[0:n, :, :], r=[b.vst_b])
                    kb.op(kb.dve, lambda: V.tensor_tensor(kv4[:, :, 0:64], kv4[:, :, 0:64], bc(s16[0:n, 2, :], [n, 16, 64], 2), ALU.mult),
                          r=[s16_b], w=[raw_b])
                    kb.op(kb.dve, lambda: V.tensor_tensor(qb3[:, :, 0:64], kv4[:, :, 0:64], bc(gk[0:n, 0:64], [n, 16, 64], 1), ALU.mult),
                          r=[raw_b, par_b], w=[qbf_b])
                    kb.op(kb.dve, lambda: V.tensor_tensor(kpg[0:n, 0, :], kpe[0:n, :], gk[0:n, 64:96], ALU.mult), r=[ln_b, par_b], w=[kpg_b])
                    rope(kpg[0:n, 0, 0:16], kpg[0:n, 0, 16:32], b.cs[0:n, 0, :], b.cs[0:n, 1, :], kpg[0:n, 1, 0:16], kpg[0:n, 1, 16:32],
                         False, kpg_b, [kpg_b, b.cs_b])
                    kb.op(kb.dve, lambda: V.tensor_tensor(qb3[:, :, 64:96], bc(kpg[0:n, 1, :], [n, 16, 32], 1), bc(s16[0:n, 2, :], [n, 16, 32], 2), ALU.mult),
                          r=[kpg_b, s16_b], wp=[qbf_b])

                def s9():
                    head_T(1, kTv)

                return [s0, s0b, s1, s2, s3, s4, s5, s6, s7, s8, s9]

            chains = [chain(k) for k in range(len(TILES))]
            NTL = len(TILES)
            for k in (0, 1):
                chains[k][0]()
                chains[k][1]()
                chains[k][2]()
            for k in range(0, NTL, 2):
                pair = [chains[k]] + ([chains[k + 1]] if k + 1 < NTL else [])
                nxt = [chains[k2] for k2 in (k + 2, k + 3) if k2 < NTL]
                for c_ in nxt:
                    c_[0]()
                for i in range(3, 11 + A1_LAG):
                    if i < 11:
                        pair[0][i]()
                    if len(pair) > 1 and 3 <= i - A1_LAG < 11:
                        pair[1][i - A1_LAG]()
                    if i == 5 + A1_LAG:
                        for c_ in nxt:
                            c_[2]()
                    if i == 9 + A1_LAG:
                        for c_ in nxt:
                            c_[1]()
            kb.barrier()
        with contextlib.ExitStack() as s2:
            qh = [self.S(s2, "b_q%d" % i, [96, NT], BF16) for i in range(2)]
            kh = [self.S(s2, "b_k%d" % i, [96, NT], BF16) for i in range(2)]
            qk_b = [Buf(), Buf()]
            va = [self.S(s2, "b_va%d" % i, [128, 33, 192], BF16) for i in range(2)]
            va_b = [Buf(), Buf()]
            NPS = 6
            pT = [self.S(s2, "b_pT%d" % i, [128, 512], BF16) for i in range(NPS)]
            pT_b = [Buf() for _ in range(NPS)]
            rden = [self.S(s2, "b_rd%d" % i, [128, 512], F32) for i in range(2)]
            rdsh = [self.S(s2, "b_rs%d" % i, [128, 512], F32) for i in range(2)]
            rd_b = [Buf(), Buf()]
            rs_b = [Buf(), Buf()]
            bnd = self.S(s2, "b_bnd", [128, 8], F32)
            bnd_b = Buf()
            ps = [self.P(s2, "b_ps%d" % i, [128, 512], F32) for i in range(NPS)]
            ps_b = [PB() for _ in range(NPS)]
            po = [self.P(s2, "b_po%d" % i, [128, 512], F32) for i in range(2)]
            po_b = [PB(), PB()]
            kb.op(kb.dve, lambda: V.tensor_reduce(bnd[:, 0:1], gq[:, :], AX.X, ALU.max), r=[par_b], w=[bnd_b])
            kb.op(kb.dve, lambda: V.tensor_reduce(bnd[:, 1:2], gq[:, :], AX.X, ALU.min), r=[par_b], wp=[bnd_b])
            kb.op(kb.dve, lambda: V.tensor_reduce(bnd[:, 2:3], gk[:, :], AX.X, ALU.max), r=[par_b], wp=[bnd_b])
            kb.op(kb.dve, lambda: V.tensor_reduce(bnd[:, 3:4], gk[:, :], AX.X, ALU.min), r=[par_b], wp=[bnd_b])
            kb.op(kb.dve, lambda: V.scalar_tensor_tensor(bnd[:, 4:5], bnd[:, 1:2], -1.0, bnd[:, 0:1], ALU.mult, ALU.max), r=[bnd_b], wp=[bnd_b])
            kb.op(kb.dve, lambda: V.scalar_tensor_tensor(bnd[:, 5:6], bnd[:, 3:4], -1.0, bnd[:, 2:3], ALU.mult, ALU.max), r=[bnd_b], wp=[bnd_b])
            kb.op(kb.dve, lambda: V.scalar_tensor_tensor(bnd[:, 6:7], bnd[:, 4:5], -float(np.sqrt(96.0)), bnd[:, 5:6], ALU.mult, ALU.mult),
                  r=[bnd_b], wp=[bnd_b])
            negB = bnd[:, 6:7]
            scale = float(96.0 ** -0.5)

            def load_head(h):
                b = h % 2
                kb.dma("sp", qh[b][:, :], self.qT_d[h], w=[qk_b[b]])
                kb.dma("sp", kh[b][:, :], self.kT_d[h], wp=[qk_b[b]])

            def load_pair(c):
                b = c % 2
                kb.dma("sp", va[b][0:16, 0, :], self.va_d[0:16, c, :], w=[va_b[b]])
                vv = self.va_d[16:NT, c, :].rearrange("(j p) w -> p j w", p=128)
                for jj in range(0, 32, 8):
                    kb.dma("sp", va[b][:, 1 + jj:9 + jj, :], vv[:, jj:jj + 8, :], wp=[va_b[b]])

            load_pair(0)
            load_head(0)
            items = []
            for h in range(MLA_H):
                for qi in range(9):
                    if qi == 0:
                        q0, nq = 0, 16
                        kts = [(0, 0, 16, 0, True)]
                    else:
                        q0, nq = 16 + 512 * (qi - 1), 512
                        kts = [(0, 0, 16, 0, False)] + [(kt, 16 + 128 * (kt - 1), 128, 0, False) for kt in range(1, 4 * (qi - 1) + 1)]
                        kts += [(4 * (qi - 1) + 1 + i, 16 + 128 * (4 * (qi - 1) + i), 128, 128 * i, True) for i in range(4)]
                    for idx, kt in enumerate(kts):
                        items.append((h, qi, q0, nq, idx, len(kts)) + kt)
            LA = 4
            NI = len(items)
            for i in range(NI + LA):
                if i < NI:
                    (h, qi, q0, nq, idx, nk_t, kt, k0, nk, qoff, diag) = items[i]
                    if qi == 0 and idx == 0 and h + 1 < MLA_H:
                        load_head(h + 1)
                        if h % 2 == 1:
                            load_pair(h // 2 + 1)
                    hb_ = h % 2
                    nqq = nq - qoff
                    p = i % NPS
                    kb.op(kb.pe, lambda: T.matmul(ps[p][0:nk, 0:nqq], kh[hb_][:, k0:k0 + nk], qh[hb_][:, q0 + qoff:q0 + nq], start=True, stop=True),
                          r=[qk_b[hb_]], w=[ps_b[p]])
                    kb.op(kb.act, lambda: A.activation(out=pT[p][0:nk, 0:nqq], in_=ps[p][0:nk, 0:nqq], func=AF.Exp, bias=negB[0:nk, :], scale=scale),
                          r=[ps_b[p], bnd_b], w=[pT_b[p]])
                    if diag:
                        kb.op(kb.dve, lambda: V.tensor_tensor(pT[p][0:nk, 0:nk], pT[p][0:nk, 0:nk], self.tri16[0:nk, 0:nk], ALU.mult),
                              r=[self.cb_], w=[pT_b[p]])
                ii = i - LA
                if ii >= 0:
                    (h, qi, q0, nq, idx, nk_t, kt, k0, nk, qoff, diag) = items[ii]
                    c, e = h // 2, h % 2
                    vb_ = c % 2
                    dlo, dhi = (0, 64) if e == 0 else (64, 128)
                    nlo, nhi = (64, 128) if e == 0 else (0, 64)
                    nqq = nq - qoff
                    p = ii % NPS
                    pp = (h * 9 + qi) % 2
                    first, last = idx == 0, idx == nk_t - 1
                    kb.op(kb.pe, lambda: T.matmul(po[pp][:, qoff:nq], va[vb_][0:nk, kt, e * 64:e * 64 + 128], pT[p][0:nk, 0:nqq], start=first, stop=last),
                          r=[pT_b[p], va_b[vb_]], w=[po_b[pp]] if first else [], wp=[] if first else [po_b[pp]], inc=last)
                    if last:
                        kb.op(kb.dve, lambda: V.reciprocal(rden[pp][nlo:nhi, 0:nq], po[pp][nlo:nhi, 0:nq]), r=[po_b[pp]], w=[rd_b[pp]])
                        kb.op(kb.dve, lambda: V.tensor_copy(rdsh[pp][dlo:dhi, 0:nq], rden[pp][nlo:nhi, 0:nq]), r=[rd_b[pp]], w=[rs_b[pp]])
                        kb.op(kb.dve, lambda: V.tensor_tensor(oT[dlo:dhi, c, q0:q0 + nq], po[pp][dlo:dhi, 0:nq], rdsh[pp][dlo:dhi, 0:nq], ALU.mult),
                              r=[po_b[pp], rs_b[pp]], wp=[oT_b])
            kb.barrier()
        with contextlib.ExitStack() as s3:
            hs = [self.S(s3, "c_h%d" % i, [128, D], F32) for i in range(3)]
            hs_b = [Buf() for _ in range(3)]
            po = [self.P(s3, "c_po%d" % i, [128, 512], F32) for i in range(4)]
            po_b = [PB() for _ in range(4)]
            ip = 0
            for ti, (t0, n) in enumerate(TILES):
                s = ti % 3
                kb.dma("sp", hs[s][0:n, :], self.h_src(src, t0, n), w=[hs_b[s]])
                for hh in range(2):
                    p = ip % 4
                    ip += 1
                    for c in range(8):
                        kb.op(kb.pe, lambda: T.matmul(po[p][0:n, :], oT[:, c, t0:t0 + n], w_o[:, c, hh * 512:(hh + 1) * 512], start=(c == 0), stop=(c == 7)),
                              r=[oT_b, w_o_b], w=[po_b[p]] if c == 0 else [], wp=[po_b[p]] if c else [], inc=(c == 7))
                    kb.op(kb.dve, lambda: V.tensor_tensor(hs[s][0:n, hh * 512:(hh + 1) * 512], po[p][0:n, :], hs[s][0:n, hh * 512:(hh + 1) * 512], ALU.add),
                          r=[po_b[p], hs_b[s]], wp=[hs_b[s]])
                dap = self.h_dst(dst, t0, n)
                if dap is not None:
                    kb.dma("sp", dap, hs[s][0:n, :], r=[hs_b[s]])


def host_inputs(inp):
    f = lambda a: np.ascontiguousarray(np.asarray(a, dtype=np.float32))
    rep = lambda a: np.ascontiguousarray(np.broadcast_to(np.asarray(a, np.float32)[:, None, :], (a.shape[0], 128, a.shape[1])))
    colT = lambda a, c: np.ascontiguousarray(np.asarray(a, np.float32).reshape(a.shape[0], c, 128).transpose(0, 2, 1))
    ln = np.concatenate([np.asarray(inp["ln_mix"], np.float32), np.asarray(inp["ln_mlp"], np.float32)], 0)
    lnT = np.ascontiguousarray(ln.reshape(8, 8, 128).transpose(2, 0, 1))
    cw = np.asarray(inp["ssd_conv_w"], np.float32)
    cwT = np.ascontiguousarray(cw.reshape(2, 4, 32, 128).transpose(0, 3, 2, 1))
    k = np.arange(128)
    tri = (k[:, None] <= k[None, :]).astype(np.float32)
    inv = 1.0 / (10000.0 ** (np.arange(0, 32, 2, dtype=np.float32) / 32.0))
    ang = np.arange(NT, dtype=np.float32)[:, None] * inv[None, :].astype(np.float32)
    common = {
        "meta": f(inp["meta_tokens"]),
        "ssd_w_in": f(inp["ssd_w_in"]), "ssd_w_out": f(inp["ssd_w_out"]),
        "mla_w_in": f(inp["mla_w_in"]), "mla_w_q_b": f(inp["mla_w_q_b"]), "mla_w_kv_b": f(inp["mla_w_kv_b"]),
        "mla_w_out": f(inp["mla_w_out"]), "mlp_w_up": f(inp["mlp_w_up"]), "mlp_w_down": f(inp["mlp_w_down"]),
        "lnT": lnT, "cw": cwT, "cb": colT(inp["ssd_conv_b"], 32),
        "dtb_rep": rep(inp["ssd_dt_bias"]), "alog_rep": rep(inp["ssd_a_log"]), "dskip_rep": rep(inp["ssd_d"]),
        "ssd_normT": colT(inp["ssd_norm"], 16), "q_a_T": colT(inp["mla_q_a_norm"], 3), "kv_a_T": colT(inp["mla_kv_a_norm"], 2),
        "gq_rep": rep(inp["mla_q_norm"]), "gk_rep": rep(inp["mla_k_norm"]),
        "ident": np.eye(128, dtype=np.float32), "tri": tri, "ltri": np.ascontiguousarray(1.0 - tri),
        "ones": np.ones((128, 128), np.float32),
        "cos": np.cos(ang).astype(np.float32), "sin": np.sin(ang).astype(np.float32),
    }
    return common


FULL_PHASES = [
    ("ssd", 0, 0, "x", "h"), ("mlp", 0, "h", "h"),
    ("mla", 0, 1, "h", "h"), ("mlp", 1, "h", "h"),
    ("ssd", 1, 2, "h", "h"), ("mlp", 2, "h", "h"),
    ("mla", 1, 3, "h", "h"), ("mlp", 3, "h", "y"),
]


def run(inputs, phases, cores=8):
    common = host_inputs(inputs)
    x = np.asarray(inputs["x"], np.float32)
    prog = Prog(phases)
    in_maps = []
    for c in range(cores):
        m = dict(common)
        m["x"] = np.ascontiguousarray(x[c])
        in_maps.append(m)
    res = run_bass_kernel_spmd(prog.nc, in_maps, core_ids=list(range(cores)))
    return np.stack([np.asarray(r["y"]) for r in res.results], 0)


def kernel(**inputs):
    return run(inputs, FULL_PHASES, 8).astype(np.float32)
```

```python
import contextlib
import numpy as np
import concourse.bass as bass
import concourse.mybir as mybir
from concourse.bass_utils import run_bass_kernel_spmd

F32, BF16 = mybir.dt.float32, mybir.dt.bfloat16
AF = mybir.ActivationFunctionType
ALU = mybir.AluOpType
AX = mybir.AxisListType

NT, NM, D, SEQ = 4112, 16, 1024, 4096
TILES = [(0, 16)] + [(16 + 128 * j, 128) for j in range(32)]
EPS = 1e-6
DFF = 4096
SSD_IN = 6176
NH_S = 32
SSD_LEAD = 0.8
A1_LAG = 1
SSD_GLAG = 1
MLA_H = 16
QK = 96


class Buf:
    __slots__ = ("w", "r", "name", "ps")

    def __init__(self, name="", ps=False):
        self.w = {}
        self.r = {}
        self.name = name
        self.ps = ps


def PB():
    return Buf(ps=True)


class Eng:
    def __init__(self, name, e, sem):
        self.name, self.e, self.sem, self.cnt, self.seen = name, e, sem, 0, {}


class KB:
    def __init__(self, nc, es):
        self.nc = nc
        mk = lambda n: es.enter_context(nc.semaphore(n))
        self.pe = Eng("pe", nc.tensor, mk("s_pe"))
        self.act = Eng("act", nc.scalar, mk("s_act"))
        self.dve = Eng("dve", nc.vector, mk("s_dve"))
        self.pool = Eng("pool", nc.gpsimd, mk("s_pool"))
        self.sp = Eng("sp", nc.sync, mk("s_sp"))
        self.engs = [self.pe, self.act, self.dve, self.pool, self.sp]
        self.dsem = {"sp": [[mk("d_sp%d" % i), 0] for i in range(24)],
                     "pool": [[mk("d_pl%d" % i), 0] for i in range(8)]}
        self.drr = {"sp": 0, "pool": 0}
        self.nins = 0

    def _wait(self, E, toks):
        for key, (sem, val) in toks.items():
            if E is self.pe and key == "pe":
                continue
            if E.seen.get(key, 0) >= val:
                continue
            E.e.wait_ge(sem, val)
            E.seen[key] = val

    @staticmethod
    def _add(need, d):
        for k, sv in d.items():
            if k not in need or need[k][1] < sv[1]:
                need[k] = sv

    def _deps(self, r, w, wp, ekey=None):
        need = {}
        for b in r:
            self._add(need, b.w)
            if b.ps:
                self._add(need, {k: v for k, v in b.r.items() if k != ekey})
        for b in w:
            self._add(need, b.w)
            self._add(need, b.r)
        for b in wp:
            self._add(need, b.r)
            if b.ps:
                self._add(need, b.w)
        return need

    def _reg(self, key, tok, r, w, wp):
        for b in r:
            if key not in b.r or b.r[key][1] < tok[1]:
                b.r[key] = tok
        for b in w:
            b.w = {key: tok}
            b.r = {}
        for b in wp:
            if key not in b.w or b.w[key][1] < tok[1]:
                b.w[key] = tok

    def op(self, E, fn, r=(), w=(), wp=(), inc=True):
        self._wait(E, self._deps(r, w, wp, E.name))
        ins = fn()
        self.nins += 1
        if inc:
            E.cnt += 1
            ins.then_inc(E.sem, 1)
            tok = (E.sem, E.cnt)
        else:
            tok = (E.sem, E.cnt + 1)
        self._reg(E.name, tok, r, w, wp)

    def dma(self, Q, out, in_, r=(), w=(), wp=()):
        E = self.sp if Q == "sp" else self.pool
        self._wait(E, self._deps(r, w, wp))
        lst = self.dsem[Q]
        i = self.drr[Q]
        self.drr[Q] = (i + 1) % len(lst)
        sem, cnt = lst[i]
        key = (Q, i)
        if cnt > 0 and E.seen.get(key, 0) < cnt:
            E.e.wait_ge(sem, cnt)
            E.seen[key] = cnt
        ins = E.e.dma_start(out=out, in_=in_)
        ins.then_inc(sem, 16)
        self.nins += 1
        lst[i][1] = cnt + 16
        self._reg(key, (sem, cnt + 16), r, w, wp)

    def barrier(self):
        toks = {}
        for E in self.engs:
            if E.cnt > 0:
                toks[E.name] = (E.sem, E.cnt)
        for Q, lst in self.dsem.items():
            for i, (sem, cnt) in enumerate(lst):
                if cnt > 0:
                    toks[(Q, i)] = (sem, cnt)
        for E in self.engs:
            self._wait(E, toks)


def bc(ap, shape, axis):
    return ap.unsqueeze(axis).to_broadcast(list(shape))


class Prog:
    def __init__(self, phases):
        self.phases = phases
        nc = bass.Bass("TRN2", target_bir_lowering=False)
        self.nc = nc
        di = lambda name, shape: nc.dram_tensor(name, list(shape), F32, kind="ExternalInput").ap()
        self.x = di("x", [SEQ, D])
        self.meta = di("meta", [NM, D])
        self.w_ssd_in = di("ssd_w_in", [2, D, SSD_IN])
        self.w_ssd_out = di("ssd_w_out", [2, 2048, D])
        self.w_mla_in = di("mla_w_in", [2, D, 672])
        self.w_mla_qb = di("mla_w_q_b", [2, 384, 1536])
        self.w_mla_kvb = di("mla_w_kv_b", [2, 256, 2048])
        self.w_mla_out = di("mla_w_out", [2, D, D])
        self.w_up = di("mlp_w_up", [4, D, DFF])
        self.w_dn = di("mlp_w_down", [4, DFF, D])
        self.lnT_d = di("lnT", [128, 8, 8])
        self.cw_d = di("cw", [2, 128, 32, 4])
        self.cb_d = di("cb", [2, 128, 32])
        self.dtb_d = di("dtb_rep", [2, 128, 32])
        self.alog_d = di("alog_rep", [2, 128, 32])
        self.dsk_d = di("dskip_rep", [2, 128, 32])
        self.sng_d = di("ssd_normT", [2, 128, 16])
        self.qag_d = di("q_a_T", [2, 128, 3])
        self.kvag_d = di("kv_a_T", [2, 128, 2])
        self.gq_d = di("gq_rep", [2, 128, 96])
        self.gk_d = di("gk_rep", [2, 128, 96])
        self.ident_d = di("ident", [128, 128])
        self.tri_d = di("tri", [128, 128])
        self.ltri_d = di("ltri", [128, 128])
        self.ones_d = di("ones", [128, 128])
        self.cos_d = di("cos", [NT, 16])
        self.sin_d = di("sin", [NT, 16])
        self.y = nc.dram_tensor("y", [SEQ, D], F32, kind="ExternalOutput").ap()
        self.hd = nc.dram_tensor("hd", [NT, D], F32, kind="Internal").ap()
        self.qT_d = nc.dram_tensor("qT_d", [MLA_H, QK, NT], BF16, kind="Internal").ap()
        self.kT_d = nc.dram_tensor("kT_d", [MLA_H, QK, NT], BF16, kind="Internal").ap()
        self.va_d = nc.dram_tensor("va_d", [NT, 8, 192], BF16, kind="Internal").ap()

        with contextlib.ExitStack() as es:
            self.kb = KB(nc, es)
            self.build(es)

    def S(self, es, name, shape, dt):
        self.uid = getattr(self, "uid", 0) + 1
        return es.enter_context(self.nc.sbuf_tensor("sb%d_%s" % (self.uid, name), list(shape), dt))

    def P(self, es, name, shape, dt):
        self.uid = getattr(self, "uid", 0) + 1
        return es.enter_context(self.nc.psum_tensor("ps%d_%s" % (self.uid, name), list(shape), dt))

    def h_src(self, kind, t0, n):
        if kind == "x":
            return self.meta[0:16, :] if t0 == 0 else self.x[t0 - 16:t0 - 16 + n, :]
        return self.hd[t0:t0 + n, :]

    def h_dst(self, kind, t0, n):
        if kind == "y":
            return None if t0 == 0 else self.y[t0 - 16:t0 - 16 + n, :]
        return self.hd[t0:t0 + n, :]

    def build(self, es):
        kb, nc = self.kb, self.nc
        self.ident = self.S(es, "ident", [128, 128], BF16)
        self.tri32 = self.S(es, "tri32", [128, 128], F32)
        self.ltri32 = self.S(es, "ltri32", [128, 128], F32)
        self.ones32 = self.S(es, "ones32", [128, 128], F32)
        self.lnT = self.S(es, "lnT", [128, 8, 8], F32)
        self.cb_ = Buf("consts")
        kb.dma("pool", self.ident[:], self.ident_d, wp=[self.cb_])
        kb.dma("sp", self.tri32[:], self.tri_d, wp=[self.cb_])
        kb.dma("sp", self.ltri32[:], self.ltri_d, wp=[self.cb_])
        kb.dma("sp", self.ones32[:], self.ones_d, wp=[self.cb_])
        kb.dma("sp", self.lnT[:], self.lnT_d, wp=[self.cb_])
        for ph in self.phases:
            kind = ph[0]
            with contextlib.ExitStack() as pes:
                if kind == "mlp":
                    self.phase_mlp(pes, *ph[1:])
                elif kind == "ssd":
                    self.phase_ssd(pes, *ph[1:])
                elif kind == "mla":
                    self.phase_mla(pes, *ph[1:])
                kb.barrier()
        kb.barrier()

    def rstd_newton(self, st, st_b, n, inv_n, eps):
        kb, nc = self.kb, self.nc
        V = nc.vector
        I32 = mybir.dt.int32
        for f in self.rstd_newton_ops(st, st_b, n, inv_n, eps):
            f()

    def rstd_newton_ops(self, st, st_b, n, inv_n, eps):
        kb, nc = self.kb, self.nc
        V = nc.vector
        I32 = mybir.dt.int32
        x, g, t = st[0:n, 1:2], st[0:n, 2:3], st[0:n, 3:4]
        ops = []
        ops.append(lambda: kb.op(kb.dve, lambda: V.tensor_scalar(x, st[0:n, 0:1], inv_n, eps, ALU.mult, ALU.add), r=[st_b], wp=[st_b]))
        ops.append(lambda: kb.op(kb.dve, lambda: V.tensor_scalar(g.bitcast(I32), x.bitcast(I32), 1, None, ALU.arith_shift_right), r=[st_b], wp=[st_b]))
        ops.append(lambda: kb.op(kb.dve, lambda: V.tensor_scalar(g.bitcast(I32), g.bitcast(I32), -1, 0x5f3759df, ALU.mult, ALU.add), r=[st_b], wp=[st_b]))
        for _ in range(2):
            ops.append(lambda: kb.op(kb.dve, lambda: V.scalar_tensor_tensor(t, g, x, g, ALU.mult, ALU.mult), r=[st_b], wp=[st_b]))
            ops.append(lambda: kb.op(kb.dve, lambda: V.tensor_scalar(t, t, -0.5, 1.5, ALU.mult, ALU.add), r=[st_b], wp=[st_b]))
            ops.append(lambda: kb.op(kb.dve, lambda: V.tensor_tensor(g, g, t, ALU.mult), r=[st_b], wp=[st_b]))
        return ops

    def norm_T(self, hb, h_ap, n, gain_ap, dst_ap, dst_b, sc, newton=False, part=None):
        kb, nc = self.kb, self.nc
        junk, junk_b, ss, ss_b, xn, xn_b, tp, tp_b = sc
        if part == "B":
            return self._norm_T_b(n, gain_ap, dst_ap, dst_b, sc)
        kb.op(kb.act, lambda: nc.scalar.activation(out=junk[0:n, :], in_=h_ap, func=AF.Square, accum_out=ss[0:n, 0:1]),
              r=[hb], w=[junk_b, ss_b] if junk_b is not xn_b else [xn_b, ss_b])
        if newton:
            self.rstd_newton(ss, ss_b, n, 1.0 / D, EPS)
        else:
            kb.op(kb.act, lambda: nc.scalar.activation(out=ss[0:n, 1:2], in_=ss[0:n, 0:1], func=AF.Ln, bias=EPS, scale=1.0 / D),
                  r=[ss_b], wp=[ss_b])
            kb.op(kb.act, lambda: nc.scalar.activation(out=ss[0:n, 2:3], in_=ss[0:n, 1:2], func=AF.Exp, scale=-0.5),
                  r=[ss_b], wp=[ss_b])
        kb.op(kb.dve, lambda: nc.vector.tensor_scalar(xn[0:n, :], h_ap, ss[0:n, 2:3], None, ALU.mult),
              r=[hb, ss_b], w=[xn_b])
        if part == "A":
            return
        self._norm_T_b(n, gain_ap, dst_ap, dst_b, sc)

    def _norm_T_b(self, n, gain_ap, dst_ap, dst_b, sc):
        kb, nc = self.kb, self.nc
        junk, junk_b, ss, ss_b, xn, xn_b, tp, tp_b = sc
        for c in range(8):
            kb.op(kb.pe, lambda c=c: nc.tensor.transpose(tp[:, c, 0:n], xn[0:n, c * 128:(c + 1) * 128], self.ident[0:n, 0:n]),
                  r=[xn_b, self.cb_], w=[tp_b] if c == 0 else [], wp=[tp_b] if c else [], inc=(c == 7))
        kb.op(kb.dve, lambda: nc.vector.tensor_tensor(dst_ap, tp[:, :, 0:n], bc(gain_ap, [128, 8, n], 2), ALU.mult),
              r=[tp_b, self.cb_], w=[dst_b])

    def norm_scratch(self, es, pfx, tp_ps):
        ss = self.S(es, pfx + "ss", [128, 4], F32)
        xn = self.S(es, pfx + "xn", [128, 1024], BF16)
        tp = tp_ps[:].bitcast(BF16)[:, 0:1024].rearrange("p (c t) -> p c t", c=8)
        xn_b = Buf()
        return (xn, xn_b, ss, Buf(), xn, xn_b, tp, PB())

    def phase_mlp(self, es, li, src, dst):
        kb, nc = self.kb, self.nc
        wup = self.S(es, "wup", [128, 8, DFF], BF16)
        wdn = self.S(es, "wdn", [128, 32, D], BF16)
        wup_b = [Buf() for _ in range(8)]
        wdn_b = [Buf() for _ in range(8)]
        upv = self.w_up[li].rearrange("(c p) f -> p c f", p=128)
        dnv = self.w_dn[li].rearrange("(c p) d -> p c d", p=128)
        for i in range(8):
            kb.dma("pool", wup[:, :, i * 512:(i + 1) * 512], upv[:, :, i * 512:(i + 1) * 512], w=[wup_b[i]])
        for i in range(8):
            kb.dma("pool", wdn[:, 4 * i:4 * i + 4, :], dnv[:, 4 * i:4 * i + 4, :], w=[wdn_b[i]])
        NSLOT = 7
        hs = [self.S(es, "mh%d" % i, [128, D], F32) for i in range(NSLOT)]
        hs_b = [Buf() for _ in range(NSLOT)]
        xnT = self.S(es, "m_xnT", [128, 8, 512], BF16)
        xnT_b = Buf()
        uT = self.S(es, "m_uT", [128, 32, 512], BF16)
        uT_b = [Buf() for _ in range(32)]
        r32 = [self.S(es, "m_r32_%d" % i, [128, 512], F32) for i in range(2)]
        r32_b = [Buf(), Buf()]
        tp_ps = [self.P(es, "m_tp%d" % i, [128, 512], F32) for i in range(2)]
        pu = [self.P(es, "m_pu%d" % i, [128, 512], F32) for i in range(3)]
        pu_b = [PB() for _ in range(3)]
        pd = [self.P(es, "m_pd%d" % i, [128, 512], F32) for i in range(3)]
        pd_b = [PB() for _ in range(3)]
        nsc = [self.norm_scratch(es, "m%d" % i, tp_ps[i]) for i in range(2)]
        gain = self.lnT[:, 4 + li, :]
        groups = [[TILES[0]]] + [TILES[1 + 4 * g:5 + 4 * g] for g in range(8)]
        NG = len(groups)
        slots_all = []
        sl_ = 0
        for grp in groups:
            slots_all.append([(sl_ + i) % NSLOT for i in range(len(grp))])
            sl_ += len(grp)
        loaded = set()

        def load_tile(gi, k):
            if (gi, k) in loaded:
                return
            loaded.add((gi, k))
            (t0, n), s = groups[gi][k], slots_all[gi][k]
            kb.dma("sp", hs[s][0:n, :], self.h_src(src, t0, n), w=[hs_b[s]])

        def norm_part(gi, k, part):
            (t0, n), s = groups[gi][k], slots_all[gi][k]
            off = sum(nn for _, nn in groups[gi][:k])
            self.norm_T(hs_b[s], hs[s][0:n, :], n, gain, xnT[:, :, off:off + n], xnT_b, nsc[k % 2], part=part)

        for k in range(len(groups[0])):
            load_tile(0, k)
            norm_part(0, k, None)
        iu = 0
        ipd = 0
        for gi, grp in enumerate(groups):
            ntok = sum(n for _, n in grp)
            myslots = slots_all[gi]
            nxt = gi + 1 if gi + 1 < NG else None
            if nxt is not None:
                for k in range(min(len(groups[nxt]), NSLOT - len(grp))):
                    load_tile(nxt, k)
            for fc in range(32):
                p = iu % 3
                for c in range(8):
                    kb.op(kb.pe, lambda c=c, fc=fc, p=p: nc.tensor.matmul(pu[p][:, 0:ntok], wup[:, c, fc * 128:(fc + 1) * 128],
                                                                      xnT[:, c, 0:ntok], start=(c == 0), stop=(c == 7)),
                          r=[wup_b[fc // 4], xnT_b], w=[pu_b[p]] if c == 0 else [], wp=[pu_b[p]] if c else [], inc=(c == 7))
                rr = iu % 2
                kb.op(kb.act, lambda p=p, rr=rr: nc.scalar.activation(out=r32[rr][:, 0:ntok], in_=pu[p][:, 0:ntok], func=AF.Relu),
                      r=[pu_b[p]], w=[r32_b[rr]])
                E = kb.dve if (fc % 2 == 0) else kb.pool
                kb.op(E, lambda rr=rr, fc=fc, E=E: E.e.tensor_tensor(uT[:, fc, 0:ntok], r32[rr][:, 0:ntok], r32[rr][:, 0:ntok], ALU.mult),
                      r=[r32_b[rr]], w=[uT_b[fc]])
                iu += 1
            off = 0
            for kk, ((t0, n), s) in enumerate(zip(grp, myslots)):
                if nxt is not None and kk < len(groups[nxt]):
                    load_tile(nxt, kk)
                    norm_part(nxt, kk, "A")
                for hh in range(2):
                    p = ipd % 3
                    ipd += 1
                    for fc in range(32):
                        kb.op(kb.pe, lambda fc=fc, p=p, off=off, n=n, hh=hh: nc.tensor.matmul(
                            pd[p][0:n, :], uT[:, fc, off:off + n], wdn[:, fc, hh * 512:(hh + 1) * 512],
                            start=(fc == 0), stop=(fc == 31)),
                            r=[uT_b[fc], wdn_b[fc // 4]], w=[pd_b[p]] if fc == 0 else [], wp=[pd_b[p]] if fc else [], inc=(fc == 31))
                    kb.op(kb.dve, lambda p=p, n=n, hh=hh, s=s: nc.vector.tensor_tensor(
                        hs[s][0:n, hh * 512:(hh + 1) * 512], pd[p][0:n, :], hs[s][0:n, hh * 512:(hh + 1) * 512], ALU.add),
                        r=[pd_b[p], hs_b[s]], wp=[hs_b[s]])
                dst_ap = self.h_dst(dst, t0, n)
                if dst_ap is not None:
                    kb.dma("sp", dst_ap, hs[s][0:n, :], r=[hs_b[s]])
                off += n
                if nxt is not None and kk < len(groups[nxt]):
                    norm_part(nxt, kk, "B")
            if nxt is not None:
                for kk in range(len(grp), len(groups[nxt])):
                    load_tile(nxt, kk)
                    norm_part(nxt, kk, None)

    def phase_ssd(self, es, j, li, src, dst):
        from functools import partial
        kb, nc = self.kb, self.nc
        V, A, G, T = nc.vector, nc.scalar, nc.gpsimd, nc.tensor
        w_in = self.S(es, "s_win", [128, 8, SSD_IN], BF16)
        w_out = self.S(es, "s_wout", [128, 16, D], BF16)
        par_b = Buf()
        wdt_b = Buf()
        wx_b = [Buf() for _ in range(8)]
        wz_b = [Buf() for _ in range(4)]
        wout_b = [Buf() for _ in range(4)]
        wv = self.w_ssd_in[j].rearrange("(c p) f -> p c f", p=128)
        kb.dma("pool", w_in[:, :, 6144:6176], wv[:, :, 6144:6176], w=[wdt_b])
        for i in range(8):
            kb.dma("pool", w_in[:, :, 2048 + 512 * i:2560 + 512 * i], wv[:, :, 2048 + 512 * i:2560 + 512 * i], w=[wx_b[i]])
        for i in range(4):
            kb.dma("pool", w_in[:, :, 512 * i:512 * (i + 1)], wv[:, :, 512 * i:512 * (i + 1)], w=[wz_b[i]])
        wov = self.w_ssd_out[j].rearrange("(c p) f -> p c f", p=128)
        for i in range(4):
            kb.dma("pool", w_out[:, 4 * i:4 * i + 4, :], wov[:, 4 * i:4 * i + 4, :], w=[wout_b[i]])
        cw = self.S(es, "s_cw", [128, 32, 4], F32)
        cb = self.S(es, "s_cb", [128, 32], F32)
        dtb = self.S(es, "s_dtb", [128, 32], F32)
        arep = self.S(es, "s_arep", [128, 32], F32)
        dsk = self.S(es, "s_dsk", [128, 32], F32)
        sng = self.S(es, "s_sng", [128, 16], F32)
        kb.dma("sp", cw[:], self.cw_d[j], wp=[par_b])
        kb.dma("sp", cb[:], self.cb_d[j], wp=[par_b])
        kb.dma("sp", dtb[:], self.dtb_d[j], wp=[par_b])
        kb.dma("sp", arep[:], self.alog_d[j], wp=[par_b])
        kb.dma("sp", dsk[:], self.dsk_d[j], wp=[par_b])
        kb.dma("sp", sng[:], self.sng_d[j], wp=[par_b])
        arep_b = Buf()
        kb.op(kb.act, lambda: A.activation(out=arep[:], in_=arep[:], func=AF.Exp), r=[par_b], w=[arep_b])
        kb.op(kb.dve, lambda: V.tensor_scalar(arep[:], arep[:], -1.0, None, ALU.mult), r=[arep_b], w=[arep_b])
        cwb = Buf()
        kb.op(kb.dve, lambda: V.tensor_scalar(cw[:], cw[:], 0.5, None, ALU.mult), r=[par_b], w=[cwb])
        kb.op(kb.dve, lambda: V.tensor_scalar(cb[:], cb[:], 0.5, None, ALU.mult), r=[par_b], wp=[cwb])
        hs = [self.S(es, "s_h%d" % i, [128, D], F32) for i in range(2)]
        hs_b = [Buf(), Buf()]
        tpF = self.P(es, "s_tpF", [128, 512], F32)
        pzs = [self.P(es, "s_pz%d" % i, [128, 512], F32) for i in range(2)]
        fm = self.P(es, "s_fm", [128, 512], F32)
        tpB = self.P(es, "s_tpB", [128, 512], F32)
        dTb = self.P(es, "s_dT", [128, 512], F32)
        bm = self.P(es, "s_bm", [128, 512], F32)
        yb = self.P(es, "s_y", [128, 512], F32)
        pz_b = [PB(), PB()]
        fm_b, tpB_b, dT_b, bm_b, y_b = PB(), PB(), PB(), PB(), PB()
        gT = self.S(es, "s_gT", [128, 16, 128], BF16)
        gT_b = Buf()
        _ss = self.S(es, "s_nss", [128, 4], F32)
        _xn = gT[:, 0:8, :].rearrange("p c t -> p (c t)")
        _tp = tpF[:].bitcast(BF16)[:, 0:1024].rearrange("p (c t) -> p c t", c=8)
        nsc = (_xn, gT_b, _ss, Buf(), _xn, gT_b, _tp, PB())
        tpB16 = tpB[:].bitcast(BF16)
        tpBg = tpB16.rearrange("p (c t) -> p c t", c=8)
        xnT = [self.S(es, "s_xnT%d" % i, [128, 8, 131], BF16) for i in range(2)]
        xnT_b = [Buf(), Buf()]
        for i in range(2):
            kb.op(kb.pool, lambda: G.memset(xnT[i][:], 0.0), w=[xnT_b[i]])
        xbc_xs = self.S(es, "s_xbcxs", [128, 16, 128], BF16)
        xsT_b = [Buf() for _ in range(16)]
        xbc_bc = [self.S(es, "s_xbcbc%d" % i, [128, 16, 128], BF16) for i in range(2)]
        bc_b = [[Buf() for _ in range(16)] for _ in range(2)]
        NU, NA = 4, 7
        u = [self.S(es, "s_u%d" % i, [128, 131], F32) for i in range(NU)]
        u_b = [Buf() for _ in range(NU)]
        acc = [self.S(es, "s_acc%d" % i, [128, 128], F32) for i in range(NA)]
        acc_b = [Buf() for _ in range(NA)]
        th = [self.S(es, "s_th%d" % i, [128, 128], F32) for i in range(2)]
        th_b = [Buf(), Buf()]
        xs_tm = self.S(es, "s_xstm", [128, 2048], BF16)
        xs_b = [Buf() for _ in range(8)]
        B_tm = self.S(es, "s_Btm", [128, 1024], BF16)
        Btm_b = Buf()
        smF = self.S(es, "s_smF", [128, 4, 32], F32)
        EXPT, ACS, TMP, DTE = range(4)
        smF_b = [Buf() for _ in range(4)]
        smB = [self.S(es, "s_smB%d" % i, [128, 5, 32], F32) for i in range(2)]
        DTV, ADT, DFS, CD, W2 = range(5)
        smB_b = [[Buf() for _ in range(5)] for _ in range(2)]
        rhsD = [self.S(es, "s_rhsD%d" % i, [128, 512], F32) for i in range(2)]
        rhsD_b = [Buf(), Buf()]
        Ee = [self.S(es, "s_E%d" % i, [128, 512], F32) for i in range(2)]
        E_b = [Buf(), Buf()]
        CBm = [self.S(es, "s_CBm0", [128, 128], F32)] * 2
        _cb = Buf()
        CBm_b = [_cb, _cb]
        MT = [self.S(es, "s_MT%d" % i, [128, 512], BF16) for i in range(2)]
        MT_b = [Buf(), Buf()]
        xdt = [self.S(es, "s_xdt%d" % i, [128, 256], BF16) for i in range(2)]
        xdt_b = [Buf(), Buf()]
        xdtd = [self.S(es, "s_xdtd%d" % i, [128, 256], BF16) for i in range(2)]
        xdtd_b = [Buf(), Buf()]
        tt = [[self.S(es, "s_t%d_%d" % (k, i), [128, 256], F32) for i in range(2)] for k in range(4)]
        tt_b = [[Buf(), Buf()] for _ in range(4)]
        tt.append(tt[0])
        tt_b.append(tt_b[0])
        stg = self.S(es, "s_stg", [128, 8, 4], F32)
        stg_b = [Buf() for _ in range(8)]
        S32 = self.S(es, "s_S32", [128, 2048], F32)
        Sbf = self.S(es, "s_Sbf", [128, 2048], BF16)
        S32_b = [Buf() for _ in range(8)]
        Sbf_b = [Buf() for _ in range(8)]
        stmp = [self.S(es, "s_stmp0", [128, 256], F32)] * 2
        _sb = Buf()
        stmp_b = [_sb, _sb]
        kb.op(kb.pool, lambda: G.memset(S32[:], 0.0), w=S32_b)
        kb.op(kb.pool, lambda: G.memset(Sbf[:], 0.0), w=Sbf_b)
        v3 = lambda ap: ap.rearrange("p (j l) -> p j l", j=4)
        NTL = len(TILES)

        def f_load(ti):
            t0, n = TILES[ti]
            cur, prev = ti % 2, (ti + 1) % 2
            nprev = TILES[ti - 1][1] if ti > 0 else 0
            X = xnT[cur]
            kb.dma("sp", hs[cur][0:n, :], self.h_src(src, t0, n), w=[hs_b[cur]])
            self.norm_T(hs_b[cur], hs[cur][0:n, :], n, self.lnT[:, li, :], X[:, :, 3:3 + n], xnT_b[cur], nsc, newton=True)
            kb.op(kb.pool, lambda: G.tensor_copy(X[:, :, 0:3], xnT[prev][:, :, nprev:nprev + 3]), r=[xnT_b[prev]], wp=[xnT_b[cur]])

        def f_dt1(ti):
            t0, n = TILES[ti]
            cur = ti % 2
            X = xnT[cur]
            sB, sBb = smB[cur], smB_b[cur]
            pdt = fm[0:n, 0:32]
            for c in range(8):
                kb.op(kb.pe, lambda: T.matmul(pdt, X[:, c, 3:3 + n], w_in[:, c, 6144:6176], start=(c == 0), stop=(c == 7)),
                      r=[wdt_b, xnT_b[cur]], w=[fm_b] if c == 0 else [], wp=[fm_b] if c else [], inc=(c == 7))
            kb.op(kb.dve, lambda: V.tensor_tensor(sB[0:n, DTV, :], pdt, dtb[0:n, :], ALU.add), r=[fm_b, par_b], w=[sBb[DTV]])
            kb.op(kb.act, lambda: A.activation(out=smF[0:n, EXPT, :], in_=sB[0:n, DTV, :], func=AF.Exp), r=[sBb[DTV]], w=[smF_b[EXPT]])
            kb.op(kb.act, lambda: A.activation(out=sB[0:n, DTV, :], in_=smF[0:n, EXPT, :], func=AF.Ln, bias=1.0), r=[smF_b[EXPT]], w=[sBb[DTV]])
            kb.op(kb.dve, lambda: V.tensor_tensor(sB[0:n, ADT, :], sB[0:n, DTV, :], arep[0:n, :], ALU.mult), r=[sBb[DTV], arep_b], w=[sBb[ADT]])

        def f_dt2(ti):
            t0, n = TILES[ti]
            cur = ti % 2
            sB, sBb = smB[cur], smB_b[cur]
            pacs, ptot = fm[0:n, 32:64], fm[:, 64:96]
            kb.op(kb.pe, lambda: T.matmul(pacs, self.tri32[0:n, 0:n], sB[0:n, ADT, :], start=True, stop=True), r=[sBb[ADT], self.cb_], w=[fm_b])
            kb.op(kb.pe, lambda: T.matmul(ptot, self.ones32[0:n, :], sB[0:n, ADT, :], start=True, stop=True), r=[sBb[ADT], self.cb_], wp=[fm_b])
            kb.op(kb.dve, lambda: V.tensor_copy(smF[0:n, ACS, :], pacs), r=[fm_b], w=[smF_b[ACS]])
            kb.op(kb.act, lambda: A.activation(out=sB[0:n, DFS, :], in_=pacs, func=AF.Exp), r=[fm_b], w=[sBb[DFS]])
            kb.op(kb.act, lambda: A.activation(out=sB[:, CD, :], in_=ptot, func=AF.Exp), r=[fm_b], w=[sBb[CD]])
            kb.op(kb.dve, lambda: V.tensor_tensor(smF[0:n, TMP, :], ptot[0:n, :], smF[0:n, ACS, :], ALU.subtract), r=[fm_b, smF_b[ACS]], w=[smF_b[TMP]])
            kb.op(kb.act, lambda: A.activation(out=smF[0:n, DTE, :], in_=smF[0:n, TMP, :], func=AF.Exp), r=[smF_b[TMP]], w=[smF_b[DTE]])
            kb.op(kb.dve, lambda: V.tensor_tensor(sB[0:n, W2, :], sB[0:n, DTV, :], smF[0:n, DTE, :], ALU.mult), r=[sBb[DTV], smF_b[DTE]], w=[sBb[W2]])

        def conv_dst(ti, cc, n):
            cur = ti % 2
            if cc < 16:
                return xbc_xs[:, cc, 0:n], xsT_b[cc]
            return xbc_bc[cur][:, cc - 16, 0:n], bc_b[cur][cc - 16]

        def f_conv(ti, s):
            t0, n = TILES[ti]
            cur = ti % 2
            X = xnT[cur]
            cc = s
            if 0 <= cc < 32:
                p = cc % 2
                pz = pzs[p][:, 0:3 + n]
                for c in range(8):
                    kb.op(kb.pe, lambda: T.matmul(pz, w_in[:, c, 2048 + cc * 128:2048 + (cc + 1) * 128], X[:, c, 0:3 + n], start=(c == 0), stop=(c == 7)),
                          r=[wx_b[cc // 4], xnT_b[cur]], w=[pz_b[p]] if c == 0 else [], wp=[pz_b[p]] if c else [], inc=(c == 7))
            cc = s - 1
            if 0 <= cc < 32:
                p = cc % 2
                pz = pzs[p][:, 0:3 + n]
                kb.op(kb.act, lambda: A.activation(out=acc[cc % NA][:, 0:n], in_=pz[:, 3:3 + n], func=AF.Identity, bias=cb[:, cc:cc + 1], scale=cw[:, cc, 3:4]),
                      r=[pz_b[p], cwb], w=[acc_b[cc % NA]])
                kb.op(kb.act, lambda: A.copy(u[cc % NU][:, 0:3 + n], pz), r=[pz_b[p]], w=[u_b[cc % NU]])
            for k in range(3):
                cc = s - 2 - k
                if 0 <= cc < 32:
                    p = cc % NU
                    a_ = acc[cc % NA][:, 0:n]
                    kb.op(kb.dve, lambda: V.scalar_tensor_tensor(a_, u[p][:, k:k + n], cw[:, cc, k:k + 1], a_, ALU.mult, ALU.add),
                          r=[u_b[p], cwb], w=[acc_b[cc % NA]])
            cc = s - 5
            if 0 <= cc < 32:
                kb.op(kb.act, lambda: A.activation(out=th[cc % 2][:, 0:n], in_=acc[cc % NA][:, 0:n], func=AF.Tanh), r=[acc_b[cc % NA]], w=[th_b[cc % 2]])
            cc = s - 6
            if 0 <= cc < 32:
                dst_ap, dst_b = conv_dst(ti, cc, n)
                kb.op(kb.dve, lambda: V.scalar_tensor_tensor(dst_ap, th[cc % 2][:, 0:n], 1.0, acc[cc % NA][:, 0:n], ALU.add, ALU.mult),
                      r=[th_b[cc % 2], acc_b[cc % NA]], w=[dst_b])

        def front(ti):
            st = [partial(f_conv, ti, 0), partial(f_dt1, ti), partial(f_conv, ti, 1), partial(f_dt2, ti)]
            st += [partial(f_conv, ti, s) for s in range(2, 38)]
            return st

        def b_trx(ti, half):
            t0, n = TILES[ti]
            for k in range(8):
                cc = half * 8 + k
                kb.op(kb.pe, lambda: T.transpose(tpB16[0:n, k * 128:(k + 1) * 128], xbc_xs[:, cc, 0:n], self.ident[:, :]),
                      r=[xsT_b[cc], self.cb_], w=[tpB_b] if k == 0 else [], wp=[tpB_b] if k else [], inc=(k == 7))
            kb.op(kb.act, lambda: A.copy(xs_tm[0:n, half * 1024:(half + 1) * 1024], tpB16[0:n, :]), r=[tpB_b], w=xs_b[4 * half:4 * half + 4])

        def b_trB(ti):
            t0, n = TILES[ti]
            cur = ti % 2
            for g in range(8):
                kb.op(kb.pe, lambda: T.transpose(tpB16[0:n, g * 128:(g + 1) * 128], xbc_bc[cur][:, g, 0:n], self.ident[:, :]),
                      r=[bc_b[cur][g], self.cb_], w=[tpB_b] if g == 0 else [], wp=[tpB_b] if g else [], inc=(g == 7))
            kb.op(kb.dve, lambda: V.tensor_copy(B_tm[0:n, :], tpB16[0:n, :]), r=[tpB_b], w=[Btm_b])

        def gchain(ti, g):
            t0, n = TILES[ti]
            cur = ti % 2
            X = xnT[cur]
            sB, sBb = smB[cur], smB_b[cur]
            b2 = g % 2
            hsl = slice(4 * g, 4 * g + 4)
            gsl = slice(g * 256, (g + 1) * 256)
            BT, CT = xbc_bc[cur][:, g, 0:n], xbc_bc[cur][:, 8 + g, 0:n]
            BTb, CTb = bc_b[cur][g], bc_b[cur][8 + g]
            x3 = lambda ap: ap.rearrange("p (j f) -> p j f", j=4)
            rD = rhsD[b2][0:n, 0:4 * n]
            Ev = Ee[b2][0:n, 0:4 * n]
            MTv = MT[b2][0:n, 0:4 * n]
            pcb = bm[0:n, 0:n]
            pst = bm[:, 256:512]
            zq = fm[0:n, 256:512]
            xs3 = x3(xs_tm[0:n, gsl])
            t1, t2, t3, szg, thg = [tt[k][b2][0:n, :] for k in range(5)]
            t1b, t2b, t3b, szb, thb = [tt_b[k][b2] for k in range(5)]
            steps = []

            def s1():
                kb.op(kb.pool, lambda: G.tensor_tensor(v3(rD), bc(sB[0:n, ADT, hsl], [n, 4, n], 2), bc(self.tri32[0:n, 0:n], [n, 4, n], 1), ALU.mult),
                      r=[sBb[ADT], self.cb_], w=[rhsD_b[b2]])
                kb.op(kb.pool, lambda: G.tensor_tensor(x3(xdt[b2][0:n, :]), xs3, bc(sB[0:n, DTV, hsl], [n, 4, 64], 2), ALU.mult),
                      r=[xs_b[g], sBb[DTV]], w=[xdt_b[b2]])
            steps.append(s1)

            def s2():
                kb.op(kb.pe, lambda: T.matmul(dTb[0:n, 0:4 * n], self.ltri32[0:n, 0:n], rD, start=True, stop=True), r=[rhsD_b[b2], self.cb_], w=[dT_b])
                kb.op(kb.act, lambda: A.activation(out=Ev, in_=dTb[0:n, 0:4 * n], func=AF.Exp), r=[dT_b], w=[E_b[b2]])
                kb.op(kb.pool, lambda: G.tensor_tensor(x3(xdtd[b2][0:n, :]), xs3, bc(sB[0:n, W2, hsl], [n, 4, 64], 2), ALU.mult),
                      r=[xs_b[g], sBb[W2]], w=[xdtd_b[b2]])
            steps.append(s2)

            def s3():
                kb.op(kb.pe, lambda: T.matmul(pcb, BT, CT, start=True, stop=True), r=[BTb, CTb], w=[bm_b])
                kb.op(kb.dve, lambda: V.tensor_tensor(CBm[b2][0:n, 0:n], pcb, self.tri32[0:n, 0:n], ALU.mult), r=[bm_b, self.cb_], w=[CBm_b[b2]])
                for c in range(8):
                    kb.op(kb.pe, lambda: T.matmul(zq, X[:, c, 3:3 + n], w_in[:, c, g * 256:(g + 1) * 256], start=(c == 0), stop=(c == 7)),
                          r=[wz_b[g // 2], xnT_b[cur]], w=[fm_b] if c == 0 else [], wp=[fm_b] if c else [], inc=(c == 7))
                kb.op(kb.act, lambda: A.activation(out=thg, in_=zq, func=AF.Tanh, scale=0.5), r=[fm_b], w=[thb])
                kb.op(kb.dve, lambda: V.scalar_tensor_tensor(szg, thg, 1.0, zq, ALU.add, ALU.mult), r=[thb, fm_b], w=[szb])
                kb.op(kb.pool, lambda: G.tensor_tensor(v3(MTv), v3(Ev), bc(CBm[b2][0:n, 0:n], [n, 4, n], 1), ALU.mult),
                      r=[E_b[b2], CBm_b[b2]], w=[MT_b[b2]])
            steps.append(s3)

            def s4():
                for jj in range(4):
                    kb.op(kb.pe, lambda: T.matmul(yb[0:n, jj * 64:(jj + 1) * 64], MTv[:, jj * n:(jj + 1) * n], xdt[b2][0:n, jj * 64:(jj + 1) * 64], start=True, stop=True),
                          r=[MT_b[b2], xdt_b[b2]], w=[y_b] if jj == 0 else [], wp=[y_b] if jj else [], inc=False)
                kb.op(kb.pe, lambda: T.matmul(yb[0:n, 256:512], CT, Sbf[:, gsl], start=True, stop=True), r=[CTb, Sbf_b[g]], wp=[y_b])
                kb.op(kb.pool, lambda: G.tensor_tensor(x3(t3), xs3, bc(dsk[0:n, hsl], [n, 4, 64], 2), ALU.mult), r=[xs_b[g], par_b], w=[t3b])
                kb.op(kb.dve, lambda: V.tensor_tensor(x3(t1), x3(yb[0:n, 256:512]), bc(sB[0:n, DFS, hsl], [n, 4, 64], 2), ALU.mult),
                      r=[y_b, sBb[DFS]], w=[t1b])
                kb.op(kb.dve, lambda: V.tensor_tensor(t2, yb[0:n, 0:256], t1, ALU.add), r=[y_b, t1b], w=[t2b])
            steps.append(s4)

            def s5():
                kb.op(kb.pe, lambda: T.matmul(pst, B_tm[0:n, g * 128:(g + 1) * 128], xdtd[b2][0:n, :], start=True, stop=True),
                      r=[Btm_b, xdtd_b[b2]], w=[bm_b])
                kb.op(kb.pool, lambda: G.tensor_tensor(x3(stmp[b2][:, :]), x3(S32[:, gsl]), bc(sB[:, CD, hsl], [128, 4, 64], 2), ALU.mult),
                      r=[S32_b[g], sBb[CD]], w=[stmp_b[b2]])
                kb.op(kb.dve, lambda: V.tensor_tensor(S32[:, gsl], stmp[b2][:, :], pst, ALU.add), r=[stmp_b[b2], bm_b], w=[S32_b[g]])
                kb.op(kb.act, lambda: A.copy(Sbf[:, gsl], S32[:, gsl]), r=[S32_b[g]], w=[Sbf_b[g]])
            steps.append(s5)

            def s6():
                kb.op(kb.pool, lambda: G.tensor_tensor(t2, t2, t3, ALU.add), r=[t3b], w=[t2b])
                kb.op(kb.pool, lambda: G.tensor_tensor(t2, t2, szg, ALU.mult), r=[szb], w=[t2b])
            steps.append(s6)

            sq = lambda: kb.op(kb.act, lambda: A.activation(out=t1, in_=t2, func=AF.Square, accum_out=stg[0:n, g, 0:1]), r=[t2b], w=[t1b, stg_b[g]])
            newt = self.rstd_newton_ops(stg[:, g, :], stg_b[g], n, 1.0 / 256, 4.0 * EPS)
            gnf = lambda: kb.op(kb.dve, lambda: V.tensor_scalar(xs_tm[0:n, gsl], t2, stg[0:n, g, 2:3], None, ALU.mult), r=[t2b, stg_b[g]], w=[xs_b[g]])
            return steps, (sq, newt, gnf)

        def b_gT(ti, half):
            t0, n = TILES[ti]
            for k in range(8):
                cc = half * 8 + k
                kb.op(kb.pe, lambda: T.transpose(tpBg[:, k, 0:n], xs_tm[0:n, cc * 128:(cc + 1) * 128], self.ident[0:n, 0:n]),
                      r=[xs_b[cc // 2], self.cb_], w=[tpB_b] if k == 0 else [], wp=[tpB_b] if k else [], inc=(k == 7))
            kb.op(kb.dve, lambda: V.tensor_tensor(gT[:, half * 8:(half + 1) * 8, 0:n], tpBg[:, :, 0:n], bc(sng[:, half * 8:(half + 1) * 8], [128, 8, n], 2), ALU.mult),
                  r=[tpB_b, par_b], w=[gT_b] if half == 0 else [], wp=[gT_b] if half else [])

        def b_out(ti, hh):
            t0, n = TILES[ti]
            cur = ti % 2
            po = dTb[0:n, :] if hh == 0 else yb[0:n, :]
            pbuf = dT_b if hh == 0 else y_b
            for cc in range(16):
                kb.op(kb.pe, lambda: T.matmul(po, gT[:, cc, 0:n], w_out[:, cc, hh * 512:(hh + 1) * 512], start=(cc == 0), stop=(cc == 15)),
                      r=[gT_b, wout_b[cc // 4]], w=[pbuf] if cc == 0 else [], wp=[pbuf] if cc else [], inc=(cc == 15))
            kb.op(kb.dve, lambda: V.tensor_tensor(hs[cur][0:n, hh * 512:(hh + 1) * 512], po, hs[cur][0:n, hh * 512:(hh + 1) * 512], ALU.add),
                  r=[pbuf, hs_b[cur]], wp=[hs_b[cur]])
            if hh == 1:
                dap = self.h_dst(dst, t0, n)
                if dap is not None:
                    kb.dma("sp", dap, hs[cur][0:n, :], r=[hs_b[cur]])

        def back(ti):
            st = [partial(b_trx, ti, 0), partial(b_trx, ti, 1), partial(b_trB, ti)]
            for gp in range(4):
                (ca, ta), (cb_, tb) = gchain(ti, 2 * gp), gchain(ti, 2 * gp + 1)
                if SSD_GLAG:
                    st += ca[:SSD_GLAG]
                    for i in range(len(cb_)):
                        if i + SSD_GLAG < len(ca):
                            st.append(ca[i + SSD_GLAG])
                        st.append(cb_[i])
                else:
                    for a_, b_ in zip(ca, cb_):
                        st += [a_, b_]

                def tail(ta=ta, tb=tb):
                    ta[0]()
                    tb[0]()
                    for fa, fb in zip(ta[1], tb[1]):
                        fa()
                        fb()
                    ta[2]()
                    tb[2]()
                st.append(tail)
            st += [partial(b_gT, ti, 0), partial(b_gT, ti, 1), partial(b_out, ti, 0), partial(b_out, ti, 1)]
            return st

        def interleave(a, b, lead):
            na, nb = len(a), len(b)
            i = jx = 0
            while i < na or jx < nb:
                if jx >= nb or (i < na and i * nb <= jx * na * lead):
                    a[i]()
                    i += 1
                else:
                    b[jx]()
                    jx += 1

        f_load(0)
        for ti in range(NTL + 1):
            f = front(ti) if ti < NTL else []
            b = back(ti - 1) if ti >= 1 else []
            if ti + 1 < NTL:
                b = b + [partial(f_load, ti + 1)]
            interleave(f, b, SSD_LEAD)

    def phase_ssd_v1(self, es, j, li, src, dst):
        kb, nc = self.kb, self.nc
        V, A, G, T = nc.vector, nc.scalar, nc.gpsimd, nc.tensor
        w_in = self.S(es, "s_win", [128, 8, SSD_IN], BF16)
        w_out = self.S(es, "s_wout", [128, 16, D], BF16)
        win_b, wout_b, par_b = Buf(), Buf(), Buf()
        wv = self.w_ssd_in[j].rearrange("(c p) f -> p c f", p=128)
        for c in range(8):
            kb.dma("pool", w_in[:, c, :], wv[:, c, :], wp=[win_b])
        wov = self.w_ssd_out[j].rearrange("(c p) f -> p c f", p=128)
        for c in range(0, 16, 4):
            kb.dma("pool", w_out[:, c:c + 4, :], wov[:, c:c + 4, :], wp=[wout_b])
        cw = self.S(es, "s_cw", [128, 32, 4], F32)
        cb = self.S(es, "s_cb", [128, 32], F32)
        dtb = self.S(es, "s_dtb", [128, 32], F32)
        arep = self.S(es, "s_arep", [128, 32], F32)
        dsk = self.S(es, "s_dsk", [128, 32], F32)
        sng = self.S(es, "s_sng", [128, 16], F32)
        kb.dma("sp", cw[:], self.cw_d[j], wp=[par_b])
        kb.dma("sp", cb[:], self.cb_d[j], wp=[par_b])
        kb.dma("sp", dtb[:], self.dtb_d[j], wp=[par_b])
        kb.dma("sp", arep[:], self.alog_d[j], wp=[par_b])
        kb.dma("sp", dsk[:], self.dsk_d[j], wp=[par_b])
        kb.dma("sp", sng[:], self.sng_d[j], wp=[par_b])
        arep_b = Buf()
        kb.op(kb.act, lambda: A.activation(out=arep[:], in_=arep[:], func=AF.Exp), r=[par_b], w=[arep_b])
        kb.op(kb.dve, lambda: V.tensor_scalar(arep[:], arep[:], -1.0, None, ALU.mult), r=[arep_b], w=[arep_b])
        hs = [self.S(es, "s_h%d" % i, [128, D], F32) for i in range(2)]
        hs_b = [Buf(), Buf()]
        tp2 = self.P(es, "s_tp2", [128, 1024], F32)
        pzbs = [self.P(es, "s_pz%d" % i, [128, 512], F32) for i in range(2)]
        zqb = self.P(es, "s_zq", [128, 512], F32)
        dTb = self.P(es, "s_dT", [128, 512], F32)
        miscb = self.P(es, "s_misc", [128, 512], F32)
        yb = self.P(es, "s_y", [128, 512], F32)
        pz_b = [PB(), PB()]
        _z = PB()
        zq_b = [_z, _z]
        dT_b = PB()
        misc_b = PB()
        pdt_b = pacs_b = ptot_b = pst_b = misc_b
        pcb_b = [misc_b, misc_b]
        ydg_b = yof_b = PB()
        nsc = self.norm_scratch(es, "s", tp2)
        tp_b = nsc[7]
        tp16 = tp2[:].bitcast(BF16)
        tpg = tp16.rearrange("p (c t) -> p c t", c=16)
        xnT = [self.S(es, "s_xnT%d" % i, [128, 8, 131], BF16) for i in range(2)]
        xnT_b = [Buf(), Buf()]
        for i in range(2):
            kb.op(kb.pool, lambda: G.memset(xnT[i][:], 0.0), w=[xnT_b[i]])
        xbcT = self.S(es, "s_xbcT", [128, 32, 128], BF16)
        xbc_b = [Buf() for _ in range(32)]
        acc = [self.S(es, "s_acc%d" % i, [128, 128], F32) for i in range(3)]
        acc_b = [Buf() for _ in range(3)]
        xs_tm = self.S(es, "s_xstm", [128, 2048], BF16)
        xs_b = Buf()
        B_tm = self.S(es, "s_Btm", [128, 1024], BF16)
        Btm_b = Buf()
        sz = self.S(es, "s_sz", [128, 2048], F32)
        sz_b = [Buf() for _ in range(8)]
        sm = self.S(es, "s_sm", [128, 10, 32], F32)
        DTV, EXPT, ADT, ACS, DFS, CD, DTE, TMP, W2 = range(9)
        sm_b = [Buf() for _ in range(10)]
        rhsD = [self.S(es, "s_rhsD0", [128, 512], F32)] * 2
        _b = Buf()
        rhsD_b = [_b, _b]
        Ee = [self.S(es, "s_E0", [128, 512], F32)] * 2
        _b = Buf()
        E_b = [_b, _b]
        CBm = [self.S(es, "s_CBm%d" % i, [128, 128], F32) for i in range(2)]
        CBm_b = [Buf(), Buf()]
        MT = [self.S(es, "s_MT%d" % i, [128, 512], BF16) for i in range(2)]
        MT_b = [Buf(), Buf()]
        xdt = [self.S(es, "s_xdt%d" % i, [128, 256], BF16) for i in range(2)]
        xdt_b = [Buf(), Buf()]
        xdtd = [self.S(es, "s_xdtd%d" % i, [128, 256], BF16) for i in range(2)]
        xdtd_b = [Buf(), Buf()]
        tt = [[self.S(es, "s_t%d_%d" % (k, i), [128, 256], F32) for i in range(2)] for k in range(3)]
        tt_b = [[Buf(), Buf()] for _ in range(3)]
        tt.append(tt[0])
        tt_b.append(tt_b[0])
        stg = self.S(es, "s_stg", [128, 8, 4], F32)
        stg_b = [Buf() for _ in range(8)]
        gn = self.S(es, "s_gn", [128, 2048], BF16)
        gn_b = Buf()
        gT = self.S(es, "s_gT", [128, 16, 128], BF16)
        gT_b = Buf()
        S32 = self.S(es, "s_S32", [128, 2048], F32)
        Sbf = self.S(es, "s_Sbf", [128, 2048], BF16)
        S32_b = [Buf() for _ in range(8)]
        Sbf_b = [Buf() for _ in range(8)]
        stmp = [self.S(es, "s_stmp0", [128, 256], F32)] * 2
        _b = Buf()
        stmp_b = [_b, _b]
        kb.op(kb.pool, lambda: G.memset(S32[:], 0.0), w=S32_b)
        kb.op(kb.pool, lambda: G.memset(Sbf[:], 0.0), w=Sbf_b)
        nprev = 0
        ipz = 0
        for ti, (t0, n) in enumerate(TILES):
            cur, prev = ti % 2, (ti + 1) % 2
            X = xnT[cur]
            kb.dma("sp", hs[cur][0:n, :], self.h_src(src, t0, n), w=[hs_b[cur]])
            self.norm_T(hs_b[cur], hs[cur][0:n, :], n, self.lnT[:, li, :], X[:, :, 3:3 + n], xnT_b[cur], nsc)
            kb.op(kb.pool, lambda: G.tensor_copy(X[:, :, 0:3], xnT[prev][:, :, nprev:nprev + 3]), r=[xnT_b[prev]], wp=[xnT_b[cur]])
            nprev = n
            for cc in range(32):
                p = ipz % 2
                ipz += 1
                pz = pzbs[p][:, 0:3 + n]
                for c in range(8):
                    kb.op(kb.pe, lambda: T.matmul(pz, w_in[:, c, 2048 + cc * 128:2048 + (cc + 1) * 128], X[:, c, 0:3 + n], start=(c == 0), stop=(c == 7)),
                          r=[win_b, xnT_b[cur]], w=[pz_b[p]] if c == 0 else [], wp=[pz_b[p]] if c else [], inc=(c == 7))
                a_ = acc[p][:, 0:n]
                kb.op(kb.act, lambda: A.activation(out=a_, in_=pz[:, 3:3 + n], func=AF.Identity, bias=cb[:, cc:cc + 1], scale=cw[:, cc, 3:4]),
                      r=[pz_b[p], par_b], w=[acc_b[p]])
                for k in range(3):
                    kb.op(kb.dve, lambda: V.scalar_tensor_tensor(a_, pz[:, k:k + n], cw[:, cc, k:k + 1], a_, ALU.mult, ALU.add),
                          r=[pz_b[p], par_b], w=[acc_b[p]])
                kb.op(kb.act, lambda: A.activation(out=xbcT[:, cc, 0:n], in_=a_, func=AF.Silu), r=[acc_b[p]], w=[xbc_b[cc]])
            for g in range(8):
                zs = g % 2
                zq = zqb[0:n, zs * 256:(zs + 1) * 256]
                for c in range(8):
                    kb.op(kb.pe, lambda: T.matmul(zq, X[:, c, 3:3 + n], w_in[:, c, g * 256:(g + 1) * 256], start=(c == 0), stop=(c == 7)),
                          r=[win_b, xnT_b[cur]], w=[zq_b[zs]] if c == 0 else [], wp=[zq_b[zs]] if c else [], inc=(c == 7))
                kb.op(kb.act, lambda: A.activation(out=sz[0:n, g * 256:(g + 1) * 256], in_=zq, func=AF.Silu), r=[zq_b[zs]], w=[sz_b[g]])
            for cc in range(16):
                kb.op(kb.pe, lambda: T.transpose(tp16[0:n, cc * 128:(cc + 1) * 128], xbcT[:, cc, 0:n], self.ident[:, :]),
                      r=[xbc_b[cc], self.cb_], w=[tp_b] if cc == 0 else [], wp=[tp_b] if cc else [], inc=(cc == 15))
            kb.op(kb.act, lambda: A.copy(xs_tm[0:n, 0:1024], tp16[0:n, 0:1024]), r=[tp_b], w=[xs_b])
            kb.op(kb.dve, lambda: V.tensor_copy(xs_tm[0:n, 1024:2048], tp16[0:n, 1024:2048]), r=[tp_b], wp=[xs_b])
            for g in range(8):
                kb.op(kb.pe, lambda: T.transpose(tp16[0:n, g * 128:(g + 1) * 128], xbcT[:, 16 + g, 0:n], self.ident[:, :]),
                      r=[xbc_b[16 + g], self.cb_], w=[tp_b] if g == 0 else [], wp=[tp_b] if g else [], inc=(g == 7))
            kb.op(kb.act, lambda: A.copy(B_tm[0:n, :], tp16[0:n, 0:1024]), r=[tp_b], w=[Btm_b])
            pdt, pacs, ptot = miscb[0:n, 0:32], miscb[0:n, 32:64], miscb[:, 64:96]
            for c in range(8):
                kb.op(kb.pe, lambda: T.matmul(pdt, X[:, c, 3:3 + n], w_in[:, c, 6144:6176], start=(c == 0), stop=(c == 7)),
                      r=[win_b, xnT_b[cur]], w=[pdt_b] if c == 0 else [], wp=[pdt_b] if c else [], inc=(c == 7))
            smv = lambda k: sm[0:n, k, :]
            kb.op(kb.dve, lambda: V.tensor_tensor(smv(DTV), pdt, dtb[0:n, :], ALU.add), r=[pdt_b, par_b], w=[sm_b[DTV]])
            kb.op(kb.act, lambda: A.activation(out=smv(EXPT), in_=smv(DTV), func=AF.Exp), r=[sm_b[DTV]], w=[sm_b[EXPT]])
            kb.op(kb.act, lambda: A.activation(out=smv(DTV), in_=smv(EXPT), func=AF.Ln, bias=1.0), r=[sm_b[EXPT]], w=[sm_b[DTV]])
            kb.op(kb.dve, lambda: V.tensor_tensor(smv(ADT), smv(DTV), arep[0:n, :], ALU.mult), r=[sm_b[DTV], arep_b], w=[sm_b[ADT]])
            kb.op(kb.pe, lambda: T.matmul(pacs, self.tri32[0:n, 0:n], smv(ADT), start=True, stop=True), r=[sm_b[ADT], self.cb_], w=[pacs_b])
            kb.op(kb.pe, lambda: T.matmul(ptot, self.ones32[0:n, :], smv(ADT), start=True, stop=True), r=[sm_b[ADT], self.cb_], w=[ptot_b])
            kb.op(kb.dve, lambda: V.tensor_copy(smv(ACS), pacs), r=[pacs_b], w=[sm_b[ACS]])
            kb.op(kb.act, lambda: A.activation(out=smv(DFS), in_=pacs, func=AF.Exp), r=[pacs_b], w=[sm_b[DFS]])
            kb.op(kb.act, lambda: A.activation(out=sm[:, CD, :], in_=ptot, func=AF.Exp), r=[ptot_b], w=[sm_b[CD]])
            kb.op(kb.dve, lambda: V.tensor_tensor(smv(TMP), ptot[0:n, :], smv(ACS), ALU.subtract), r=[ptot_b, sm_b[ACS]], w=[sm_b[TMP]])
            kb.op(kb.act, lambda: A.activation(out=smv(DTE), in_=smv(TMP), func=AF.Exp), r=[sm_b[TMP]], w=[sm_b[DTE]])
            kb.op(kb.dve, lambda: V.tensor_tensor(smv(W2), smv(DTV), smv(DTE), ALU.mult), r=[sm_b[DTV], sm_b[DTE]], w=[sm_b[W2]])
            for g in range(8):
                b2 = g % 2
                hsl = slice(4 * g, 4 * g + 4)
                gsl = slice(g * 256, (g + 1) * 256)
                v3 = lambda ap: ap.rearrange("p (j l) -> p j l", j=4)
                rD = rhsD[b2][0:n, 0:4 * n]
                kb.op(kb.pool, lambda: G.tensor_tensor(v3(rD), bc(sm[0:n, ADT, hsl], [n, 4, n], 2), bc(self.tri32[0:n, 0:n], [n, 4, n], 1), ALU.mult),
                      r=[sm_b[ADT], self.cb_], w=[rhsD_b[b2]])
                kb.op(kb.pe, lambda: T.matmul(dTb[0:n, 0:4 * n], self.ltri32[0:n, 0:n], rD, start=True, stop=True), r=[rhsD_b[b2], self.cb_], w=[dT_b])
                Ev = Ee[b2][0:n, 0:4 * n]
                kb.op(kb.act, lambda: A.activation(out=Ev, in_=dTb[0:n, 0:4 * n], func=AF.Exp), r=[dT_b], w=[E_b[b2]])
                pcb = miscb[0:n, 128:128 + n]
                kb.op(kb.pe, lambda: T.matmul(pcb, xbcT[:, 16 + g, 0:n], xbcT[:, 24 + g, 0:n], start=True, stop=True),
                      r=[xbc_b[16 + g], xbc_b[24 + g]], w=[pcb_b[b2]])
                kb.op(kb.dve, lambda: V.tensor_tensor(CBm[b2][0:n, 0:n], pcb, self.tri32[0:n, 0:n], ALU.mult), r=[pcb_b[b2], self.cb_], w=[CBm_b[b2]])
                MTv = MT[b2][0:n, 0:4 * n]
                kb.op(kb.pool, lambda: G.tensor_tensor(v3(MTv), v3(Ev), bc(CBm[b2][0:n, 0:n], [n, 4, n], 1), ALU.mult),
                      r=[E_b[b2], CBm_b[b2]], w=[MT_b[b2]])
                xs3 = xs_tm[0:n, gsl].rearrange("p (j f) -> p j f", j=4)
                x3 = lambda ap: ap.rearrange("p (j f) -> p j f", j=4)
                kb.op(kb.pool, lambda: G.tensor_tensor(x3(xdt[b2][0:n, :]), xs3, bc(sm[0:n, DTV, hsl], [n, 4, 64], 2), ALU.mult),
                      r=[xs_b, sm_b[DTV]], w=[xdt_b[b2]])
                kb.op(kb.pool, lambda: G.tensor_tensor(x3(xdtd[b2][0:n, :]), xs3, bc(sm[0:n, W2, hsl], [n, 4, 64], 2), ALU.mult),
                      r=[xs_b, sm_b[W2]], w=[xdtd_b[b2]])
                for jj in range(4):
                    kb.op(kb.pe, lambda: T.matmul(yb[0:n, jj * 64:(jj + 1) * 64], MTv[:, jj * n:(jj + 1) * n], xdt[b2][0:n, jj * 64:(jj + 1) * 64], start=True, stop=True),
                          r=[MT_b[b2], xdt_b[b2]], w=[ydg_b] if jj == 0 else [], wp=[ydg_b] if jj else [], inc=(jj == 3))
                kb.op(kb.pe, lambda: T.matmul(yb[0:n, 256:512], xbcT[:, 24 + g, 0:n], Sbf[:, gsl], start=True, stop=True),
                      r=[xbc_b[24 + g], Sbf_b[g]], w=[yof_b])
                t1, t2, t3, tj = [tt[k][b2][0:n, :] for k in range(4)]
                kb.op(kb.dve, lambda: V.tensor_tensor(x3(t1), x3(yb[0:n, 256:512]), bc(sm[0:n, DFS, hsl], [n, 4, 64], 2), ALU.mult),
                      r=[yof_b, sm_b[DFS]], w=[tt_b[0][b2]])
                kb.op(kb.dve, lambda: V.tensor_tensor(t2, yb[0:n, 0:256], t1, ALU.add), r=[ydg_b, tt_b[0][b2]], w=[tt_b[1][b2]])
                kb.op(kb.pool, lambda: G.tensor_tensor(x3(t3), xs3, bc(dsk[0:n, hsl], [n, 4, 64], 2), ALU.mult), r=[xs_b, par_b], w=[tt_b[2][b2]])
                kb.op(kb.pool, lambda: G.tensor_tensor(t2, t2, t3, ALU.add), r=[tt_b[2][b2]], w=[tt_b[1][b2]])
                kb.op(kb.pool, lambda: G.tensor_tensor(t2, t2, sz[0:n, gsl], ALU.mult), r=[sz_b[g]], w=[tt_b[1][b2]])
                kb.op(kb.act, lambda: A.activation(out=tj, in_=t2, func=AF.Square, accum_out=stg[0:n, g, 0:1]), r=[tt_b[1][b2]], w=[tt_b[3][b2], stg_b[g]])
                self.rstd_ops(stg[:, g, :], stg_b[g], n, 0, 1.0 / 256)
                kb.op(kb.dve, lambda: V.tensor_scalar(gn[0:n, gsl], t2, stg[0:n, g, 2:3], None, ALU.mult), r=[tt_b[1][b2], stg_b[g]],
                      w=[gn_b] if g == 0 else [], wp=[gn_b] if g else [])
                kb.op(kb.pe, lambda: T.matmul(miscb[:, 256:512], B_tm[0:n, g * 128:(g + 1) * 128], xdtd[b2][0:n, :], start=True, stop=True),
                      r=[Btm_b, xdtd_b[b2]], w=[pst_b])
                kb.op(kb.pool, lambda: G.tensor_tensor(x3(stmp[b2][:, :]), x3(S32[:, gsl]), bc(sm[:, CD, hsl], [128, 4, 64], 2), ALU.mult),
                      r=[S32_b[g], sm_b[CD]], w=[stmp_b[b2]])
                kb.op(kb.dve, lambda: V.tensor_tensor(S32[:, gsl], stmp[b2][:, :], miscb[:, 256:512], ALU.add), r=[stmp_b[b2], pst_b], w=[S32_b[g]])
                kb.op(kb.act, lambda: A.copy(Sbf[:, gsl], S32[:, gsl]), r=[S32_b[g]], w=[Sbf_b[g]])
            for cc in range(16):
                kb.op(kb.pe, lambda: T.transpose(tpg[:, cc, 0:n], gn[0:n, cc * 128:(cc + 1) * 128], self.ident[0:n, 0:n]),
                      r=[gn_b, self.cb_], w=[tp_b] if cc == 0 else [], wp=[tp_b] if cc else [], inc=(cc == 15))
            kb.op(kb.dve, lambda: V.tensor_tensor(gT[:, :, 0:n], tpg[:, :, 0:n], bc(sng[:, :], [128, 16, n], 2), ALU.mult), r=[tp_b, par_b], w=[gT_b])
            for hh in range(2):
                po = dTb[0:n, :] if hh == 0 else yb[0:n, :]
                pbufs = [dT_b] if hh == 0 else [ydg_b]
                for cc in range(16):
                    kb.op(kb.pe, lambda: T.matmul(po, gT[:, cc, 0:n], w_out[:, cc, hh * 512:(hh + 1) * 512], start=(cc == 0), stop=(cc == 15)),
                          r=[gT_b, wout_b], w=pbufs if cc == 0 else [], wp=pbufs if cc else [], inc=(cc == 15))
                kb.op(kb.dve, lambda: V.tensor_tensor(hs[cur][0:n, hh * 512:(hh + 1) * 512], po, hs[cur][0:n, hh * 512:(hh + 1) * 512], ALU.add),
                      r=pbufs + [hs_b[cur]], wp=[hs_b[cur]])
            dap = self.h_dst(dst, t0, n)
            if dap is not None:
                kb.dma("sp", dap, hs[cur][0:n, :], r=[hs_b[cur]])

    def rstd_ops(self, st, st_b, n, c0, inv_n):
        kb, nc = self.kb, self.nc
        kb.op(kb.act, lambda: nc.scalar.activation(out=st[0:n, c0 + 1:c0 + 2], in_=st[0:n, c0:c0 + 1], func=AF.Ln, bias=EPS, scale=inv_n),
              r=[st_b], wp=[st_b])
        kb.op(kb.act, lambda: nc.scalar.activation(out=st[0:n, c0 + 2:c0 + 3], in_=st[0:n, c0 + 1:c0 + 2], func=AF.Exp, scale=-0.5),
              r=[st_b], wp=[st_b])

    def phase_mla(self, es, j, li, src, dst):
        kb, nc = self.kb, self.nc
        V, A, G, T = nc.vector, nc.scalar, nc.gpsimd, nc.tensor
        oT = self.S(es, "oT", [128, 8, NT], BF16)
        oT_b = Buf()
        w_o = self.S(es, "w_o", [128, 8, D], BF16)
        w_o_b = Buf()
        gq = self.S(es, "gq", [128, 96], F32)
        gk = self.S(es, "gk", [128, 96], F32)
        par_b = Buf()
        self.tri16 = self.S(es, "tri16", [128, 128], BF16)
        kb.dma("pool", self.tri16[:], self.tri_d, wp=[par_b])
        kb.dma("sp", gq[:], self.gq_d[j], wp=[par_b])
        kb.dma("sp", gk[:], self.gk_d[j], wp=[par_b])
        with contextlib.ExitStack() as s1:
            w_in = self.S(s1, "a_win", [128, 8, 672], BF16)
            w_qb = self.S(s1, "a_wqb", [128, 3, 1536], BF16)
            w_kvb = self.S(s1, "a_wkvb", [128, 2, 2048], BF16)
            wb = Buf()
            kb.dma("pool", w_in[:], self.w_mla_in[j].rearrange("(c p) f -> p c f", p=128), wp=[wb])
            kb.dma("pool", w_qb[:], self.w_mla_qb[j].rearrange("(c p) f -> p c f", p=128), wp=[wb])
            kb.dma("pool", w_kvb[:], self.w_mla_kvb[j].rearrange("(c p) f -> p c f", p=128), wp=[wb])
            kb.dma("pool", w_o[:], self.w_mla_out[j].rearrange("(c p) f -> p c f", p=128), wp=[w_o_b])
            qag = self.S(s1, "a_qag", [128, 3], F32)
            kvag = self.S(s1, "a_kvag", [128, 2], F32)
            kb.dma("sp", qag[:], self.qag_d[j], wp=[par_b])
            kb.dma("sp", kvag[:], self.kvag_d[j], wp=[par_b])
            tp_ps = self.P(s1, "a_tp", [128, 512], F32)
            latA = self.P(s1, "a_latA", [128, 512], F32)
            latB = self.P(s1, "a_latB", [128, 512], F32)
            big = self.P(s1, "a_big", [128, 2048], F32)
            tph_ps = self.P(s1, "a_tph", [128, 512], F32)
            latA_b, latB_b, tph_b = PB(), PB(), PB()
            big_b = [PB() for _ in range(4)]
            nsc = self.norm_scratch(s1, "a", tp_ps)
            tp5 = tp_ps[:].bitcast(BF16)[:, 0:640].rearrange("p (c t) -> p c t", c=5)
            tp_b = nsc[7]
            tph = tph_ps[:].bitcast(BF16)[:, 0:1024].rearrange("p (c t) -> p c t", c=8)
            qTv = self.qT_d.rearrange("h p t -> p h t")
            kTv = self.kT_d.rearrange("h p t -> p h t")

            class NS:
                pass

            def mkbufs(ci):
                b = NS()
                nm = lambda x: "a%d_%s" % (ci, x)
                b.hs = self.S(s1, nm("h"), [128, D], F32); b.hs_b = Buf()
                b.cs = self.S(s1, nm("cs"), [128, 2, 16], F32); b.cs_b = Buf()
                b.xnT = self.S(s1, nm("xnT"), [128, 8, 128], BF16); b.xnT_b = Buf()
                b.st = self.S(s1, nm("st"), [128, 8], F32); b.st_b = Buf()
                b.qln = self.S(s1, nm("qln"), [128, 384], BF16)
                b.kvln = self.S(s1, nm("kvln"), [128, 256], BF16)
                b.kpe = self.S(s1, nm("kpe"), [128, 32], F32); b.ln_b = Buf()
                b.sqj = self.S(s1, nm("sqj"), [128, 2048], F32); b.sqj_b = Buf()
                b.qlT = self.S(s1, nm("qlT"), [128, 3, 128], BF16)
                b.kvlT = self.S(s1, nm("kvlT"), [128, 2, 128], BF16); b.lT_b = Buf()
                b.raw = self.S(s1, nm("raw"), [128, 2048], F32); b.raw_b = Buf()
                b.s16 = self.S(s1, nm("s16"), [128, 3, 16], F32); b.s16_b = Buf()
                b.rt = self.S(s1, nm("rt"), [128, 4, 256], F32); b.rt_b = [Buf() for _ in range(4)]
                b.kpg = self.S(s1, nm("kpg"), [128, 2, 32], F32); b.kpg_b = Buf()
                b.qbf = self.S(s1, nm("qbf"), [128, 1536], BF16); b.qbf_b = Buf()
                b.stg = [self.S(s1, nm("stg%d" % i), [96, 16, 128], BF16) for i in range(2)]; b.stg_b = [Buf(), Buf()]
                b.vst = self.S(s1, nm("vst"), [128, 8, 192], BF16); b.vst_b = Buf()
                kb.op(kb.pool, lambda: G.memset(b.vst[:], 1.0), w=[b.vst_b])
                return b

            CH = [mkbufs(0), mkbufs(1)]

            def chain(ti):
                t0, n = TILES[ti]
                b = CH[ti % 2]
                xnT, st, st_b, qln, kvln, kpe, ln_b = b.xnT, b.st, b.st_b, b.qln, b.kvln, b.kpe, b.ln_b
                sqj, sqj_b, qlT, kvlT, lT_b, raw, raw_b = b.sqj, b.sqj_b, b.qlT, b.kvlT, b.lT_b, b.raw, b.raw_b
                s16, s16_b, rt, rt_b, kpg, kpg_b, qbf, qbf_b = b.s16, b.s16_b, b.rt, b.rt_b, b.kpg, b.kpg_b, b.qbf, b.qbf_b
                raw3 = raw[0:n, 0:1536].rearrange("p (h f) -> p h f", h=16)
                sq3 = sqj[0:n, 0:1536].rearrange("p (h f) -> p h f", h=16)
                qb3 = qbf[0:n, :].rearrange("p (h f) -> p h f", h=16)
                kv4 = raw[0:n, :].rearrange("p (h f) -> p h f", h=16)
                sqk = sqj[0:n, 0:1024].rearrange("p (h f) -> p h f", h=16)
                kv5 = raw[0:n, :].rearrange("p (c e f) -> p c e f", c=8, e=2)

                def head_T(sg, dstv):
                    for half in range(2):
                        for hh in range(8):
                            h = half * 8 + hh
                            kb.op(kb.pe, lambda: T.transpose(tph[0:96, hh, 0:n], qbf[0:n, h * 96:(h + 1) * 96], self.ident[0:n, 0:n]),
                                  r=[qbf_b, self.cb_], w=[tph_b] if hh == 0 else [], wp=[tph_b] if hh else [], inc=(hh == 7))
                        kb.op(kb.act, lambda: A.copy(b.stg[sg][0:96, half * 8:(half + 1) * 8, 0:n], tph[0:96, :, 0:n]),
                              r=[tph_b], w=[b.stg_b[sg]] if half == 0 else [], wp=[b.stg_b[sg]] if half else [])
                    kb.dma("sp", dstv[:, :, t0:t0 + n], b.stg[sg][0:96, :, 0:n], r=[b.stg_b[sg]])

                def rope(t1, t2, cosb, sinb, o1, o2, three_d, wb_, rd_bufs):
                    if three_d:
                        a_, b_, c_, d_ = [rt[0:n, i, :].rearrange("p (h f) -> p h f", h=16) for i in range(4)]
                    else:
                        a_, b_, c_, d_ = [rt[0:n, i, 0:16] for i in range(4)]
                    kb.op(kb.dve, lambda: V.tensor_tensor(a_, t1, cosb, ALU.mult), r=rd_bufs, w=[rt_b[0]])
                    kb.op(kb.dve, lambda: V.tensor_tensor(b_, t2, sinb, ALU.mult), r=rd_bufs, w=[rt_b[1]])
                    kb.op(kb.dve, lambda: V.tensor_tensor(c_, t1, sinb, ALU.mult), r=rd_bufs, w=[rt_b[2]])
                    kb.op(kb.dve, lambda: V.tensor_tensor(d_, t2, cosb, ALU.mult), r=rd_bufs, w=[rt_b[3]])
                    kb.op(kb.dve, lambda: V.tensor_tensor(o1, a_, b_, ALU.subtract), r=[rt_b[0], rt_b[1]], wp=[wb_])
                    kb.op(kb.dve, lambda: V.tensor_tensor(o2, c_, d_, ALU.add), r=[rt_b[2], rt_b[3]], wp=[wb_])

                def s0():
                    kb.dma("sp", b.hs[0:n, :], self.h_src(src, t0, n), w=[b.hs_b])

                def s0b():
                    kb.dma("sp", b.cs[0:n, 0, :], self.cos_d[t0:t0 + n, :], w=[b.cs_b])
                    kb.dma("sp", b.cs[0:n, 1, :], self.sin_d[t0:t0 + n, :], wp=[b.cs_b])

                def s1():
                    self.norm_T(b.hs_b, b.hs[0:n, :], n, self.lnT[:, li, :], xnT[:, :, 0:n], b.xnT_b, nsc)

                def s2():
                    for c in range(8):
                        kb.op(kb.pe, lambda: T.matmul(latA[0:n, 0:384], xnT[:, c, 0:n], w_in[:, c, 0:384], start=(c == 0), stop=(c == 7)),
                              r=[b.xnT_b, wb], w=[latA_b] if c == 0 else [], wp=[latA_b] if c else [], inc=(c == 7))
                    for c in range(8):
                        kb.op(kb.pe, lambda: T.matmul(latB[0:n, 0:288], xnT[:, c, 0:n], w_in[:, c, 384:672], start=(c == 0), stop=(c == 7)),
                              r=[b.xnT_b, wb], w=[latB_b] if c == 0 else [], wp=[latB_b] if c else [], inc=(c == 7))
                    kb.op(kb.act, lambda: A.activation(out=sqj[0:n, 0:384], in_=latA[0:n, 0:384], func=AF.Square, accum_out=st[0:n, 0:1]),
                          r=[latA_b], w=[sqj_b, st_b])
                    kb.op(kb.act, lambda: A.activation(out=sqj[0:n, 512:768], in_=latB[0:n, 0:256], func=AF.Square, accum_out=st[0:n, 3:4]),
                          r=[latB_b], wp=[sqj_b, st_b])
                    self.rstd_ops(st, st_b, n, 0, 1.0 / 384)
                    self.rstd_ops(st, st_b, n, 3, 1.0 / 256)
                    kb.op(kb.dve, lambda: V.tensor_scalar(qln[0:n, :], latA[0:n, 0:384], st[0:n, 2:3], None, ALU.mult), r=[latA_b, st_b], w=[ln_b])
                    kb.op(kb.dve, lambda: V.tensor_scalar(kvln[0:n, :], latB[0:n, 0:256], st[0:n, 5:6], None, ALU.mult), r=[latB_b, st_b], wp=[ln_b])
                    kb.op(kb.act, lambda: A.copy(kpe[0:n, :], latB[0:n, 256:288]), r=[latB_b], wp=[ln_b])

                def s3():
                    for c in range(5):
                        srcap = qln[0:n, c * 128:(c + 1) * 128] if c < 3 else kvln[0:n, (c - 3) * 128:(c - 2) * 128]
                        kb.op(kb.pe, lambda: T.transpose(tp5[:, c, 0:n], srcap, self.ident[0:n, 0:n]),
                              r=[ln_b, self.cb_], w=[tp_b] if c == 0 else [], wp=[tp_b] if c else [], inc=(c == 4))
                    kb.op(kb.dve, lambda: V.tensor_tensor(qlT[:, :, 0:n], tp5[:, 0:3, 0:n], bc(qag[:, :], [128, 3, n], 2), ALU.mult),
                          r=[tp_b, par_b], w=[lT_b])
                    kb.op(kb.dve, lambda: V.tensor_tensor(kvlT[:, :, 0:n], tp5[:, 3:5, 0:n], bc(kvag[:, :], [128, 2, n], 2), ALU.mult),
                          r=[tp_b, par_b], wp=[lT_b])

                def s4():
                    for ct in range(3):
                        for kc in range(3):
                            kb.op(kb.pe, lambda: T.matmul(big[0:n, ct * 512:(ct + 1) * 512], qlT[:, kc, 0:n], w_qb[:, kc, ct * 512:(ct + 1) * 512],
                                                          start=(kc == 0), stop=(kc == 2)),
                                  r=[lT_b, wb], w=[big_b[ct]] if kc == 0 else [], wp=[big_b[ct]] if kc else [], inc=(kc == 2))
                    for ct in range(3):
                        E_ = kb.act if ct != 1 else kb.dve
                        fn = (lambda: A.copy(raw[0:n, ct * 512:(ct + 1) * 512], big[0:n, ct * 512:(ct + 1) * 512])) if ct != 1 else \
                             (lambda: V.tensor_copy(raw[0:n, ct * 512:(ct + 1) * 512], big[0:n, ct * 512:(ct + 1) * 512]))
                        kb.op(E_, fn, r=[big_b[ct]], w=[raw_b] if ct == 0 else [], wp=[raw_b] if ct else [])

                def s5():
                    kb.op(kb.dve, lambda: V.tensor_tensor(sqj[0:n, 0:1536], raw[0:n, 0:1536], raw[0:n, 0:1536], ALU.mult), r=[raw_b], w=[sqj_b])
                    kb.op(kb.dve, lambda: V.tensor_reduce(s16[0:n, 0, :], sq3, AX.X, ALU.add), r=[sqj_b], w=[s16_b])
                    kb.op(kb.act, lambda: A.activation(out=s16[0:n, 1, :], in_=s16[0:n, 0, :], func=AF.Ln, bias=EPS, scale=1.0 / 96), r=[s16_b], wp=[s16_b])
                    kb.op(kb.act, lambda: A.activation(out=s16[0:n, 2, :], in_=s16[0:n, 1, :], func=AF.Exp, scale=-0.5), r=[s16_b], wp=[s16_b])
                    kb.op(kb.dve, lambda: V.tensor_tensor(raw3, raw3, bc(s16[0:n, 2, :], [n, 16, 96], 2), ALU.mult), r=[s16_b], w=[raw_b])
                    kb.op(kb.dve, lambda: V.tensor_tensor(raw3, raw3, bc(gq[0:n, :], [n, 16, 96], 1), ALU.mult), r=[par_b], w=[raw_b])
                    cosb = bc(b.cs[0:n, 0, :], [n, 16, 16], 1)
                    sinb = bc(b.cs[0:n, 1, :], [n, 16, 16], 1)
                    kb.op(kb.act, lambda: A.copy(qb3[:, :, 0:64], raw3[:, :, 0:64]), r=[raw_b], w=[qbf_b])
                    rope(raw3[:, :, 64:80], raw3[:, :, 80:96], cosb, sinb, qb3[:, :, 64:80], qb3[:, :, 80:96], True, qbf_b, [raw_b, b.cs_b])

                def s6():
                    head_T(0, qTv)

                def s7():
                    for ct in range(4):
                        for kc in range(2):
                            kb.op(kb.pe, lambda: T.matmul(big[0:n, ct * 512:(ct + 1) * 512], kvlT[:, kc, 0:n], w_kvb[:, kc, ct * 512:(ct + 1) * 512],
                                                          start=(kc == 0), stop=(kc == 1)),
                                  r=[lT_b, wb], w=[big_b[ct]] if kc == 0 else [], wp=[big_b[ct]] if kc else [], inc=(kc == 1))
                    for ct in range(4):
                        E_ = kb.act if ct % 2 == 0 else kb.dve
                        fn = (lambda: A.copy(raw[0:n, ct * 512:(ct + 1) * 512], big[0:n, ct * 512:(ct + 1) * 512])) if ct % 2 == 0 else \
                             (lambda: V.tensor_copy(raw[0:n, ct * 512:(ct + 1) * 512], big[0:n, ct * 512:(ct + 1) * 512]))
                        kb.op(E_, fn, r=[big_b[ct]], w=[raw_b] if ct == 0 else [], wp=[raw_b] if ct else [])

                def s8():
                    kb.op(kb.dve, lambda: V.tensor_tensor(sqk, kv4[:, :, 0:64], kv4[:, :, 0:64], ALU.mult), r=[raw_b], w=[sqj_b])
                    kb.op(kb.dve, lambda: V.tensor_reduce(s16[0:n, 0, :], sqk, AX.X, ALU.add), r=[sqj_b], w=[s16_b])
                    kb.op(kb.act, lambda: A.activation(out=kpg[0:n, 1, :], in_=kpe[0:n, :], func=AF.Square, accum_out=st[0:n, 6:7]),
                          r=[ln_b], w=[kpg_b], wp=[st_b])
                    kb.op(kb.dve, lambda: V.tensor_scalar(s16[0:n, 0, :], s16[0:n, 0, :], st[0:n, 6:7], None, ALU.add), r=[st_b, s16_b], wp=[s16_b])
                    kb.op(kb.act, lambda: A.activation(out=s16[0:n, 1, :], in_=s16[0:n, 0, :], func=AF.Ln, bias=EPS, scale=1.0 / 96), r=[s16_b], wp=[s16_b])
                    kb.op(kb.act, lambda: A.activation(out=s16[0:n, 2, :], in_=s16[0:n, 1, :], func=AF.Exp, scale=-0.5), r=[s16_b], wp=[s16_b])
                    kb.op(kb.act, lambda: A.copy(b.vst[0:n, :, 0:64], kv5[:, :, 0, 64:128]), r=[raw_b, b.vst_b], wp=[b.vst_b])
                    kb.op(kb.dve, lambda: V.tensor_copy(b.vst[0:n, :, 128:192], kv5[:, :, 1, 64:128]), r=[raw_b, b.vst_b], wp=[b.vst_b])
                    kb.dma("sp", self.va_d[t0:t0 + n, :, :], b.vst[0:n, :, :], r=[b.vst_b])
                    kb.op(kb.dve, lambda: V.tensor_tensor(kv4[:, :, 0:64], kv4[:, :, 0:64], bc(s16[0:n, 2, :], [n, 16, 64], 2), ALU.mult),
                          r=[s16_b], w=[raw_b])
                    kb.op(kb.dve, lambda: V.tensor_tensor(qb3[:, :, 0:64], kv4[:, :, 0:64], bc(gk[0:n, 0:64], [n, 16, 64], 1), ALU.mult),
                          r=[raw_b, par_b], w=[qbf_b])
                    kb.op(kb.dve, lambda: V.tensor_tensor(kpg[0:n, 0, :], kpe[0:n, :], gk[0:n, 64:96], ALU.mult), r=[ln_b, par_b], w=[kpg_b])
                    rope(kpg[0:n, 0, 0:16], kpg[0:n, 0, 16:32], b.cs[0:n, 0, :], b.cs[0:n, 1, :], kpg[0:n, 1, 0:16], kpg[0:n, 1, 16:32],
                         False, kpg_b, [kpg_b, b.cs_b])
                    kb.op(kb.dve, lambda: V.tensor_tensor(qb3[:, :, 64:96], bc(kpg[0:n, 1, :], [n, 16, 32], 1), bc(s16[0:n, 2, :], [n, 16, 32], 2), ALU.mult),
                          r=[kpg_b, s16_b], wp=[qbf_b])

                def s9():
                    head_T(1, kTv)

                return [s0, s0b, s1, s2, s3, s4, s5, s6, s7, s8, s9]

            chains = [chain(k) for k in range(len(TILES))]
            NTL = len(TILES)
            for k in (0, 1):
                chains[k][0]()
                chains[k][1]()
                chains[k][2]()
            for k in range(0, NTL, 2):
                pair = [chains[k]] + ([chains[k + 1]] if k + 1 < NTL else [])
                nxt = [chains[k2] for k2 in (k + 2, k + 3) if k2 < NTL]
                for c_ in nxt:
                    c_[0]()
                for i in range(3, 11 + A1_LAG):
                    if i < 11:
                        pair[0][i]()
                    if len(pair) > 1 and 3 <= i - A1_LAG < 11:
                        pair[1][i - A1_LAG]()
                    if i == 5 + A1_LAG:
                        for c_ in nxt:
                            c_[2]()
                    if i == 9 + A1_LAG:
                        for c_ in nxt:
                            c_[1]()
            kb.barrier()
        with contextlib.ExitStack() as s2:
            qh = [self.S(s2, "b_q%d" % i, [96, NT], BF16) for i in range(2)]
            kh = [self.S(s2, "b_k%d" % i, [96, NT], BF16) for i in range(2)]
            qk_b = [Buf(), Buf()]
            va = [self.S(s2, "b_va%d" % i, [128, 33, 192], BF16) for i in range(2)]
            va_b = [Buf(), Buf()]
            NPS = 6
            pT = [self.S(s2, "b_pT%d" % i, [128, 512], BF16) for i in range(NPS)]
            pT_b = [Buf() for _ in range(NPS)]
            rden = [self.S(s2, "b_rd%d" % i, [128, 512], F32) for i in range(2)]
            rdsh = [self.S(s2, "b_rs%d" % i, [128, 512], F32) for i in range(2)]
            rd_b = [Buf(), Buf()]
            rs_b = [Buf(), Buf()]
            bnd = self.S(s2, "b_bnd", [128, 8], F32)
            bnd_b = Buf()
            ps = [self.P(s2, "b_ps%d" % i, [128, 512], F32) for i in range(NPS)]
            ps_b = [PB() for _ in range(NPS)]
            po = [self.P(s2, "b_po%d" % i, [128, 512], F32) for i in range(2)]
            po_b = [PB(), PB()]
            kb.op(kb.dve, lambda: V.tensor_reduce(bnd[:, 0:1], gq[:, :], AX.X, ALU.max), r=[par_b], w=[bnd_b])
            kb.op(kb.dve, lambda: V.tensor_reduce(bnd[:, 1:2], gq[:, :], AX.X, ALU.min), r=[par_b], wp=[bnd_b])
            kb.op(kb.dve, lambda: V.tensor_reduce(bnd[:, 2:3], gk[:, :], AX.X, ALU.max), r=[par_b], wp=[bnd_b])
            kb.op(kb.dve, lambda: V.tensor_reduce(bnd[:, 3:4], gk[:, :], AX.X, ALU.min), r=[par_b], wp=[bnd_b])
            kb.op(kb.dve, lambda: V.scalar_tensor_tensor(bnd[:, 4:5], bnd[:, 1:2], -1.0, bnd[:, 0:1], ALU.mult, ALU.max), r=[bnd_b], wp=[bnd_b])
            kb.op(kb.dve, lambda: V.scalar_tensor_tensor(bnd[:, 5:6], bnd[:, 3:4], -1.0, bnd[:, 2:3], ALU.mult, ALU.max), r=[bnd_b], wp=[bnd_b])
            kb.op(kb.dve, lambda: V.scalar_tensor_tensor(bnd[:, 6:7], bnd[:, 4:5], -float(np.sqrt(96.0)), bnd[:, 5:6], ALU.mult, ALU.mult),
                  r=[bnd_b], wp=[bnd_b])
            negB = bnd[:, 6:7]
            scale = float(96.0 ** -0.5)

            def load_head(h):
                b = h % 2
                kb.dma("sp", qh[b][:, :], self.qT_d[h], w=[qk_b[b]])
                kb.dma("sp", kh[b][:, :], self.kT_d[h], wp=[qk_b[b]])

            def load_pair(c):
                b = c % 2
                kb.dma("sp", va[b][0:16, 0, :], self.va_d[0:16, c, :], w=[va_b[b]])
                vv = self.va_d[16:NT, c, :].rearrange("(j p) w -> p j w", p=128)
                for jj in range(0, 32, 8):
                    kb.dma("sp", va[b][:, 1 + jj:9 + jj, :], vv[:, jj:jj + 8, :], wp=[va_b[b]])

            load_pair(0)
            load_head(0)
            items = []
            for h in range(MLA_H):
                for qi in range(9):
                    if qi == 0:
                        q0, nq = 0, 16
                        kts = [(0, 0, 16, 0, True)]
                    else:
                        q0, nq = 16 + 512 * (qi - 1), 512
                        kts = [(0, 0, 16, 0, False)] + [(kt, 16 + 128 * (kt - 1), 128, 0, False) for kt in range(1, 4 * (qi - 1) + 1)]
                        kts += [(4 * (qi - 1) + 1 + i, 16 + 128 * (4 * (qi - 1) + i), 128, 128 * i, True) for i in range(4)]
                    for idx, kt in enumerate(kts):
                        items.append((h, qi, q0, nq, idx, len(kts)) + kt)
            LA = 4
            NI = len(items)
            for i in range(NI + LA):
                if i < NI:
                    (h, qi, q0, nq, idx, nk_t, kt, k0, nk, qoff, diag) = items[i]
                    if qi == 0 and idx == 0 and h + 1 < MLA_H:
                        load_head(h + 1)
                        if h % 2 == 1:
                            load_pair(h // 2 + 1)
                    hb_ = h % 2
                    nqq = nq - qoff
                    p = i % NPS
                    kb.op(kb.pe, lambda: T.matmul(ps[p][0:nk, 0:nqq], kh[hb_][:, k0:k0 + nk], qh[hb_][:, q0 + qoff:q0 + nq], start=True, stop=True),
                          r=[qk_b[hb_]], w=[ps_b[p]])
                    kb.op(kb.act, lambda: A.activation(out=pT[p][0:nk, 0:nqq], in_=ps[p][0:nk, 0:nqq], func=AF.Exp, bias=negB[0:nk, :], scale=scale),
                          r=[ps_b[p], bnd_b], w=[pT_b[p]])
                    if diag:
                        kb.op(kb.dve, lambda: V.tensor_tensor(pT[p][0:nk, 0:nk], pT[p][0:nk, 0:nk], self.tri16[0:nk, 0:nk], ALU.mult),
                              r=[self.cb_], w=[pT_b[p]])
                ii = i - LA
                if ii >= 0:
                    (h, qi, q0, nq, idx, nk_t, kt, k0, nk, qoff, diag) = items[ii]
                    c, e = h // 2, h % 2
                    vb_ = c % 2
                    dlo, dhi = (0, 64) if e == 0 else (64, 128)
                    nlo, nhi = (64, 128) if e == 0 else (0, 64)
                    nqq = nq - qoff
                    p = ii % NPS
                    pp = (h * 9 + qi) % 2
                    first, last = idx == 0, idx == nk_t - 1
                    kb.op(kb.pe, lambda: T.matmul(po[pp][:, qoff:nq], va[vb_][0:nk, kt, e * 64:e * 64 + 128], pT[p][0:nk, 0:nqq], start=first, stop=last),
                          r=[pT_b[p], va_b[vb_]], w=[po_b[pp]] if first else [], wp=[] if first else [po_b[pp]], inc=last)
                    if last:
                        kb.op(kb.dve, lambda: V.reciprocal(rden[pp][nlo:nhi, 0:nq], po[pp][nlo:nhi, 0:nq]), r=[po_b[pp]], w=[rd_b[pp]])
                        kb.op(kb.dve, lambda: V.tensor_copy(rdsh[pp][dlo:dhi, 0:nq], rden[pp][nlo:nhi, 0:nq]), r=[rd_b[pp]], w=[rs_b[pp]])
                        kb.op(kb.dve, lambda: V.tensor_tensor(oT[dlo:dhi, c, q0:q0 + nq], po[pp][dlo:dhi, 0:nq], rdsh[pp][dlo:dhi, 0:nq], ALU.mult),
                              r=[po_b[pp], rs_b[pp]], wp=[oT_b])
            kb.barrier()
        with contextlib.ExitStack() as s3:
            hs = [self.S(s3, "c_h%d" % i, [128, D], F32) for i in range(3)]
            hs_b = [Buf() for _ in range(3)]
            po = [self.P(s3, "c_po%d" % i, [128, 512], F32) for i in range(4)]
            po_b = [PB() for _ in range(4)]
            ip = 0
            for ti, (t0, n) in enumerate(TILES):
                s = ti % 3
                kb.dma("sp", hs[s][0:n, :], self.h_src(src, t0, n), w=[hs_b[s]])
                for hh in range(2):
                    p = ip % 4
                    ip += 1
                    for c in range(8):
                        kb.op(kb.pe, lambda: T.matmul(po[p][0:n, :], oT[:, c, t0:t0 + n], w_o[:, c, hh * 512:(hh + 1) * 512], start=(c == 0), stop=(c == 7)),
                              r=[oT_b, w_o_b], w=[po_b[p]] if c == 0 else [], wp=[po_b[p]] if c else [], inc=(c == 7))
                    kb.op(kb.dve, lambda: V.tensor_tensor(hs[s][0:n, hh * 512:(hh + 1) * 512], po[p][0:n, :], hs[s][0:n, hh * 512:(hh + 1) * 512], ALU.add),
                          r=[po_b[p], hs_b[s]], wp=[hs_b[s]])
                dap = self.h_dst(dst, t0, n)
                if dap is not None:
                    kb.dma("sp", dap, hs[s][0:n, :], r=[hs_b[s]])


def host_inputs(inp):
    f = lambda a: np.ascontiguousarray(np.asarray(a, dtype=np.float32))
    rep = lambda a: np.ascontiguousarray(np.broadcast_to(np.asarray(a, np.float32)[:, None, :], (a.shape[0], 128, a.shape[1])))
    colT = lambda a, c: np.ascontiguousarray(np.asarray(a, np.float32).reshape(a.shape[0], c, 128).transpose(0, 2, 1))
    ln = np.concatenate([np.asarray(inp["ln_mix"], np.float32), np.asarray(inp["ln_mlp"], np.float32)], 0)
    lnT = np.ascontiguousarray(ln.reshape(8, 8, 128).transpose(2, 0, 1))
    cw = np.asarray(inp["ssd_conv_w"], np.float32)
    cwT = np.ascontiguousarray(cw.reshape(2, 4, 32, 128).transpose(0, 3, 2, 1))
    k = np.arange(128)
    tri = (k[:, None] <= k[None, :]).astype(np.float32)
    inv = 1.0 / (10000.0 ** (np.arange(0, 32, 2, dtype=np.float32) / 32.0))
    ang = np.arange(NT, dtype=np.float32)[:, None] * inv[None, :].astype(np.float32)
    common = {
        "meta": f(inp["meta_tokens"]),
        "ssd_w_in": f(inp["ssd_w_in"]), "ssd_w_out": f(inp["ssd_w_out"]),
        "mla_w_in": f(inp["mla_w_in"]), "mla_w_q_b": f(inp["mla_w_q_b"]), "mla_w_kv_b": f(inp["mla_w_kv_b"]),
        "mla_w_out": f(inp["mla_w_out"]), "mlp_w_up": f(inp["mlp_w_up"]), "mlp_w_down": f(inp["mlp_w_down"]),
        "lnT": lnT, "cw": cwT, "cb": colT(inp["ssd_conv_b"], 32),
        "dtb_rep": rep(inp["ssd_dt_bias"]), "alog_rep": rep(inp["ssd_a_log"]), "dskip_rep": rep(inp["ssd_d"]),
        "ssd_normT": colT(inp["ssd_norm"], 16), "q_a_T": colT(inp["mla_q_a_norm"], 3), "kv_a_T": colT(inp["mla_kv_a_norm"], 2),
        "gq_rep": rep(inp["mla_q_norm"]), "gk_rep": rep(inp["mla_k_norm"]),
        "ident": np.eye(128, dtype=np.float32), "tri": tri, "ltri": np.ascontiguousarray(1.0 - tri),
        "ones": np.ones((128, 128), np.float32),
        "cos": np.cos(ang).astype(np.float32), "sin": np.sin(ang).astype(np.float32),
    }
    return common


FULL_PHASES = [
    ("ssd", 0, 0, "x", "h"), ("mlp", 0, "h", "h"),
    ("mla", 0, 1, "h", "h"), ("mlp", 1, "h", "h"),
    ("ssd", 1, 2, "h", "h"), ("mlp", 2, "h", "h"),
    ("mla", 1, 3, "h", "h"), ("mlp", 3, "h", "y"),
]


def run(inputs, phases, cores=8):
    common = host_inputs(inputs)
    x = np.asarray(inputs["x"], np.float32)
    prog = Prog(phases)
    in_maps = []
    for c in range(cores):
        m = dict(common)
        m["x"] = np.ascontiguousarray(x[c])
        in_maps.append(m)
    res = run_bass_kernel_spmd(prog.nc, in_maps, core_ids=list(range(cores)))
    return np.stack([np.asarray(r["y"]) for r in res.results], 0)


def kernel(**inputs):
    return run(inputs, FULL_PHASES, 8).astype(np.float32)
```

```python
import contextlib
import numpy as np
import concourse.bass as bass
import concourse.mybir as mybir
from concourse.bass_utils import run_bass_kernel_spmd

F32, BF16 = mybir.dt.float32, mybir.dt.bfloat16
AF = mybir.ActivationFunctionType
ALU = mybir.AluOpType
AX = mybir.AxisListType

NT, NM, D, SEQ = 4112, 16, 1024, 4096
TILES = [(0, 16)] + [(16 + 128 * j, 128) for j in range(32)]
EPS = 1e-6
DFF = 4096
SSD_IN = 6176
NH_S = 32
SSD_LEAD = 0.8
A1_LAG = 1
SSD_GLAG = 1
MLA_H = 16
QK = 96


class Buf:
    __slots__ = ("w", "r", "name", "ps")

    def __init__(self, name="", ps=False):
        self.w = {}
        self.r = {}
        self.name = name
        self.ps = ps


def PB():
    return Buf(ps=True)


class Eng:
    def __init__(self, name, e, sem):
        self.name, self.e, self.sem, self.cnt, self.seen = name, e, sem, 0, {}


class KB:
    def __init__(self, nc, es):
        self.nc = nc
        mk = lambda n: es.enter_context(nc.semaphore(n))
        self.pe = Eng("pe", nc.tensor, mk("s_pe"))
        self.act = Eng("act", nc.scalar, mk("s_act"))
        self.dve = Eng("dve", nc.vector, mk("s_dve"))
        self.pool = Eng("pool", nc.gpsimd, mk("s_pool"))
        self.sp = Eng("sp", nc.sync, mk("s_sp"))
        self.engs = [self.pe, self.act, self.dve, self.pool, self.sp]
        self.dsem = {"sp": [[mk("d_sp%d" % i), 0] for i in range(24)],
                     "pool": [[mk("d_pl%d" % i), 0] for i in range(8)]}
        self.drr = {"sp": 0, "pool": 0}
        self.nins = 0

    def _wait(self, E, toks):
        for key, (sem, val) in toks.items():
            if E is self.pe and key == "pe":
                continue
            if E.seen.get(key, 0) >= val:
                continue
            E.e.wait_ge(sem, val)
            E.seen[key] = val

    @staticmethod
    def _add(need, d):
        for k, sv in d.items():
            if k not in need or need[k][1] < sv[1]:
                need[k] = sv

    def _deps(self, r, w, wp, ekey=None):
        need = {}
        for b in r:
            self._add(need, b.w)
            if b.ps:
                self._add(need, {k: v for k, v in b.r.items() if k != ekey})
        for b in w:
            self._add(need, b.w)
            self._add(need, b.r)
        for b in wp:
            self._add(need, b.r)
            if b.ps:
                self._add(need, b.w)
        return need

    def _reg(self, key, tok, r, w, wp):
        for b in r:
            if key not in b.r or b.r[key][1] < tok[1]:
                b.r[key] = tok
        for b in w:
            b.w = {key: tok}
            b.r = {}
        for b in wp:
            if key not in b.w or b.w[key][1] < tok[1]:
                b.w[key] = tok

    def op(self, E, fn, r=(), w=(), wp=(), inc=True):
        self._wait(E, self._deps(r, w, wp, E.name))
        ins = fn()
        self.nins += 1
        if inc:
            E.cnt += 1
            ins.then_inc(E.sem, 1)
            tok = (E.sem, E.cnt)
        else:
            tok = (E.sem, E.cnt + 1)
        self._reg(E.name, tok, r, w, wp)

    def dma(self, Q, out, in_, r=(), w=(), wp=()):
        E = self.sp if Q == "sp" else self.pool
        self._wait(E, self._deps(r, w, wp))
        lst = self.dsem[Q]
        i = self.drr[Q]
        self.drr[Q] = (i + 1) % len(lst)
        sem, cnt = lst[i]
        key = (Q, i)
        if cnt > 0 and E.seen.get(key, 0) < cnt:
            E.e.wait_ge(sem, cnt)
            E.seen[key] = cnt
        ins = E.e.dma_start(out=out, in_=in_)
        ins.then_inc(sem, 16)
        self.nins += 1
        lst[i][1] = cnt + 16
        self._reg(key, (sem, cnt + 16), r, w, wp)

    def barrier(self):
        toks = {}
        for E in self.engs:
            if E.cnt > 0:
                toks[E.name] = (E.sem, E.cnt)
        for Q, lst in self.dsem.items():
            for i, (sem, cnt) in enumerate(lst):
                if cnt > 0:
                    toks[(Q, i)] = (sem, cnt)
        for E in self.engs:
            self._wait(E, toks)


def bc(ap, shape, axis):
    return ap.unsqueeze(axis).to_broadcast(list(shape))


class Prog:
    def __init__(self, phases):
        self.phases = phases
        nc = bass.Bass("TRN2", target_bir_lowering=False)
        self.nc = nc
        di = lambda name, shape: nc.dram_tensor(name, list(shape), F32, kind="ExternalInput").ap()
        self.x = di("x", [SEQ, D])
        self.meta = di("meta", [NM, D])
        self.w_ssd_in = di("ssd_w_in", [2, D, SSD_IN])
        self.w_ssd_out = di("ssd_w_out", [2, 2048, D])
        self.w_mla_in = di("mla_w_in", [2, D, 672])
        self.w_mla_qb = di("mla_w_q_b", [2, 384, 1536])
        self.w_mla_kvb = di("mla_w_kv_b", [2, 256, 2048])
        self.w_mla_out = di("mla_w_out", [2, D, D])
        self.w_up = di("mlp_w_up", [4, D, DFF])
        self.w_dn = di("mlp_w_down", [4, DFF, D])
        self.lnT_d = di("lnT", [128, 8, 8])
        self.cw_d = di("cw", [2, 128, 32, 4])
        self.cb_d = di("cb", [2, 128, 32])
        self.dtb_d = di("dtb_rep", [2, 128, 32])
        self.alog_d = di("alog_rep", [2, 128, 32])
        self.dsk_d = di("dskip_rep", [2, 128, 32])
        self.sng_d = di("ssd_normT", [2, 128, 16])
        self.qag_d = di("q_a_T", [2, 128, 3])
        self.kvag_d = di("kv_a_T", [2, 128, 2])
        self.gq_d = di("gq_rep", [2, 128, 96])
        self.gk_d = di("gk_rep", [2, 128, 96])
        self.ident_d = di("ident", [128, 128])
        self.tri_d = di("tri", [128, 128])
        self.ltri_d = di("ltri", [128, 128])
        self.ones_d = di("ones", [128, 128])
        self.cos_d = di("cos", [NT, 16])
        self.sin_d = di("sin", [NT, 16])
        self.y = nc.dram_tensor("y", [SEQ, D], F32, kind="ExternalOutput").ap()
        self.hd = nc.dram_tensor("hd", [NT, D], F32, kind="Internal").ap()
        self.qT_d = nc.dram_tensor("qT_d", [MLA_H, QK, NT], BF16, kind="Internal").ap()
        self.kT_d = nc.dram_tensor("kT_d", [MLA_H, QK, NT], BF16, kind="Internal").ap()
        self.va_d = nc.dram_tensor("va_d", [NT, 8, 192], BF16, kind="Internal").ap()

        with contextlib.ExitStack() as es:
            self.kb = KB(nc, es)
            self.build(es)

    def S(self, es, name, shape, dt):
        self.uid = getattr(self, "uid", 0) + 1
        return es.enter_context(self.nc.sbuf_tensor("sb%d_%s" % (self.uid, name), list(shape), dt))

    def P(self, es, name, shape, dt):
        self.uid = getattr(self, "uid", 0) + 1
        return es.enter_context(self.nc.psum_tensor("ps%d_%s" % (self.uid, name), list(shape), dt))

    def h_src(self, kind, t0, n):
        if kind == "x":
            return self.meta[0:16, :] if t0 == 0 else self.x[t0 - 16:t0 - 16 + n, :]
        return self.hd[t0:t0 + n, :]

    def h_dst(self, kind, t0, n):
        if kind == "y":
            return None if t0 == 0 else self.y[t0 - 16:t0 - 16 + n, :]
        return self.hd[t0:t0 + n, :]

    def build(self, es):
        kb, nc = self.kb, self.nc
        self.ident = self.S(es, "ident", [128, 128], BF16)
        self.tri32 = self.S(es, "tri32", [128, 128], F32)
        self.ltri32 = self.S(es, "ltri32", [128, 128], F32)
        self.ones32 = self.S(es, "ones32", [128, 128], F32)
        self.lnT = self.S(es, "lnT", [128, 8, 8], F32)
        self.cb_ = Buf("consts")
        kb.dma("pool", self.ident[:], self.ident_d, wp=[self.cb_])
        kb.dma("sp", self.tri32[:], self.tri_d, wp=[self.cb_])
        kb.dma("sp", self.ltri32[:], self.ltri_d, wp=[self.cb_])
        kb.dma("sp", self.ones32[:], self.ones_d, wp=[self.cb_])
        kb.dma("sp", self.lnT[:], self.lnT_d, wp=[self.cb_])
        for ph in self.phases:
            kind = ph[0]
            with contextlib.ExitStack() as pes:
                if kind == "mlp":
                    self.phase_mlp(pes, *ph[1:])
                elif kind == "ssd":
                    self.phase_ssd(pes, *ph[1:])
                elif kind == "mla":
                    self.phase_mla(pes, *ph[1:])
                kb.barrier()
        kb.barrier()

    def rstd_newton(self, st, st_b, n, inv_n, eps):
        kb, nc = self.kb, self.nc
        V = nc.vector
        I32 = mybir.dt.int32
        for f in self.rstd_newton_ops(st, st_b, n, inv_n, eps):
            f()

    def rstd_newton_ops(self, st, st_b, n, inv_n, eps):
        kb, nc = self.kb, self.nc
        V = nc.vector
        I32 = mybir.dt.int32
        x, g, t = st[0:n, 1:2], st[0:n, 2:3], st[0:n, 3:4]
        ops = []
        ops.append(lambda: kb.op(kb.dve, lambda: V.tensor_scalar(x, st[0:n, 0:1], inv_n, eps, ALU.mult, ALU.add), r=[st_b], wp=[st_b]))
        ops.append(lambda: kb.op(kb.dve, lambda: V.tensor_scalar(g.bitcast(I32), x.bitcast(I32), 1, None, ALU.arith_shift_right), r=[st_b], wp=[st_b]))
        ops.append(lambda: kb.op(kb.dve, lambda: V.tensor_scalar(g.bitcast(I32), g.bitcast(I32), -1, 0x5f3759df, ALU.mult, ALU.add), r=[st_b], wp=[st_b]))
        for _ in range(2):
            ops.append(lambda: kb.op(kb.dve, lambda: V.scalar_tensor_tensor(t, g, x, g, ALU.mult, ALU.mult), r=[st_b], wp=[st_b]))
            ops.append(lambda: kb.op(kb.dve, lambda: V.tensor_scalar(t, t, -0.5, 1.5, ALU.mult, ALU.add), r=[st_b], wp=[st_b]))
            ops.append(lambda: kb.op(kb.dve, lambda: V.tensor_tensor(g, g, t, ALU.mult), r=[st_b], wp=[st_b]))
        return ops

    def norm_T(self, hb, h_ap, n, gain_ap, dst_ap, dst_b, sc, newton=False, part=None):
        kb, nc = self.kb, self.nc
        junk, junk_b, ss, ss_b, xn, xn_b, tp, tp_b = sc
        if part == "B":
            return self._norm_T_b(n, gain_ap, dst_ap, dst_b, sc)
        kb.op(kb.act, lambda: nc.scalar.activation(out=junk[0:n, :], in_=h_ap, func=AF.Square, accum_out=ss[0:n, 0:1]),
              r=[hb], w=[junk_b, ss_b] if junk_b is not xn_b else [xn_b, ss_b])
        if newton:
            self.rstd_newton(ss, ss_b, n, 1.0 / D, EPS)
        else:
            kb.op(kb.act, lambda: nc.scalar.activation(out=ss[0:n, 1:2], in_=ss[0:n, 0:1], func=AF.Ln, bias=EPS, scale=1.0 / D),
                  r=[ss_b], wp=[ss_b])
            kb.op(kb.act, lambda: nc.scalar.activation(out=ss[0:n, 2:3], in_=ss[0:n, 1:2], func=AF.Exp, scale=-0.5),
                  r=[ss_b], wp=[ss_b])
        kb.op(kb.dve, lambda: nc.vector.tensor_scalar(xn[0:n, :], h_ap, ss[0:n, 2:3], None, ALU.mult),
              r=[hb, ss_b], w=[xn_b])
        if part == "A":
            return
        self._norm_T_b(n, gain_ap, dst_ap, dst_b, sc)

    def _norm_T_b(self, n, gain_ap, dst_ap, dst_b, sc):
        kb, nc = self.kb, self.nc
        junk, junk_b, ss, ss_b, xn, xn_b, tp, tp_b = sc
        for c in range(8):
            kb.op(kb.pe, lambda c=c: nc.tensor.transpose(tp[:, c, 0:n], xn[0:n, c * 128:(c + 1) * 128], self.ident[0:n, 0:n]),
                  r=[xn_b, self.cb_], w=[tp_b] if c == 0 else [], wp=[tp_b] if c else [], inc=(c == 7))
        kb.op(kb.dve, lambda: nc.vector.tensor_tensor(dst_ap, tp[:, :, 0:n], bc(gain_ap, [128, 8, n], 2), ALU.mult),
              r=[tp_b, self.cb_], w=[dst_b])

    def norm_scratch(self, es, pfx, tp_ps):
        ss = self.S(es, pfx + "ss", [128, 4], F32)
        xn = self.S(es, pfx + "xn", [128, 1024], BF16)
        tp = tp_ps[:].bitcast(BF16)[:, 0:1024].rearrange("p (c t) -> p c t", c=8)
        xn_b = Buf()
        return (xn, xn_b, ss, Buf(), xn, xn_b, tp, PB())

    def phase_mlp(self, es, li, src, dst):
        kb, nc = self.kb, self.nc
        wup = self.S(es, "wup", [128, 8, DFF], BF16)
        wdn = self.S(es, "wdn", [128, 32, D], BF16)
        wup_b = [Buf() for _ in range(8)]
        wdn_b = [Buf() for _ in range(8)]
        upv = self.w_up[li].rearrange("(c p) f -> p c f", p=128)
        dnv = self.w_dn[li].rearrange("(c p) d -> p c d", p=128)
        for i in range(8):
            kb.dma("pool", wup[:, :, i * 512:(i + 1) * 512], upv[:, :, i * 512:(i + 1) * 512], w=[wup_b[i]])
        for i in range(8):
            kb.dma("pool", wdn[:, 4 * i:4 * i + 4, :], dnv[:, 4 * i:4 * i + 4, :], w=[wdn_b[i]])
        NSLOT = 7
        hs = [self.S(es, "mh%d" % i, [128, D], F32) for i in range(NSLOT)]
        hs_b = [Buf() for _ in range(NSLOT)]
        xnT = self.S(es, "m_xnT", [128, 8, 512], BF16)
        xnT_b = Buf()
        uT = self.S(es, "m_uT", [128, 32, 512], BF16)
        uT_b = [Buf() for _ in range(32)]
        r32 = [self.S(es, "m_r32_%d" % i, [128, 512], F32) for i in range(2)]
        r32_b = [Buf(), Buf()]
        tp_ps = [self.P(es, "m_tp%d" % i, [128, 512], F32) for i in range(2)]
        pu = [self.P(es, "m_pu%d" % i, [128, 512], F32) for i in range(3)]
        pu_b = [PB() for _ in range(3)]
        pd = [self.P(es, "m_pd%d" % i, [128, 512], F32) for i in range(3)]
        pd_b = [PB() for _ in range(3)]
        nsc = [self.norm_scratch(es, "m%d" % i, tp_ps[i]) for i in range(2)]
        gain = self.lnT[:, 4 + li, :]
        groups = [[TILES[0]]] + [TILES[1 + 4 * g:5 + 4 * g] for g in range(8)]
        if dst == "y":
            groups = groups[1:]
        NG = len(groups)
        slots_all = []
        sl_ = 0
        for grp in groups:
            slots_all.append([(sl_ + i) % NSLOT for i in range(len(grp))])
            sl_ += len(grp)
        loaded = set()

        def load_tile(gi, k):
            if (gi, k) in loaded:
                return
            loaded.add((gi, k))
            (t0, n), s = groups[gi][k], slots_all[gi][k]
            kb.dma("sp", hs[s][0:n, :], self.h_src(src, t0, n), w=[hs_b[s]])

        def norm_part(gi, k, part):
            (t0, n), s = groups[gi][k], slots_all[gi][k]
            off = sum(nn for _, nn in groups[gi][:k])
            self.norm_T(hs_b[s], hs[s][0:n, :], n, gain, xnT[:, :, off:off + n], xnT_b, nsc[k % 2], part=part)

        for k in range(len(groups[0])):
            load_tile(0, k)
            norm_part(0, k, None)
        iu = 0
        ipd = 0
        for gi, grp in enumerate(groups):
            ntok = sum(n for _, n in grp)
            myslots = slots_all[gi]
            nxt = gi + 1 if gi + 1 < NG else None
            if nxt is not None:
                for k in range(min(len(groups[nxt]), NSLOT - len(grp))):
                    load_tile(nxt, k)
            for fc in range(32):
                p = iu % 3
                for c in range(8):
                    kb.op(kb.pe, lambda c=c, fc=fc, p=p: nc.tensor.matmul(pu[p][:, 0:ntok], wup[:, c, fc * 128:(fc + 1) * 128],
                                                                      xnT[:, c, 0:ntok], start=(c == 0), stop=(c == 7)),
                          r=[wup_b[fc // 4], xnT_b], w=[pu_b[p]] if c == 0 else [], wp=[pu_b[p]] if c else [], inc=(c == 7))
                rr = iu % 2
                kb.op(kb.act, lambda p=p, rr=rr: nc.scalar.activation(out=r32[rr][:, 0:ntok], in_=pu[p][:, 0:ntok], func=AF.Relu),
                      r=[pu_b[p]], w=[r32_b[rr]])
                E = kb.dve if (fc % 2 == 0) else kb.pool
                kb.op(E, lambda rr=rr, fc=fc, E=E: E.e.tensor_tensor(uT[:, fc, 0:ntok], r32[rr][:, 0:ntok], r32[rr][:, 0:ntok], ALU.mult),
                      r=[r32_b[rr]], w=[uT_b[fc]])
                iu += 1
            off = 0
            for kk, ((t0, n), s) in enumerate(zip(grp, myslots)):
                if nxt is not None and kk < len(groups[nxt]):
                    load_tile(nxt, kk)
                    norm_part(nxt, kk, "A")
                for hh in range(2):
                    p = ipd % 3
                    ipd += 1
                    for fc in range(32):
                        kb.op(kb.pe, lambda fc=fc, p=p, off=off, n=n, hh=hh: nc.tensor.matmul(
                            pd[p][0:n, :], uT[:, fc, off:off + n], wdn[:, fc, hh * 512:(hh + 1) * 512],
                            start=(fc == 0), stop=(fc == 31)),
                            r=[uT_b[fc], wdn_b[fc // 4]], w=[pd_b[p]] if fc == 0 else [], wp=[pd_b[p]] if fc else [], inc=(fc == 31))
                    kb.op(kb.dve, lambda p=p, n=n, hh=hh, s=s: nc.vector.tensor_tensor(
                        hs[s][0:n, hh * 512:(hh + 1) * 512], pd[p][0:n, :], hs[s][0:n, hh * 512:(hh + 1) * 512], ALU.add),
                        r=[pd_b[p], hs_b[s]], wp=[hs_b[s]])
                dst_ap = self.h_dst(dst, t0, n)
                if dst_ap is not None:
                    kb.dma("sp", dst_ap, hs[s][0:n, :], r=[hs_b[s]])
                off += n
                if nxt is not None and kk < len(groups[nxt]):
                    norm_part(nxt, kk, "B")
            if nxt is not None:
                for kk in range(len(grp), len(groups[nxt])):
                    load_tile(nxt, kk)
                    norm_part(nxt, kk, None)

    def phase_ssd(self, es, j, li, src, dst):
        from functools import partial
        kb, nc = self.kb, self.nc
        V, A, G, T = nc.vector, nc.scalar, nc.gpsimd, nc.tensor
        w_in = self.S(es, "s_win", [128, 8, SSD_IN], BF16)
        w_out = self.S(es, "s_wout", [128, 16, D], BF16)
        par_b = Buf()
        wdt_b = Buf()
        wx_b = [Buf() for _ in range(8)]
        wz_b = [Buf() for _ in range(4)]
        wout_b = [Buf() for _ in range(4)]
        wv = self.w_ssd_in[j].rearrange("(c p) f -> p c f", p=128)
        kb.dma("pool", w_in[:, :, 6144:6176], wv[:, :, 6144:6176], w=[wdt_b])
        for i in range(8):
            kb.dma("pool", w_in[:, :, 2048 + 512 * i:2560 + 512 * i], wv[:, :, 2048 + 512 * i:2560 + 512 * i], w=[wx_b[i]])
        for i in range(4):
            kb.dma("pool", w_in[:, :, 512 * i:512 * (i + 1)], wv[:, :, 512 * i:512 * (i + 1)], w=[wz_b[i]])
        wov = self.w_ssd_out[j].rearrange("(c p) f -> p c f", p=128)
        for i in range(4):
            kb.dma("pool", w_out[:, 4 * i:4 * i + 4, :], wov[:, 4 * i:4 * i + 4, :], w=[wout_b[i]])
        cw = self.S(es, "s_cw", [128, 32, 4], F32)
        cb = self.S(es, "s_cb", [128, 32], F32)
        dtb = self.S(es, "s_dtb", [128, 32], F32)
        arep = self.S(es, "s_arep", [128, 32], F32)
        dsk = self.S(es, "s_dsk", [128, 32], F32)
        sng = self.S(es, "s_sng", [128, 16], F32)
        kb.dma("sp", cw[:], self.cw_d[j], wp=[par_b])
        kb.dma("sp", cb[:], self.cb_d[j], wp=[par_b])
        kb.dma("sp", dtb[:], self.dtb_d[j], wp=[par_b])
        kb.dma("sp", arep[:], self.alog_d[j], wp=[par_b])
        kb.dma("sp", dsk[:], self.dsk_d[j], wp=[par_b])
        kb.dma("sp", sng[:], self.sng_d[j], wp=[par_b])
        arep_b = Buf()
        kb.op(kb.act, lambda: A.activation(out=arep[:], in_=arep[:], func=AF.Exp), r=[par_b], w=[arep_b])
        kb.op(kb.dve, lambda: V.tensor_scalar(arep[:], arep[:], -1.0, None, ALU.mult), r=[arep_b], w=[arep_b])
        cwb = Buf()
        kb.op(kb.dve, lambda: V.tensor_scalar(cw[:], cw[:], 0.5, None, ALU.mult), r=[par_b], w=[cwb])
        kb.op(kb.dve, lambda: V.tensor_scalar(cb[:], cb[:], 0.5, None, ALU.mult), r=[par_b], wp=[cwb])
        hs = [self.S(es, "s_h%d" % i, [128, D], F32) for i in range(2)]
        hs_b = [Buf(), Buf()]
        tpF = self.P(es, "s_tpF", [128, 512], F32)
        pzs = [self.P(es, "s_pz%d" % i, [128, 512], F32) for i in range(2)]
        fm = self.P(es, "s_fm", [128, 512], F32)
        tpB = self.P(es, "s_tpB", [128, 512], F32)
        dTb = self.P(es, "s_dT", [128, 512], F32)
        bm = self.P(es, "s_bm", [128, 512], F32)
        yb = self.P(es, "s_y", [128, 512], F32)
        pz_b = [PB(), PB()]
        fm_b, tpB_b, dT_b, bm_b, y_b = PB(), PB(), PB(), PB(), PB()
        gT = self.S(es, "s_gT", [128, 16, 128], BF16)
        gT_b = Buf()
        _ss = self.S(es, "s_nss", [128, 4], F32)
        _xn = gT[:, 0:8, :].rearrange("p c t -> p (c t)")
        _tp = tpF[:].bitcast(BF16)[:, 0:1024].rearrange("p (c t) -> p c t", c=8)
        nsc = (_xn, gT_b, _ss, Buf(), _xn, gT_b, _tp, PB())
        tpB16 = tpB[:].bitcast(BF16)
        tpBg = tpB16.rearrange("p (c t) -> p c t", c=8)
        xnT = [self.S(es, "s_xnT%d" % i, [128, 8, 131], BF16) for i in range(2)]
        xnT_b = [Buf(), Buf()]
        for i in range(2):
            kb.op(kb.pool, lambda: G.memset(xnT[i][:], 0.0), w=[xnT_b[i]])
        xbc_xs = self.S(es, "s_xbcxs", [128, 16, 128], BF16)
        xsT_b = [Buf() for _ in range(16)]
        xbc_bc = [self.S(es, "s_xbcbc%d" % i, [128, 16, 128], BF16) for i in range(2)]
        bc_b = [[Buf() for _ in range(16)] for _ in range(2)]
        NU, NA = 4, 7
        u = [self.S(es, "s_u%d" % i, [128, 131], F32) for i in range(NU)]
        u_b = [Buf() for _ in range(NU)]
        acc = [self.S(es, "s_acc%d" % i, [128, 128], F32) for i in range(NA)]
        acc_b = [Buf() for _ in range(NA)]
        th = [self.S(es, "s_th%d" % i, [128, 128], F32) for i in range(2)]
        th_b = [Buf(), Buf()]
        xs_tm = self.S(es, "s_xstm", [128, 2048], BF16)
        xs_b = [Buf() for _ in range(8)]
        B_tm = self.S(es, "s_Btm", [128, 1024], BF16)
        Btm_b = Buf()
        smF = self.S(es, "s_smF", [128, 4, 32], F32)
        EXPT, ACS, TMP, DTE = range(4)
        smF_b = [Buf() for _ in range(4)]
        smB = [self.S(es, "s_smB%d" % i, [128, 5, 32], F32) for i in range(2)]
        DTV, ADT, DFS, CD, W2 = range(5)
        smB_b = [[Buf() for _ in range(5)] for _ in range(2)]
        rhsD = [self.S(es, "s_rhsD%d" % i, [128, 512], F32) for i in range(2)]
        rhsD_b = [Buf(), Buf()]
        Ee = [self.S(es, "s_E%d" % i, [128, 512], F32) for i in range(2)]
        E_b = [Buf(), Buf()]
        CBm = [self.S(es, "s_CBm0", [128, 128], F32)] * 2
        _cb = Buf()
        CBm_b = [_cb, _cb]
        MT = [self.S(es, "s_MT%d" % i, [128, 512], BF16) for i in range(2)]
        MT_b = [Buf(), Buf()]
        xdt = [self.S(es, "s_xdt%d" % i, [128, 256], BF16) for i in range(2)]
        xdt_b = [Buf(), Buf()]
        xdtd = [self.S(es, "s_xdtd%d" % i, [128, 256], BF16) for i in range(2)]
        xdtd_b = [Buf(), Buf()]
        tt = [[self.S(es, "s_t%d_%d" % (k, i), [128, 256], F32) for i in range(2)] for k in range(4)]
        tt_b = [[Buf(), Buf()] for _ in range(4)]
        tt.append(tt[0])
        tt_b.append(tt_b[0])
        stg = self.S(es, "s_stg", [128, 8, 4], F32)
        stg_b = [Buf() for _ in range(8)]
        S32 = self.S(es, "s_S32", [128, 2048], F32)
        Sbf = self.S(es, "s_Sbf", [128, 2048], BF16)
        S32_b = [Buf() for _ in range(8)]
        Sbf_b = [Buf() for _ in range(8)]
        stmp = [self.S(es, "s_stmp0", [128, 256], F32)] * 2
        _sb = Buf()
        stmp_b = [_sb, _sb]
        kb.op(kb.pool, lambda: G.memset(S32[:], 0.0), w=S32_b)
        kb.op(kb.pool, lambda: G.memset(Sbf[:], 0.0), w=Sbf_b)
        v3 = lambda ap: ap.rearrange("p (j l) -> p j l", j=4)
        NTL = len(TILES)

        def f_load(ti):
            t0, n = TILES[ti]
            cur, prev = ti % 2, (ti + 1) % 2
            nprev = TILES[ti - 1][1] if ti > 0 else 0
            X = xnT[cur]
            kb.dma("sp", hs[cur][0:n, :], self.h_src(src, t0, n), w=[hs_b[cur]])
            self.norm_T(hs_b[cur], hs[cur][0:n, :], n, self.lnT[:, li, :], X[:, :, 3:3 + n], xnT_b[cur], nsc, newton=True)
            kb.op(kb.pool, lambda: G.tensor_copy(X[:, :, 0:3], xnT[prev][:, :, nprev:nprev + 3]), r=[xnT_b[prev]], wp=[xnT_b[cur]])

        def f_dt1(ti):
            t0, n = TILES[ti]
            cur = ti % 2
            X = xnT[cur]
            sB, sBb = smB[cur], smB_b[cur]
            pdt = fm[0:n, 0:32]
            for c in range(8):
                kb.op(kb.pe, lambda: T.matmul(pdt, X[:, c, 3:3 + n], w_in[:, c, 6144:6176], start=(c == 0), stop=(c == 7)),
                      r=[wdt_b, xnT_b[cur]], w=[fm_b] if c == 0 else [], wp=[fm_b] if c else [], inc=(c == 7))
            kb.op(kb.dve, lambda: V.tensor_tensor(sB[0:n, DTV, :], pdt, dtb[0:n, :], ALU.add), r=[fm_b, par_b], w=[sBb[DTV]])
            kb.op(kb.act, lambda: A.activation(out=smF[0:n, EXPT, :], in_=sB[0:n, DTV, :], func=AF.Exp), r=[sBb[DTV]], w=[smF_b[EXPT]])
            kb.op(kb.act, lambda: A.activation(out=sB[0:n, DTV, :], in_=smF[0:n, EXPT, :], func=AF.Ln, bias=1.0), r=[smF_b[EXPT]], w=[sBb[DTV]])
            kb.op(kb.dve, lambda: V.tensor_tensor(sB[0:n, ADT, :], sB[0:n, DTV, :], arep[0:n, :], ALU.mult), r=[sBb[DTV], arep_b], w=[sBb[ADT]])

        def f_dt2(ti):
            t0, n = TILES[ti]
            cur = ti % 2
            sB, sBb = smB[cur], smB_b[cur]
            pacs, ptot = fm[0:n, 32:64], fm[:, 64:96]
            kb.op(kb.pe, lambda: T.matmul(pacs, self.tri32[0:n, 0:n], sB[0:n, ADT, :], start=True, stop=True), r=[sBb[ADT], self.cb_], w=[fm_b])
            kb.op(kb.pe, lambda: T.matmul(ptot, self.ones32[0:n, :], sB[0:n, ADT, :], start=True, stop=True), r=[sBb[ADT], self.cb_], wp=[fm_b])
            kb.op(kb.dve, lambda: V.tensor_copy(smF[0:n, ACS, :], pacs), r=[fm_b], w=[smF_b[ACS]])
            kb.op(kb.act, lambda: A.activation(out=sB[0:n, DFS, :], in_=pacs, func=AF.Exp), r=[fm_b], w=[sBb[DFS]])
            kb.op(kb.act, lambda: A.activation(out=sB[:, CD, :], in_=ptot, func=AF.Exp), r=[fm_b], w=[sBb[CD]])
            kb.op(kb.dve, lambda: V.tensor_tensor(smF[0:n, TMP, :], ptot[0:n, :], smF[0:n, ACS, :], ALU.subtract), r=[fm_b, smF_b[ACS]], w=[smF_b[TMP]])
            kb.op(kb.act, lambda: A.activation(out=smF[0:n, DTE, :], in_=smF[0:n, TMP, :], func=AF.Exp), r=[smF_b[TMP]], w=[smF_b[DTE]])
            kb.op(kb.dve, lambda: V.tensor_tensor(sB[0:n, W2, :], sB[0:n, DTV, :], smF[0:n, DTE, :], ALU.mult), r=[sBb[DTV], smF_b[DTE]], w=[sBb[W2]])

        def conv_dst(ti, cc, n):
            cur = ti % 2
            if cc < 16:
                return xbc_xs[:, cc, 0:n], xsT_b[cc]
            return xbc_bc[cur][:, cc - 16, 0:n], bc_b[cur][cc - 16]

        def f_conv(ti, s):
            t0, n = TILES[ti]
            cur = ti % 2
            X = xnT[cur]
            cc = s
            if 0 <= cc < 32:
                p = cc % 2
                pz = pzs[p][:, 0:3 + n]
                for c in range(8):
                    kb.op(kb.pe, lambda: T.matmul(pz, w_in[:, c, 2048 + cc * 128:2048 + (cc + 1) * 128], X[:, c, 0:3 + n], start=(c == 0), stop=(c == 7)),
                          r=[wx_b[cc // 4], xnT_b[cur]], w=[pz_b[p]] if c == 0 else [], wp=[pz_b[p]] if c else [], inc=(c == 7))
            cc = s - 1
            if 0 <= cc < 32:
                p = cc % 2
                pz = pzs[p][:, 0:3 + n]
                kb.op(kb.act, lambda: A.activation(out=acc[cc % NA][:, 0:n], in_=pz[:, 3:3 + n], func=AF.Identity, bias=cb[:, cc:cc + 1], scale=cw[:, cc, 3:4]),
                      r=[pz_b[p], cwb], w=[acc_b[cc % NA]])
                kb.op(kb.act, lambda: A.copy(u[cc % NU][:, 0:3 + n], pz), r=[pz_b[p]], w=[u_b[cc % NU]])
            for k in range(3):
                cc = s - 2 - k
                if 0 <= cc < 32:
                    p = cc % NU
                    a_ = acc[cc % NA][:, 0:n]
                    kb.op(kb.dve, lambda: V.scalar_tensor_tensor(a_, u[p][:, k:k + n], cw[:, cc, k:k + 1], a_, ALU.mult, ALU.add),
                          r=[u_b[p], cwb], w=[acc_b[cc % NA]])
            cc = s - 5
            if 0 <= cc < 32:
                kb.op(kb.act, lambda: A.activation(out=th[cc % 2][:, 0:n], in_=acc[cc % NA][:, 0:n], func=AF.Tanh), r=[acc_b[cc % NA]], w=[th_b[cc % 2]])
            cc = s - 6
            if 0 <= cc < 32:
                dst_ap, dst_b = conv_dst(ti, cc, n)
                kb.op(kb.dve, lambda: V.scalar_tensor_tensor(dst_ap, th[cc % 2][:, 0:n], 1.0, acc[cc % NA][:, 0:n], ALU.add, ALU.mult),
                      r=[th_b[cc % 2], acc_b[cc % NA]], w=[dst_b])

        def front(ti):
            st = [partial(f_conv, ti, 0), partial(f_dt1, ti), partial(f_conv, ti, 1), partial(f_dt2, ti)]
            st += [partial(f_conv, ti, s) for s in range(2, 38)]
            return st

        def b_trx(ti, half):
            t0, n = TILES[ti]
            for k in range(8):
                cc = half * 8 + k
                kb.op(kb.pe, lambda: T.transpose(tpB16[0:n, k * 128:(k + 1) * 128], xbc_xs[:, cc, 0:n], self.ident[:, :]),
                      r=[xsT_b[cc], self.cb_], w=[tpB_b] if k == 0 else [], wp=[tpB_b] if k else [], inc=(k == 7))
            kb.op(kb.act, lambda: A.copy(xs_tm[0:n, half * 1024:(half + 1) * 1024], tpB16[0:n, :]), r=[tpB_b], w=xs_b[4 * half:4 * half + 4])

        def b_trB(ti):
            t0, n = TILES[ti]
            cur = ti % 2
            for g in range(8):
                kb.op(kb.pe, lambda: T.transpose(tpB16[0:n, g * 128:(g + 1) * 128], xbc_bc[cur][:, g, 0:n], self.ident[:, :]),
                      r=[bc_b[cur][g], self.cb_], w=[tpB_b] if g == 0 else [], wp=[tpB_b] if g else [], inc=(g == 7))
            kb.op(kb.dve, lambda: V.tensor_copy(B_tm[0:n, :], tpB16[0:n, :]), r=[tpB_b], w=[Btm_b])

        def gchain(ti, g):
            t0, n = TILES[ti]
            cur = ti % 2
            X = xnT[cur]
            sB, sBb = smB[cur], smB_b[cur]
            b2 = g % 2
            hsl = slice(4 * g, 4 * g + 4)
            gsl = slice(g * 256, (g + 1) * 256)
            BT, CT = xbc_bc[cur][:, g, 0:n], xbc_bc[cur][:, 8 + g, 0:n]
            BTb, CTb = bc_b[cur][g], bc_b[cur][8 + g]
            x3 = lambda ap: ap.rearrange("p (j f) -> p j f", j=4)
            rD = rhsD[b2][0:n, 0:4 * n]
            Ev = Ee[b2][0:n, 0:4 * n]
            MTv = MT[b2][0:n, 0:4 * n]
            pcb = bm[0:n, 0:n]
            pst = bm[:, 256:512]
            zq = fm[0:n, 256:512]
            xs3 = x3(xs_tm[0:n, gsl])
            t1, t2, t3, szg, thg = [tt[k][b2][0:n, :] for k in range(5)]
            t1b, t2b, t3b, szb, thb = [tt_b[k][b2] for k in range(5)]
            steps = []

            def s1():
                kb.op(kb.pool, lambda: G.tensor_tensor(v3(rD), bc(sB[0:n, ADT, hsl], [n, 4, n], 2), bc(self.tri32[0:n, 0:n], [n, 4, n], 1), ALU.mult),
                      r=[sBb[ADT], self.cb_], w=[rhsD_b[b2]])
                kb.op(kb.pool, lambda: G.tensor_tensor(x3(xdt[b2][0:n, :]), xs3, bc(sB[0:n, DTV, hsl], [n, 4, 64], 2), ALU.mult),
                      r=[xs_b[g], sBb[DTV]], w=[xdt_b[b2]])
            steps.append(s1)

            def s2():
                kb.op(kb.pe, lambda: T.matmul(dTb[0:n, 0:4 * n], self.ltri32[0:n, 0:n], rD, start=True, stop=True), r=[rhsD_b[b2], self.cb_], w=[dT_b])
                kb.op(kb.act, lambda: A.activation(out=Ev, in_=dTb[0:n, 0:4 * n], func=AF.Exp), r=[dT_b], w=[E_b[b2]])
                kb.op(kb.pool, lambda: G.tensor_tensor(x3(xdtd[b2][0:n, :]), xs3, bc(sB[0:n, W2, hsl], [n, 4, 64], 2), ALU.mult),
                      r=[xs_b[g], sBb[W2]], w=[xdtd_b[b2]])
            steps.append(s2)

            def s3():
                kb.op(kb.pe, lambda: T.matmul(pcb, BT, CT, start=True, stop=True), r=[BTb, CTb], w=[bm_b])
                kb.op(kb.dve, lambda: V.tensor_tensor(CBm[b2][0:n, 0:n], pcb, self.tri32[0:n, 0:n], ALU.mult), r=[bm_b, self.cb_], w=[CBm_b[b2]])
                for c in range(8):
                    kb.op(kb.pe, lambda: T.matmul(zq, X[:, c, 3:3 + n], w_in[:, c, g * 256:(g + 1) * 256], start=(c == 0), stop=(c == 7)),
                          r=[wz_b[g // 2], xnT_b[cur]], w=[fm_b] if c == 0 else [], wp=[fm_b] if c else [], inc=(c == 7))
                kb.op(kb.act, lambda: A.activation(out=thg, in_=zq, func=AF.Tanh, scale=0.5), r=[fm_b], w=[thb])
                kb.op(kb.dve, lambda: V.scalar_tensor_tensor(szg, thg, 1.0, zq, ALU.add, ALU.mult), r=[thb, fm_b], w=[szb])
                kb.op(kb.pool, lambda: G.tensor_tensor(v3(MTv), v3(Ev), bc(CBm[b2][0:n, 0:n], [n, 4, n], 1), ALU.mult),
                      r=[E_b[b2], CBm_b[b2]], w=[MT_b[b2]])
            steps.append(s3)

            def s4():
                for jj in range(4):
                    kb.op(kb.pe, lambda: T.matmul(yb[0:n, jj * 64:(jj + 1) * 64], MTv[:, jj * n:(jj + 1) * n], xdt[b2][0:n, jj * 64:(jj + 1) * 64], start=True, stop=True),
                          r=[MT_b[b2], xdt_b[b2]], w=[y_b] if jj == 0 else [], wp=[y_b] if jj else [], inc=False)
                kb.op(kb.pe, lambda: T.matmul(yb[0:n, 256:512], CT, Sbf[:, gsl], start=True, stop=True), r=[CTb, Sbf_b[g]], wp=[y_b])
                kb.op(kb.pool, lambda: G.tensor_tensor(x3(t3), xs3, bc(dsk[0:n, hsl], [n, 4, 64], 2), ALU.mult), r=[xs_b[g], par_b], w=[t3b])
                kb.op(kb.dve, lambda: V.tensor_tensor(x3(t1), x3(yb[0:n, 256:512]), bc(sB[0:n, DFS, hsl], [n, 4, 64], 2), ALU.mult),
                      r=[y_b, sBb[DFS]], w=[t1b])
                kb.op(kb.dve, lambda: V.tensor_tensor(t2, yb[0:n, 0:256], t1, ALU.add), r=[y_b, t1b], w=[t2b])
            steps.append(s4)

            def s5():
                kb.op(kb.pe, lambda: T.matmul(pst, B_tm[0:n, g * 128:(g + 1) * 128], xdtd[b2][0:n, :], start=True, stop=True),
                      r=[Btm_b, xdtd_b[b2]], w=[bm_b])
                kb.op(kb.pool, lambda: G.tensor_tensor(x3(stmp[b2][:, :]), x3(S32[:, gsl]), bc(sB[:, CD, hsl], [128, 4, 64], 2), ALU.mult),
                      r=[S32_b[g], sBb[CD]], w=[stmp_b[b2]])
                kb.op(kb.dve, lambda: V.tensor_tensor(S32[:, gsl], stmp[b2][:, :], pst, ALU.add), r=[stmp_b[b2], bm_b], w=[S32_b[g]])
                kb.op(kb.act, lambda: A.copy(Sbf[:, gsl], S32[:, gsl]), r=[S32_b[g]], w=[Sbf_b[g]])
            steps.append(s5)

            def s6():
                kb.op(kb.pool, lambda: G.tensor_tensor(t2, t2, t3, ALU.add), r=[t3b], w=[t2b])
                kb.op(kb.pool, lambda: G.tensor_tensor(t2, t2, szg, ALU.mult), r=[szb], w=[t2b])
            steps.append(s6)

            sq = lambda: kb.op(kb.act, lambda: A.activation(out=t1, in_=t2, func=AF.Square, accum_out=stg[0:n, g, 0:1]), r=[t2b], w=[t1b, stg_b[g]])
            newt = self.rstd_newton_ops(stg[:, g, :], stg_b[g], n, 1.0 / 256, 4.0 * EPS)
            gnf = lambda: kb.op(kb.dve, lambda: V.tensor_scalar(xs_tm[0:n, gsl], t2, stg[0:n, g, 2:3], None, ALU.mult), r=[t2b, stg_b[g]], w=[xs_b[g]])
            return steps, (sq, newt, gnf)

        def b_gT(ti, half):
            t0, n = TILES[ti]
            for k in range(8):
                cc = half * 8 + k
                kb.op(kb.pe, lambda: T.transpose(tpBg[:, k, 0:n], xs_tm[0:n, cc * 128:(cc + 1) * 128], self.ident[0:n, 0:n]),
                      r=[xs_b[cc // 2], self.cb_], w=[tpB_b] if k == 0 else [], wp=[tpB_b] if k else [], inc=(k == 7))
            kb.op(kb.dve, lambda: V.tensor_tensor(gT[:, half * 8:(half + 1) * 8, 0:n], tpBg[:, :, 0:n], bc(sng[:, half * 8:(half + 1) * 8], [128, 8, n], 2), ALU.mult),
                  r=[tpB_b, par_b], w=[gT_b] if half == 0 else [], wp=[gT_b] if half else [])

        def b_out(ti, hh):
            t0, n = TILES[ti]
            cur = ti % 2
            po = dTb[0:n, :] if hh == 0 else yb[0:n, :]
            pbuf = dT_b if hh == 0 else y_b
            for cc in range(16):
                kb.op(kb.pe, lambda: T.matmul(po, gT[:, cc, 0:n], w_out[:, cc, hh * 512:(hh + 1) * 512], start=(cc == 0), stop=(cc == 15)),
                      r=[gT_b, wout_b[cc // 4]], w=[pbuf] if cc == 0 else [], wp=[pbuf] if cc else [], inc=(cc == 15))
            kb.op(kb.dve, lambda: V.tensor_tensor(hs[cur][0:n, hh * 512:(hh + 1) * 512], po, hs[cur][0:n, hh * 512:(hh + 1) * 512], ALU.add),
                  r=[pbuf, hs_b[cur]], wp=[hs_b[cur]])
            if hh == 1:
                dap = self.h_dst(dst, t0, n)
                if dap is not None:
                    kb.dma("sp", dap, hs[cur][0:n, :], r=[hs_b[cur]])

        def back(ti):
            st = [partial(b_trx, ti, 0), partial(b_trx, ti, 1), partial(b_trB, ti)]
            for gp in range(4):
                (ca, ta), (cb_, tb) = gchain(ti, 2 * gp), gchain(ti, 2 * gp + 1)
                if SSD_GLAG:
                    st += ca[:SSD_GLAG]
                    for i in range(len(cb_)):
                        if i + SSD_GLAG < len(ca):
                            st.append(ca[i + SSD_GLAG])
                        st.append(cb_[i])
                else:
                    for a_, b_ in zip(ca, cb_):
                        st += [a_, b_]

                def tail(ta=ta, tb=tb):
                    ta[0]()
                    tb[0]()
                    for fa, fb in zip(ta[1], tb[1]):
                        fa()
                        fb()
                    ta[2]()
                    tb[2]()
                st.append(tail)
            st += [partial(b_gT, ti, 0), partial(b_gT, ti, 1), partial(b_out, ti, 0), partial(b_out, ti, 1)]
            return st

        def interleave(a, b, lead):
            na, nb = len(a), len(b)
            i = jx = 0
            while i < na or jx < nb:
                if jx >= nb or (i < na and i * nb <= jx * na * lead):
                    a[i]()
                    i += 1
                else:
                    b[jx]()
                    jx += 1

        f_load(0)
        for ti in range(NTL + 1):
            f = front(ti) if ti < NTL else []
            b = back(ti - 1) if ti >= 1 else []
            if ti + 1 < NTL:
                b = b + [partial(f_load, ti + 1)]
            interleave(f, b, SSD_LEAD)

    def phase_ssd_v1(self, es, j, li, src, dst):
        kb, nc = self.kb, self.nc
        V, A, G, T = nc.vector, nc.scalar, nc.gpsimd, nc.tensor
        w_in = self.S(es, "s_win", [128, 8, SSD_IN], BF16)
        w_out = self.S(es, "s_wout", [128, 16, D], BF16)
        win_b, wout_b, par_b = Buf(), Buf(), Buf()
        wv = self.w_ssd_in[j].rearrange("(c p) f -> p c f", p=128)
        for c in range(8):
            kb.dma("pool", w_in[:, c, :], wv[:, c, :], wp=[win_b])
        wov = self.w_ssd_out[j].rearrange("(c p) f -> p c f", p=128)
        for c in range(0, 16, 4):
            kb.dma("pool", w_out[:, c:c + 4, :], wov[:, c:c + 4, :], wp=[wout_b])
        cw = self.S(es, "s_cw", [128, 32, 4], F32)
        cb = self.S(es, "s_cb", [128, 32], F32)
        dtb = self.S(es, "s_dtb", [128, 32], F32)
        arep = self.S(es, "s_arep", [128, 32], F32)
        dsk = self.S(es, "s_dsk", [128, 32], F32)
        sng = self.S(es, "s_sng", [128, 16], F32)
        kb.dma("sp", cw[:], self.cw_d[j], wp=[par_b])
        kb.dma("sp", cb[:], self.cb_d[j], wp=[par_b])
        kb.dma("sp", dtb[:], self.dtb_d[j], wp=[par_b])
        kb.dma("sp", arep[:], self.alog_d[j], wp=[par_b])
        kb.dma("sp", dsk[:], self.dsk_d[j], wp=[par_b])
        kb.dma("sp", sng[:], self.sng_d[j], wp=[par_b])
        arep_b = Buf()
        kb.op(kb.act, lambda: A.activation(out=arep[:], in_=arep[:], func=AF.Exp), r=[par_b], w=[arep_b])
        kb.op(kb.dve, lambda: V.tensor_scalar(arep[:], arep[:], -1.0, None, ALU.mult), r=[arep_b], w=[arep_b])
        hs = [self.S(es, "s_h%d" % i, [128, D], F32) for i in range(2)]
        hs_b = [Buf(), Buf()]
        tp2 = self.P(es, "s_tp2", [128, 1024], F32)
        pzbs = [self.P(es, "s_pz%d" % i, [128, 512], F32) for i in range(2)]
        zqb = self.P(es, "s_zq", [128, 512], F32)
        dTb = self.P(es, "s_dT", [128, 512], F32)
        miscb = self.P(es, "s_misc", [128, 512], F32)
        yb = self.P(es, "s_y", [128, 512], F32)
        pz_b = [PB(), PB()]
        _z = PB()
        zq_b = [_z, _z]
        dT_b = PB()
        misc_b = PB()
        pdt_b = pacs_b = ptot_b = pst_b = misc_b
        pcb_b = [misc_b, misc_b]
        ydg_b = yof_b = PB()
        nsc = self.norm_scratch(es, "s", tp2)
        tp_b = nsc[7]
        tp16 = tp2[:].bitcast(BF16)
        tpg = tp16.rearrange("p (c t) -> p c t", c=16)
        xnT = [self.S(es, "s_xnT%d" % i, [128, 8, 131], BF16) for i in range(2)]
        xnT_b = [Buf(), Buf()]
        for i in range(2):
            kb.op(kb.pool, lambda: G.memset(xnT[i][:], 0.0), w=[xnT_b[i]])
        xbcT = self.S(es, "s_xbcT", [128, 32, 128], BF16)
        xbc_b = [Buf() for _ in range(32)]
        acc = [self.S(es, "s_acc%d" % i, [128, 128], F32) for i in range(3)]
        acc_b = [Buf() for _ in range(3)]
        xs_tm = self.S(es, "s_xstm", [128, 2048], BF16)
        xs_b = Buf()
        B_tm = self.S(es, "s_Btm", [128, 1024], BF16)
        Btm_b = Buf()
        sz = self.S(es, "s_sz", [128, 2048], F32)
        sz_b = [Buf() for _ in range(8)]
        sm = self.S(es, "s_sm", [128, 10, 32], F32)
        DTV, EXPT, ADT, ACS, DFS, CD, DTE, TMP, W2 = range(9)
        sm_b = [Buf() for _ in range(10)]
        rhsD = [self.S(es, "s_rhsD0", [128, 512], F32)] * 2
        _b = Buf()
        rhsD_b = [_b, _b]
        Ee = [self.S(es, "s_E0", [128, 512], F32)] * 2
        _b = Buf()
        E_b = [_b, _b]
        CBm = [self.S(es, "s_CBm%d" % i, [128, 128], F32) for i in range(2)]
        CBm_b = [Buf(), Buf()]
        MT = [self.S(es, "s_MT%d" % i, [128, 512], BF16) for i in range(2)]
        MT_b = [Buf(), Buf()]
        xdt = [self.S(es, "s_xdt%d" % i, [128, 256], BF16) for i in range(2)]
        xdt_b = [Buf(), Buf()]
        xdtd = [self.S(es, "s_xdtd%d" % i, [128, 256], BF16) for i in range(2)]
        xdtd_b = [Buf(), Buf()]
        tt = [[self.S(es, "s_t%d_%d" % (k, i), [128, 256], F32) for i in range(2)] for k in range(3)]
        tt_b = [[Buf(), Buf()] for _ in range(3)]
        tt.append(tt[0])
        tt_b.append(tt_b[0])
        stg = self.S(es, "s_stg", [128, 8, 4], F32)
        stg_b = [Buf() for _ in range(8)]
        gn = self.S(es, "s_gn", [128, 2048], BF16)
        gn_b = Buf()
        gT = self.S(es, "s_gT", [128, 16, 128], BF16)
        gT_b = Buf()
        S32 = self.S(es, "s_S32", [128, 2048], F32)
        Sbf = self.S(es, "s_Sbf", [128, 2048], BF16)
        S32_b = [Buf() for _ in range(8)]
        Sbf_b = [Buf() for _ in range(8)]
        stmp = [self.S(es, "s_stmp0", [128, 256], F32)] * 2
        _b = Buf()
        stmp_b = [_b, _b]
        kb.op(kb.pool, lambda: G.memset(S32[:], 0.0), w=S32_b)
        kb.op(kb.pool, lambda: G.memset(Sbf[:], 0.0), w=Sbf_b)
        nprev = 0
        ipz = 0
        for ti, (t0, n) in enumerate(TILES):
            cur, prev = ti % 2, (ti + 1) % 2
            X = xnT[cur]
            kb.dma("sp", hs[cur][0:n, :], self.h_src(src, t0, n), w=[hs_b[cur]])
            self.norm_T(hs_b[cur], hs[cur][0:n, :], n, self.lnT[:, li, :], X[:, :, 3:3 + n], xnT_b[cur], nsc)
            kb.op(kb.pool, lambda: G.tensor_copy(X[:, :, 0:3], xnT[prev][:, :, nprev:nprev + 3]), r=[xnT_b[prev]], wp=[xnT_b[cur]])
            nprev = n
            for cc in range(32):
                p = ipz % 2
                ipz += 1
                pz = pzbs[p][:, 0:3 + n]
                for c in range(8):
                    kb.op(kb.pe, lambda: T.matmul(pz, w_in[:, c, 2048 + cc * 128:2048 + (cc + 1) * 128], X[:, c, 0:3 + n], start=(c == 0), stop=(c == 7)),
                          r=[win_b, xnT_b[cur]], w=[pz_b[p]] if c == 0 else [], wp=[pz_b[p]] if c else [], inc=(c == 7))
                a_ = acc[p][:, 0:n]
                kb.op(kb.act, lambda: A.activation(out=a_, in_=pz[:, 3:3 + n], func=AF.Identity, bias=cb[:, cc:cc + 1], scale=cw[:, cc, 3:4]),
                      r=[pz_b[p], par_b], w=[acc_b[p]])
                for k in range(3):
                    kb.op(kb.dve, lambda: V.scalar_tensor_tensor(a_, pz[:, k:k + n], cw[:, cc, k:k + 1], a_, ALU.mult, ALU.add),
                          r=[pz_b[p], par_b], w=[acc_b[p]])
                kb.op(kb.act, lambda: A.activation(out=xbcT[:, cc, 0:n], in_=a_, func=AF.Silu), r=[acc_b[p]], w=[xbc_b[cc]])
            for g in range(8):
                zs = g % 2
                zq = zqb[0:n, zs * 256:(zs + 1) * 256]
                for c in range(8):
                    kb.op(kb.pe, lambda: T.matmul(zq, X[:, c, 3:3 + n], w_in[:, c, g * 256:(g + 1) * 256], start=(c == 0), stop=(c == 7)),
                          r=[win_b, xnT_b[cur]], w=[zq_b[zs]] if c == 0 else [], wp=[zq_b[zs]] if c else [], inc=(c == 7))
                kb.op(kb.act, lambda: A.activation(out=sz[0:n, g * 256:(g + 1) * 256], in_=zq, func=AF.Silu), r=[zq_b[zs]], w=[sz_b[g]])
            for cc in range(16):
                kb.op(kb.pe, lambda: T.transpose(tp16[0:n, cc * 128:(cc + 1) * 128], xbcT[:, cc, 0:n], self.ident[:, :]),
                      r=[xbc_b[cc], self.cb_], w=[tp_b] if cc == 0 else [], wp=[tp_b] if cc else [], inc=(cc == 15))
            kb.op(kb.act, lambda: A.copy(xs_tm[0:n, 0:1024], tp16[0:n, 0:1024]), r=[tp_b], w=[xs_b])
            kb.op(kb.dve, lambda: V.tensor_copy(xs_tm[0:n, 1024:2048], tp16[0:n, 1024:2048]), r=[tp_b], wp=[xs_b])
            for g in range(8):
                kb.op(kb.pe, lambda: T.transpose(tp16[0:n, g * 128:(g + 1) * 128], xbcT[:, 16 + g, 0:n], self.ident[:, :]),
                      r=[xbc_b[16 + g], self.cb_], w=[tp_b] if g == 0 else [], wp=[tp_b] if g else [], inc=(g == 7))
            kb.op(kb.act, lambda: A.copy(B_tm[0:n, :], tp16[0:n, 0:1024]), r=[tp_b], w=[Btm_b])
            pdt, pacs, ptot = miscb[0:n, 0:32], miscb[0:n, 32:64], miscb[:, 64:96]
            for c in range(8):
                kb.op(kb.pe, lambda: T.matmul(pdt, X[:, c, 3:3 + n], w_in[:, c, 6144:6176], start=(c == 0), stop=(c == 7)),
                      r=[win_b, xnT_b[cur]], w=[pdt_b] if c == 0 else [], wp=[pdt_b] if c else [], inc=(c == 7))
            smv = lambda k: sm[0:n, k, :]
            kb.op(kb.dve, lambda: V.tensor_tensor(smv(DTV), pdt, dtb[0:n, :], ALU.add), r=[pdt_b, par_b], w=[sm_b[DTV]])
            kb.op(kb.act, lambda: A.activation(out=smv(EXPT), in_=smv(DTV), func=AF.Exp), r=[sm_b[DTV]], w=[sm_b[EXPT]])
            kb.op(kb.act, lambda: A.activation(out=smv(DTV), in_=smv(EXPT), func=AF.Ln, bias=1.0), r=[sm_b[EXPT]], w=[sm_b[DTV]])
            kb.op(kb.dve, lambda: V.tensor_tensor(smv(ADT), smv(DTV), arep[0:n, :], ALU.mult), r=[sm_b[DTV], arep_b], w=[sm_b[ADT]])
            kb.op(kb.pe, lambda: T.matmul(pacs, self.tri32[0:n, 0:n], smv(ADT), start=True, stop=True), r=[sm_b[ADT], self.cb_], w=[pacs_b])
            kb.op(kb.pe, lambda: T.matmul(ptot, self.ones32[0:n, :], smv(ADT), start=True, stop=True), r=[sm_b[ADT], self.cb_], w=[ptot_b])
            kb.op(kb.dve, lambda: V.tensor_copy(smv(ACS), pacs), r=[pacs_b], w=[sm_b[ACS]])
            kb.op(kb.act, lambda: A.activation(out=smv(DFS), in_=pacs, func=AF.Exp), r=[pacs_b], w=[sm_b[DFS]])
            kb.op(kb.act, lambda: A.activation(out=sm[:, CD, :], in_=ptot, func=AF.Exp), r=[ptot_b], w=[sm_b[CD]])
            kb.op(kb.dve, lambda: V.tensor_tensor(smv(TMP), ptot[0:n, :], smv(ACS), ALU.subtract), r=[ptot_b, sm_b[ACS]], w=[sm_b[TMP]])
            kb.op(kb.act, lambda: A.activation(out=smv(DTE), in_=smv(TMP), func=AF.Exp), r=[sm_b[TMP]], w=[sm_b[DTE]])
            kb.op(kb.dve, lambda: V.tensor_tensor(smv(W2), smv(DTV), smv(DTE), ALU.mult), r=[sm_b[DTV], sm_b[DTE]], w=[sm_b[W2]])
            for g in range(8):
                b2 = g % 2
                hsl = slice(4 * g, 4 * g + 4)
                gsl = slice(g * 256, (g + 1) * 256)
                v3 = lambda ap: ap.rearrange("p (j l) -> p j l", j=4)
                rD = rhsD[b2][0:n, 0:4 * n]
                kb.op(kb.pool, lambda: G.tensor_tensor(v3(rD), bc(sm[0:n, ADT, hsl], [n, 4, n], 2), bc(self.tri32[0:n, 0:n], [n, 4, n], 1), ALU.mult),
                      r=[sm_b[ADT], self.cb_], w=[rhsD_b[b2]])
                kb.op(kb.pe, lambda: T.matmul(dTb[0:n, 0:4 * n], self.ltri32[0:n, 0:n], rD, start=True, stop=True), r=[rhsD_b[b2], self.cb_], w=[dT_b])
                Ev = Ee[b2][0:n, 0:4 * n]
                kb.op(kb.act, lambda: A.activation(out=Ev, in_=dTb[0:n, 0:4 * n], func=AF.Exp), r=[dT_b], w=[E_b[b2]])
                pcb = miscb[0:n, 128:128 + n]
                kb.op(kb.pe, lambda: T.matmul(pcb, xbcT[:, 16 + g, 0:n], xbcT[:, 24 + g, 0:n], start=True, stop=True),
                      r=[xbc_b[16 + g], xbc_b[24 + g]], w=[pcb_b[b2]])
                kb.op(kb.dve, lambda: V.tensor_tensor(CBm[b2][0:n, 0:n], pcb, self.tri32[0:n, 0:n], ALU.mult), r=[pcb_b[b2], self.cb_], w=[CBm_b[b2]])
                MTv = MT[b2][0:n, 0:4 * n]
                kb.op(kb.pool, lambda: G.tensor_tensor(v3(MTv), v3(Ev), bc(CBm[b2][0:n, 0:n], [n, 4, n], 1), ALU.mult),
                      r=[E_b[b2], CBm_b[b2]], w=[MT_b[b2]])
                xs3 = xs_tm[0:n, gsl].rearrange("p (j f) -> p j f", j=4)
                x3 = lambda ap: ap.rearrange("p (j f) -> p j f", j=4)
                kb.op(kb.pool, lambda: G.tensor_tensor(x3(xdt[b2][0:n, :]), xs3, bc(sm[0:n, DTV, hsl], [n, 4, 64], 2), ALU.mult),
                      r=[xs_b, sm_b[DTV]], w=[xdt_b[b2]])
                kb.op(kb.pool, lambda: G.tensor_tensor(x3(xdtd[b2][0:n, :]), xs3, bc(sm[0:n, W2, hsl], [n, 4, 64], 2), ALU.mult),
                      r=[xs_b, sm_b[W2]], w=[xdtd_b[b2]])
                for jj in range(4):
                    kb.op(kb.pe, lambda: T.matmul(yb[0:n, jj * 64:(jj + 1) * 64], MTv[:, jj * n:(jj + 1) * n], xdt[b2][0:n, jj * 64:(jj + 1) * 64], start=True, stop=True),
                          r=[MT_b[b2], xdt_b[b2]], w=[ydg_b] if jj == 0 else [], wp=[ydg_b] if jj else [], inc=(jj == 3))
                kb.op(kb.pe, lambda: T.matmul(yb[0:n, 256:512], xbcT[:, 24 + g, 0:n], Sbf[:, gsl], start=True, stop=True),
                      r=[xbc_b[24 + g], Sbf_b[g]], w=[yof_b])
                t1, t2, t3, tj = [tt[k][b2][0:n, :] for k in range(4)]
                kb.op(kb.dve, lambda: V.tensor_tensor(x3(t1), x3(yb[0:n, 256:512]), bc(sm[0:n, DFS, hsl], [n, 4, 64], 2), ALU.mult),
                      r=[yof_b, sm_b[DFS]], w=[tt_b[0][b2]])
                kb.op(kb.dve, lambda: V.tensor_tensor(t2, yb[0:n, 0:256], t1, ALU.add), r=[ydg_b, tt_b[0][b2]], w=[tt_b[1][b2]])
                kb.op(kb.pool, lambda: G.tensor_tensor(x3(t3), xs3, bc(dsk[0:n, hsl], [n, 4, 64], 2), ALU.mult), r=[xs_b, par_b], w=[tt_b[2][b2]])
                kb.op(kb.pool, lambda: G.tensor_tensor(t2, t2, t3, ALU.add), r=[tt_b[2][b2]], w=[tt_b[1][b2]])
                kb.op(kb.pool, lambda: G.tensor_tensor(t2, t2, sz[0:n, gsl], ALU.mult), r=[sz_b[g]], w=[tt_b[1][b2]])
                kb.op(kb.act, lambda: A.activation(out=tj, in_=t2, func=AF.Square, accum_out=stg[0:n, g, 0:1]), r=[tt_b[1][b2]], w=[tt_b[3][b2], stg_b[g]])
                self.rstd_ops(stg[:, g, :], stg_b[g], n, 0, 1.0 / 256)
                kb.op(kb.dve, lambda: V.tensor_scalar(gn[0:n, gsl], t2, stg[0:n, g, 2:3], None, ALU.mult), r=[tt_b[1][b2], stg_b[g]],
                      w=[gn_b] if g == 0 else [], wp=[gn_b] if g else [])
                kb.op(kb.pe, lambda: T.matmul(miscb[:, 256:512], B_tm[0:n, g * 128:(g + 1) * 128], xdtd[b2][0:n, :], start=True, stop=True),
                      r=[Btm_b, xdtd_b[b2]], w=[pst_b])
                kb.op(kb.pool, lambda: G.tensor_tensor(x3(stmp[b2][:, :]), x3(S32[:, gsl]), bc(sm[:, CD, hsl], [128, 4, 64], 2), ALU.mult),
                      r=[S32_b[g], sm_b[CD]], w=[stmp_b[b2]])
                kb.op(kb.dve, lambda: V.tensor_tensor(S32[:, gsl], stmp[b2][:, :], miscb[:, 256:512], ALU.add), r=[stmp_b[b2], pst_b], w=[S32_b[g]])
                kb.op(kb.act, lambda: A.copy(Sbf[:, gsl], S32[:, gsl]), r=[S32_b[g]], w=[Sbf_b[g]])
            for cc in range(16):
                kb.op(kb.pe, lambda: T.transpose(tpg[:, cc, 0:n], gn[0:n, cc * 128:(cc + 1) * 128], self.ident[0:n, 0:n]),
                      r=[gn_b, self.cb_], w=[tp_b] if cc == 0 else [], wp=[tp_b] if cc else [], inc=(cc == 15))
            kb.op(kb.dve, lambda: V.tensor_tensor(gT[:, :, 0:n], tpg[:, :, 0:n], bc(sng[:, :], [128, 16, n], 2), ALU.mult), r=[tp_b, par_b], w=[gT_b])
            for hh in range(2):
                po = dTb[0:n, :] if hh == 0 else yb[0:n, :]
                pbufs = [dT_b] if hh == 0 else [ydg_b]
                for cc in range(16):
                    kb.op(kb.pe, lambda: T.matmul(po, gT[:, cc, 0:n], w_out[:, cc, hh * 512:(hh + 1) * 512], start=(cc == 0), stop=(cc == 15)),
                          r=[gT_b, wout_b], w=pbufs if cc == 0 else [], wp=pbufs if cc else [], inc=(cc == 15))
                kb.op(kb.dve, lambda: V.tensor_tensor(hs[cur][0:n, hh * 512:(hh + 1) * 512], po, hs[cur][0:n, hh * 512:(hh + 1) * 512], ALU.add),
                      r=pbufs + [hs_b[cur]], wp=[hs_b[cur]])
            dap = self.h_dst(dst, t0, n)
            if dap is not None:
                kb.dma("sp", dap, hs[cur][0:n, :], r=[hs_b[cur]])

    def rstd_ops(self, st, st_b, n, c0, inv_n):
        kb, nc = self.kb, self.nc
        kb.op(kb.act, lambda: nc.scalar.activation(out=st[0:n, c0 + 1:c0 + 2], in_=st[0:n, c0:c0 + 1], func=AF.Ln, bias=EPS, scale=inv_n),
              r=[st_b], wp=[st_b])
        kb.op(kb.act, lambda: nc.scalar.activation(out=st[0:n, c0 + 2:c0 + 3], in_=st[0:n, c0 + 1:c0 + 2], func=AF.Exp, scale=-0.5),
              r=[st_b], wp=[st_b])

    def phase_mla(self, es, j, li, src, dst):
        kb, nc = self.kb, self.nc
        V, A, G, T = nc.vector, nc.scalar, nc.gpsimd, nc.tensor
        oT = self.S(es, "oT", [128, 8, NT], BF16)
        oT_b = Buf()
        w_o = self.S(es, "w_o", [128, 8, D], BF16)
        w_o_b = Buf()
        gq = self.S(es, "gq", [128, 96], F32)
        gk = self.S(es, "gk", [128, 96], F32)
        par_b = Buf()
        self.tri16 = self.S(es, "tri16", [128, 128], BF16)
        kb.dma("pool", self.tri16[:], self.tri_d, wp=[par_b])
        kb.dma("sp", gq[:], self.gq_d[j], wp=[par_b])
        kb.dma("sp", gk[:], self.gk_d[j], wp=[par_b])
        with contextlib.ExitStack() as s1:
            w_in = self.S(s1, "a_win", [128, 8, 672], BF16)
            w_qb = self.S(s1, "a_wqb", [128, 3, 1536], BF16)
            w_kvb = self.S(s1, "a_wkvb", [128, 2, 2048], BF16)
            wb = Buf()
            kb.dma("pool", w_in[:], self.w_mla_in[j].rearrange("(c p) f -> p c f", p=128), wp=[wb])
            kb.dma("pool", w_qb[:], self.w_mla_qb[j].rearrange("(c p) f -> p c f", p=128), wp=[wb])
            kb.dma("pool", w_kvb[:], self.w_mla_kvb[j].rearrange("(c p) f -> p c f", p=128), wp=[wb])
            kb.dma("pool", w_o[:], self.w_mla_out[j].rearrange("(c p) f -> p c f", p=128), wp=[w_o_b])
            qag = self.S(s1, "a_qag", [128, 3], F32)
            kvag = self.S(s1, "a_kvag", [128, 2], F32)
            kb.dma("sp", qag[:], self.qag_d[j], wp=[par_b])
            kb.dma("sp", kvag[:], self.kvag_d[j], wp=[par_b])
            tp_ps = self.P(s1, "a_tp", [128, 512], F32)
            latA = self.P(s1, "a_latA", [128, 512], F32)
            latB = self.P(s1, "a_latB", [128, 512], F32)
            big = self.P(s1, "a_big", [128, 2048], F32)
            tph_ps = self.P(s1, "a_tph", [128, 512], F32)
            latA_b, latB_b, tph_b = PB(), PB(), PB()
            big_b = [PB() for _ in range(4)]
            nsc = self.norm_scratch(s1, "a", tp_ps)
            tp5 = tp_ps[:].bitcast(BF16)[:, 0:640].rearrange("p (c t) -> p c t", c=5)
            tp_b = nsc[7]
            tph = tph_ps[:].bitcast(BF16)[:, 0:1024].rearrange("p (c t) -> p c t", c=8)
            qTv = self.qT_d.rearrange("h p t -> p h t")
            kTv = self.kT_d.rearrange("h p t -> p h t")

            class NS:
                pass

            def mkbufs(ci):
                b = NS()
                nm = lambda x: "a%d_%s" % (ci, x)
                b.hs = self.S(s1, nm("h"), [128, D], F32); b.hs_b = Buf()
                b.cs = self.S(s1, nm("cs"), [128, 2, 16], F32); b.cs_b = Buf()
                b.xnT = self.S(s1, nm("xnT"), [128, 8, 128], BF16); b.xnT_b = Buf()
                b.st = self.S(s1, nm("st"), [128, 8], F32); b.st_b = Buf()
                b.qln = self.S(s1, nm("qln"), [128, 384], BF16)
                b.kvln = self.S(s1, nm("kvln"), [128, 256], BF16)
                b.kpe = self.S(s1, nm("kpe"), [128, 32], F32); b.ln_b = Buf()
                b.sqj = self.S(s1, nm("sqj"), [128, 2048], F32); b.sqj_b = Buf()
                b.qlT = self.S(s1, nm("qlT"), [128, 3, 128], BF16)
                b.kvlT = self.S(s1, nm("kvlT"), [128, 2, 128], BF16); b.lT_b = Buf()
                b.raw = self.S(s1, nm("raw"), [128, 2048], F32); b.raw_b = Buf()
                b.s16 = self.S(s1, nm("s16"), [128, 3, 16], F32); b.s16_b = Buf()
                b.rt = self.S(s1, nm("rt"), [128, 4, 256], F32); b.rt_b = [Buf() for _ in range(4)]
                b.kpg = self.S(s1, nm("kpg"), [128, 2, 32], F32); b.kpg_b = Buf()
                b.qbf = self.S(s1, nm("qbf"), [128, 1536], BF16); b.qbf_b = Buf()
                b.stg = [self.S(s1, nm("stg%d" % i), [96, 16, 128], BF16) for i in range(2)]; b.stg_b = [Buf(), Buf()]
                b.vst = self.S(s1, nm("vst"), [128, 8, 192], BF16); b.vst_b = Buf()
                kb.op(kb.pool, lambda: G.memset(b.vst[:], 1.0), w=[b.vst_b])
                return b

            CH = [mkbufs(0), mkbufs(1)]

            def chain(ti):
                t0, n = TILES[ti]
                b = CH[ti % 2]
                xnT, st, st_b, qln, kvln, kpe, ln_b = b.xnT, b.st, b.st_b, b.qln, b.kvln, b.kpe, b.ln_b
                sqj, sqj_b, qlT, kvlT, lT_b, raw, raw_b = b.sqj, b.sqj_b, b.qlT, b.kvlT, b.lT_b, b.raw, b.raw_b
                s16, s16_b, rt, rt_b, kpg, kpg_b, qbf, qbf_b = b.s16, b.s16_b, b.rt, b.rt_b, b.kpg, b.kpg_b, b.qbf, b.qbf_b
                raw3 = raw[0:n, 0:1536].rearrange("p (h f) -> p h f", h=16)
                sq3 = sqj[0:n, 0:1536].rearrange("p (h f) -> p h f", h=16)
                qb3 = qbf[0:n, :].rearrange("p (h f) -> p h f", h=16)
                kv4 = raw[0:n, :].rearrange("p (h f) -> p h f", h=16)
                sqk = sqj[0:n, 0:1024].rearrange("p (h f) -> p h f", h=16)
                kv5 = raw[0:n, :].rearrange("p (c e f) -> p c e f", c=8, e=2)

                def head_T(sg, dstv):
                    for half in range(2):
                        for hh in range(8):
                            h = half * 8 + hh
                            kb.op(kb.pe, lambda: T.transpose(tph[0:96, hh, 0:n], qbf[0:n, h * 96:(h + 1) * 96], self.ident[0:n, 0:n]),
                                  r=[qbf_b, self.cb_], w=[tph_b] if hh == 0 else [], wp=[tph_b] if hh else [], inc=(hh == 7))
                        kb.op(kb.act, lambda: A.copy(b.stg[sg][0:96, half * 8:(half + 1) * 8, 0:n], tph[0:96, :, 0:n]),
                              r=[tph_b], w=[b.stg_b[sg]] if half == 0 else [], wp=[b.stg_b[sg]] if half else [])
                    kb.dma("sp", dstv[:, :, t0:t0 + n], b.stg[sg][0:96, :, 0:n], r=[b.stg_b[sg]])

                def rope(t1, t2, cosb, sinb, o1, o2, three_d, wb_, rd_bufs):
                    if three_d:
                        a_, b_, c_, d_ = [rt[0:n, i, :].rearrange("p (h f) -> p h f", h=16) for i in range(4)]
                    else:
                        a_, b_, c_, d_ = [rt[0:n, i, 0:16] for i in range(4)]
                    kb.op(kb.dve, lambda: V.tensor_tensor(a_, t1, cosb, ALU.mult), r=rd_bufs, w=[rt_b[0]])
                    kb.op(kb.dve, lambda: V.tensor_tensor(b_, t2, sinb, ALU.mult), r=rd_bufs, w=[rt_b[1]])
                    kb.op(kb.dve, lambda: V.tensor_tensor(c_, t1, sinb, ALU.mult), r=rd_bufs, w=[rt_b[2]])
                    kb.op(kb.dve, lambda: V.tensor_tensor(d_, t2, cosb, ALU.mult), r=rd_bufs, w=[rt_b[3]])
                    kb.op(kb.dve, lambda: V.tensor_tensor(o1, a_, b_, ALU.subtract), r=[rt_b[0], rt_b[1]], wp=[wb_])
                    kb.op(kb.dve, lambda: V.tensor_tensor(o2, c_, d_, ALU.add), r=[rt_b[2], rt_b[3]], wp=[wb_])

                def s0():
                    kb.dma("sp", b.hs[0:n, :], self.h_src(src, t0, n), w=[b.hs_b])

                def s0b():
                    kb.dma("sp", b.cs[0:n, 0, :], self.cos_d[t0:t0 + n, :], w=[b.cs_b])
                    kb.dma("sp", b.cs[0:n, 1, :], self.sin_d[t0:t0 + n, :], wp=[b.cs_b])

                def s1():
                    self.norm_T(b.hs_b, b.hs[0:n, :], n, self.lnT[:, li, :], xnT[:, :, 0:n], b.xnT_b, nsc)

                def s2():
                    for c in range(8):
                        kb.op(kb.pe, lambda: T.matmul(latA[0:n, 0:384], xnT[:, c, 0:n], w_in[:, c, 0:384], start=(c == 0), stop=(c == 7)),
                              r=[b.xnT_b, wb], w=[latA_b] if c == 0 else [], wp=[latA_b] if c else [], inc=(c == 7))
                    for c in range(8):
                        kb.op(kb.pe, lambda: T.matmul(latB[0:n, 0:288], xnT[:, c, 0:n], w_in[:, c, 384:672], start=(c == 0), stop=(c == 7)),
                              r=[b.xnT_b, wb], w=[latB_b] if c == 0 else [], wp=[latB_b] if c else [], inc=(c == 7))
                    kb.op(kb.act, lambda: A.activation(out=sqj[0:n, 0:384], in_=latA[0:n, 0:384], func=AF.Square, accum_out=st[0:n, 0:1]),
                          r=[latA_b], w=[sqj_b, st_b])
                    kb.op(kb.act, lambda: A.activation(out=sqj[0:n, 512:768], in_=latB[0:n, 0:256], func=AF.Square, accum_out=st[0:n, 3:4]),
                          r=[latB_b], wp=[sqj_b, st_b])
                    self.rstd_ops(st, st_b, n, 0, 1.0 / 384)
                    self.rstd_ops(st, st_b, n, 3, 1.0 / 256)
                    kb.op(kb.dve, lambda: V.tensor_scalar(qln[0:n, :], latA[0:n, 0:384], st[0:n, 2:3], None, ALU.mult), r=[latA_b, st_b], w=[ln_b])
                    kb.op(kb.dve, lambda: V.tensor_scalar(kvln[0:n, :], latB[0:n, 0:256], st[0:n, 5:6], None, ALU.mult), r=[latB_b, st_b], wp=[ln_b])
                    kb.op(kb.act, lambda: A.copy(kpe[0:n, :], latB[0:n, 256:288]), r=[latB_b], wp=[ln_b])

                def s3():
                    for c in range(5):
                        srcap = qln[0:n, c * 128:(c + 1) * 128] if c < 3 else kvln[0:n, (c - 3) * 128:(c - 2) * 128]
                        kb.op(kb.pe, lambda: T.transpose(tp5[:, c, 0:n], srcap, self.ident[0:n, 0:n]),
                              r=[ln_b, self.cb_], w=[tp_b] if c == 0 else [], wp=[tp_b] if c else [], inc=(c == 4))
                    kb.op(kb.dve, lambda: V.tensor_tensor(qlT[:, :, 0:n], tp5[:, 0:3, 0:n], bc(qag[:, :], [128, 3, n], 2), ALU.mult),
                          r=[tp_b, par_b], w=[lT_b])
                    kb.op(kb.dve, lambda: V.tensor_tensor(kvlT[:, :, 0:n], tp5[:, 3:5, 0:n], bc(kvag[:, :], [128, 2, n], 2), ALU.mult),
                          r=[tp_b, par_b], wp=[lT_b])

                def s4():
                    for ct in range(3):
                        for kc in range(3):
                            kb.op(kb.pe, lambda: T.matmul(big[0:n, ct * 512:(ct + 1) * 512], qlT[:, kc, 0:n], w_qb[:, kc, ct * 512:(ct + 1) * 512],
                                                          start=(kc == 0), stop=(kc == 2)),
                                  r=[lT_b, wb], w=[big_b[ct]] if kc == 0 else [], wp=[big_b[ct]] if kc else [], inc=(kc == 2))
                    for ct in range(3):
                        E_ = kb.act if ct != 1 else kb.dve
                        fn = (lambda: A.copy(raw[0:n, ct * 512:(ct + 1) * 512], big[0:n, ct * 512:(ct + 1) * 512])) if ct != 1 else \
                             (lambda: V.tensor_copy(raw[0:n, ct * 512:(ct + 1) * 512], big[0:n, ct * 512:(ct + 1) * 512]))
                        kb.op(E_, fn, r=[big_b[ct]], w=[raw_b] if ct == 0 else [], wp=[raw_b] if ct else [])

                def s5():
                    kb.op(kb.dve, lambda: V.tensor_tensor(sqj[0:n, 0:1536], raw[0:n, 0:1536], raw[0:n, 0:1536], ALU.mult), r=[raw_b], w=[sqj_b])
                    kb.op(kb.dve, lambda: V.tensor_reduce(s16[0:n, 0, :], sq3, AX.X, ALU.add), r=[sqj_b], w=[s16_b])
                    kb.op(kb.act, lambda: A.activation(out=s16[0:n, 1, :], in_=s16[0:n, 0, :], func=AF.Ln, bias=EPS, scale=1.0 / 96), r=[s16_b], wp=[s16_b])
                    kb.op(kb.act, lambda: A.activation(out=s16[0:n, 2, :], in_=s16[0:n, 1, :], func=AF.Exp, scale=-0.5), r=[s16_b], wp=[s16_b])
                    kb.op(kb.dve, lambda: V.tensor_tensor(raw3, raw3, bc(s16[0:n, 2, :], [n, 16, 96], 2), ALU.mult), r=[s16_b], w=[raw_b])
                    kb.op(kb.dve, lambda: V.tensor_tensor(raw3, raw3, bc(gq[0:n, :], [n, 16, 96], 1), ALU.mult), r=[par_b], w=[raw_b])
                    cosb = bc(b.cs[0:n, 0, :], [n, 16, 16], 1)
                    sinb = bc(b.cs[0:n, 1, :], [n, 16, 16], 1)
                    kb.op(kb.act, lambda: A.copy(qb3[:, :, 0:64], raw3[:, :, 0:64]), r=[raw_b], w=[qbf_b])
                    rope(raw3[:, :, 64:80], raw3[:, :, 80:96], cosb, sinb, qb3[:, :, 64:80], qb3[:, :, 80:96], True, qbf_b, [raw_b, b.cs_b])

                def s6():
                    head_T(0, qTv)

                def s7():
                    for ct in range(4):
                        for kc in range(2):
                            kb.op(kb.pe, lambda: T.matmul(big[0:n, ct * 512:(ct + 1) * 512], kvlT[:, kc, 0:n], w_kvb[:, kc, ct * 512:(ct + 1) * 512],
                                                          start=(kc == 0), stop=(kc == 1)),
                                  r=[lT_b, wb], w=[big_b[ct]] if kc == 0 else [], wp=[big_b[ct]] if kc else [], inc=(kc == 1))
                    for ct in range(4):
                        E_ = kb.act if ct % 2 == 0 else kb.dve
                        fn = (lambda: A.copy(raw[0:n, ct * 512:(ct + 1) * 512], big[0:n, ct * 512:(ct + 1) * 512])) if ct % 2 == 0 else \
                             (lambda: V.tensor_copy(raw[0:n, ct * 512:(ct + 1) * 512], big[0:n, ct * 512:(ct + 1) * 512]))
                        kb.op(E_, fn, r=[big_b[ct]], w=[raw_b] if ct == 0 else [], wp=[raw_b] if ct else [])

                def s8():
                    kb.op(kb.dve, lambda: V.tensor_tensor(sqk, kv4[:, :, 0:64], kv4[:, :, 0:64], ALU.mult), r=[raw_b], w=[sqj_b])
                    kb.op(kb.dve, lambda: V.tensor_reduce(s16[0:n, 0, :], sqk, AX.X, ALU.add), r=[sqj_b], w=[s16_b])
                    kb.op(kb.act, lambda: A.activation(out=kpg[0:n, 1, :], in_=kpe[0:n, :], func=AF.Square, accum_out=st[0:n, 6:7]),
                          r=[ln_b], w=[kpg_b], wp=[st_b])
                    kb.op(kb.dve, lambda: V.tensor_scalar(s16[0:n, 0, :], s16[0:n, 0, :], st[0:n, 6:7], None, ALU.add), r=[st_b, s16_b], wp=[s16_b])
                    kb.op(kb.act, lambda: A.activation(out=s16[0:n, 1, :], in_=s16[0:n, 0, :], func=AF.Ln, bias=EPS, scale=1.0 / 96), r=[s16_b], wp=[s16_b])
                    kb.op(kb.act, lambda: A.activation(out=s16[0:n, 2, :], in_=s16[0:n, 1, :], func=AF.Exp, scale=-0.5), r=[s16_b], wp=[s16_b])
                    kb.op(kb.act, lambda: A.copy(b.vst[0:n, :, 0:64], kv5[:, :, 0, 64:128]), r=[raw_b, b.vst_b], wp=[b.vst_b])
                    kb.op(kb.dve, lambda: V.tensor_copy(b.vst[0:n, :, 128:192], kv5[:, :, 1, 64:128]), r=[raw_b, b.vst_b], wp=[b.vst_b])
                    kb.dma("sp", self.va_d[t0:t0 + n, :, :], b.vst[0:n, :, :], r=[b.vst_b])
                    kb.op(kb.dve, lambda: V.tensor_tensor(kv4[:, :, 0:64], kv4[:, :, 0:64], bc(s16[0:n, 2, :], [n, 16, 64], 2), ALU.mult),
                          r=[s16_b], w=[raw_b])
                    kb.op(kb.dve, lambda: V.tensor_tensor(qb3[:, :, 0:64], kv4[:, :, 0:64], bc(gk[0:n, 0:64], [n, 16, 64], 1), ALU.mult),
                          r=[raw_b, par_b], w=[qbf_b])
                    kb.op(kb.dve, lambda: V.tensor_tensor(kpg[0:n, 0, :], kpe[0:n, :], gk[0:n, 64:96], ALU.mult), r=[ln_b, par_b], w=[kpg_b])
                    rope(kpg[0:n, 0, 0:16], kpg[0:n, 0, 16:32], b.cs[0:n, 0, :], b.cs[0:n, 1, :], kpg[0:n, 1, 0:16], kpg[0:n, 1, 16:32],
                         False, kpg_b, [kpg_b, b.cs_b])
                    kb.op(kb.dve, lambda: V.tensor_tensor(qb3[:, :, 64:96], bc(kpg[0:n, 1, :], [n, 16, 32], 1), bc(s16[0:n, 2, :], [n, 16, 32], 2), ALU.mult),
                          r=[kpg_b, s16_b], wp=[qbf_b])

                def s9():
                    head_T(1, kTv)

                return [s0, s0b, s1, s2, s3, s4, s5, s6, s7, s8, s9]

            chains = [chain(k) for k in range(len(TILES))]
            NTL = len(TILES)
            for k in (0, 1):
                chains[k][0]()
                chains[k][1]()
                chains[k][2]()
            for k in range(0, NTL, 2):
                pair = [chains[k]] + ([chains[k + 1]] if k + 1 < NTL else [])
                nxt = [chains[k2] for k2 in (k + 2, k + 3) if k2 < NTL]
                for c_ in nxt:
                    c_[0]()
                for i in range(3, 11 + A1_LAG):
                    if i < 11:
                        pair[0][i]()
                    if len(pair) > 1 and 3 <= i - A1_LAG < 11:
                        pair[1][i - A1_LAG]()
                    if i == 5 + A1_LAG:
                        for c_ in nxt:
                            c_[2]()
                    if i == 9 + A1_LAG:
                        for c_ in nxt:
                            c_[1]()
            kb.barrier()
        with contextlib.ExitStack() as s2:
            qh = [self.S(s2, "b_q%d" % i, [96, NT], BF16) for i in range(2)]
            kh = [self.S(s2, "b_k%d" % i, [96, NT], BF16) for i in range(2)]
            qk_b = [Buf(), Buf()]
            va = [self.S(s2, "b_va%d" % i, [128, 33, 192], BF16) for i in range(2)]
            va_b = [Buf(), Buf()]
            NPS = 6
            pT = [self.S(s2, "b_pT%d" % i, [128, 512], BF16) for i in range(NPS)]
            pT_b = [Buf() for _ in range(NPS)]
            rden = [self.S(s2, "b_rd%d" % i, [128, 512], F32) for i in range(2)]
            rdsh = [self.S(s2, "b_rs%d" % i, [128, 512], F32) for i in range(2)]
            rd_b = [Buf(), Buf()]
            rs_b = [Buf(), Buf()]
            bnd = self.S(s2, "b_bnd", [128, 8], F32)
            bnd_b = Buf()
            ps = [self.P(s2, "b_ps%d" % i, [128, 512], F32) for i in range(NPS)]
            ps_b = [PB() for _ in range(NPS)]
            po = [self.P(s2, "b_po%d" % i, [128, 512], F32) for i in range(2)]
            po_b = [PB(), PB()]
            kb.op(kb.dve, lambda: V.tensor_reduce(bnd[:, 0:1], gq[:, :], AX.X, ALU.max), r=[par_b], w=[bnd_b])
            kb.op(kb.dve, lambda: V.tensor_reduce(bnd[:, 1:2], gq[:, :], AX.X, ALU.min), r=[par_b], wp=[bnd_b])
            kb.op(kb.dve, lambda: V.tensor_reduce(bnd[:, 2:3], gk[:, :], AX.X, ALU.max), r=[par_b], wp=[bnd_b])
            kb.op(kb.dve, lambda: V.tensor_reduce(bnd[:, 3:4], gk[:, :], AX.X, ALU.min), r=[par_b], wp=[bnd_b])
            kb.op(kb.dve, lambda: V.scalar_tensor_tensor(bnd[:, 4:5], bnd[:, 1:2], -1.0, bnd[:, 0:1], ALU.mult, ALU.max), r=[bnd_b], wp=[bnd_b])
            kb.op(kb.dve, lambda: V.scalar_tensor_tensor(bnd[:, 5:6], bnd[:, 3:4], -1.0, bnd[:, 2:3], ALU.mult, ALU.max), r=[bnd_b], wp=[bnd_b])
            kb.op(kb.dve, lambda: V.scalar_tensor_tensor(bnd[:, 6:7], bnd[:, 4:5], -float(np.sqrt(96.0)), bnd[:, 5:6], ALU.mult, ALU.mult),
                  r=[bnd_b], wp=[bnd_b])
            negB = bnd[:, 6:7]
            scale = float(96.0 ** -0.5)

            def load_head(h):
                b = h % 2
                kb.dma("sp", qh[b][:, :], self.qT_d[h], w=[qk_b[b]])
                kb.dma("sp", kh[b][:, :], self.kT_d[h], wp=[qk_b[b]])

            def load_pair(c):
                b = c % 2
                kb.dma("sp", va[b][0:16, 0, :], self.va_d[0:16, c, :], w=[va_b[b]])
                vv = self.va_d[16:NT, c, :].rearrange("(j p) w -> p j w", p=128)
                for jj in range(0, 32, 8):
                    kb.dma("sp", va[b][:, 1 + jj:9 + jj, :], vv[:, jj:jj + 8, :], wp=[va_b[b]])

            load_pair(0)
            load_head(0)
            items = []
            for h in range(MLA_H):
                for qi in range(9):
                    if qi == 0:
                        q0, nq = 0, 16
                        kts = [(0, 0, 16, 0, True)]
                    else:
                        q0, nq = 16 + 512 * (qi - 1), 512
                        kts = [(0, 0, 16, 0, False)] + [(kt, 16 + 128 * (kt - 1), 128, 0, False) for kt in range(1, 4 * (qi - 1) + 1)]
                        kts += [(4 * (qi - 1) + 1 + i, 16 + 128 * (4 * (qi - 1) + i), 128, 128 * i, True) for i in range(4)]
                    for idx, kt in enumerate(kts):
                        items.append((h, qi, q0, nq, idx, len(kts)) + kt)
            LA = 4
            NI = len(items)
            for i in range(NI + LA):
                if i < NI:
                    (h, qi, q0, nq, idx, nk_t, kt, k0, nk, qoff, diag) = items[i]
                    if qi == 0 and idx == 0 and h + 1 < MLA_H:
                        load_head(h + 1)
                        if h % 2 == 1:
                            load_pair(h // 2 + 1)
                    hb_ = h % 2
                    nqq = nq - qoff
                    p = i % NPS
                    kb.op(kb.pe, lambda: T.matmul(ps[p][0:nk, 0:nqq], kh[hb_][:, k0:k0 + nk], qh[hb_][:, q0 + qoff:q0 + nq], start=True, stop=True),
                          r=[qk_b[hb_]], w=[ps_b[p]])
                    kb.op(kb.act, lambda: A.activation(out=pT[p][0:nk, 0:nqq], in_=ps[p][0:nk, 0:nqq], func=AF.Exp, bias=negB[0:nk, :], scale=scale),
                          r=[ps_b[p], bnd_b], w=[pT_b[p]])
                    if diag:
                        kb.op(kb.dve, lambda: V.tensor_tensor(pT[p][0:nk, 0:nk], pT[p][0:nk, 0:nk], self.tri16[0:nk, 0:nk], ALU.mult),
                              r=[self.cb_], w=[pT_b[p]])
                ii = i - LA
                if ii >= 0:
                    (h, qi, q0, nq, idx, nk_t, kt, k0, nk, qoff, diag) = items[ii]
                    c, e = h // 2, h % 2
                    vb_ = c % 2
                    dlo, dhi = (0, 64) if e == 0 else (64, 128)
                    nlo, nhi = (64, 128) if e == 0 else (0, 64)
                    nqq = nq - qoff
                    p = ii % NPS
                    pp = (h * 9 + qi) % 2
                    first, last = idx == 0, idx == nk_t - 1
                    kb.op(kb.pe, lambda: T.matmul(po[pp][:, qoff:nq], va[vb_][0:nk, kt, e * 64:e * 64 + 128], pT[p][0:nk, 0:nqq], start=first, stop=last),
                          r=[pT_b[p], va_b[vb_]], w=[po_b[pp]] if first else [], wp=[] if first else [po_b[pp]], inc=last)
                    if last:
                        kb.op(kb.dve, lambda: V.reciprocal(rden[pp][nlo:nhi, 0:nq], po[pp][nlo:nhi, 0:nq]), r=[po_b[pp]], w=[rd_b[pp]])
                        kb.op(kb.dve, lambda: V.tensor_copy(rdsh[pp][dlo:dhi, 0:nq], rden[pp][nlo:nhi, 0:nq]), r=[rd_b[pp]], w=[rs_b[pp]])
                        kb.op(kb.dve, lambda: V.tensor_tensor(oT[dlo:dhi, c, q0:q0 + nq], po[pp][dlo:dhi, 0:nq], rdsh[pp][dlo:dhi, 0:nq], ALU.mult),
                              r=[po_b[pp], rs_b[pp]], wp=[oT_b])
            kb.barrier()
        with contextlib.ExitStack() as s3:
            hs = [self.S(s3, "c_h%d" % i, [128, D], F32) for i in range(3)]
            hs_b = [Buf() for _ in range(3)]
            po = [self.P(s3, "c_po%d" % i, [128, 512], F32) for i in range(4)]
            po_b = [PB() for _ in range(4)]
            ip = 0
            for ti, (t0, n) in enumerate(TILES):
                s = ti % 3
                kb.dma("sp", hs[s][0:n, :], self.h_src(src, t0, n), w=[hs_b[s]])
                for hh in range(2):
                    p = ip % 4
                    ip += 1
                    for c in range(8):
                        kb.op(kb.pe, lambda: T.matmul(po[p][0:n, :], oT[:, c, t0:t0 + n], w_o[:, c, hh * 512:(hh + 1) * 512], start=(c == 0), stop=(c == 7)),
                              r=[oT_b, w_o_b], w=[po_b[p]] if c == 0 else [], wp=[po_b[p]] if c else [], inc=(c == 7))
                    kb.op(kb.dve, lambda: V.tensor_tensor(hs[s][0:n, hh * 512:(hh + 1) * 512], po[p][0:n, :], hs[s][0:n, hh * 512:(hh + 1) * 512], ALU.add),
                          r=[po_b[p], hs_b[s]], wp=[hs_b[s]])
                dap = self.h_dst(dst, t0, n)
                if dap is not None:
                    kb.dma("sp", dap, hs[s][0:n, :], r=[hs_b[s]])


def host_inputs(inp):
    f = lambda a: np.ascontiguousarray(np.asarray(a, dtype=np.float32))
    rep = lambda a: np.ascontiguousarray(np.broadcast_to(np.asarray(a, np.float32)[:, None, :], (a.shape[0], 128, a.shape[1])))
    colT = lambda a, c: np.ascontiguousarray(np.asarray(a, np.float32).reshape(a.shape[0], c, 128).transpose(0, 2, 1))
    ln = np.concatenate([np.asarray(inp["ln_mix"], np.float32), np.asarray(inp["ln_mlp"], np.float32)], 0)
    lnT = np.ascontiguousarray(ln.reshape(8, 8, 128).transpose(2, 0, 1))
    cw = np.asarray(inp["ssd_conv_w"], np.float32)
    cwT = np.ascontiguousarray(cw.reshape(2, 4, 32, 128).transpose(0, 3, 2, 1))
    k = np.arange(128)
    tri = (k[:, None] <= k[None, :]).astype(np.float32)
    inv = 1.0 / (10000.0 ** (np.arange(0, 32, 2, dtype=np.float32) / 32.0))
    ang = np.arange(NT, dtype=np.float32)[:, None] * inv[None, :].astype(np.float32)
    common = {
        "meta": f(inp["meta_tokens"]),
        "ssd_w_in": f(inp["ssd_w_in"]), "ssd_w_out": f(inp["ssd_w_out"]),
        "mla_w_in": f(inp["mla_w_in"]), "mla_w_q_b": f(inp["mla_w_q_b"]), "mla_w_kv_b": f(inp["mla_w_kv_b"]),
        "mla_w_out": f(inp["mla_w_out"]), "mlp_w_up": f(inp["mlp_w_up"]), "mlp_w_down": f(inp["mlp_w_down"]),
        "lnT": lnT, "cw": cwT, "cb": colT(inp["ssd_conv_b"], 32),
        "dtb_rep": rep(inp["ssd_dt_bias"]), "alog_rep": rep(inp["ssd_a_log"]), "dskip_rep": rep(inp["ssd_d"]),
        "ssd_normT": colT(inp["ssd_norm"], 16), "q_a_T": colT(inp["mla_q_a_norm"], 3), "kv_a_T": colT(inp["mla_kv_a_norm"], 2),
        "gq_rep": rep(inp["mla_q_norm"]), "gk_rep": rep(inp["mla_k_norm"]),
        "ident": np.eye(128, dtype=np.float32), "tri": tri, "ltri": np.ascontiguousarray(1.0 - tri),
        "ones": np.ones((128, 128), np.float32),
        "cos": np.cos(ang).astype(np.float32), "sin": np.sin(ang).astype(np.float32),
    }
    return common


FULL_PHASES = [
    ("ssd", 0, 0, "x", "h"), ("mlp", 0, "h", "h"),
    ("mla", 0, 1, "h", "h"), ("mlp", 1, "h", "h"),
    ("ssd", 1, 2, "h", "h"), ("mlp", 2, "h", "h"),
    ("mla", 1, 3, "h", "h"), ("mlp", 3, "h", "y"),
]


def run(inputs, phases, cores=8):
    common = host_inputs(inputs)
    x = np.asarray(inputs["x"], np.float32)
    prog = Prog(phases)
    in_maps = []
    for c in range(cores):
        m = dict(common)
        m["x"] = np.ascontiguousarray(x[c])
        in_maps.append(m)
    res = run_bass_kernel_spmd(prog.nc, in_maps, core_ids=list(range(cores)))
    return np.stack([np.asarray(r["y"]) for r in res.results], 0)


def kernel(**inputs):
    return run(inputs, FULL_PHASES, 8).astype(np.float32)
```
